# Optimizing a Trainium2 kernel written in Bass

```python
import jax, jax.numpy as jnp
from jax import lax
import numpy as np

D_MODEL = 1024
BATCH = 4
SEQ = 8192
DEPTH = 2

GRID_W = 64
CTX_LEN = 256
A_HEADS = 8
A_KV_HEADS = 2
A_HEAD_DIM = 64
WINDOW = 128
A_BLOCK = 128
ROPE_BASE = 10000.0
G_HEADS = 4
G_DK = 64
G_DV = 128
G_RANK = 16
G_TAU = 16.0
M_HEADS = 4
M_HEAD_DIM = 128
M_CONV = 5
CHUNK = 64
D_FF = 2816
N_MOD = 9
EPS = 1e-6

A_Q = A_HEADS * A_HEAD_DIM
A_KV = A_KV_HEADS * A_HEAD_DIM
G_QK = G_HEADS * G_DK
G_V = G_HEADS * G_DV
M_W = M_HEADS * M_HEAD_DIM
IN_SPLITS = (A_Q, A_KV, A_KV,
             G_QK, G_QK, G_V, G_V, 2 * G_RANK,
             M_W, M_W, M_W, M_W, 2 * M_HEADS, 2 * M_HEADS,
             D_MODEL, D_MODEL, D_MODEL)
D_IN = sum(IN_SPLITS)

kernel_name = 'hybrid_diffusion_gated_parallel_mixers'

F32 = jnp.float32


def rmsnorm(x, g):
    xf = x.astype(F32)
    y = xf * lax.rsqrt(jnp.mean(xf * xf, axis=-1, keepdims=True) + EPS)
    return (y * g.astype(F32)).astype(x.dtype)


def swiglu(h, w13, w2):
    a, b = jnp.split(h @ w13, 2, axis=-1)
    return (jax.nn.silu(a) * b) @ w2


def split_cols(z):
    idx = [int(i) for i in np.cumsum(IN_SPLITS)[:-1]]
    return jnp.split(z, idx, axis=-1)


def dwconv_centred(x, w, b):
    pad = w.shape[0] // 2
    y = lax.conv_general_dilated(x, w[:, None, :].astype(x.dtype), (1,), [(pad, pad)],
                                 dimension_numbers=('NWC', 'WIO', 'NWC'),
                                 feature_group_count=x.shape[-1])
    return y + b.astype(x.dtype)


def axial_rope(rows):
    r = jnp.repeat(jnp.arange(rows), GRID_W).astype(F32)
    col = jnp.tile(jnp.arange(GRID_W), rows).astype(F32)
    n_freq = A_HEAD_DIM // 4
    inv = ROPE_BASE ** (-jnp.arange(n_freq, dtype=F32) / n_freq)
    ang = jnp.concatenate([r[:, None] * inv, col[:, None] * inv], axis=-1)
    return jnp.cos(ang), jnp.sin(ang)


def apply_rope(x, cos, sin):
    x1, x2 = jnp.split(x.astype(F32), 2, axis=-1)
    c, s = cos[:, None, :], sin[:, None, :]
    return jnp.concatenate([x1 * c - x2 * s, x1 * s + x2 * c], axis=-1).astype(x.dtype)


def softmax_with_sink(logits, sink):
    full = jnp.concatenate([logits, jnp.broadcast_to(sink, logits.shape[:-1] + (1,))], axis=-1)
    return jax.nn.softmax(full, axis=-1)[..., :-1]


def latent_window_attention(q, k, v, k_ctx, v_ctx, sink):
    B, S, H, hd = q.shape
    G = H // A_KV_HEADS
    nb = S // A_BLOCK
    scale = hd ** -0.5
    qb = q.reshape(B, nb, A_BLOCK, A_KV_HEADS, G, hd)

    def band(a):
        ap = jnp.pad(a, ((0, 0), (A_BLOCK, A_BLOCK), (0, 0), (0, 0)))
        ap = ap.reshape(B, nb + 2, A_BLOCK, A_KV_HEADS, hd)
        return jnp.concatenate([ap[:, :-2], ap[:, 1:-1], ap[:, 2:]], axis=2)

    kb, vb = band(k), band(v)
    s_loc = jnp.einsum('bnqhgd,bnkhd->bnhgqk', qb, kb).astype(F32) * scale
    s_ctx = jnp.einsum('bnqhgd,bchd->bnhgqc', qb, k_ctx).astype(F32) * scale
    qpos = jnp.arange(S).reshape(nb, A_BLOCK)
    kpos = (jnp.arange(nb)[:, None] - 1) * A_BLOCK + jnp.arange(3 * A_BLOCK)[None, :]
    valid = ((jnp.abs(qpos[:, :, None] - kpos[:, None, :]) <= WINDOW)
             & (kpos[:, None, :] >= 0) & (kpos[:, None, :] < S))
    s_loc = jnp.where(valid[None, :, None, None], s_loc, -jnp.inf)
    sink_b = sink.astype(F32).reshape(1, 1, A_KV_HEADS, G, 1, 1)
    p = softmax_with_sink(jnp.concatenate([s_loc, s_ctx], axis=-1), sink_b)
    p_loc = p[..., :3 * A_BLOCK].astype(v.dtype)
    p_ctx = p[..., 3 * A_BLOCK:].astype(v.dtype)
    o = (jnp.einsum('bnhgqk,bnkhd->bnqhgd', p_loc, vb)
         + jnp.einsum('bnhgqc,bchd->bnqhgd', p_ctx, v_ctx))
    return o.reshape(B, S, H * hd)


def context_attention(q, k, v, sink):
    B, L, H, hd = q.shape
    G = H // A_KV_HEADS
    qg = q.reshape(B, L, A_KV_HEADS, G, hd)
    s = jnp.einsum('bqhgd,bkhd->bhgqk', qg, k).astype(F32) * hd ** -0.5
    p = softmax_with_sink(s, sink.astype(F32).reshape(1, A_KV_HEADS, G, 1, 1)).astype(v.dtype)
    return jnp.einsum('bhgqk,bkhd->bqhgd', p, v).reshape(B, L, H * hd)


def to_chunks(a):
    B, T, H, d = a.shape
    return a.reshape(B, T // CHUNK, CHUNK, H, d).transpose(1, 0, 3, 2, 4)


def from_chunks(a):
    n, B, H, L, d = a.shape
    return a.transpose(1, 0, 3, 2, 4).reshape(B, n * L, H, d)


def gla_scan(q, k, v, log_a, state):
    tri = jnp.tril(jnp.ones((CHUNK, CHUNK), dtype=bool))

    def step(S, inp):
        qc, kc, vc, ac = inp
        b = jnp.cumsum(ac, axis=2)
        diff = b[:, :, :, None, :] - b[:, :, None, :, :]
        dec = jnp.where(tri[:, :, None], jnp.exp(jnp.where(tri[:, :, None], diff, 0.0)), 0.0)
        att = jnp.einsum('bhid,bhjd,bhijd->bhij', qc, kc, dec)
        o = (jnp.einsum('bhid,bhde->bhie', qc * jnp.exp(b), S)
             + jnp.einsum('bhij,bhje->bhie', att, vc))
        b_last = b[:, :, -1:, :]
        S = (jnp.exp(b_last[:, :, 0, :])[..., None] * S
             + jnp.einsum('bhjd,bhje->bhde', kc * jnp.exp(b_last - b), vc))
        return S, o

    xs = tuple(to_chunks(a.astype(F32)) for a in (q, k, v, log_a))
    S, o = lax.scan(step, state, xs)
    return S, from_chunks(o)


def mlstm_scan(q, k, v, i_pre, log_f, carry):
    tri = jnp.tril(jnp.ones((CHUNK, CHUNK), dtype=bool))

    def step(carry, inp):
        C, n, m = carry
        qc, kc, vc, ic, fc = inp
        ic, fc = ic[..., 0], fc[..., 0]
        b = jnp.cumsum(fc, axis=-1)
        log_w = jnp.where(tri, b[..., :, None] - b[..., None, :] + ic[..., None, :], -jnp.inf)
        log_inter = b + m[..., None]
        m_i = jnp.maximum(log_inter, jnp.max(log_w, axis=-1))
        w = jnp.exp(log_w - m_i[..., None])
        w_inter = jnp.exp(log_inter - m_i)
        s = jnp.einsum('bhid,bhjd->bhij', qc, kc) * w
        num = (w_inter[..., None] * jnp.einsum('bhid,bhde->bhie', qc, C)
               + jnp.einsum('bhij,bhje->bhie', s, vc))
        den = w_inter * jnp.einsum('bhid,bhd->bhi', qc, n) + jnp.sum(s, axis=-1)
        h = num / jnp.maximum(jnp.abs(den), jnp.exp(-m_i))[..., None]
        m_new = m_i[..., -1]
        w_state = jnp.exp(b[..., -1:] - b + ic - m_new[..., None])
        decay = jnp.exp(b[..., -1] + m - m_new)
        C = decay[..., None, None] * C + jnp.einsum('bhj,bhjd,bhje->bhde', w_state, kc, vc)
        n = decay[..., None] * n + jnp.einsum('bhj,bhjd->bhd', w_state, kc)
        return (C, n, m_new), h

    xs = tuple(to_chunks(a.astype(F32)) for a in (q, k, v, i_pre, log_f))
    carry, h = lax.scan(step, carry, xs)
    return carry, from_chunks(h)


def flip_time(a, rev):
    return a[:, ::-1] if rev else a


def bidirectional(scan_fn, init, ctx_dirs, lat_dirs):
    out_ctx, out_lat = [], []
    for d in range(2):
        rev = d == 1
        state, o_c = scan_fn(*(flip_time(a, rev) for a in ctx_dirs[d]), init)
        _, o_l = scan_fn(*(flip_time(a, rev) for a in lat_dirs[d]), state)
        out_ctx.append(flip_time(o_c, rev))
        out_lat.append(flip_time(o_l, rev))
    return out_ctx[0] + out_ctx[1], out_lat[0] + out_lat[1]


def stream_features(h, lp):
    B, T, _ = h.shape
    (aq, ak, av, gq, gk, gv, gr, gg, mq, mk, mv, mo, mi, mf, s_a, s_g, s_m) = split_cols(h @ lp['w_in'])
    f = {}
    f['aq'] = rmsnorm(aq.reshape(B, T, A_HEADS, A_HEAD_DIM), lp['attn_q_norm'])
    f['ak'] = rmsnorm(ak.reshape(B, T, A_KV_HEADS, A_HEAD_DIM), lp['attn_k_norm'])
    f['av'] = av.reshape(B, T, A_KV_HEADS, A_HEAD_DIM)
    f['gq'] = gq.reshape(B, T, G_HEADS, G_DK) * (G_DK ** -0.5)
    f['gk'] = gk.reshape(B, T, G_HEADS, G_DK)
    f['gv'] = gv.reshape(B, T, G_HEADS, G_DV)
    f['gr'] = gr
    gate_lr = gg.reshape(B, T, 2, G_RANK)
    log_a = jax.nn.log_sigmoid(
        (jnp.einsum('btnr,nrk->btnk', gate_lr, lp['gla_w2']) + lp['gla_b']).astype(F32)) / G_TAU
    f['ga'] = tuple(log_a[:, :, d].reshape(B, T, G_HEADS, G_DK) for d in range(2))
    mqk = jax.nn.silu(dwconv_centred(jnp.concatenate([mq, mk], axis=-1),
                                     lp['mlstm_conv_w'], lp['mlstm_conv_b']))
    mq, mk = jnp.split(mqk, 2, axis=-1)
    f['mq'] = mq.reshape(B, T, M_HEADS, M_HEAD_DIM)
    f['mk'] = mk.reshape(B, T, M_HEADS, M_HEAD_DIM) * (M_HEAD_DIM ** -0.5)
    f['mv'] = mv.reshape(B, T, M_HEADS, M_HEAD_DIM)
    f['mo'] = mo
    i_pre = (mi.reshape(B, T, 2, M_HEADS) + lp['mlstm_ib']).astype(F32)
    log_f = jax.nn.log_sigmoid((mf.reshape(B, T, 2, M_HEADS) + lp['mlstm_fb']).astype(F32))
    f['mi'] = tuple(i_pre[:, :, d, :, None] for d in range(2))
    f['mf'] = tuple(log_f[:, :, d, :, None] for d in range(2))
    f['branch_gates'] = (jax.nn.sigmoid(s_a), jax.nn.sigmoid(s_g), jax.nn.sigmoid(s_m))
    return f


def merge_branches(f, att, gla, mlstm, lp):
    dt = att.dtype
    B, T = att.shape[:2]
    g = rmsnorm(gla.astype(dt), lp['gla_norm']).reshape(B, T, G_V) * jax.nn.silu(f['gr'])
    m = rmsnorm(mlstm.astype(dt), lp['mlstm_norm']).reshape(B, T, M_W) * jax.nn.sigmoid(f['mo'])
    ga, gg, gm = f['branch_gates']
    y = (ga * (att @ lp['w_out_attn']) + gg * (g @ lp['w_out_gla'])
         + gm * (m @ lp['w_out_mlstm']))
    return y @ lp['w_o']


def token_mixer(xn, cn, cos, sin, lp, want_ctx):
    B = xn.shape[0]
    fl = stream_features(xn, lp)
    fc = stream_features(cn, lp)
    q_lat = apply_rope(fl['aq'], cos, sin)
    k_lat = apply_rope(fl['ak'], cos, sin)
    att_l = latent_window_attention(q_lat, k_lat, fl['av'], fc['ak'], fc['av'], lp['attn_sink'])
    g0 = jnp.zeros((B, G_HEADS, G_DK, G_DV), F32)
    gla_c, gla_l = bidirectional(
        gla_scan, g0,
        tuple((fc['gq'], fc['gk'], fc['gv'], fc['ga'][d]) for d in range(2)),
        tuple((fl['gq'], fl['gk'], fl['gv'], fl['ga'][d]) for d in range(2)))
    m0 = (jnp.zeros((B, M_HEADS, M_HEAD_DIM, M_HEAD_DIM), F32),
          jnp.zeros((B, M_HEADS, M_HEAD_DIM), F32),
          jnp.zeros((B, M_HEADS), F32))
    ml_c, ml_l = bidirectional(
        mlstm_scan, m0,
        tuple((fc['mq'], fc['mk'], fc['mv'], fc['mi'][d], fc['mf'][d]) for d in range(2)),
        tuple((fl['mq'], fl['mk'], fl['mv'], fl['mi'][d], fl['mf'][d]) for d in range(2)))
    y_lat = merge_branches(fl, att_l, gla_l, ml_l, lp)
    if not want_ctx:
        return y_lat, None
    att_c = context_attention(fc['aq'], fc['ak'], fc['av'], lp['attn_sink'])
    y_ctx = merge_branches(fc, att_c, gla_c, ml_c, lp)
    return y_lat, y_ctx


def adaln(h, m, j, gain):
    return rmsnorm(h, gain) * (1 + m[..., 3 * j + 1, :, :]) + m[..., 3 * j, :, :]


def trunk_layer(x, ctx, c, c_ctx, cos, sin, lp, last):
    B = x.shape[0]
    mod_l = (jax.nn.silu(c) @ lp['mod_w'] + lp['mod_b']).reshape(B, N_MOD, 1, D_MODEL)
    mod_c = (jax.nn.silu(c_ctx) @ lp['mod_w'] + lp['mod_b']).reshape(N_MOD, 1, D_MODEL)
    g = lp['norm_g']
    x = x + 0.5 * mod_l[:, 2] * swiglu(adaln(x, mod_l, 0, g[0]), lp['ffn1_w13'], lp['ffn1_w2'])
    ctx = ctx + 0.5 * mod_c[2] * swiglu(adaln(ctx, mod_c, 0, g[0]), lp['ffn1_w13'], lp['ffn1_w2'])
    y_l, y_c = token_mixer(adaln(x, mod_l, 1, g[1]), adaln(ctx, mod_c, 1, g[1]), cos, sin, lp, not last)
    x = x + mod_l[:, 5] * y_l
    x = x + 0.5 * mod_l[:, 8] * swiglu(adaln(x, mod_l, 2, g[2]), lp['ffn2_w13'], lp['ffn2_w2'])
    if not last:
        ctx = ctx + mod_c[5] * y_c
        ctx = ctx + 0.5 * mod_c[8] * swiglu(adaln(ctx, mod_c, 2, g[2]), lp['ffn2_w13'], lp['ffn2_w2'])
    return x, ctx


def setup_inputs(seed: int = 0) -> dict:
    key = jax.random.key(seed)
    ks = iter(jax.random.split(key, 40))

    def nrm(shape, s):
        return jax.random.normal(next(ks), shape, F32) * s

    def uni(shape):
        return jax.random.uniform(next(ks), shape, F32)

    D = D_MODEL
    return {
        'x': nrm((BATCH, SEQ, D), 1.0),
        'c': nrm((BATCH, D), 1.0),
        'ctx': nrm((BATCH, CTX_LEN, D), 1.0),
        'c_ctx': nrm((D,), 1.0),
        'mod_w': nrm((DEPTH, D, N_MOD * D), D ** -0.5),
        'mod_b': nrm((DEPTH, N_MOD * D), 0.02),
        'norm_g': 1.0 + nrm((DEPTH, 3, D), 0.02),
        'ffn1_w13': nrm((DEPTH, D, 2 * D_FF), D ** -0.5),
        'ffn1_w2': nrm((DEPTH, D_FF, D), D_FF ** -0.5),
        'ffn2_w13': nrm((DEPTH, D, 2 * D_FF), D ** -0.5),
        'ffn2_w2': nrm((DEPTH, D_FF, D), D_FF ** -0.5),
        'w_in': nrm((DEPTH, D, D_IN), D ** -0.5),
        'attn_q_norm': 1.0 + nrm((DEPTH, A_HEAD_DIM), 0.02),
        'attn_k_norm': 1.0 + nrm((DEPTH, A_HEAD_DIM), 0.02),
        'attn_sink': nrm((DEPTH, A_HEADS), 0.5),
        'gla_w2': nrm((DEPTH, 2, G_RANK, G_QK), G_RANK ** -0.5),
        'gla_b': 1.0 + nrm((DEPTH, 2, G_QK), 0.1),
        'gla_norm': 1.0 + nrm((DEPTH, G_DV), 0.02),
        'mlstm_conv_w': nrm((DEPTH, M_CONV, 2 * M_W), M_CONV ** -0.5),
        'mlstm_conv_b': nrm((DEPTH, 2 * M_W), 0.02),
        'mlstm_ib': nrm((DEPTH, 2, M_HEADS), 0.1),
        'mlstm_fb': 3.0 + 3.0 * uni((DEPTH, 2, M_HEADS)),
        'mlstm_norm': 1.0 + nrm((DEPTH, M_HEAD_DIM), 0.02),
        'w_out_attn': nrm((DEPTH, A_Q, D), A_Q ** -0.5),
        'w_out_gla': nrm((DEPTH, G_V, D), G_V ** -0.5),
        'w_out_mlstm': nrm((DEPTH, M_W, D), M_W ** -0.5),
        'w_o': nrm((DEPTH, D, D), D ** -0.5),
    }


def reference(x, c, ctx, c_ctx, mod_w, mod_b, norm_g, ffn1_w13, ffn1_w2, ffn2_w13, ffn2_w2,
              w_in, attn_q_norm, attn_k_norm, attn_sink, gla_w2, gla_b, gla_norm,
              mlstm_conv_w, mlstm_conv_b, mlstm_ib, mlstm_fb, mlstm_norm,
              w_out_attn, w_out_gla, w_out_mlstm, w_o):
    rows = x.shape[1] // GRID_W
    cos, sin = axial_rope(rows)
    for l in range(DEPTH):
        lp = {
            'mod_w': mod_w[l], 'mod_b': mod_b[l], 'norm_g': norm_g[l],
            'ffn1_w13': ffn1_w13[l], 'ffn1_w2': ffn1_w2[l],
            'ffn2_w13': ffn2_w13[l], 'ffn2_w2': ffn2_w2[l],
            'w_in': w_in[l], 'attn_q_norm': attn_q_norm[l], 'attn_k_norm': attn_k_norm[l],
            'attn_sink': attn_sink[l], 'gla_w2': gla_w2[l], 'gla_b': gla_b[l],
            'gla_norm': gla_norm[l], 'mlstm_conv_w': mlstm_conv_w[l],
            'mlstm_conv_b': mlstm_conv_b[l], 'mlstm_ib': mlstm_ib[l], 'mlstm_fb': mlstm_fb[l],
            'mlstm_norm': mlstm_norm[l], 'w_out_attn': w_out_attn[l], 'w_out_gla': w_out_gla[l],
            'w_out_mlstm': w_out_mlstm[l], 'w_o': w_o[l],
        }
        x, ctx = trunk_layer(x, ctx, c, c_ctx, cos, sin, lp, l == DEPTH - 1)
    return x
```

```python
import numpy as np
from contextlib import ExitStack
import concourse.bass as bass
import concourse.mybir as mybir
from concourse.bass_utils import run_bass_kernel_spmd

F32 = mybir.dt.float32
BF16 = mybir.dt.bfloat16
AF = mybir.ActivationFunctionType
ALU = mybir.AluOpType
AX = mybir.AxisListType

D = 1024
DFF = 2816
NMOD = 9
LC = 256
EPS = 1e-6
DEPTH = 2
D_IN = 7472


class Res:
    __slots__ = ("name", "w", "r")

    def __init__(self, name=""):
        self.name = name
        self.w = None
        self.r = []


class Eng:
    def __init__(self, name, is_pe=False):
        self.name = name
        self.is_pe = is_pe
        self.ops = []
        self.sems = []
        self.si = 0
        self.cnt = 0
        self.seen = {}
        self.pend_r = []
        self.pend_w = []
        self.pool = []
        self.pi = 0


ROT = 30000


class Sched:
    def __init__(self, nc, es):
        self.nc = nc
        self.es = es
        self.pe = Eng("pe", True)
        self.act = Eng("act")
        self.dve = Eng("dve")
        self.pool = Eng("pool")
        self.sp = Eng("sp")
        self.engs = [self.pe, self.act, self.dve, self.pool, self.sp]
        self.semid = {}
        n_rot = {"pe": 6, "act": 3, "dve": 3, "pool": 3, "sp": 1}
        for e in self.engs:
            for i in range(n_rot[e.name]):
                s = es.enter_context(nc.semaphore(f"s_{e.name}{i}"))
                e.sems.append(s)
        for e, n in ((self.sp, 20), (self.pool, 10), (self.act, 4)):
            for i in range(n):
                s = es.enter_context(nc.semaphore(f"d_{e.name}{i}"))
                e.pool.append([s, 0])
        self.n_ops = 0

    def _need(self, eng, tok, raw):
        if tok is None:
            return None
        sem, val, owner = tok
        if owner == eng.name:
            if eng.is_pe:
                return None
            if not raw:
                return None
        key = id(sem)
        if eng.seen.get(key, 0) >= val:
            return None
        eng.seen[key] = val
        return (sem, val)

    def _waits(self, eng, reads, writes):
        ws = []
        for r in reads:
            w = self._need(eng, r.w, True)
            if w:
                ws.append(w)
        for wr in writes:
            w = self._need(eng, wr.w, False)
            if w:
                ws.append(w)
            for t in wr.r:
                w = self._need(eng, t, False)
                if w:
                    ws.append(w)
        for (sem, val) in ws:
            eng.ops.append(lambda h, sem=sem, val=val: h.wait_ge(sem, val))

    def _record(self, tok, reads, writes):
        for r in reads:
            r.r = [t for t in r.r if t[2] != tok[2] or t[0] is not tok[0]] + [tok]
        for w in writes:
            w.w = tok
            w.r = []

    def op(self, eng, fn, reads=(), writes=(), inc=True):
        self.n_ops += 1
        reads = list(reads)
        writes = list(writes)
        self._waits(eng, reads, writes)
        if not inc:
            eng.ops.append(lambda h, fn=fn: fn(h))
            eng.pend_r += reads
            eng.pend_w += writes
            return
        if eng.cnt >= ROT:
            eng.si += 1
            eng.cnt = 0
        eng.cnt += 1
        sem = eng.sems[eng.si]
        tok = (sem, eng.cnt, eng.name)
        eng.ops.append(lambda h, fn=fn, sem=sem: fn(h).then_inc(sem, 1))
        self._record(tok, reads + eng.pend_r, writes + eng.pend_w)
        eng.pend_r = []
        eng.pend_w = []

    def dma(self, eng, pairs, reads=(), writes=(), **kw):
        self.n_ops += 1
        reads = list(reads)
        writes = list(writes)
        self._waits(eng, reads, writes)
        ent = eng.pool[eng.pi]
        eng.pi = (eng.pi + 1) % len(eng.pool)
        sem = ent[0]
        if ent[1] > 0 and eng.seen.get(id(sem), 0) < ent[1]:
            v = ent[1]
            eng.ops.append(lambda h, sem=sem, v=v: h.wait_ge(sem, v))
            eng.seen[id(sem)] = v
        for (o, i) in pairs:
            ent[1] += 16
            eng.ops.append(lambda h, o=o, i=i, sem=sem: h.dma_start(out=o, in_=i, **kw).then_inc(sem, 16))
        tok = (sem, ent[1], "dma_" + eng.name + str(id(sem)))
        self._record(tok, reads, writes)

    def barrier(self):
        toks = []
        for e in self.engs:
            assert not e.pend_r and not e.pend_w, e.name
            for i in range(e.si + 1):
                v = ROT if i < e.si else e.cnt
                if v > 0:
                    toks.append((e, e.sems[i], v))
            for ent in e.pool:
                if ent[1] > 0:
                    toks.append((None, ent[0], ent[1]))
        for e in self.engs:
            for (own, sem, v) in toks:
                if own is e:
                    continue
                if e.seen.get(id(sem), 0) >= v:
                    continue
                e.seen[id(sem)] = v
                e.ops.append(lambda h, sem=sem, v=v: h.wait_ge(sem, v))

    def finish(self):
        for e in (self.sp, self.pool, self.act):
            for ent in e.pool:
                if ent[1] > 0:
                    self.sp.ops.append(lambda h, sem=ent[0], v=ent[1]: h.wait_ge(sem, v))

    def replay(self):
        nc = self.nc
        with nc.Block() as block:
            @block.tensor
            def _(h):
                for f in self.pe.ops:
                    f(h)

            @block.scalar
            def _(h):
                for f in self.act.ops:
                    f(h)

            @block.vector
            def _(h):
                for f in self.dve.ops:
                    f(h)

            @block.gpsimd
            def _(h):
                for f in self.pool.ops:
                    f(h)

            @block.sync
            def _(h):
                for f in self.sp.ops:
                    f(h)


class Tile:
    def __init__(self, t, name):
        self.t = t
        self.res = Res(name)

    def __getitem__(self, k):
        return self.t[k]


def dram_ap(t, offset, pattern):
    return bass.AP(t.tensor, offset, pattern)


class Builder:
    def __init__(self, T, depth=DEPTH, stop=None, dbg=()):
        self.T = T
        self.depth = depth
        self.stop = stop
        self.dbg = dbg
        self.nc = bass.Bass("TRN2", target_bir_lowering=False)
        self.es = ExitStack()
        self.S = None
        self.dres = {}
        self.rr = {}

    def din(self, name, shape):
        return self.nc.dram_tensor(name, list(shape), F32, kind="ExternalInput").ap()

    def dout(self, name, shape, dt=F32):
        return self.nc.dram_tensor(name, list(shape), dt, kind="ExternalOutput").ap()

    def dscr(self, name, shape, dt=F32):
        if name in self.dbg:
            return self.nc.dram_tensor(name, list(shape), dt, kind="ExternalOutput").ap()
        return self.nc.dram_tensor(name, list(shape), dt).ap()

    def R(self, *key):
        if key not in self.dres:
            self.dres[key] = Res(str(key))
        return self.dres[key]

    def sb(self, name, shape, dt):
        self.uid = getattr(self, "uid", 0) + 1
        name = f"{name}_{self.uid}"
        t = self.cur.enter_context(self.nc.sbuf_tensor(name, list(shape), dt))
        return Tile(t, name)

    def ps(self, name, shape, dt=F32):
        self.uid = getattr(self, "uid", 0) + 1
        name = f"{name}_{self.uid}"
        t = self.cur.enter_context(self.nc.psum_tensor(name, list(shape), dt))
        return Tile(t, name)

    def build(self):
        nc = self.nc
        T = self.T
        L = self.depth
        with self.es as es:
            self.S = S = Sched(nc, es)
            self.x_in = self.din("x", [T, D])
            self.c_in = self.din("c", [D])
            self.ctx_in = self.din("ctx", [LC, D])
            self.cctx_in = self.din("c_ctx", [D])
            self.mod_w = self.din("mod_w", [L, D, NMOD * D])
            self.mod_b = self.din("mod_b", [L, NMOD * D])
            self.norm_g = self.din("norm_g", [L, 3, D])
            self.ffn_w13 = [self.din("ffn1_w13", [L, D, 2 * DFF]), self.din("ffn2_w13", [L, D, 2 * DFF])]
            self.ffn_w2 = [self.din("ffn1_w2", [L, DFF, D]), self.din("ffn2_w2", [L, DFF, D])]
            self.w_in = self.din("w_in", [L, D, D_IN])
            self.attn_q_norm = self.din("attn_q_norm", [L, 64])
            self.attn_k_norm = self.din("attn_k_norm", [L, 64])
            self.attn_sink = self.din("attn_sink", [L, 8])
            self.gla_w2 = self.din("gla_w2", [L, 2, 16, 256])
            self.gla_b = self.din("gla_b", [L, 2, 256])
            self.rope_cs = self.din("rope_cs", [T, 2, 32])
            self.y_out = self.dout("y", [T, D])
            TT = self.TT = LC + T
            self.H2T = [self.dscr(f"H2T{l}", [D, TT], BF16) for l in range(L)]
            self.QT = [self.dscr(f"QT{l}", [64, 8, TT], BF16) for l in range(L)]
            self.KT = [self.dscr(f"KT{l}", [64, 2, TT], BF16) for l in range(L)]
            self.VA = [self.dscr(f"VA{l}", [TT, 128], BF16) for l in range(L)]
            self.GV = [self.dscr(f"GV{l}", [TT, 512], BF16) for l in range(L)]
            self.MV = [self.dscr(f"MV{l}", [TT, 512], BF16) for l in range(L)]
            self.GATES = [self.dscr(f"GATES{l}", [16, TT]) for l in range(L)]
            self.QG = [self.dscr(f"QG{l}", [2, 64, 4, TT], BF16) for l in range(L)]
            self.KG = [self.dscr(f"KG{l}", [2, 64, 4, TT], BF16) for l in range(L)]
            self.KH = [self.dscr(f"KH{l}", [2, 64, 4, TT], BF16) for l in range(L)]
            self.MQK = [self.dscr(f"MQK{l}", [D, TT + 4], BF16) for l in range(L)]
            self.ATT = [self.dscr(f"ATT{l}", [64, 8, TT], BF16) for l in range(L)]
            self.OG = [self.dscr(f"OG{l}", [2, TT, 512]) for l in range(L)]
            self.OM = [self.dscr(f"OM{l}", [2, TT, 512]) for l in range(L)]
            self.MROWS = [self.dscr(f"MROWS{l}", [2, 5, 4, TT]) for l in range(L)]
            self.MQC = [self.dscr(f"MQC{l}", [D, TT], BF16) for l in range(L)]
            self.gla_norm = self.din("gla_norm", [L, 128])
            self.mlstm_norm = self.din("mlstm_norm", [L, 128])
            self.w_out_attn = self.din("w_out_attn", [L, 512, D])
            self.w_out_gla = self.din("w_out_gla", [L, 512, D])
            self.w_out_mlstm = self.din("w_out_mlstm", [L, 512, D])
            self.w_o = self.din("w_o", [L, D, D])
            self.X2 = [self.dscr(f"X2_{l}", [T, D]) for l in range(L)]
            self.C2 = [self.dscr(f"C2_{l}", [LC, D]) for l in range(L)]
            self.X3 = [self.dscr(f"X3_{l}", [T, D]) for l in range(L)]
            self.C3 = [self.dscr(f"C3_{l}", [LC, D]) for l in range(L)]
            self.conv_w = self.din("mlstm_conv_w", [L, 5, D])
            self.conv_b = self.din("mlstm_conv_b", [L, D])
            self.mlstm_ib = self.din("mlstm_ib", [L, 2, 4])
            self.mlstm_fb = self.din("mlstm_fb", [L, 2, 4])
            self.MOD = [self.dscr(f"MOD{l}", [2, NMOD * D]) for l in range(L)]
            self.X1 = [self.dscr(f"X1_{l}", [T, D]) for l in range(L)]
            self.C1 = [self.dscr(f"C1_{l}", [LC, D]) for l in range(L)]
            with ExitStack() as cst:
                self.cur = cst
                self.ident = self.sb("ident", [128, 128], BF16)
                self.identf = self.sb("identf", [128, 128], F32)
                self.eps_col = self.sb("eps_col", [128, 1], F32)
                S.op(S.dve, lambda h: h.memset(self.eps_col[:], EPS), writes=[self.eps_col.res])
                self.make_consts()
                for l in range(L):
                    xin = self.x_in if l == 0 else self.X3[l - 1]
                    cin = self.ctx_in if l == 0 else self.C3[l - 1]
                    with ExitStack() as ph:
                        self.cur = ph
                        self.phase_mod(l)
                        S.barrier()
                    if self.stop == ("mod", l):
                        break
                    with ExitStack() as ph:
                        self.cur = ph
                        self.phase_ffn(l, 0, [("ctx", cin, self.C1[l], LC, 1), ("lat", xin, self.X1[l], T, 0)])
                        S.barrier()
                    if self.stop == ("ffn1", l):
                        break
                    with ExitStack() as lay:
                        self.cur = lay
                        self.EL = self.sb("EL", [64, 4, 2, TT // 64], F32)
                        with ExitStack() as ph:
                            self.cur = ph
                            self.phase_feat(l, [("ctx", self.C1[l], LC, 1, 0, False), ("lat", self.X1[l], T, 0, LC, True)])
                            S.barrier()
                        if self.stop == ("feat", l):
                            break
                        with ExitStack() as ph:
                            self.cur = ph
                            self.phase_attn(l, l < L - 1)
                            S.barrier()
                        if self.stop == ("attn", l):
                            break
                        with ExitStack() as ph:
                            self.cur = ph
                            self.phase_gla(l)
                            S.barrier()
                        if self.stop == ("gla", l):
                            break
                        self.cur = lay
                        self.DEC = self.sb("DEC", [128, 2, 4, TT // 64], F32)
                        self.sel = self.sb("sel", [4, 4, 128], F32)
                        S.op(S.dve, lambda h, sel_t=self.sel: h.tensor_copy(out=sel_t[:], in_=self.identf[0:4, 0:4].unsqueeze(2).to_broadcast([4, 4, 128])),
                             reads=[self.identf.res], writes=[self.sel.res])
                        for ph_fn in (self.phase_ml_gates, self.phase_ml_conv, self.phase_ml_scan):
                            with ExitStack() as ph:
                                self.cur = ph
                                ph_fn(l)
                                S.barrier()
                        if self.stop == ("ml", l):
                            break
                        last = (l == L - 1)
                        with ExitStack() as ph:
                            self.cur = ph
                            st = [("lat", self.X1[l], self.X2[l], T, 0, LC)]
                            if not last:
                                st = [("ctx", self.C1[l], self.C2[l], LC, 1, 0)] + st
                            self.phase_merge(l, st)
                            S.barrier()
                        if self.stop == ("merge", l):
                            break
                    with ExitStack() as ph:
                        self.cur = ph
                        xdst = self.y_out if last else self.X3[l]
                        st = [("lat", self.X2[l], xdst, T, 0)]
                        import os
                        if not last and not os.environ.get("NOCTX2"):
                            st = [("ctx", self.C2[l], self.C3[l], LC, 1)] + st
                        self.phase_ffn(l, 1, [(a, b, c, d_, e) for (a, b, c, d_, e) in st])
                        S.barrier()
                    if self.stop == ("ffn2", l):
                        break
                S.finish()
                S.replay()
        return nc

    def make_consts(self):
        S = self.S
        nc = self.nc
        idf = self.identf
        S.op(S.pool, lambda h: h.memset(idf[:], 0.0), writes=[idf.res])
        S.op(S.pool, lambda h: h.affine_select(out=idf[:], in_=idf[:], pattern=[[-1, 128]],
                                                compare_op=ALU.not_equal, fill=1.0, base=0,
                                                channel_multiplier=1),
             reads=[idf.res], writes=[idf.res])
        S.op(S.dve, lambda h: h.tensor_copy(out=self.ident[:], in_=idf[:]), reads=[idf.res], writes=[self.ident.res])

    def phase_mod(self, l):
        S = self.S
        cl = self.sb("cl", [128, 8, 2], F32)
        cs = self.sb("cs", [128, 8, 2], F32)
        S.dma(S.sp, [(cl[:, :, 0], self.c_in.rearrange("(kc p) -> p kc", p=128)),
                     (cl[:, :, 1], self.cctx_in.rearrange("(kc p) -> p kc", p=128))],
              writes=[cl.res], allow_slow_non_contiguous=True)
        S.op(S.act, lambda h: h.activation(out=cs[:], in_=cl[:], func=AF.Silu), reads=[cl.res], writes=[cs.res])
        wm = [self.sb(f"wm{i}", [128, 8, 512], F32) for i in range(2)]
        mb = [self.sb(f"mb{i}", [2, 512], F32) for i in range(2)]
        mo = [self.sb(f"mo{i}", [2, 512], F32) for i in range(2)]
        pm = [self.ps(f"pm{i}", [2, 512]) for i in range(2)]
        mw = self.mod_w[l].rearrange("(kc p) n -> p kc n", p=128)
        for n in range(18):
            i = n % 2
            S.dma(S.sp, [(wm[i][:, 0:4, :], mw[:, 0:4, n * 512:(n + 1) * 512]),
                         (wm[i][:, 4:8, :], mw[:, 4:8, n * 512:(n + 1) * 512])], writes=[wm[i].res])
            mbsrc = self.mod_b[l:l + 1, n * 512:(n + 1) * 512]
            S.dma(S.sp, [(mb[i][0:1, :], mbsrc), (mb[i][1:2, :], mbsrc)], writes=[mb[i].res])
            for kc in range(8):
                S.op(S.pe, lambda h, kc=kc, i=i: h.matmul(pm[i][:], cs[:, kc, :], wm[i][:, kc, :],
                                                           start=(kc == 0), stop=(kc == 7)),
                     reads=[cs.res, wm[i].res], writes=[pm[i].res], inc=(kc == 7))
            S.op(S.dve, lambda h, i=i: h.tensor_tensor(out=mo[i][:], in0=pm[i][:], in1=mb[i][:], op=ALU.add),
                 reads=[pm[i].res, mb[i].res], writes=[mo[i].res])
            S.dma(S.sp, [(self.MOD[l][:, n * 512:(n + 1) * 512], mo[i][:])], reads=[mo[i].res],
                  writes=[self.R("MOD", l)])

    def load_cols(self, dst_ap, src_row_ap, res):
        self.S.dma(self.S.sp, [(dst_ap, src_row_ap.rearrange("(kc p) -> p kc", p=128))], writes=[res],
                   allow_slow_non_contiguous=True)

    def adaln_cols(self, l, j, row, tag):
        S = self.S
        tmp = self.sb(f"adt_{tag}", [128, 3, 8], F32)
        A = self.sb(f"adA_{tag}", [128, 8], F32)
        MODr = self.MOD[l]
        S.dma(S.sp, [(tmp[:, 0, :], MODr[row, (3 * j) * D:(3 * j + 1) * D].rearrange("(kc p) -> p kc", p=128)),
                     (tmp[:, 1, :], MODr[row, (3 * j + 1) * D:(3 * j + 2) * D].rearrange("(kc p) -> p kc", p=128)),
                     (tmp[:, 2, :], self.norm_g[l, j, :].rearrange("(kc p) -> p kc", p=128))],
              reads=[self.R("MOD", l)], writes=[tmp.res], allow_slow_non_contiguous=True)
        S.op(S.dve, lambda h: h.scalar_tensor_tensor(out=A[:], in0=tmp[:, 1, :], scalar=1.0, in1=tmp[:, 2, :],
                                                      op0=ALU.add, op1=ALU.mult),
             reads=[tmp.res], writes=[A.res])
        return A, tmp

    def gate_bc(self, l, j, row, tag, mul):
        S = self.S
        G = self.sb(f"gate_{tag}", [128, D], F32)
        src = self.MOD[l][row:row + 1, (3 * j + 2) * D:(3 * j + 3) * D]
        src_b = dram_ap(src, src.offset, [[0, 128], [1, D]])
        S.dma(S.sp, [(G[:], src_b)], reads=[self.R("MOD", l)], writes=[G.res])
        if mul != 1.0:
            S.op(S.pool, lambda h: h.tensor_scalar(out=G[:], in0=G[:], scalar1=float(mul), scalar2=None, op0=ALU.mult),
                 reads=[G.res], writes=[G.res])
        return G

    def load_weight_bf16(self, dst, src3, nsplit):
        S = self.S
        kcn = dst.t.shape[1]
        step = max(1, kcn // nsplit)
        for k0 in range(0, kcn, step):
            k1 = min(kcn, k0 + step)
            S.dma(S.pool, [(dst[:, k0:k1, :], src3[:, k0:k1, :])], writes=[dst.res])

    def norm_part(self, xt, nb, ss, rs):
        S = self.S
        junk = self.junk
        S.op(S.act, lambda h: h.activation(out=junk[:], in_=xt[:], func=AF.Square, accum_out=ss[:]),
             reads=[xt.res], writes=[junk.res, ss.res])
        S.op(S.act, lambda h: h.activation(out=rs[:], in_=ss[:], func=AF.Sqrt, scale=1.0 / D, bias=self.eps_col[:]),
             reads=[ss.res], writes=[rs.res])
        S.op(S.dve, lambda h: h.reciprocal(out=rs[:], in_=rs[:]), reads=[rs.res], writes=[rs.res])
        S.op(S.dve, lambda h: h.tensor_scalar(out=nb[:], in0=xt[:], scalar1=rs[:], scalar2=None, op0=ALU.mult),
             reads=[xt.res, rs.res], writes=[nb.res])

    def transpose_part(self, nb, pT, hT, col0, A, sh, evac_engs):
        S = self.S
        for kc in range(8):
            S.op(S.pe, lambda h, kc=kc: h.transpose(out=pT[:, kc * 128:(kc + 1) * 128], in_=nb[:, kc * 128:(kc + 1) * 128],
                                                     identity=self.ident[:]),
                 reads=[nb.res, self.ident.res], writes=[pT.res], inc=(kc == 7))
        for kc in range(8):
            e = evac_engs[kc % len(evac_engs)]
            if e is S.act:
                S.op(e, lambda h, kc=kc: h.activation(out=hT[:, kc, col0:col0 + 128], in_=pT[:, kc * 128:(kc + 1) * 128],
                                                      func=AF.Identity, scale=A[:, kc:kc + 1], bias=sh[:, kc:kc + 1]),
                     reads=[pT.res, A.res, self.shres], writes=[hT.res])
            else:
                S.op(e, lambda h, kc=kc: h.tensor_scalar(out=hT[:, kc, col0:col0 + 128], in0=pT[:, kc * 128:(kc + 1) * 128],
                                                         scalar1=A[:, kc:kc + 1], scalar2=sh[:, kc:kc + 1],
                                                         op0=ALU.mult, op1=ALU.add),
                     reads=[pT.res, A.res, self.shres], writes=[hT.res])

    def phase_ffn(self, l, which, streams):
        S = self.S
        j = 0 if which == 0 else 2
        W13 = self.sb("W13", [128, 8, 2 * DFF], BF16)
        W2 = self.sb("W2", [128, 22, D], BF16)
        self.load_weight_bf16(W13, self.ffn_w13[which][l].rearrange("(kc p) n -> p kc n", p=128), 8)
        self.load_weight_bf16(W2, self.ffn_w2[which][l].rearrange("(fc p) n -> p fc n", p=128), 11)
        import os
        if os.environ.get("FFN_WONLY") and which == 1:
            return
        self.junk = self.sb("junk", [128, D], BF16)
        xl = [self.sb(f"xl{i}", [128, D], F32) for i in range(3)]
        xr = [self.sb(f"xr{i}", [128, D], F32) for i in range(2)]
        nb = [self.sb(f"nb{i}", [128, D], BF16) for i in range(2)]
        ss = [self.sb(f"ss{i}", [128, 1], F32) for i in range(2)]
        rs = [self.sb(f"rs{i}", [128, 1], F32) for i in range(2)]
        hT = self.sb("hT", [128, 8, 512], BF16)
        gT = self.sb("gT", [128, 22, 512], BF16)
        sa = [self.sb(f"sa{i}", [128, 512], F32) for i in range(2)]
        tt = [self.sb(f"tt{i}", [128, 512], F32) for i in range(2)]
        pT = [self.ps(f"pT{i}", [128, D], BF16) for i in range(2)]
        pA = [self.ps(f"pA{i}", [128, 512]) for i in range(2)]
        pB = [self.ps(f"pB{i}", [128, 512]) for i in range(2)]
        pY = [self.ps(f"pY{i}", [128, 512]) for i in range(2)]
        cnt = {"xl": 0, "xr": 0, "nb": 0, "pT": 0, "pAB": 0, "sa": 0, "tt": 0}

        for (tag, src, dst, ntok, row) in streams:
            A, tmp = self.adaln_cols(l, j, row, f"{which}{tag}")
            sh = tmp[:, 0, :]
            self.shres = tmp.res
            G = self.gate_bc(l, j, row, f"{which}{tag}", 0.5)
            tiles = [(t0, min(512, ntok - t0)) for t0 in range(0, ntok, 512)]
            rtag = ("xs", l, which, tag)

            def prep_norm(t0, s):
                i = cnt["xl"] % 3
                cnt["xl"] += 1
                k = cnt["nb"] % 2
                cnt["nb"] += 1
                S.dma(S.sp, [(xl[i][:], src[t0 + s * 128:t0 + (s + 1) * 128, :])], reads=[self.R(src.name)],
                      writes=[xl[i].res])
                self.norm_part(xl[i], nb[k], ss[k], rs[k])
                return nb[k]

            def prep_tr(nbt, s):
                k = cnt["pT"] % 2
                cnt["pT"] += 1
                self.transpose_part(nbt, pT[k], hT, s * 128, A, sh, [S.act, S.dve])

            def prep(t0, n):
                for s in range(n // 128):
                    nbt = prep_norm(t0, s)
                    prep_tr(nbt, s)

            prep(*tiles[0])
            for ti, (t0, n) in enumerate(tiles):
                nt = n // 128
                for p in range(22):
                    k = cnt["pAB"] % 2
                    cnt["pAB"] += 1
                    for kc in range(8):
                        S.op(S.pe, lambda h, kc=kc, p=p, k=k, n=n: h.matmul(pA[k][:, :n], W13[:, kc, p * 128:(p + 1) * 128], hT[:, kc, :n],
                                                                        start=(kc == 0), stop=(kc == 7)),
                             reads=[W13.res, hT.res], writes=[pA[k].res], inc=(kc == 7))
                    for kc in range(8):
                        S.op(S.pe, lambda h, kc=kc, p=p, k=k, n=n: h.matmul(pB[k][:, :n], W13[:, kc, DFF + p * 128:DFF + (p + 1) * 128], hT[:, kc, :n],
                                                                        start=(kc == 0), stop=(kc == 7)),
                             reads=[W13.res, hT.res], writes=[pB[k].res], inc=(kc == 7))
                    q = cnt["sa"] % 2
                    cnt["sa"] += 1
                    S.op(S.act, lambda h, k=k, q=q, n=n: h.activation(out=sa[q][:, :n], in_=pA[k][:, :n], func=AF.Silu),
                         reads=[pA[k].res], writes=[sa[q].res])
                    S.op(S.dve, lambda h, k=k, q=q, p=p, n=n: h.tensor_tensor(out=gT[:, p, :n], in0=sa[q][:, :n], in1=pB[k][:, :n], op=ALU.mult),
                         reads=[sa[q].res, pB[k].res], writes=[gT.res])
                xrs = []
                for s in range(nt):
                    pass
                if ti + 1 < len(tiles):
                    pending = tiles[ti + 1]
                else:
                    pending = None
                for s in range(nt):
                    i = cnt["xr"] % 2
                    cnt["xr"] += 1
                    S.dma(S.sp, [(xr[i][:], src[t0 + s * 128:t0 + (s + 1) * 128, :])], reads=[self.R(src.name)],
                          writes=[xr[i].res])
                    for dh in range(2):
                        for fc in range(22):
                            S.op(S.pe, lambda h, fc=fc, dh=dh, s=s: h.matmul(pY[dh][:], gT[:, fc, s * 128:(s + 1) * 128], W2[:, fc, dh * 512:(dh + 1) * 512],
                                                                              start=(fc == 0), stop=(fc == 21)),
                                 reads=[gT.res, W2.res], writes=[pY[dh].res], inc=(fc == 21))
                    for dh in range(2):
                        q = cnt["tt"] % 2
                        cnt["tt"] += 1
                        S.op(S.dve, lambda h, dh=dh, q=q, G=G: h.tensor_tensor(out=tt[q][:], in0=pY[dh][:], in1=G[:, dh * 512:(dh + 1) * 512], op=ALU.mult),
                             reads=[pY[dh].res, G.res], writes=[tt[q].res])
                        S.op(S.pool, lambda h, dh=dh, q=q, i=i: h.tensor_tensor(out=xr[i][:, dh * 512:(dh + 1) * 512], in0=xr[i][:, dh * 512:(dh + 1) * 512],
                                                                                 in1=tt[q][:], op=ALU.add),
                             reads=[tt[q].res, xr[i].res], writes=[xr[i].res])
                    S.dma(S.sp, [(dst[t0 + s * 128:t0 + (s + 1) * 128, :], xr[i][:])], reads=[xr[i].res],
                          writes=[self.R(dst.name)])
                    if pending is not None and s == 0:
                        prep(*pending)


O_AQ, O_AK, O_AV = 0, 512, 640
O_GQ, O_GK, O_GV, O_GR, O_GG = 768, 1024, 1280, 1792, 2304
O_MQ, O_MK, O_MV, O_MO, O_MI, O_MF = 2336, 2848, 3360, 3872, 4384, 4392
O_SA, O_SG, O_SM = 4400, 5424, 6448


def bcast_rows(ap2d, nparts):
    return bass.AP(ap2d.tensor, ap2d.offset, [[0, nparts]] + [list(x) for x in ap2d.ap[1:]])


def rev_last(ap):
    pat = [list(x) for x in ap.ap]
    st, n = pat[-1]
    return bass.AP(ap.tensor, ap.offset + st * (n - 1), pat[:-1] + [[-st, n]])


def phase_feat(self, l, streams):
    S = self.S
    TT = self.TT
    win = self.w_in[l].rearrange("(kc p) n -> p kc n", p=128)
    Wa = self.sb("Wa", [128, 8, 768], BF16)
    Wv = self.sb("Wv", [128, 8, 1024], BF16)
    Wf = self.sb("Wf", [128, 8, 1536], BF16)
    Wg = self.sb("Wg", [128, 8, 48], BF16)
    S.dma(S.pool, [(Wa[:, 0:4, :], win[:, 0:4, 0:768]), (Wa[:, 4:8, :], win[:, 4:8, 0:768])], writes=[Wa.res])
    for k0 in range(0, 8, 2):
        S.dma(S.pool, [(Wv[:, k0:k0 + 2, 0:512], win[:, k0:k0 + 2, O_GV:O_GV + 512]),
                       (Wv[:, k0:k0 + 2, 512:1024], win[:, k0:k0 + 2, O_MV:O_MV + 512])], writes=[Wv.res])
        S.dma(S.pool, [(Wf[:, k0:k0 + 2, 0:512], win[:, k0:k0 + 2, O_GQ:O_GQ + 512]),
                       (Wf[:, k0:k0 + 2, 512:1536], win[:, k0:k0 + 2, O_MQ:O_MQ + 1024])], writes=[Wf.res])
    S.dma(S.pool, [(Wg[:, :, 0:32], win[:, :, O_GG:O_GG + 32]), (Wg[:, :, 32:48], win[:, :, O_MI:O_MI + 16])], writes=[Wg.res])
    W2p = self.sb("W2p", [32, 2, 256], F32)
    S.op(S.dve, lambda h: h.memset(W2p[:], 0.0), writes=[W2p.res])
    S.dma(S.sp, [(W2p[0:16, 0, :], self.gla_w2[l, 0]), (W2p[16:32, 1, :], self.gla_w2[l, 1])], writes=[W2p.res])
    negb = self.sb("negb", [128, 2, 2], F32)
    S.dma(S.sp, [(negb[:, d, :], self.gla_b[l, d, :].rearrange("(c p) -> p c", p=128)) for d in range(2)],
          writes=[negb.res], allow_slow_non_contiguous=True)
    S.op(S.dve, lambda h: h.tensor_scalar(out=negb[:], in0=negb[:], scalar1=-1.0, scalar2=None, op0=ALU.mult),
         reads=[negb.res], writes=[negb.res])
    gain = self.sb("gain", [128, 10, 64], F32)
    qn_src = self.attn_q_norm[l:l + 1, :]
    kn_src = self.attn_k_norm[l:l + 1, :]
    S.dma(S.sp, [(gain[:, 0:8, :], bass.AP(qn_src.tensor, qn_src.offset, [[0, 128], [0, 8], [1, 64]])),
                 (gain[:, 8:10, :], bass.AP(kn_src.tensor, kn_src.offset, [[0, 128], [0, 2], [1, 64]]))],
          writes=[gain.res])
    mask01 = self.sb("mask01", [128, 8, 64], F32)
    S.op(S.pool, lambda h: h.memset(mask01[:], 1.0), writes=[mask01.res])
    S.op(S.pool, lambda h: h.memset(mask01[:, :, 0:1], 0.0), writes=[mask01.res])
    self.junk = self.sb("junk", [128, D], BF16)

    xl = [self.sb(f"xl{i}", [128, D], F32) for i in range(2)]
    nb = [self.sb(f"nb{i}", [128, D], BF16) for i in range(2)]
    ss = [self.sb(f"ss{i}", [128, 1], F32) for i in range(2)]
    rs = [self.sb(f"rs{i}", [128, 1], F32) for i in range(2)]
    hT = self.sb("hT", [128, 8, 512], BF16)
    sqt = self.sb("sqt", [128, 640], F32)
    ssh = self.sb("ssh", [128, 10], F32)
    rinv = self.sb("rinv", [128, 10], F32)
    qn = self.sb("qn", [128, 10, 64], F32)
    rt = [self.sb(f"rt{i}", [128, 10, 32], F32) for i in range(4)]
    cs_t = [self.sb(f"cst{i}", [128, 2, 32], F32) for i in range(2)]
    qr = [self.sb(f"qr{i}", [128, 10, 64], BF16) for i in range(2)]
    vb = [self.sb(f"vb{i}", [128, 128], BF16) for i in range(2)]
    vb2 = [self.sb(f"vb2{i}", [128, 512], BF16) for i in range(2)]
    QTs = self.sb("QTs", [64, 8, 512], BF16)
    KTs = self.sb("KTs", [64, 2, 512], BF16)
    ggT = self.sb("ggT", [32, 512], F32)
    gts = self.sb("gts", [16, 512], F32)
    ex = [self.sb(f"ex{i}", [128, 512], F32) for i in range(2)]
    csum = [self.sb(f"csum{i}", [128, 512], F32) for i in range(2)]
    eb = [[self.sb(f"eb{d}{c}", [128, 512], F32) for c in range(2)] for d in range(2)]
    enb = [[self.sb(f"enb{d}{c}", [128, 512], F32) for c in range(2)] for d in range(2)]
    ebl = [[self.sb(f"ebl{d}{c}", [128, 512], F32) for c in range(2)] for d in range(2)]
    fo = [self.sb(f"fo{i}", [128, 512], BF16) for i in range(4)]
    elcs = [self.sb(f"elc{i}", [128, 8], F32) for i in range(2)]
    pT = self.ps("pT", [128, D], BF16)
    pq = self.ps("pq", [128, 512])
    pkv = self.ps("pkv", [128, 256])
    pqt = self.ps("pqt", [64, 8, 128], BF16)
    pkt = self.ps("pkt", [64, 2, 128], BF16)
    pf = [self.ps(f"pf{i}", [128, 512]) for i in range(2)]
    pz = self.ps("pz", [128, 512])
    cnt = {"xl": 0, "pf": 0, "fo": 0, "qr": 0, "vb": 0, "vb2": 0, "ex": 0, "rt": 0, "cs": 0, "elc": 0}
    H2Tv = self.H2T[l].rearrange("(kc p) t -> p kc t", p=128)

    def nxt(key, n):
        v = cnt[key] % n
        cnt[key] += 1
        return v

    for (tag, src, ntok, row, uoff, rope) in streams:
        A, tmp = self.adaln_cols(l, 1, row, f"f{tag}")
        sh = tmp[:, 0, :]
        self.shres = tmp.res
        for t0 in range(0, ntok, 512):
            n = min(512, ntok - t0)
            nt = n // 128
            u0 = uoff + t0
            nch = n // 64
            for s in range(nt):
                i = nxt("xl", 2)
                S.dma(S.sp, [(xl[i][:], src[t0 + s * 128:t0 + (s + 1) * 128, :])], reads=[self.R(src.name)], writes=[xl[i].res])
                self.norm_part(xl[i], nb[i], ss[i], rs[i])
                self.transpose_part(nb[i], pT, hT, s * 128, A, sh, [S.act, S.dve])
            S.dma(S.sp, [(H2Tv[:, :, u0:u0 + n], hT[:, :, :n])], reads=[hT.res], writes=[self.R(self.H2T[l].name)])
            for s in range(nt):
                c0 = s * 128
                for kc in range(8):
                    S.op(S.pe, lambda h, kc=kc, c0=c0: h.matmul(pq[:], hT[:, kc, c0:c0 + 128], Wa[:, kc, 0:512], start=(kc == 0), stop=(kc == 7)),
                         reads=[hT.res, Wa.res], writes=[pq.res], inc=(kc == 7))
                for kc in range(8):
                    S.op(S.pe, lambda h, kc=kc, c0=c0: h.matmul(pkv[:], hT[:, kc, c0:c0 + 128], Wa[:, kc, 512:768], start=(kc == 0), stop=(kc == 7)),
                         reads=[hT.res, Wa.res], writes=[pkv.res], inc=(kc == 7))
                S.op(S.act, lambda h: h.activation(out=sqt[:, 0:512], in_=pq[:], func=AF.Square), reads=[pq.res], writes=[sqt.res])
                S.op(S.act, lambda h: h.activation(out=sqt[:, 512:640], in_=pkv[:, 0:128], func=AF.Square), reads=[pkv.res], writes=[sqt.res])
                S.op(S.dve, lambda h: h.tensor_reduce(out=ssh[:], in_=sqt[:].rearrange("p (a b) -> p a b", b=64), axis=AX.X, op=ALU.add),
                     reads=[sqt.res], writes=[ssh.res])
                S.op(S.act, lambda h: h.activation(out=rinv[:], in_=ssh[:], func=AF.Sqrt, scale=1.0 / 64, bias=self.eps_col[:]),
                     reads=[ssh.res, self.eps_col.res], writes=[rinv.res])
                S.op(S.dve, lambda h: h.reciprocal(out=rinv[:], in_=rinv[:]), reads=[rinv.res], writes=[rinv.res])
                S.op(S.dve, lambda h: h.tensor_tensor(out=qn[:, 0:8, :], in0=pq[:].rearrange("p (a b) -> p a b", b=64),
                                                       in1=rinv[:, 0:8].unsqueeze(2).to_broadcast([128, 8, 64]), op=ALU.mult),
                     reads=[pq.res, rinv.res], writes=[qn.res])
                S.op(S.dve, lambda h: h.tensor_tensor(out=qn[:, 8:10, :], in0=pkv[:, 0:128].rearrange("p (a b) -> p a b", b=64),
                                                       in1=rinv[:, 8:10].unsqueeze(2).to_broadcast([128, 2, 64]), op=ALU.mult),
                     reads=[pkv.res, rinv.res], writes=[qn.res])
                S.op(S.pool, lambda h: h.tensor_tensor(out=qn[:], in0=qn[:], in1=gain[:], op=ALU.mult),
                     reads=[qn.res, gain.res], writes=[qn.res])
                qi = nxt("qr", 2)
                q_ = qr[qi]
                if rope:
                    cst = cs_t[nxt("cs", 2)]
                    S.dma(S.sp, [(cst[:], self.rope_cs[t0 + c0:t0 + c0 + 128, :, :])], writes=[cst.res])
                    cosb = cst[:, 0:1, :].to_broadcast([128, 10, 32])
                    sinb = cst[:, 1:2, :].to_broadcast([128, 10, 32])
                    x1 = qn[:, :, 0:32]
                    x2 = qn[:, :, 32:64]
                    r = [rt[nxt("rt", 4)] for _ in range(4)]
                    S.op(S.pool, lambda h, r=r, cosb=cosb, x1=x1: h.tensor_tensor(out=r[0][:], in0=x1, in1=cosb, op=ALU.mult),
                         reads=[qn.res, cst.res], writes=[r[0].res])
                    S.op(S.pool, lambda h, r=r, sinb=sinb, x2=x2: h.tensor_tensor(out=r[1][:], in0=x2, in1=sinb, op=ALU.mult),
                         reads=[qn.res, cst.res], writes=[r[1].res])
                    S.op(S.dve, lambda h, r=r, q_=q_: h.tensor_tensor(out=q_[:, :, 0:32], in0=r[0][:], in1=r[1][:], op=ALU.subtract),
                         reads=[r[0].res, r[1].res], writes=[q_.res])
                    S.op(S.pool, lambda h, r=r, sinb=sinb, x1=x1: h.tensor_tensor(out=r[2][:], in0=x1, in1=sinb, op=ALU.mult),
                         reads=[qn.res, cst.res], writes=[r[2].res])
                    S.op(S.pool, lambda h, r=r, cosb=cosb, x2=x2: h.tensor_tensor(out=r[3][:], in0=x2, in1=cosb, op=ALU.mult),
                         reads=[qn.res, cst.res], writes=[r[3].res])
                    S.op(S.dve, lambda h, r=r, q_=q_: h.tensor_tensor(out=q_[:, :, 32:64], in0=r[2][:], in1=r[3][:], op=ALU.add),
                         reads=[r[2].res, r[3].res], writes=[q_.res])
                else:
                    S.op(S.dve, lambda h, q_=q_: h.tensor_copy(out=q_[:], in_=qn[:]), reads=[qn.res], writes=[q_.res])
                vi = nxt("vb", 2)
                S.op(S.act, lambda h, vi=vi: h.activation(out=vb[vi][:], in_=pkv[:, 128:256], func=AF.Copy), reads=[pkv.res], writes=[vb[vi].res])
                S.dma(S.sp, [(self.VA[l][u0 + c0:u0 + c0 + 128, :], vb[vi][:])], reads=[vb[vi].res], writes=[self.R(self.VA[l].name)])
                for hh in range(8):
                    S.op(S.pe, lambda h, hh=hh, q_=q_: h.transpose(out=pqt[:, hh, :], in_=q_[:, hh, :], identity=self.ident[:]),
                         reads=[q_.res, self.ident.res], writes=[pqt.res], inc=(hh == 7))
                for hh in range(2):
                    S.op(S.pe, lambda h, hh=hh, q_=q_: h.transpose(out=pkt[:, hh, :], in_=q_[:, 8 + hh, :], identity=self.ident[:]),
                         reads=[q_.res, self.ident.res], writes=[pkt.res], inc=(hh == 1))
                S.op(S.act, lambda h, c0=c0: h.activation(out=QTs[:, :, c0:c0 + 128], in_=pqt[:], func=AF.Copy), reads=[pqt.res], writes=[QTs.res])
                S.op(S.dve, lambda h, c0=c0: h.tensor_copy(out=KTs[:, :, c0:c0 + 128], in_=pkt[:]), reads=[pkt.res], writes=[KTs.res])
            S.dma(S.sp, [(self.QT[l][:, :, u0:u0 + n], QTs[:, :, :n])], reads=[QTs.res], writes=[self.R(self.QT[l].name)])
            S.dma(S.sp, [(self.KT[l][:, :, u0:u0 + n], KTs[:, :, :n])], reads=[KTs.res], writes=[self.R(self.KT[l].name)])
            for s in range(nt):
                c0 = s * 128
                for half in range(2):
                    k = nxt("pf", 2)
                    for kc in range(8):
                        S.op(S.pe, lambda h, kc=kc, c0=c0, k=k, half=half: h.matmul(pf[k][:], hT[:, kc, c0:c0 + 128], Wv[:, kc, half * 512:(half + 1) * 512],
                                                                                   start=(kc == 0), stop=(kc == 7)),
                             reads=[hT.res, Wv.res], writes=[pf[k].res], inc=(kc == 7))
                    vi = nxt("vb2", 2)
                    eng = S.act if half == 0 else S.dve
                    if half == 0:
                        S.op(S.act, lambda h, k=k, vi=vi: h.activation(out=vb2[vi][:], in_=pf[k][:], func=AF.Copy), reads=[pf[k].res], writes=[vb2[vi].res])
                    else:
                        S.op(S.dve, lambda h, k=k, vi=vi: h.tensor_copy(out=vb2[vi][:], in_=pf[k][:]), reads=[pf[k].res], writes=[vb2[vi].res])
                    dstv = self.GV[l] if half == 0 else self.MV[l]
                    S.dma(S.sp, [(dstv[u0 + c0:u0 + c0 + 128, :], vb2[vi][:])], reads=[vb2[vi].res], writes=[self.R(dstv.name)])
            for kc in range(8):
                S.op(S.pe, lambda h, kc=kc, n=n: h.matmul(pz[0:32, :n], Wg[:, kc, 0:32], hT[:, kc, :n], start=(kc == 0), stop=(kc == 7)),
                     reads=[hT.res, Wg.res], writes=[pz.res], inc=(kc == 7))
            S.op(S.act, lambda h, n=n: h.activation(out=ggT[:, :n], in_=pz[0:32, :n], func=AF.Copy), reads=[pz.res], writes=[ggT.res])
            for kc in range(8):
                S.op(S.pe, lambda h, kc=kc, n=n: h.matmul(pz[0:16, :n], Wg[:, kc, 32:48], hT[:, kc, :n], start=(kc == 0), stop=(kc == 7)),
                     reads=[hT.res, Wg.res], writes=[pz.res], inc=(kc == 7))
            S.op(S.act, lambda h, n=n: h.activation(out=gts[:, :n], in_=pz[0:16, :n], func=AF.Copy), reads=[pz.res], writes=[gts.res])
            S.dma(S.sp, [(self.GATES[l][:, u0:u0 + n], gts[:, :n])], reads=[gts.res], writes=[self.R(self.GATES[l].name)])
            for d in range(2):
                for c2 in range(2):
                    S.op(S.pe, lambda h, d=d, c2=c2, n=n: h.matmul(pz[:, :n], W2p[:, d, c2 * 128:(c2 + 1) * 128], ggT[:, :n], start=True, stop=True),
                         reads=[W2p.res, ggT.res], writes=[pz.res])
                    e_ = ex[nxt("ex", 2)]
                    c_ = csum[(cnt["ex"]) % 2]
                    S.op(S.act, lambda h, d=d, c2=c2, n=n, e_=e_: h.activation(out=e_[:, :n], in_=pz[:, :n], func=AF.Exp, scale=-1.0, bias=negb[:, d, c2:c2 + 1]),
                         reads=[pz.res, negb.res], writes=[e_.res])
                    S.op(S.act, lambda h, n=n, e_=e_: h.activation(out=e_[:, :n], in_=e_[:, :n], func=AF.Ln, bias=1.0), reads=[e_.res], writes=[e_.res])
                    m01 = mask01[:].rearrange("p a b -> p (a b)")[:, :n]
                    if d == 0:
                        S.op(S.dve, lambda h, n=n, e_=e_, c_=c_, m01=m01: h.tensor_tensor_scan(out=c_[:, :n], data0=m01, data1=e_[:, :n], initial=0.0, op0=ALU.mult, op1=ALU.add),
                             reads=[e_.res, mask01.res], writes=[c_.res])
                        last = 63
                    else:
                        S.op(S.dve, lambda h, n=n, e_=e_, c_=c_, m01=m01: h.tensor_tensor_scan(out=rev_last(c_[:, :n]), data0=m01, data1=rev_last(e_[:, :n]), initial=0.0,
                                                                                             op0=ALU.mult, op1=ALU.add),
                             reads=[e_.res, mask01.res], writes=[c_.res])
                        last = 0
                    EB, ENB, EBL = eb[d][c2], enb[d][c2], ebl[d][c2]
                    S.op(S.act, lambda h, n=n, c_=c_, EB=EB: h.activation(out=EB[:, :n], in_=c_[:, :n], func=AF.Exp, scale=-1.0 / 16), reads=[c_.res], writes=[EB.res])
                    S.op(S.act, lambda h, n=n, c_=c_, ENB=ENB: h.activation(out=ENB[:, :n], in_=c_[:, :n], func=AF.Exp, scale=1.0 / 16), reads=[c_.res], writes=[ENB.res])
                    c3 = c_[:, :n].rearrange("p (a b) -> p a b", b=64)
                    S.op(S.dve, lambda h, n=n, c_=c_, c3=c3, last=last, nch=nch: h.tensor_tensor(out=c3, in0=c3, in1=c3[:, :, last:last + 1].to_broadcast([128, nch, 64]), op=ALU.subtract),
                         reads=[c_.res], writes=[c_.res])
                    S.op(S.act, lambda h, n=n, c_=c_, EBL=EBL: h.activation(out=EBL[:, :n], in_=c_[:, :n], func=AF.Exp, scale=1.0 / 16), reads=[c_.res], writes=[EBL.res])
                    ch0 = u0 // 64
                    elc = elcs[nxt("elc", 2)]
                    S.op(S.pool, lambda h, n=n, EB=EB, last=last, elc=elc, nch=nch: h.tensor_copy(
                        out=elc[:, :nch], in_=EB[:, :n].rearrange("p (a b) -> p a b", b=64)[:, :, last]),
                         reads=[EB.res], writes=[elc.res])
                    S.dma(S.sp, [(self.EL[:, 2 * c2 + hh2, d, ch0:ch0 + nch], elc[hh2 * 64:(hh2 + 1) * 64, :nch]) for hh2 in range(2)],
                          reads=[elc.res], writes=[self.EL.res])
            for fc in range(12):
                k = nxt("pf", 2)
                for kc in range(8):
                    S.op(S.pe, lambda h, kc=kc, fc=fc, k=k, n=n: h.matmul(pf[k][:, :n], Wf[:, kc, fc * 128:(fc + 1) * 128], hT[:, kc, :n], start=(kc == 0), stop=(kc == 7)),
                         reads=[hT.res, Wf.res], writes=[pf[k].res], inc=(kc == 7))
                if fc < 2:
                    for d in range(2):
                        o_ = fo[nxt("fo", 4)]
                        S.op(S.dve, lambda h, k=k, n=n, d=d, fc=fc, o_=o_: h.scalar_tensor_tensor(out=o_[:, :n], in0=pf[k][:, :n], scalar=0.125, in1=eb[d][fc][:, :n],
                                                                                              op0=ALU.mult, op1=ALU.mult),
                             reads=[pf[k].res, eb[d][fc].res], writes=[o_.res])
                        S.dma(S.sp, [(self.QG[l][d, :, 2 * fc + hh2, u0:u0 + n], o_[hh2 * 64:(hh2 + 1) * 64, :n]) for hh2 in range(2)], reads=[o_.res], writes=[self.R(self.QG[l].name)])
                elif fc < 4:
                    c2 = fc - 2
                    for d in range(2):
                        o_ = fo[nxt("fo", 4)]
                        S.op(S.dve, lambda h, k=k, n=n, d=d, c2=c2, o_=o_: h.tensor_tensor(out=o_[:, :n], in0=pf[k][:, :n], in1=enb[d][c2][:, :n], op=ALU.mult),
                             reads=[pf[k].res, enb[d][c2].res], writes=[o_.res])
                        S.dma(S.sp, [(self.KG[l][d, :, 2 * c2 + hh2, u0:u0 + n], o_[hh2 * 64:(hh2 + 1) * 64, :n]) for hh2 in range(2)], reads=[o_.res], writes=[self.R(self.KG[l].name)])
                        o_ = fo[nxt("fo", 4)]
                        S.op(S.dve, lambda h, k=k, n=n, d=d, c2=c2, o_=o_: h.tensor_tensor(out=o_[:, :n], in0=pf[k][:, :n], in1=ebl[d][c2][:, :n], op=ALU.mult),
                             reads=[pf[k].res, ebl[d][c2].res], writes=[o_.res])
                        S.dma(S.sp, [(self.KH[l][d, :, 2 * c2 + hh2, u0:u0 + n], o_[hh2 * 64:(hh2 + 1) * 64, :n]) for hh2 in range(2)], reads=[o_.res], writes=[self.R(self.KH[l].name)])
                else:
                    o_ = fo[nxt("fo", 4)]
                    S.op(S.act, lambda h, k=k, n=n, o_=o_: h.activation(out=o_[:, :n], in_=pf[k][:, :n], func=AF.Copy), reads=[pf[k].res], writes=[o_.res])
                    r0 = (fc - 4) * 128
                    S.dma(S.sp, [(self.MQK[l][r0:r0 + 128, 2 + u0:2 + u0 + n], o_[:, :n])], reads=[o_.res], writes=[self.R(self.MQK[l].name)])


Builder.phase_feat = phase_feat


def phase_attn(self, l, do_ctx):
    S = self.S
    T = self.T
    nbk = T // 128
    ones = self.sb("ones", [128, 128], F32)
    S.op(S.pool, lambda h: h.memset(ones[:], 1.0), writes=[ones.res])
    mP = self.sb("mP", [128, 128], BF16)
    mN = self.sb("mN", [128, 128], BF16)
    mtmp = self.sb("mtmp", [128, 128], F32)
    for (m_, sgn) in ((mP, 1), (mN, -1)):
        S.op(S.pool, lambda h, sgn=sgn: h.affine_select(out=mtmp[:], in_=ones[:], pattern=[[-sgn, 128]], compare_op=ALU.is_ge, fill=0.0,
                                                         base=0, channel_multiplier=sgn), reads=[ones.res], writes=[mtmp.res])
        S.op(S.pool, lambda h, m_=m_: h.tensor_copy(out=m_[:], in_=mtmp[:]), reads=[mtmp.res], writes=[m_.res])
    esk = self.sb("esk", [128, 2, 4, 128], F32)
    sk8 = self.sb("sk8", [128, 8], F32)
    S.dma(S.sp, [(sk8[64:65, :], self.attn_sink[l:l + 1, :])], writes=[sk8.res])
    S.op(S.act, lambda h: h.activation(out=sk8[64:65, :], in_=sk8[64:65, :], func=AF.Exp), reads=[sk8.res], writes=[sk8.res])
    S.op(S.dve, lambda h: h.tensor_copy(out=esk[64:65].rearrange("p g a b -> p (g a) b"), in_=sk8[64:65, :].unsqueeze(2).to_broadcast([1, 8, 128])),
         reads=[sk8.res], writes=[esk.res])
    KTc = self.sb("KTc", [64, 2, 256], BF16)
    S.dma(S.sp, [(KTc[:], self.KT[l][:, :, 0:256])], reads=[self.R(self.KT[l].name)], writes=[KTc.res])
    Vc = [self.sb(f"Vc{j}", [128, 2, 65], BF16) for j in range(2)]
    Vb = [self.sb(f"Vb{j}", [128, 2, 65], BF16) for j in range(4)]
    KTb = [self.sb(f"KTb{j}", [64, 2, 128], BF16) for j in range(4)]
    for v in Vc + Vb:
        S.op(S.pool, lambda h, v=v: h.memset(v[:], 1.0), writes=[v.res])
    for j in range(2):
        S.dma(S.sp, [(Vc[j][:, :, 0:64], self.VA[l][j * 128:(j + 1) * 128, :].rearrange("p (g d) -> p g d", d=64))],
              reads=[self.R(self.VA[l].name)], writes=[Vc[j].res])
    QTb = [self.sb(f"QTb{j}", [64, 8, 128], BF16) for j in range(2)]
    E = [self.sb(f"E{j}", [128, 4, 128], BF16) for j in range(3)]
    dn = [self.sb(f"dn{j}", [128, 512], F32) for j in range(2)]
    bcs = [self.sb(f"bcs{j}", [64, 512], F32) for j in range(2)]
    aT = [self.sb(f"aT{j}", [64, 4, 128], BF16) for j in range(2)]
    pS = [self.ps(f"pS{j}", [128, 512]) for j in range(2)]
    pO = [self.ps(f"pO{j}", [128, 512]) for j in range(2)]
    pB = [self.ps(f"pB{j}", [64, 512]) for j in range(2)]
    cnt = {}

    def nxt(key, n):
        v = cnt.get(key, 0)
        cnt[key] = v + 1
        return v % n

    def load_kb(m):
        i = m % 4
        u = LC + m * 128
        S.dma(S.sp, [(KTb[i][:], self.KT[l][:, :, u:u + 128])], reads=[self.R(self.KT[l].name)], writes=[KTb[i].res])
        S.dma(S.sp, [(Vb[i][:, :, 0:64], self.VA[l][u:u + 128, :].rearrange("p (g d) -> p g d", d=64))],
              reads=[self.R(self.VA[l].name)], writes=[Vb[i].res])

    def qblock(u0, kbs):
        qi = nxt("q", 2)
        Q = QTb[qi]
        S.dma(S.sp, [(Q[:], self.QT[l][:, :, u0:u0 + 128])], reads=[self.R(self.QT[l].name)], writes=[Q.res])
        for g in range(2):
            po = pO[nxt("po", 2)]
            rhsq = Q[:, 4 * g:4 * g + 4, :].rearrange("p a b -> p (a b)")

            def score(idx):
                kt, vt, msk = kbs[idx]
                p = pS[idx % 2]
                S.op(S.pe, lambda h, kt=kt, p=p, g=g, rhsq=rhsq: h.matmul(p[:], kt[0][:, g, kt[1]:kt[1] + 128], rhsq, start=True, stop=True),
                     reads=[kt[0].res, Q.res], writes=[p.res])
            score(0)
            for idx in range(len(kbs)):
                kt, vt, msk = kbs[idx]
                if idx + 1 < len(kbs):
                    score(idx + 1)
                p = pS[idx % 2]
                e = E[nxt("e", 3)]
                S.op(S.act, lambda h, p=p, e=e: h.activation(out=e[:].rearrange("p a b -> p (a b)"), in_=p[:], func=AF.Exp, scale=0.125),
                     reads=[p.res], writes=[e.res])
                if msk is not None:
                    S.op(S.pool, lambda h, e=e, msk=msk: h.tensor_tensor(out=e[:], in0=e[:], in1=msk[:].unsqueeze(1).to_broadcast([128, 4, 128]), op=ALU.mult),
                         reads=[e.res, msk.res], writes=[e.res])
                S.op(S.pe, lambda h, e=e, vt=vt, po=po, idx=idx, g=g, kbs=kbs: h.matmul(po[0:65, :], vt[:, g, :], e[:].rearrange("p a b -> p (a b)"),
                                                                         start=(idx == 0), stop=(idx == len(kbs) - 1)),
                     reads=[e.res, vt.res], writes=[po.res], inc=(idx == len(kbs) - 1))
            d_ = dn[nxt("dn", 2)]
            S.op(S.dve, lambda h, d_=d_, po=po, g=g: h.tensor_tensor(out=d_[64:65, :], in0=po[64:65, :], in1=esk[64:65, g].rearrange("p a b -> p (a b)"), op=ALU.add),
                 reads=[po.res, esk.res], writes=[d_.res])
            S.op(S.dve, lambda h, d_=d_: h.reciprocal(out=d_[64:65, :], in_=d_[64:65, :]), reads=[d_.res], writes=[d_.res])
            pb = pB[nxt("pb", 2)]
            S.op(S.pe, lambda h, d_=d_, pb=pb: h.matmul(pb[:], ones[64:65, 0:64], d_[64:65, :], start=True, stop=True),
                 reads=[d_.res, ones.res], writes=[pb.res])
            b_ = bcs[nxt("bcs", 2)]
            S.op(S.act, lambda h, b_=b_, pb=pb: h.activation(out=b_[:], in_=pb[:], func=AF.Copy), reads=[pb.res], writes=[b_.res])
            a_ = aT[nxt("aT", 2)]
            S.op(S.dve, lambda h, a_=a_, b_=b_, po=po: h.tensor_tensor(out=a_[:].rearrange("p a b -> p (a b)"), in0=po[0:64, :], in1=b_[:], op=ALU.mult),
                 reads=[po.res, b_.res], writes=[a_.res])
            S.dma(S.sp, [(self.ATT[l][:, 4 * g:4 * g + 4, u0:u0 + 128], a_[:])], reads=[a_.res], writes=[self.R(self.ATT[l].name)])

    ckb = [((KTc, 0), Vc[0], None), ((KTc, 128), Vc[1], None)]
    if do_ctx:
        for n in range(2):
            qblock(n * 128, ckb)
    load_kb(0)
    for n in range(nbk):
        if n + 1 < nbk:
            load_kb(n + 1)
        kbs = []
        if n - 1 >= 0:
            kbs.append(((KTb[(n - 1) % 4], 0), Vb[(n - 1) % 4], mP))
        kbs.append(((KTb[n % 4], 0), Vb[n % 4], None))
        if n + 1 < nbk:
            kbs.append(((KTb[(n + 1) % 4], 0), Vb[(n + 1) % 4], mN))
        qblock(LC + n * 128, kbs + ckb)


Builder.phase_attn = phase_attn


def scan_groups(T):
    return [(0, LC)] + [(LC + t0, min(512, T - t0)) for t0 in range(0, T, 512)]


def scan_order(T, d):
    groups = scan_groups(T)
    order = []
    if d == 0:
        for gi, (u0, n) in enumerate(groups):
            for c in range(n // 64):
                order.append((gi, c))
    else:
        gis = [0] + list(range(len(groups) - 1, 0, -1))
        for gi in gis:
            u0, n = groups[gi]
            for c in range(n // 64 - 1, -1, -1):
                order.append((gi, c))
    return order


def phase_gla(self, l):
    S = self.S
    T = self.T
    EL = self.EL
    groups = scan_groups(T)
    ones = self.sb("ones", [64, 64], F32)
    S.op(S.pool, lambda h: h.memset(ones[:], 1.0), writes=[ones.res])
    mtmp = self.sb("mtmp", [64, 64], F32)
    msk = [self.sb(f"msk{d}", [64, 64], BF16) for d in range(2)]
    for d, sgn in ((0, -1), (1, 1)):
        S.op(S.pool, lambda h, sgn=sgn: h.affine_select(out=mtmp[:], in_=ones[:], pattern=[[-sgn, 64]], compare_op=ALU.is_ge, fill=0.0,
                                                         base=0, channel_multiplier=sgn), reads=[ones.res], writes=[mtmp.res])
        S.op(S.pool, lambda h, d=d: h.tensor_copy(out=msk[d][:], in_=mtmp[:]), reads=[mtmp.res], writes=[msk[d].res])
    Sf = [self.sb(f"Sf{d}", [64, 4, 128], F32) for d in range(2)]
    Sb = [self.sb(f"Sb{d}", [64, 4, 128], BF16) for d in range(2)]
    for d in range(2):
        S.op(S.pool, lambda h, d=d: h.memset(Sf[d][:], 0.0), writes=[Sf[d].res])
        S.op(S.pool, lambda h, d=d: h.memset(Sb[d][:], 0.0), writes=[Sb[d].res])
    qg = [[self.sb(f"qg{d}{i}", [64, 4, 512], BF16) for i in range(2)] for d in range(2)]
    kg = [[self.sb(f"kg{d}{i}", [64, 4, 512], BF16) for i in range(2)] for d in range(2)]
    kh = [[self.sb(f"kh{d}{i}", [64, 4, 512], BF16) for i in range(2)] for d in range(2)]
    vg = [[self.sb(f"vg{d}{i}", [64, 8, 512], BF16) for i in range(2)] for d in range(2)]
    am = [[self.sb(f"am{d}{i}", [64, 4, 64], BF16) for i in range(2)] for d in range(2)]
    kt = [[self.sb(f"kt{d}{i}", [64, 4, 64], BF16) for i in range(2)] for d in range(2)]
    ob = [[self.sb(f"ob{d}{i}", [64, 512], F32) for i in range(2)] for d in range(2)]
    pA = [self.ps(f"pA{d}", [64, 256]) for d in range(2)]
    pK = [self.ps(f"pK{d}", [64, 256], BF16) for d in range(2)]
    pO = [self.ps(f"pO{d}", [64, 512]) for d in range(2)]
    pN = [self.ps(f"pN{d}", [64, 512]) for d in range(2)]
    orders = [scan_order(T, d) for d in range(2)]
    nsteps = len(orders[0])
    gcount = [0, 0]
    cur = [None, None]

    def load_group(d, gi):
        i = gcount[d] % 2
        gcount[d] += 1
        u0, n = groups[gi]
        nch = n // 64
        for (dst, srcT) in ((qg[d][i], self.QG[l]), (kg[d][i], self.KG[l]), (kh[d][i], self.KH[l])):
            S.dma(S.sp, [(dst[:, :, :n], srcT[d, :, :, u0:u0 + n])], reads=[self.R(srcT.name)], writes=[dst.res])
        S.dma(S.sp, [(vg[d][i][:, :nch, :], self.GV[l][u0:u0 + n, :].rearrange("(c p) f -> p c f", p=64))], reads=[self.R(self.GV[l].name)],
              writes=[vg[d][i].res])
        return i

    for step in range(nsteps):
        for d in range(2):
            gi, c = orders[d][step]
            if cur[d] is None or cur[d][0] != gi:
                cur[d] = (gi, load_group(d, gi))
            bi = cur[d][1]
            u0, n = groups[gi]
            o = c * 64
            chunk = (u0 + o) // 64
            Q, Kg, Kh, V = qg[d][bi], kg[d][bi], kh[d][bi], vg[d][bi]
            k2 = step % 2
            AM, KTt, OB = am[d][k2], kt[d][k2], ob[d][k2]
            for hh in range(4):
                S.op(S.pe, lambda h, hh=hh, d=d, Kg=Kg, Q=Q, o=o: h.matmul(pA[d][:, hh * 64:(hh + 1) * 64], Kg[:, hh, o:o + 64], Q[:, hh, o:o + 64], start=True, stop=True),
                     reads=[Kg.res, Q.res], writes=[pA[d].res], inc=(hh == 3))
            for hh in range(4):
                S.op(S.pe, lambda h, hh=hh, d=d, Kh=Kh, o=o: h.transpose(out=pK[d][:, hh * 64:(hh + 1) * 64], in_=Kh[:, hh, o:o + 64], identity=self.ident[0:64, 0:64]),
                     reads=[Kh.res, self.ident.res], writes=[pK[d].res], inc=(hh == 3))
            S.op(S.dve, lambda h, d=d, AM=AM: h.tensor_tensor(out=AM[:], in0=pA[d][:].rearrange("p (a b) -> p a b", b=64),
                                                              in1=msk[d][:].unsqueeze(1).to_broadcast([64, 4, 64]), op=ALU.mult),
                 reads=[pA[d].res, msk[d].res], writes=[AM.res])
            S.op(S.act, lambda h, d=d, KTt=KTt: h.activation(out=KTt[:].rearrange("p a b -> p (a b)"), in_=pK[d][:], func=AF.Copy), reads=[pK[d].res], writes=[KTt.res])
            for hh in range(4):
                S.op(S.pe, lambda h, hh=hh, d=d, AM=AM, V=V, c=c: h.matmul(pO[d][:, hh * 128:(hh + 1) * 128], AM[:, hh, :], V[:, c, hh * 128:(hh + 1) * 128], start=True, stop=False),
                     reads=[AM.res, V.res], writes=[pO[d].res], inc=False)
                S.op(S.pe, lambda h, hh=hh, d=d, Q=Q, o=o: h.matmul(pO[d][:, hh * 128:(hh + 1) * 128], Q[:, hh, o:o + 64], Sb[d][:, hh, :], start=False, stop=True),
                     reads=[Q.res, Sb[d].res], writes=[pO[d].res], inc=(hh == 3))
            for hh in range(4):
                S.op(S.pe, lambda h, hh=hh, d=d, KTt=KTt, V=V, c=c: h.matmul(pN[d][:, hh * 128:(hh + 1) * 128], KTt[:, hh, :], V[:, c, hh * 128:(hh + 1) * 128], start=True, stop=True),
                     reads=[KTt.res, V.res], writes=[pN[d].res], inc=(hh == 3))
            S.op(S.act, lambda h, d=d, OB=OB: h.activation(out=OB[:], in_=pO[d][:], func=AF.Copy), reads=[pO[d].res], writes=[OB.res])
            S.dma(S.sp, [(self.OG[l][d, u0 + o:u0 + o + 64, :], OB[:])], reads=[OB.res], writes=[self.R(self.OG[l].name)])
            S.op(S.dve, lambda h, d=d, chunk=chunk: h.tensor_tensor(out=Sf[d][:], in0=Sf[d][:], in1=EL[:, :, d, chunk:chunk + 1].to_broadcast([64, 4, 128]), op=ALU.mult),
                 reads=[Sf[d].res, self.EL.res], writes=[Sf[d].res])
            S.op(S.dve, lambda h, d=d: h.tensor_tensor(out=Sf[d][:].rearrange("p a b -> p (a b)"), in0=Sf[d][:].rearrange("p a b -> p (a b)"), in1=pN[d][:], op=ALU.add),
                 reads=[Sf[d].res, pN[d].res], writes=[Sf[d].res])
            S.op(S.act, lambda h, d=d: h.activation(out=Sb[d][:], in_=Sf[d][:], func=AF.Copy), reads=[Sf[d].res], writes=[Sb[d].res])


Builder.phase_gla = phase_gla


LN_KS = float(-0.5 * np.log(128.0))


def phase_ml_gates(self, l):
    S = self.S
    sel = self.sel
    DEC = self.DEC
    T = self.T
    TT = self.TT
    nch = TT // 64
    bA = self.sb("bA", [4, TT], F32)
    bL = self.sb("bL", [4, TT], F32)
    bC = self.sb("bC", [4, TT], F32)
    bG = self.sb("bG", [4, TT], F32)
    bX = self.sb("bX", [4, TT], F32)
    onesr = self.sb("onesr", [4, TT], BF16)
    S.op(S.pool, lambda h: h.memset(onesr[:], 1.0), writes=[onesr.res])
    gl = self.sb("gl", [4, nch], F32)
    gp = self.sb("gp", [4, nch], F32)
    dd = self.sb("dd", [4, nch], F32)
    ibc = self.sb("ibc", [4, 2], F32)
    pD = self.ps("pD", [128, 512])
    for d in range(2):
        S.dma(S.sp, [(bA[:], self.GATES[l][d * 4:(d + 1) * 4, :])], reads=[self.R(self.GATES[l].name)], writes=[bA.res])
        S.dma(S.sp, [(bL[:], self.GATES[l][8 + d * 4:8 + (d + 1) * 4, :])], reads=[self.R(self.GATES[l].name)], writes=[bL.res])
        S.dma(S.sp, [(ibc[:, 0:1], self.mlstm_ib[l, d, :].rearrange("(h o) -> h o", o=1)), (ibc[:, 1:2], self.mlstm_fb[l, d, :].rearrange("(h o) -> h o", o=1))],
              writes=[ibc.res])
        S.op(S.dve, lambda h: h.tensor_scalar(out=ibc[:, 1:2], in0=ibc[:, 1:2], scalar1=-1.0, scalar2=None, op0=ALU.mult), reads=[ibc.res], writes=[ibc.res])
        S.op(S.act, lambda h: h.activation(out=bL[:], in_=bL[:], func=AF.Exp, scale=-1.0, bias=ibc[:, 1:2]), reads=[bL.res, ibc.res], writes=[bL.res])
        S.op(S.act, lambda h: h.activation(out=bL[:], in_=bL[:], func=AF.Ln, bias=1.0), reads=[bL.res], writes=[bL.res])

        def scan(out, src, op1):
            if d == 0:
                S.op(S.dve, lambda h: h.tensor_tensor_scan(out=out[:], data0=onesr[:], data1=src[:], initial=0.0, op0=ALU.mult, op1=op1),
                     reads=[src.res, onesr.res], writes=[out.res])
            else:
                S.op(S.dve, lambda h: h.tensor_tensor_scan(out=rev_last(out[:, 0:LC]), data0=onesr[:, 0:LC], data1=rev_last(src[:, 0:LC]), initial=0.0, op0=ALU.mult, op1=op1),
                     reads=[src.res, onesr.res], writes=[out.res])
                S.op(S.dve, lambda h: h.tensor_tensor_scan(out=rev_last(out[:, LC:TT]), data0=onesr[:, LC:TT], data1=rev_last(src[:, LC:TT]), initial=out[:, 0:1], op0=ALU.mult, op1=op1),
                     reads=[src.res, onesr.res, out.res], writes=[out.res])
        scan(bC, bL, ALU.add)
        S.op(S.dve, lambda h: h.scalar_tensor_tensor(out=bA[:], in0=bA[:], scalar=ibc[:, 0:1], in1=bC[:], op0=ALU.add, op1=ALU.add),
             reads=[bA.res, ibc.res, bC.res], writes=[bA.res])
        scan(bG, bA, ALU.max)
        G3 = bG[:].rearrange("p (c b) -> p c b", b=64)
        lastpos = 63 if d == 0 else 0
        S.op(S.dve, lambda h, lastpos=lastpos, G3=G3: h.tensor_copy(out=gl[:], in_=G3[:, :, lastpos]), reads=[bG.res], writes=[gl.res])
        S.op(S.dve, lambda h: h.memset(gp[:], 0.0), writes=[gp.res])
        if d == 0:
            S.op(S.dve, lambda h: h.tensor_copy(out=gp[:, 1:nch], in_=gl[:, 0:nch - 1]), reads=[gl.res], writes=[gp.res])
        else:
            S.op(S.dve, lambda h: h.tensor_copy(out=gp[:, 0:3], in_=gl[:, 1:4]), reads=[gl.res], writes=[gp.res])
            S.op(S.dve, lambda h: h.tensor_copy(out=gp[:, 4:nch - 1], in_=gl[:, 5:nch]), reads=[gl.res], writes=[gp.res])
            S.op(S.dve, lambda h: h.tensor_copy(out=gp[:, nch - 1:nch], in_=gl[:, 0:1]), reads=[gl.res], writes=[gp.res])
        L3 = bL[:].rearrange("p (c b) -> p c b", b=64)
        X3 = bX[:].rearrange("p (c b) -> p c b", b=64)
        A3 = bA[:].rearrange("p (c b) -> p c b", b=64)
        S.op(S.dve, lambda h, L3=L3, G3=G3: h.tensor_tensor(out=L3, in0=gp[:].unsqueeze(2).to_broadcast([4, nch, 64]), in1=G3, op=ALU.subtract),
             reads=[gp.res, bG.res], writes=[bL.res])
        S.op(S.act, lambda h: h.activation(out=bL[:], in_=bL[:], func=AF.Exp), reads=[bL.res], writes=[bL.res])
        S.op(S.dve, lambda h: h.tensor_tensor(out=bC[:], in0=bC[:], in1=bG[:], op=ALU.subtract), reads=[bC.res, bG.res], writes=[bC.res])
        S.op(S.act, lambda h: h.activation(out=bC[:], in_=bC[:], func=AF.Exp), reads=[bC.res], writes=[bC.res])
        S.op(S.dve, lambda h, X3=X3, A3=A3: h.tensor_tensor(out=X3, in0=A3, in1=gl[:].unsqueeze(2).to_broadcast([4, nch, 64]), op=ALU.subtract),
             reads=[bA.res, gl.res], writes=[bX.res])
        S.op(S.dve, lambda h: h.tensor_scalar(out=bX[:], in0=bX[:], scalar1=LN_KS, scalar2=None, op0=ALU.add), reads=[bX.res], writes=[bX.res])
        S.op(S.act, lambda h: h.activation(out=bX[:], in_=bX[:], func=AF.Exp), reads=[bX.res], writes=[bX.res])
        S.op(S.dve, lambda h: h.tensor_tensor(out=dd[:], in0=gp[:], in1=gl[:], op=ALU.subtract), reads=[gp.res, gl.res], writes=[dd.res])
        S.op(S.act, lambda h: h.activation(out=dd[:], in_=dd[:], func=AF.Exp), reads=[dd.res], writes=[dd.res])
        for hh in range(4):
            S.op(S.pe, lambda h, hh=hh: h.matmul(pD[:, :nch], sel[:, hh, :], dd[:], start=True, stop=True), reads=[self.sel.res, dd.res], writes=[pD.res])
            S.op(S.dve, lambda h, hh=hh, d=d: h.tensor_copy(out=DEC[:, d, hh, :], in_=pD[:, :nch]), reads=[pD.res], writes=[self.DEC.res])
        for qi, buf in enumerate((bA, bG, bL, bC, bX)):
            S.dma(S.sp, [(self.MROWS[l][d, qi, :, :], buf[:])], reads=[buf.res], writes=[self.R(self.MROWS[l].name)])


def phase_ml_conv(self, l):
    S = self.S
    T = self.T
    TT = self.TT
    wcol = self.sb("wcol", [128, 8, 5], F32)
    cb = self.sb("cb", [128, 8], F32)
    S.dma(S.sp, [(wcol[:, :, k], self.conv_w[l, k, :].rearrange("(fc p) -> p fc", p=128)) for k in range(5)], writes=[wcol.res], allow_slow_non_contiguous=True)
    S.dma(S.sp, [(cb[:], self.conv_b[l, :].rearrange("(fc p) -> p fc", p=128))], writes=[cb.res], allow_slow_non_contiguous=True)
    diagw = self.sb("diagw", [128, 8, 5, 128], BF16)
    for fc in range(8):
        for k in range(5):
            e = S.dve if (fc * 5 + k) % 2 == 0 else S.pool
            S.op(e, lambda h, fc=fc, k=k: h.tensor_scalar(out=diagw[:, fc, k, :], in0=self.identf[:], scalar1=wcol[:, fc, k:k + 1], scalar2=None, op0=ALU.mult),
                 reads=[self.identf.res, wcol.res], writes=[diagw.res])
    xq = [self.sb(f"xq{i}", [128, 8, 516], BF16) for i in range(2)]
    oc = [self.sb(f"oc{i}", [128, 512], BF16) for i in range(3)]
    pc = [self.ps(f"pc{i}", [128, 512]) for i in range(2)]
    MQKv = self.MQK[l].rearrange("(c p) t -> p c t", p=128)
    k2 = 0
    for gi, (u0, n) in enumerate(scan_groups(T)):
        X = xq[gi % 2]
        S.dma(S.sp, [(X[:, 0:4, 0:n + 4], MQKv[:, 0:4, u0:u0 + n + 4]), (X[:, 4:8, 0:n + 4], MQKv[:, 4:8, u0:u0 + n + 4])], reads=[self.R(self.MQK[l].name)], writes=[X.res])
        if u0 == 0 or u0 == LC:
            S.op(S.pool, lambda h, X=X: h.memset(X[:, :, 0:2], 0.0), writes=[X.res])
        if u0 + n == LC or u0 + n == TT:
            S.op(S.pool, lambda h, X=X, n=n: h.memset(X[:, :, n + 2:n + 4], 0.0), writes=[X.res])
        for fc in range(8):
            p = pc[k2 % 2]
            o_ = oc[k2 % 3]
            k2 += 1
            for k in range(5):
                S.op(S.pe, lambda h, fc=fc, k=k, p=p, X=X, n=n: h.matmul(p[:, :n], diagw[:, fc, k, :], X[:, fc, k:k + n], start=(k == 0), stop=(k == 4)),
                     reads=[diagw.res, X.res], writes=[p.res], inc=(k == 4))
            S.op(S.act, lambda h, fc=fc, p=p, o_=o_, n=n: h.activation(out=o_[:, :n], in_=p[:, :n], func=AF.Silu, bias=cb[:, fc:fc + 1]), reads=[p.res, cb.res], writes=[o_.res])
            S.dma(S.sp, [(self.MQC[l][fc * 128:(fc + 1) * 128, u0:u0 + n], o_[:, :n])], reads=[o_.res], writes=[self.R(self.MQC[l].name)])


def phase_ml_scan(self, l):
    S = self.S
    T = self.T
    sel = self.sel
    DEC = self.DEC
    groups = scan_groups(T)
    cfill = self.sb("cfill", [64, 64], F32)
    S.op(S.pool, lambda h: h.memset(cfill[:], LN_KS), writes=[cfill.res])
    mb = [self.sb(f"mb{d}", [64, 64], F32) for d in range(2)]
    for d, sgn in ((0, -1), (1, 1)):
        S.op(S.pool, lambda h, sgn=sgn, d=d: h.affine_select(out=mb[d][:], in_=cfill[:], pattern=[[-sgn, 64]], compare_op=ALU.is_ge, fill=-30000.0,
                                                              base=0, channel_multiplier=sgn), reads=[cfill.res], writes=[mb[d].res])
    nsel = self.sb("nsel", [4, 4, 64], F32)
    S.op(S.dve, lambda h: h.tensor_scalar(out=nsel[:], in0=sel[:, :, 0:64], scalar1=-1.0, scalar2=None, op0=ALU.mult), reads=[self.sel.res], writes=[nsel.res])
    Cf = self.sb("Cf", [128, 4, 129], F32)
    Cb = self.sb("Cb", [128, 4, 129], BF16)
    qk = [self.sb(f"qk{i}", [128, 8, 512], BF16) for i in range(2)]
    vg = [self.sb(f"vgm{i}", [64, 8, 4, 129], BF16) for i in range(2)]
    rows = [self.sb(f"rows{i}", [4, 5, 512], F32) for i in range(2)]
    for v in vg:
        S.op(S.pool, lambda h, v=v: h.memset(v[:], 1.0), writes=[v.res])
    wT = [self.sb(f"wT{i}", [64, 256], F32) for i in range(2)]
    sT = [self.sb(f"sT{i}", [64, 4, 64], BF16) for i in range(2)]
    qks = [self.sb(f"qks{i}", [128, 8, 64], BF16) for i in range(2)]
    khat = [self.sb(f"khat{i}", [64, 4, 128], BF16) for i in range(2)]
    enm = [self.sb(f"enm{i}", [64, 4], F32) for i in range(2)]
    rr = [self.sb(f"rr{i}", [64, 4], F32) for i in range(2)]
    ho = [self.sb(f"ho{i}", [64, 4, 128], F32) for i in range(2)]
    pWS = self.ps("pWS", [64, 512])
    pB = self.ps("pB", [128, 8, 64])
    pK = self.ps("pK", [64, 4, 128], BF16)
    pO = self.ps("pO", [64, 1024])
    pN = self.ps("pN", [128, 1024])
    pO3 = pO[:].rearrange("p (h e) -> p h e", e=256)
    pN3 = pN[:].rearrange("p (h e) -> p h e", e=256)
    gcount = [0]

    def load_group(d, gi):
        i = gcount[0] % 2
        gcount[0] += 1
        u0, n = groups[gi]
        nchg = n // 64
        S.dma(S.sp, [(qk[i][:, 0:4, :n], self.MQC[l].rearrange("(c p) t -> p c t", p=128)[:, 0:4, u0:u0 + n]),
                     (qk[i][:, 4:8, :n], self.MQC[l].rearrange("(c p) t -> p c t", p=128)[:, 4:8, u0:u0 + n])], reads=[self.R(self.MQC[l].name)], writes=[qk[i].res])
        S.dma(S.sp, [(vg[i][:, c, :, 0:128], self.MV[l][u0 + c * 64:u0 + (c + 1) * 64, :].rearrange("p (h e) -> p h e", e=128)) for c in range(nchg)],
              reads=[self.R(self.MV[l].name)], writes=[vg[i].res])
        S.dma(S.sp, [(rows[i][:, :, :n], self.MROWS[l][d, :, :, u0:u0 + n].rearrange("q h t -> h q t"))], reads=[self.R(self.MROWS[l].name)], writes=[rows[i].res])
        return i

    for d in range(2):
        S.op(S.pool, lambda h: h.memset(Cf[:], 0.0), writes=[Cf.res])
        S.op(S.pool, lambda h: h.memset(Cb[:], 0.0), writes=[Cb.res])
        order = scan_order(T, d)
        cur = None
        info = []
        for (gi, c) in order:
            if cur is None or cur[0] != gi:
                cur = (gi, None)
            info.append((gi, c))
        bufof = {}

        def stageA(step):
            gi, c = order[step]
            if gi not in bufof:
                bufof.clear()
                bufof[gi] = load_group(d, gi)
            bi = bufof[gi]
            o = c * 64
            k2 = step % 2
            QK, R_ = qk[bi], rows[bi]
            for hh in range(4):
                S.op(S.pe, lambda h, hh=hh, R_=R_, o=o: h.matmul(pWS[:, hh * 64:(hh + 1) * 64], R_[:, 0, o:o + 64], sel[:, hh, 0:64], start=True, stop=False),
                     reads=[R_.res, self.sel.res], writes=[pWS.res], inc=False)
                S.op(S.pe, lambda h, hh=hh, R_=R_, o=o: h.matmul(pWS[:, hh * 64:(hh + 1) * 64], nsel[:, hh, :], R_[:, 1, o:o + 64], start=False, stop=False),
                     reads=[R_.res, nsel.res], writes=[pWS.res], inc=False)
                S.op(S.pe, lambda h, hh=hh, d=d: h.matmul(pWS[:, hh * 64:(hh + 1) * 64], self.identf[0:64, 0:64], mb[d][:], start=False, stop=True),
                     reads=[self.identf.res, mb[d].res], writes=[pWS.res], inc=False)
            for hh in range(4):
                S.op(S.pe, lambda h, hh=hh, QK=QK, o=o: h.matmul(pWS[:, 256 + hh * 64:256 + (hh + 1) * 64], QK[:, 4 + hh, o:o + 64], QK[:, hh, o:o + 64], start=True, stop=True),
                     reads=[QK.res], writes=[pWS.res], inc=(hh == 3))
            S.op(S.act, lambda h, k2=k2: h.activation(out=wT[k2][:], in_=pWS[:, 0:256], func=AF.Exp), reads=[pWS.res], writes=[wT[k2].res])
            S.op(S.dve, lambda h, k2=k2: h.tensor_tensor(out=sT[k2][:].rearrange("p a b -> p (a b)"), in0=pWS[:, 256:512], in1=wT[k2][:], op=ALU.mult),
                 reads=[pWS.res, wT[k2].res], writes=[sT[k2].res])
            for hh in range(4):
                S.op(S.pe, lambda h, hh=hh, R_=R_, o=o: h.matmul(pB[:, hh, :], sel[:, hh, :], R_[:, 2, o:o + 64], start=True, stop=True),
                     reads=[R_.res, self.sel.res], writes=[pB.res], inc=False)
                S.op(S.pe, lambda h, hh=hh, R_=R_, o=o: h.matmul(pB[:, 4 + hh, :], sel[:, hh, :], R_[:, 4, o:o + 64], start=True, stop=True),
                     reads=[R_.res, self.sel.res], writes=[pB.res], inc=(hh == 3))
            S.op(S.dve, lambda h, k2=k2, QK=QK, o=o: h.tensor_tensor(out=qks[k2][:], in0=QK[:, :, o:o + 64], in1=pB[:], op=ALU.mult),
                 reads=[QK.res, pB.res], writes=[qks[k2].res])
            for hh in range(4):
                S.op(S.pe, lambda h, hh=hh, k2=k2: h.transpose(out=pK[:, hh, :], in_=qks[k2][:, 4 + hh, :], identity=self.ident[:]),
                     reads=[qks[k2].res, self.ident.res], writes=[pK.res], inc=(hh == 3))
            S.op(S.act, lambda h, k2=k2: h.activation(out=khat[k2][:], in_=pK[:], func=AF.Copy), reads=[pK.res], writes=[khat[k2].res])

        def stageB(step):
            gi, c = order[step]
            bi = bufof[gi] if gi in bufof else None
            u0, n = groups[gi]
            o = c * 64
            chunk = (u0 + o) // 64
            k2 = step % 2
            V, R_ = vgbuf[step], rowbuf[step]
            S.op(S.pe, lambda h, R_=R_, o=o: h.matmul(pO[:, 200:204], R_[:, 3, o:o + 64], self.identf[0:4, 0:4], start=True, stop=True),
                 reads=[R_.res, self.identf.res], writes=[pO.res], inc=False)
            for hh in range(4):
                S.op(S.pe, lambda h, hh=hh, k2=k2, V=V, c=c: h.matmul(pO[:, hh * 256:hh * 256 + 129], sT[k2][:, hh, :], V[:, c, hh, :], start=True, stop=False),
                     reads=[sT[k2].res, V.res], writes=[pO.res], inc=False)
                S.op(S.pe, lambda h, hh=hh, k2=k2: h.matmul(pO[:, hh * 256:hh * 256 + 129], qks[k2][:, hh, :], Cb[:, hh, :], start=False, stop=True),
                     reads=[qks[k2].res, Cb.res], writes=[pO.res], inc=(hh == 3))
            for hh in range(4):
                S.op(S.pe, lambda h, hh=hh, k2=k2, V=V, c=c: h.matmul(pN[:, hh * 256:hh * 256 + 129], khat[k2][:, hh, :], V[:, c, hh, :], start=True, stop=True),
                     reads=[khat[k2].res, V.res], writes=[pN.res], inc=(hh == 3))
            S.op(S.act, lambda h, k2=k2: h.activation(out=enm[k2][:], in_=pO[:, 200:204], func=AF.Copy), reads=[pO.res], writes=[enm[k2].res])
            S.op(S.act, lambda h, k2=k2: h.activation(out=rr[k2][:], in_=pO3[:, :, 128], func=AF.Abs), reads=[pO.res], writes=[rr[k2].res])
            S.op(S.dve, lambda h, k2=k2: h.tensor_tensor(out=rr[k2][:], in0=rr[k2][:], in1=enm[k2][:], op=ALU.max), reads=[rr[k2].res, enm[k2].res], writes=[rr[k2].res])
            S.op(S.dve, lambda h, k2=k2: h.reciprocal(out=rr[k2][:], in_=rr[k2][:]), reads=[rr[k2].res], writes=[rr[k2].res])
            S.op(S.dve, lambda h, k2=k2: h.tensor_tensor(out=ho[k2][:], in0=pO3[:, :, 0:128], in1=rr[k2][:].unsqueeze(2).to_broadcast([64, 4, 128]), op=ALU.mult),
                 reads=[pO.res, rr[k2].res], writes=[ho[k2].res])
            S.dma(S.sp, [(self.OM[l][d, u0 + o:u0 + o + 64, :], ho[k2][:].rearrange("p a b -> p (a b)"))], reads=[ho[k2].res], writes=[self.R(self.OM[l].name)])
            S.op(S.dve, lambda h, d=d, chunk=chunk: h.tensor_tensor(out=Cf[:], in0=Cf[:], in1=DEC[:, d, :, chunk:chunk + 1].to_broadcast([128, 4, 129]), op=ALU.mult),
                 reads=[Cf.res, self.DEC.res], writes=[Cf.res])
            S.op(S.dve, lambda h: h.tensor_tensor(out=Cf[:], in0=Cf[:], in1=pN3[:, :, 0:129], op=ALU.add), reads=[Cf.res, pN.res], writes=[Cf.res])
            S.op(S.act, lambda h: h.activation(out=Cb[:], in_=Cf[:], func=AF.Copy), reads=[Cf.res], writes=[Cb.res])

        vgbuf = {}
        rowbuf = {}

        def A(step):
            stageA(step)
            gi, c = order[step]
            vgbuf[step] = vg[bufof[gi]]
            rowbuf[step] = rows[bufof[gi]]
        A(0)
        for step in range(len(order)):
            if step + 1 < len(order):
                A(step + 1)
            stageB(step)


Builder.phase_ml_gates = phase_ml_gates
Builder.phase_ml_conv = phase_ml_conv
Builder.phase_ml_scan = phase_ml_scan


def phase_merge(self, l, streams):
    S = self.S
    win = self.w_in[l].rearrange("(kc p) n -> p kc n", p=128)
    Wm = self.sb("Wm", [128, 8, 4096], BF16)
    for k0 in range(0, 8, 2):
        S.dma(S.pool, [(Wm[:, k0:k0 + 2, 0:512], win[:, k0:k0 + 2, O_GR:O_GR + 512]),
                       (Wm[:, k0:k0 + 2, 512:1024], win[:, k0:k0 + 2, O_MO:O_MO + 512]),
                       (Wm[:, k0:k0 + 2, 1024:4096], win[:, k0:k0 + 2, O_SA:O_SA + 3072])], writes=[Wm.res])
    Woa = self.sb("Woa", [64, 8, D], BF16)
    Wog = self.sb("Wog", [128, 4, D], BF16)
    Wom = self.sb("Wom", [128, 4, D], BF16)
    Wo = self.sb("Wo", [128, 8, D], BF16)
    S.dma(S.pool, [(Woa[:], self.w_out_attn[l].rearrange("(h p) n -> p h n", p=64))], writes=[Woa.res])
    S.dma(S.pool, [(Wog[:], self.w_out_gla[l].rearrange("(c p) n -> p c n", p=128))], writes=[Wog.res])
    S.dma(S.pool, [(Wom[:], self.w_out_mlstm[l].rearrange("(c p) n -> p c n", p=128))], writes=[Wom.res])
    S.dma(S.pool, [(Wo[:, 0:4, :], self.w_o[l].rearrange("(c p) n -> p c n", p=128)[:, 0:4, :]),
                   (Wo[:, 4:8, :], self.w_o[l].rearrange("(c p) n -> p c n", p=128)[:, 4:8, :])], writes=[Wo.res])
    gains = self.sb("gains", [128, 2, 128], F32)
    S.dma(S.sp, [(gains[:, 0, :], bcast_rows(self.gla_norm[l:l + 1, :], 128)), (gains[:, 1, :], bcast_rows(self.mlstm_norm[l:l + 1, :], 128))], writes=[gains.res])
    eps_col = self.sb("eps_col", [128, 1], F32)
    S.op(S.dve, lambda h: h.memset(eps_col[:], EPS), writes=[eps_col.res])
    hT = [self.sb(f"mhT{i}", [128, 8, 128], BF16) for i in range(2)]
    aTt = [self.sb(f"maT{i}", [64, 8, 128], BF16) for i in range(2)]
    og = [self.sb(f"mog{i}", [128, 2, 512], F32) for i in range(2)]
    om = [self.sb(f"mom{i}", [128, 2, 512], F32) for i in range(2)]
    xr = [self.sb(f"mxr{i}", [128, D], F32) for i in range(2)]
    gt = self.sb("mgt", [128, 8, 512], F32)
    sq = self.sb("msq", [128, 512], F32)
    ssq = self.sb("mssq", [128, 2, 4], F32)
    bn = [self.sb(f"mbn{i}", [128, 512], F32) for i in range(2)]
    bb = [self.sb(f"mbb{i}", [128, 512], BF16) for i in range(2)]
    bT = [self.sb(f"mbT{i}", [128, 4, 128], BF16) for i in range(2)]
    yb = self.sb("myb", [128, D], BF16)
    yT = self.sb("myT", [128, 8, 128], BF16)
    t1 = [self.sb(f"mt1{i}", [128, 512], F32) for i in range(3)]
    pg = [self.ps(f"mpg{i}", [128, 512]) for i in range(2)]
    pT = self.ps("mpT", [128, D], BF16)
    py = [self.ps(f"mpy{i}", [128, 512]) for i in range(3)]
    pY = self.ps("mpY", [128, 512])
    cnt = {}

    def nxt(key, n):
        v = cnt.get(key, 0)
        cnt[key] = v + 1
        return v % n
    H2Tv = self.H2T[l].rearrange("(kc p) t -> p kc t", p=128)
    for (tag, src, dst, ntok, row, uoff) in streams:
        G5 = self.gate_bc(l, 1, row, f"m{tag}", 1.0)
        for t0 in range(0, ntok, 128):
            u = uoff + t0
            i = nxt("buf", 2)
            H, AT, OGt, OMt, XR = hT[i], aTt[i], og[i], om[i], xr[i]
            S.dma(S.sp, [(H[:], H2Tv[:, :, u:u + 128])], reads=[self.R(self.H2T[l].name)], writes=[H.res])
            S.dma(S.sp, [(AT[:], self.ATT[l][:, :, u:u + 128])], reads=[self.R(self.ATT[l].name)], writes=[AT.res])
            S.dma(S.sp, [(OGt[:, 0, :], self.OG[l][0, u:u + 128, :]), (OGt[:, 1, :], self.OG[l][1, u:u + 128, :])], reads=[self.R(self.OG[l].name)], writes=[OGt.res])
            S.dma(S.sp, [(OMt[:, 0, :], self.OM[l][0, u:u + 128, :]), (OMt[:, 1, :], self.OM[l][1, u:u + 128, :])], reads=[self.R(self.OM[l].name)], writes=[OMt.res])
            S.dma(S.sp, [(XR[:], src[t0:t0 + 128, :])], reads=[self.R(src.name)], writes=[XR.res])
            for blk in range(8):
                p = pg[nxt("pg", 2)]
                for kc in range(8):
                    S.op(S.pe, lambda h, kc=kc, blk=blk, p=p, H=H: h.matmul(p[:], H[:, kc, :], Wm[:, kc, blk * 512:(blk + 1) * 512], start=(kc == 0), stop=(kc == 7)),
                         reads=[H.res, Wm.res], writes=[p.res], inc=(kc == 7))
                fn = AF.Silu if blk == 0 else AF.Sigmoid
                S.op(S.act, lambda h, blk=blk, p=p, fn=fn: h.activation(out=gt[:, blk, :], in_=p[:], func=fn), reads=[p.res], writes=[gt.res])
            for br, (Ot, gidx) in enumerate(((OGt, 0), (OMt, 1))):
                S.op(S.pool, lambda h, Ot=Ot: h.tensor_tensor(out=Ot[:, 0, :], in0=Ot[:, 0, :], in1=Ot[:, 1, :], op=ALU.add), reads=[Ot.res], writes=[Ot.res])
                S.op(S.act, lambda h, Ot=Ot: h.activation(out=sq[:], in_=Ot[:, 0, :], func=AF.Square), reads=[Ot.res], writes=[sq.res])
                S.op(S.dve, lambda h, br=br: h.tensor_reduce(out=ssq[:, br, :], in_=sq[:].rearrange("p (a b) -> p a b", b=128), axis=AX.X, op=ALU.add),
                     reads=[sq.res], writes=[ssq.res])
                S.op(S.act, lambda h, br=br: h.activation(out=ssq[:, br, :], in_=ssq[:, br, :], func=AF.Sqrt, scale=1.0 / 128, bias=eps_col[:]),
                     reads=[ssq.res, eps_col.res], writes=[ssq.res])
                S.op(S.dve, lambda h, br=br: h.reciprocal(out=ssq[:, br, :], in_=ssq[:, br, :]), reads=[ssq.res], writes=[ssq.res])
                B_ = bn[br]
                S.op(S.dve, lambda h, br=br, Ot=Ot, B_=B_: h.tensor_tensor(out=B_[:].rearrange("p (a b) -> p a b", b=128), in0=Ot[:, 0, :].rearrange("p (a b) -> p a b", b=128),
                                                                         in1=ssq[:, br, :].unsqueeze(2).to_broadcast([128, 4, 128]), op=ALU.mult),
                     reads=[Ot.res, ssq.res], writes=[B_.res])
                S.op(S.pool, lambda h, br=br, B_=B_: h.tensor_tensor(out=B_[:].rearrange("p (a b) -> p a b", b=128), in0=B_[:].rearrange("p (a b) -> p a b", b=128),
                                                                  in1=gains[:, br:br + 1, :].to_broadcast([128, 4, 128]), op=ALU.mult),
                     reads=[B_.res, gains.res], writes=[B_.res])
                BB = bb[br]
                S.op(S.dve, lambda h, br=br, B_=B_, BB=BB: h.tensor_tensor(out=BB[:], in0=B_[:], in1=gt[:, br, :], op=ALU.mult), reads=[B_.res, gt.res], writes=[BB.res])
                for c in range(4):
                    S.op(S.pe, lambda h, c=c, BB=BB: h.transpose(out=pT[:, c * 128:(c + 1) * 128], in_=BB[:, c * 128:(c + 1) * 128], identity=self.ident[:]),
                         reads=[BB.res, self.ident.res], writes=[pT.res], inc=(c == 3))
                BT = bT[br]
                if br == 0:
                    S.op(S.act, lambda h, BT=BT: h.activation(out=BT[:].rearrange("p a b -> p (a b)"), in_=pT[:, 0:512], func=AF.Copy), reads=[pT.res], writes=[BT.res])
                else:
                    S.op(S.dve, lambda h, BT=BT: h.tensor_copy(out=BT[:].rearrange("p a b -> p (a b)"), in_=pT[:, 0:512]), reads=[pT.res], writes=[BT.res])
            for half in range(2):
                cs_ = slice(half * 512, (half + 1) * 512)
                for hh in range(8):
                    S.op(S.pe, lambda h, hh=hh, AT=AT, cs_=cs_: h.matmul(py[0][:], AT[:, hh, :], Woa[:, hh, cs_], start=(hh == 0), stop=(hh == 7)),
                         reads=[AT.res, Woa.res], writes=[py[0].res], inc=(hh == 7))
                for c in range(4):
                    S.op(S.pe, lambda h, c=c, cs_=cs_: h.matmul(py[1][:], bT[0][:, c, :], Wog[:, c, cs_], start=(c == 0), stop=(c == 3)),
                         reads=[bT[0].res, Wog.res], writes=[py[1].res], inc=(c == 3))
                for c in range(4):
                    S.op(S.pe, lambda h, c=c, cs_=cs_: h.matmul(py[2][:], bT[1][:, c, :], Wom[:, c, cs_], start=(c == 0), stop=(c == 3)),
                         reads=[bT[1].res, Wom.res], writes=[py[2].res], inc=(c == 3))
                S.op(S.dve, lambda h, half=half: h.tensor_tensor(out=t1[0][:], in0=py[0][:], in1=gt[:, 2 + half, :], op=ALU.mult), reads=[py[0].res, gt.res], writes=[t1[0].res])
                S.op(S.dve, lambda h, half=half: h.tensor_tensor(out=t1[1][:], in0=py[1][:], in1=gt[:, 4 + half, :], op=ALU.mult), reads=[py[1].res, gt.res], writes=[t1[1].res])
                S.op(S.dve, lambda h, half=half: h.tensor_tensor(out=t1[2][:], in0=py[2][:], in1=gt[:, 6 + half, :], op=ALU.mult), reads=[py[2].res, gt.res], writes=[t1[2].res])
                S.op(S.pool, lambda h: h.tensor_tensor(out=t1[0][:], in0=t1[0][:], in1=t1[1][:], op=ALU.add), reads=[t1[0].res, t1[1].res], writes=[t1[0].res])
                S.op(S.pool, lambda h, cs_=cs_: h.tensor_tensor(out=yb[:, cs_], in0=t1[0][:], in1=t1[2][:], op=ALU.add), reads=[t1[0].res, t1[2].res], writes=[yb.res])
            if "DBGY" in self.dbg:
                if not hasattr(self, "DBGY"):
                    self.DBGY = self.dscr("DBGY", [self.TT, D], BF16)
                    self.DBGG = self.dscr("DBGG", [self.TT, 8, 512], F32)
                    self.DBGB = self.dscr("DBGB", [self.TT, 2, 512], BF16)
                import os
                sel_ = int(os.environ.get("DBGSEL", "7"))
                if sel_ & 1:
                    S.dma(S.pool, [(self.DBGY[u:u + 128, :], yb[:])], reads=[yb.res], writes=[self.R("DBGY")])
                if sel_ & 2:
                    S.dma(S.pool, [(self.DBGG[u:u + 128], gt[:])], reads=[gt.res], writes=[self.R("DBGG")])
                if sel_ & 4:
                    S.dma(S.pool, [(self.DBGB[u:u + 128, 0, :], bb[0][:]), (self.DBGB[u:u + 128, 1, :], bb[1][:])], reads=[bb[0].res, bb[1].res], writes=[self.R("DBGB")])
            for kc in range(8):
                S.op(S.pe, lambda h, kc=kc: h.transpose(out=pT[:, kc * 128:(kc + 1) * 128], in_=yb[:, kc * 128:(kc + 1) * 128], identity=self.ident[:]),
                     reads=[yb.res, self.ident.res], writes=[pT.res], inc=(kc == 7))
            S.op(S.act, lambda h: h.activation(out=yT[:].rearrange("p a b -> p (a b)"), in_=pT[:], func=AF.Copy), reads=[pT.res], writes=[yT.res])
            for half in range(2):
                cs_ = slice(half * 512, (half + 1) * 512)
                for kc in range(8):
                    S.op(S.pe, lambda h, kc=kc, cs_=cs_: h.matmul(pY[:], yT[:, kc, :], Wo[:, kc, cs_], start=(kc == 0), stop=(kc == 7)),
                         reads=[yT.res, Wo.res], writes=[pY.res], inc=(kc == 7))
                S.op(S.dve, lambda h, cs_=cs_, G5=G5: h.tensor_tensor(out=t1[0][:], in0=pY[:], in1=G5[:, cs_], op=ALU.mult), reads=[pY.res, G5.res], writes=[t1[0].res])
                S.op(S.pool, lambda h, cs_=cs_, XR=XR: h.tensor_tensor(out=XR[:, cs_], in0=XR[:, cs_], in1=t1[0][:], op=ALU.add), reads=[XR.res, t1[0].res], writes=[XR.res])
            S.dma(S.sp, [(dst[t0:t0 + 128, :], XR[:])], reads=[XR.res], writes=[self.R(dst.name)])


Builder.phase_merge = phase_merge

_NC_CACHE = {}


def kernel(**inputs):
    inp = {k: np.asarray(v) for k, v in inputs.items()}
    Bsz, SEQ, _ = inp["x"].shape
    T = SEQ
    if T not in _NC_CACHE:
        _NC_CACHE[T] = Builder(T).build()
    nc = _NC_CACHE[T]
    in_maps = [make_in_map(inp, b, 0, T) for b in range(Bsz)]
    res = run_bass_kernel_spmd(nc, in_maps, core_ids=list(range(Bsz)))
    out = np.stack([np.asarray(r["y"], dtype=np.float32) for r in res.results], axis=0)
    return out


W_NAMES = ["mod_w", "mod_b", "norm_g", "ffn1_w13", "ffn1_w2", "ffn2_w13", "ffn2_w2", "w_in", "attn_q_norm", "attn_k_norm", "attn_sink", "gla_w2", "gla_b", "mlstm_conv_w", "mlstm_conv_b", "mlstm_ib", "mlstm_fb", "gla_norm", "mlstm_norm", "w_out_attn", "w_out_gla", "w_out_mlstm", "w_o"]


def make_in_map(inp, b, t0, T):
    m = {"x": np.ascontiguousarray(inp["x"][b, t0:t0 + T]), "c": np.ascontiguousarray(inp["c"][b]),
         "ctx": np.ascontiguousarray(inp["ctx"][b]), "c_ctx": np.ascontiguousarray(inp["c_ctx"])}
    for k in W_NAMES:
        m[k] = np.ascontiguousarray(inp[k])
    m["rope_cs"] = rope_table(t0, T)
    return m


def rope_table(t0, T):
    pos = np.arange(t0, t0 + T)
    r = (pos // 64).astype(np.float32)
    col = (pos % 64).astype(np.float32)
    inv = (np.float32(10000.0) ** (-np.arange(16, dtype=np.float32) / np.float32(16))).astype(np.float32)
    ang = np.concatenate([r[:, None] * inv, col[:, None] * inv], axis=-1).astype(np.float32)
    return np.ascontiguousarray(np.stack([np.cos(ang), np.sin(ang)], axis=1).astype(np.float32))
```

```python
import numpy as np
from contextlib import ExitStack
import concourse.bass as bass
import concourse.mybir as mybir
from concourse.bass_utils import run_bass_kernel_spmd

F32 = mybir.dt.float32
BF16 = mybir.dt.bfloat16
AF = mybir.ActivationFunctionType
ALU = mybir.AluOpType
AX = mybir.AxisListType

D = 1024
DFF = 2816
NMOD = 9
LC = 256
EPS = 1e-6
DEPTH = 2
D_IN = 7472


class Res:
    __slots__ = ("name", "w", "r")

    def __init__(self, name=""):
        self.name = name
        self.w = None
        self.r = []


class Eng:
    def __init__(self, name, is_pe=False):
        self.name = name
        self.is_pe = is_pe
        self.ops = []
        self.sems = []
        self.si = 0
        self.cnt = 0
        self.seen = {}
        self.pend_r = []
        self.pend_w = []
        self.pool = []
        self.pi = 0


ROT = 30000


class Sched:
    def __init__(self, nc, es):
        self.nc = nc
        self.es = es
        self.pe = Eng("pe", True)
        self.act = Eng("act")
        self.dve = Eng("dve")
        self.pool = Eng("pool")
        self.sp = Eng("sp")
        self.engs = [self.pe, self.act, self.dve, self.pool, self.sp]
        self.semid = {}
        n_rot = {"pe": 6, "act": 3, "dve": 3, "pool": 3, "sp": 1}
        for e in self.engs:
            for i in range(n_rot[e.name]):
                s = es.enter_context(nc.semaphore(f"s_{e.name}{i}"))
                e.sems.append(s)
        for e, n in ((self.sp, 20), (self.pool, 10), (self.act, 4)):
            for i in range(n):
                s = es.enter_context(nc.semaphore(f"d_{e.name}{i}"))
                e.pool.append([s, 0])
        self.n_ops = 0

    def _need(self, eng, tok, raw):
        if tok is None:
            return None
        sem, val, owner = tok
        if owner == eng.name:
            if eng.is_pe:
                return None
            if not raw:
                return None
        key = id(sem)
        if eng.seen.get(key, 0) >= val:
            return None
        eng.seen[key] = val
        return (sem, val)

    def _waits(self, eng, reads, writes):
        ws = []
        for r in reads:
            w = self._need(eng, r.w, True)
            if w:
                ws.append(w)
        for wr in writes:
            w = self._need(eng, wr.w, False)
            if w:
                ws.append(w)
            for t in wr.r:
                w = self._need(eng, t, False)
                if w:
                    ws.append(w)
        for (sem, val) in ws:
            eng.ops.append(lambda h, sem=sem, val=val: h.wait_ge(sem, val))

    def _record(self, tok, reads, writes):
        for r in reads:
            r.r = [t for t in r.r if t[2] != tok[2] or t[0] is not tok[0]] + [tok]
        for w in writes:
            w.w = tok
            w.r = []

    def op(self, eng, fn, reads=(), writes=(), inc=True):
        self.n_ops += 1
        reads = list(reads)
        writes = list(writes)
        self._waits(eng, reads, writes)
        if not inc:
            eng.ops.append(lambda h, fn=fn: fn(h))
            eng.pend_r += reads
            eng.pend_w += writes
            return
        if eng.cnt >= ROT:
            eng.si += 1
            eng.cnt = 0
        eng.cnt += 1
        sem = eng.sems[eng.si]
        tok = (sem, eng.cnt, eng.name)
        eng.ops.append(lambda h, fn=fn, sem=sem: fn(h).then_inc(sem, 1))
        self._record(tok, reads + eng.pend_r, writes + eng.pend_w)
        eng.pend_r = []
        eng.pend_w = []

    def dma(self, eng, pairs, reads=(), writes=(), **kw):
        self.n_ops += 1
        reads = list(reads)
        writes = list(writes)
        self._waits(eng, reads, writes)
        ent = eng.pool[eng.pi]
        eng.pi = (eng.pi + 1) % len(eng.pool)
        sem = ent[0]
        if ent[1] > 0 and eng.seen.get(id(sem), 0) < ent[1]:
            v = ent[1]
            eng.ops.append(lambda h, sem=sem, v=v: h.wait_ge(sem, v))
            eng.seen[id(sem)] = v
        for (o, i) in pairs:
            ent[1] += 16
            eng.ops.append(lambda h, o=o, i=i, sem=sem: h.dma_start(out=o, in_=i, **kw).then_inc(sem, 16))
        tok = (sem, ent[1], "dma_" + eng.name + str(id(sem)))
        self._record(tok, reads, writes)

    def barrier(self):
        toks = []
        for e in self.engs:
            assert not e.pend_r and not e.pend_w, e.name
            for i in range(e.si + 1):
                v = ROT if i < e.si else e.cnt
                if v > 0:
                    toks.append((e, e.sems[i], v))
            for ent in e.pool:
                if ent[1] > 0:
                    toks.append((None, ent[0], ent[1]))
        for e in self.engs:
            for (own, sem, v) in toks:
                if own is e:
                    continue
                if e.seen.get(id(sem), 0) >= v:
                    continue
                e.seen[id(sem)] = v
                e.ops.append(lambda h, sem=sem, v=v: h.wait_ge(sem, v))

    def finish(self):
        for e in (self.sp, self.pool, self.act):
            for ent in e.pool:
                if ent[1] > 0:
                    self.sp.ops.append(lambda h, sem=ent[0], v=ent[1]: h.wait_ge(sem, v))

    def replay(self):
        nc = self.nc
        with nc.Block() as block:
            @block.tensor
            def _(h):
                for f in self.pe.ops:
                    f(h)

            @block.scalar
            def _(h):
                for f in self.act.ops:
                    f(h)

            @block.vector
            def _(h):
                for f in self.dve.ops:
                    f(h)

            @block.gpsimd
            def _(h):
                for f in self.pool.ops:
                    f(h)

            @block.sync
            def _(h):
                for f in self.sp.ops:
                    f(h)


class Tile:
    def __init__(self, t, name):
        self.t = t
        self.res = Res(name)

    def __getitem__(self, k):
        return self.t[k]


def dram_ap(t, offset, pattern):
    return bass.AP(t.tensor, offset, pattern)


class Builder:
    def __init__(self, T, depth=DEPTH, stop=None, dbg=()):
        self.T = T
        self.depth = depth
        self.stop = stop
        self.dbg = dbg
        self.nc = bass.Bass("TRN2", target_bir_lowering=False)
        self.es = ExitStack()
        self.S = None
        self.dres = {}
        self.rr = {}

    def din(self, name, shape):
        return self.nc.dram_tensor(name, list(shape), F32, kind="ExternalInput").ap()

    def dout(self, name, shape, dt=F32):
        return self.nc.dram_tensor(name, list(shape), dt, kind="ExternalOutput").ap()

    def dscr(self, name, shape, dt=F32):
        if name in self.dbg:
            return self.nc.dram_tensor(name, list(shape), dt, kind="ExternalOutput").ap()
        return self.nc.dram_tensor(name, list(shape), dt).ap()

    def R(self, *key):
        if key not in self.dres:
            self.dres[key] = Res(str(key))
        return self.dres[key]

    def sb(self, name, shape, dt):
        self.uid = getattr(self, "uid", 0) + 1
        name = f"{name}_{self.uid}"
        t = self.cur.enter_context(self.nc.sbuf_tensor(name, list(shape), dt))
        return Tile(t, name)

    def ps(self, name, shape, dt=F32):
        self.uid = getattr(self, "uid", 0) + 1
        name = f"{name}_{self.uid}"
        t = self.cur.enter_context(self.nc.psum_tensor(name, list(shape), dt))
        return Tile(t, name)

    def build(self):
        nc = self.nc
        T = self.T
        L = self.depth
        with self.es as es:
            self.S = S = Sched(nc, es)
            self.x_in = self.din("x", [T, D])
            self.c_in = self.din("c", [D])
            self.ctx_in = self.din("ctx", [LC, D])
            self.cctx_in = self.din("c_ctx", [D])
            self.mod_w = self.din("mod_w", [L, D, NMOD * D])
            self.mod_b = self.din("mod_b", [L, NMOD * D])
            self.norm_g = self.din("norm_g", [L, 3, D])
            self.ffn_w13 = [self.din("ffn1_w13", [L, D, 2 * DFF]), self.din("ffn2_w13", [L, D, 2 * DFF])]
            self.ffn_w2 = [self.din("ffn1_w2", [L, DFF, D]), self.din("ffn2_w2", [L, DFF, D])]
            self.w_in = self.din("w_in", [L, D, D_IN])
            self.attn_q_norm = self.din("attn_q_norm", [L, 64])
            self.attn_k_norm = self.din("attn_k_norm", [L, 64])
            self.attn_sink = self.din("attn_sink", [L, 8])
            self.gla_w2 = self.din("gla_w2", [L, 2, 16, 256])
            self.gla_b = self.din("gla_b", [L, 2, 256])
            self.rope_cs = self.din("rope_cs", [T, 2, 32])
            self.y_out = self.dout("y", [T, D])
            TT = self.TT = LC + T
            self.H2T = [self.dscr(f"H2T{l}", [D, TT], BF16) for l in range(L)]
            self.QT = [self.dscr(f"QT{l}", [64, 8, TT], BF16) for l in range(L)]
            self.KT = [self.dscr(f"KT{l}", [64, 2, TT], BF16) for l in range(L)]
            self.VA = [self.dscr(f"VA{l}", [TT, 128], BF16) for l in range(L)]
            self.GV = [self.dscr(f"GV{l}", [TT, 512], BF16) for l in range(L)]
            self.MV = [self.dscr(f"MV{l}", [TT, 512], BF16) for l in range(L)]
            self.GATES = [self.dscr(f"GATES{l}", [16, TT]) for l in range(L)]
            self.QG = [self.dscr(f"QG{l}", [2, 64, 4, TT], BF16) for l in range(L)]
            self.KG = [self.dscr(f"KG{l}", [2, 64, 4, TT], BF16) for l in range(L)]
            self.KH = [self.dscr(f"KH{l}", [2, 64, 4, TT], BF16) for l in range(L)]
            self.MQK = [self.dscr(f"MQK{l}", [D, TT + 4], BF16) for l in range(L)]
            self.ATT = [self.dscr(f"ATT{l}", [64, 8, TT], BF16) for l in range(L)]
            self.OG = [self.dscr(f"OG{l}", [2, TT, 512]) for l in range(L)]
            self.OM = [self.dscr(f"OM{l}", [2, TT, 512]) for l in range(L)]
            self.MROWS = [self.dscr(f"MROWS{l}", [2, 5, 4, TT]) for l in range(L)]
            self.MQC = [self.dscr(f"MQC{l}", [D, TT], BF16) for l in range(L)]
            self.gla_norm = self.din("gla_norm", [L, 128])
            self.mlstm_norm = self.din("mlstm_norm", [L, 128])
            self.w_out_attn = self.din("w_out_attn", [L, 512, D])
            self.w_out_gla = self.din("w_out_gla", [L, 512, D])
            self.w_out_mlstm = self.din("w_out_mlstm", [L, 512, D])
            self.w_o = self.din("w_o", [L, D, D])
            self.X2 = [self.dscr(f"X2_{l}", [T, D]) for l in range(L)]
            self.C2 = [self.dscr(f"C2_{l}", [LC, D]) for l in range(L)]
            self.X3 = [self.dscr(f"X3_{l}", [T, D]) for l in range(L)]
            self.C3 = [self.dscr(f"C3_{l}", [LC, D]) for l in range(L)]
            self.conv_w = self.din("mlstm_conv_w", [L, 5, D])
            self.conv_b = self.din("mlstm_conv_b", [L, D])
            self.mlstm_ib = self.din("mlstm_ib", [L, 2, 4])
            self.mlstm_fb = self.din("mlstm_fb", [L, 2, 4])
            self.MOD = [self.dscr(f"MOD{l}", [2, NMOD * D]) for l in range(L)]
            self.X1 = [self.dscr(f"X1_{l}", [T, D]) for l in range(L)]
            self.C1 = [self.dscr(f"C1_{l}", [LC, D]) for l in range(L)]
            with ExitStack() as cst:
                self.cur = cst
                self.ident = self.sb("ident", [128, 128], BF16)
                self.identf = self.sb("identf", [128, 128], F32)
                self.eps_col = self.sb("eps_col", [128, 1], F32)
                S.op(S.dve, lambda h: h.memset(self.eps_col[:], EPS), writes=[self.eps_col.res])
                self.make_consts()
                for l in range(L):
                    xin = self.x_in if l == 0 else self.X3[l - 1]
                    cin = self.ctx_in if l == 0 else self.C3[l - 1]
                    with ExitStack() as ph:
                        self.cur = ph
                        self.phase_mod(l)
                        S.barrier()
                    if self.stop == ("mod", l):
                        break
                    with ExitStack() as ph:
                        self.cur = ph
                        self.phase_ffn(l, 0, [("ctx", cin, self.C1[l], LC, 1), ("lat", xin, self.X1[l], T, 0)])
                        S.barrier()
                    if self.stop == ("ffn1", l):
                        break
                    with ExitStack() as lay:
                        self.cur = lay
                        self.EL = self.sb("EL", [64, 4, 2, TT // 64], F32)
                        with ExitStack() as ph:
                            self.cur = ph
                            self.phase_feat(l, [("ctx", self.C1[l], LC, 1, 0, False), ("lat", self.X1[l], T, 0, LC, True)])
                            S.barrier()
                        if self.stop == ("feat", l):
                            break
                        with ExitStack() as ph:
                            self.cur = ph
                            self.phase_attn(l, l < L - 1)
                            S.barrier()
                        if self.stop == ("attn", l):
                            break
                        with ExitStack() as ph:
                            self.cur = ph
                            self.phase_gla(l)
                            S.barrier()
                        if self.stop == ("gla", l):
                            break
                        self.cur = lay
                        self.DEC = self.sb("DEC", [128, 2, 4, TT // 64], F32)
                        self.sel = self.sb("sel", [4, 4, 128], F32)
                        S.op(S.dve, lambda h, sel_t=self.sel: h.tensor_copy(out=sel_t[:], in_=self.identf[0:4, 0:4].unsqueeze(2).to_broadcast([4, 4, 128])),
                             reads=[self.identf.res], writes=[self.sel.res])
                        stop_ml = False
                        for ph_name, ph_fn in (("mlg", self.phase_ml_gates), ("mlc", self.phase_ml_conv), ("mls", self.phase_ml_scan)):
                            with ExitStack() as ph:
                                self.cur = ph
                                ph_fn(l)
                                S.barrier()
                            if self.stop == (ph_name, l):
                                stop_ml = True
                                break
                        if stop_ml:
                            break
                        if self.stop == ("ml", l):
                            break
                    last = (l == L - 1)
                    with ExitStack() as ph:
                        self.cur = ph
                        st = [("lat", self.X1[l], self.X2[l], T, 0, LC)]
                        if not last:
                            st = [("ctx", self.C1[l], self.C2[l], LC, 1, 0)] + st
                        self.phase_merge(l, st)
                        S.barrier()
                    if self.stop == ("merge", l):
                        break
                    with ExitStack() as ph:
                        self.cur = ph
                        xdst = self.y_out if last else self.X3[l]
                        st = [("lat", self.X2[l], xdst, T, 0)]
                        import os
                        if not last and not os.environ.get("NOCTX2"):
                            st = [("ctx", self.C2[l], self.C3[l], LC, 1)] + st
                        self.phase_ffn(l, 1, [(a, b, c, d_, e) for (a, b, c, d_, e) in st])
                        S.barrier()
                    if self.stop == ("ffn2", l):
                        break
                S.finish()
                S.replay()
        return nc

    def make_consts(self):
        S = self.S
        nc = self.nc
        idf = self.identf
        S.op(S.pool, lambda h: h.memset(idf[:], 0.0), writes=[idf.res])
        S.op(S.pool, lambda h: h.affine_select(out=idf[:], in_=idf[:], pattern=[[-1, 128]],
                                                compare_op=ALU.not_equal, fill=1.0, base=0,
                                                channel_multiplier=1),
             reads=[idf.res], writes=[idf.res])
        S.op(S.dve, lambda h: h.tensor_copy(out=self.ident[:], in_=idf[:]), reads=[idf.res], writes=[self.ident.res])

    def phase_mod(self, l):
        S = self.S
        cl = self.sb("cl", [128, 8, 2], F32)
        cs = self.sb("cs", [128, 8, 2], F32)
        S.dma(S.sp, [(cl[:, :, 0], self.c_in.rearrange("(kc p) -> p kc", p=128)),
                     (cl[:, :, 1], self.cctx_in.rearrange("(kc p) -> p kc", p=128))],
              writes=[cl.res], allow_slow_non_contiguous=True)
        S.op(S.act, lambda h: h.activation(out=cs[:], in_=cl[:], func=AF.Silu), reads=[cl.res], writes=[cs.res])
        wm = [self.sb(f"wm{i}", [128, 8, 512], F32) for i in range(2)]
        mb = [self.sb(f"mb{i}", [2, 512], F32) for i in range(2)]
        mo = [self.sb(f"mo{i}", [2, 512], F32) for i in range(2)]
        pm = [self.ps(f"pm{i}", [2, 512]) for i in range(2)]
        mw = self.mod_w[l].rearrange("(kc p) n -> p kc n", p=128)
        for n in range(18):
            i = n % 2
            S.dma(S.sp, [(wm[i][:, 0:4, :], mw[:, 0:4, n * 512:(n + 1) * 512]),
                         (wm[i][:, 4:8, :], mw[:, 4:8, n * 512:(n + 1) * 512])], writes=[wm[i].res])
            mbsrc = self.mod_b[l:l + 1, n * 512:(n + 1) * 512]
            S.dma(S.sp, [(mb[i][0:1, :], mbsrc), (mb[i][1:2, :], mbsrc)], writes=[mb[i].res])
            for kc in range(8):
                S.op(S.pe, lambda h, kc=kc, i=i: h.matmul(pm[i][:], cs[:, kc, :], wm[i][:, kc, :],
                                                           start=(kc == 0), stop=(kc == 7)),
                     reads=[cs.res, wm[i].res], writes=[pm[i].res], inc=(kc == 7))
            S.op(S.dve, lambda h, i=i: h.tensor_tensor(out=mo[i][:], in0=pm[i][:], in1=mb[i][:], op=ALU.add),
                 reads=[pm[i].res, mb[i].res], writes=[mo[i].res])
            S.dma(S.sp, [(self.MOD[l][:, n * 512:(n + 1) * 512], mo[i][:])], reads=[mo[i].res],
                  writes=[self.R("MOD", l)])

    def load_cols(self, dst_ap, src_row_ap, res):
        self.S.dma(self.S.sp, [(dst_ap, src_row_ap.rearrange("(kc p) -> p kc", p=128))], writes=[res],
                   allow_slow_non_contiguous=True)

    def adaln_cols(self, l, j, row, tag):
        S = self.S
        tmp = self.sb(f"adt_{tag}", [128, 3, 8], F32)
        A = self.sb(f"adA_{tag}", [128, 8], F32)
        MODr = self.MOD[l]
        S.dma(S.sp, [(tmp[:, 0, :], MODr[row, (3 * j) * D:(3 * j + 1) * D].rearrange("(kc p) -> p kc", p=128)),
                     (tmp[:, 1, :], MODr[row, (3 * j + 1) * D:(3 * j + 2) * D].rearrange("(kc p) -> p kc", p=128)),
                     (tmp[:, 2, :], self.norm_g[l, j, :].rearrange("(kc p) -> p kc", p=128))],
              reads=[self.R("MOD", l)], writes=[tmp.res], allow_slow_non_contiguous=True)
        S.op(S.dve, lambda h: h.scalar_tensor_tensor(out=A[:], in0=tmp[:, 1, :], scalar=1.0, in1=tmp[:, 2, :],
                                                      op0=ALU.add, op1=ALU.mult),
             reads=[tmp.res], writes=[A.res])
        return A, tmp

    def gate_bc(self, l, j, row, tag, mul):
        S = self.S
        G = self.sb(f"gate_{tag}", [128, D], F32)
        src = self.MOD[l][row:row + 1, (3 * j + 2) * D:(3 * j + 3) * D]
        src_b = dram_ap(src, src.offset, [[0, 128], [1, D]])
        S.dma(S.sp, [(G[:], src_b)], reads=[self.R("MOD", l)], writes=[G.res])
        if mul != 1.0:
            S.op(S.pool, lambda h: h.tensor_scalar(out=G[:], in0=G[:], scalar1=float(mul), scalar2=None, op0=ALU.mult),
                 reads=[G.res], writes=[G.res])
        return G

    def load_weight_bf16(self, dst, src3, nsplit):
        S = self.S
        kcn = dst.t.shape[1]
        step = max(1, kcn // nsplit)
        for k0 in range(0, kcn, step):
            k1 = min(kcn, k0 + step)
            S.dma(S.pool, [(dst[:, k0:k1, :], src3[:, k0:k1, :])], writes=[dst.res])

    def norm_part(self, xt, nb, ss, rs):
        S = self.S
        junk = self.junk
        S.op(S.act, lambda h: h.activation(out=junk[:], in_=xt[:], func=AF.Square, accum_out=ss[:]),
             reads=[xt.res], writes=[junk.res, ss.res])
        S.op(S.act, lambda h: h.activation(out=rs[:], in_=ss[:], func=AF.Sqrt, scale=1.0 / D, bias=self.eps_col[:]),
             reads=[ss.res], writes=[rs.res])
        S.op(S.dve, lambda h: h.reciprocal(out=rs[:], in_=rs[:]), reads=[rs.res], writes=[rs.res])
        S.op(S.dve, lambda h: h.tensor_scalar(out=nb[:], in0=xt[:], scalar1=rs[:], scalar2=None, op0=ALU.mult),
             reads=[xt.res, rs.res], writes=[nb.res])

    def transpose_part(self, nb, pT, hT, col0, A, sh, evac_engs):
        S = self.S
        for kc in range(8):
            S.op(S.pe, lambda h, kc=kc: h.transpose(out=pT[:, kc * 128:(kc + 1) * 128], in_=nb[:, kc * 128:(kc + 1) * 128],
                                                     identity=self.ident[:]),
                 reads=[nb.res, self.ident.res], writes=[pT.res], inc=(kc == 7))
        for kc in range(8):
            e = evac_engs[kc % len(evac_engs)]
            if e is S.act:
                S.op(e, lambda h, kc=kc: h.activation(out=hT[:, kc, col0:col0 + 128], in_=pT[:, kc * 128:(kc + 1) * 128],
                                                      func=AF.Identity, scale=A[:, kc:kc + 1], bias=sh[:, kc:kc + 1]),
                     reads=[pT.res, A.res, self.shres], writes=[hT.res])
            else:
                S.op(e, lambda h, kc=kc: h.tensor_scalar(out=hT[:, kc, col0:col0 + 128], in0=pT[:, kc * 128:(kc + 1) * 128],
                                                         scalar1=A[:, kc:kc + 1], scalar2=sh[:, kc:kc + 1],
                                                         op0=ALU.mult, op1=ALU.add),
                     reads=[pT.res, A.res, self.shres], writes=[hT.res])

    def phase_ffn(self, l, which, streams):
        S = self.S
        j = 0 if which == 0 else 2
        W13 = self.sb("W13", [128, 8, 2 * DFF], BF16)
        W2 = self.sb("W2", [128, 22, D], BF16)
        self.load_weight_bf16(W13, self.ffn_w13[which][l].rearrange("(kc p) n -> p kc n", p=128), 8)
        self.load_weight_bf16(W2, self.ffn_w2[which][l].rearrange("(fc p) n -> p fc n", p=128), 11)
        import os
        if os.environ.get("FFN_WONLY") and which == 1:
            return
        self.junk = self.sb("junk", [128, D], BF16)
        xl = [self.sb(f"xl{i}", [128, D], F32) for i in range(3)]
        xr = [self.sb(f"xr{i}", [128, D], F32) for i in range(2)]
        nb = [self.sb(f"nb{i}", [128, D], BF16) for i in range(2)]
        ss = [self.sb(f"ss{i}", [128, 1], F32) for i in range(2)]
        rs = [self.sb(f"rs{i}", [128, 1], F32) for i in range(2)]
        hT = self.sb("hT", [128, 8, 512], BF16)
        gT = self.sb("gT", [128, 22, 512], BF16)
        sa = [self.sb(f"sa{i}", [128, 512], F32) for i in range(2)]
        tt = [self.sb(f"tt{i}", [128, 512], F32) for i in range(2)]
        pT = [self.ps(f"pT{i}", [128, D], BF16) for i in range(2)]
        pA = [self.ps(f"pA{i}", [128, 512]) for i in range(2)]
        pB = [self.ps(f"pB{i}", [128, 512]) for i in range(2)]
        pY = [self.ps(f"pY{i}", [128, 512]) for i in range(2)]
        cnt = {"xl": 0, "xr": 0, "nb": 0, "pT": 0, "pAB": 0, "sa": 0, "tt": 0}

        for (tag, src, dst, ntok, row) in streams:
            A, tmp = self.adaln_cols(l, j, row, f"{which}{tag}")
            sh = tmp[:, 0, :]
            self.shres = tmp.res
            G = self.gate_bc(l, j, row, f"{which}{tag}", 0.5)
            tiles = [(t0, min(512, ntok - t0)) for t0 in range(0, ntok, 512)]
            rtag = ("xs", l, which, tag)

            def prep_norm(t0, s):
                i = cnt["xl"] % 3
                cnt["xl"] += 1
                k = cnt["nb"] % 2
                cnt["nb"] += 1
                S.dma(S.sp, [(xl[i][:], src[t0 + s * 128:t0 + (s + 1) * 128, :])], reads=[self.R(src.name)],
                      writes=[xl[i].res])
                self.norm_part(xl[i], nb[k], ss[k], rs[k])
                return nb[k]

            def prep_tr(nbt, s):
                k = cnt["pT"] % 2
                cnt["pT"] += 1
                self.transpose_part(nbt, pT[k], hT, s * 128, A, sh, [S.act, S.dve])

            def prep(t0, n):
                for s in range(n // 128):
                    nbt = prep_norm(t0, s)
                    prep_tr(nbt, s)

            prep(*tiles[0])
            for ti, (t0, n) in enumerate(tiles):
                nt = n // 128
                for p in range(22):
                    k = cnt["pAB"] % 2
                    cnt["pAB"] += 1
                    for kc in range(8):
                        S.op(S.pe, lambda h, kc=kc, p=p, k=k, n=n: h.matmul(pA[k][:, :n], W13[:, kc, p * 128:(p + 1) * 128], hT[:, kc, :n],
                                                                        start=(kc == 0), stop=(kc == 7)),
                             reads=[W13.res, hT.res], writes=[pA[k].res], inc=(kc == 7))
                    for kc in range(8):
                        S.op(S.pe, lambda h, kc=kc, p=p, k=k, n=n: h.matmul(pB[k][:, :n], W13[:, kc, DFF + p * 128:DFF + (p + 1) * 128], hT[:, kc, :n],
                                                                        start=(kc == 0), stop=(kc == 7)),
                             reads=[W13.res, hT.res], writes=[pB[k].res], inc=(kc == 7))
                    q = cnt["sa"] % 2
                    cnt["sa"] += 1
                    S.op(S.act, lambda h, k=k, q=q, n=n: h.activation(out=sa[q][:, :n], in_=pA[k][:, :n], func=AF.Silu),
                         reads=[pA[k].res], writes=[sa[q].res])
                    S.op(S.dve, lambda h, k=k, q=q, p=p, n=n: h.tensor_tensor(out=gT[:, p, :n], in0=sa[q][:, :n], in1=pB[k][:, :n], op=ALU.mult),
                         reads=[sa[q].res, pB[k].res], writes=[gT.res])
                xrs = []
                for s in range(nt):
                    pass
                if ti + 1 < len(tiles):
                    pending = tiles[ti + 1]
                else:
                    pending = None
                for s in range(nt):
                    i = cnt["xr"] % 2
                    cnt["xr"] += 1
                    S.dma(S.sp, [(xr[i][:], src[t0 + s * 128:t0 + (s + 1) * 128, :])], reads=[self.R(src.name)],
                          writes=[xr[i].res])
                    for dh in range(2):
                        for fc in range(22):
                            S.op(S.pe, lambda h, fc=fc, dh=dh, s=s: h.matmul(pY[dh][:], gT[:, fc, s * 128:(s + 1) * 128], W2[:, fc, dh * 512:(dh + 1) * 512],
                                                                              start=(fc == 0), stop=(fc == 21)),
                                 reads=[gT.res, W2.res], writes=[pY[dh].res], inc=(fc == 21))
                    for dh in range(2):
                        q = cnt["tt"] % 2
                        cnt["tt"] += 1
                        S.op(S.dve, lambda h, dh=dh, q=q, G=G: h.tensor_tensor(out=tt[q][:], in0=pY[dh][:], in1=G[:, dh * 512:(dh + 1) * 512], op=ALU.mult),
                             reads=[pY[dh].res, G.res], writes=[tt[q].res])
                        S.op(S.pool, lambda h, dh=dh, q=q, i=i: h.tensor_tensor(out=xr[i][:, dh * 512:(dh + 1) * 512], in0=xr[i][:, dh * 512:(dh + 1) * 512],
                                                                                 in1=tt[q][:], op=ALU.add),
                             reads=[tt[q].res, xr[i].res], writes=[xr[i].res])
                    S.dma(S.sp, [(dst[t0 + s * 128:t0 + (s + 1) * 128, :], xr[i][:])], reads=[xr[i].res],
                          writes=[self.R(dst.name)])
                    if pending is not None and s == 0:
                        prep(*pending)


O_AQ, O_AK, O_AV = 0, 512, 640
O_GQ, O_GK, O_GV, O_GR, O_GG = 768, 1024, 1280, 1792, 2304
O_MQ, O_MK, O_MV, O_MO, O_MI, O_MF = 2336, 2848, 3360, 3872, 4384, 4392
O_SA, O_SG, O_SM = 4400, 5424, 6448


def bcast_rows(ap2d, nparts):
    return bass.AP(ap2d.tensor, ap2d.offset, [[0, nparts]] + [list(x) for x in ap2d.ap[1:]])


def rev_last(ap):
    pat = [list(x) for x in ap.ap]
    st, n = pat[-1]
    return bass.AP(ap.tensor, ap.offset + st * (n - 1), pat[:-1] + [[-st, n]])


def phase_feat(self, l, streams):
    S = self.S
    TT = self.TT
    win = self.w_in[l].rearrange("(kc p) n -> p kc n", p=128)
    Wa = self.sb("Wa", [128, 8, 768], BF16)
    Wv = self.sb("Wv", [128, 8, 1024], BF16)
    Wf = self.sb("Wf", [128, 8, 1536], BF16)
    Wg = self.sb("Wg", [128, 8, 48], BF16)
    S.dma(S.pool, [(Wa[:, 0:4, :], win[:, 0:4, 0:768]), (Wa[:, 4:8, :], win[:, 4:8, 0:768])], writes=[Wa.res])
    for k0 in range(0, 8, 2):
        S.dma(S.pool, [(Wv[:, k0:k0 + 2, 0:512], win[:, k0:k0 + 2, O_GV:O_GV + 512]),
                       (Wv[:, k0:k0 + 2, 512:1024], win[:, k0:k0 + 2, O_MV:O_MV + 512])], writes=[Wv.res])
        S.dma(S.pool, [(Wf[:, k0:k0 + 2, 0:512], win[:, k0:k0 + 2, O_GQ:O_GQ + 512]),
                       (Wf[:, k0:k0 + 2, 512:1536], win[:, k0:k0 + 2, O_MQ:O_MQ + 1024])], writes=[Wf.res])
    S.dma(S.pool, [(Wg[:, :, 0:32], win[:, :, O_GG:O_GG + 32]), (Wg[:, :, 32:48], win[:, :, O_MI:O_MI + 16])], writes=[Wg.res])
    W2p = self.sb("W2p", [32, 2, 256], F32)
    S.op(S.dve, lambda h: h.memset(W2p[:], 0.0), writes=[W2p.res])
    S.dma(S.sp, [(W2p[0:16, 0, :], self.gla_w2[l, 0]), (W2p[16:32, 1, :], self.gla_w2[l, 1])], writes=[W2p.res])
    negb = self.sb("negb", [128, 2, 2], F32)
    S.dma(S.sp, [(negb[:, d, :], self.gla_b[l, d, :].rearrange("(c p) -> p c", p=128)) for d in range(2)],
          writes=[negb.res], allow_slow_non_contiguous=True)
    S.op(S.dve, lambda h: h.tensor_scalar(out=negb[:], in0=negb[:], scalar1=-1.0, scalar2=None, op0=ALU.mult),
         reads=[negb.res], writes=[negb.res])
    gain = self.sb("gain", [128, 10, 64], F32)
    qn_src = self.attn_q_norm[l:l + 1, :]
    kn_src = self.attn_k_norm[l:l + 1, :]
    S.dma(S.sp, [(gain[:, 0:8, :], bass.AP(qn_src.tensor, qn_src.offset, [[0, 128], [0, 8], [1, 64]])),
                 (gain[:, 8:10, :], bass.AP(kn_src.tensor, kn_src.offset, [[0, 128], [0, 2], [1, 64]]))],
          writes=[gain.res])
    mask01 = self.sb("mask01", [128, 8, 64], F32)
    S.op(S.pool, lambda h: h.memset(mask01[:], 1.0), writes=[mask01.res])
    S.op(S.pool, lambda h: h.memset(mask01[:, :, 0:1], 0.0), writes=[mask01.res])
    self.junk = self.sb("junk", [128, D], BF16)

    xl = [self.sb(f"xl{i}", [128, D], F32) for i in range(2)]
    nb = [self.sb(f"nb{i}", [128, D], BF16) for i in range(2)]
    ss = [self.sb(f"ss{i}", [128, 1], F32) for i in range(2)]
    rs = [self.sb(f"rs{i}", [128, 1], F32) for i in range(2)]
    hT = self.sb("hT", [128, 8, 512], BF16)
    sqt = self.sb("sqt", [128, 640], F32)
    ssh = self.sb("ssh", [128, 10], F32)
    rinv = self.sb("rinv", [128, 10], F32)
    qn = self.sb("qn", [128, 10, 64], F32)
    rt = [self.sb(f"rt{i}", [128, 10, 32], F32) for i in range(4)]
    cs_t = [self.sb(f"cst{i}", [128, 2, 32], F32) for i in range(2)]
    qr = [self.sb(f"qr{i}", [128, 10, 64], BF16) for i in range(2)]
    vb = [self.sb(f"vb{i}", [128, 128], BF16) for i in range(2)]
    vb2 = [self.sb(f"vb2{i}", [128, 512], BF16) for i in range(2)]
    QTs = self.sb("QTs", [64, 8, 512], BF16)
    KTs = self.sb("KTs", [64, 2, 512], BF16)
    ggT = self.sb("ggT", [32, 512], F32)
    gts = self.sb("gts", [16, 512], F32)
    ex = [self.sb(f"ex{i}", [128, 512], F32) for i in range(2)]
    csum = [self.sb(f"csum{i}", [128, 512], F32) for i in range(2)]
    eb = [[self.sb(f"eb{d}{c}", [128, 512], F32) for c in range(2)] for d in range(2)]
    enb = [[self.sb(f"enb{d}{c}", [128, 512], F32) for c in range(2)] for d in range(2)]
    ebl = [[self.sb(f"ebl{d}{c}", [128, 512], F32) for c in range(2)] for d in range(2)]
    fo = [self.sb(f"fo{i}", [128, 512], BF16) for i in range(4)]
    elcs = [self.sb(f"elc{i}", [128, 8], F32) for i in range(2)]
    pT = self.ps("pT", [128, D], BF16)
    pq = self.ps("pq", [128, 512])
    pkv = self.ps("pkv", [128, 256])
    pqt = self.ps("pqt", [64, 8, 128], BF16)
    pkt = self.ps("pkt", [64, 2, 128], BF16)
    pf = [self.ps(f"pf{i}", [128, 512]) for i in range(2)]
    pz = self.ps("pz", [128, 512])
    cnt = {"xl": 0, "pf": 0, "fo": 0, "qr": 0, "vb": 0, "vb2": 0, "ex": 0, "rt": 0, "cs": 0, "elc": 0}
    H2Tv = self.H2T[l].rearrange("(kc p) t -> p kc t", p=128)

    def nxt(key, n):
        v = cnt[key] % n
        cnt[key] += 1
        return v

    for (tag, src, ntok, row, uoff, rope) in streams:
        A, tmp = self.adaln_cols(l, 1, row, f"f{tag}")
        sh = tmp[:, 0, :]
        self.shres = tmp.res
        for t0 in range(0, ntok, 512):
            n = min(512, ntok - t0)
            nt = n // 128
            u0 = uoff + t0
            nch = n // 64
            for s in range(nt):
                i = nxt("xl", 2)
                S.dma(S.sp, [(xl[i][:], src[t0 + s * 128:t0 + (s + 1) * 128, :])], reads=[self.R(src.name)], writes=[xl[i].res])
                self.norm_part(xl[i], nb[i], ss[i], rs[i])
                self.transpose_part(nb[i], pT, hT, s * 128, A, sh, [S.act, S.dve])
            S.dma(S.sp, [(H2Tv[:, :, u0:u0 + n], hT[:, :, :n])], reads=[hT.res], writes=[self.R(self.H2T[l].name)])
            for s in range(nt):
                c0 = s * 128
                for kc in range(8):
                    S.op(S.pe, lambda h, kc=kc, c0=c0: h.matmul(pq[:], hT[:, kc, c0:c0 + 128], Wa[:, kc, 0:512], start=(kc == 0), stop=(kc == 7)),
                         reads=[hT.res, Wa.res], writes=[pq.res], inc=(kc == 7))
                for kc in range(8):
                    S.op(S.pe, lambda h, kc=kc, c0=c0: h.matmul(pkv[:], hT[:, kc, c0:c0 + 128], Wa[:, kc, 512:768], start=(kc == 0), stop=(kc == 7)),
                         reads=[hT.res, Wa.res], writes=[pkv.res], inc=(kc == 7))
                S.op(S.act, lambda h: h.activation(out=sqt[:, 0:512], in_=pq[:], func=AF.Square), reads=[pq.res], writes=[sqt.res])
                S.op(S.act, lambda h: h.activation(out=sqt[:, 512:640], in_=pkv[:, 0:128], func=AF.Square), reads=[pkv.res], writes=[sqt.res])
                S.op(S.dve, lambda h: h.tensor_reduce(out=ssh[:], in_=sqt[:].rearrange("p (a b) -> p a b", b=64), axis=AX.X, op=ALU.add),
                     reads=[sqt.res], writes=[ssh.res])
                S.op(S.act, lambda h: h.activation(out=rinv[:], in_=ssh[:], func=AF.Sqrt, scale=1.0 / 64, bias=self.eps_col[:]),
                     reads=[ssh.res, self.eps_col.res], writes=[rinv.res])
                S.op(S.dve, lambda h: h.reciprocal(out=rinv[:], in_=rinv[:]), reads=[rinv.res], writes=[rinv.res])
                S.op(S.dve, lambda h: h.tensor_tensor(out=qn[:, 0:8, :], in0=pq[:].rearrange("p (a b) -> p a b", b=64),
                                                       in1=rinv[:, 0:8].unsqueeze(2).to_broadcast([128, 8, 64]), op=ALU.mult),
                     reads=[pq.res, rinv.res], writes=[qn.res])
                S.op(S.dve, lambda h: h.tensor_tensor(out=qn[:, 8:10, :], in0=pkv[:, 0:128].rearrange("p (a b) -> p a b", b=64),
                                                       in1=rinv[:, 8:10].unsqueeze(2).to_broadcast([128, 2, 64]), op=ALU.mult),
                     reads=[pkv.res, rinv.res], writes=[qn.res])
                S.op(S.pool, lambda h: h.tensor_tensor(out=qn[:], in0=qn[:], in1=gain[:], op=ALU.mult),
                     reads=[qn.res, gain.res], writes=[qn.res])
                qi = nxt("qr", 2)
                q_ = qr[qi]
                if rope:
                    cst = cs_t[nxt("cs", 2)]
                    S.dma(S.sp, [(cst[:], self.rope_cs[t0 + c0:t0 + c0 + 128, :, :])], writes=[cst.res])
                    cosb = cst[:, 0:1, :].to_broadcast([128, 10, 32])
                    sinb = cst[:, 1:2, :].to_broadcast([128, 10, 32])
                    x1 = qn[:, :, 0:32]
                    x2 = qn[:, :, 32:64]
                    r = [rt[nxt("rt", 4)] for _ in range(4)]
                    S.op(S.pool, lambda h, r=r, cosb=cosb, x1=x1: h.tensor_tensor(out=r[0][:], in0=x1, in1=cosb, op=ALU.mult),
                         reads=[qn.res, cst.res], writes=[r[0].res])
                    S.op(S.pool, lambda h, r=r, sinb=sinb, x2=x2: h.tensor_tensor(out=r[1][:], in0=x2, in1=sinb, op=ALU.mult),
                         reads=[qn.res, cst.res], writes=[r[1].res])
                    S.op(S.dve, lambda h, r=r, q_=q_: h.tensor_tensor(out=q_[:, :, 0:32], in0=r[0][:], in1=r[1][:], op=ALU.subtract),
                         reads=[r[0].res, r[1].res], writes=[q_.res])
                    S.op(S.pool, lambda h, r=r, sinb=sinb, x1=x1: h.tensor_tensor(out=r[2][:], in0=x1, in1=sinb, op=ALU.mult),
                         reads=[qn.res, cst.res], writes=[r[2].res])
                    S.op(S.pool, lambda h, r=r, cosb=cosb, x2=x2: h.tensor_tensor(out=r[3][:], in0=x2, in1=cosb, op=ALU.mult),
                         reads=[qn.res, cst.res], writes=[r[3].res])
                    S.op(S.dve, lambda h, r=r, q_=q_: h.tensor_tensor(out=q_[:, :, 32:64], in0=r[2][:], in1=r[3][:], op=ALU.add),
                         reads=[r[2].res, r[3].res], writes=[q_.res])
                else:
                    S.op(S.dve, lambda h, q_=q_: h.tensor_copy(out=q_[:], in_=qn[:]), reads=[qn.res], writes=[q_.res])
                vi = nxt("vb", 2)
                S.op(S.act, lambda h, vi=vi: h.activation(out=vb[vi][:], in_=pkv[:, 128:256], func=AF.Copy), reads=[pkv.res], writes=[vb[vi].res])
                S.dma(S.sp, [(self.VA[l][u0 + c0:u0 + c0 + 128, :], vb[vi][:])], reads=[vb[vi].res], writes=[self.R(self.VA[l].name)])
                for hh in range(8):
                    S.op(S.pe, lambda h, hh=hh, q_=q_: h.transpose(out=pqt[:, hh, :], in_=q_[:, hh, :], identity=self.ident[:]),
                         reads=[q_.res, self.ident.res], writes=[pqt.res], inc=(hh == 7))
                for hh in range(2):
                    S.op(S.pe, lambda h, hh=hh, q_=q_: h.transpose(out=pkt[:, hh, :], in_=q_[:, 8 + hh, :], identity=self.ident[:]),
                         reads=[q_.res, self.ident.res], writes=[pkt.res], inc=(hh == 1))
                S.op(S.act, lambda h, c0=c0: h.activation(out=QTs[:, :, c0:c0 + 128], in_=pqt[:], func=AF.Copy), reads=[pqt.res], writes=[QTs.res])
                S.op(S.dve, lambda h, c0=c0: h.tensor_copy(out=KTs[:, :, c0:c0 + 128], in_=pkt[:]), reads=[pkt.res], writes=[KTs.res])
            S.dma(S.sp, [(self.QT[l][:, :, u0:u0 + n], QTs[:, :, :n])], reads=[QTs.res], writes=[self.R(self.QT[l].name)])
            S.dma(S.sp, [(self.KT[l][:, :, u0:u0 + n], KTs[:, :, :n])], reads=[KTs.res], writes=[self.R(self.KT[l].name)])
            for s in range(nt):
                c0 = s * 128
                for half in range(2):
                    k = nxt("pf", 2)
                    for kc in range(8):
                        S.op(S.pe, lambda h, kc=kc, c0=c0, k=k, half=half: h.matmul(pf[k][:], hT[:, kc, c0:c0 + 128], Wv[:, kc, half * 512:(half + 1) * 512],
                                                                                   start=(kc == 0), stop=(kc == 7)),
                             reads=[hT.res, Wv.res], writes=[pf[k].res], inc=(kc == 7))
                    vi = nxt("vb2", 2)
                    eng = S.act if half == 0 else S.dve
                    if half == 0:
                        S.op(S.act, lambda h, k=k, vi=vi: h.activation(out=vb2[vi][:], in_=pf[k][:], func=AF.Copy), reads=[pf[k].res], writes=[vb2[vi].res])
                    else:
                        S.op(S.dve, lambda h, k=k, vi=vi: h.tensor_copy(out=vb2[vi][:], in_=pf[k][:]), reads=[pf[k].res], writes=[vb2[vi].res])
                    dstv = self.GV[l] if half == 0 else self.MV[l]
                    S.dma(S.sp, [(dstv[u0 + c0:u0 + c0 + 128, :], vb2[vi][:])], reads=[vb2[vi].res], writes=[self.R(dstv.name)])
            for kc in range(8):
                S.op(S.pe, lambda h, kc=kc, n=n: h.matmul(pz[0:32, :n], Wg[:, kc, 0:32], hT[:, kc, :n], start=(kc == 0), stop=(kc == 7)),
                     reads=[hT.res, Wg.res], writes=[pz.res], inc=(kc == 7))
            S.op(S.act, lambda h, n=n: h.activation(out=ggT[:, :n], in_=pz[0:32, :n], func=AF.Copy), reads=[pz.res], writes=[ggT.res])
            for kc in range(8):
                S.op(S.pe, lambda h, kc=kc, n=n: h.matmul(pz[0:16, :n], Wg[:, kc, 32:48], hT[:, kc, :n], start=(kc == 0), stop=(kc == 7)),
                     reads=[hT.res, Wg.res], writes=[pz.res], inc=(kc == 7))
            S.op(S.act, lambda h, n=n: h.activation(out=gts[:, :n], in_=pz[0:16, :n], func=AF.Copy), reads=[pz.res], writes=[gts.res])
            S.dma(S.sp, [(self.GATES[l][:, u0:u0 + n], gts[:, :n])], reads=[gts.res], writes=[self.R(self.GATES[l].name)])
            for d in range(2):
                for c2 in range(2):
                    S.op(S.pe, lambda h, d=d, c2=c2, n=n: h.matmul(pz[:, :n], W2p[:, d, c2 * 128:(c2 + 1) * 128], ggT[:, :n], start=True, stop=True),
                         reads=[W2p.res, ggT.res], writes=[pz.res])
                    e_ = ex[nxt("ex", 2)]
                    c_ = csum[(cnt["ex"]) % 2]
                    S.op(S.act, lambda h, d=d, c2=c2, n=n, e_=e_: h.activation(out=e_[:, :n], in_=pz[:, :n], func=AF.Exp, scale=-1.0, bias=negb[:, d, c2:c2 + 1]),
                         reads=[pz.res, negb.res], writes=[e_.res])
                    S.op(S.act, lambda h, n=n, e_=e_: h.activation(out=e_[:, :n], in_=e_[:, :n], func=AF.Ln, bias=1.0), reads=[e_.res], writes=[e_.res])
                    m01 = mask01[:].rearrange("p a b -> p (a b)")[:, :n]
                    if d == 0:
                        S.op(S.dve, lambda h, n=n, e_=e_, c_=c_, m01=m01: h.tensor_tensor_scan(out=c_[:, :n], data0=m01, data1=e_[:, :n], initial=0.0, op0=ALU.mult, op1=ALU.add),
                             reads=[e_.res, mask01.res], writes=[c_.res])
                        last = 63
                    else:
                        S.op(S.dve, lambda h, n=n, e_=e_, c_=c_, m01=m01: h.tensor_tensor_scan(out=rev_last(c_[:, :n]), data0=m01, data1=rev_last(e_[:, :n]), initial=0.0,
                                                                                             op0=ALU.mult, op1=ALU.add),
                             reads=[e_.res, mask01.res], writes=[c_.res])
                        last = 0
                    EB, ENB, EBL = eb[d][c2], enb[d][c2], ebl[d][c2]
                    S.op(S.act, lambda h, n=n, c_=c_, EB=EB: h.activation(out=EB[:, :n], in_=c_[:, :n], func=AF.Exp, scale=-1.0 / 16), reads=[c_.res], writes=[EB.res])
                    S.op(S.act, lambda h, n=n, c_=c_, ENB=ENB: h.activation(out=ENB[:, :n], in_=c_[:, :n], func=AF.Exp, scale=1.0 / 16), reads=[c_.res], writes=[ENB.res])
                    c3 = c_[:, :n].rearrange("p (a b) -> p a b", b=64)
                    S.op(S.dve, lambda h, n=n, c_=c_, c3=c3, last=last, nch=nch: h.tensor_tensor(out=c3, in0=c3, in1=c3[:, :, last:last + 1].to_broadcast([128, nch, 64]), op=ALU.subtract),
                         reads=[c_.res], writes=[c_.res])
                    S.op(S.act, lambda h, n=n, c_=c_, EBL=EBL: h.activation(out=EBL[:, :n], in_=c_[:, :n], func=AF.Exp, scale=1.0 / 16), reads=[c_.res], writes=[EBL.res])
                    ch0 = u0 // 64
                    elc = elcs[nxt("elc", 2)]
                    S.op(S.pool, lambda h, n=n, EB=EB, last=last, elc=elc, nch=nch: h.tensor_copy(
                        out=elc[:, :nch], in_=EB[:, :n].rearrange("p (a b) -> p a b", b=64)[:, :, last]),
                         reads=[EB.res], writes=[elc.res])
                    S.dma(S.sp, [(self.EL[:, 2 * c2 + hh2, d, ch0:ch0 + nch], elc[hh2 * 64:(hh2 + 1) * 64, :nch]) for hh2 in range(2)],
                          reads=[elc.res], writes=[self.EL.res])
            for fc in range(12):
                k = nxt("pf", 2)
                for kc in range(8):
                    S.op(S.pe, lambda h, kc=kc, fc=fc, k=k, n=n: h.matmul(pf[k][:, :n], Wf[:, kc, fc * 128:(fc + 1) * 128], hT[:, kc, :n], start=(kc == 0), stop=(kc == 7)),
                         reads=[hT.res, Wf.res], writes=[pf[k].res], inc=(kc == 7))
                if fc < 2:
                    for d in range(2):
                        o_ = fo[nxt("fo", 4)]
                        S.op(S.dve, lambda h, k=k, n=n, d=d, fc=fc, o_=o_: h.scalar_tensor_tensor(out=o_[:, :n], in0=pf[k][:, :n], scalar=0.125, in1=eb[d][fc][:, :n],
                                                                                              op0=ALU.mult, op1=ALU.mult),
                             reads=[pf[k].res, eb[d][fc].res], writes=[o_.res])
                        S.dma(S.sp, [(self.QG[l][d, :, 2 * fc + hh2, u0:u0 + n], o_[hh2 * 64:(hh2 + 1) * 64, :n]) for hh2 in range(2)], reads=[o_.res], writes=[self.R(self.QG[l].name)])
                elif fc < 4:
                    c2 = fc - 2
                    for d in range(2):
                        o_ = fo[nxt("fo", 4)]
                        S.op(S.dve, lambda h, k=k, n=n, d=d, c2=c2, o_=o_: h.tensor_tensor(out=o_[:, :n], in0=pf[k][:, :n], in1=enb[d][c2][:, :n], op=ALU.mult),
                             reads=[pf[k].res, enb[d][c2].res], writes=[o_.res])
                        S.dma(S.sp, [(self.KG[l][d, :, 2 * c2 + hh2, u0:u0 + n], o_[hh2 * 64:(hh2 + 1) * 64, :n]) for hh2 in range(2)], reads=[o_.res], writes=[self.R(self.KG[l].name)])
                        o_ = fo[nxt("fo", 4)]
                        S.op(S.dve, lambda h, k=k, n=n, d=d, c2=c2, o_=o_: h.tensor_tensor(out=o_[:, :n], in0=pf[k][:, :n], in1=ebl[d][c2][:, :n], op=ALU.mult),
                             reads=[pf[k].res, ebl[d][c2].res], writes=[o_.res])
                        S.dma(S.sp, [(self.KH[l][d, :, 2 * c2 + hh2, u0:u0 + n], o_[hh2 * 64:(hh2 + 1) * 64, :n]) for hh2 in range(2)], reads=[o_.res], writes=[self.R(self.KH[l].name)])
                else:
                    o_ = fo[nxt("fo", 4)]
                    S.op(S.act, lambda h, k=k, n=n, o_=o_: h.activation(out=o_[:, :n], in_=pf[k][:, :n], func=AF.Copy), reads=[pf[k].res], writes=[o_.res])
                    r0 = (fc - 4) * 128
                    S.dma(S.sp, [(self.MQK[l][r0:r0 + 128, 2 + u0:2 + u0 + n], o_[:, :n])], reads=[o_.res], writes=[self.R(self.MQK[l].name)])


Builder.phase_feat = phase_feat


def phase_attn(self, l, do_ctx):
    S = self.S
    T = self.T
    nbk = T // 128
    ones = self.sb("ones", [128, 128], F32)
    S.op(S.pool, lambda h: h.memset(ones[:], 1.0), writes=[ones.res])
    mP = self.sb("mP", [128, 4, 128], BF16)
    mN = self.sb("mN", [128, 4, 128], BF16)
    mtmp = self.sb("mtmp", [128, 128], F32)
    zer = self.sb("zer", [128, 128], F32)
    S.op(S.pool, lambda h: h.memset(zer[:], 0.0), writes=[zer.res])
    for (m_, sgn) in ((mP, 1), (mN, -1)):
        S.op(S.pool, lambda h, sgn=sgn: h.affine_select(out=mtmp[:], in_=zer[:], pattern=[[-sgn, 128]], compare_op=ALU.is_ge, fill=-30000.0,
                                                         base=0, channel_multiplier=sgn), reads=[zer.res], writes=[mtmp.res])
        S.op(S.pool, lambda h, m_=m_: h.tensor_copy(out=m_[:], in_=mtmp[:].unsqueeze(1).to_broadcast([128, 4, 128])), reads=[mtmp.res], writes=[m_.res])
    esk = self.sb("esk", [128, 2, 4, 128], F32)
    sk8 = self.sb("sk8", [128, 8], F32)
    S.dma(S.sp, [(sk8[64:65, :], self.attn_sink[l:l + 1, :])], writes=[sk8.res])
    S.op(S.act, lambda h: h.activation(out=sk8[64:65, :], in_=sk8[64:65, :], func=AF.Exp), reads=[sk8.res], writes=[sk8.res])
    S.op(S.dve, lambda h: h.tensor_copy(out=esk[64:65].rearrange("p g a b -> p (g a) b"), in_=sk8[64:65, :].unsqueeze(2).to_broadcast([1, 8, 128])),
         reads=[sk8.res], writes=[esk.res])
    KTc = self.sb("KTc", [64, 2, 256], BF16)
    S.dma(S.sp, [(KTc[:], self.KT[l][:, :, 0:256])], reads=[self.R(self.KT[l].name)], writes=[KTc.res])
    Vc = [self.sb(f"Vc{j}", [128, 2, 65], BF16) for j in range(2)]
    Vb = [self.sb(f"Vb{j}", [128, 2, 65], BF16) for j in range(4)]
    KTb = [self.sb(f"KTb{j}", [64, 2, 128], BF16) for j in range(4)]
    for v in Vc + Vb:
        S.op(S.pool, lambda h, v=v: h.memset(v[:], 1.0), writes=[v.res])
    for j in range(2):
        S.dma(S.sp, [(Vc[j][:, :, 0:64], self.VA[l][j * 128:(j + 1) * 128, :].rearrange("p (g d) -> p g d", d=64))],
              reads=[self.R(self.VA[l].name)], writes=[Vc[j].res])
    QTb = [self.sb(f"QTb{j}", [64, 8, 128], BF16) for j in range(2)]
    E = [self.sb(f"E{j}", [128, 4, 128], BF16) for j in range(4)]
    dn = [self.sb(f"dn{j}", [128, 512], F32) for j in range(2)]
    bcs = [self.sb(f"bcs{j}", [64, 512], F32) for j in range(2)]
    aT = [self.sb(f"aT{j}", [64, 4, 128], BF16) for j in range(2)]
    pS = [self.ps(f"pS{j}", [128, 512]) for j in range(3)]
    pO = [self.ps(f"pO{j}", [128, 512]) for j in range(2)]
    pB = [self.ps(f"pB{j}", [64, 512]) for j in range(2)]
    cnt = {}

    def nxt(key, n):
        v = cnt.get(key, 0)
        cnt[key] = v + 1
        return v % n

    def load_kb(m):
        i = m % 4
        u = LC + m * 128
        S.dma(S.sp, [(KTb[i][:], self.KT[l][:, :, u:u + 128])], reads=[self.R(self.KT[l].name)], writes=[KTb[i].res])
        S.dma(S.sp, [(Vb[i][:, :, 0:64], self.VA[l][u:u + 128, :].rearrange("p (g d) -> p g d", d=64))],
              reads=[self.R(self.VA[l].name)], writes=[Vb[i].res])

    pending = []

    def norm(g, po, u0):
        d_ = dn[nxt("dn", 2)]
        S.op(S.dve, lambda h, d_=d_, po=po, g=g: h.tensor_tensor(out=d_[64:65, :], in0=po[64:65, :], in1=esk[64:65, g].rearrange("p a b -> p (a b)"), op=ALU.add),
             reads=[po.res, esk.res], writes=[d_.res])
        S.op(S.dve, lambda h, d_=d_: h.reciprocal(out=d_[64:65, :], in_=d_[64:65, :]), reads=[d_.res], writes=[d_.res])
        pb = pB[nxt("pb", 2)]
        S.op(S.pe, lambda h, d_=d_, pb=pb: h.matmul(pb[:], ones[64:65, 0:64], d_[64:65, :], start=True, stop=True),
             reads=[d_.res, ones.res], writes=[pb.res])
        b_ = bcs[nxt("bcs", 2)]
        S.op(S.act, lambda h, b_=b_, pb=pb: h.activation(out=b_[:], in_=pb[:], func=AF.Copy), reads=[pb.res], writes=[b_.res])
        a_ = aT[nxt("aT", 2)]
        S.op(S.dve, lambda h, a_=a_, b_=b_, po=po: h.tensor_tensor(out=a_[:].rearrange("p a b -> p (a b)"), in0=po[0:64, :], in1=b_[:], op=ALU.mult),
             reads=[po.res, b_.res], writes=[a_.res])
        S.dma(S.sp, [(self.ATT[l][:, 4 * g:4 * g + 4, u0:u0 + 128], a_[:])], reads=[a_.res], writes=[self.R(self.ATT[l].name)])

    def qblock(u0, kbs):
        qi = nxt("q", 2)
        Q = QTb[qi]
        S.dma(S.sp, [(Q[:], self.QT[l][:, :, u0:u0 + 128])], reads=[self.R(self.QT[l].name)], writes=[Q.res])
        for g in range(2):
            po = pO[nxt("po", 2)]
            rhsq = Q[:, 4 * g:4 * g + 4, :].rearrange("p a b -> p (a b)")

            def score(idx, g=g, rhsq=rhsq):
                kt, vt, msk = kbs[idx]
                p = pS[nxt("ps", 3)]
                S.op(S.pe, lambda h, kt=kt, p=p, g=g, rhsq=rhsq, msk=msk: h.matmul(p[:], kt[0][:, g, kt[1]:kt[1] + 128], rhsq, start=True, stop=(msk is None)),
                     reads=[kt[0].res, Q.res], writes=[p.res], inc=(msk is None))
                if msk is not None:
                    S.op(S.pe, lambda h, p=p, msk=msk: h.matmul(p[:], self.ident[:], msk[:].rearrange("p a b -> p (a b)"), start=False, stop=True),
                         reads=[self.ident.res, msk.res], writes=[p.res])
                return p
            ps_list = [score(0)]
            for idx in range(len(kbs)):
                kt, vt, msk = kbs[idx]
                if idx + 1 < len(kbs):
                    ps_list.append(score(idx + 1))
                p = ps_list[idx]
                e = E[nxt("e", 4)]
                S.op(S.act, lambda h, p=p, e=e: h.activation(out=e[:].rearrange("p a b -> p (a b)"), in_=p[:], func=AF.Exp, scale=0.125),
                     reads=[p.res], writes=[e.res])
                S.op(S.pe, lambda h, e=e, vt=vt, po=po, idx=idx, g=g, kbs=kbs: h.matmul(po[0:65, :], vt[:, g, :], e[:].rearrange("p a b -> p (a b)"),
                                                                                      start=(idx == 0), stop=(idx == len(kbs) - 1)),
                     reads=[e.res, vt.res], writes=[po.res], inc=(idx == len(kbs) - 1))
            pending.append((g, po, u0))
            if len(pending) > 1:
                norm(*pending.pop(0))

    ckb = [((KTc, 0), Vc[0], None), ((KTc, 128), Vc[1], None)]
    if do_ctx:
        for n in range(2):
            qblock(n * 128, ckb)
    load_kb(0)
    for n in range(nbk):
        if n + 1 < nbk:
            load_kb(n + 1)
        kbs = []
        if n - 1 >= 0:
            kbs.append(((KTb[(n - 1) % 4], 0), Vb[(n - 1) % 4], mP))
        kbs.append(((KTb[n % 4], 0), Vb[n % 4], None))
        if n + 1 < nbk:
            kbs.append(((KTb[(n + 1) % 4], 0), Vb[(n + 1) % 4], mN))
        qblock(LC + n * 128, kbs + ckb)
    while pending:
        norm(*pending.pop(0))


Builder.phase_attn = phase_attn


def scan_groups(T):
    return [(0, LC)] + [(LC + t0, min(512, T - t0)) for t0 in range(0, T, 512)]


def scan_order(T, d):
    groups = scan_groups(T)
    order = []
    if d == 0:
        for gi, (u0, n) in enumerate(groups):
            for c in range(n // 64):
                order.append((gi, c))
    else:
        gis = [0] + list(range(len(groups) - 1, 0, -1))
        for gi in gis:
            u0, n = groups[gi]
            for c in range(n // 64 - 1, -1, -1):
                order.append((gi, c))
    return order


def phase_gla(self, l):
    S = self.S
    T = self.T
    EL = self.EL
    groups = scan_groups(T)
    ones = self.sb("ones", [64, 64], F32)
    S.op(S.pool, lambda h: h.memset(ones[:], 1.0), writes=[ones.res])
    mtmp = self.sb("mtmp", [64, 64], F32)
    msk = [self.sb(f"msk{d}", [64, 64], BF16) for d in range(2)]
    for d, sgn in ((0, -1), (1, 1)):
        S.op(S.pool, lambda h, sgn=sgn: h.affine_select(out=mtmp[:], in_=ones[:], pattern=[[-sgn, 64]], compare_op=ALU.is_ge, fill=0.0,
                                                         base=0, channel_multiplier=sgn), reads=[ones.res], writes=[mtmp.res])
        S.op(S.pool, lambda h, d=d: h.tensor_copy(out=msk[d][:], in_=mtmp[:]), reads=[mtmp.res], writes=[msk[d].res])
    Sf = [self.sb(f"Sf{d}", [64, 4, 128], F32) for d in range(2)]
    Sb = [self.sb(f"Sb{d}", [64, 4, 128], BF16) for d in range(2)]
    for d in range(2):
        S.op(S.pool, lambda h, d=d: h.memset(Sf[d][:], 0.0), writes=[Sf[d].res])
        S.op(S.pool, lambda h, d=d: h.memset(Sb[d][:], 0.0), writes=[Sb[d].res])
    qg = [[self.sb(f"qg{d}{i}", [64, 4, 512], BF16) for i in range(2)] for d in range(2)]
    kg = [[self.sb(f"kg{d}{i}", [64, 4, 512], BF16) for i in range(2)] for d in range(2)]
    kh = [[self.sb(f"kh{d}{i}", [64, 4, 512], BF16) for i in range(2)] for d in range(2)]
    vg = [[self.sb(f"vg{d}{i}", [64, 8, 512], BF16) for i in range(2)] for d in range(2)]
    am = [[self.sb(f"am{d}{i}", [64, 4, 64], BF16) for i in range(2)] for d in range(2)]
    kt = [[self.sb(f"kt{d}{i}", [64, 4, 64], BF16) for i in range(2)] for d in range(2)]
    ob = [[self.sb(f"ob{d}{i}", [64, 512], F32) for i in range(2)] for d in range(2)]
    pA = [self.ps(f"pA{d}", [64, 256]) for d in range(2)]
    pK = [self.ps(f"pK{d}", [64, 256], BF16) for d in range(2)]
    pO = [self.ps(f"pO{d}", [64, 512]) for d in range(2)]
    pN = [self.ps(f"pN{d}", [64, 512]) for d in range(2)]
    orders = [scan_order(T, d) for d in range(2)]
    nsteps = len(orders[0])
    gcount = [0, 0]
    cur = [None, None]

    def load_group(d, gi):
        i = gcount[d] % 2
        gcount[d] += 1
        u0, n = groups[gi]
        nch = n // 64
        for (dst, srcT) in ((qg[d][i], self.QG[l]), (kg[d][i], self.KG[l]), (kh[d][i], self.KH[l])):
            S.dma(S.sp, [(dst[:, :, :n], srcT[d, :, :, u0:u0 + n])], reads=[self.R(srcT.name)], writes=[dst.res])
        S.dma(S.sp, [(vg[d][i][:, :nch, :], self.GV[l][u0:u0 + n, :].rearrange("(c p) f -> p c f", p=64))], reads=[self.R(self.GV[l].name)],
              writes=[vg[d][i].res])
        return i

    for step in range(nsteps):
        for d in range(2):
            gi, c = orders[d][step]
            if cur[d] is None or cur[d][0] != gi:
                cur[d] = (gi, load_group(d, gi))
            bi = cur[d][1]
            u0, n = groups[gi]
            o = c * 64
            chunk = (u0 + o) // 64
            Q, Kg, Kh, V = qg[d][bi], kg[d][bi], kh[d][bi], vg[d][bi]
            k2 = step % 2
            AM, KTt, OB = am[d][k2], kt[d][k2], ob[d][k2]
            for hh in range(4):
                S.op(S.pe, lambda h, hh=hh, d=d, Kg=Kg, Q=Q, o=o: h.matmul(pA[d][:, hh * 64:(hh + 1) * 64], Kg[:, hh, o:o + 64], Q[:, hh, o:o + 64], start=True, stop=True),
                     reads=[Kg.res, Q.res], writes=[pA[d].res], inc=(hh == 3))
            for hh in range(4):
                S.op(S.pe, lambda h, hh=hh, d=d, Kh=Kh, o=o: h.transpose(out=pK[d][:, hh * 64:(hh + 1) * 64], in_=Kh[:, hh, o:o + 64], identity=self.ident[0:64, 0:64]),
                     reads=[Kh.res, self.ident.res], writes=[pK[d].res], inc=(hh == 3))
            S.op(S.dve, lambda h, d=d, AM=AM: h.tensor_tensor(out=AM[:], in0=pA[d][:].rearrange("p (a b) -> p a b", b=64),
                                                              in1=msk[d][:].unsqueeze(1).to_broadcast([64, 4, 64]), op=ALU.mult),
                 reads=[pA[d].res, msk[d].res], writes=[AM.res])
            S.op(S.act, lambda h, d=d, KTt=KTt: h.activation(out=KTt[:].rearrange("p a b -> p (a b)"), in_=pK[d][:], func=AF.Copy), reads=[pK[d].res], writes=[KTt.res])
            for hh in range(4):
                S.op(S.pe, lambda h, hh=hh, d=d, AM=AM, V=V, c=c: h.matmul(pO[d][:, hh * 128:(hh + 1) * 128], AM[:, hh, :], V[:, c, hh * 128:(hh + 1) * 128], start=True, stop=False),
                     reads=[AM.res, V.res], writes=[pO[d].res], inc=False)
                S.op(S.pe, lambda h, hh=hh, d=d, Q=Q, o=o: h.matmul(pO[d][:, hh * 128:(hh + 1) * 128], Q[:, hh, o:o + 64], Sb[d][:, hh, :], start=False, stop=True),
                     reads=[Q.res, Sb[d].res], writes=[pO[d].res], inc=(hh == 3))
            for hh in range(4):
                S.op(S.pe, lambda h, hh=hh, d=d, KTt=KTt, V=V, c=c: h.matmul(pN[d][:, hh * 128:(hh + 1) * 128], KTt[:, hh, :], V[:, c, hh * 128:(hh + 1) * 128], start=True, stop=True),
                     reads=[KTt.res, V.res], writes=[pN[d].res], inc=(hh == 3))
            S.op(S.act, lambda h, d=d, OB=OB: h.activation(out=OB[:], in_=pO[d][:], func=AF.Copy), reads=[pO[d].res], writes=[OB.res])
            S.dma(S.sp, [(self.OG[l][d, u0 + o:u0 + o + 64, :], OB[:])], reads=[OB.res], writes=[self.R(self.OG[l].name)])
            S.op(S.dve, lambda h, d=d, chunk=chunk: h.tensor_tensor(out=Sf[d][:], in0=Sf[d][:], in1=EL[:, :, d, chunk:chunk + 1].to_broadcast([64, 4, 128]), op=ALU.mult),
                 reads=[Sf[d].res, self.EL.res], writes=[Sf[d].res])
            S.op(S.dve, lambda h, d=d: h.tensor_tensor(out=Sf[d][:].rearrange("p a b -> p (a b)"), in0=Sf[d][:].rearrange("p a b -> p (a b)"), in1=pN[d][:], op=ALU.add),
                 reads=[Sf[d].res, pN[d].res], writes=[Sf[d].res])
            S.op(S.act, lambda h, d=d: h.activation(out=Sb[d][:], in_=Sf[d][:], func=AF.Copy), reads=[Sf[d].res], writes=[Sb[d].res])


Builder.phase_gla = phase_gla


LN_KS = float(-0.5 * np.log(128.0))


def phase_ml_gates(self, l):
    S = self.S
    sel = self.sel
    DEC = self.DEC
    T = self.T
    TT = self.TT
    nch = TT // 64
    bA = self.sb("bA", [4, TT], F32)
    bL = self.sb("bL", [4, TT], F32)
    bC = self.sb("bC", [4, TT], F32)
    bG = self.sb("bG", [4, TT], F32)
    bX = self.sb("bX", [4, TT], F32)
    onesr = self.sb("onesr", [4, TT], BF16)
    S.op(S.pool, lambda h: h.memset(onesr[:], 1.0), writes=[onesr.res])
    gl = self.sb("gl", [4, nch], F32)
    gp = self.sb("gp", [4, nch], F32)
    dd = self.sb("dd", [4, nch], F32)
    ibc = self.sb("ibc", [4, 2], F32)
    pD = self.ps("pD", [128, 512])
    for d in range(2):
        S.dma(S.sp, [(bA[:], self.GATES[l][d * 4:(d + 1) * 4, :])], reads=[self.R(self.GATES[l].name)], writes=[bA.res])
        S.dma(S.sp, [(bL[:], self.GATES[l][8 + d * 4:8 + (d + 1) * 4, :])], reads=[self.R(self.GATES[l].name)], writes=[bL.res])
        S.dma(S.sp, [(ibc[:, 0:1], self.mlstm_ib[l, d, :].rearrange("(h o) -> h o", o=1)), (ibc[:, 1:2], self.mlstm_fb[l, d, :].rearrange("(h o) -> h o", o=1))],
              writes=[ibc.res])
        S.op(S.dve, lambda h: h.tensor_scalar(out=ibc[:, 1:2], in0=ibc[:, 1:2], scalar1=-1.0, scalar2=None, op0=ALU.mult), reads=[ibc.res], writes=[ibc.res])
        S.op(S.act, lambda h: h.activation(out=bL[:], in_=bL[:], func=AF.Exp, scale=-1.0, bias=ibc[:, 1:2]), reads=[bL.res, ibc.res], writes=[bL.res])
        S.op(S.act, lambda h: h.activation(out=bL[:], in_=bL[:], func=AF.Ln, bias=1.0), reads=[bL.res], writes=[bL.res])

        def scan(out, src, op1):
            if d == 0:
                S.op(S.dve, lambda h: h.tensor_tensor_scan(out=out[:], data0=onesr[:], data1=src[:], initial=0.0, op0=ALU.mult, op1=op1),
                     reads=[src.res, onesr.res], writes=[out.res])
            else:
                S.op(S.dve, lambda h: h.tensor_tensor_scan(out=rev_last(out[:, 0:LC]), data0=onesr[:, 0:LC], data1=rev_last(src[:, 0:LC]), initial=0.0, op0=ALU.mult, op1=op1),
                     reads=[src.res, onesr.res], writes=[out.res])
                S.op(S.dve, lambda h: h.tensor_tensor_scan(out=rev_last(out[:, LC:TT]), data0=onesr[:, LC:TT], data1=rev_last(src[:, LC:TT]), initial=out[:, 0:1], op0=ALU.mult, op1=op1),
                     reads=[src.res, onesr.res, out.res], writes=[out.res])
        scan(bC, bL, ALU.add)
        S.op(S.dve, lambda h: h.scalar_tensor_tensor(out=bA[:], in0=bA[:], scalar=ibc[:, 0:1], in1=bC[:], op0=ALU.add, op1=ALU.add),
             reads=[bA.res, ibc.res, bC.res], writes=[bA.res])
        scan(bG, bA, ALU.max)
        G3 = bG[:].rearrange("p (c b) -> p c b", b=64)
        lastpos = 63 if d == 0 else 0
        S.op(S.dve, lambda h, lastpos=lastpos, G3=G3: h.tensor_copy(out=gl[:], in_=G3[:, :, lastpos]), reads=[bG.res], writes=[gl.res])
        S.op(S.dve, lambda h: h.memset(gp[:], 0.0), writes=[gp.res])
        if d == 0:
            S.op(S.dve, lambda h: h.tensor_copy(out=gp[:, 1:nch], in_=gl[:, 0:nch - 1]), reads=[gl.res], writes=[gp.res])
        else:
            S.op(S.dve, lambda h: h.tensor_copy(out=gp[:, 0:3], in_=gl[:, 1:4]), reads=[gl.res], writes=[gp.res])
            S.op(S.dve, lambda h: h.tensor_copy(out=gp[:, 4:nch - 1], in_=gl[:, 5:nch]), reads=[gl.res], writes=[gp.res])
            S.op(S.dve, lambda h: h.tensor_copy(out=gp[:, nch - 1:nch], in_=gl[:, 0:1]), reads=[gl.res], writes=[gp.res])
        L3 = bL[:].rearrange("p (c b) -> p c b", b=64)
        X3 = bX[:].rearrange("p (c b) -> p c b", b=64)
        A3 = bA[:].rearrange("p (c b) -> p c b", b=64)
        S.op(S.dve, lambda h, L3=L3, G3=G3: h.tensor_tensor(out=L3, in0=gp[:].unsqueeze(2).to_broadcast([4, nch, 64]), in1=G3, op=ALU.subtract),
             reads=[gp.res, bG.res], writes=[bL.res])
        S.op(S.act, lambda h: h.activation(out=bL[:], in_=bL[:], func=AF.Exp), reads=[bL.res], writes=[bL.res])
        S.op(S.dve, lambda h: h.tensor_tensor(out=bC[:], in0=bC[:], in1=bG[:], op=ALU.subtract), reads=[bC.res, bG.res], writes=[bC.res])
        S.op(S.act, lambda h: h.activation(out=bC[:], in_=bC[:], func=AF.Exp), reads=[bC.res], writes=[bC.res])
        S.op(S.dve, lambda h, X3=X3, A3=A3: h.tensor_tensor(out=X3, in0=A3, in1=gl[:].unsqueeze(2).to_broadcast([4, nch, 64]), op=ALU.subtract),
             reads=[bA.res, gl.res], writes=[bX.res])
        S.op(S.dve, lambda h: h.tensor_scalar(out=bX[:], in0=bX[:], scalar1=LN_KS, scalar2=None, op0=ALU.add), reads=[bX.res], writes=[bX.res])
        S.op(S.act, lambda h: h.activation(out=bX[:], in_=bX[:], func=AF.Exp), reads=[bX.res], writes=[bX.res])
        S.op(S.dve, lambda h: h.tensor_tensor(out=dd[:], in0=gp[:], in1=gl[:], op=ALU.subtract), reads=[gp.res, gl.res], writes=[dd.res])
        S.op(S.act, lambda h: h.activation(out=dd[:], in_=dd[:], func=AF.Exp), reads=[dd.res], writes=[dd.res])
        for hh in range(4):
            S.op(S.pe, lambda h, hh=hh: h.matmul(pD[:, :nch], sel[:, hh, :], dd[:], start=True, stop=True), reads=[self.sel.res, dd.res], writes=[pD.res])
            S.op(S.dve, lambda h, hh=hh, d=d: h.tensor_copy(out=DEC[:, d, hh, :], in_=pD[:, :nch]), reads=[pD.res], writes=[self.DEC.res])
        for qi, buf in enumerate((bA, bG, bL, bC, bX)):
            S.dma(S.sp, [(self.MROWS[l][d, qi, :, :], buf[:])], reads=[buf.res], writes=[self.R(self.MROWS[l].name)])


def phase_ml_conv(self, l):
    S = self.S
    T = self.T
    TT = self.TT
    wcol = self.sb("wcol", [128, 8, 5], F32)
    cb = self.sb("cb", [128, 8], F32)
    S.dma(S.sp, [(wcol[:, :, k], self.conv_w[l, k, :].rearrange("(fc p) -> p fc", p=128)) for k in range(5)], writes=[wcol.res], allow_slow_non_contiguous=True)
    S.dma(S.sp, [(cb[:], self.conv_b[l, :].rearrange("(fc p) -> p fc", p=128))], writes=[cb.res], allow_slow_non_contiguous=True)
    diagw = self.sb("diagw", [128, 8, 5, 128], BF16)
    for fc in range(8):
        for k in range(5):
            e = S.dve if (fc * 5 + k) % 2 == 0 else S.pool
            S.op(e, lambda h, fc=fc, k=k: h.tensor_scalar(out=diagw[:, fc, k, :], in0=self.identf[:], scalar1=wcol[:, fc, k:k + 1], scalar2=None, op0=ALU.mult),
                 reads=[self.identf.res, wcol.res], writes=[diagw.res])
    xq = [self.sb(f"xq{i}", [128, 8, 516], BF16) for i in range(2)]
    oc = [self.sb(f"oc{i}", [128, 512], BF16) for i in range(3)]
    pc = [self.ps(f"pc{i}", [128, 512]) for i in range(2)]
    MQKv = self.MQK[l].rearrange("(c p) t -> p c t", p=128)
    k2 = 0
    for gi, (u0, n) in enumerate(scan_groups(T)):
        X = xq[gi % 2]
        S.dma(S.sp, [(X[:, 0:4, 0:n + 4], MQKv[:, 0:4, u0:u0 + n + 4]), (X[:, 4:8, 0:n + 4], MQKv[:, 4:8, u0:u0 + n + 4])], reads=[self.R(self.MQK[l].name)], writes=[X.res])
        if u0 == 0 or u0 == LC:
            S.op(S.pool, lambda h, X=X: h.memset(X[:, :, 0:2], 0.0), writes=[X.res])
        if u0 + n == LC or u0 + n == TT:
            S.op(S.pool, lambda h, X=X, n=n: h.memset(X[:, :, n + 2:n + 4], 0.0), writes=[X.res])
        for fc in range(8):
            p = pc[k2 % 2]
            o_ = oc[k2 % 3]
            k2 += 1
            for k in range(5):
                S.op(S.pe, lambda h, fc=fc, k=k, p=p, X=X, n=n: h.matmul(p[:, :n], diagw[:, fc, k, :], X[:, fc, k:k + n], start=(k == 0), stop=(k == 4)),
                     reads=[diagw.res, X.res], writes=[p.res], inc=(k == 4))
            S.op(S.act, lambda h, fc=fc, p=p, o_=o_, n=n: h.activation(out=o_[:, :n], in_=p[:, :n], func=AF.Silu, bias=cb[:, fc:fc + 1]), reads=[p.res, cb.res], writes=[o_.res])
            S.dma(S.sp, [(self.MQC[l][fc * 128:(fc + 1) * 128, u0:u0 + n], o_[:, :n])], reads=[o_.res], writes=[self.R(self.MQC[l].name)])


def phase_ml_scan(self, l):
    S = self.S
    T = self.T
    sel = self.sel
    DEC = self.DEC
    groups = scan_groups(T)
    cfill = self.sb("cfill", [64, 64], F32)
    S.op(S.pool, lambda h: h.memset(cfill[:], LN_KS), writes=[cfill.res])
    mb = [self.sb(f"mb{d}", [64, 64], F32) for d in range(2)]
    for d, sgn in ((0, -1), (1, 1)):
        S.op(S.pool, lambda h, sgn=sgn, d=d: h.affine_select(out=mb[d][:], in_=cfill[:], pattern=[[-sgn, 64]], compare_op=ALU.is_ge, fill=-30000.0,
                                                              base=0, channel_multiplier=sgn), reads=[cfill.res], writes=[mb[d].res])
    negones = self.sb("negones", [4, 128], F32)
    S.op(S.pool, lambda h: h.memset(negones[:], -1.0), writes=[negones.res])
    posones = self.sb("posones", [4, 128], F32)
    S.op(S.pool, lambda h: h.memset(posones[:], 1.0), writes=[posones.res])
    mbr = [self.sb(f"mbr{d}", [64, 4, 64], F32) for d in range(2)]
    for d in range(2):
        S.op(S.pool, lambda h, d=d: h.tensor_copy(out=mbr[d][:], in_=mb[d][:].unsqueeze(1).to_broadcast([64, 4, 64])), reads=[mb[d].res], writes=[mbr[d].res])
    Dg = [self.sb(f"Dg{i}", [4, 3, 4, 512], F32) for i in range(2)]
    Cf = self.sb("Cf", [128, 4, 129], F32)
    Cb = self.sb("Cb", [128, 4, 129], BF16)
    qk = [self.sb(f"qk{i}", [128, 8, 512], BF16) for i in range(2)]
    vg = [self.sb(f"vgm{i}", [64, 8, 4, 129], BF16) for i in range(2)]
    rows = [self.sb(f"rows{i}", [4, 5, 512], F32) for i in range(2)]
    for v in vg:
        S.op(S.pool, lambda h, v=v: h.memset(v[:], 1.0), writes=[v.res])
    wT = [self.sb(f"wT{i}", [64, 256], F32) for i in range(2)]
    sT = [self.sb(f"sT{i}", [64, 4, 64], BF16) for i in range(2)]
    qks = [self.sb(f"qks{i}", [128, 8, 64], BF16) for i in range(2)]
    khat = [self.sb(f"khat{i}", [64, 4, 128], BF16) for i in range(2)]
    enm = [self.sb(f"enm{i}", [64, 4], F32) for i in range(2)]
    rr = [self.sb(f"rr{i}", [64, 4], F32) for i in range(2)]
    ho = [self.sb(f"ho{i}", [64, 4, 128], F32) for i in range(2)]
    pWS = self.ps("pWS", [64, 512])
    pB = self.ps("pB", [128, 8, 64])
    pK = self.ps("pK", [64, 4, 128], BF16)
    pO = self.ps("pO", [64, 1024])
    pN = self.ps("pN", [128, 1024])
    pO3 = pO[:].rearrange("p (h e) -> p h e", e=256)
    pN3 = pN[:].rearrange("p (h e) -> p h e", e=256)
    gcount = [0]

    def load_group(d, gi):
        i = gcount[0] % 2
        gcount[0] += 1
        u0, n = groups[gi]
        nchg = n // 64
        S.dma(S.sp, [(qk[i][:, 0:4, :n], self.MQC[l].rearrange("(c p) t -> p c t", p=128)[:, 0:4, u0:u0 + n]),
                     (qk[i][:, 4:8, :n], self.MQC[l].rearrange("(c p) t -> p c t", p=128)[:, 4:8, u0:u0 + n])], reads=[self.R(self.MQC[l].name)], writes=[qk[i].res])
        S.dma(S.sp, [(vg[i][:, c, :, 0:128], self.MV[l][u0 + c * 64:u0 + (c + 1) * 64, :].rearrange("p (h e) -> p h e", e=128)) for c in range(nchg)],
              reads=[self.R(self.MV[l].name)], writes=[vg[i].res])
        S.dma(S.sp, [(rows[i][:, :, :n], self.MROWS[l][d, :, :, u0:u0 + n].rearrange("q h t -> h q t"))], reads=[self.R(self.MROWS[l].name)], writes=[rows[i].res])
        for qd, qs in enumerate((1, 2, 4)):
            S.op(S.pool, lambda h, i=i, qd=qd, qs=qs, n=n: h.tensor_tensor(out=Dg[i][:, qd, :, :n], in0=rows[i][:, qs:qs + 1, :n].to_broadcast([4, 4, n]),
                                                                       in1=self.identf[0:4, 0:4].unsqueeze(2).to_broadcast([4, 4, n]), op=ALU.mult),
                 reads=[rows[i].res, self.identf.res], writes=[Dg[i].res])
        return i

    for d in range(2):
        S.op(S.pool, lambda h: h.memset(Cf[:], 0.0), writes=[Cf.res])
        S.op(S.pool, lambda h: h.memset(Cb[:], 0.0), writes=[Cb.res])
        order = scan_order(T, d)
        cur = None
        info = []
        for (gi, c) in order:
            if cur is None or cur[0] != gi:
                cur = (gi, None)
            info.append((gi, c))
        bufof = {}

        def stageA(step):
            gi, c = order[step]
            if gi not in bufof:
                bufof.clear()
                bufof[gi] = load_group(d, gi)
            bi = bufof[gi]
            o = c * 64
            k2 = step % 2
            QK, R_ = qk[bi], rows[bi]
            DG = Dg[bi]
            S.op(S.pe, lambda h, R_=R_, o=o: h.matmul(pWS[:, 0:256], R_[:, 0, o:o + 64], sel[:, :, 0:64], start=True, stop=False),
                 reads=[R_.res, sel.res], writes=[pWS.res], inc=False)
            S.op(S.pe, lambda h, DG=DG, o=o: h.matmul(pWS[:, 0:256], negones[:, 0:64], DG[:, 0, :, o:o + 64], start=False, stop=False),
                 reads=[DG.res, negones.res], writes=[pWS.res], inc=False)
            S.op(S.pe, lambda h, d=d: h.matmul(pWS[:, 0:256], self.identf[0:64, 0:64], mbr[d][:], start=False, stop=True),
                 reads=[self.identf.res, mbr[d].res], writes=[pWS.res], inc=False)
            for hh in range(4):
                S.op(S.pe, lambda h, hh=hh, QK=QK, o=o: h.matmul(pWS[:, 256 + hh * 64:256 + (hh + 1) * 64], QK[:, 4 + hh, o:o + 64], QK[:, hh, o:o + 64], start=True, stop=True),
                     reads=[QK.res], writes=[pWS.res], inc=(hh == 3))
            S.op(S.act, lambda h, k2=k2: h.activation(out=wT[k2][:], in_=pWS[:, 0:256], func=AF.Exp), reads=[pWS.res], writes=[wT[k2].res])
            S.op(S.dve, lambda h, k2=k2: h.tensor_tensor(out=sT[k2][:].rearrange("p a b -> p (a b)"), in0=pWS[:, 256:512], in1=wT[k2][:], op=ALU.mult),
                 reads=[pWS.res, wT[k2].res], writes=[sT[k2].res])
            S.op(S.pe, lambda h, DG=DG, o=o: h.matmul(pB[:], posones[:, :], DG[:, 1:3, :, o:o + 64], start=True, stop=True),
                 reads=[DG.res, posones.res], writes=[pB.res])
            S.op(S.dve, lambda h, k2=k2, QK=QK, o=o: h.tensor_tensor(out=qks[k2][:], in0=QK[:, :, o:o + 64], in1=pB[:], op=ALU.mult),
                 reads=[QK.res, pB.res], writes=[qks[k2].res])
            for hh in range(4):
                S.op(S.pe, lambda h, hh=hh, k2=k2: h.transpose(out=pK[:, hh, :], in_=qks[k2][:, 4 + hh, :], identity=self.ident[:]),
                     reads=[qks[k2].res, self.ident.res], writes=[pK.res], inc=(hh == 3))
            S.op(S.act, lambda h, k2=k2: h.activation(out=khat[k2][:], in_=pK[:], func=AF.Copy), reads=[pK.res], writes=[khat[k2].res])

        def stageB(step):
            gi, c = order[step]
            u0, n = groups[gi]
            o = c * 64
            chunk = (u0 + o) // 64
            k2 = step % 2
            V, R_ = vgbuf[step], rowbuf[step]
            for hh in range(4):
                S.op(S.pe, lambda h, hh=hh, k2=k2, V=V, c=c: h.matmul(pN[:, hh * 256:hh * 256 + 129], khat[k2][:, hh, :], V[:, c, hh, :], start=True, stop=True),
                     reads=[khat[k2].res, V.res], writes=[pN.res], inc=(hh == 3))
            S.op(S.pe, lambda h, R_=R_, o=o: h.matmul(pO[:, 200:204], R_[:, 3, o:o + 64], self.identf[0:4, 0:4], start=True, stop=True),
                 reads=[R_.res, self.identf.res], writes=[pO.res], inc=False)
            for hh in range(4):
                S.op(S.pe, lambda h, hh=hh, k2=k2, V=V, c=c: h.matmul(pO[:, hh * 256:hh * 256 + 129], sT[k2][:, hh, :], V[:, c, hh, :], start=True, stop=False),
                     reads=[sT[k2].res, V.res], writes=[pO.res], inc=False)
                S.op(S.pe, lambda h, hh=hh, k2=k2: h.matmul(pO[:, hh * 256:hh * 256 + 129], qks[k2][:, hh, :], Cb[:, hh, :], start=False, stop=True),
                     reads=[qks[k2].res, Cb.res], writes=[pO.res], inc=(hh == 3))
            S.op(S.act, lambda h, k2=k2: h.activation(out=enm[k2][:], in_=pO[:, 200:204], func=AF.Copy), reads=[pO.res], writes=[enm[k2].res])
            S.op(S.act, lambda h, k2=k2: h.activation(out=rr[k2][:], in_=pO3[:, :, 128], func=AF.Abs), reads=[pO.res], writes=[rr[k2].res])
            S.op(S.pool, lambda h, d=d, chunk=chunk: h.tensor_tensor(out=Cf[:], in0=Cf[:], in1=DEC[:, d, :, chunk:chunk + 1].to_broadcast([128, 4, 129]), op=ALU.mult),
                 reads=[Cf.res, DEC.res], writes=[Cf.res])
            S.op(S.dve, lambda h: h.tensor_tensor(out=Cf[:], in0=Cf[:], in1=pN3[:, :, 0:129], op=ALU.add), reads=[Cf.res, pN.res], writes=[Cf.res])
            S.op(S.act, lambda h: h.activation(out=Cb[:], in_=Cf[:], func=AF.Copy), reads=[Cf.res], writes=[Cb.res])
            S.op(S.dve, lambda h, k2=k2: h.tensor_tensor(out=rr[k2][:], in0=rr[k2][:], in1=enm[k2][:], op=ALU.max), reads=[rr[k2].res, enm[k2].res], writes=[rr[k2].res])
            S.op(S.dve, lambda h, k2=k2: h.reciprocal(out=rr[k2][:], in_=rr[k2][:]), reads=[rr[k2].res], writes=[rr[k2].res])
            S.op(S.dve, lambda h, k2=k2: h.tensor_tensor(out=ho[k2][:], in0=pO3[:, :, 0:128], in1=rr[k2][:].unsqueeze(2).to_broadcast([64, 4, 128]), op=ALU.mult),
                 reads=[pO.res, rr[k2].res], writes=[ho[k2].res])
            S.dma(S.sp, [(self.OM[l][d, u0 + o:u0 + o + 64, :], ho[k2][:].rearrange("p a b -> p (a b)"))], reads=[ho[k2].res], writes=[self.R(self.OM[l].name)])

        vgbuf = {}
        rowbuf = {}

        def A(step):
            stageA(step)
            gi, c = order[step]
            vgbuf[step] = vg[bufof[gi]]
            rowbuf[step] = rows[bufof[gi]]
        A(0)
        for step in range(len(order)):
            if step + 1 < len(order):
                A(step + 1)
            stageB(step)


Builder.phase_ml_gates = phase_ml_gates
Builder.phase_ml_conv = phase_ml_conv
Builder.phase_ml_scan = phase_ml_scan


def phase_merge(self, l, streams):
    S = self.S
    win = self.w_in[l].rearrange("(kc p) n -> p kc n", p=128)
    Wm = self.sb("Wm", [128, 8, 4096], BF16)
    Wmr = [Res(f"Wm{k}") for k in range(4)]
    for ki, k0 in enumerate(range(0, 8, 2)):
        S.dma(S.pool, [(Wm[:, k0:k0 + 2, 0:512], win[:, k0:k0 + 2, O_GR:O_GR + 512]),
                       (Wm[:, k0:k0 + 2, 512:1024], win[:, k0:k0 + 2, O_MO:O_MO + 512]),
                       (Wm[:, k0:k0 + 2, 1024:4096], win[:, k0:k0 + 2, O_SA:O_SA + 3072])], writes=[Wmr[ki]])
    Woa = self.sb("Woa", [64, 8, D], BF16)
    Wog = self.sb("Wog", [128, 4, D], BF16)
    Wom = self.sb("Wom", [128, 4, D], BF16)
    Wo = self.sb("Wo", [128, 8, D], BF16)
    S.dma(S.pool, [(Woa[:], self.w_out_attn[l].rearrange("(h p) n -> p h n", p=64))], writes=[Woa.res])
    S.dma(S.pool, [(Wog[:], self.w_out_gla[l].rearrange("(c p) n -> p c n", p=128))], writes=[Wog.res])
    S.dma(S.pool, [(Wom[:], self.w_out_mlstm[l].rearrange("(c p) n -> p c n", p=128))], writes=[Wom.res])
    S.dma(S.pool, [(Wo[:, 0:4, :], self.w_o[l].rearrange("(c p) n -> p c n", p=128)[:, 0:4, :]),
                   (Wo[:, 4:8, :], self.w_o[l].rearrange("(c p) n -> p c n", p=128)[:, 4:8, :])], writes=[Wo.res])
    gains = self.sb("gains", [128, 2, 128], F32)
    S.dma(S.sp, [(gains[:, 0, :], bcast_rows(self.gla_norm[l:l + 1, :], 128)), (gains[:, 1, :], bcast_rows(self.mlstm_norm[l:l + 1, :], 128))], writes=[gains.res])
    eps_col = self.eps_col
    hT = [self.sb(f"mhT{i}", [128, 8, 128], BF16) for i in range(2)]
    aTt = [self.sb(f"maT{i}", [64, 8, 128], BF16) for i in range(2)]
    og = [self.sb(f"mog{i}", [128, 2, 512], F32) for i in range(2)]
    om = [self.sb(f"mom{i}", [128, 2, 512], F32) for i in range(2)]
    xr = [self.sb(f"mxr{i}", [128, D], F32) for i in range(2)]
    gts = [self.sb(f"mgt{i}", [128, 8, 512], F32) for i in range(2)]
    sq = self.sb("msq", [128, 512], F32)
    ssqs = [self.sb(f"mssq{i}", [128, 2, 4], F32) for i in range(2)]
    bn = [self.sb(f"mbn{i}", [128, 512], F32) for i in range(2)]
    bbs = [[self.sb(f"mbb{i}{j}", [128, 512], BF16) for j in range(2)] for i in range(2)]
    bTs = [[self.sb(f"mbT{i}{j}", [128, 4, 128], BF16) for j in range(2)] for i in range(2)]
    yb = self.sb("myb", [128, D], BF16)
    yT = self.sb("myT", [128, 8, 128], BF16)
    t1 = [self.sb(f"mt1{i}", [128, 512], F32) for i in range(3)]
    G5 = self.sb("mG5", [128, D], F32)
    pg = [self.ps(f"mpg{i}", [128, 512]) for i in range(2)]
    pT1s = [self.ps(f"mpT1{j}", [128, 512], BF16) for j in range(2)]
    pT2 = self.ps("mpT2", [128, D], BF16)
    py = [self.ps(f"mpy{i}", [128, 512]) for i in range(3)]
    pY = py[0]
    cnt = {}

    def nxt(key, n):
        v = cnt.get(key, 0)
        cnt[key] = v + 1
        return v % n
    H2Tv = self.H2T[l].rearrange("(kc p) t -> p kc t", p=128)
    work = []
    for (tag, src, dst, ntok, row, uoff) in streams:
        for t0 in range(0, ntok, 128):
            work.append((tag, src, dst, row, uoff + t0, t0))

    def stage1(w, i):
        (tag, src, dst, row, u, t0) = work[w]
        H, AT, OGt, OMt, XR, gt, ssq = hT[i], aTt[i], og[i], om[i], xr[i], gts[i], ssqs[i]
        S.dma(S.sp, [(H[:], H2Tv[:, :, u:u + 128])], reads=[self.R(self.H2T[l].name)], writes=[H.res])
        S.dma(S.sp, [(AT[:], self.ATT[l][:, :, u:u + 128])], reads=[self.R(self.ATT[l].name)], writes=[AT.res])
        S.dma(S.sp, [(OGt[:, 0, :], self.OG[l][0, u:u + 128, :]), (OGt[:, 1, :], self.OG[l][1, u:u + 128, :])], reads=[self.R(self.OG[l].name)], writes=[OGt.res])
        S.dma(S.sp, [(OMt[:, 0, :], self.OM[l][0, u:u + 128, :]), (OMt[:, 1, :], self.OM[l][1, u:u + 128, :])], reads=[self.R(self.OM[l].name)], writes=[OMt.res])
        S.dma(S.sp, [(XR[:], src[t0:t0 + 128, :])], reads=[self.R(src.name)], writes=[XR.res])
        for blk in range(8):
            p = pg[nxt("pg", 2)]
            for kc in range(8):
                S.op(S.pe, lambda h, kc=kc, blk=blk, p=p, H=H: h.matmul(p[:], H[:, kc, :], Wm[:, kc, blk * 512:(blk + 1) * 512], start=(kc == 0), stop=(kc == 7)),
                     reads=[H.res] + Wmr, writes=[p.res], inc=(kc == 7))
            fn = AF.Silu if blk == 0 else AF.Sigmoid
            S.op(S.act, lambda h, blk=blk, p=p, fn=fn, gt=gt: h.activation(out=gt[:, blk, :], in_=p[:], func=fn), reads=[p.res], writes=[gt.res])

    def stage1c(w, i):
        OGt, OMt, gt, ssq = og[i], om[i], gts[i], ssqs[i]
        for br, Ot in enumerate((OGt, OMt)):
            S.op(S.pool, lambda h, Ot=Ot: h.tensor_tensor(out=Ot[:, 0, :], in0=Ot[:, 0, :], in1=Ot[:, 1, :], op=ALU.add), reads=[Ot.res], writes=[Ot.res])
            S.op(S.act, lambda h, Ot=Ot: h.activation(out=sq[:], in_=Ot[:, 0, :], func=AF.Square), reads=[Ot.res], writes=[sq.res])
            S.op(S.dve, lambda h, br=br, ssq=ssq: h.tensor_reduce(out=ssq[:, br, :], in_=sq[:].rearrange("p (a b) -> p a b", b=128), axis=AX.X, op=ALU.add),
                 reads=[sq.res], writes=[ssq.res])
        S.op(S.act, lambda h, ssq=ssq: h.activation(out=ssq[:], in_=ssq[:], func=AF.Sqrt, scale=1.0 / 128, bias=eps_col[:]),
             reads=[ssq.res, eps_col.res], writes=[ssq.res])
        S.op(S.dve, lambda h, ssq=ssq: h.reciprocal(out=ssq[:], in_=ssq[:]), reads=[ssq.res], writes=[ssq.res])
        for br, Ot in enumerate((OGt, OMt)):
            B_ = bn[br]
            S.op(S.dve, lambda h, br=br, Ot=Ot, B_=B_, ssq=ssq: h.tensor_tensor(out=B_[:].rearrange("p (a b) -> p a b", b=128), in0=Ot[:, 0, :].rearrange("p (a b) -> p a b", b=128),
                                                                              in1=ssq[:, br, :].unsqueeze(2).to_broadcast([128, 4, 128]), op=ALU.mult),
                 reads=[Ot.res, ssq.res], writes=[B_.res])
            S.op(S.pool, lambda h, br=br, B_=B_: h.tensor_tensor(out=B_[:].rearrange("p (a b) -> p a b", b=128), in0=B_[:].rearrange("p (a b) -> p a b", b=128),
                                                              in1=gains[:, br:br + 1, :].to_broadcast([128, 4, 128]), op=ALU.mult),
                 reads=[B_.res, gains.res], writes=[B_.res])
            BB = bbs[i][br]
            S.op(S.dve, lambda h, br=br, B_=B_, BB=BB, gt=gt: h.tensor_tensor(out=BB[:], in0=B_[:], in1=gt[:, br, :], op=ALU.mult), reads=[B_.res, gt.res], writes=[BB.res])

    def stage1b(w, i):
        for br in range(2):
            BB = bbs[i][br]
            pT1 = pT1s[br]
            for c in range(4):
                S.op(S.pe, lambda h, c=c, BB=BB, pT1=pT1: h.transpose(out=pT1[:, c * 128:(c + 1) * 128], in_=BB[:, c * 128:(c + 1) * 128], identity=self.ident[:]),
                     reads=[BB.res, self.ident.res], writes=[pT1.res], inc=(c == 3))
            BT = bTs[i][br]
            if br == 0:
                S.op(S.act, lambda h, BT=BT, pT1=pT1: h.activation(out=BT[:].rearrange("p a b -> p (a b)"), in_=pT1[:], func=AF.Copy), reads=[pT1.res], writes=[BT.res])
            else:
                S.op(S.dve, lambda h, BT=BT, pT1=pT1: h.tensor_copy(out=BT[:].rearrange("p a b -> p (a b)"), in_=pT1[:]), reads=[pT1.res], writes=[BT.res])

    cur_row = [None]

    def stage2(w, i):
        (tag, src, dst, row, u, t0) = work[w]
        AT, XR, gt = aTt[i], xr[i], gts[i]
        if cur_row[0] != row:
            cur_row[0] = row
            srcg = self.MOD[l][row:row + 1, 5 * D:6 * D]
            S.dma(S.sp, [(G5[:], dram_ap(srcg, srcg.offset, [[0, 128], [1, D]]))], reads=[self.R("MOD", l)], writes=[G5.res])
        for half in range(2):
            cs_ = slice(half * 512, (half + 1) * 512)
            for hh in range(8):
                S.op(S.pe, lambda h, hh=hh, AT=AT, cs_=cs_: h.matmul(py[0][:], AT[:, hh, :], Woa[:, hh, cs_], start=(hh == 0), stop=(hh == 7)),
                     reads=[AT.res, Woa.res], writes=[py[0].res], inc=(hh == 7))
            for c in range(4):
                S.op(S.pe, lambda h, c=c, cs_=cs_, BT=bTs[i][0]: h.matmul(py[1][:], BT[:, c, :], Wog[:, c, cs_], start=(c == 0), stop=(c == 3)),
                     reads=[bTs[i][0].res, Wog.res], writes=[py[1].res], inc=(c == 3))
            for c in range(4):
                S.op(S.pe, lambda h, c=c, cs_=cs_, BT=bTs[i][1]: h.matmul(py[2][:], BT[:, c, :], Wom[:, c, cs_], start=(c == 0), stop=(c == 3)),
                     reads=[bTs[i][1].res, Wom.res], writes=[py[2].res], inc=(c == 3))
            S.op(S.dve, lambda h, half=half, gt=gt: h.tensor_tensor(out=t1[0][:], in0=py[0][:], in1=gt[:, 2 + half, :], op=ALU.mult), reads=[py[0].res, gt.res], writes=[t1[0].res])
            S.op(S.dve, lambda h, half=half, gt=gt: h.tensor_tensor(out=t1[1][:], in0=py[1][:], in1=gt[:, 4 + half, :], op=ALU.mult), reads=[py[1].res, gt.res], writes=[t1[1].res])
            S.op(S.dve, lambda h, half=half, gt=gt: h.tensor_tensor(out=t1[2][:], in0=py[2][:], in1=gt[:, 6 + half, :], op=ALU.mult), reads=[py[2].res, gt.res], writes=[t1[2].res])
            S.op(S.pool, lambda h: h.tensor_tensor(out=t1[0][:], in0=t1[0][:], in1=t1[1][:], op=ALU.add), reads=[t1[0].res, t1[1].res], writes=[t1[0].res])
            S.op(S.pool, lambda h, cs_=cs_: h.tensor_tensor(out=yb[:, cs_], in0=t1[0][:], in1=t1[2][:], op=ALU.add), reads=[t1[0].res, t1[2].res], writes=[yb.res])
        for kc in range(8):
            S.op(S.pe, lambda h, kc=kc: h.transpose(out=pT2[:, kc * 128:(kc + 1) * 128], in_=yb[:, kc * 128:(kc + 1) * 128], identity=self.ident[:]),
                 reads=[yb.res, self.ident.res], writes=[pT2.res], inc=(kc == 7))
        S.op(S.act, lambda h: h.activation(out=yT[:].rearrange("p a b -> p (a b)"), in_=pT2[:], func=AF.Copy), reads=[pT2.res], writes=[yT.res])
        for half in range(2):
            cs_ = slice(half * 512, (half + 1) * 512)
            for kc in range(8):
                S.op(S.pe, lambda h, kc=kc, cs_=cs_: h.matmul(pY[:], yT[:, kc, :], Wo[:, kc, cs_], start=(kc == 0), stop=(kc == 7)),
                     reads=[yT.res, Wo.res], writes=[pY.res], inc=(kc == 7))
            S.op(S.dve, lambda h, cs_=cs_: h.tensor_tensor(out=t1[0][:], in0=pY[:], in1=G5[:, cs_], op=ALU.mult), reads=[pY.res, G5.res], writes=[t1[0].res])
            S.op(S.pool, lambda h, cs_=cs_, XR=XR: h.tensor_tensor(out=XR[:, cs_], in0=XR[:, cs_], in1=t1[0][:], op=ALU.add), reads=[XR.res, t1[0].res], writes=[XR.res])
        S.dma(S.sp, [(dst[t0:t0 + 128, :], XR[:])], reads=[XR.res], writes=[self.R(dst.name)])

    stage1(0, 0)
    stage1c(0, 0)
    stage1b(0, 0)
    for w in range(len(work)):
        if w + 1 < len(work):
            stage1(w + 1, (w + 1) % 2)
        stage2(w, w % 2)
        if w + 1 < len(work):
            stage1c(w + 1, (w + 1) % 2)
            stage1b(w + 1, (w + 1) % 2)


Builder.phase_merge = phase_merge

_NC_CACHE = {}


def kernel(**inputs):
    inp = {k: np.asarray(v) for k, v in inputs.items()}
    Bsz, SEQ, _ = inp["x"].shape
    T = SEQ
    if T not in _NC_CACHE:
        _NC_CACHE[T] = Builder(T).build()
    nc = _NC_CACHE[T]
    in_maps = [make_in_map(inp, b, 0, T) for b in range(Bsz)]
    res = run_bass_kernel_spmd(nc, in_maps, core_ids=list(range(Bsz)))
    out = np.stack([np.asarray(r["y"], dtype=np.float32) for r in res.results], axis=0)
    return out


W_NAMES = ["mod_w", "mod_b", "norm_g", "ffn1_w13", "ffn1_w2", "ffn2_w13", "ffn2_w2", "w_in", "attn_q_norm", "attn_k_norm", "attn_sink", "gla_w2", "gla_b", "mlstm_conv_w", "mlstm_conv_b", "mlstm_ib", "mlstm_fb", "gla_norm", "mlstm_norm", "w_out_attn", "w_out_gla", "w_out_mlstm", "w_o"]


def make_in_map(inp, b, t0, T):
    m = {"x": np.ascontiguousarray(inp["x"][b, t0:t0 + T]), "c": np.ascontiguousarray(inp["c"][b]),
         "ctx": np.ascontiguousarray(inp["ctx"][b]), "c_ctx": np.ascontiguousarray(inp["c_ctx"])}
    for k in W_NAMES:
        m[k] = np.ascontiguousarray(inp[k])
    m["rope_cs"] = rope_table(t0, T)
    return m


def rope_table(t0, T):
    pos = np.arange(t0, t0 + T)
    r = (pos // 64).astype(np.float32)
    col = (pos % 64).astype(np.float32)
    inv = (np.float32(10000.0) ** (-np.arange(16, dtype=np.float32) / np.float32(16))).astype(np.float32)
    ang = np.concatenate([r[:, None] * inv, col[:, None] * inv], axis=-1).astype(np.float32)
    return np.ascontiguousarray(np.stack([np.cos(ang), np.sin(ang)], axis=1).astype(np.float32))
```

```python
import numpy as np
from contextlib import ExitStack
import concourse.bass as bass
import concourse.mybir as mybir
from concourse.bass_utils import run_bass_kernel_spmd

F32 = mybir.dt.float32
BF16 = mybir.dt.bfloat16
AF = mybir.ActivationFunctionType
ALU = mybir.AluOpType
AX = mybir.AxisListType

D = 1024
DFF = 2816
NMOD = 9
LC = 256
EPS = 1e-6
DEPTH = 2
D_IN = 7472


class Res:
    __slots__ = ("name", "w", "r")

    def __init__(self, name=""):
        self.name = name
        self.w = None
        self.r = []


class Eng:
    def __init__(self, name, is_pe=False):
        self.name = name
        self.is_pe = is_pe
        self.ops = []
        self.sems = []
        self.si = 0
        self.cnt = 0
        self.seen = {}
        self.pend_r = []
        self.pend_w = []
        self.pool = []
        self.pi = 0


ROT = 30000


class Sched:
    def __init__(self, nc, es):
        self.nc = nc
        self.es = es
        self.pe = Eng("pe", True)
        self.act = Eng("act")
        self.dve = Eng("dve")
        self.pool = Eng("pool")
        self.sp = Eng("sp")
        self.engs = [self.pe, self.act, self.dve, self.pool, self.sp]
        self.semid = {}
        n_rot = {"pe": 6, "act": 3, "dve": 3, "pool": 3, "sp": 1}
        for e in self.engs:
            for i in range(n_rot[e.name]):
                s = es.enter_context(nc.semaphore(f"s_{e.name}{i}"))
                e.sems.append(s)
        for e, n in ((self.sp, 20), (self.pool, 10), (self.act, 4)):
            for i in range(n):
                s = es.enter_context(nc.semaphore(f"d_{e.name}{i}"))
                e.pool.append([s, 0])
        self.n_ops = 0

    def _need(self, eng, tok, raw):
        if tok is None:
            return None
        sem, val, owner = tok
        if owner == eng.name:
            if eng.is_pe:
                return None
            if not raw:
                return None
        key = id(sem)
        if eng.seen.get(key, 0) >= val:
            return None
        eng.seen[key] = val
        return (sem, val)

    def _waits(self, eng, reads, writes):
        ws = []
        for r in reads:
            w = self._need(eng, r.w, True)
            if w:
                ws.append(w)
        for wr in writes:
            w = self._need(eng, wr.w, False)
            if w:
                ws.append(w)
            for t in wr.r:
                w = self._need(eng, t, False)
                if w:
                    ws.append(w)
        for (sem, val) in ws:
            eng.ops.append(lambda h, sem=sem, val=val: h.wait_ge(sem, val))

    def _record(self, tok, reads, writes):
        for r in reads:
            r.r = [t for t in r.r if t[2] != tok[2] or t[0] is not tok[0]] + [tok]
        for w in writes:
            w.w = tok
            w.r = []

    def op(self, eng, fn, reads=(), writes=(), inc=True):
        self.n_ops += 1
        reads = list(reads)
        writes = list(writes)
        self._waits(eng, reads, writes)
        if not inc:
            eng.ops.append(lambda h, fn=fn: fn(h))
            eng.pend_r += reads
            eng.pend_w += writes
            return
        if eng.cnt >= ROT:
            eng.si += 1
            eng.cnt = 0
        eng.cnt += 1
        sem = eng.sems[eng.si]
        tok = (sem, eng.cnt, eng.name)
        eng.ops.append(lambda h, fn=fn, sem=sem: fn(h).then_inc(sem, 1))
        self._record(tok, reads + eng.pend_r, writes + eng.pend_w)
        eng.pend_r = []
        eng.pend_w = []

    def dma(self, eng, pairs, reads=(), writes=(), **kw):
        self.n_ops += 1
        reads = list(reads)
        writes = list(writes)
        self._waits(eng, reads, writes)
        ent = eng.pool[eng.pi]
        eng.pi = (eng.pi + 1) % len(eng.pool)
        sem = ent[0]
        if ent[1] > 0 and eng.seen.get(id(sem), 0) < ent[1]:
            v = ent[1]
            eng.ops.append(lambda h, sem=sem, v=v: h.wait_ge(sem, v))
            eng.seen[id(sem)] = v
        for (o, i) in pairs:
            ent[1] += 16
            eng.ops.append(lambda h, o=o, i=i, sem=sem: h.dma_start(out=o, in_=i, **kw).then_inc(sem, 16))
        tok = (sem, ent[1], "dma_" + eng.name + str(id(sem)))
        self._record(tok, reads, writes)

    def barrier(self):
        toks = []
        for e in self.engs:
            assert not e.pend_r and not e.pend_w, e.name
            for i in range(e.si + 1):
                v = ROT if i < e.si else e.cnt
                if v > 0:
                    toks.append((e, e.sems[i], v))
            for ent in e.pool:
                if ent[1] > 0:
                    toks.append((None, ent[0], ent[1]))
        for e in self.engs:
            for (own, sem, v) in toks:
                if own is e:
                    continue
                if e.seen.get(id(sem), 0) >= v:
                    continue
                e.seen[id(sem)] = v
                e.ops.append(lambda h, sem=sem, v=v: h.wait_ge(sem, v))

    def finish(self):
        for e in (self.sp, self.pool, self.act):
            for ent in e.pool:
                if ent[1] > 0:
                    self.sp.ops.append(lambda h, sem=ent[0], v=ent[1]: h.wait_ge(sem, v))

    def replay(self):
        nc = self.nc
        with nc.Block() as block:
            @block.tensor
            def _(h):
                for f in self.pe.ops:
                    f(h)

            @block.scalar
            def _(h):
                for f in self.act.ops:
                    f(h)

            @block.vector
            def _(h):
                for f in self.dve.ops:
                    f(h)

            @block.gpsimd
            def _(h):
                for f in self.pool.ops:
                    f(h)

            @block.sync
            def _(h):
                for f in self.sp.ops:
                    f(h)


class Tile:
    def __init__(self, t, name):
        self.t = t
        self.res = Res(name)

    def __getitem__(self, k):
        return self.t[k]


def dram_ap(t, offset, pattern):
    return bass.AP(t.tensor, offset, pattern)


class Builder:
    def __init__(self, T, depth=DEPTH, stop=None, dbg=()):
        self.T = T
        self.depth = depth
        self.stop = stop
        self.dbg = dbg
        self.nc = bass.Bass("TRN2", target_bir_lowering=False)
        self.es = ExitStack()
        self.S = None
        self.dres = {}
        self.rr = {}

    def din(self, name, shape):
        return self.nc.dram_tensor(name, list(shape), F32, kind="ExternalInput").ap()

    def dout(self, name, shape, dt=F32):
        return self.nc.dram_tensor(name, list(shape), dt, kind="ExternalOutput").ap()

    def dscr(self, name, shape, dt=F32):
        if name in self.dbg:
            return self.nc.dram_tensor(name, list(shape), dt, kind="ExternalOutput").ap()
        return self.nc.dram_tensor(name, list(shape), dt).ap()

    def R(self, *key):
        if key not in self.dres:
            self.dres[key] = Res(str(key))
        return self.dres[key]

    def sb(self, name, shape, dt):
        self.uid = getattr(self, "uid", 0) + 1
        name = f"{name}_{self.uid}"
        t = self.cur.enter_context(self.nc.sbuf_tensor(name, list(shape), dt))
        return Tile(t, name)

    def ps(self, name, shape, dt=F32):
        self.uid = getattr(self, "uid", 0) + 1
        name = f"{name}_{self.uid}"
        t = self.cur.enter_context(self.nc.psum_tensor(name, list(shape), dt))
        return Tile(t, name)

    def build(self):
        nc = self.nc
        T = self.T
        L = self.depth
        with self.es as es:
            self.S = S = Sched(nc, es)
            self.x_in = self.din("x", [T, D])
            self.c_in = self.din("c", [D])
            self.ctx_in = self.din("ctx", [LC, D])
            self.cctx_in = self.din("c_ctx", [D])
            self.mod_w = self.din("mod_w", [L, D, NMOD * D])
            self.mod_b = self.din("mod_b", [L, NMOD * D])
            self.norm_g = self.din("norm_g", [L, 3, D])
            self.ffn_w13 = [self.din("ffn1_w13", [L, D, 2 * DFF]), self.din("ffn2_w13", [L, D, 2 * DFF])]
            self.ffn_w2 = [self.din("ffn1_w2", [L, DFF, D]), self.din("ffn2_w2", [L, DFF, D])]
            self.w_in = self.din("w_in", [L, D, D_IN])
            self.attn_q_norm = self.din("attn_q_norm", [L, 64])
            self.attn_k_norm = self.din("attn_k_norm", [L, 64])
            self.attn_sink = self.din("attn_sink", [L, 8])
            self.gla_w2 = self.din("gla_w2", [L, 2, 16, 256])
            self.gla_b = self.din("gla_b", [L, 2, 256])
            self.rope_cs = self.din("rope_cs", [T, 2, 32])
            self.y_out = self.dout("y", [T, D])
            TT = self.TT = LC + T
            self.H2T = [self.dscr(f"H2T{l}", [D, TT], BF16) for l in range(L)]
            self.QT = [self.dscr(f"QT{l}", [64, 8, TT], BF16) for l in range(L)]
            self.KT = [self.dscr(f"KT{l}", [64, 2, TT], BF16) for l in range(L)]
            self.VA = [self.dscr(f"VA{l}", [TT, 128], BF16) for l in range(L)]
            self.GV = [self.dscr(f"GV{l}", [TT, 512], BF16) for l in range(L)]
            self.MV = [self.dscr(f"MV{l}", [TT, 512], BF16) for l in range(L)]
            self.GATES = [self.dscr(f"GATES{l}", [16, TT]) for l in range(L)]
            self.QG = [self.dscr(f"QG{l}", [2, 64, 4, TT], BF16) for l in range(L)]
            self.KG = [self.dscr(f"KG{l}", [2, 64, 4, TT], BF16) for l in range(L)]
            self.KH = [self.dscr(f"KH{l}", [2, 64, 4, TT], BF16) for l in range(L)]
            self.MQK = [self.dscr(f"MQK{l}", [D, TT + 4], BF16) for l in range(L)]
            self.ATT = [self.dscr(f"ATT{l}", [64, 8, TT], BF16) for l in range(L)]
            self.OG = [self.dscr(f"OG{l}", [2, TT, 512]) for l in range(L)]
            self.OM = [self.dscr(f"OM{l}", [2, TT, 512]) for l in range(L)]
            self.MROWS = [self.dscr(f"MROWS{l}", [2, 5, 4, TT]) for l in range(L)]
            self.MQC = [self.dscr(f"MQC{l}", [D, TT], BF16) for l in range(L)]
            self.gla_norm = self.din("gla_norm", [L, 128])
            self.mlstm_norm = self.din("mlstm_norm", [L, 128])
            self.w_out_attn = self.din("w_out_attn", [L, 512, D])
            self.w_out_gla = self.din("w_out_gla", [L, 512, D])
            self.w_out_mlstm = self.din("w_out_mlstm", [L, 512, D])
            self.w_o = self.din("w_o", [L, D, D])
            self.X2 = [self.dscr(f"X2_{l}", [T, D]) for l in range(L)]
            self.C2 = [self.dscr(f"C2_{l}", [LC, D]) for l in range(L)]
            self.X3 = [self.dscr(f"X3_{l}", [T, D]) for l in range(L)]
            self.C3 = [self.dscr(f"C3_{l}", [LC, D]) for l in range(L)]
            self.conv_w = self.din("mlstm_conv_w", [L, 5, D])
            self.conv_b = self.din("mlstm_conv_b", [L, D])
            self.mlstm_ib = self.din("mlstm_ib", [L, 2, 4])
            self.mlstm_fb = self.din("mlstm_fb", [L, 2, 4])
            self.MOD = [self.dscr(f"MOD{l}", [2, NMOD * D]) for l in range(L)]
            self.X1 = [self.dscr(f"X1_{l}", [T, D]) for l in range(L)]
            self.C1 = [self.dscr(f"C1_{l}", [LC, D]) for l in range(L)]
            with ExitStack() as cst:
                self.cur = cst
                self.ident = self.sb("ident", [128, 128], BF16)
                self.identf = self.sb("identf", [128, 128], F32)
                self.eps_col = self.sb("eps_col", [128, 1], F32)
                S.op(S.dve, lambda h: h.memset(self.eps_col[:], EPS), writes=[self.eps_col.res])
                self.make_consts()
                for l in range(L):
                    xin = self.x_in if l == 0 else self.X3[l - 1]
                    cin = self.ctx_in if l == 0 else self.C3[l - 1]
                    with ExitStack() as ph:
                        self.cur = ph
                        self.phase_mod(l)
                        S.barrier()
                    if self.stop == ("mod", l):
                        break
                    with ExitStack() as ph:
                        self.cur = ph
                        self.phase_ffn(l, 0, [("ctx", cin, self.C1[l], LC, 1), ("lat", xin, self.X1[l], T, 0)])
                        S.barrier()
                    if self.stop == ("ffn1", l):
                        break
                    with ExitStack() as lay:
                        self.cur = lay
                        self.EL = self.sb("EL", [64, 4, 2, TT // 64], F32)
                        with ExitStack() as ph:
                            self.cur = ph
                            self.phase_feat(l, [("ctx", self.C1[l], LC, 1, 0, False), ("lat", self.X1[l], T, 0, LC, True)])
                            S.barrier()
                        if self.stop == ("feat", l):
                            break
                        with ExitStack() as ph:
                            self.cur = ph
                            self.phase_attn(l, l < L - 1)
                            S.barrier()
                        if self.stop == ("attn", l):
                            break
                        with ExitStack() as ph:
                            self.cur = ph
                            self.phase_gla(l)
                            S.barrier()
                        if self.stop == ("gla", l):
                            break
                        self.cur = lay
                        self.DEC = self.sb("DEC", [128, 2, 4, TT // 64], F32)
                        self.sel = self.sb("sel", [4, 4, 128], F32)
                        S.op(S.dve, lambda h, sel_t=self.sel: h.tensor_copy(out=sel_t[:], in_=self.identf[0:4, 0:4].unsqueeze(2).to_broadcast([4, 4, 128])),
                             reads=[self.identf.res], writes=[self.sel.res])
                        stop_ml = False
                        for ph_name, ph_fn in (("mlg", self.phase_ml_gates), ("mlc", self.phase_ml_conv), ("mls", self.phase_ml_scan)):
                            with ExitStack() as ph:
                                self.cur = ph
                                ph_fn(l)
                                S.barrier()
                            if self.stop == (ph_name, l):
                                stop_ml = True
                                break
                        if stop_ml:
                            break
                        if self.stop == ("ml", l):
                            break
                    last = (l == L - 1)
                    with ExitStack() as ph:
                        self.cur = ph
                        st = [("lat", self.X1[l], self.X2[l], T, 0, LC)]
                        if not last:
                            st = [("ctx", self.C1[l], self.C2[l], LC, 1, 0)] + st
                        self.phase_merge(l, st)
                        S.barrier()
                    if self.stop == ("merge", l):
                        break
                    with ExitStack() as ph:
                        self.cur = ph
                        xdst = self.y_out if last else self.X3[l]
                        st = [("lat", self.X2[l], xdst, T, 0)]
                        import os
                        if not last and not os.environ.get("NOCTX2"):
                            st = [("ctx", self.C2[l], self.C3[l], LC, 1)] + st
                        self.phase_ffn(l, 1, [(a, b, c, d_, e) for (a, b, c, d_, e) in st])
                        S.barrier()
                    if self.stop == ("ffn2", l):
                        break
                S.finish()
                S.replay()
        return nc

    def make_consts(self):
        S = self.S
        nc = self.nc
        idf = self.identf
        S.op(S.pool, lambda h: h.memset(idf[:], 0.0), writes=[idf.res])
        S.op(S.pool, lambda h: h.affine_select(out=idf[:], in_=idf[:], pattern=[[-1, 128]],
                                                compare_op=ALU.not_equal, fill=1.0, base=0,
                                                channel_multiplier=1),
             reads=[idf.res], writes=[idf.res])
        S.op(S.dve, lambda h: h.tensor_copy(out=self.ident[:], in_=idf[:]), reads=[idf.res], writes=[self.ident.res])

    def phase_mod(self, l):
        S = self.S
        cl = self.sb("cl", [128, 8, 2], F32)
        cs = self.sb("cs", [128, 8, 2], F32)
        S.dma(S.sp, [(cl[:, :, 0], self.c_in.rearrange("(kc p) -> p kc", p=128)),
                     (cl[:, :, 1], self.cctx_in.rearrange("(kc p) -> p kc", p=128))],
              writes=[cl.res], allow_slow_non_contiguous=True)
        S.op(S.act, lambda h: h.activation(out=cs[:], in_=cl[:], func=AF.Silu), reads=[cl.res], writes=[cs.res])
        wm = [self.sb(f"wm{i}", [128, 8, 512], F32) for i in range(2)]
        mb = [self.sb(f"mb{i}", [2, 512], F32) for i in range(2)]
        mo = [self.sb(f"mo{i}", [2, 512], F32) for i in range(2)]
        pm = [self.ps(f"pm{i}", [2, 512]) for i in range(2)]
        mw = self.mod_w[l].rearrange("(kc p) n -> p kc n", p=128)
        for n in range(18):
            i = n % 2
            S.dma(S.sp, [(wm[i][:, 0:4, :], mw[:, 0:4, n * 512:(n + 1) * 512]),
                         (wm[i][:, 4:8, :], mw[:, 4:8, n * 512:(n + 1) * 512])], writes=[wm[i].res])
            mbsrc = self.mod_b[l:l + 1, n * 512:(n + 1) * 512]
            S.dma(S.sp, [(mb[i][0:1, :], mbsrc), (mb[i][1:2, :], mbsrc)], writes=[mb[i].res])
            for kc in range(8):
                S.op(S.pe, lambda h, kc=kc, i=i: h.matmul(pm[i][:], cs[:, kc, :], wm[i][:, kc, :],
                                                           start=(kc == 0), stop=(kc == 7)),
                     reads=[cs.res, wm[i].res], writes=[pm[i].res], inc=(kc == 7))
            S.op(S.dve, lambda h, i=i: h.tensor_tensor(out=mo[i][:], in0=pm[i][:], in1=mb[i][:], op=ALU.add),
                 reads=[pm[i].res, mb[i].res], writes=[mo[i].res])
            S.dma(S.sp, [(self.MOD[l][:, n * 512:(n + 1) * 512], mo[i][:])], reads=[mo[i].res],
                  writes=[self.R("MOD", l)])

    def load_cols(self, dst_ap, src_row_ap, res):
        self.S.dma(self.S.sp, [(dst_ap, src_row_ap.rearrange("(kc p) -> p kc", p=128))], writes=[res],
                   allow_slow_non_contiguous=True)

    def adaln_cols(self, l, j, row, tag):
        S = self.S
        tmp = self.sb(f"adt_{tag}", [128, 3, 8], F32)
        A = self.sb(f"adA_{tag}", [128, 8], F32)
        MODr = self.MOD[l]
        S.dma(S.sp, [(tmp[:, 0, :], MODr[row, (3 * j) * D:(3 * j + 1) * D].rearrange("(kc p) -> p kc", p=128)),
                     (tmp[:, 1, :], MODr[row, (3 * j + 1) * D:(3 * j + 2) * D].rearrange("(kc p) -> p kc", p=128)),
                     (tmp[:, 2, :], self.norm_g[l, j, :].rearrange("(kc p) -> p kc", p=128))],
              reads=[self.R("MOD", l)], writes=[tmp.res], allow_slow_non_contiguous=True)
        S.op(S.dve, lambda h: h.scalar_tensor_tensor(out=A[:], in0=tmp[:, 1, :], scalar=1.0, in1=tmp[:, 2, :],
                                                      op0=ALU.add, op1=ALU.mult),
             reads=[tmp.res], writes=[A.res])
        return A, tmp

    def gate_bc(self, l, j, row, tag, mul):
        S = self.S
        G = self.sb(f"gate_{tag}", [128, D], F32)
        src = self.MOD[l][row:row + 1, (3 * j + 2) * D:(3 * j + 3) * D]
        src_b = dram_ap(src, src.offset, [[0, 128], [1, D]])
        S.dma(S.sp, [(G[:], src_b)], reads=[self.R("MOD", l)], writes=[G.res])
        if mul != 1.0:
            S.op(S.pool, lambda h: h.tensor_scalar(out=G[:], in0=G[:], scalar1=float(mul), scalar2=None, op0=ALU.mult),
                 reads=[G.res], writes=[G.res])
        return G

    def load_weight_bf16(self, dst, src3, nsplit):
        S = self.S
        kcn = dst.t.shape[1]
        step = max(1, kcn // nsplit)
        for k0 in range(0, kcn, step):
            k1 = min(kcn, k0 + step)
            S.dma(S.pool, [(dst[:, k0:k1, :], src3[:, k0:k1, :])], writes=[dst.res])

    def norm_part(self, xt, nb, ss, rs):
        S = self.S
        junk = self.junk
        S.op(S.act, lambda h: h.activation(out=junk[:], in_=xt[:], func=AF.Square, accum_out=ss[:]),
             reads=[xt.res], writes=[junk.res, ss.res])
        S.op(S.act, lambda h: h.activation(out=rs[:], in_=ss[:], func=AF.Sqrt, scale=1.0 / D, bias=self.eps_col[:]),
             reads=[ss.res], writes=[rs.res])
        S.op(S.dve, lambda h: h.reciprocal(out=rs[:], in_=rs[:]), reads=[rs.res], writes=[rs.res])
        S.op(S.dve, lambda h: h.tensor_scalar(out=nb[:], in0=xt[:], scalar1=rs[:], scalar2=None, op0=ALU.mult),
             reads=[xt.res, rs.res], writes=[nb.res])

    def transpose_part(self, nb, pT, hT, col0, A, sh, evac_engs):
        S = self.S
        for kc in range(8):
            S.op(S.pe, lambda h, kc=kc: h.transpose(out=pT[:, kc * 128:(kc + 1) * 128], in_=nb[:, kc * 128:(kc + 1) * 128],
                                                     identity=self.ident[:]),
                 reads=[nb.res, self.ident.res], writes=[pT.res], inc=(kc == 7))
        for kc in range(8):
            e = evac_engs[kc % len(evac_engs)]
            if e is S.act:
                S.op(e, lambda h, kc=kc: h.activation(out=hT[:, kc, col0:col0 + 128], in_=pT[:, kc * 128:(kc + 1) * 128],
                                                      func=AF.Identity, scale=A[:, kc:kc + 1], bias=sh[:, kc:kc + 1]),
                     reads=[pT.res, A.res, self.shres], writes=[hT.res])
            else:
                S.op(e, lambda h, kc=kc: h.tensor_scalar(out=hT[:, kc, col0:col0 + 128], in0=pT[:, kc * 128:(kc + 1) * 128],
                                                         scalar1=A[:, kc:kc + 1], scalar2=sh[:, kc:kc + 1],
                                                         op0=ALU.mult, op1=ALU.add),
                     reads=[pT.res, A.res, self.shres], writes=[hT.res])

    def phase_ffn(self, l, which, streams):
        S = self.S
        j = 0 if which == 0 else 2
        W13 = self.sb("W13", [128, 8, 2 * DFF], BF16)
        W2 = self.sb("W2", [128, 22, D], BF16)
        self.load_weight_bf16(W13, self.ffn_w13[which][l].rearrange("(kc p) n -> p kc n", p=128), 8)
        self.load_weight_bf16(W2, self.ffn_w2[which][l].rearrange("(fc p) n -> p fc n", p=128), 11)
        import os
        if os.environ.get("FFN_WONLY") and which == 1:
            return
        self.junk = self.sb("junk", [128, D], BF16)
        xl = [self.sb(f"xl{i}", [128, D], F32) for i in range(3)]
        xr = [self.sb(f"xr{i}", [128, D], F32) for i in range(2)]
        nb = [self.sb(f"nb{i}", [128, D], BF16) for i in range(2)]
        ss = [self.sb(f"ss{i}", [128, 1], F32) for i in range(2)]
        rs = [self.sb(f"rs{i}", [128, 1], F32) for i in range(2)]
        hT = self.sb("hT", [128, 8, 512], BF16)
        gT = self.sb("gT", [128, 22, 512], BF16)
        sa = [self.sb(f"sa{i}", [128, 512], F32) for i in range(2)]
        tt = [self.sb(f"tt{i}", [128, 512], F32) for i in range(2)]
        pT = [self.ps(f"pT{i}", [128, D], BF16) for i in range(2)]
        pA = [self.ps(f"pA{i}", [128, 512]) for i in range(2)]
        pB = [self.ps(f"pB{i}", [128, 512]) for i in range(2)]
        pY = [self.ps(f"pY{i}", [128, 512]) for i in range(2)]
        cnt = {"xl": 0, "xr": 0, "nb": 0, "pT": 0, "pAB": 0, "sa": 0, "tt": 0}

        for (tag, src, dst, ntok, row) in streams:
            A, tmp = self.adaln_cols(l, j, row, f"{which}{tag}")
            sh = tmp[:, 0, :]
            self.shres = tmp.res
            G = self.gate_bc(l, j, row, f"{which}{tag}", 0.5)
            tiles = [(t0, min(512, ntok - t0)) for t0 in range(0, ntok, 512)]
            rtag = ("xs", l, which, tag)

            def prep_norm(t0, s):
                i = cnt["xl"] % 3
                cnt["xl"] += 1
                k = cnt["nb"] % 2
                cnt["nb"] += 1
                S.dma(S.sp, [(xl[i][:], src[t0 + s * 128:t0 + (s + 1) * 128, :])], reads=[self.R(src.name)],
                      writes=[xl[i].res])
                self.norm_part(xl[i], nb[k], ss[k], rs[k])
                return nb[k]

            def prep_tr(nbt, s):
                k = cnt["pT"] % 2
                cnt["pT"] += 1
                self.transpose_part(nbt, pT[k], hT, s * 128, A, sh, [S.act, S.dve])

            def prep(t0, n):
                for s in range(n // 128):
                    nbt = prep_norm(t0, s)
                    prep_tr(nbt, s)

            prep(*tiles[0])
            for ti, (t0, n) in enumerate(tiles):
                nt = n // 128
                for p in range(22):
                    k = cnt["pAB"] % 2
                    cnt["pAB"] += 1
                    for kc in range(8):
                        S.op(S.pe, lambda h, kc=kc, p=p, k=k, n=n: h.matmul(pA[k][:, :n], W13[:, kc, p * 128:(p + 1) * 128], hT[:, kc, :n],
                                                                        start=(kc == 0), stop=(kc == 7)),
                             reads=[W13.res, hT.res], writes=[pA[k].res], inc=(kc == 7))
                    for kc in range(8):
                        S.op(S.pe, lambda h, kc=kc, p=p, k=k, n=n: h.matmul(pB[k][:, :n], W13[:, kc, DFF + p * 128:DFF + (p + 1) * 128], hT[:, kc, :n],
                                                                        start=(kc == 0), stop=(kc == 7)),
                             reads=[W13.res, hT.res], writes=[pB[k].res], inc=(kc == 7))
                    q = cnt["sa"] % 2
                    cnt["sa"] += 1
                    S.op(S.act, lambda h, k=k, q=q, n=n: h.activation(out=sa[q][:, :n], in_=pA[k][:, :n], func=AF.Silu),
                         reads=[pA[k].res], writes=[sa[q].res])
                    S.op(S.dve, lambda h, k=k, q=q, p=p, n=n: h.tensor_tensor(out=gT[:, p, :n], in0=sa[q][:, :n], in1=pB[k][:, :n], op=ALU.mult),
                         reads=[sa[q].res, pB[k].res], writes=[gT.res])
                xrs = []
                for s in range(nt):
                    pass
                if ti + 1 < len(tiles):
                    pending = tiles[ti + 1]
                else:
                    pending = None
                for s in range(nt):
                    i = cnt["xr"] % 2
                    cnt["xr"] += 1
                    S.dma(S.sp, [(xr[i][:], src[t0 + s * 128:t0 + (s + 1) * 128, :])], reads=[self.R(src.name)],
                          writes=[xr[i].res])
                    for dh in range(2):
                        for fc in range(22):
                            S.op(S.pe, lambda h, fc=fc, dh=dh, s=s: h.matmul(pY[dh][:], gT[:, fc, s * 128:(s + 1) * 128], W2[:, fc, dh * 512:(dh + 1) * 512],
                                                                              start=(fc == 0), stop=(fc == 21)),
                                 reads=[gT.res, W2.res], writes=[pY[dh].res], inc=(fc == 21))
                    for dh in range(2):
                        q = cnt["tt"] % 2
                        cnt["tt"] += 1
                        S.op(S.dve, lambda h, dh=dh, q=q, G=G: h.tensor_tensor(out=tt[q][:], in0=pY[dh][:], in1=G[:, dh * 512:(dh + 1) * 512], op=ALU.mult),
                             reads=[pY[dh].res, G.res], writes=[tt[q].res])
                        S.op(S.pool, lambda h, dh=dh, q=q, i=i: h.tensor_tensor(out=xr[i][:, dh * 512:(dh + 1) * 512], in0=xr[i][:, dh * 512:(dh + 1) * 512],
                                                                                 in1=tt[q][:], op=ALU.add),
                             reads=[tt[q].res, xr[i].res], writes=[xr[i].res])
                    S.dma(S.sp, [(dst[t0 + s * 128:t0 + (s + 1) * 128, :], xr[i][:])], reads=[xr[i].res],
                          writes=[self.R(dst.name)])
                    if pending is not None and s == 0:
                        prep(*pending)


O_AQ, O_AK, O_AV = 0, 512, 640
O_GQ, O_GK, O_GV, O_GR, O_GG = 768, 1024, 1280, 1792, 2304
O_MQ, O_MK, O_MV, O_MO, O_MI, O_MF = 2336, 2848, 3360, 3872, 4384, 4392
O_SA, O_SG, O_SM = 4400, 5424, 6448


def bcast_rows(ap2d, nparts):
    return bass.AP(ap2d.tensor, ap2d.offset, [[0, nparts]] + [list(x) for x in ap2d.ap[1:]])


def rev_last(ap):
    pat = [list(x) for x in ap.ap]
    st, n = pat[-1]
    return bass.AP(ap.tensor, ap.offset + st * (n - 1), pat[:-1] + [[-st, n]])


def phase_feat(self, l, streams):
    S = self.S
    TT = self.TT
    win = self.w_in[l].rearrange("(kc p) n -> p kc n", p=128)
    Wa = self.sb("Wa", [128, 8, 768], BF16)
    Wv = self.sb("Wv", [128, 8, 1024], BF16)
    Wf = self.sb("Wf", [128, 8, 1536], BF16)
    Wg = self.sb("Wg", [128, 8, 48], BF16)
    S.dma(S.pool, [(Wa[:, 0:4, :], win[:, 0:4, 0:768]), (Wa[:, 4:8, :], win[:, 4:8, 0:768])], writes=[Wa.res])
    for k0 in range(0, 8, 2):
        S.dma(S.pool, [(Wv[:, k0:k0 + 2, 0:512], win[:, k0:k0 + 2, O_GV:O_GV + 512]),
                       (Wv[:, k0:k0 + 2, 512:1024], win[:, k0:k0 + 2, O_MV:O_MV + 512])], writes=[Wv.res])
        S.dma(S.pool, [(Wf[:, k0:k0 + 2, 0:512], win[:, k0:k0 + 2, O_GQ:O_GQ + 512]),
                       (Wf[:, k0:k0 + 2, 512:1536], win[:, k0:k0 + 2, O_MQ:O_MQ + 1024])], writes=[Wf.res])
    S.dma(S.pool, [(Wg[:, :, 0:32], win[:, :, O_GG:O_GG + 32]), (Wg[:, :, 32:48], win[:, :, O_MI:O_MI + 16])], writes=[Wg.res])
    W2p = self.sb("W2p", [32, 2, 256], F32)
    S.op(S.dve, lambda h: h.memset(W2p[:], 0.0), writes=[W2p.res])
    S.dma(S.sp, [(W2p[0:16, 0, :], self.gla_w2[l, 0]), (W2p[16:32, 1, :], self.gla_w2[l, 1])], writes=[W2p.res])
    negb = self.sb("negb", [128, 2, 2], F32)
    S.dma(S.sp, [(negb[:, d, :], self.gla_b[l, d, :].rearrange("(c p) -> p c", p=128)) for d in range(2)],
          writes=[negb.res], allow_slow_non_contiguous=True)
    S.op(S.dve, lambda h: h.tensor_scalar(out=negb[:], in0=negb[:], scalar1=-1.0, scalar2=None, op0=ALU.mult),
         reads=[negb.res], writes=[negb.res])
    gain = self.sb("gain", [128, 10, 64], F32)
    qn_src = self.attn_q_norm[l:l + 1, :]
    kn_src = self.attn_k_norm[l:l + 1, :]
    S.dma(S.sp, [(gain[:, 0:8, :], bass.AP(qn_src.tensor, qn_src.offset, [[0, 128], [0, 8], [1, 64]])),
                 (gain[:, 8:10, :], bass.AP(kn_src.tensor, kn_src.offset, [[0, 128], [0, 2], [1, 64]]))],
          writes=[gain.res])
    mask01 = self.sb("mask01", [128, 8, 64], F32)
    S.op(S.pool, lambda h: h.memset(mask01[:], 1.0), writes=[mask01.res])
    S.op(S.pool, lambda h: h.memset(mask01[:, :, 0:1], 0.0), writes=[mask01.res])
    self.junk = self.sb("junk", [128, D], BF16)

    xl = [self.sb(f"xl{i}", [128, D], F32) for i in range(3)]
    nb = [self.sb(f"nb{i}", [128, D], BF16) for i in range(2)]
    ss = [self.sb(f"ss{i}", [128, 1], F32) for i in range(2)]
    rs = [self.sb(f"rs{i}", [128, 1], F32) for i in range(2)]
    hTs = [self.sb(f"hT{i}", [128, 8, 512], BF16) for i in range(2)]
    sqt = self.sb("sqt", [128, 640], F32)
    ssh = self.sb("ssh", [128, 10], F32)
    rinv = self.sb("rinv", [128, 10], F32)
    qn = self.sb("qn", [128, 10, 64], F32)
    rt = [self.sb(f"rt{i}", [128, 10, 32], F32) for i in range(4)]
    cs_t = [self.sb(f"cst{i}", [128, 2, 32], F32) for i in range(3)]
    qr = [self.sb(f"qr{i}", [128, 10, 64], BF16) for i in range(2)]
    vb = [self.sb(f"vb{i}", [128, 128], BF16) for i in range(4)]
    vb2 = [self.sb(f"vb2{i}", [128, 512], BF16) for i in range(4)]
    QTs = self.sb("QTs", [64, 8, 512], BF16)
    KTs = self.sb("KTs", [64, 2, 512], BF16)
    ggT = self.sb("ggT", [32, 512], F32)
    gts = self.sb("gts", [16, 512], F32)
    ex = [self.sb(f"ex{i}", [128, 512], F32) for i in range(2)]
    csum = [self.sb(f"csum{i}", [128, 512], F32) for i in range(2)]
    eb = [[self.sb(f"eb{d}{c}", [128, 512], F32) for c in range(2)] for d in range(2)]
    enb = [[self.sb(f"enb{d}{c}", [128, 512], F32) for c in range(2)] for d in range(2)]
    ebl = [[self.sb(f"ebl{d}{c}", [128, 512], F32) for c in range(2)] for d in range(2)]
    fo = [self.sb(f"fo{i}", [128, 512], BF16) for i in range(8)]
    elcs = [self.sb(f"elc{i}", [128, 8], F32) for i in range(2)]
    pT = self.ps("pT", [128, D], BF16)
    pq = self.ps("pq", [128, 512])
    pkv = self.ps("pkv", [128, 256])
    pqt = self.ps("pqt", [64, 8, 128], BF16)
    pkt = self.ps("pkt", [64, 2, 128], BF16)
    pf = [self.ps(f"pf{i}", [128, 512]) for i in range(2)]
    pz = self.ps("pz", [128, 512])
    cnt = {"xl": 0, "pf": 0, "fo": 0, "qr": 0, "vb": 0, "vb2": 0, "ex": 0, "rt": 0, "cs": 0, "elc": 0}
    H2Tv = self.H2T[l].rearrange("(kc p) t -> p kc t", p=128)

    def nxt(key, n):
        v = cnt[key] % n
        cnt[key] += 1
        return v

    st_info = []
    for (tag, src, ntok, row, uoff, rope) in streams:
        A, tmp = self.adaln_cols(l, 1, row, f"f{tag}")
        st_info.append((A, tmp, src, uoff, rope))
    tiles = []
    for si, (tag, src, ntok, row, uoff, rope) in enumerate(streams):
        for t0 in range(0, ntok, 512):
            tiles.append((si, t0, min(512, ntok - t0)))

    def prep_load(k, s):
        si, t0, n = tiles[k]
        A, tmp, src, uoff, rope = st_info[si]
        i = nxt("xl", 3)
        S.dma(S.sp, [(xl[i][:], src[t0 + s * 128:t0 + (s + 1) * 128, :])], reads=[self.R(src.name)], writes=[xl[i].res])
        cst = None
        return i

    def rope_load(k, s):
        si, t0, n = tiles[k]
        A, tmp, src, uoff, rope = st_info[si]
        if not rope:
            return None
        cst = cs_t[nxt("cs", 3)]
        S.dma(S.sp, [(cst[:], self.rope_cs[t0 + s * 128:t0 + s * 128 + 128, :, :])], writes=[cst.res])
        return cst

    def prep_sub(k, s, i=None):
        si, t0, n = tiles[k]
        A, tmp, src, uoff, rope = st_info[si]
        hT = hTs[k % 2]
        if i is None:
            i = prep_load(k, s)
        self.norm_part(xl[i], nb[i % 2], ss[i % 2], rs[i % 2])
        self.shres = tmp.res
        self.transpose_part(nb[i % 2], pT, hT, s * 128, A, tmp[:, 0, :], [S.act, S.dve])

    def g_part(k):
        si, t0, n = tiles[k]
        A, tmp, src, uoff, rope = st_info[si]
        hT = hTs[k % 2]
        u0 = uoff + t0
        nch = n // 64
        for kc in range(8):
            S.op(S.pe, lambda h, kc=kc, n=n, hT=hT: h.matmul(pz[0:32, :n], Wg[:, kc, 0:32], hT[:, kc, :n], start=(kc == 0), stop=(kc == 7)),
                 reads=[hT.res, Wg.res], writes=[pz.res], inc=(kc == 7))
        S.op(S.act, lambda h, n=n: h.activation(out=ggT[:, :n], in_=pz[0:32, :n], func=AF.Copy), reads=[pz.res], writes=[ggT.res])
        for kc in range(8):
            S.op(S.pe, lambda h, kc=kc, n=n, hT=hT: h.matmul(pz[0:16, :n], Wg[:, kc, 32:48], hT[:, kc, :n], start=(kc == 0), stop=(kc == 7)),
                 reads=[hT.res, Wg.res], writes=[pz.res], inc=(kc == 7))
        S.op(S.act, lambda h, n=n: h.activation(out=gts[:, :n], in_=pz[0:16, :n], func=AF.Copy), reads=[pz.res], writes=[gts.res])
        S.dma(S.sp, [(self.GATES[l][:, u0:u0 + n], gts[:, :n])], reads=[gts.res], writes=[self.R(self.GATES[l].name)])
        for d in range(2):
            for c2 in range(2):
                S.op(S.pe, lambda h, d=d, c2=c2, n=n: h.matmul(pz[:, :n], W2p[:, d, c2 * 128:(c2 + 1) * 128], ggT[:, :n], start=True, stop=True),
                     reads=[W2p.res, ggT.res], writes=[pz.res])
                e_ = ex[nxt("ex", 2)]
                c_ = csum[(cnt["ex"]) % 2]
                S.op(S.act, lambda h, d=d, c2=c2, n=n, e_=e_: h.activation(out=e_[:, :n], in_=pz[:, :n], func=AF.Exp, scale=-1.0, bias=negb[:, d, c2:c2 + 1]),
                     reads=[pz.res, negb.res], writes=[e_.res])
                S.op(S.act, lambda h, n=n, e_=e_: h.activation(out=e_[:, :n], in_=e_[:, :n], func=AF.Ln, bias=1.0), reads=[e_.res], writes=[e_.res])
                m01 = mask01[:].rearrange("p a b -> p (a b)")[:, :n]
                if d == 0:
                    S.op(S.dve, lambda h, n=n, e_=e_, c_=c_, m01=m01: h.tensor_tensor_scan(out=c_[:, :n], data0=m01, data1=e_[:, :n], initial=0.0, op0=ALU.mult, op1=ALU.add),
                         reads=[e_.res, mask01.res], writes=[c_.res])
                    last = 63
                else:
                    S.op(S.dve, lambda h, n=n, e_=e_, c_=c_, m01=m01: h.tensor_tensor_scan(out=rev_last(c_[:, :n]), data0=m01, data1=rev_last(e_[:, :n]), initial=0.0,
                                                                                         op0=ALU.mult, op1=ALU.add),
                         reads=[e_.res, mask01.res], writes=[c_.res])
                    last = 0
                EB, ENB, EBL = eb[d][c2], enb[d][c2], ebl[d][c2]
                S.op(S.act, lambda h, n=n, c_=c_, EB=EB: h.activation(out=EB[:, :n], in_=c_[:, :n], func=AF.Exp, scale=-1.0 / 16), reads=[c_.res], writes=[EB.res])
                S.op(S.act, lambda h, n=n, c_=c_, ENB=ENB: h.activation(out=ENB[:, :n], in_=c_[:, :n], func=AF.Exp, scale=1.0 / 16), reads=[c_.res], writes=[ENB.res])
                c3 = c_[:, :n].rearrange("p (a b) -> p a b", b=64)
                S.op(S.pool, lambda h, n=n, c_=c_, c3=c3, last=last, nch=nch: h.tensor_tensor(out=c3, in0=c3, in1=c3[:, :, last:last + 1].to_broadcast([128, nch, 64]), op=ALU.subtract),
                     reads=[c_.res], writes=[c_.res])
                S.op(S.act, lambda h, n=n, c_=c_, EBL=EBL: h.activation(out=EBL[:, :n], in_=c_[:, :n], func=AF.Exp, scale=1.0 / 16), reads=[c_.res], writes=[EBL.res])
                ch0 = u0 // 64
                elc = elcs[nxt("elc", 2)]
                S.op(S.pool, lambda h, n=n, EB=EB, last=last, elc=elc, nch=nch: h.tensor_copy(
                    out=elc[:, :nch], in_=EB[:, :n].rearrange("p (a b) -> p a b", b=64)[:, :, last]),
                     reads=[EB.res], writes=[elc.res])
                S.dma(S.sp, [(self.EL[:, 2 * c2 + hh2, d, ch0:ch0 + nch], elc[hh2 * 64:(hh2 + 1) * 64, :nch]) for hh2 in range(2)],
                      reads=[elc.res], writes=[self.EL.res])

    def a_mm(k, s):
        hT = hTs[k % 2]
        c0 = s * 128
        for kc in range(8):
            S.op(S.pe, lambda h, kc=kc, c0=c0, hT=hT: h.matmul(pq[:], hT[:, kc, c0:c0 + 128], Wa[:, kc, 0:512], start=(kc == 0), stop=(kc == 7)),
                 reads=[hT.res, Wa.res], writes=[pq.res], inc=(kc == 7))
        for kc in range(8):
            S.op(S.pe, lambda h, kc=kc, c0=c0, hT=hT: h.matmul(pkv[:], hT[:, kc, c0:c0 + 128], Wa[:, kc, 512:768], start=(kc == 0), stop=(kc == 7)),
                 reads=[hT.res, Wa.res], writes=[pkv.res], inc=(kc == 7))

    def a_chain(k, s, cst=None):
        si, t0, n = tiles[k]
        A, tmp, src, uoff, rope = st_info[si]
        u0 = uoff + t0
        c0 = s * 128
        S.op(S.act, lambda h: h.activation(out=sqt[:, 0:512], in_=pq[:], func=AF.Square), reads=[pq.res], writes=[sqt.res])
        S.op(S.act, lambda h: h.activation(out=sqt[:, 512:640], in_=pkv[:, 0:128], func=AF.Square), reads=[pkv.res], writes=[sqt.res])
        vi = nxt("vb", 4)
        S.op(S.act, lambda h, vi=vi: h.activation(out=vb[vi][:], in_=pkv[:, 128:256], func=AF.Copy), reads=[pkv.res], writes=[vb[vi].res])
        S.dma(S.sp, [(self.VA[l][u0 + c0:u0 + c0 + 128, :], vb[vi][:])], reads=[vb[vi].res], writes=[self.R(self.VA[l].name)])
        S.op(S.dve, lambda h: h.tensor_reduce(out=ssh[:], in_=sqt[:].rearrange("p (a b) -> p a b", b=64), axis=AX.X, op=ALU.add),
             reads=[sqt.res], writes=[ssh.res])
        S.op(S.act, lambda h: h.activation(out=rinv[:], in_=ssh[:], func=AF.Sqrt, scale=1.0 / 64, bias=self.eps_col[:]),
             reads=[ssh.res, self.eps_col.res], writes=[rinv.res])
        S.op(S.dve, lambda h: h.reciprocal(out=rinv[:], in_=rinv[:]), reads=[rinv.res], writes=[rinv.res])
        S.op(S.dve, lambda h: h.tensor_tensor(out=qn[:, 0:8, :], in0=pq[:].rearrange("p (a b) -> p a b", b=64),
                                               in1=rinv[:, 0:8].unsqueeze(2).to_broadcast([128, 8, 64]), op=ALU.mult),
             reads=[pq.res, rinv.res], writes=[qn.res])
        S.op(S.dve, lambda h: h.tensor_tensor(out=qn[:, 8:10, :], in0=pkv[:, 0:128].rearrange("p (a b) -> p a b", b=64),
                                               in1=rinv[:, 8:10].unsqueeze(2).to_broadcast([128, 2, 64]), op=ALU.mult),
             reads=[pkv.res, rinv.res], writes=[qn.res])
        S.op(S.pool, lambda h: h.tensor_tensor(out=qn[:], in0=qn[:], in1=gain[:], op=ALU.mult),
             reads=[qn.res, gain.res], writes=[qn.res])
        q_ = qr[nxt("qr", 2)]
        if rope:
            cosb = cst[:, 0:1, :].to_broadcast([128, 10, 32])
            sinb = cst[:, 1:2, :].to_broadcast([128, 10, 32])
            x1 = qn[:, :, 0:32]
            x2 = qn[:, :, 32:64]
            r = [rt[nxt("rt", 4)] for _ in range(4)]
            S.op(S.pool, lambda h, r=r, cosb=cosb, x1=x1: h.tensor_tensor(out=r[0][:], in0=x1, in1=cosb, op=ALU.mult),
                 reads=[qn.res, cst.res], writes=[r[0].res])
            S.op(S.dve, lambda h, r=r, sinb=sinb, x2=x2: h.tensor_tensor(out=r[1][:], in0=x2, in1=sinb, op=ALU.mult),
                 reads=[qn.res, cst.res], writes=[r[1].res])
            S.op(S.dve, lambda h, r=r, q_=q_: h.tensor_tensor(out=q_[:, :, 0:32], in0=r[0][:], in1=r[1][:], op=ALU.subtract),
                 reads=[r[0].res, r[1].res], writes=[q_.res])
            S.op(S.pool, lambda h, r=r, sinb=sinb, x1=x1: h.tensor_tensor(out=r[2][:], in0=x1, in1=sinb, op=ALU.mult),
                 reads=[qn.res, cst.res], writes=[r[2].res])
            S.op(S.dve, lambda h, r=r, cosb=cosb, x2=x2: h.tensor_tensor(out=r[3][:], in0=x2, in1=cosb, op=ALU.mult),
                 reads=[qn.res, cst.res], writes=[r[3].res])
            S.op(S.dve, lambda h, r=r, q_=q_: h.tensor_tensor(out=q_[:, :, 32:64], in0=r[2][:], in1=r[3][:], op=ALU.add),
                 reads=[r[2].res, r[3].res], writes=[q_.res])
        else:
            S.op(S.dve, lambda h, q_=q_: h.tensor_copy(out=q_[:], in_=qn[:]), reads=[qn.res], writes=[q_.res])
        return q_

    def a_tr(q_, s):
        c0 = s * 128
        for hh in range(8):
            S.op(S.pe, lambda h, hh=hh, q_=q_: h.transpose(out=pqt[:, hh, :], in_=q_[:, hh, :], identity=self.ident[:]),
                 reads=[q_.res, self.ident.res], writes=[pqt.res], inc=(hh == 7))
        for hh in range(2):
            S.op(S.pe, lambda h, hh=hh, q_=q_: h.transpose(out=pkt[:, hh, :], in_=q_[:, 8 + hh, :], identity=self.ident[:]),
                 reads=[q_.res, self.ident.res], writes=[pkt.res], inc=(hh == 1))
        S.op(S.act, lambda h, c0=c0: h.activation(out=QTs[:, :, c0:c0 + 128], in_=pqt[:], func=AF.Copy), reads=[pqt.res], writes=[QTs.res])
        S.op(S.dve, lambda h, c0=c0: h.tensor_copy(out=KTs[:, :, c0:c0 + 128], in_=pkt[:]), reads=[pkt.res], writes=[KTs.res])

    def b_part(k, s):
        si, t0, n = tiles[k]
        uoff = st_info[si][3]
        u0 = uoff + t0
        hT = hTs[k % 2]
        c0 = s * 128
        for half in range(2):
            kk = nxt("pf", 2)
            for kc in range(8):
                S.op(S.pe, lambda h, kc=kc, c0=c0, kk=kk, half=half, hT=hT: h.matmul(pf[kk][:], hT[:, kc, c0:c0 + 128], Wv[:, kc, half * 512:(half + 1) * 512],
                                                                                  start=(kc == 0), stop=(kc == 7)),
                     reads=[hT.res, Wv.res], writes=[pf[kk].res], inc=(kc == 7))
            vi = nxt("vb2", 4)
            if half == 0:
                S.op(S.act, lambda h, kk=kk, vi=vi: h.activation(out=vb2[vi][:], in_=pf[kk][:], func=AF.Copy), reads=[pf[kk].res], writes=[vb2[vi].res])
            else:
                S.op(S.dve, lambda h, kk=kk, vi=vi: h.tensor_copy(out=vb2[vi][:], in_=pf[kk][:]), reads=[pf[kk].res], writes=[vb2[vi].res])
            dstv = self.GV[l] if half == 0 else self.MV[l]
            S.dma(S.sp, [(dstv[u0 + c0:u0 + c0 + 128, :], vb2[vi][:])], reads=[vb2[vi].res], writes=[self.R(dstv.name)])

    def c_part(k, fcs):
        si, t0, n = tiles[k]
        uoff = st_info[si][3]
        u0 = uoff + t0
        hT = hTs[k % 2]
        for fc in fcs:
            kk = nxt("pf", 2)
            for kc in range(8):
                S.op(S.pe, lambda h, kc=kc, fc=fc, kk=kk, n=n, hT=hT: h.matmul(pf[kk][:, :n], Wf[:, kc, fc * 128:(fc + 1) * 128], hT[:, kc, :n], start=(kc == 0), stop=(kc == 7)),
                     reads=[hT.res, Wf.res], writes=[pf[kk].res], inc=(kc == 7))
            if fc < 2:
                for d in range(2):
                    o_ = fo[nxt("fo", 8)]
                    S.op(S.dve, lambda h, kk=kk, n=n, d=d, fc=fc, o_=o_: h.scalar_tensor_tensor(out=o_[:, :n], in0=pf[kk][:, :n], scalar=0.125, in1=eb[d][fc][:, :n],
                                                                                           op0=ALU.mult, op1=ALU.mult),
                         reads=[pf[kk].res, eb[d][fc].res], writes=[o_.res])
                    S.dma(S.sp, [(self.QG[l][d, :, 2 * fc + hh2, u0:u0 + n], o_[hh2 * 64:(hh2 + 1) * 64, :n]) for hh2 in range(2)], reads=[o_.res], writes=[self.R(self.QG[l].name)])
            elif fc < 4:
                c2 = fc - 2
                for d in range(2):
                    o_ = fo[nxt("fo", 8)]
                    S.op(S.dve, lambda h, kk=kk, n=n, d=d, c2=c2, o_=o_: h.tensor_tensor(out=o_[:, :n], in0=pf[kk][:, :n], in1=enb[d][c2][:, :n], op=ALU.mult),
                         reads=[pf[kk].res, enb[d][c2].res], writes=[o_.res])
                    S.dma(S.sp, [(self.KG[l][d, :, 2 * c2 + hh2, u0:u0 + n], o_[hh2 * 64:(hh2 + 1) * 64, :n]) for hh2 in range(2)], reads=[o_.res], writes=[self.R(self.KG[l].name)])
                    o_ = fo[nxt("fo", 8)]
                    S.op(S.dve, lambda h, kk=kk, n=n, d=d, c2=c2, o_=o_: h.tensor_tensor(out=o_[:, :n], in0=pf[kk][:, :n], in1=ebl[d][c2][:, :n], op=ALU.mult),
                         reads=[pf[kk].res, ebl[d][c2].res], writes=[o_.res])
                    S.dma(S.sp, [(self.KH[l][d, :, 2 * c2 + hh2, u0:u0 + n], o_[hh2 * 64:(hh2 + 1) * 64, :n]) for hh2 in range(2)], reads=[o_.res], writes=[self.R(self.KH[l].name)])
            else:
                o_ = fo[nxt("fo", 8)]
                S.op(S.act, lambda h, kk=kk, n=n, o_=o_: h.activation(out=o_[:, :n], in_=pf[kk][:, :n], func=AF.Copy), reads=[pf[kk].res], writes=[o_.res])
                r0 = (fc - 4) * 128
                S.dma(S.sp, [(self.MQK[l][r0:r0 + 128, 2 + u0:2 + u0 + n], o_[:, :n])], reads=[o_.res], writes=[self.R(self.MQK[l].name)])

    for s in range(tiles[0][2] // 128):
        prep_sub(0, s)
    for k, (si, t0, n) in enumerate(tiles):
        A, tmp, src, uoff, rope_ = st_info[si]
        rope = rope_
        nt = n // 128
        u0 = uoff + t0
        hT = hTs[k % 2]
        S.dma(S.sp, [(H2Tv[:, :, u0:u0 + n], hT[:, :, :n])], reads=[hT.res], writes=[self.R(self.H2T[l].name)])
        g_part(k)
        order = [4, 5, 6, 7, 8, 9, 10, 11, 0, 1, 2, 3]
        per = (12 + nt - 1) // nt
        pend = None
        nxt_nt = tiles[k + 1][2] // 128 if k + 1 < len(tiles) else 0
        for s in range(nt):
            xi = prep_load(k + 1, s) if s < nxt_nt else None
            cst = rope_load(k, s)
            a_mm(k, s)
            q_ = a_chain(k, s, cst)
            if pend is not None:
                a_tr(*pend)
            pend = (q_, s)
            b_part(k, s)
            c_part(k, order[s * per:(s + 1) * per])
            if s < nxt_nt:
                prep_sub(k + 1, s, xi)
        a_tr(*pend)
        for s in range(nt, nxt_nt):
            prep_sub(k + 1, s)
        S.dma(S.sp, [(self.QT[l][:, :, u0:u0 + n], QTs[:, :, :n])], reads=[QTs.res], writes=[self.R(self.QT[l].name)])
        S.dma(S.sp, [(self.KT[l][:, :, u0:u0 + n], KTs[:, :, :n])], reads=[KTs.res], writes=[self.R(self.KT[l].name)])


Builder.phase_feat = phase_feat


def phase_attn(self, l, do_ctx):
    S = self.S
    T = self.T
    nbk = T // 128
    ones = self.sb("ones", [128, 128], F32)
    S.op(S.pool, lambda h: h.memset(ones[:], 1.0), writes=[ones.res])
    mP = self.sb("mP", [128, 4, 128], BF16)
    mN = self.sb("mN", [128, 4, 128], BF16)
    mtmp = self.sb("mtmp", [128, 128], F32)
    zer = self.sb("zer", [128, 128], F32)
    S.op(S.pool, lambda h: h.memset(zer[:], 0.0), writes=[zer.res])
    for (m_, sgn) in ((mP, 1), (mN, -1)):
        S.op(S.pool, lambda h, sgn=sgn: h.affine_select(out=mtmp[:], in_=zer[:], pattern=[[-sgn, 128]], compare_op=ALU.is_ge, fill=-30000.0,
                                                         base=0, channel_multiplier=sgn), reads=[zer.res], writes=[mtmp.res])
        S.op(S.pool, lambda h, m_=m_: h.tensor_copy(out=m_[:], in_=mtmp[:].unsqueeze(1).to_broadcast([128, 4, 128])), reads=[mtmp.res], writes=[m_.res])
    esk = self.sb("esk", [128, 2, 4, 128], F32)
    sk8 = self.sb("sk8", [128, 8], F32)
    S.dma(S.sp, [(sk8[64:65, :], self.attn_sink[l:l + 1, :])], writes=[sk8.res])
    S.op(S.act, lambda h: h.activation(out=sk8[64:65, :], in_=sk8[64:65, :], func=AF.Exp), reads=[sk8.res], writes=[sk8.res])
    S.op(S.dve, lambda h: h.tensor_copy(out=esk[64:65].rearrange("p g a b -> p (g a) b"), in_=sk8[64:65, :].unsqueeze(2).to_broadcast([1, 8, 128])),
         reads=[sk8.res], writes=[esk.res])
    KTc = self.sb("KTc", [64, 2, 256], BF16)
    S.dma(S.sp, [(KTc[:], self.KT[l][:, :, 0:256])], reads=[self.R(self.KT[l].name)], writes=[KTc.res])
    Vc = [self.sb(f"Vc{j}", [128, 2, 65], BF16) for j in range(2)]
    Vb = [self.sb(f"Vb{j}", [128, 2, 65], BF16) for j in range(4)]
    KTb = [self.sb(f"KTb{j}", [64, 2, 128], BF16) for j in range(4)]
    for v in Vc + Vb:
        S.op(S.pool, lambda h, v=v: h.memset(v[:], 1.0), writes=[v.res])
    for j in range(2):
        S.dma(S.sp, [(Vc[j][:, :, 0:64], self.VA[l][j * 128:(j + 1) * 128, :].rearrange("p (g d) -> p g d", d=64))],
              reads=[self.R(self.VA[l].name)], writes=[Vc[j].res])
    QTb = [self.sb(f"QTb{j}", [64, 8, 128], BF16) for j in range(2)]
    E = [self.sb(f"E{j}", [128, 4, 128], BF16) for j in range(4)]
    dn = [self.sb(f"dn{j}", [128, 512], F32) for j in range(2)]
    bcs = [self.sb(f"bcs{j}", [64, 512], F32) for j in range(2)]
    aT = [self.sb(f"aT{j}", [64, 4, 128], BF16) for j in range(2)]
    pS = [self.ps(f"pS{j}", [128, 512]) for j in range(3)]
    pO = [self.ps(f"pO{j}", [128, 512]) for j in range(2)]
    pB = [self.ps(f"pB{j}", [64, 512]) for j in range(2)]
    cnt = {}

    def nxt(key, n):
        v = cnt.get(key, 0)
        cnt[key] = v + 1
        return v % n

    def load_kb(m):
        i = m % 4
        u = LC + m * 128
        S.dma(S.sp, [(KTb[i][:], self.KT[l][:, :, u:u + 128])], reads=[self.R(self.KT[l].name)], writes=[KTb[i].res])
        S.dma(S.sp, [(Vb[i][:, :, 0:64], self.VA[l][u:u + 128, :].rearrange("p (g d) -> p g d", d=64))],
              reads=[self.R(self.VA[l].name)], writes=[Vb[i].res])

    pending = []

    def norm(g, po, u0):
        d_ = dn[nxt("dn", 2)]
        S.op(S.dve, lambda h, d_=d_, po=po, g=g: h.tensor_tensor(out=d_[64:65, :], in0=po[64:65, :], in1=esk[64:65, g].rearrange("p a b -> p (a b)"), op=ALU.add),
             reads=[po.res, esk.res], writes=[d_.res])
        S.op(S.dve, lambda h, d_=d_: h.reciprocal(out=d_[64:65, :], in_=d_[64:65, :]), reads=[d_.res], writes=[d_.res])
        pb = pB[nxt("pb", 2)]
        S.op(S.pe, lambda h, d_=d_, pb=pb: h.matmul(pb[:], ones[64:65, 0:64], d_[64:65, :], start=True, stop=True),
             reads=[d_.res, ones.res], writes=[pb.res])
        b_ = bcs[nxt("bcs", 2)]
        S.op(S.act, lambda h, b_=b_, pb=pb: h.activation(out=b_[:], in_=pb[:], func=AF.Copy), reads=[pb.res], writes=[b_.res])
        a_ = aT[nxt("aT", 2)]
        S.op(S.dve, lambda h, a_=a_, b_=b_, po=po: h.tensor_tensor(out=a_[:].rearrange("p a b -> p (a b)"), in0=po[0:64, :], in1=b_[:], op=ALU.mult),
             reads=[po.res, b_.res], writes=[a_.res])
        S.dma(S.sp, [(self.ATT[l][:, 4 * g:4 * g + 4, u0:u0 + 128], a_[:])], reads=[a_.res], writes=[self.R(self.ATT[l].name)])

    def qblock(u0, kbs):
        qi = nxt("q", 2)
        Q = QTb[qi]
        S.dma(S.sp, [(Q[:], self.QT[l][:, :, u0:u0 + 128])], reads=[self.R(self.QT[l].name)], writes=[Q.res])
        for g in range(2):
            po = pO[nxt("po", 2)]
            rhsq = Q[:, 4 * g:4 * g + 4, :].rearrange("p a b -> p (a b)")

            def score(idx, g=g, rhsq=rhsq):
                kt, vt, msk = kbs[idx]
                p = pS[nxt("ps", 3)]
                S.op(S.pe, lambda h, kt=kt, p=p, g=g, rhsq=rhsq, msk=msk: h.matmul(p[:], kt[0][:, g, kt[1]:kt[1] + 128], rhsq, start=True, stop=(msk is None)),
                     reads=[kt[0].res, Q.res], writes=[p.res], inc=(msk is None))
                if msk is not None:
                    S.op(S.pe, lambda h, p=p, msk=msk: h.matmul(p[:], self.ident[:], msk[:].rearrange("p a b -> p (a b)"), start=False, stop=True),
                         reads=[self.ident.res, msk.res], writes=[p.res])
                return p
            ps_list = [score(0)]
            for idx in range(len(kbs)):
                kt, vt, msk = kbs[idx]
                if idx + 1 < len(kbs):
                    ps_list.append(score(idx + 1))
                p = ps_list[idx]
                e = E[nxt("e", 4)]
                S.op(S.act, lambda h, p=p, e=e: h.activation(out=e[:].rearrange("p a b -> p (a b)"), in_=p[:], func=AF.Exp, scale=0.125),
                     reads=[p.res], writes=[e.res])
                S.op(S.pe, lambda h, e=e, vt=vt, po=po, idx=idx, g=g, kbs=kbs: h.matmul(po[0:65, :], vt[:, g, :], e[:].rearrange("p a b -> p (a b)"),
                                                                                      start=(idx == 0), stop=(idx == len(kbs) - 1)),
                     reads=[e.res, vt.res], writes=[po.res], inc=(idx == len(kbs) - 1))
            pending.append((g, po, u0))
            if len(pending) > 1:
                norm(*pending.pop(0))

    ckb = [((KTc, 0), Vc[0], None), ((KTc, 128), Vc[1], None)]
    if do_ctx:
        for n in range(2):
            qblock(n * 128, ckb)
    load_kb(0)
    for n in range(nbk):
        if n + 1 < nbk:
            load_kb(n + 1)
        kbs = []
        if n - 1 >= 0:
            kbs.append(((KTb[(n - 1) % 4], 0), Vb[(n - 1) % 4], mP))
        kbs.append(((KTb[n % 4], 0), Vb[n % 4], None))
        if n + 1 < nbk:
            kbs.append(((KTb[(n + 1) % 4], 0), Vb[(n + 1) % 4], mN))
        qblock(LC + n * 128, kbs + ckb)
    while pending:
        norm(*pending.pop(0))


Builder.phase_attn = phase_attn


def scan_groups(T):
    return [(0, LC)] + [(LC + t0, min(512, T - t0)) for t0 in range(0, T, 512)]


def scan_order(T, d):
    groups = scan_groups(T)
    order = []
    if d == 0:
        for gi, (u0, n) in enumerate(groups):
            for c in range(n // 64):
                order.append((gi, c))
    else:
        gis = [0] + list(range(len(groups) - 1, 0, -1))
        for gi in gis:
            u0, n = groups[gi]
            for c in range(n // 64 - 1, -1, -1):
                order.append((gi, c))
    return order


def phase_gla(self, l):
    S = self.S
    T = self.T
    EL = self.EL
    groups = scan_groups(T)
    ones = self.sb("ones", [64, 64], F32)
    S.op(S.pool, lambda h: h.memset(ones[:], 1.0), writes=[ones.res])
    mtmp = self.sb("mtmp", [64, 64], F32)
    msk = [self.sb(f"msk{d}", [64, 64], BF16) for d in range(2)]
    for d, sgn in ((0, -1), (1, 1)):
        S.op(S.pool, lambda h, sgn=sgn: h.affine_select(out=mtmp[:], in_=ones[:], pattern=[[-sgn, 64]], compare_op=ALU.is_ge, fill=0.0,
                                                         base=0, channel_multiplier=sgn), reads=[ones.res], writes=[mtmp.res])
        S.op(S.pool, lambda h, d=d: h.tensor_copy(out=msk[d][:], in_=mtmp[:]), reads=[mtmp.res], writes=[msk[d].res])
    Sf = [self.sb(f"Sf{d}", [64, 4, 128], F32) for d in range(2)]
    Sb = [self.sb(f"Sb{d}", [64, 4, 128], BF16) for d in range(2)]
    for d in range(2):
        S.op(S.pool, lambda h, d=d: h.memset(Sf[d][:], 0.0), writes=[Sf[d].res])
        S.op(S.pool, lambda h, d=d: h.memset(Sb[d][:], 0.0), writes=[Sb[d].res])
    qg = [[self.sb(f"qg{d}{i}", [64, 4, 512], BF16) for i in range(2)] for d in range(2)]
    kg = [[self.sb(f"kg{d}{i}", [64, 4, 512], BF16) for i in range(2)] for d in range(2)]
    kh = [[self.sb(f"kh{d}{i}", [64, 4, 512], BF16) for i in range(2)] for d in range(2)]
    vg = [[self.sb(f"vg{d}{i}", [64, 8, 512], BF16) for i in range(2)] for d in range(2)]
    am = [[self.sb(f"am{d}{i}", [64, 4, 64], BF16) for i in range(2)] for d in range(2)]
    kt = [[self.sb(f"kt{d}{i}", [64, 4, 64], BF16) for i in range(2)] for d in range(2)]
    ob = [[self.sb(f"ob{d}{i}", [64, 512], F32) for i in range(2)] for d in range(2)]
    pA = [self.ps(f"pA{d}", [64, 256]) for d in range(2)]
    pK = [self.ps(f"pK{d}", [64, 256], BF16) for d in range(2)]
    pO = [self.ps(f"pO{d}", [64, 512]) for d in range(2)]
    pN = [self.ps(f"pN{d}", [64, 512]) for d in range(2)]
    orders = [scan_order(T, d) for d in range(2)]
    nsteps = len(orders[0])
    gcount = [0, 0]
    cur = [None, None]

    def load_group(d, gi):
        i = gcount[d] % 2
        gcount[d] += 1
        u0, n = groups[gi]
        nch = n // 64
        for (dst, srcT) in ((qg[d][i], self.QG[l]), (kg[d][i], self.KG[l]), (kh[d][i], self.KH[l])):
            S.dma(S.sp, [(dst[:, :, :n], srcT[d, :, :, u0:u0 + n])], reads=[self.R(srcT.name)], writes=[dst.res])
        S.dma(S.sp, [(vg[d][i][:, :nch, :], self.GV[l][u0:u0 + n, :].rearrange("(c p) f -> p c f", p=64))], reads=[self.R(self.GV[l].name)],
              writes=[vg[d][i].res])
        return i

    for step in range(nsteps):
        for d in range(2):
            gi, c = orders[d][step]
            if cur[d] is None or cur[d][0] != gi:
                cur[d] = (gi, load_group(d, gi))
            bi = cur[d][1]
            u0, n = groups[gi]
            o = c * 64
            chunk = (u0 + o) // 64
            Q, Kg, Kh, V = qg[d][bi], kg[d][bi], kh[d][bi], vg[d][bi]
            k2 = step % 2
            AM, KTt, OB = am[d][k2], kt[d][k2], ob[d][k2]
            for hh in range(4):
                S.op(S.pe, lambda h, hh=hh, d=d, Kg=Kg, Q=Q, o=o: h.matmul(pA[d][:, hh * 64:(hh + 1) * 64], Kg[:, hh, o:o + 64], Q[:, hh, o:o + 64], start=True, stop=True),
                     reads=[Kg.res, Q.res], writes=[pA[d].res], inc=(hh == 3))
            for hh in range(4):
                S.op(S.pe, lambda h, hh=hh, d=d, Kh=Kh, o=o: h.transpose(out=pK[d][:, hh * 64:(hh + 1) * 64], in_=Kh[:, hh, o:o + 64], identity=self.ident[0:64, 0:64]),
                     reads=[Kh.res, self.ident.res], writes=[pK[d].res], inc=(hh == 3))
            S.op(S.dve, lambda h, d=d, AM=AM: h.tensor_tensor(out=AM[:], in0=pA[d][:].rearrange("p (a b) -> p a b", b=64),
                                                              in1=msk[d][:].unsqueeze(1).to_broadcast([64, 4, 64]), op=ALU.mult),
                 reads=[pA[d].res, msk[d].res], writes=[AM.res])
            S.op(S.act, lambda h, d=d, KTt=KTt: h.activation(out=KTt[:].rearrange("p a b -> p (a b)"), in_=pK[d][:], func=AF.Copy), reads=[pK[d].res], writes=[KTt.res])
            for hh in range(4):
                S.op(S.pe, lambda h, hh=hh, d=d, AM=AM, V=V, c=c: h.matmul(pO[d][:, hh * 128:(hh + 1) * 128], AM[:, hh, :], V[:, c, hh * 128:(hh + 1) * 128], start=True, stop=False),
                     reads=[AM.res, V.res], writes=[pO[d].res], inc=False)
                S.op(S.pe, lambda h, hh=hh, d=d, Q=Q, o=o: h.matmul(pO[d][:, hh * 128:(hh + 1) * 128], Q[:, hh, o:o + 64], Sb[d][:, hh, :], start=False, stop=True),
                     reads=[Q.res, Sb[d].res], writes=[pO[d].res], inc=(hh == 3))
            for hh in range(4):
                S.op(S.pe, lambda h, hh=hh, d=d, KTt=KTt, V=V, c=c: h.matmul(pN[d][:, hh * 128:(hh + 1) * 128], KTt[:, hh, :], V[:, c, hh * 128:(hh + 1) * 128], start=True, stop=True),
                     reads=[KTt.res, V.res], writes=[pN[d].res], inc=(hh == 3))
            S.op(S.act, lambda h, d=d, OB=OB: h.activation(out=OB[:], in_=pO[d][:], func=AF.Copy), reads=[pO[d].res], writes=[OB.res])
            S.dma(S.sp, [(self.OG[l][d, u0 + o:u0 + o + 64, :], OB[:])], reads=[OB.res], writes=[self.R(self.OG[l].name)])
            S.op(S.dve, lambda h, d=d, chunk=chunk: h.tensor_tensor(out=Sf[d][:], in0=Sf[d][:], in1=EL[:, :, d, chunk:chunk + 1].to_broadcast([64, 4, 128]), op=ALU.mult),
                 reads=[Sf[d].res, self.EL.res], writes=[Sf[d].res])
            S.op(S.dve, lambda h, d=d: h.tensor_tensor(out=Sf[d][:].rearrange("p a b -> p (a b)"), in0=Sf[d][:].rearrange("p a b -> p (a b)"), in1=pN[d][:], op=ALU.add),
                 reads=[Sf[d].res, pN[d].res], writes=[Sf[d].res])
            S.op(S.act, lambda h, d=d: h.activation(out=Sb[d][:], in_=Sf[d][:], func=AF.Copy), reads=[Sf[d].res], writes=[Sb[d].res])


Builder.phase_gla = phase_gla


LN_KS = float(-0.5 * np.log(128.0))


def phase_ml_gates(self, l):
    S = self.S
    sel = self.sel
    DEC = self.DEC
    T = self.T
    TT = self.TT
    nch = TT // 64
    bA = self.sb("bA", [4, TT], F32)
    bL = self.sb("bL", [4, TT], F32)
    bC = self.sb("bC", [4, TT], F32)
    bG = self.sb("bG", [4, TT], F32)
    bX = self.sb("bX", [4, TT], F32)
    onesr = self.sb("onesr", [4, TT], BF16)
    S.op(S.pool, lambda h: h.memset(onesr[:], 1.0), writes=[onesr.res])
    gl = self.sb("gl", [4, nch], F32)
    gp = self.sb("gp", [4, nch], F32)
    dd = self.sb("dd", [4, nch], F32)
    ibc = self.sb("ibc", [4, 2], F32)
    pD = self.ps("pD", [128, 512])
    for d in range(2):
        S.dma(S.sp, [(bA[:], self.GATES[l][d * 4:(d + 1) * 4, :])], reads=[self.R(self.GATES[l].name)], writes=[bA.res])
        S.dma(S.sp, [(bL[:], self.GATES[l][8 + d * 4:8 + (d + 1) * 4, :])], reads=[self.R(self.GATES[l].name)], writes=[bL.res])
        S.dma(S.sp, [(ibc[:, 0:1], self.mlstm_ib[l, d, :].rearrange("(h o) -> h o", o=1)), (ibc[:, 1:2], self.mlstm_fb[l, d, :].rearrange("(h o) -> h o", o=1))],
              writes=[ibc.res])
        S.op(S.dve, lambda h: h.tensor_scalar(out=ibc[:, 1:2], in0=ibc[:, 1:2], scalar1=-1.0, scalar2=None, op0=ALU.mult), reads=[ibc.res], writes=[ibc.res])
        S.op(S.act, lambda h: h.activation(out=bL[:], in_=bL[:], func=AF.Exp, scale=-1.0, bias=ibc[:, 1:2]), reads=[bL.res, ibc.res], writes=[bL.res])
        S.op(S.act, lambda h: h.activation(out=bL[:], in_=bL[:], func=AF.Ln, bias=1.0), reads=[bL.res], writes=[bL.res])

        def scan(out, src, op1):
            if d == 0:
                S.op(S.dve, lambda h: h.tensor_tensor_scan(out=out[:], data0=onesr[:], data1=src[:], initial=0.0, op0=ALU.mult, op1=op1),
                     reads=[src.res, onesr.res], writes=[out.res])
            else:
                S.op(S.dve, lambda h: h.tensor_tensor_scan(out=rev_last(out[:, 0:LC]), data0=onesr[:, 0:LC], data1=rev_last(src[:, 0:LC]), initial=0.0, op0=ALU.mult, op1=op1),
                     reads=[src.res, onesr.res], writes=[out.res])
                S.op(S.dve, lambda h: h.tensor_tensor_scan(out=rev_last(out[:, LC:TT]), data0=onesr[:, LC:TT], data1=rev_last(src[:, LC:TT]), initial=out[:, 0:1], op0=ALU.mult, op1=op1),
                     reads=[src.res, onesr.res, out.res], writes=[out.res])
        scan(bC, bL, ALU.add)
        S.op(S.dve, lambda h: h.scalar_tensor_tensor(out=bA[:], in0=bA[:], scalar=ibc[:, 0:1], in1=bC[:], op0=ALU.add, op1=ALU.add),
             reads=[bA.res, ibc.res, bC.res], writes=[bA.res])
        scan(bG, bA, ALU.max)
        G3 = bG[:].rearrange("p (c b) -> p c b", b=64)
        lastpos = 63 if d == 0 else 0
        S.op(S.dve, lambda h, lastpos=lastpos, G3=G3: h.tensor_copy(out=gl[:], in_=G3[:, :, lastpos]), reads=[bG.res], writes=[gl.res])
        S.op(S.dve, lambda h: h.memset(gp[:], 0.0), writes=[gp.res])
        if d == 0:
            S.op(S.dve, lambda h: h.tensor_copy(out=gp[:, 1:nch], in_=gl[:, 0:nch - 1]), reads=[gl.res], writes=[gp.res])
        else:
            S.op(S.dve, lambda h: h.tensor_copy(out=gp[:, 0:3], in_=gl[:, 1:4]), reads=[gl.res], writes=[gp.res])
            S.op(S.dve, lambda h: h.tensor_copy(out=gp[:, 4:nch - 1], in_=gl[:, 5:nch]), reads=[gl.res], writes=[gp.res])
            S.op(S.dve, lambda h: h.tensor_copy(out=gp[:, nch - 1:nch], in_=gl[:, 0:1]), reads=[gl.res], writes=[gp.res])
        L3 = bL[:].rearrange("p (c b) -> p c b", b=64)
        X3 = bX[:].rearrange("p (c b) -> p c b", b=64)
        A3 = bA[:].rearrange("p (c b) -> p c b", b=64)
        S.op(S.dve, lambda h, L3=L3, G3=G3: h.tensor_tensor(out=L3, in0=gp[:].unsqueeze(2).to_broadcast([4, nch, 64]), in1=G3, op=ALU.subtract),
             reads=[gp.res, bG.res], writes=[bL.res])
        S.op(S.act, lambda h: h.activation(out=bL[:], in_=bL[:], func=AF.Exp), reads=[bL.res], writes=[bL.res])
        S.op(S.dve, lambda h: h.tensor_tensor(out=bC[:], in0=bC[:], in1=bG[:], op=ALU.subtract), reads=[bC.res, bG.res], writes=[bC.res])
        S.op(S.act, lambda h: h.activation(out=bC[:], in_=bC[:], func=AF.Exp), reads=[bC.res], writes=[bC.res])
        S.op(S.dve, lambda h, X3=X3, A3=A3: h.tensor_tensor(out=X3, in0=A3, in1=gl[:].unsqueeze(2).to_broadcast([4, nch, 64]), op=ALU.subtract),
             reads=[bA.res, gl.res], writes=[bX.res])
        S.op(S.dve, lambda h: h.tensor_scalar(out=bX[:], in0=bX[:], scalar1=LN_KS, scalar2=None, op0=ALU.add), reads=[bX.res], writes=[bX.res])
        S.op(S.act, lambda h: h.activation(out=bX[:], in_=bX[:], func=AF.Exp), reads=[bX.res], writes=[bX.res])
        S.op(S.dve, lambda h: h.tensor_tensor(out=dd[:], in0=gp[:], in1=gl[:], op=ALU.subtract), reads=[gp.res, gl.res], writes=[dd.res])
        S.op(S.act, lambda h: h.activation(out=dd[:], in_=dd[:], func=AF.Exp), reads=[dd.res], writes=[dd.res])
        for hh in range(4):
            S.op(S.pe, lambda h, hh=hh: h.matmul(pD[:, :nch], sel[:, hh, :], dd[:], start=True, stop=True), reads=[self.sel.res, dd.res], writes=[pD.res])
            S.op(S.dve, lambda h, hh=hh, d=d: h.tensor_copy(out=DEC[:, d, hh, :], in_=pD[:, :nch]), reads=[pD.res], writes=[self.DEC.res])
        for qi, buf in enumerate((bA, bG, bL, bC, bX)):
            S.dma(S.sp, [(self.MROWS[l][d, qi, :, :], buf[:])], reads=[buf.res], writes=[self.R(self.MROWS[l].name)])


def phase_ml_conv(self, l):
    S = self.S
    T = self.T
    TT = self.TT
    wcol = self.sb("wcol", [128, 8, 5], F32)
    cb = self.sb("cb", [128, 8], F32)
    S.dma(S.sp, [(wcol[:, :, k], self.conv_w[l, k, :].rearrange("(fc p) -> p fc", p=128)) for k in range(5)], writes=[wcol.res], allow_slow_non_contiguous=True)
    S.dma(S.sp, [(cb[:], self.conv_b[l, :].rearrange("(fc p) -> p fc", p=128))], writes=[cb.res], allow_slow_non_contiguous=True)
    diagw = self.sb("diagw", [128, 8, 5, 128], BF16)
    for fc in range(8):
        for k in range(5):
            e = S.dve if (fc * 5 + k) % 2 == 0 else S.pool
            S.op(e, lambda h, fc=fc, k=k: h.tensor_scalar(out=diagw[:, fc, k, :], in0=self.identf[:], scalar1=wcol[:, fc, k:k + 1], scalar2=None, op0=ALU.mult),
                 reads=[self.identf.res, wcol.res], writes=[diagw.res])
    xq = [self.sb(f"xq{i}", [128, 8, 516], BF16) for i in range(2)]
    oc = [self.sb(f"oc{i}", [128, 512], BF16) for i in range(3)]
    pc = [self.ps(f"pc{i}", [128, 512]) for i in range(2)]
    MQKv = self.MQK[l].rearrange("(c p) t -> p c t", p=128)
    k2 = 0
    for gi, (u0, n) in enumerate(scan_groups(T)):
        X = xq[gi % 2]
        S.dma(S.sp, [(X[:, 0:4, 0:n + 4], MQKv[:, 0:4, u0:u0 + n + 4]), (X[:, 4:8, 0:n + 4], MQKv[:, 4:8, u0:u0 + n + 4])], reads=[self.R(self.MQK[l].name)], writes=[X.res])
        if u0 == 0 or u0 == LC:
            S.op(S.pool, lambda h, X=X: h.memset(X[:, :, 0:2], 0.0), writes=[X.res])
        if u0 + n == LC or u0 + n == TT:
            S.op(S.pool, lambda h, X=X, n=n: h.memset(X[:, :, n + 2:n + 4], 0.0), writes=[X.res])
        for fc in range(8):
            p = pc[k2 % 2]
            o_ = oc[k2 % 3]
            k2 += 1
            for k in range(5):
                S.op(S.pe, lambda h, fc=fc, k=k, p=p, X=X, n=n: h.matmul(p[:, :n], diagw[:, fc, k, :], X[:, fc, k:k + n], start=(k == 0), stop=(k == 4)),
                     reads=[diagw.res, X.res], writes=[p.res], inc=(k == 4))
            S.op(S.act, lambda h, fc=fc, p=p, o_=o_, n=n: h.activation(out=o_[:, :n], in_=p[:, :n], func=AF.Silu, bias=cb[:, fc:fc + 1]), reads=[p.res, cb.res], writes=[o_.res])
            S.dma(S.sp, [(self.MQC[l][fc * 128:(fc + 1) * 128, u0:u0 + n], o_[:, :n])], reads=[o_.res], writes=[self.R(self.MQC[l].name)])


def phase_ml_scan(self, l):
    S = self.S
    T = self.T
    sel = self.sel
    DEC = self.DEC
    groups = scan_groups(T)
    cfill = self.sb("cfill", [64, 64], F32)
    S.op(S.pool, lambda h: h.memset(cfill[:], LN_KS), writes=[cfill.res])
    mb = [self.sb(f"mb{d}", [64, 64], F32) for d in range(2)]
    for d, sgn in ((0, -1), (1, 1)):
        S.op(S.pool, lambda h, sgn=sgn, d=d: h.affine_select(out=mb[d][:], in_=cfill[:], pattern=[[-sgn, 64]], compare_op=ALU.is_ge, fill=-30000.0,
                                                              base=0, channel_multiplier=sgn), reads=[cfill.res], writes=[mb[d].res])
    negones = self.sb("negones", [4, 128], F32)
    S.op(S.pool, lambda h: h.memset(negones[:], -1.0), writes=[negones.res])
    posones = self.sb("posones", [4, 128], F32)
    S.op(S.pool, lambda h: h.memset(posones[:], 1.0), writes=[posones.res])
    mbr = [self.sb(f"mbr{d}", [64, 4, 64], F32) for d in range(2)]
    for d in range(2):
        S.op(S.pool, lambda h, d=d: h.tensor_copy(out=mbr[d][:], in_=mb[d][:].unsqueeze(1).to_broadcast([64, 4, 64])), reads=[mb[d].res], writes=[mbr[d].res])
    Dg = [self.sb(f"Dg{i}", [4, 3, 4, 512], F32) for i in range(2)]
    Cf = self.sb("Cf", [128, 4, 129], F32)
    Cb = self.sb("Cb", [128, 4, 129], BF16)
    qk = [self.sb(f"qk{i}", [128, 8, 512], BF16) for i in range(2)]
    vg = [self.sb(f"vgm{i}", [64, 8, 4, 129], BF16) for i in range(2)]
    rows = [self.sb(f"rows{i}", [4, 5, 512], F32) for i in range(2)]
    for v in vg:
        S.op(S.pool, lambda h, v=v: h.memset(v[:], 1.0), writes=[v.res])
    wT = [self.sb(f"wT{i}", [64, 256], F32) for i in range(2)]
    sT = [self.sb(f"sT{i}", [64, 4, 64], BF16) for i in range(2)]
    qks = [self.sb(f"qks{i}", [128, 8, 64], BF16) for i in range(2)]
    khat = [self.sb(f"khat{i}", [64, 4, 128], BF16) for i in range(2)]
    enm = [self.sb(f"enm{i}", [64, 4], F32) for i in range(2)]
    rr = [self.sb(f"rr{i}", [64, 4], F32) for i in range(2)]
    ho = [self.sb(f"ho{i}", [64, 4, 128], F32) for i in range(2)]
    pWS = self.ps("pWS", [64, 512])
    pB = self.ps("pB", [128, 8, 64])
    pK = self.ps("pK", [64, 4, 128], BF16)
    pO = self.ps("pO", [64, 1024])
    pN = self.ps("pN", [128, 1024])
    pO3 = pO[:].rearrange("p (h e) -> p h e", e=256)
    pN3 = pN[:].rearrange("p (h e) -> p h e", e=256)
    gcount = [0]

    def load_group(d, gi):
        i = gcount[0] % 2
        gcount[0] += 1
        u0, n = groups[gi]
        nchg = n // 64
        S.dma(S.sp, [(qk[i][:, 0:4, :n], self.MQC[l].rearrange("(c p) t -> p c t", p=128)[:, 0:4, u0:u0 + n]),
                     (qk[i][:, 4:8, :n], self.MQC[l].rearrange("(c p) t -> p c t", p=128)[:, 4:8, u0:u0 + n])], reads=[self.R(self.MQC[l].name)], writes=[qk[i].res])
        S.dma(S.sp, [(vg[i][:, c, :, 0:128], self.MV[l][u0 + c * 64:u0 + (c + 1) * 64, :].rearrange("p (h e) -> p h e", e=128)) for c in range(nchg)],
              reads=[self.R(self.MV[l].name)], writes=[vg[i].res])
        S.dma(S.sp, [(rows[i][:, :, :n], self.MROWS[l][d, :, :, u0:u0 + n].rearrange("q h t -> h q t"))], reads=[self.R(self.MROWS[l].name)], writes=[rows[i].res])
        for qd, qs in enumerate((1, 2, 4)):
            S.op(S.pool, lambda h, i=i, qd=qd, qs=qs, n=n: h.tensor_tensor(out=Dg[i][:, qd, :, :n], in0=rows[i][:, qs:qs + 1, :n].to_broadcast([4, 4, n]),
                                                                       in1=self.identf[0:4, 0:4].unsqueeze(2).to_broadcast([4, 4, n]), op=ALU.mult),
                 reads=[rows[i].res, self.identf.res], writes=[Dg[i].res])
        return i

    for d in range(2):
        S.op(S.pool, lambda h: h.memset(Cf[:], 0.0), writes=[Cf.res])
        S.op(S.pool, lambda h: h.memset(Cb[:], 0.0), writes=[Cb.res])
        order = scan_order(T, d)
        cur = None
        info = []
        for (gi, c) in order:
            if cur is None or cur[0] != gi:
                cur = (gi, None)
            info.append((gi, c))
        bufof = {}

        def stageA(step):
            gi, c = order[step]
            if gi not in bufof:
                bufof.clear()
                bufof[gi] = load_group(d, gi)
            bi = bufof[gi]
            o = c * 64
            k2 = step % 2
            QK, R_ = qk[bi], rows[bi]
            DG = Dg[bi]
            S.op(S.pe, lambda h, R_=R_, o=o: h.matmul(pWS[:, 0:256], R_[:, 0, o:o + 64], sel[:, :, 0:64], start=True, stop=False),
                 reads=[R_.res, sel.res], writes=[pWS.res], inc=False)
            S.op(S.pe, lambda h, DG=DG, o=o: h.matmul(pWS[:, 0:256], negones[:, 0:64], DG[:, 0, :, o:o + 64], start=False, stop=False),
                 reads=[DG.res, negones.res], writes=[pWS.res], inc=False)
            S.op(S.pe, lambda h, d=d: h.matmul(pWS[:, 0:256], self.identf[0:64, 0:64], mbr[d][:], start=False, stop=True),
                 reads=[self.identf.res, mbr[d].res], writes=[pWS.res], inc=False)
            for hh in range(4):
                S.op(S.pe, lambda h, hh=hh, QK=QK, o=o: h.matmul(pWS[:, 256 + hh * 64:256 + (hh + 1) * 64], QK[:, 4 + hh, o:o + 64], QK[:, hh, o:o + 64], start=True, stop=True),
                     reads=[QK.res], writes=[pWS.res], inc=(hh == 3))
            S.op(S.act, lambda h, k2=k2: h.activation(out=wT[k2][:], in_=pWS[:, 0:256], func=AF.Exp), reads=[pWS.res], writes=[wT[k2].res])
            S.op(S.dve, lambda h, k2=k2: h.tensor_tensor(out=sT[k2][:].rearrange("p a b -> p (a b)"), in0=pWS[:, 256:512], in1=wT[k2][:], op=ALU.mult),
                 reads=[pWS.res, wT[k2].res], writes=[sT[k2].res])
            S.op(S.pe, lambda h, DG=DG, o=o: h.matmul(pB[:], posones[:, :], DG[:, 1:3, :, o:o + 64], start=True, stop=True),
                 reads=[DG.res, posones.res], writes=[pB.res])
            S.op(S.dve, lambda h, k2=k2, QK=QK, o=o: h.tensor_tensor(out=qks[k2][:], in0=QK[:, :, o:o + 64], in1=pB[:], op=ALU.mult),
                 reads=[QK.res, pB.res], writes=[qks[k2].res])
            for hh in range(4):
                S.op(S.pe, lambda h, hh=hh, k2=k2: h.transpose(out=pK[:, hh, :], in_=qks[k2][:, 4 + hh, :], identity=self.ident[:]),
                     reads=[qks[k2].res, self.ident.res], writes=[pK.res], inc=(hh == 3))
            S.op(S.act, lambda h, k2=k2: h.activation(out=khat[k2][:], in_=pK[:], func=AF.Copy), reads=[pK.res], writes=[khat[k2].res])

        def stageB(step):
            gi, c = order[step]
            u0, n = groups[gi]
            o = c * 64
            chunk = (u0 + o) // 64
            k2 = step % 2
            V, R_ = vgbuf[step], rowbuf[step]
            for hh in range(4):
                S.op(S.pe, lambda h, hh=hh, k2=k2, V=V, c=c: h.matmul(pN[:, hh * 256:hh * 256 + 129], khat[k2][:, hh, :], V[:, c, hh, :], start=True, stop=True),
                     reads=[khat[k2].res, V.res], writes=[pN.res], inc=(hh == 3))
            S.op(S.pe, lambda h, R_=R_, o=o: h.matmul(pO[:, 200:204], R_[:, 3, o:o + 64], self.identf[0:4, 0:4], start=True, stop=True),
                 reads=[R_.res, self.identf.res], writes=[pO.res], inc=False)
            for hh in range(4):
                S.op(S.pe, lambda h, hh=hh, k2=k2, V=V, c=c: h.matmul(pO[:, hh * 256:hh * 256 + 129], sT[k2][:, hh, :], V[:, c, hh, :], start=True, stop=False),
                     reads=[sT[k2].res, V.res], writes=[pO.res], inc=False)
                S.op(S.pe, lambda h, hh=hh, k2=k2: h.matmul(pO[:, hh * 256:hh * 256 + 129], qks[k2][:, hh, :], Cb[:, hh, :], start=False, stop=True),
                     reads=[qks[k2].res, Cb.res], writes=[pO.res], inc=(hh == 3))
            S.op(S.act, lambda h, k2=k2: h.activation(out=enm[k2][:], in_=pO[:, 200:204], func=AF.Copy), reads=[pO.res], writes=[enm[k2].res])
            S.op(S.act, lambda h, k2=k2: h.activation(out=rr[k2][:], in_=pO3[:, :, 128], func=AF.Abs), reads=[pO.res], writes=[rr[k2].res])
            S.op(S.pool, lambda h, d=d, chunk=chunk: h.tensor_tensor(out=Cf[:], in0=Cf[:], in1=DEC[:, d, :, chunk:chunk + 1].to_broadcast([128, 4, 129]), op=ALU.mult),
                 reads=[Cf.res, DEC.res], writes=[Cf.res])
            S.op(S.dve, lambda h: h.tensor_tensor(out=Cf[:], in0=Cf[:], in1=pN3[:, :, 0:129], op=ALU.add), reads=[Cf.res, pN.res], writes=[Cf.res])
            S.op(S.act, lambda h: h.activation(out=Cb[:], in_=Cf[:], func=AF.Copy), reads=[Cf.res], writes=[Cb.res])
            S.op(S.dve, lambda h, k2=k2: h.tensor_tensor(out=rr[k2][:], in0=rr[k2][:], in1=enm[k2][:], op=ALU.max), reads=[rr[k2].res, enm[k2].res], writes=[rr[k2].res])
            S.op(S.dve, lambda h, k2=k2: h.reciprocal(out=rr[k2][:], in_=rr[k2][:]), reads=[rr[k2].res], writes=[rr[k2].res])
            S.op(S.dve, lambda h, k2=k2: h.tensor_tensor(out=ho[k2][:], in0=pO3[:, :, 0:128], in1=rr[k2][:].unsqueeze(2).to_broadcast([64, 4, 128]), op=ALU.mult),
                 reads=[pO.res, rr[k2].res], writes=[ho[k2].res])
            S.dma(S.sp, [(self.OM[l][d, u0 + o:u0 + o + 64, :], ho[k2][:].rearrange("p a b -> p (a b)"))], reads=[ho[k2].res], writes=[self.R(self.OM[l].name)])

        vgbuf = {}
        rowbuf = {}

        def A(step):
            stageA(step)
            gi, c = order[step]
            vgbuf[step] = vg[bufof[gi]]
            rowbuf[step] = rows[bufof[gi]]
        A(0)
        for step in range(len(order)):
            if step + 1 < len(order):
                A(step + 1)
            stageB(step)


Builder.phase_ml_gates = phase_ml_gates
Builder.phase_ml_conv = phase_ml_conv
Builder.phase_ml_scan = phase_ml_scan


def phase_merge(self, l, streams):
    S = self.S
    win = self.w_in[l].rearrange("(kc p) n -> p kc n", p=128)
    Wm = self.sb("Wm", [128, 8, 4096], BF16)
    Wmr = [Res(f"Wm{k}") for k in range(4)]
    for ki, k0 in enumerate(range(0, 8, 2)):
        S.dma(S.pool, [(Wm[:, k0:k0 + 2, 0:512], win[:, k0:k0 + 2, O_GR:O_GR + 512]),
                       (Wm[:, k0:k0 + 2, 512:1024], win[:, k0:k0 + 2, O_MO:O_MO + 512]),
                       (Wm[:, k0:k0 + 2, 1024:4096], win[:, k0:k0 + 2, O_SA:O_SA + 3072])], writes=[Wmr[ki]])
    Woa = self.sb("Woa", [64, 8, D], BF16)
    Wog = self.sb("Wog", [128, 4, D], BF16)
    Wom = self.sb("Wom", [128, 4, D], BF16)
    Wo = self.sb("Wo", [128, 8, D], BF16)
    S.dma(S.pool, [(Woa[:], self.w_out_attn[l].rearrange("(h p) n -> p h n", p=64))], writes=[Woa.res])
    S.dma(S.pool, [(Wog[:], self.w_out_gla[l].rearrange("(c p) n -> p c n", p=128))], writes=[Wog.res])
    S.dma(S.pool, [(Wom[:], self.w_out_mlstm[l].rearrange("(c p) n -> p c n", p=128))], writes=[Wom.res])
    S.dma(S.pool, [(Wo[:, 0:4, :], self.w_o[l].rearrange("(c p) n -> p c n", p=128)[:, 0:4, :]),
                   (Wo[:, 4:8, :], self.w_o[l].rearrange("(c p) n -> p c n", p=128)[:, 4:8, :])], writes=[Wo.res])
    gains = self.sb("gains", [128, 2, 128], F32)
    S.dma(S.sp, [(gains[:, 0, :], bcast_rows(self.gla_norm[l:l + 1, :], 128)), (gains[:, 1, :], bcast_rows(self.mlstm_norm[l:l + 1, :], 128))], writes=[gains.res])
    eps_col = self.eps_col
    hT = [self.sb(f"mhT{i}", [128, 8, 128], BF16) for i in range(2)]
    aTt = [self.sb(f"maT{i}", [64, 8, 128], BF16) for i in range(2)]
    og = [self.sb(f"mog{i}", [128, 2, 512], F32) for i in range(2)]
    om = [self.sb(f"mom{i}", [128, 2, 512], F32) for i in range(2)]
    xr = [self.sb(f"mxr{i}", [128, D], F32) for i in range(2)]
    gts = [self.sb(f"mgt{i}", [128, 8, 512], F32) for i in range(2)]
    sq = self.sb("msq", [128, 512], F32)
    ssqs = [self.sb(f"mssq{i}", [128, 2, 4], F32) for i in range(2)]
    bn = [self.sb(f"mbn{i}", [128, 512], F32) for i in range(2)]
    bbs = [[self.sb(f"mbb{i}{j}", [128, 512], BF16) for j in range(2)] for i in range(2)]
    bTs = [[self.sb(f"mbT{i}{j}", [128, 4, 128], BF16) for j in range(2)] for i in range(2)]
    yb = self.sb("myb", [128, D], BF16)
    yT = self.sb("myT", [128, 8, 128], BF16)
    t1 = [self.sb(f"mt1{i}", [128, 512], F32) for i in range(3)]
    G5 = self.sb("mG5", [128, D], F32)
    pg = [self.ps(f"mpg{i}", [128, 512]) for i in range(2)]
    pT1s = [self.ps(f"mpT1{j}", [128, 512], BF16) for j in range(2)]
    pT2 = self.ps("mpT2", [128, D], BF16)
    py = [self.ps(f"mpy{i}", [128, 512]) for i in range(3)]
    pY = py[0]
    cnt = {}

    def nxt(key, n):
        v = cnt.get(key, 0)
        cnt[key] = v + 1
        return v % n
    H2Tv = self.H2T[l].rearrange("(kc p) t -> p kc t", p=128)
    work = []
    for (tag, src, dst, ntok, row, uoff) in streams:
        for t0 in range(0, ntok, 128):
            work.append((tag, src, dst, row, uoff + t0, t0))

    def stage1(w, i):
        (tag, src, dst, row, u, t0) = work[w]
        H, AT, OGt, OMt, XR, gt, ssq = hT[i], aTt[i], og[i], om[i], xr[i], gts[i], ssqs[i]
        S.dma(S.sp, [(H[:], H2Tv[:, :, u:u + 128])], reads=[self.R(self.H2T[l].name)], writes=[H.res])
        S.dma(S.sp, [(AT[:], self.ATT[l][:, :, u:u + 128])], reads=[self.R(self.ATT[l].name)], writes=[AT.res])
        S.dma(S.sp, [(OGt[:, 0, :], self.OG[l][0, u:u + 128, :]), (OGt[:, 1, :], self.OG[l][1, u:u + 128, :])], reads=[self.R(self.OG[l].name)], writes=[OGt.res])
        S.dma(S.sp, [(OMt[:, 0, :], self.OM[l][0, u:u + 128, :]), (OMt[:, 1, :], self.OM[l][1, u:u + 128, :])], reads=[self.R(self.OM[l].name)], writes=[OMt.res])
        S.dma(S.sp, [(XR[:], src[t0:t0 + 128, :])], reads=[self.R(src.name)], writes=[XR.res])
        for blk in range(8):
            p = pg[nxt("pg", 2)]
            for kc in range(8):
                S.op(S.pe, lambda h, kc=kc, blk=blk, p=p, H=H: h.matmul(p[:], H[:, kc, :], Wm[:, kc, blk * 512:(blk + 1) * 512], start=(kc == 0), stop=(kc == 7)),
                     reads=[H.res] + Wmr, writes=[p.res], inc=(kc == 7))
            fn = AF.Silu if blk == 0 else AF.Sigmoid
            S.op(S.act, lambda h, blk=blk, p=p, fn=fn, gt=gt: h.activation(out=gt[:, blk, :], in_=p[:], func=fn), reads=[p.res], writes=[gt.res])

    def stage1c(w, i):
        OGt, OMt, gt, ssq = og[i], om[i], gts[i], ssqs[i]
        for br, Ot in enumerate((OGt, OMt)):
            S.op(S.pool, lambda h, Ot=Ot: h.tensor_tensor(out=Ot[:, 0, :], in0=Ot[:, 0, :], in1=Ot[:, 1, :], op=ALU.add), reads=[Ot.res], writes=[Ot.res])
            S.op(S.act, lambda h, Ot=Ot: h.activation(out=sq[:], in_=Ot[:, 0, :], func=AF.Square), reads=[Ot.res], writes=[sq.res])
            S.op(S.dve, lambda h, br=br, ssq=ssq: h.tensor_reduce(out=ssq[:, br, :], in_=sq[:].rearrange("p (a b) -> p a b", b=128), axis=AX.X, op=ALU.add),
                 reads=[sq.res], writes=[ssq.res])
        S.op(S.act, lambda h, ssq=ssq: h.activation(out=ssq[:], in_=ssq[:], func=AF.Sqrt, scale=1.0 / 128, bias=eps_col[:]),
             reads=[ssq.res, eps_col.res], writes=[ssq.res])
        S.op(S.dve, lambda h, ssq=ssq: h.reciprocal(out=ssq[:], in_=ssq[:]), reads=[ssq.res], writes=[ssq.res])
        for br, Ot in enumerate((OGt, OMt)):
            B_ = bn[br]
            S.op(S.dve, lambda h, br=br, Ot=Ot, B_=B_, ssq=ssq: h.tensor_tensor(out=B_[:].rearrange("p (a b) -> p a b", b=128), in0=Ot[:, 0, :].rearrange("p (a b) -> p a b", b=128),
                                                                              in1=ssq[:, br, :].unsqueeze(2).to_broadcast([128, 4, 128]), op=ALU.mult),
                 reads=[Ot.res, ssq.res], writes=[B_.res])
            S.op(S.pool, lambda h, br=br, B_=B_: h.tensor_tensor(out=B_[:].rearrange("p (a b) -> p a b", b=128), in0=B_[:].rearrange("p (a b) -> p a b", b=128),
                                                              in1=gains[:, br:br + 1, :].to_broadcast([128, 4, 128]), op=ALU.mult),
                 reads=[B_.res, gains.res], writes=[B_.res])
            BB = bbs[i][br]
            S.op(S.dve, lambda h, br=br, B_=B_, BB=BB, gt=gt: h.tensor_tensor(out=BB[:], in0=B_[:], in1=gt[:, br, :], op=ALU.mult), reads=[B_.res, gt.res], writes=[BB.res])

    def stage1b(w, i):
        for br in range(2):
            BB = bbs[i][br]
            pT1 = pT1s[br]
            for c in range(4):
                S.op(S.pe, lambda h, c=c, BB=BB, pT1=pT1: h.transpose(out=pT1[:, c * 128:(c + 1) * 128], in_=BB[:, c * 128:(c + 1) * 128], identity=self.ident[:]),
                     reads=[BB.res, self.ident.res], writes=[pT1.res], inc=(c == 3))
            BT = bTs[i][br]
            if br == 0:
                S.op(S.act, lambda h, BT=BT, pT1=pT1: h.activation(out=BT[:].rearrange("p a b -> p (a b)"), in_=pT1[:], func=AF.Copy), reads=[pT1.res], writes=[BT.res])
            else:
                S.op(S.dve, lambda h, BT=BT, pT1=pT1: h.tensor_copy(out=BT[:].rearrange("p a b -> p (a b)"), in_=pT1[:]), reads=[pT1.res], writes=[BT.res])

    cur_row = [None]

    def stage2(w, i):
        (tag, src, dst, row, u, t0) = work[w]
        AT, XR, gt = aTt[i], xr[i], gts[i]
        if cur_row[0] != row:
            cur_row[0] = row
            srcg = self.MOD[l][row:row + 1, 5 * D:6 * D]
            S.dma(S.sp, [(G5[:], dram_ap(srcg, srcg.offset, [[0, 128], [1, D]]))], reads=[self.R("MOD", l)], writes=[G5.res])
        for half in range(2):
            cs_ = slice(half * 512, (half + 1) * 512)
            for hh in range(8):
                S.op(S.pe, lambda h, hh=hh, AT=AT, cs_=cs_: h.matmul(py[0][:], AT[:, hh, :], Woa[:, hh, cs_], start=(hh == 0), stop=(hh == 7)),
                     reads=[AT.res, Woa.res], writes=[py[0].res], inc=(hh == 7))
            for c in range(4):
                S.op(S.pe, lambda h, c=c, cs_=cs_, BT=bTs[i][0]: h.matmul(py[1][:], BT[:, c, :], Wog[:, c, cs_], start=(c == 0), stop=(c == 3)),
                     reads=[bTs[i][0].res, Wog.res], writes=[py[1].res], inc=(c == 3))
            for c in range(4):
                S.op(S.pe, lambda h, c=c, cs_=cs_, BT=bTs[i][1]: h.matmul(py[2][:], BT[:, c, :], Wom[:, c, cs_], start=(c == 0), stop=(c == 3)),
                     reads=[bTs[i][1].res, Wom.res], writes=[py[2].res], inc=(c == 3))
            S.op(S.dve, lambda h, half=half, gt=gt: h.tensor_tensor(out=t1[0][:], in0=py[0][:], in1=gt[:, 2 + half, :], op=ALU.mult), reads=[py[0].res, gt.res], writes=[t1[0].res])
            S.op(S.dve, lambda h, half=half, gt=gt: h.tensor_tensor(out=t1[1][:], in0=py[1][:], in1=gt[:, 4 + half, :], op=ALU.mult), reads=[py[1].res, gt.res], writes=[t1[1].res])
            S.op(S.dve, lambda h, half=half, gt=gt: h.tensor_tensor(out=t1[2][:], in0=py[2][:], in1=gt[:, 6 + half, :], op=ALU.mult), reads=[py[2].res, gt.res], writes=[t1[2].res])
            S.op(S.pool, lambda h: h.tensor_tensor(out=t1[0][:], in0=t1[0][:], in1=t1[1][:], op=ALU.add), reads=[t1[0].res, t1[1].res], writes=[t1[0].res])
            S.op(S.pool, lambda h, cs_=cs_: h.tensor_tensor(out=yb[:, cs_], in0=t1[0][:], in1=t1[2][:], op=ALU.add), reads=[t1[0].res, t1[2].res], writes=[yb.res])
        for kc in range(8):
            S.op(S.pe, lambda h, kc=kc: h.transpose(out=pT2[:, kc * 128:(kc + 1) * 128], in_=yb[:, kc * 128:(kc + 1) * 128], identity=self.ident[:]),
                 reads=[yb.res, self.ident.res], writes=[pT2.res], inc=(kc == 7))
        S.op(S.act, lambda h: h.activation(out=yT[:].rearrange("p a b -> p (a b)"), in_=pT2[:], func=AF.Copy), reads=[pT2.res], writes=[yT.res])
        for half in range(2):
            cs_ = slice(half * 512, (half + 1) * 512)
            for kc in range(8):
                S.op(S.pe, lambda h, kc=kc, cs_=cs_: h.matmul(pY[:], yT[:, kc, :], Wo[:, kc, cs_], start=(kc == 0), stop=(kc == 7)),
                     reads=[yT.res, Wo.res], writes=[pY.res], inc=(kc == 7))
            S.op(S.dve, lambda h, cs_=cs_: h.tensor_tensor(out=t1[0][:], in0=pY[:], in1=G5[:, cs_], op=ALU.mult), reads=[pY.res, G5.res], writes=[t1[0].res])
            S.op(S.pool, lambda h, cs_=cs_, XR=XR: h.tensor_tensor(out=XR[:, cs_], in0=XR[:, cs_], in1=t1[0][:], op=ALU.add), reads=[XR.res, t1[0].res], writes=[XR.res])
        S.dma(S.sp, [(dst[t0:t0 + 128, :], XR[:])], reads=[XR.res], writes=[self.R(dst.name)])

    stage1(0, 0)
    stage1c(0, 0)
    stage1b(0, 0)
    for w in range(len(work)):
        if w + 1 < len(work):
            stage1(w + 1, (w + 1) % 2)
        stage2(w, w % 2)
        if w + 1 < len(work):
            stage1c(w + 1, (w + 1) % 2)
            stage1b(w + 1, (w + 1) % 2)


Builder.phase_merge = phase_merge

_NC_CACHE = {}


def kernel(**inputs):
    inp = {k: np.asarray(v) for k, v in inputs.items()}
    Bsz, SEQ, _ = inp["x"].shape
    T = SEQ
    if T not in _NC_CACHE:
        _NC_CACHE[T] = Builder(T).build()
    nc = _NC_CACHE[T]
    in_maps = [make_in_map(inp, b, 0, T) for b in range(Bsz)]
    res = run_bass_kernel_spmd(nc, in_maps, core_ids=list(range(Bsz)))
    out = np.stack([np.asarray(r["y"], dtype=np.float32) for r in res.results], axis=0)
    return out


W_NAMES = ["mod_w", "mod_b", "norm_g", "ffn1_w13", "ffn1_w2", "ffn2_w13", "ffn2_w2", "w_in", "attn_q_norm", "attn_k_norm", "attn_sink", "gla_w2", "gla_b", "mlstm_conv_w", "mlstm_conv_b", "mlstm_ib", "mlstm_fb", "gla_norm", "mlstm_norm", "w_out_attn", "w_out_gla", "w_out_mlstm", "w_o"]


def make_in_map(inp, b, t0, T):
    m = {"x": np.ascontiguousarray(inp["x"][b, t0:t0 + T]), "c": np.ascontiguousarray(inp["c"][b]),
         "ctx": np.ascontiguousarray(inp["ctx"][b]), "c_ctx": np.ascontiguousarray(inp["c_ctx"])}
    for k in W_NAMES:
        m[k] = np.ascontiguousarray(inp[k])
    m["rope_cs"] = rope_table(t0, T)
    return m


def rope_table(t0, T):
    pos = np.arange(t0, t0 + T)
    r = (pos // 64).astype(np.float32)
    col = (pos % 64).astype(np.float32)
    inv = (np.float32(10000.0) ** (-np.arange(16, dtype=np.float32) / np.float32(16))).astype(np.float32)
    ang = np.concatenate([r[:, None] * inv, col[:, None] * inv], axis=-1).astype(np.float32)
    return np.ascontiguousarray(np.stack([np.cos(ang), np.sin(ang)], axis=1).astype(np.float32))
```

```python
import numpy as np
from contextlib import ExitStack
import concourse.bass as bass
import concourse.mybir as mybir
from concourse.bass_utils import run_bass_kernel_spmd

F32 = mybir.dt.float32
BF16 = mybir.dt.bfloat16
AF = mybir.ActivationFunctionType
ALU = mybir.AluOpType
AX = mybir.AxisListType

D = 1024
DFF = 2816
NMOD = 9
LC = 256
EPS = 1e-6
DEPTH = 2
D_IN = 7472


class Res:
    __slots__ = ("name", "w", "r")

    def __init__(self, name=""):
        self.name = name
        self.w = None
        self.r = []


class Eng:
    def __init__(self, name, is_pe=False):
        self.name = name
        self.is_pe = is_pe
        self.ops = []
        self.sems = []
        self.si = 0
        self.cnt = 0
        self.seen = {}
        self.pend_r = []
        self.pend_w = []
        self.pool = []
        self.pi = 0


ROT = 30000


class Sched:
    def __init__(self, nc, es):
        self.nc = nc
        self.es = es
        self.pe = Eng("pe", True)
        self.act = Eng("act")
        self.dve = Eng("dve")
        self.pool = Eng("pool")
        self.sp = Eng("sp")
        self.engs = [self.pe, self.act, self.dve, self.pool, self.sp]
        self.semid = {}
        n_rot = {"pe": 6, "act": 3, "dve": 3, "pool": 3, "sp": 1}
        for e in self.engs:
            for i in range(n_rot[e.name]):
                s = es.enter_context(nc.semaphore(f"s_{e.name}{i}"))
                e.sems.append(s)
        for e, n in ((self.sp, 20), (self.pool, 10), (self.act, 4)):
            for i in range(n):
                s = es.enter_context(nc.semaphore(f"d_{e.name}{i}"))
                e.pool.append([s, 0])
        self.n_ops = 0

    def _need(self, eng, tok, raw):
        if tok is None:
            return None
        sem, val, owner = tok
        if owner == eng.name:
            if eng.is_pe:
                return None
            if not raw:
                return None
        key = id(sem)
        if eng.seen.get(key, 0) >= val:
            return None
        eng.seen[key] = val
        return (sem, val)

    def _waits(self, eng, reads, writes):
        ws = []
        for r in reads:
            w = self._need(eng, r.w, True)
            if w:
                ws.append(w)
        for wr in writes:
            w = self._need(eng, wr.w, False)
            if w:
                ws.append(w)
            for t in wr.r:
                w = self._need(eng, t, False)
                if w:
                    ws.append(w)
        for (sem, val) in ws:
            eng.ops.append(lambda h, sem=sem, val=val: h.wait_ge(sem, val))

    def _record(self, tok, reads, writes):
        for r in reads:
            r.r = [t for t in r.r if t[2] != tok[2] or t[0] is not tok[0]] + [tok]
        for w in writes:
            w.w = tok
            w.r = []

    def op(self, eng, fn, reads=(), writes=(), inc=True):
        self.n_ops += 1
        reads = list(reads)
        writes = list(writes)
        self._waits(eng, reads, writes)
        if not inc:
            eng.ops.append(lambda h, fn=fn: fn(h))
            eng.pend_r += reads
            eng.pend_w += writes
            return
        if eng.cnt >= ROT:
            eng.si += 1
            eng.cnt = 0
        eng.cnt += 1
        sem = eng.sems[eng.si]
        tok = (sem, eng.cnt, eng.name)
        eng.ops.append(lambda h, fn=fn, sem=sem: fn(h).then_inc(sem, 1))
        self._record(tok, reads + eng.pend_r, writes + eng.pend_w)
        eng.pend_r = []
        eng.pend_w = []

    def dma(self, eng, pairs, reads=(), writes=(), **kw):
        self.n_ops += 1
        reads = list(reads)
        writes = list(writes)
        self._waits(eng, reads, writes)
        ent = eng.pool[eng.pi]
        eng.pi = (eng.pi + 1) % len(eng.pool)
        sem = ent[0]
        if ent[1] > 0 and eng.seen.get(id(sem), 0) < ent[1]:
            v = ent[1]
            eng.ops.append(lambda h, sem=sem, v=v: h.wait_ge(sem, v))
            eng.seen[id(sem)] = v
        for (o, i) in pairs:
            ent[1] += 16
            eng.ops.append(lambda h, o=o, i=i, sem=sem: h.dma_start(out=o, in_=i, **kw).then_inc(sem, 16))
        tok = (sem, ent[1], "dma_" + eng.name + str(id(sem)))
        self._record(tok, reads, writes)

    def barrier(self):
        toks = []
        for e in self.engs:
            assert not e.pend_r and not e.pend_w, e.name
            for i in range(e.si + 1):
                v = ROT if i < e.si else e.cnt
                if v > 0:
                    toks.append((e, e.sems[i], v))
            for ent in e.pool:
                if ent[1] > 0:
                    toks.append((None, ent[0], ent[1]))
        for e in self.engs:
            for (own, sem, v) in toks:
                if own is e:
                    continue
                if e.seen.get(id(sem), 0) >= v:
                    continue
                e.seen[id(sem)] = v
                e.ops.append(lambda h, sem=sem, v=v: h.wait_ge(sem, v))

    def finish(self):
        for e in (self.sp, self.pool, self.act):
            for ent in e.pool:
                if ent[1] > 0:
                    self.sp.ops.append(lambda h, sem=ent[0], v=ent[1]: h.wait_ge(sem, v))

    def replay(self):
        nc = self.nc
        with nc.Block() as block:
            @block.tensor
            def _(h):
                for f in self.pe.ops:
                    f(h)

            @block.scalar
            def _(h):
                for f in self.act.ops:
                    f(h)

            @block.vector
            def _(h):
                for f in self.dve.ops:
                    f(h)

            @block.gpsimd
            def _(h):
                for f in self.pool.ops:
                    f(h)

            @block.sync
            def _(h):
                for f in self.sp.ops:
                    f(h)


class Tile:
    def __init__(self, t, name):
        self.t = t
        self.res = Res(name)

    def __getitem__(self, k):
        return self.t[k]


def dram_ap(t, offset, pattern):
    return bass.AP(t.tensor, offset, pattern)


class Builder:
    def __init__(self, T, depth=DEPTH, stop=None, dbg=()):
        self.T = T
        self.depth = depth
        self.stop = stop
        self.dbg = dbg
        self.nc = bass.Bass("TRN2", target_bir_lowering=False)
        self.es = ExitStack()
        self.S = None
        self.dres = {}
        self.rr = {}

    def din(self, name, shape):
        return self.nc.dram_tensor(name, list(shape), F32, kind="ExternalInput").ap()

    def dout(self, name, shape, dt=F32):
        return self.nc.dram_tensor(name, list(shape), dt, kind="ExternalOutput").ap()

    def dscr(self, name, shape, dt=F32):
        if name in self.dbg:
            return self.nc.dram_tensor(name, list(shape), dt, kind="ExternalOutput").ap()
        return self.nc.dram_tensor(name, list(shape), dt).ap()

    def R(self, *key):
        if key not in self.dres:
            self.dres[key] = Res(str(key))
        return self.dres[key]

    def sb(self, name, shape, dt):
        self.uid = getattr(self, "uid", 0) + 1
        name = f"{name}_{self.uid}"
        t = self.cur.enter_context(self.nc.sbuf_tensor(name, list(shape), dt))
        return Tile(t, name)

    def ps(self, name, shape, dt=F32):
        self.uid = getattr(self, "uid", 0) + 1
        name = f"{name}_{self.uid}"
        t = self.cur.enter_context(self.nc.psum_tensor(name, list(shape), dt))
        return Tile(t, name)

    def build(self):
        nc = self.nc
        T = self.T
        L = self.depth
        with self.es as es:
            self.S = S = Sched(nc, es)
            self.x_in = self.din("x", [T, D])
            self.c_in = self.din("c", [D])
            self.ctx_in = self.din("ctx", [LC, D])
            self.cctx_in = self.din("c_ctx", [D])
            self.mod_w = self.din("mod_w", [L, D, NMOD * D])
            self.mod_b = self.din("mod_b", [L, NMOD * D])
            self.norm_g = self.din("norm_g", [L, 3, D])
            self.ffn_w13 = [self.din("ffn1_w13", [L, D, 2 * DFF]), self.din("ffn2_w13", [L, D, 2 * DFF])]
            self.ffn_w2 = [self.din("ffn1_w2", [L, DFF, D]), self.din("ffn2_w2", [L, DFF, D])]
            self.w_in = self.din("w_in", [L, D, D_IN])
            self.attn_q_norm = self.din("attn_q_norm", [L, 64])
            self.attn_k_norm = self.din("attn_k_norm", [L, 64])
            self.attn_sink = self.din("attn_sink", [L, 8])
            self.gla_w2 = self.din("gla_w2", [L, 2, 16, 256])
            self.gla_b = self.din("gla_b", [L, 2, 256])
            self.rope_cs = self.din("rope_cs", [T, 2, 32])
            self.y_out = self.dout("y", [T, D])
            TT = self.TT = LC + T
            self.H2T = [self.dscr(f"H2T{l}", [D, TT], BF16) for l in range(L)]
            self.QT = [self.dscr(f"QT{l}", [64, 8, TT], BF16) for l in range(L)]
            self.KT = [self.dscr(f"KT{l}", [64, 2, TT], BF16) for l in range(L)]
            self.VA = [self.dscr(f"VA{l}", [TT, 128], BF16) for l in range(L)]
            self.GV = [self.dscr(f"GV{l}", [TT, 512], BF16) for l in range(L)]
            self.MV = [self.dscr(f"MV{l}", [TT, 512], BF16) for l in range(L)]
            self.GATES = [self.dscr(f"GATES{l}", [16, TT]) for l in range(L)]
            self.QG = [self.dscr(f"QG{l}", [2, 64, 4, TT], BF16) for l in range(L)]
            self.KG = [self.dscr(f"KG{l}", [2, 64, 4, TT], BF16) for l in range(L)]
            self.KH = [self.dscr(f"KH{l}", [2, 64, 4, TT], BF16) for l in range(L)]
            self.MQK = [self.dscr(f"MQK{l}", [D, TT + 4], BF16) for l in range(L)]
            self.ATT = [self.dscr(f"ATT{l}", [64, 8, TT], BF16) for l in range(L)]
            self.OG = [self.dscr(f"OG{l}", [2, TT, 512]) for l in range(L)]
            self.OM = [self.dscr(f"OM{l}", [2, TT, 512]) for l in range(L)]
            self.MROWS = [self.dscr(f"MROWS{l}", [2, 5, 4, TT]) for l in range(L)]
            self.MQC = [self.dscr(f"MQC{l}", [D, TT], BF16) for l in range(L)]
            self.gla_norm = self.din("gla_norm", [L, 128])
            self.mlstm_norm = self.din("mlstm_norm", [L, 128])
            self.w_out_attn = self.din("w_out_attn", [L, 512, D])
            self.w_out_gla = self.din("w_out_gla", [L, 512, D])
            self.w_out_mlstm = self.din("w_out_mlstm", [L, 512, D])
            self.w_o = self.din("w_o", [L, D, D])
            self.X2 = [self.dscr(f"X2_{l}", [T, D]) for l in range(L)]
            self.C2 = [self.dscr(f"C2_{l}", [LC, D]) for l in range(L)]
            self.X3 = [self.dscr(f"X3_{l}", [T, D]) for l in range(L)]
            self.C3 = [self.dscr(f"C3_{l}", [LC, D]) for l in range(L)]
            self.conv_w = self.din("mlstm_conv_w", [L, 5, D])
            self.conv_b = self.din("mlstm_conv_b", [L, D])
            self.mlstm_ib = self.din("mlstm_ib", [L, 2, 4])
            self.mlstm_fb = self.din("mlstm_fb", [L, 2, 4])
            self.MOD = [self.dscr(f"MOD{l}", [2, NMOD * D]) for l in range(L)]
            self.X1 = [self.dscr(f"X1_{l}", [T, D]) for l in range(L)]
            self.C1 = [self.dscr(f"C1_{l}", [LC, D]) for l in range(L)]
            with ExitStack() as cst:
                self.cur = cst
                self.ident = self.sb("ident", [128, 128], BF16)
                self.identf = self.sb("identf", [128, 128], F32)
                self.eps_col = self.sb("eps_col", [128, 1], F32)
                S.op(S.dve, lambda h: h.memset(self.eps_col[:], EPS), writes=[self.eps_col.res])
                self.make_consts()
                for l in range(L):
                    xin = self.x_in if l == 0 else self.X3[l - 1]
                    cin = self.ctx_in if l == 0 else self.C3[l - 1]
                    with ExitStack() as ph:
                        self.cur = ph
                        self.phase_mod(l)
                        S.barrier()
                    if self.stop == ("mod", l):
                        break
                    with ExitStack() as ph:
                        self.cur = ph
                        self.phase_ffn(l, 0, [("ctx", cin, self.C1[l], LC, 1), ("lat", xin, self.X1[l], T, 0)])
                        S.barrier()
                    if self.stop == ("ffn1", l):
                        break
                    with ExitStack() as lay:
                        self.cur = lay
                        self.EL = self.sb("EL", [64, 4, 2, TT // 64], F32)
                        with ExitStack() as ph:
                            self.cur = ph
                            self.phase_feat(l, [("ctx", self.C1[l], LC, 1, 0, False), ("lat", self.X1[l], T, 0, LC, True)])
                            S.barrier()
                        if self.stop == ("feat", l):
                            break
                        with ExitStack() as ph:
                            self.cur = ph
                            self.phase_attn(l, l < L - 1)
                            S.barrier()
                        if self.stop == ("attn", l):
                            break
                        with ExitStack() as ph:
                            self.cur = ph
                            self.phase_gla(l)
                            S.barrier()
                        if self.stop == ("gla", l):
                            break
                        self.cur = lay
                        self.DEC = self.sb("DEC", [128, 2, 4, TT // 64], F32)
                        self.sel = self.sb("sel", [4, 4, 128], F32)
                        S.op(S.dve, lambda h, sel_t=self.sel: h.tensor_copy(out=sel_t[:], in_=self.identf[0:4, 0:4].unsqueeze(2).to_broadcast([4, 4, 128])),
                             reads=[self.identf.res], writes=[self.sel.res])
                        stop_ml = False
                        for ph_name, ph_fn in (("mlg", self.phase_ml_gates), ("mlc", self.phase_ml_conv), ("mls", self.phase_ml_scan)):
                            with ExitStack() as ph:
                                self.cur = ph
                                ph_fn(l)
                                S.barrier()
                            if self.stop == (ph_name, l):
                                stop_ml = True
                                break
                        if stop_ml:
                            break
                        if self.stop == ("ml", l):
                            break
                    last = (l == L - 1)
                    with ExitStack() as ph:
                        self.cur = ph
                        st = [("lat", self.X1[l], self.X2[l], T, 0, LC)]
                        if not last:
                            st = [("ctx", self.C1[l], self.C2[l], LC, 1, 0)] + st
                        self.phase_merge(l, st)
                        S.barrier()
                    if self.stop == ("merge", l):
                        break
                    with ExitStack() as ph:
                        self.cur = ph
                        xdst = self.y_out if last else self.X3[l]
                        st = [("lat", self.X2[l], xdst, T, 0)]
                        import os
                        if not last and not os.environ.get("NOCTX2"):
                            st = [("ctx", self.C2[l], self.C3[l], LC, 1)] + st
                        self.phase_ffn(l, 1, [(a, b, c, d_, e) for (a, b, c, d_, e) in st])
                        S.barrier()
                    if self.stop == ("ffn2", l):
                        break
                S.finish()
                S.replay()
        return nc

    def make_consts(self):
        S = self.S
        nc = self.nc
        idf = self.identf
        S.op(S.pool, lambda h: h.memset(idf[:], 0.0), writes=[idf.res])
        S.op(S.pool, lambda h: h.affine_select(out=idf[:], in_=idf[:], pattern=[[-1, 128]],
                                                compare_op=ALU.not_equal, fill=1.0, base=0,
                                                channel_multiplier=1),
             reads=[idf.res], writes=[idf.res])
        S.op(S.dve, lambda h: h.tensor_copy(out=self.ident[:], in_=idf[:]), reads=[idf.res], writes=[self.ident.res])

    def phase_mod(self, l):
        S = self.S
        cl = self.sb("cl", [128, 8, 2], F32)
        cs = self.sb("cs", [128, 8, 2], F32)
        S.dma(S.sp, [(cl[:, :, 0], self.c_in.rearrange("(kc p) -> p kc", p=128)),
                     (cl[:, :, 1], self.cctx_in.rearrange("(kc p) -> p kc", p=128))],
              writes=[cl.res], allow_slow_non_contiguous=True)
        S.op(S.act, lambda h: h.activation(out=cs[:], in_=cl[:], func=AF.Silu), reads=[cl.res], writes=[cs.res])
        wm = [self.sb(f"wm{i}", [128, 8, 512], F32) for i in range(2)]
        mb = [self.sb(f"mb{i}", [2, 512], F32) for i in range(2)]
        mo = [self.sb(f"mo{i}", [2, 512], F32) for i in range(2)]
        pm = [self.ps(f"pm{i}", [2, 512]) for i in range(2)]
        mw = self.mod_w[l].rearrange("(kc p) n -> p kc n", p=128)
        for n in range(18):
            i = n % 2
            S.dma(S.sp, [(wm[i][:, 0:4, :], mw[:, 0:4, n * 512:(n + 1) * 512]),
                         (wm[i][:, 4:8, :], mw[:, 4:8, n * 512:(n + 1) * 512])], writes=[wm[i].res])
            mbsrc = self.mod_b[l:l + 1, n * 512:(n + 1) * 512]
            S.dma(S.sp, [(mb[i][0:1, :], mbsrc), (mb[i][1:2, :], mbsrc)], writes=[mb[i].res])
            for kc in range(8):
                S.op(S.pe, lambda h, kc=kc, i=i: h.matmul(pm[i][:], cs[:, kc, :], wm[i][:, kc, :],
                                                           start=(kc == 0), stop=(kc == 7)),
                     reads=[cs.res, wm[i].res], writes=[pm[i].res], inc=(kc == 7))
            S.op(S.dve, lambda h, i=i: h.tensor_tensor(out=mo[i][:], in0=pm[i][:], in1=mb[i][:], op=ALU.add),
                 reads=[pm[i].res, mb[i].res], writes=[mo[i].res])
            S.dma(S.sp, [(self.MOD[l][:, n * 512:(n + 1) * 512], mo[i][:])], reads=[mo[i].res],
                  writes=[self.R("MOD", l)])

    def load_cols(self, dst_ap, src_row_ap, res):
        self.S.dma(self.S.sp, [(dst_ap, src_row_ap.rearrange("(kc p) -> p kc", p=128))], writes=[res],
                   allow_slow_non_contiguous=True)

    def adaln_cols(self, l, j, row, tag):
        S = self.S
        tmp = self.sb(f"adt_{tag}", [128, 3, 8], F32)
        A = self.sb(f"adA_{tag}", [128, 8], F32)
        MODr = self.MOD[l]
        S.dma(S.sp, [(tmp[:, 0, :], MODr[row, (3 * j) * D:(3 * j + 1) * D].rearrange("(kc p) -> p kc", p=128)),
                     (tmp[:, 1, :], MODr[row, (3 * j + 1) * D:(3 * j + 2) * D].rearrange("(kc p) -> p kc", p=128)),
                     (tmp[:, 2, :], self.norm_g[l, j, :].rearrange("(kc p) -> p kc", p=128))],
              reads=[self.R("MOD", l)], writes=[tmp.res], allow_slow_non_contiguous=True)
        S.op(S.dve, lambda h: h.scalar_tensor_tensor(out=A[:], in0=tmp[:, 1, :], scalar=1.0, in1=tmp[:, 2, :],
                                                      op0=ALU.add, op1=ALU.mult),
             reads=[tmp.res], writes=[A.res])
        return A, tmp

    def gate_bc(self, l, j, row, tag, mul):
        S = self.S
        G = self.sb(f"gate_{tag}", [128, D], F32)
        src = self.MOD[l][row:row + 1, (3 * j + 2) * D:(3 * j + 3) * D]
        src_b = dram_ap(src, src.offset, [[0, 128], [1, D]])
        S.dma(S.sp, [(G[:], src_b)], reads=[self.R("MOD", l)], writes=[G.res])
        if mul != 1.0:
            S.op(S.pool, lambda h: h.tensor_scalar(out=G[:], in0=G[:], scalar1=float(mul), scalar2=None, op0=ALU.mult),
                 reads=[G.res], writes=[G.res])
        return G

    def load_weight_bf16(self, dst, src3, nsplit):
        S = self.S
        kcn = dst.t.shape[1]
        step = max(1, kcn // nsplit)
        for k0 in range(0, kcn, step):
            k1 = min(kcn, k0 + step)
            S.dma(S.pool, [(dst[:, k0:k1, :], src3[:, k0:k1, :])], writes=[dst.res])

    def norm_part(self, xt, nb, ss, rs):
        S = self.S
        junk = self.junk
        S.op(S.act, lambda h: h.activation(out=junk[:], in_=xt[:], func=AF.Square, accum_out=ss[:]),
             reads=[xt.res], writes=[junk.res, ss.res])
        S.op(S.act, lambda h: h.activation(out=rs[:], in_=ss[:], func=AF.Sqrt, scale=1.0 / D, bias=self.eps_col[:]),
             reads=[ss.res], writes=[rs.res])
        S.op(S.dve, lambda h: h.reciprocal(out=rs[:], in_=rs[:]), reads=[rs.res], writes=[rs.res])
        S.op(S.dve, lambda h: h.tensor_scalar(out=nb[:], in0=xt[:], scalar1=rs[:], scalar2=None, op0=ALU.mult),
             reads=[xt.res, rs.res], writes=[nb.res])

    def transpose_part(self, nb, pT, hT, col0, A, sh, evac_engs):
        S = self.S
        for kc in range(8):
            S.op(S.pe, lambda h, kc=kc: h.transpose(out=pT[:, kc * 128:(kc + 1) * 128], in_=nb[:, kc * 128:(kc + 1) * 128],
                                                     identity=self.ident[:]),
                 reads=[nb.res, self.ident.res], writes=[pT.res], inc=(kc == 7))
        for kc in range(8):
            e = evac_engs[kc % len(evac_engs)]
            if e is S.act:
                S.op(e, lambda h, kc=kc: h.activation(out=hT[:, kc, col0:col0 + 128], in_=pT[:, kc * 128:(kc + 1) * 128],
                                                      func=AF.Identity, scale=A[:, kc:kc + 1], bias=sh[:, kc:kc + 1]),
                     reads=[pT.res, A.res, self.shres], writes=[hT.res])
            else:
                S.op(e, lambda h, kc=kc: h.tensor_scalar(out=hT[:, kc, col0:col0 + 128], in0=pT[:, kc * 128:(kc + 1) * 128],
                                                         scalar1=A[:, kc:kc + 1], scalar2=sh[:, kc:kc + 1],
                                                         op0=ALU.mult, op1=ALU.add),
                     reads=[pT.res, A.res, self.shres], writes=[hT.res])

    def phase_ffn(self, l, which, streams):
        S = self.S
        j = 0 if which == 0 else 2
        W13 = self.sb("W13", [128, 8, 2 * DFF], BF16)
        W2 = self.sb("W2", [128, 22, D], BF16)
        self.load_weight_bf16(W13, self.ffn_w13[which][l].rearrange("(kc p) n -> p kc n", p=128), 8)
        self.load_weight_bf16(W2, self.ffn_w2[which][l].rearrange("(fc p) n -> p fc n", p=128), 11)
        import os
        if os.environ.get("FFN_WONLY") and which == 1:
            return
        self.junk = self.sb("junk", [128, D], BF16)
        xl = [self.sb(f"xl{i}", [128, D], F32) for i in range(3)]
        xr = [self.sb(f"xr{i}", [128, D], F32) for i in range(2)]
        nb = [self.sb(f"nb{i}", [128, D], BF16) for i in range(2)]
        ss = [self.sb(f"ss{i}", [128, 1], F32) for i in range(2)]
        rs = [self.sb(f"rs{i}", [128, 1], F32) for i in range(2)]
        hT = self.sb("hT", [128, 8, 512], BF16)
        gT = self.sb("gT", [128, 22, 512], BF16)
        sa = [self.sb(f"sa{i}", [128, 512], F32) for i in range(2)]
        tt = [self.sb(f"tt{i}", [128, 512], F32) for i in range(2)]
        pT = [self.ps(f"pT{i}", [128, D], BF16) for i in range(2)]
        pA = [self.ps(f"pA{i}", [128, 512]) for i in range(2)]
        pB = [self.ps(f"pB{i}", [128, 512]) for i in range(2)]
        pY = [self.ps(f"pY{i}", [128, 512]) for i in range(2)]
        cnt = {"xl": 0, "xr": 0, "nb": 0, "pT": 0, "pAB": 0, "sa": 0, "tt": 0}

        for (tag, src, dst, ntok, row) in streams:
            A, tmp = self.adaln_cols(l, j, row, f"{which}{tag}")
            sh = tmp[:, 0, :]
            self.shres = tmp.res
            G = self.gate_bc(l, j, row, f"{which}{tag}", 0.5)
            tiles = [(t0, min(512, ntok - t0)) for t0 in range(0, ntok, 512)]
            rtag = ("xs", l, which, tag)

            def prep_norm(t0, s):
                i = cnt["xl"] % 3
                cnt["xl"] += 1
                k = cnt["nb"] % 2
                cnt["nb"] += 1
                S.dma(S.sp, [(xl[i][:], src[t0 + s * 128:t0 + (s + 1) * 128, :])], reads=[self.R(src.name)],
                      writes=[xl[i].res])
                self.norm_part(xl[i], nb[k], ss[k], rs[k])
                return nb[k]

            def prep_tr(nbt, s):
                k = cnt["pT"] % 2
                cnt["pT"] += 1
                self.transpose_part(nbt, pT[k], hT, s * 128, A, sh, [S.act, S.dve])

            def prep(t0, n):
                for s in range(n // 128):
                    nbt = prep_norm(t0, s)
                    prep_tr(nbt, s)

            prep(*tiles[0])
            for ti, (t0, n) in enumerate(tiles):
                nt = n // 128
                for p in range(22):
                    k = cnt["pAB"] % 2
                    cnt["pAB"] += 1
                    for kc in range(8):
                        S.op(S.pe, lambda h, kc=kc, p=p, k=k, n=n: h.matmul(pA[k][:, :n], W13[:, kc, p * 128:(p + 1) * 128], hT[:, kc, :n],
                                                                        start=(kc == 0), stop=(kc == 7)),
                             reads=[W13.res, hT.res], writes=[pA[k].res], inc=(kc == 7))
                    for kc in range(8):
                        S.op(S.pe, lambda h, kc=kc, p=p, k=k, n=n: h.matmul(pB[k][:, :n], W13[:, kc, DFF + p * 128:DFF + (p + 1) * 128], hT[:, kc, :n],
                                                                        start=(kc == 0), stop=(kc == 7)),
                             reads=[W13.res, hT.res], writes=[pB[k].res], inc=(kc == 7))
                    q = cnt["sa"] % 2
                    cnt["sa"] += 1
                    S.op(S.act, lambda h, k=k, q=q, n=n: h.activation(out=sa[q][:, :n], in_=pA[k][:, :n], func=AF.Silu),
                         reads=[pA[k].res], writes=[sa[q].res])
                    S.op(S.dve, lambda h, k=k, q=q, p=p, n=n: h.tensor_tensor(out=gT[:, p, :n], in0=sa[q][:, :n], in1=pB[k][:, :n], op=ALU.mult),
                         reads=[sa[q].res, pB[k].res], writes=[gT.res])
                xrs = []
                for s in range(nt):
                    pass
                if ti + 1 < len(tiles):
                    pending = tiles[ti + 1]
                else:
                    pending = None
                for s in range(nt):
                    i = cnt["xr"] % 2
                    cnt["xr"] += 1
                    S.dma(S.sp, [(xr[i][:], src[t0 + s * 128:t0 + (s + 1) * 128, :])], reads=[self.R(src.name)],
                          writes=[xr[i].res])
                    for dh in range(2):
                        for fc in range(22):
                            S.op(S.pe, lambda h, fc=fc, dh=dh, s=s: h.matmul(pY[dh][:], gT[:, fc, s * 128:(s + 1) * 128], W2[:, fc, dh * 512:(dh + 1) * 512],
                                                                              start=(fc == 0), stop=(fc == 21)),
                                 reads=[gT.res, W2.res], writes=[pY[dh].res], inc=(fc == 21))
                    for dh in range(2):
                        q = cnt["tt"] % 2
                        cnt["tt"] += 1
                        S.op(S.dve, lambda h, dh=dh, q=q, G=G: h.tensor_tensor(out=tt[q][:], in0=pY[dh][:], in1=G[:, dh * 512:(dh + 1) * 512], op=ALU.mult),
                             reads=[pY[dh].res, G.res], writes=[tt[q].res])
                        S.op(S.pool, lambda h, dh=dh, q=q, i=i: h.tensor_tensor(out=xr[i][:, dh * 512:(dh + 1) * 512], in0=xr[i][:, dh * 512:(dh + 1) * 512],
                                                                                 in1=tt[q][:], op=ALU.add),
                             reads=[tt[q].res, xr[i].res], writes=[xr[i].res])
                    S.dma(S.sp, [(dst[t0 + s * 128:t0 + (s + 1) * 128, :], xr[i][:])], reads=[xr[i].res],
                          writes=[self.R(dst.name)])
                    if pending is not None and s == 0:
                        prep(*pending)


O_AQ, O_AK, O_AV = 0, 512, 640
O_GQ, O_GK, O_GV, O_GR, O_GG = 768, 1024, 1280, 1792, 2304
O_MQ, O_MK, O_MV, O_MO, O_MI, O_MF = 2336, 2848, 3360, 3872, 4384, 4392
O_SA, O_SG, O_SM = 4400, 5424, 6448


def bcast_rows(ap2d, nparts):
    return bass.AP(ap2d.tensor, ap2d.offset, [[0, nparts]] + [list(x) for x in ap2d.ap[1:]])


def rev_last(ap):
    pat = [list(x) for x in ap.ap]
    st, n = pat[-1]
    return bass.AP(ap.tensor, ap.offset + st * (n - 1), pat[:-1] + [[-st, n]])


def phase_feat(self, l, streams):
    S = self.S
    TT = self.TT
    win = self.w_in[l].rearrange("(kc p) n -> p kc n", p=128)
    Wa = self.sb("Wa", [128, 8, 768], BF16)
    Wv = self.sb("Wv", [128, 8, 1024], BF16)
    Wf = self.sb("Wf", [128, 8, 1536], BF16)
    Wg = self.sb("Wg", [128, 8, 48], BF16)
    S.dma(S.pool, [(Wa[:, 0:4, :], win[:, 0:4, 0:768]), (Wa[:, 4:8, :], win[:, 4:8, 0:768])], writes=[Wa.res])
    for k0 in range(0, 8, 2):
        S.dma(S.pool, [(Wv[:, k0:k0 + 2, 0:512], win[:, k0:k0 + 2, O_GV:O_GV + 512]),
                       (Wv[:, k0:k0 + 2, 512:1024], win[:, k0:k0 + 2, O_MV:O_MV + 512])], writes=[Wv.res])
        S.dma(S.pool, [(Wf[:, k0:k0 + 2, 0:512], win[:, k0:k0 + 2, O_GQ:O_GQ + 512]),
                       (Wf[:, k0:k0 + 2, 512:1536], win[:, k0:k0 + 2, O_MQ:O_MQ + 1024])], writes=[Wf.res])
    S.dma(S.pool, [(Wg[:, :, 0:32], win[:, :, O_GG:O_GG + 32]), (Wg[:, :, 32:48], win[:, :, O_MI:O_MI + 16])], writes=[Wg.res])
    W2p = self.sb("W2p", [32, 2, 256], F32)
    S.op(S.dve, lambda h: h.memset(W2p[:], 0.0), writes=[W2p.res])
    S.dma(S.sp, [(W2p[0:16, 0, :], self.gla_w2[l, 0]), (W2p[16:32, 1, :], self.gla_w2[l, 1])], writes=[W2p.res])
    negb = self.sb("negb", [128, 2, 2], F32)
    S.dma(S.sp, [(negb[:, d, :], self.gla_b[l, d, :].rearrange("(c p) -> p c", p=128)) for d in range(2)],
          writes=[negb.res], allow_slow_non_contiguous=True)
    S.op(S.dve, lambda h: h.tensor_scalar(out=negb[:], in0=negb[:], scalar1=-1.0, scalar2=None, op0=ALU.mult),
         reads=[negb.res], writes=[negb.res])
    gain = self.sb("gain", [128, 10, 64], F32)
    qn_src = self.attn_q_norm[l:l + 1, :]
    kn_src = self.attn_k_norm[l:l + 1, :]
    S.dma(S.sp, [(gain[:, 0:8, :], bass.AP(qn_src.tensor, qn_src.offset, [[0, 128], [0, 8], [1, 64]])),
                 (gain[:, 8:10, :], bass.AP(kn_src.tensor, kn_src.offset, [[0, 128], [0, 2], [1, 64]]))],
          writes=[gain.res])
    mask01 = self.sb("mask01", [128, 8, 64], F32)
    S.op(S.pool, lambda h: h.memset(mask01[:], 1.0), writes=[mask01.res])
    S.op(S.pool, lambda h: h.memset(mask01[:, :, 0:1], 0.0), writes=[mask01.res])
    self.junk = self.sb("junk", [128, D], BF16)

    xl = [self.sb(f"xl{i}", [128, D], F32) for i in range(3)]
    nb = [self.sb(f"nb{i}", [128, D], BF16) for i in range(2)]
    ss = [self.sb(f"ss{i}", [128, 1], F32) for i in range(2)]
    rs = [self.sb(f"rs{i}", [128, 1], F32) for i in range(2)]
    hTs = [self.sb(f"hT{i}", [128, 8, 512], BF16) for i in range(2)]
    sqt = self.sb("sqt", [128, 640], F32)
    ssh = self.sb("ssh", [128, 10], F32)
    rinv = self.sb("rinv", [128, 10], F32)
    qn = self.sb("qn", [128, 10, 64], F32)
    rt = [self.sb(f"rt{i}", [128, 10, 32], F32) for i in range(4)]
    cs_t = [self.sb(f"cst{i}", [128, 2, 32], F32) for i in range(3)]
    qr = [self.sb(f"qr{i}", [128, 10, 64], BF16) for i in range(2)]
    vb = [self.sb(f"vb{i}", [128, 128], BF16) for i in range(4)]
    vb2 = [self.sb(f"vb2{i}", [128, 512], BF16) for i in range(4)]
    QTs = self.sb("QTs", [64, 8, 512], BF16)
    KTs = self.sb("KTs", [64, 2, 512], BF16)
    ggT = self.sb("ggT", [32, 512], F32)
    gts = self.sb("gts", [16, 512], F32)
    ex = [self.sb(f"ex{i}", [128, 512], F32) for i in range(2)]
    csum = [self.sb(f"csum{i}", [128, 512], F32) for i in range(2)]
    eb = [[self.sb(f"eb{d}{c}", [128, 512], F32) for c in range(2)] for d in range(2)]
    enb = [[self.sb(f"enb{d}{c}", [128, 512], F32) for c in range(2)] for d in range(2)]
    ebl = [[self.sb(f"ebl{d}{c}", [128, 512], F32) for c in range(2)] for d in range(2)]
    fo = [self.sb(f"fo{i}", [128, 512], BF16) for i in range(8)]
    elcs = [self.sb(f"elc{i}", [128, 8], F32) for i in range(2)]
    pT = self.ps("pT", [128, D], BF16)
    pq = self.ps("pq", [128, 512])
    pkv = self.ps("pkv", [128, 256])
    pqt = self.ps("pqt", [64, 8, 128], BF16)
    pkt = self.ps("pkt", [64, 2, 128], BF16)
    pf = [self.ps(f"pf{i}", [128, 512]) for i in range(2)]
    pz = self.ps("pz", [128, 512])
    cnt = {"xl": 0, "pf": 0, "fo": 0, "qr": 0, "vb": 0, "vb2": 0, "ex": 0, "rt": 0, "cs": 0, "elc": 0}
    H2Tv = self.H2T[l].rearrange("(kc p) t -> p kc t", p=128)

    def nxt(key, n):
        v = cnt[key] % n
        cnt[key] += 1
        return v

    st_info = []
    for (tag, src, ntok, row, uoff, rope) in streams:
        A, tmp = self.adaln_cols(l, 1, row, f"f{tag}")
        st_info.append((A, tmp, src, uoff, rope))
    tiles = []
    for si, (tag, src, ntok, row, uoff, rope) in enumerate(streams):
        for t0 in range(0, ntok, 512):
            tiles.append((si, t0, min(512, ntok - t0)))

    def prep_load(k, s):
        si, t0, n = tiles[k]
        A, tmp, src, uoff, rope = st_info[si]
        i = nxt("xl", 3)
        S.dma(S.sp, [(xl[i][:], src[t0 + s * 128:t0 + (s + 1) * 128, :])], reads=[self.R(src.name)], writes=[xl[i].res])
        cst = None
        return i

    def rope_load(k, s):
        si, t0, n = tiles[k]
        A, tmp, src, uoff, rope = st_info[si]
        if not rope:
            return None
        cst = cs_t[nxt("cs", 3)]
        S.dma(S.sp, [(cst[:], self.rope_cs[t0 + s * 128:t0 + s * 128 + 128, :, :])], writes=[cst.res])
        return cst

    def prep_sub(k, s, i=None):
        si, t0, n = tiles[k]
        A, tmp, src, uoff, rope = st_info[si]
        hT = hTs[k % 2]
        if i is None:
            i = prep_load(k, s)
        self.norm_part(xl[i], nb[i % 2], ss[i % 2], rs[i % 2])
        self.shres = tmp.res
        self.transpose_part(nb[i % 2], pT, hT, s * 128, A, tmp[:, 0, :], [S.act, S.dve])

    def g_part(k):
        si, t0, n = tiles[k]
        A, tmp, src, uoff, rope = st_info[si]
        hT = hTs[k % 2]
        u0 = uoff + t0
        nch = n // 64
        for kc in range(8):
            S.op(S.pe, lambda h, kc=kc, n=n, hT=hT: h.matmul(pz[0:32, :n], Wg[:, kc, 0:32], hT[:, kc, :n], start=(kc == 0), stop=(kc == 7)),
                 reads=[hT.res, Wg.res], writes=[pz.res], inc=(kc == 7))
        S.op(S.act, lambda h, n=n: h.activation(out=ggT[:, :n], in_=pz[0:32, :n], func=AF.Copy), reads=[pz.res], writes=[ggT.res])
        for kc in range(8):
            S.op(S.pe, lambda h, kc=kc, n=n, hT=hT: h.matmul(pz[0:16, :n], Wg[:, kc, 32:48], hT[:, kc, :n], start=(kc == 0), stop=(kc == 7)),
                 reads=[hT.res, Wg.res], writes=[pz.res], inc=(kc == 7))
        S.op(S.act, lambda h, n=n: h.activation(out=gts[:, :n], in_=pz[0:16, :n], func=AF.Copy), reads=[pz.res], writes=[gts.res])
        S.dma(S.sp, [(self.GATES[l][:, u0:u0 + n], gts[:, :n])], reads=[gts.res], writes=[self.R(self.GATES[l].name)])
        for d in range(2):
            for c2 in range(2):
                S.op(S.pe, lambda h, d=d, c2=c2, n=n: h.matmul(pz[:, :n], W2p[:, d, c2 * 128:(c2 + 1) * 128], ggT[:, :n], start=True, stop=True),
                     reads=[W2p.res, ggT.res], writes=[pz.res])
                e_ = ex[nxt("ex", 2)]
                c_ = csum[(cnt["ex"]) % 2]
                S.op(S.act, lambda h, d=d, c2=c2, n=n, e_=e_: h.activation(out=e_[:, :n], in_=pz[:, :n], func=AF.Exp, scale=-1.0, bias=negb[:, d, c2:c2 + 1]),
                     reads=[pz.res, negb.res], writes=[e_.res])
                S.op(S.act, lambda h, n=n, e_=e_: h.activation(out=e_[:, :n], in_=e_[:, :n], func=AF.Ln, bias=1.0), reads=[e_.res], writes=[e_.res])
                m01 = mask01[:].rearrange("p a b -> p (a b)")[:, :n]
                if d == 0:
                    S.op(S.dve, lambda h, n=n, e_=e_, c_=c_, m01=m01: h.tensor_tensor_scan(out=c_[:, :n], data0=m01, data1=e_[:, :n], initial=0.0, op0=ALU.mult, op1=ALU.add),
                         reads=[e_.res, mask01.res], writes=[c_.res])
                    last = 63
                else:
                    S.op(S.dve, lambda h, n=n, e_=e_, c_=c_, m01=m01: h.tensor_tensor_scan(out=rev_last(c_[:, :n]), data0=m01, data1=rev_last(e_[:, :n]), initial=0.0,
                                                                                         op0=ALU.mult, op1=ALU.add),
                         reads=[e_.res, mask01.res], writes=[c_.res])
                    last = 0
                EB, ENB, EBL = eb[d][c2], enb[d][c2], ebl[d][c2]
                S.op(S.act, lambda h, n=n, c_=c_, EB=EB: h.activation(out=EB[:, :n], in_=c_[:, :n], func=AF.Exp, scale=-1.0 / 16), reads=[c_.res], writes=[EB.res])
                S.op(S.act, lambda h, n=n, c_=c_, ENB=ENB: h.activation(out=ENB[:, :n], in_=c_[:, :n], func=AF.Exp, scale=1.0 / 16), reads=[c_.res], writes=[ENB.res])
                c3 = c_[:, :n].rearrange("p (a b) -> p a b", b=64)
                S.op(S.pool, lambda h, n=n, c_=c_, c3=c3, last=last, nch=nch: h.tensor_tensor(out=c3, in0=c3, in1=c3[:, :, last:last + 1].to_broadcast([128, nch, 64]), op=ALU.subtract),
                     reads=[c_.res], writes=[c_.res])
                S.op(S.act, lambda h, n=n, c_=c_, EBL=EBL: h.activation(out=EBL[:, :n], in_=c_[:, :n], func=AF.Exp, scale=1.0 / 16), reads=[c_.res], writes=[EBL.res])
                ch0 = u0 // 64
                elc = elcs[nxt("elc", 2)]
                S.op(S.pool, lambda h, n=n, EB=EB, last=last, elc=elc, nch=nch: h.tensor_copy(
                    out=elc[:, :nch], in_=EB[:, :n].rearrange("p (a b) -> p a b", b=64)[:, :, last]),
                     reads=[EB.res], writes=[elc.res])
                S.dma(S.sp, [(self.EL[:, 2 * c2 + hh2, d, ch0:ch0 + nch], elc[hh2 * 64:(hh2 + 1) * 64, :nch]) for hh2 in range(2)],
                      reads=[elc.res], writes=[self.EL.res])

    def a_mm(k, s):
        hT = hTs[k % 2]
        c0 = s * 128
        for kc in range(8):
            S.op(S.pe, lambda h, kc=kc, c0=c0, hT=hT: h.matmul(pq[:], hT[:, kc, c0:c0 + 128], Wa[:, kc, 0:512], start=(kc == 0), stop=(kc == 7)),
                 reads=[hT.res, Wa.res], writes=[pq.res], inc=(kc == 7))
        for kc in range(8):
            S.op(S.pe, lambda h, kc=kc, c0=c0, hT=hT: h.matmul(pkv[:], hT[:, kc, c0:c0 + 128], Wa[:, kc, 512:768], start=(kc == 0), stop=(kc == 7)),
                 reads=[hT.res, Wa.res], writes=[pkv.res], inc=(kc == 7))

    def a_chain(k, s, cst=None):
        si, t0, n = tiles[k]
        A, tmp, src, uoff, rope = st_info[si]
        u0 = uoff + t0
        c0 = s * 128
        S.op(S.act, lambda h: h.activation(out=sqt[:, 0:512], in_=pq[:], func=AF.Square), reads=[pq.res], writes=[sqt.res])
        S.op(S.act, lambda h: h.activation(out=sqt[:, 512:640], in_=pkv[:, 0:128], func=AF.Square), reads=[pkv.res], writes=[sqt.res])
        vi = nxt("vb", 4)
        S.op(S.act, lambda h, vi=vi: h.activation(out=vb[vi][:], in_=pkv[:, 128:256], func=AF.Copy), reads=[pkv.res], writes=[vb[vi].res])
        S.dma(S.sp, [(self.VA[l][u0 + c0:u0 + c0 + 128, :], vb[vi][:])], reads=[vb[vi].res], writes=[self.R(self.VA[l].name)])
        S.op(S.dve, lambda h: h.tensor_reduce(out=ssh[:], in_=sqt[:].rearrange("p (a b) -> p a b", b=64), axis=AX.X, op=ALU.add),
             reads=[sqt.res], writes=[ssh.res])
        S.op(S.act, lambda h: h.activation(out=rinv[:], in_=ssh[:], func=AF.Sqrt, scale=1.0 / 64, bias=self.eps_col[:]),
             reads=[ssh.res, self.eps_col.res], writes=[rinv.res])
        S.op(S.dve, lambda h: h.reciprocal(out=rinv[:], in_=rinv[:]), reads=[rinv.res], writes=[rinv.res])
        S.op(S.dve, lambda h: h.tensor_tensor(out=qn[:, 0:8, :], in0=pq[:].rearrange("p (a b) -> p a b", b=64),
                                               in1=rinv[:, 0:8].unsqueeze(2).to_broadcast([128, 8, 64]), op=ALU.mult),
             reads=[pq.res, rinv.res], writes=[qn.res])
        S.op(S.dve, lambda h: h.tensor_tensor(out=qn[:, 8:10, :], in0=pkv[:, 0:128].rearrange("p (a b) -> p a b", b=64),
                                               in1=rinv[:, 8:10].unsqueeze(2).to_broadcast([128, 2, 64]), op=ALU.mult),
             reads=[pkv.res, rinv.res], writes=[qn.res])
        S.op(S.pool, lambda h: h.tensor_tensor(out=qn[:], in0=qn[:], in1=gain[:], op=ALU.mult),
             reads=[qn.res, gain.res], writes=[qn.res])
        q_ = qr[nxt("qr", 2)]
        if rope:
            cosb = cst[:, 0:1, :].to_broadcast([128, 10, 32])
            sinb = cst[:, 1:2, :].to_broadcast([128, 10, 32])
            x1 = qn[:, :, 0:32]
            x2 = qn[:, :, 32:64]
            r = [rt[nxt("rt", 4)] for _ in range(4)]
            S.op(S.pool, lambda h, r=r, cosb=cosb, x1=x1: h.tensor_tensor(out=r[0][:], in0=x1, in1=cosb, op=ALU.mult),
                 reads=[qn.res, cst.res], writes=[r[0].res])
            S.op(S.dve, lambda h, r=r, sinb=sinb, x2=x2: h.tensor_tensor(out=r[1][:], in0=x2, in1=sinb, op=ALU.mult),
                 reads=[qn.res, cst.res], writes=[r[1].res])
            S.op(S.dve, lambda h, r=r, q_=q_: h.tensor_tensor(out=q_[:, :, 0:32], in0=r[0][:], in1=r[1][:], op=ALU.subtract),
                 reads=[r[0].res, r[1].res], writes=[q_.res])
            S.op(S.pool, lambda h, r=r, sinb=sinb, x1=x1: h.tensor_tensor(out=r[2][:], in0=x1, in1=sinb, op=ALU.mult),
                 reads=[qn.res, cst.res], writes=[r[2].res])
            S.op(S.dve, lambda h, r=r, cosb=cosb, x2=x2: h.tensor_tensor(out=r[3][:], in0=x2, in1=cosb, op=ALU.mult),
                 reads=[qn.res, cst.res], writes=[r[3].res])
            S.op(S.dve, lambda h, r=r, q_=q_: h.tensor_tensor(out=q_[:, :, 32:64], in0=r[2][:], in1=r[3][:], op=ALU.add),
                 reads=[r[2].res, r[3].res], writes=[q_.res])
        else:
            S.op(S.dve, lambda h, q_=q_: h.tensor_copy(out=q_[:], in_=qn[:]), reads=[qn.res], writes=[q_.res])
        return q_

    def a_tr(q_, s):
        c0 = s * 128
        for hh in range(8):
            S.op(S.pe, lambda h, hh=hh, q_=q_: h.transpose(out=pqt[:, hh, :], in_=q_[:, hh, :], identity=self.ident[:]),
                 reads=[q_.res, self.ident.res], writes=[pqt.res], inc=(hh == 7))
        for hh in range(2):
            S.op(S.pe, lambda h, hh=hh, q_=q_: h.transpose(out=pkt[:, hh, :], in_=q_[:, 8 + hh, :], identity=self.ident[:]),
                 reads=[q_.res, self.ident.res], writes=[pkt.res], inc=(hh == 1))
        S.op(S.act, lambda h, c0=c0: h.activation(out=QTs[:, :, c0:c0 + 128], in_=pqt[:], func=AF.Copy), reads=[pqt.res], writes=[QTs.res])
        S.op(S.dve, lambda h, c0=c0: h.tensor_copy(out=KTs[:, :, c0:c0 + 128], in_=pkt[:]), reads=[pkt.res], writes=[KTs.res])

    def b_part(k, s):
        si, t0, n = tiles[k]
        uoff = st_info[si][3]
        u0 = uoff + t0
        hT = hTs[k % 2]
        c0 = s * 128
        for half in range(2):
            kk = nxt("pf", 2)
            for kc in range(8):
                S.op(S.pe, lambda h, kc=kc, c0=c0, kk=kk, half=half, hT=hT: h.matmul(pf[kk][:], hT[:, kc, c0:c0 + 128], Wv[:, kc, half * 512:(half + 1) * 512],
                                                                                  start=(kc == 0), stop=(kc == 7)),
                     reads=[hT.res, Wv.res], writes=[pf[kk].res], inc=(kc == 7))
            vi = nxt("vb2", 4)
            if half == 0:
                S.op(S.act, lambda h, kk=kk, vi=vi: h.activation(out=vb2[vi][:], in_=pf[kk][:], func=AF.Copy), reads=[pf[kk].res], writes=[vb2[vi].res])
            else:
                S.op(S.dve, lambda h, kk=kk, vi=vi: h.tensor_copy(out=vb2[vi][:], in_=pf[kk][:]), reads=[pf[kk].res], writes=[vb2[vi].res])
            dstv = self.GV[l] if half == 0 else self.MV[l]
            S.dma(S.sp, [(dstv[u0 + c0:u0 + c0 + 128, :], vb2[vi][:])], reads=[vb2[vi].res], writes=[self.R(dstv.name)])

    def c_part(k, fcs):
        si, t0, n = tiles[k]
        uoff = st_info[si][3]
        u0 = uoff + t0
        hT = hTs[k % 2]
        for fc in fcs:
            kk = nxt("pf", 2)
            for kc in range(8):
                S.op(S.pe, lambda h, kc=kc, fc=fc, kk=kk, n=n, hT=hT: h.matmul(pf[kk][:, :n], Wf[:, kc, fc * 128:(fc + 1) * 128], hT[:, kc, :n], start=(kc == 0), stop=(kc == 7)),
                     reads=[hT.res, Wf.res], writes=[pf[kk].res], inc=(kc == 7))
            if fc < 2:
                for d in range(2):
                    o_ = fo[nxt("fo", 8)]
                    S.op(S.dve, lambda h, kk=kk, n=n, d=d, fc=fc, o_=o_: h.scalar_tensor_tensor(out=o_[:, :n], in0=pf[kk][:, :n], scalar=0.125, in1=eb[d][fc][:, :n],
                                                                                           op0=ALU.mult, op1=ALU.mult),
                         reads=[pf[kk].res, eb[d][fc].res], writes=[o_.res])
                    S.dma(S.sp, [(self.QG[l][d, :, 2 * fc + hh2, u0:u0 + n], o_[hh2 * 64:(hh2 + 1) * 64, :n]) for hh2 in range(2)], reads=[o_.res], writes=[self.R(self.QG[l].name)])
            elif fc < 4:
                c2 = fc - 2
                for d in range(2):
                    o_ = fo[nxt("fo", 8)]
                    S.op(S.dve, lambda h, kk=kk, n=n, d=d, c2=c2, o_=o_: h.tensor_tensor(out=o_[:, :n], in0=pf[kk][:, :n], in1=enb[d][c2][:, :n], op=ALU.mult),
                         reads=[pf[kk].res, enb[d][c2].res], writes=[o_.res])
                    S.dma(S.sp, [(self.KG[l][d, :, 2 * c2 + hh2, u0:u0 + n], o_[hh2 * 64:(hh2 + 1) * 64, :n]) for hh2 in range(2)], reads=[o_.res], writes=[self.R(self.KG[l].name)])
                    o_ = fo[nxt("fo", 8)]
                    S.op(S.dve, lambda h, kk=kk, n=n, d=d, c2=c2, o_=o_: h.tensor_tensor(out=o_[:, :n], in0=pf[kk][:, :n], in1=ebl[d][c2][:, :n], op=ALU.mult),
                         reads=[pf[kk].res, ebl[d][c2].res], writes=[o_.res])
                    S.dma(S.sp, [(self.KH[l][d, :, 2 * c2 + hh2, u0:u0 + n], o_[hh2 * 64:(hh2 + 1) * 64, :n]) for hh2 in range(2)], reads=[o_.res], writes=[self.R(self.KH[l].name)])
            else:
                o_ = fo[nxt("fo", 8)]
                S.op(S.act, lambda h, kk=kk, n=n, o_=o_: h.activation(out=o_[:, :n], in_=pf[kk][:, :n], func=AF.Copy), reads=[pf[kk].res], writes=[o_.res])
                r0 = (fc - 4) * 128
                S.dma(S.sp, [(self.MQK[l][r0:r0 + 128, 2 + u0:2 + u0 + n], o_[:, :n])], reads=[o_.res], writes=[self.R(self.MQK[l].name)])

    for s in range(tiles[0][2] // 128):
        prep_sub(0, s)
    for k, (si, t0, n) in enumerate(tiles):
        A, tmp, src, uoff, rope_ = st_info[si]
        rope = rope_
        nt = n // 128
        u0 = uoff + t0
        hT = hTs[k % 2]
        S.dma(S.sp, [(H2Tv[:, :, u0:u0 + n], hT[:, :, :n])], reads=[hT.res], writes=[self.R(self.H2T[l].name)])
        g_part(k)
        order = [4, 5, 6, 7, 8, 9, 10, 11, 0, 1, 2, 3]
        per = (12 + nt - 1) // nt
        pend = None
        nxt_nt = tiles[k + 1][2] // 128 if k + 1 < len(tiles) else 0
        for s in range(nt):
            xi = prep_load(k + 1, s) if s < nxt_nt else None
            cst = rope_load(k, s)
            a_mm(k, s)
            q_ = a_chain(k, s, cst)
            if pend is not None:
                a_tr(*pend)
            pend = (q_, s)
            b_part(k, s)
            c_part(k, order[s * per:(s + 1) * per])
            if s < nxt_nt:
                prep_sub(k + 1, s, xi)
        a_tr(*pend)
        for s in range(nt, nxt_nt):
            prep_sub(k + 1, s)
        S.dma(S.sp, [(self.QT[l][:, :, u0:u0 + n], QTs[:, :, :n])], reads=[QTs.res], writes=[self.R(self.QT[l].name)])
        S.dma(S.sp, [(self.KT[l][:, :, u0:u0 + n], KTs[:, :, :n])], reads=[KTs.res], writes=[self.R(self.KT[l].name)])


Builder.phase_feat = phase_feat


def phase_attn(self, l, do_ctx):
    S = self.S
    T = self.T
    nbk = T // 128
    ones = self.sb("ones", [128, 128], F32)
    S.op(S.pool, lambda h: h.memset(ones[:], 1.0), writes=[ones.res])
    mP = self.sb("mP", [128, 4, 128], BF16)
    mN = self.sb("mN", [128, 4, 128], BF16)
    mtmp = self.sb("mtmp", [128, 128], F32)
    zer = self.sb("zer", [128, 128], F32)
    S.op(S.pool, lambda h: h.memset(zer[:], 0.0), writes=[zer.res])
    for (m_, sgn) in ((mP, 1), (mN, -1)):
        S.op(S.pool, lambda h, sgn=sgn: h.affine_select(out=mtmp[:], in_=zer[:], pattern=[[-sgn, 128]], compare_op=ALU.is_ge, fill=-30000.0,
                                                         base=0, channel_multiplier=sgn), reads=[zer.res], writes=[mtmp.res])
        S.op(S.pool, lambda h, m_=m_: h.tensor_copy(out=m_[:], in_=mtmp[:].unsqueeze(1).to_broadcast([128, 4, 128])), reads=[mtmp.res], writes=[m_.res])
    esk = self.sb("esk", [128, 2, 4, 128], F32)
    sk8 = self.sb("sk8", [128, 8], F32)
    S.dma(S.sp, [(sk8[64:65, :], self.attn_sink[l:l + 1, :])], writes=[sk8.res])
    S.op(S.act, lambda h: h.activation(out=sk8[64:65, :], in_=sk8[64:65, :], func=AF.Exp), reads=[sk8.res], writes=[sk8.res])
    S.op(S.dve, lambda h: h.tensor_copy(out=esk[64:65].rearrange("p g a b -> p (g a) b"), in_=sk8[64:65, :].unsqueeze(2).to_broadcast([1, 8, 128])),
         reads=[sk8.res], writes=[esk.res])
    KTc = self.sb("KTc", [64, 2, 256], BF16)
    S.dma(S.sp, [(KTc[:], self.KT[l][:, :, 0:256])], reads=[self.R(self.KT[l].name)], writes=[KTc.res])
    Vc = [self.sb(f"Vc{j}", [128, 2, 65], BF16) for j in range(2)]
    Vb = [self.sb(f"Vb{j}", [128, 2, 65], BF16) for j in range(4)]
    KTb = [self.sb(f"KTb{j}", [64, 2, 128], BF16) for j in range(4)]
    for v in Vc + Vb:
        S.op(S.pool, lambda h, v=v: h.memset(v[:], 1.0), writes=[v.res])
    for j in range(2):
        S.dma(S.sp, [(Vc[j][:, :, 0:64], self.VA[l][j * 128:(j + 1) * 128, :].rearrange("p (g d) -> p g d", d=64))],
              reads=[self.R(self.VA[l].name)], writes=[Vc[j].res])
    QTb = [self.sb(f"QTb{j}", [64, 8, 128], BF16) for j in range(2)]
    E = [self.sb(f"E{j}", [128, 4, 128], BF16) for j in range(4)]
    dn = [self.sb(f"dn{j}", [128, 512], F32) for j in range(2)]
    bcs = [self.sb(f"bcs{j}", [64, 512], F32) for j in range(2)]
    aT = [self.sb(f"aT{j}", [64, 4, 128], BF16) for j in range(2)]
    pS = [self.ps(f"pS{j}", [128, 512]) for j in range(3)]
    pO = [self.ps(f"pO{j}", [128, 512]) for j in range(2)]
    pB = [self.ps(f"pB{j}", [64, 512]) for j in range(2)]
    cnt = {}

    def nxt(key, n):
        v = cnt.get(key, 0)
        cnt[key] = v + 1
        return v % n

    def load_kb(m):
        i = m % 4
        u = LC + m * 128
        S.dma(S.sp, [(KTb[i][:], self.KT[l][:, :, u:u + 128])], reads=[self.R(self.KT[l].name)], writes=[KTb[i].res])
        S.dma(S.sp, [(Vb[i][:, :, 0:64], self.VA[l][u:u + 128, :].rearrange("p (g d) -> p g d", d=64))],
              reads=[self.R(self.VA[l].name)], writes=[Vb[i].res])

    pending = []

    def norm(g, po, u0):
        d_ = dn[nxt("dn", 2)]
        S.op(S.dve, lambda h, d_=d_, po=po, g=g: h.tensor_tensor(out=d_[64:65, :], in0=po[64:65, :], in1=esk[64:65, g].rearrange("p a b -> p (a b)"), op=ALU.add),
             reads=[po.res, esk.res], writes=[d_.res])
        S.op(S.dve, lambda h, d_=d_: h.reciprocal(out=d_[64:65, :], in_=d_[64:65, :]), reads=[d_.res], writes=[d_.res])
        pb = pB[nxt("pb", 2)]
        S.op(S.pe, lambda h, d_=d_, pb=pb: h.matmul(pb[:], ones[64:65, 0:64], d_[64:65, :], start=True, stop=True),
             reads=[d_.res, ones.res], writes=[pb.res])
        b_ = bcs[nxt("bcs", 2)]
        S.op(S.act, lambda h, b_=b_, pb=pb: h.activation(out=b_[:], in_=pb[:], func=AF.Copy), reads=[pb.res], writes=[b_.res])
        a_ = aT[nxt("aT", 2)]
        S.op(S.dve, lambda h, a_=a_, b_=b_, po=po: h.tensor_tensor(out=a_[:].rearrange("p a b -> p (a b)"), in0=po[0:64, :], in1=b_[:], op=ALU.mult),
             reads=[po.res, b_.res], writes=[a_.res])
        S.dma(S.sp, [(self.ATT[l][:, 4 * g:4 * g + 4, u0:u0 + 128], a_[:])], reads=[a_.res], writes=[self.R(self.ATT[l].name)])

    def qblock(u0, kbs):
        qi = nxt("q", 2)
        Q = QTb[qi]
        S.dma(S.sp, [(Q[:], self.QT[l][:, :, u0:u0 + 128])], reads=[self.R(self.QT[l].name)], writes=[Q.res])
        for g in range(2):
            po = pO[nxt("po", 2)]
            rhsq = Q[:, 4 * g:4 * g + 4, :].rearrange("p a b -> p (a b)")

            def score(idx, g=g, rhsq=rhsq):
                kt, vt, msk = kbs[idx]
                p = pS[nxt("ps", 3)]
                S.op(S.pe, lambda h, kt=kt, p=p, g=g, rhsq=rhsq, msk=msk: h.matmul(p[:], kt[0][:, g, kt[1]:kt[1] + 128], rhsq, start=True, stop=(msk is None)),
                     reads=[kt[0].res, Q.res], writes=[p.res], inc=(msk is None))
                if msk is not None:
                    S.op(S.pe, lambda h, p=p, msk=msk: h.matmul(p[:], self.ident[:], msk[:].rearrange("p a b -> p (a b)"), start=False, stop=True),
                         reads=[self.ident.res, msk.res], writes=[p.res])
                return p
            ps_list = [score(0)]
            for idx in range(len(kbs)):
                kt, vt, msk = kbs[idx]
                if idx + 1 < len(kbs):
                    ps_list.append(score(idx + 1))
                p = ps_list[idx]
                e = E[nxt("e", 4)]
                S.op(S.act, lambda h, p=p, e=e: h.activation(out=e[:].rearrange("p a b -> p (a b)"), in_=p[:], func=AF.Exp, scale=0.125),
                     reads=[p.res], writes=[e.res])
                S.op(S.pe, lambda h, e=e, vt=vt, po=po, idx=idx, g=g, kbs=kbs: h.matmul(po[0:65, :], vt[:, g, :], e[:].rearrange("p a b -> p (a b)"),
                                                                                      start=(idx == 0), stop=(idx == len(kbs) - 1)),
                     reads=[e.res, vt.res], writes=[po.res], inc=(idx == len(kbs) - 1))
            pending.append((g, po, u0))
            if len(pending) > 1:
                norm(*pending.pop(0))

    ckb = [((KTc, 0), Vc[0], None), ((KTc, 128), Vc[1], None)]
    if do_ctx:
        for n in range(2):
            qblock(n * 128, ckb)
    load_kb(0)
    for n in range(nbk):
        if n + 1 < nbk:
            load_kb(n + 1)
        kbs = []
        if n - 1 >= 0:
            kbs.append(((KTb[(n - 1) % 4], 0), Vb[(n - 1) % 4], mP))
        kbs.append(((KTb[n % 4], 0), Vb[n % 4], None))
        if n + 1 < nbk:
            kbs.append(((KTb[(n + 1) % 4], 0), Vb[(n + 1) % 4], mN))
        qblock(LC + n * 128, kbs + ckb)
    while pending:
        norm(*pending.pop(0))


Builder.phase_attn = phase_attn


def scan_groups(T):
    return [(0, LC)] + [(LC + t0, min(512, T - t0)) for t0 in range(0, T, 512)]


def scan_order(T, d):
    groups = scan_groups(T)
    order = []
    if d == 0:
        for gi, (u0, n) in enumerate(groups):
            for c in range(n // 64):
                order.append((gi, c))
    else:
        gis = [0] + list(range(len(groups) - 1, 0, -1))
        for gi in gis:
            u0, n = groups[gi]
            for c in range(n // 64 - 1, -1, -1):
                order.append((gi, c))
    return order


def phase_gla(self, l):
    S = self.S
    T = self.T
    EL = self.EL
    groups = scan_groups(T)
    ones = self.sb("ones", [64, 64], F32)
    S.op(S.pool, lambda h: h.memset(ones[:], 1.0), writes=[ones.res])
    mtmp = self.sb("mtmp", [64, 64], F32)
    msk = [self.sb(f"msk{d}", [64, 64], BF16) for d in range(2)]
    for d, sgn in ((0, -1), (1, 1)):
        S.op(S.pool, lambda h, sgn=sgn: h.affine_select(out=mtmp[:], in_=ones[:], pattern=[[-sgn, 64]], compare_op=ALU.is_ge, fill=0.0,
                                                         base=0, channel_multiplier=sgn), reads=[ones.res], writes=[mtmp.res])
        S.op(S.pool, lambda h, d=d: h.tensor_copy(out=msk[d][:], in_=mtmp[:]), reads=[mtmp.res], writes=[msk[d].res])
    Sf = [self.sb(f"Sf{d}", [64, 4, 128], F32) for d in range(2)]
    Sb = [self.sb(f"Sb{d}", [64, 4, 128], BF16) for d in range(2)]
    for d in range(2):
        S.op(S.pool, lambda h, d=d: h.memset(Sf[d][:], 0.0), writes=[Sf[d].res])
        S.op(S.pool, lambda h, d=d: h.memset(Sb[d][:], 0.0), writes=[Sb[d].res])
    qg = [[self.sb(f"qg{d}{i}", [64, 4, 512], BF16) for i in range(2)] for d in range(2)]
    kg = [[self.sb(f"kg{d}{i}", [64, 4, 512], BF16) for i in range(2)] for d in range(2)]
    kh = [[self.sb(f"kh{d}{i}", [64, 4, 512], BF16) for i in range(2)] for d in range(2)]
    vg = [[self.sb(f"vg{d}{i}", [64, 8, 512], BF16) for i in range(2)] for d in range(2)]
    am = [[self.sb(f"am{d}{i}", [64, 4, 64], BF16) for i in range(2)] for d in range(2)]
    kt = [[self.sb(f"kt{d}{i}", [64, 4, 64], BF16) for i in range(2)] for d in range(2)]
    ob = [[self.sb(f"ob{d}{i}", [64, 512], F32) for i in range(2)] for d in range(2)]
    pA = [self.ps(f"pA{d}", [64, 256]) for d in range(2)]
    pK = [self.ps(f"pK{d}", [64, 256], BF16) for d in range(2)]
    pO = [self.ps(f"pO{d}", [64, 512]) for d in range(2)]
    pN = [self.ps(f"pN{d}", [64, 512]) for d in range(2)]
    orders = [scan_order(T, d) for d in range(2)]
    nsteps = len(orders[0])
    gcount = [0, 0]
    cur = [None, None]

    def load_group(d, gi):
        i = gcount[d] % 2
        gcount[d] += 1
        u0, n = groups[gi]
        nch = n // 64
        for (dst, srcT) in ((qg[d][i], self.QG[l]), (kg[d][i], self.KG[l]), (kh[d][i], self.KH[l])):
            S.dma(S.sp, [(dst[:, :, :n], srcT[d, :, :, u0:u0 + n])], reads=[self.R(srcT.name)], writes=[dst.res])
        S.dma(S.sp, [(vg[d][i][:, :nch, :], self.GV[l][u0:u0 + n, :].rearrange("(c p) f -> p c f", p=64))], reads=[self.R(self.GV[l].name)],
              writes=[vg[d][i].res])
        return i

    for step in range(nsteps):
        for d in range(2):
            gi, c = orders[d][step]
            if cur[d] is None or cur[d][0] != gi:
                cur[d] = (gi, load_group(d, gi))
            bi = cur[d][1]
            u0, n = groups[gi]
            o = c * 64
            chunk = (u0 + o) // 64
            Q, Kg, Kh, V = qg[d][bi], kg[d][bi], kh[d][bi], vg[d][bi]
            k2 = step % 2
            AM, KTt, OB = am[d][k2], kt[d][k2], ob[d][k2]
            for hh in range(4):
                S.op(S.pe, lambda h, hh=hh, d=d, Kg=Kg, Q=Q, o=o: h.matmul(pA[d][:, hh * 64:(hh + 1) * 64], Kg[:, hh, o:o + 64], Q[:, hh, o:o + 64], start=True, stop=True),
                     reads=[Kg.res, Q.res], writes=[pA[d].res], inc=(hh == 3))
            for hh in range(4):
                S.op(S.pe, lambda h, hh=hh, d=d, Kh=Kh, o=o: h.transpose(out=pK[d][:, hh * 64:(hh + 1) * 64], in_=Kh[:, hh, o:o + 64], identity=self.ident[0:64, 0:64]),
                     reads=[Kh.res, self.ident.res], writes=[pK[d].res], inc=(hh == 3))
            S.op(S.dve, lambda h, d=d, AM=AM: h.tensor_tensor(out=AM[:], in0=pA[d][:].rearrange("p (a b) -> p a b", b=64),
                                                              in1=msk[d][:].unsqueeze(1).to_broadcast([64, 4, 64]), op=ALU.mult),
                 reads=[pA[d].res, msk[d].res], writes=[AM.res])
            S.op(S.act, lambda h, d=d, KTt=KTt: h.activation(out=KTt[:].rearrange("p a b -> p (a b)"), in_=pK[d][:], func=AF.Copy), reads=[pK[d].res], writes=[KTt.res])
            for hh in range(4):
                S.op(S.pe, lambda h, hh=hh, d=d, AM=AM, V=V, c=c: h.matmul(pO[d][:, hh * 128:(hh + 1) * 128], AM[:, hh, :], V[:, c, hh * 128:(hh + 1) * 128], start=True, stop=False),
                     reads=[AM.res, V.res], writes=[pO[d].res], inc=False)
                S.op(S.pe, lambda h, hh=hh, d=d, Q=Q, o=o: h.matmul(pO[d][:, hh * 128:(hh + 1) * 128], Q[:, hh, o:o + 64], Sb[d][:, hh, :], start=False, stop=True),
                     reads=[Q.res, Sb[d].res], writes=[pO[d].res], inc=(hh == 3))
            for hh in range(4):
                S.op(S.pe, lambda h, hh=hh, d=d, KTt=KTt, V=V, c=c: h.matmul(pN[d][:, hh * 128:(hh + 1) * 128], KTt[:, hh, :], V[:, c, hh * 128:(hh + 1) * 128], start=True, stop=True),
                     reads=[KTt.res, V.res], writes=[pN[d].res], inc=(hh == 3))
            S.op(S.act, lambda h, d=d, OB=OB: h.activation(out=OB[:], in_=pO[d][:], func=AF.Copy), reads=[pO[d].res], writes=[OB.res])
            S.dma(S.sp, [(self.OG[l][d, u0 + o:u0 + o + 64, :], OB[:])], reads=[OB.res], writes=[self.R(self.OG[l].name)])
            S.op(S.dve, lambda h, d=d, chunk=chunk: h.tensor_tensor(out=Sf[d][:], in0=Sf[d][:], in1=EL[:, :, d, chunk:chunk + 1].to_broadcast([64, 4, 128]), op=ALU.mult),
                 reads=[Sf[d].res, self.EL.res], writes=[Sf[d].res])
            S.op(S.dve, lambda h, d=d: h.tensor_tensor(out=Sf[d][:].rearrange("p a b -> p (a b)"), in0=Sf[d][:].rearrange("p a b -> p (a b)"), in1=pN[d][:], op=ALU.add),
                 reads=[Sf[d].res, pN[d].res], writes=[Sf[d].res])
            S.op(S.act, lambda h, d=d: h.activation(out=Sb[d][:], in_=Sf[d][:], func=AF.Copy), reads=[Sf[d].res], writes=[Sb[d].res])


Builder.phase_gla = phase_gla


LN_KS = float(-0.5 * np.log(128.0))


def phase_ml_gates(self, l):
    S = self.S
    sel = self.sel
    DEC = self.DEC
    T = self.T
    TT = self.TT
    nch = TT // 64
    bA = self.sb("bA", [4, TT], F32)
    bL = self.sb("bL", [4, TT], F32)
    bC = self.sb("bC", [4, TT], F32)
    bG = self.sb("bG", [4, TT], F32)
    bX = self.sb("bX", [4, TT], F32)
    onesr = self.sb("onesr", [4, TT], BF16)
    S.op(S.pool, lambda h: h.memset(onesr[:], 1.0), writes=[onesr.res])
    gl = self.sb("gl", [4, nch], F32)
    gp = self.sb("gp", [4, nch], F32)
    dd = self.sb("dd", [4, nch], F32)
    ibc = self.sb("ibc", [4, 2], F32)
    pD = self.ps("pD", [128, 512])
    for d in range(2):
        S.dma(S.sp, [(bA[:], self.GATES[l][d * 4:(d + 1) * 4, :])], reads=[self.R(self.GATES[l].name)], writes=[bA.res])
        S.dma(S.sp, [(bL[:], self.GATES[l][8 + d * 4:8 + (d + 1) * 4, :])], reads=[self.R(self.GATES[l].name)], writes=[bL.res])
        S.dma(S.sp, [(ibc[:, 0:1], self.mlstm_ib[l, d, :].rearrange("(h o) -> h o", o=1)), (ibc[:, 1:2], self.mlstm_fb[l, d, :].rearrange("(h o) -> h o", o=1))],
              writes=[ibc.res])
        S.op(S.dve, lambda h: h.tensor_scalar(out=ibc[:, 1:2], in0=ibc[:, 1:2], scalar1=-1.0, scalar2=None, op0=ALU.mult), reads=[ibc.res], writes=[ibc.res])
        S.op(S.act, lambda h: h.activation(out=bL[:], in_=bL[:], func=AF.Exp, scale=-1.0, bias=ibc[:, 1:2]), reads=[bL.res, ibc.res], writes=[bL.res])
        S.op(S.act, lambda h: h.activation(out=bL[:], in_=bL[:], func=AF.Ln, bias=1.0), reads=[bL.res], writes=[bL.res])

        def scan(out, src, op1):
            if d == 0:
                S.op(S.dve, lambda h: h.tensor_tensor_scan(out=out[:], data0=onesr[:], data1=src[:], initial=0.0, op0=ALU.mult, op1=op1),
                     reads=[src.res, onesr.res], writes=[out.res])
            else:
                S.op(S.dve, lambda h: h.tensor_tensor_scan(out=rev_last(out[:, 0:LC]), data0=onesr[:, 0:LC], data1=rev_last(src[:, 0:LC]), initial=0.0, op0=ALU.mult, op1=op1),
                     reads=[src.res, onesr.res], writes=[out.res])
                S.op(S.dve, lambda h: h.tensor_tensor_scan(out=rev_last(out[:, LC:TT]), data0=onesr[:, LC:TT], data1=rev_last(src[:, LC:TT]), initial=out[:, 0:1], op0=ALU.mult, op1=op1),
                     reads=[src.res, onesr.res, out.res], writes=[out.res])
        scan(bC, bL, ALU.add)
        S.op(S.dve, lambda h: h.scalar_tensor_tensor(out=bA[:], in0=bA[:], scalar=ibc[:, 0:1], in1=bC[:], op0=ALU.add, op1=ALU.add),
             reads=[bA.res, ibc.res, bC.res], writes=[bA.res])
        scan(bG, bA, ALU.max)
        G3 = bG[:].rearrange("p (c b) -> p c b", b=64)
        lastpos = 63 if d == 0 else 0
        S.op(S.dve, lambda h, lastpos=lastpos, G3=G3: h.tensor_copy(out=gl[:], in_=G3[:, :, lastpos]), reads=[bG.res], writes=[gl.res])
        S.op(S.dve, lambda h: h.memset(gp[:], 0.0), writes=[gp.res])
        if d == 0:
            S.op(S.dve, lambda h: h.tensor_copy(out=gp[:, 1:nch], in_=gl[:, 0:nch - 1]), reads=[gl.res], writes=[gp.res])
        else:
            S.op(S.dve, lambda h: h.tensor_copy(out=gp[:, 0:3], in_=gl[:, 1:4]), reads=[gl.res], writes=[gp.res])
            S.op(S.dve, lambda h: h.tensor_copy(out=gp[:, 4:nch - 1], in_=gl[:, 5:nch]), reads=[gl.res], writes=[gp.res])
            S.op(S.dve, lambda h: h.tensor_copy(out=gp[:, nch - 1:nch], in_=gl[:, 0:1]), reads=[gl.res], writes=[gp.res])
        L3 = bL[:].rearrange("p (c b) -> p c b", b=64)
        X3 = bX[:].rearrange("p (c b) -> p c b", b=64)
        A3 = bA[:].rearrange("p (c b) -> p c b", b=64)
        S.op(S.dve, lambda h, L3=L3, G3=G3: h.tensor_tensor(out=L3, in0=gp[:].unsqueeze(2).to_broadcast([4, nch, 64]), in1=G3, op=ALU.subtract),
             reads=[gp.res, bG.res], writes=[bL.res])
        S.op(S.act, lambda h: h.activation(out=bL[:], in_=bL[:], func=AF.Exp), reads=[bL.res], writes=[bL.res])
        S.op(S.dve, lambda h: h.tensor_tensor(out=bC[:], in0=bC[:], in1=bG[:], op=ALU.subtract), reads=[bC.res, bG.res], writes=[bC.res])
        S.op(S.act, lambda h: h.activation(out=bC[:], in_=bC[:], func=AF.Exp), reads=[bC.res], writes=[bC.res])
        S.op(S.dve, lambda h, X3=X3, A3=A3: h.tensor_tensor(out=X3, in0=A3, in1=gl[:].unsqueeze(2).to_broadcast([4, nch, 64]), op=ALU.subtract),
             reads=[bA.res, gl.res], writes=[bX.res])
        S.op(S.dve, lambda h: h.tensor_scalar(out=bX[:], in0=bX[:], scalar1=LN_KS, scalar2=None, op0=ALU.add), reads=[bX.res], writes=[bX.res])
        S.op(S.act, lambda h: h.activation(out=bX[:], in_=bX[:], func=AF.Exp), reads=[bX.res], writes=[bX.res])
        S.op(S.dve, lambda h: h.tensor_tensor(out=dd[:], in0=gp[:], in1=gl[:], op=ALU.subtract), reads=[gp.res, gl.res], writes=[dd.res])
        S.op(S.act, lambda h: h.activation(out=dd[:], in_=dd[:], func=AF.Exp), reads=[dd.res], writes=[dd.res])
        for hh in range(4):
            S.op(S.pe, lambda h, hh=hh: h.matmul(pD[:, :nch], sel[:, hh, :], dd[:], start=True, stop=True), reads=[self.sel.res, dd.res], writes=[pD.res])
            S.op(S.dve, lambda h, hh=hh, d=d: h.tensor_copy(out=DEC[:, d, hh, :], in_=pD[:, :nch]), reads=[pD.res], writes=[self.DEC.res])
        for qi, buf in enumerate((bA, bG, bL, bC, bX)):
            S.dma(S.sp, [(self.MROWS[l][d, qi, :, :], buf[:])], reads=[buf.res], writes=[self.R(self.MROWS[l].name)])


def phase_ml_conv(self, l):
    S = self.S
    T = self.T
    TT = self.TT
    wcol = self.sb("wcol", [128, 8, 5], F32)
    cb = self.sb("cb", [128, 8], F32)
    S.dma(S.sp, [(wcol[:, :, k], self.conv_w[l, k, :].rearrange("(fc p) -> p fc", p=128)) for k in range(5)], writes=[wcol.res], allow_slow_non_contiguous=True)
    S.dma(S.sp, [(cb[:], self.conv_b[l, :].rearrange("(fc p) -> p fc", p=128))], writes=[cb.res], allow_slow_non_contiguous=True)
    diagw = self.sb("diagw", [128, 8, 5, 128], BF16)
    for fc in range(8):
        for k in range(5):
            e = S.dve if (fc * 5 + k) % 2 == 0 else S.pool
            S.op(e, lambda h, fc=fc, k=k: h.tensor_scalar(out=diagw[:, fc, k, :], in0=self.identf[:], scalar1=wcol[:, fc, k:k + 1], scalar2=None, op0=ALU.mult),
                 reads=[self.identf.res, wcol.res], writes=[diagw.res])
    xq = [self.sb(f"xq{i}", [128, 8, 516], BF16) for i in range(2)]
    oc = [self.sb(f"oc{i}", [128, 512], BF16) for i in range(3)]
    pc = [self.ps(f"pc{i}", [128, 512]) for i in range(2)]
    MQKv = self.MQK[l].rearrange("(c p) t -> p c t", p=128)
    k2 = 0
    for gi, (u0, n) in enumerate(scan_groups(T)):
        X = xq[gi % 2]
        S.dma(S.sp, [(X[:, 0:4, 0:n + 4], MQKv[:, 0:4, u0:u0 + n + 4]), (X[:, 4:8, 0:n + 4], MQKv[:, 4:8, u0:u0 + n + 4])], reads=[self.R(self.MQK[l].name)], writes=[X.res])
        if u0 == 0 or u0 == LC:
            S.op(S.pool, lambda h, X=X: h.memset(X[:, :, 0:2], 0.0), writes=[X.res])
        if u0 + n == LC or u0 + n == TT:
            S.op(S.pool, lambda h, X=X, n=n: h.memset(X[:, :, n + 2:n + 4], 0.0), writes=[X.res])
        for fc in range(8):
            p = pc[k2 % 2]
            o_ = oc[k2 % 3]
            k2 += 1
            for k in range(5):
                S.op(S.pe, lambda h, fc=fc, k=k, p=p, X=X, n=n: h.matmul(p[:, :n], diagw[:, fc, k, :], X[:, fc, k:k + n], start=(k == 0), stop=(k == 4)),
                     reads=[diagw.res, X.res], writes=[p.res], inc=(k == 4))
            S.op(S.act, lambda h, fc=fc, p=p, o_=o_, n=n: h.activation(out=o_[:, :n], in_=p[:, :n], func=AF.Silu, bias=cb[:, fc:fc + 1]), reads=[p.res, cb.res], writes=[o_.res])
            S.dma(S.sp, [(self.MQC[l][fc * 128:(fc + 1) * 128, u0:u0 + n], o_[:, :n])], reads=[o_.res], writes=[self.R(self.MQC[l].name)])


def phase_ml_scan(self, l):
    S = self.S
    T = self.T
    sel = self.sel
    DEC = self.DEC
    groups = scan_groups(T)
    cfill = self.sb("cfill", [64, 64], F32)
    S.op(S.pool, lambda h: h.memset(cfill[:], LN_KS), writes=[cfill.res])
    mb = [self.sb(f"mb{d}", [64, 64], F32) for d in range(2)]
    for d, sgn in ((0, -1), (1, 1)):
        S.op(S.pool, lambda h, sgn=sgn, d=d: h.affine_select(out=mb[d][:], in_=cfill[:], pattern=[[-sgn, 64]], compare_op=ALU.is_ge, fill=-30000.0,
                                                              base=0, channel_multiplier=sgn), reads=[cfill.res], writes=[mb[d].res])
    negones = self.sb("negones", [4, 128], F32)
    S.op(S.pool, lambda h: h.memset(negones[:], -1.0), writes=[negones.res])
    posones = self.sb("posones", [4, 128], F32)
    S.op(S.pool, lambda h: h.memset(posones[:], 1.0), writes=[posones.res])
    mbr = [self.sb(f"mbr{d}", [64, 4, 64], F32) for d in range(2)]
    for d in range(2):
        S.op(S.pool, lambda h, d=d: h.tensor_copy(out=mbr[d][:], in_=mb[d][:].unsqueeze(1).to_broadcast([64, 4, 64])), reads=[mb[d].res], writes=[mbr[d].res])
    Dg = [self.sb(f"Dg{i}", [4, 3, 4, 512], F32) for i in range(2)]
    DgH = [self.sb(f"DgH{i}", [4, 2, 4, 512], BF16) for i in range(2)]
    DgL = [self.sb(f"DgL{i}", [4, 2, 4, 512], BF16) for i in range(2)]
    posb = self.sb("posb", [4, 128], BF16)
    S.op(S.pool, lambda h: h.memset(posb[:], 1.0), writes=[posb.res])
    Cf = self.sb("Cf", [128, 4, 129], F32)
    Cb = self.sb("Cb", [128, 4, 129], BF16)
    qk = [self.sb(f"qk{i}", [128, 8, 512], BF16) for i in range(2)]
    vg = [self.sb(f"vgm{i}", [64, 8, 4, 129], BF16) for i in range(2)]
    rows = [self.sb(f"rows{i}", [4, 5, 512], F32) for i in range(2)]
    for v in vg:
        S.op(S.pool, lambda h, v=v: h.memset(v[:], 1.0), writes=[v.res])
    wT = [self.sb(f"wT{i}", [64, 256], F32) for i in range(3)]
    sT = [self.sb(f"sT{i}", [64, 4, 64], BF16) for i in range(3)]
    qks = [self.sb(f"qks{i}", [128, 8, 64], BF16) for i in range(3)]
    khat = [self.sb(f"khat{i}", [64, 4, 128], BF16) for i in range(3)]
    enm = [self.sb(f"enm{i}", [64, 4], F32) for i in range(3)]
    rr = [self.sb(f"rr{i}", [64, 4], F32) for i in range(3)]
    ho = [self.sb(f"ho{i}", [64, 4, 128], F32) for i in range(3)]
    pWS = self.ps("pWS", [64, 512])
    pB = self.ps("pB", [128, 8, 64])
    pK = self.ps("pK", [64, 4, 128], BF16)
    pO = self.ps("pO", [64, 1024])
    pN = self.ps("pN", [128, 1024])
    pO3 = pO[:].rearrange("p (h e) -> p h e", e=256)
    pN3 = pN[:].rearrange("p (h e) -> p h e", e=256)
    gcount = [0]

    def load_group(d, gi):
        i = gcount[0] % 2
        gcount[0] += 1
        u0, n = groups[gi]
        nchg = n // 64
        S.dma(S.sp, [(qk[i][:, 0:4, :n], self.MQC[l].rearrange("(c p) t -> p c t", p=128)[:, 0:4, u0:u0 + n]),
                     (qk[i][:, 4:8, :n], self.MQC[l].rearrange("(c p) t -> p c t", p=128)[:, 4:8, u0:u0 + n])], reads=[self.R(self.MQC[l].name)], writes=[qk[i].res])
        S.dma(S.sp, [(vg[i][:, c, :, 0:128], self.MV[l][u0 + c * 64:u0 + (c + 1) * 64, :].rearrange("p (h e) -> p h e", e=128)) for c in range(nchg)],
              reads=[self.R(self.MV[l].name)], writes=[vg[i].res])
        S.dma(S.sp, [(rows[i][:, :, :n], self.MROWS[l][d, :, :, u0:u0 + n].rearrange("q h t -> h q t"))], reads=[self.R(self.MROWS[l].name)], writes=[rows[i].res])
        for qd, qs in enumerate((1, 2, 4)):
            S.op(S.pool, lambda h, i=i, qd=qd, qs=qs, n=n: h.tensor_tensor(out=Dg[i][:, qd, :, :n], in0=rows[i][:, qs:qs + 1, :n].to_broadcast([4, 4, n]),
                                                                       in1=self.identf[0:4, 0:4].unsqueeze(2).to_broadcast([4, 4, n]), op=ALU.mult),
                 reads=[rows[i].res, self.identf.res], writes=[Dg[i].res])
        return i

    for d in range(2):
        S.op(S.pool, lambda h: h.memset(Cf[:], 0.0), writes=[Cf.res])
        S.op(S.pool, lambda h: h.memset(Cb[:], 0.0), writes=[Cb.res])
        order = scan_order(T, d)
        cur = None
        info = []
        for (gi, c) in order:
            if cur is None or cur[0] != gi:
                cur = (gi, None)
            info.append((gi, c))
        bufof = {}

        def stageA(step):
            gi, c = order[step]
            if gi not in bufof:
                bufof.clear()
                bufof[gi] = load_group(d, gi)
            bi = bufof[gi]
            o = c * 64
            k2 = step % 3
            QK, R_ = qk[bi], rows[bi]
            DG = Dg[bi]
            S.op(S.pe, lambda h, R_=R_, o=o: h.matmul(pWS[:, 0:256], R_[:, 0, o:o + 64], sel[:, :, 0:64], start=True, stop=False),
                 reads=[R_.res, sel.res], writes=[pWS.res], inc=False)
            S.op(S.pe, lambda h, DG=DG, o=o: h.matmul(pWS[:, 0:256], negones[:, 0:64], DG[:, 0, :, o:o + 64], start=False, stop=False),
                 reads=[DG.res, negones.res], writes=[pWS.res], inc=False)
            S.op(S.pe, lambda h, d=d: h.matmul(pWS[:, 0:256], self.identf[0:64, 0:64], mbr[d][:], start=False, stop=True),
                 reads=[self.identf.res, mbr[d].res], writes=[pWS.res], inc=False)
            for hh in range(4):
                S.op(S.pe, lambda h, hh=hh, QK=QK, o=o: h.matmul(pWS[:, 256 + hh * 64:256 + (hh + 1) * 64], QK[:, 4 + hh, o:o + 64], QK[:, hh, o:o + 64], start=True, stop=True),
                     reads=[QK.res], writes=[pWS.res], inc=(hh == 3))
            S.op(S.act, lambda h, k2=k2: h.activation(out=wT[k2][:], in_=pWS[:, 0:256], func=AF.Exp), reads=[pWS.res], writes=[wT[k2].res])
            S.op(S.dve, lambda h, k2=k2: h.tensor_tensor(out=sT[k2][:].rearrange("p a b -> p (a b)"), in0=pWS[:, 256:512], in1=wT[k2][:], op=ALU.mult),
                 reads=[pWS.res, wT[k2].res], writes=[sT[k2].res])
            S.op(S.pe, lambda h, DG=DG, o=o: h.matmul(pB[:], posones[:, :], DG[:, 1:3, :, o:o + 64], start=True, stop=True),
                 reads=[DG.res, posones.res], writes=[pB.res])
            S.op(S.dve, lambda h, k2=k2, QK=QK, o=o: h.tensor_tensor(out=qks[k2][:], in0=QK[:, :, o:o + 64], in1=pB[:], op=ALU.mult),
                 reads=[QK.res, pB.res], writes=[qks[k2].res])

        def stageA2(step):
            k2 = step % 3
            for hh in range(4):
                S.op(S.pe, lambda h, hh=hh, k2=k2: h.transpose(out=pK[:, hh, :], in_=qks[k2][:, 4 + hh, :], identity=self.ident[:]),
                     reads=[qks[k2].res, self.ident.res], writes=[pK.res], inc=(hh == 3))
            S.op(S.act, lambda h, k2=k2: h.activation(out=khat[k2][:], in_=pK[:], func=AF.Copy), reads=[pK.res], writes=[khat[k2].res])

        def stageB(step):
            gi, c = order[step]
            u0, n = groups[gi]
            o = c * 64
            chunk = (u0 + o) // 64
            k2 = step % 3
            V, R_ = vgbuf[step], rowbuf[step]
            for hh in range(4):
                S.op(S.pe, lambda h, hh=hh, k2=k2, V=V, c=c: h.matmul(pN[:, hh * 256:hh * 256 + 129], khat[k2][:, hh, :], V[:, c, hh, :], start=True, stop=True),
                     reads=[khat[k2].res, V.res], writes=[pN.res], inc=(hh == 3))
            S.op(S.pe, lambda h, R_=R_, o=o: h.matmul(pO[:, 200:204], R_[:, 3, o:o + 64], self.identf[0:4, 0:4], start=True, stop=True),
                 reads=[R_.res, self.identf.res], writes=[pO.res], inc=False)
            for hh in range(4):
                S.op(S.pe, lambda h, hh=hh, k2=k2, V=V, c=c: h.matmul(pO[:, hh * 256:hh * 256 + 129], sT[k2][:, hh, :], V[:, c, hh, :], start=True, stop=False),
                     reads=[sT[k2].res, V.res], writes=[pO.res], inc=False)
                S.op(S.pe, lambda h, hh=hh, k2=k2: h.matmul(pO[:, hh * 256:hh * 256 + 129], qks[k2][:, hh, :], Cb[:, hh, :], start=False, stop=True),
                     reads=[qks[k2].res, Cb.res], writes=[pO.res], inc=(hh == 3))
            S.op(S.act, lambda h, k2=k2: h.activation(out=enm[k2][:], in_=pO[:, 200:204], func=AF.Copy), reads=[pO.res], writes=[enm[k2].res])
            S.op(S.act, lambda h, k2=k2: h.activation(out=rr[k2][:], in_=pO3[:, :, 128], func=AF.Abs), reads=[pO.res], writes=[rr[k2].res])
            S.op(S.pool, lambda h, d=d, chunk=chunk: h.tensor_tensor(out=Cf[:], in0=Cf[:], in1=DEC[:, d, :, chunk:chunk + 1].to_broadcast([128, 4, 129]), op=ALU.mult),
                 reads=[Cf.res, DEC.res], writes=[Cf.res])
            S.op(S.dve, lambda h: h.tensor_tensor(out=Cf[:], in0=Cf[:], in1=pN3[:, :, 0:129], op=ALU.add), reads=[Cf.res, pN.res], writes=[Cf.res])
            S.op(S.act, lambda h: h.activation(out=Cb[:], in_=Cf[:], func=AF.Copy), reads=[Cf.res], writes=[Cb.res])
            S.op(S.dve, lambda h, k2=k2: h.tensor_tensor(out=rr[k2][:], in0=rr[k2][:], in1=enm[k2][:], op=ALU.max), reads=[rr[k2].res, enm[k2].res], writes=[rr[k2].res])
            S.op(S.dve, lambda h, k2=k2: h.reciprocal(out=rr[k2][:], in_=rr[k2][:]), reads=[rr[k2].res], writes=[rr[k2].res])
            S.op(S.dve, lambda h, k2=k2: h.tensor_tensor(out=ho[k2][:], in0=pO3[:, :, 0:128], in1=rr[k2][:].unsqueeze(2).to_broadcast([64, 4, 128]), op=ALU.mult),
                 reads=[pO.res, rr[k2].res], writes=[ho[k2].res])
            S.dma(S.sp, [(self.OM[l][d, u0 + o:u0 + o + 64, :], ho[k2][:].rearrange("p a b -> p (a b)"))], reads=[ho[k2].res], writes=[self.R(self.OM[l].name)])

        vgbuf = {}
        rowbuf = {}

        def A(step):
            stageA(step)
            gi, c = order[step]
            vgbuf[step] = vg[bufof[gi]]
            rowbuf[step] = rows[bufof[gi]]
        A(0)
        if len(order) > 1:
            A(1)
        stageA2(0)
        for step in range(len(order)):
            if step + 2 < len(order):
                A(step + 2)
            if step + 1 < len(order):
                stageA2(step + 1)
            stageB(step)


Builder.phase_ml_gates = phase_ml_gates
Builder.phase_ml_conv = phase_ml_conv
Builder.phase_ml_scan = phase_ml_scan


def phase_merge(self, l, streams):
    S = self.S
    win = self.w_in[l].rearrange("(kc p) n -> p kc n", p=128)
    Wm = self.sb("Wm", [128, 8, 4096], BF16)
    Wmr = [Res(f"Wm{k}") for k in range(4)]
    for ki, k0 in enumerate(range(0, 8, 2)):
        S.dma(S.pool, [(Wm[:, k0:k0 + 2, 0:512], win[:, k0:k0 + 2, O_GR:O_GR + 512]),
                       (Wm[:, k0:k0 + 2, 512:1024], win[:, k0:k0 + 2, O_MO:O_MO + 512]),
                       (Wm[:, k0:k0 + 2, 1024:4096], win[:, k0:k0 + 2, O_SA:O_SA + 3072])], writes=[Wmr[ki]])
    Woa = self.sb("Woa", [64, 8, D], BF16)
    Wog = self.sb("Wog", [128, 4, D], BF16)
    Wom = self.sb("Wom", [128, 4, D], BF16)
    Wo = self.sb("Wo", [128, 8, D], BF16)
    S.dma(S.pool, [(Woa[:], self.w_out_attn[l].rearrange("(h p) n -> p h n", p=64))], writes=[Woa.res])
    S.dma(S.pool, [(Wog[:], self.w_out_gla[l].rearrange("(c p) n -> p c n", p=128))], writes=[Wog.res])
    S.dma(S.pool, [(Wom[:], self.w_out_mlstm[l].rearrange("(c p) n -> p c n", p=128))], writes=[Wom.res])
    S.dma(S.pool, [(Wo[:, 0:4, :], self.w_o[l].rearrange("(c p) n -> p c n", p=128)[:, 0:4, :]),
                   (Wo[:, 4:8, :], self.w_o[l].rearrange("(c p) n -> p c n", p=128)[:, 4:8, :])], writes=[Wo.res])
    gains = self.sb("gains", [128, 2, 128], F32)
    S.dma(S.sp, [(gains[:, 0, :], bcast_rows(self.gla_norm[l:l + 1, :], 128)), (gains[:, 1, :], bcast_rows(self.mlstm_norm[l:l + 1, :], 128))], writes=[gains.res])
    eps_col = self.eps_col
    hT = [self.sb(f"mhT{i}", [128, 8, 128], BF16) for i in range(2)]
    aTt = [self.sb(f"maT{i}", [64, 8, 128], BF16) for i in range(2)]
    og = [self.sb(f"mog{i}", [128, 2, 512], F32) for i in range(2)]
    om = [self.sb(f"mom{i}", [128, 2, 512], F32) for i in range(2)]
    xr = [self.sb(f"mxr{i}", [128, D], F32) for i in range(2)]
    gts = [self.sb(f"mgt{i}", [128, 8, 512], F32) for i in range(2)]
    sq = self.sb("msq", [128, 512], F32)
    ssqs = [self.sb(f"mssq{i}", [128, 2, 4], F32) for i in range(2)]
    bn = [self.sb(f"mbn{i}", [128, 512], F32) for i in range(2)]
    bbs = [[self.sb(f"mbb{i}{j}", [128, 512], BF16) for j in range(2)] for i in range(2)]
    bTs = [[self.sb(f"mbT{i}{j}", [128, 4, 128], BF16) for j in range(2)] for i in range(2)]
    yb = self.sb("myb", [128, D], BF16)
    yT = self.sb("myT", [128, 8, 128], BF16)
    t1 = [self.sb(f"mt1{i}", [128, 512], F32) for i in range(3)]
    G5 = self.sb("mG5", [128, D], F32)
    pg = [self.ps(f"mpg{i}", [128, 512]) for i in range(2)]
    pT1s = [self.ps(f"mpT1{j}", [128, 512], BF16) for j in range(2)]
    pT2 = self.ps("mpT2", [128, D], BF16)
    py = [self.ps(f"mpy{i}", [128, 512]) for i in range(3)]
    pY = py[0]
    cnt = {}

    def nxt(key, n):
        v = cnt.get(key, 0)
        cnt[key] = v + 1
        return v % n
    H2Tv = self.H2T[l].rearrange("(kc p) t -> p kc t", p=128)
    work = []
    for (tag, src, dst, ntok, row, uoff) in streams:
        for t0 in range(0, ntok, 128):
            work.append((tag, src, dst, row, uoff + t0, t0))

    def stage1(w, i):
        (tag, src, dst, row, u, t0) = work[w]
        H, AT, OGt, OMt, XR, gt, ssq = hT[i], aTt[i], og[i], om[i], xr[i], gts[i], ssqs[i]
        S.dma(S.sp, [(H[:], H2Tv[:, :, u:u + 128])], reads=[self.R(self.H2T[l].name)], writes=[H.res])
        S.dma(S.sp, [(AT[:], self.ATT[l][:, :, u:u + 128])], reads=[self.R(self.ATT[l].name)], writes=[AT.res])
        S.dma(S.sp, [(OGt[:, 0, :], self.OG[l][0, u:u + 128, :]), (OGt[:, 1, :], self.OG[l][1, u:u + 128, :])], reads=[self.R(self.OG[l].name)], writes=[OGt.res])
        S.dma(S.sp, [(OMt[:, 0, :], self.OM[l][0, u:u + 128, :]), (OMt[:, 1, :], self.OM[l][1, u:u + 128, :])], reads=[self.R(self.OM[l].name)], writes=[OMt.res])
        S.dma(S.sp, [(XR[:], src[t0:t0 + 128, :])], reads=[self.R(src.name)], writes=[XR.res])
        for blk in range(8):
            p = pg[nxt("pg", 2)]
            for kc in range(8):
                S.op(S.pe, lambda h, kc=kc, blk=blk, p=p, H=H: h.matmul(p[:], H[:, kc, :], Wm[:, kc, blk * 512:(blk + 1) * 512], start=(kc == 0), stop=(kc == 7)),
                     reads=[H.res] + Wmr, writes=[p.res], inc=(kc == 7))
            fn = AF.Silu if blk == 0 else AF.Sigmoid
            S.op(S.act, lambda h, blk=blk, p=p, fn=fn, gt=gt: h.activation(out=gt[:, blk, :], in_=p[:], func=fn), reads=[p.res], writes=[gt.res])

    def stage1c(w, i):
        OGt, OMt, gt, ssq = og[i], om[i], gts[i], ssqs[i]
        for br, Ot in enumerate((OGt, OMt)):
            S.op(S.pool, lambda h, Ot=Ot: h.tensor_tensor(out=Ot[:, 0, :], in0=Ot[:, 0, :], in1=Ot[:, 1, :], op=ALU.add), reads=[Ot.res], writes=[Ot.res])
            S.op(S.act, lambda h, Ot=Ot: h.activation(out=sq[:], in_=Ot[:, 0, :], func=AF.Square), reads=[Ot.res], writes=[sq.res])
            S.op(S.dve, lambda h, br=br, ssq=ssq: h.tensor_reduce(out=ssq[:, br, :], in_=sq[:].rearrange("p (a b) -> p a b", b=128), axis=AX.X, op=ALU.add),
                 reads=[sq.res], writes=[ssq.res])
        S.op(S.act, lambda h, ssq=ssq: h.activation(out=ssq[:], in_=ssq[:], func=AF.Sqrt, scale=1.0 / 128, bias=eps_col[:]),
             reads=[ssq.res, eps_col.res], writes=[ssq.res])
        S.op(S.dve, lambda h, ssq=ssq: h.reciprocal(out=ssq[:], in_=ssq[:]), reads=[ssq.res], writes=[ssq.res])
        for br, Ot in enumerate((OGt, OMt)):
            B_ = bn[br]
            S.op(S.dve, lambda h, br=br, Ot=Ot, B_=B_, ssq=ssq: h.tensor_tensor(out=B_[:].rearrange("p (a b) -> p a b", b=128), in0=Ot[:, 0, :].rearrange("p (a b) -> p a b", b=128),
                                                                              in1=ssq[:, br, :].unsqueeze(2).to_broadcast([128, 4, 128]), op=ALU.mult),
                 reads=[Ot.res, ssq.res], writes=[B_.res])
            S.op(S.pool, lambda h, br=br, B_=B_: h.tensor_tensor(out=B_[:].rearrange("p (a b) -> p a b", b=128), in0=B_[:].rearrange("p (a b) -> p a b", b=128),
                                                              in1=gains[:, br:br + 1, :].to_broadcast([128, 4, 128]), op=ALU.mult),
                 reads=[B_.res, gains.res], writes=[B_.res])
            BB = bbs[i][br]
            S.op(S.dve, lambda h, br=br, B_=B_, BB=BB, gt=gt: h.tensor_tensor(out=BB[:], in0=B_[:], in1=gt[:, br, :], op=ALU.mult), reads=[B_.res, gt.res], writes=[BB.res])

    def stage1b(w, i):
        for br in range(2):
            BB = bbs[i][br]
            pT1 = pT1s[br]
            for c in range(4):
                S.op(S.pe, lambda h, c=c, BB=BB, pT1=pT1: h.transpose(out=pT1[:, c * 128:(c + 1) * 128], in_=BB[:, c * 128:(c + 1) * 128], identity=self.ident[:]),
                     reads=[BB.res, self.ident.res], writes=[pT1.res], inc=(c == 3))
            BT = bTs[i][br]
            if br == 0:
                S.op(S.act, lambda h, BT=BT, pT1=pT1: h.activation(out=BT[:].rearrange("p a b -> p (a b)"), in_=pT1[:], func=AF.Copy), reads=[pT1.res], writes=[BT.res])
            else:
                S.op(S.dve, lambda h, BT=BT, pT1=pT1: h.tensor_copy(out=BT[:].rearrange("p a b -> p (a b)"), in_=pT1[:]), reads=[pT1.res], writes=[BT.res])

    cur_row = [None]

    def stage2(w, i):
        (tag, src, dst, row, u, t0) = work[w]
        AT, XR, gt = aTt[i], xr[i], gts[i]
        if cur_row[0] != row:
            cur_row[0] = row
            srcg = self.MOD[l][row:row + 1, 5 * D:6 * D]
            S.dma(S.sp, [(G5[:], dram_ap(srcg, srcg.offset, [[0, 128], [1, D]]))], reads=[self.R("MOD", l)], writes=[G5.res])
        for half in range(2):
            cs_ = slice(half * 512, (half + 1) * 512)
            for hh in range(8):
                S.op(S.pe, lambda h, hh=hh, AT=AT, cs_=cs_: h.matmul(py[0][:], AT[:, hh, :], Woa[:, hh, cs_], start=(hh == 0), stop=(hh == 7)),
                     reads=[AT.res, Woa.res], writes=[py[0].res], inc=(hh == 7))
            for c in range(4):
                S.op(S.pe, lambda h, c=c, cs_=cs_, BT=bTs[i][0]: h.matmul(py[1][:], BT[:, c, :], Wog[:, c, cs_], start=(c == 0), stop=(c == 3)),
                     reads=[bTs[i][0].res, Wog.res], writes=[py[1].res], inc=(c == 3))
            for c in range(4):
                S.op(S.pe, lambda h, c=c, cs_=cs_, BT=bTs[i][1]: h.matmul(py[2][:], BT[:, c, :], Wom[:, c, cs_], start=(c == 0), stop=(c == 3)),
                     reads=[bTs[i][1].res, Wom.res], writes=[py[2].res], inc=(c == 3))
            S.op(S.dve, lambda h, half=half, gt=gt: h.tensor_tensor(out=t1[0][:], in0=py[0][:], in1=gt[:, 2 + half, :], op=ALU.mult), reads=[py[0].res, gt.res], writes=[t1[0].res])
            S.op(S.dve, lambda h, half=half, gt=gt: h.tensor_tensor(out=t1[1][:], in0=py[1][:], in1=gt[:, 4 + half, :], op=ALU.mult), reads=[py[1].res, gt.res], writes=[t1[1].res])
            S.op(S.dve, lambda h, half=half, gt=gt: h.tensor_tensor(out=t1[2][:], in0=py[2][:], in1=gt[:, 6 + half, :], op=ALU.mult), reads=[py[2].res, gt.res], writes=[t1[2].res])
            S.op(S.pool, lambda h: h.tensor_tensor(out=t1[0][:], in0=t1[0][:], in1=t1[1][:], op=ALU.add), reads=[t1[0].res, t1[1].res], writes=[t1[0].res])
            S.op(S.pool, lambda h, cs_=cs_: h.tensor_tensor(out=yb[:, cs_], in0=t1[0][:], in1=t1[2][:], op=ALU.add), reads=[t1[0].res, t1[2].res], writes=[yb.res])
        for kc in range(8):
            S.op(S.pe, lambda h, kc=kc: h.transpose(out=pT2[:, kc * 128:(kc + 1) * 128], in_=yb[:, kc * 128:(kc + 1) * 128], identity=self.ident[:]),
                 reads=[yb.res, self.ident.res], writes=[pT2.res], inc=(kc == 7))
        S.op(S.act, lambda h: h.activation(out=yT[:].rearrange("p a b -> p (a b)"), in_=pT2[:], func=AF.Copy), reads=[pT2.res], writes=[yT.res])
        for half in range(2):
            cs_ = slice(half * 512, (half + 1) * 512)
            for kc in range(8):
                S.op(S.pe, lambda h, kc=kc, cs_=cs_: h.matmul(pY[:], yT[:, kc, :], Wo[:, kc, cs_], start=(kc == 0), stop=(kc == 7)),
                     reads=[yT.res, Wo.res], writes=[pY.res], inc=(kc == 7))
            S.op(S.dve, lambda h, cs_=cs_: h.tensor_tensor(out=t1[0][:], in0=pY[:], in1=G5[:, cs_], op=ALU.mult), reads=[pY.res, G5.res], writes=[t1[0].res])
            S.op(S.pool, lambda h, cs_=cs_, XR=XR: h.tensor_tensor(out=XR[:, cs_], in0=XR[:, cs_], in1=t1[0][:], op=ALU.add), reads=[XR.res, t1[0].res], writes=[XR.res])
        S.dma(S.sp, [(dst[t0:t0 + 128, :], XR[:])], reads=[XR.res], writes=[self.R(dst.name)])

    stage1(0, 0)
    stage1c(0, 0)
    stage1b(0, 0)
    for w in range(len(work)):
        if w + 1 < len(work):
            stage1(w + 1, (w + 1) % 2)
        stage2(w, w % 2)
        if w + 1 < len(work):
            stage1c(w + 1, (w + 1) % 2)
            stage1b(w + 1, (w + 1) % 2)


Builder.phase_merge = phase_merge

_NC_CACHE = {}


def kernel(**inputs):
    inp = {k: np.asarray(v) for k, v in inputs.items()}
    Bsz, SEQ, _ = inp["x"].shape
    T = SEQ
    if T not in _NC_CACHE:
        _NC_CACHE[T] = Builder(T).build()
    nc = _NC_CACHE[T]
    in_maps = [make_in_map(inp, b, 0, T) for b in range(Bsz)]
    res = run_bass_kernel_spmd(nc, in_maps, core_ids=list(range(Bsz)))
    out = np.stack([np.asarray(r["y"], dtype=np.float32) for r in res.results], axis=0)
    return out


W_NAMES = ["mod_w", "mod_b", "norm_g", "ffn1_w13", "ffn1_w2", "ffn2_w13", "ffn2_w2", "w_in", "attn_q_norm", "attn_k_norm", "attn_sink", "gla_w2", "gla_b", "mlstm_conv_w", "mlstm_conv_b", "mlstm_ib", "mlstm_fb", "gla_norm", "mlstm_norm", "w_out_attn", "w_out_gla", "w_out_mlstm", "w_o"]


def make_in_map(inp, b, t0, T):
    m = {"x": np.ascontiguousarray(inp["x"][b, t0:t0 + T]), "c": np.ascontiguousarray(inp["c"][b]),
         "ctx": np.ascontiguousarray(inp["ctx"][b]), "c_ctx": np.ascontiguousarray(inp["c_ctx"])}
    for k in W_NAMES:
        m[k] = np.ascontiguousarray(inp[k])
    m["rope_cs"] = rope_table(t0, T)
    return m


def rope_table(t0, T):
    pos = np.arange(t0, t0 + T)
    r = (pos // 64).astype(np.float32)
    col = (pos % 64).astype(np.float32)
    inv = (np.float32(10000.0) ** (-np.arange(16, dtype=np.float32) / np.float32(16))).astype(np.float32)
    ang = np.concatenate([r[:, None] * inv, col[:, None] * inv], axis=-1).astype(np.float32)
    return np.ascontiguousarray(np.stack([np.cos(ang), np.sin(ang)], axis=1).astype(np.float32))
```

```python
import numpy as np
from contextlib import ExitStack
import concourse.bass as bass
import concourse.mybir as mybir
from concourse.bass_utils import run_bass_kernel_spmd

F32 = mybir.dt.float32
BF16 = mybir.dt.bfloat16
AF = mybir.ActivationFunctionType
ALU = mybir.AluOpType
AX = mybir.AxisListType

D = 1024
DFF = 2816
NMOD = 9
LC = 256
EPS = 1e-6
DEPTH = 2
D_IN = 7472


class Res:
    __slots__ = ("name", "w", "r")

    def __init__(self, name=""):
        self.name = name
        self.w = None
        self.r = []


class Eng:
    def __init__(self, name, is_pe=False):
        self.name = name
        self.is_pe = is_pe
        self.ops = []
        self.sems = []
        self.si = 0
        self.cnt = 0
        self.seen = {}
        self.pend_r = []
        self.pend_w = []
        self.pool = []
        self.pi = 0


ROT = 30000


class Sched:
    def __init__(self, nc, es):
        self.nc = nc
        self.es = es
        self.pe = Eng("pe", True)
        self.act = Eng("act")
        self.dve = Eng("dve")
        self.pool = Eng("pool")
        self.sp = Eng("sp")
        self.engs = [self.pe, self.act, self.dve, self.pool, self.sp]
        self.semid = {}
        n_rot = {"pe": 6, "act": 3, "dve": 3, "pool": 3, "sp": 1}
        for e in self.engs:
            for i in range(n_rot[e.name]):
                s = es.enter_context(nc.semaphore(f"s_{e.name}{i}"))
                e.sems.append(s)
        for e, n in ((self.sp, 20), (self.pool, 10), (self.act, 4)):
            for i in range(n):
                s = es.enter_context(nc.semaphore(f"d_{e.name}{i}"))
                e.pool.append([s, 0])
        self.n_ops = 0

    def _need(self, eng, tok, raw):
        if tok is None:
            return None
        sem, val, owner = tok
        if owner == eng.name:
            if eng.is_pe:
                return None
            if not raw:
                return None
        key = id(sem)
        if eng.seen.get(key, 0) >= val:
            return None
        eng.seen[key] = val
        return (sem, val)

    def _waits(self, eng, reads, writes):
        ws = []
        for r in reads:
            w = self._need(eng, r.w, True)
            if w:
                ws.append(w)
        for wr in writes:
            w = self._need(eng, wr.w, False)
            if w:
                ws.append(w)
            for t in wr.r:
                w = self._need(eng, t, False)
                if w:
                    ws.append(w)
        for (sem, val) in ws:
            eng.ops.append(lambda h, sem=sem, val=val: h.wait_ge(sem, val))

    def _record(self, tok, reads, writes):
        for r in reads:
            r.r = [t for t in r.r if t[2] != tok[2] or t[0] is not tok[0]] + [tok]
        for w in writes:
            w.w = tok
            w.r = []

    def op(self, eng, fn, reads=(), writes=(), inc=True):
        self.n_ops += 1
        reads = list(reads)
        writes = list(writes)
        self._waits(eng, reads, writes)
        if not inc:
            eng.ops.append(lambda h, fn=fn: fn(h))
            eng.pend_r += reads
            eng.pend_w += writes
            return
        if eng.cnt >= ROT:
            eng.si += 1
            eng.cnt = 0
        eng.cnt += 1
        sem = eng.sems[eng.si]
        tok = (sem, eng.cnt, eng.name)
        eng.ops.append(lambda h, fn=fn, sem=sem: fn(h).then_inc(sem, 1))
        self._record(tok, reads + eng.pend_r, writes + eng.pend_w)
        eng.pend_r = []
        eng.pend_w = []

    def dma(self, eng, pairs, reads=(), writes=(), **kw):
        self.n_ops += 1
        reads = list(reads)
        writes = list(writes)
        self._waits(eng, reads, writes)
        ent = eng.pool[eng.pi]
        eng.pi = (eng.pi + 1) % len(eng.pool)
        sem = ent[0]
        if ent[1] > 0 and eng.seen.get(id(sem), 0) < ent[1]:
            v = ent[1]
            eng.ops.append(lambda h, sem=sem, v=v: h.wait_ge(sem, v))
            eng.seen[id(sem)] = v
        for (o, i) in pairs:
            ent[1] += 16
            eng.ops.append(lambda h, o=o, i=i, sem=sem: h.dma_start(out=o, in_=i, **kw).then_inc(sem, 16))
        tok = (sem, ent[1], "dma_" + eng.name + str(id(sem)))
        self._record(tok, reads, writes)

    def barrier(self):
        toks = []
        for e in self.engs:
            assert not e.pend_r and not e.pend_w, e.name
            for i in range(e.si + 1):
                v = ROT if i < e.si else e.cnt
                if v > 0:
                    toks.append((e, e.sems[i], v))
            for ent in e.pool:
                if ent[1] > 0:
                    toks.append((None, ent[0], ent[1]))
        for e in self.engs:
            for (own, sem, v) in toks:
                if own is e:
                    continue
                if e.seen.get(id(sem), 0) >= v:
                    continue
                e.seen[id(sem)] = v
                e.ops.append(lambda h, sem=sem, v=v: h.wait_ge(sem, v))

    def finish(self):
        for e in (self.sp, self.pool, self.act):
            for ent in e.pool:
                if ent[1] > 0:
                    self.sp.ops.append(lambda h, sem=ent[0], v=ent[1]: h.wait_ge(sem, v))

    def replay(self):
        nc = self.nc
        with nc.Block() as block:
            @block.tensor
            def _(h):
                for f in self.pe.ops:
                    f(h)

            @block.scalar
            def _(h):
                for f in self.act.ops:
                    f(h)

            @block.vector
            def _(h):
                for f in self.dve.ops:
                    f(h)

            @block.gpsimd
            def _(h):
                for f in self.pool.ops:
                    f(h)

            @block.sync
            def _(h):
                for f in self.sp.ops:
                    f(h)


class Tile:
    def __init__(self, t, name):
        self.t = t
        self.res = Res(name)

    def __getitem__(self, k):
        return self.t[k]


def dram_ap(t, offset, pattern):
    return bass.AP(t.tensor, offset, pattern)


class Builder:
    def __init__(self, T, depth=DEPTH, stop=None, dbg=()):
        self.T = T
        self.depth = depth
        self.stop = stop
        self.dbg = dbg
        self.nc = bass.Bass("TRN2", target_bir_lowering=False)
        self.es = ExitStack()
        self.S = None
        self.dres = {}
        self.rr = {}

    def din(self, name, shape):
        return self.nc.dram_tensor(name, list(shape), F32, kind="ExternalInput").ap()

    def dout(self, name, shape, dt=F32):
        return self.nc.dram_tensor(name, list(shape), dt, kind="ExternalOutput").ap()

    def dscr(self, name, shape, dt=F32):
        if name in self.dbg:
            return self.nc.dram_tensor(name, list(shape), dt, kind="ExternalOutput").ap()
        return self.nc.dram_tensor(name, list(shape), dt).ap()

    def R(self, *key):
        if key not in self.dres:
            self.dres[key] = Res(str(key))
        return self.dres[key]

    def sb(self, name, shape, dt):
        self.uid = getattr(self, "uid", 0) + 1
        name = f"{name}_{self.uid}"
        t = self.cur.enter_context(self.nc.sbuf_tensor(name, list(shape), dt))
        return Tile(t, name)

    def ps(self, name, shape, dt=F32):
        self.uid = getattr(self, "uid", 0) + 1
        name = f"{name}_{self.uid}"
        t = self.cur.enter_context(self.nc.psum_tensor(name, list(shape), dt))
        return Tile(t, name)

    def build(self):
        nc = self.nc
        T = self.T
        L = self.depth
        with self.es as es:
            self.S = S = Sched(nc, es)
            self.x_in = self.din("x", [T, D])
            self.c_in = self.din("c", [D])
            self.ctx_in = self.din("ctx", [LC, D])
            self.cctx_in = self.din("c_ctx", [D])
            self.mod_w = self.din("mod_w", [L, D, NMOD * D])
            self.mod_b = self.din("mod_b", [L, NMOD * D])
            self.norm_g = self.din("norm_g", [L, 3, D])
            self.ffn_w13 = [self.din("ffn1_w13", [L, D, 2 * DFF]), self.din("ffn2_w13", [L, D, 2 * DFF])]
            self.ffn_w2 = [self.din("ffn1_w2", [L, DFF, D]), self.din("ffn2_w2", [L, DFF, D])]
            self.w_in = self.din("w_in", [L, D, D_IN])
            self.attn_q_norm = self.din("attn_q_norm", [L, 64])
            self.attn_k_norm = self.din("attn_k_norm", [L, 64])
            self.attn_sink = self.din("attn_sink", [L, 8])
            self.gla_w2 = self.din("gla_w2", [L, 2, 16, 256])
            self.gla_b = self.din("gla_b", [L, 2, 256])
            self.rope_cs = self.din("rope_cs", [T, 2, 32])
            self.y_out = self.dout("y", [T, D])
            TT = self.TT = LC + T
            self.H2T = [self.dscr(f"H2T{l}", [D, TT], BF16) for l in range(L)]
            self.QT = [self.dscr(f"QT{l}", [64, 8, TT], BF16) for l in range(L)]
            self.KT = [self.dscr(f"KT{l}", [64, 2, TT], BF16) for l in range(L)]
            self.VA = [self.dscr(f"VA{l}", [TT, 128], BF16) for l in range(L)]
            self.GV = [self.dscr(f"GV{l}", [TT, 512], BF16) for l in range(L)]
            self.MV = [self.dscr(f"MV{l}", [TT, 512], BF16) for l in range(L)]
            self.GATES = [self.dscr(f"GATES{l}", [16, TT]) for l in range(L)]
            self.QG = [self.dscr(f"QG{l}", [2, 64, 4, TT], BF16) for l in range(L)]
            self.KG = [self.dscr(f"KG{l}", [2, 64, 4, TT], BF16) for l in range(L)]
            self.KH = [self.dscr(f"KH{l}", [2, 64, 4, TT], BF16) for l in range(L)]
            self.MQK = [self.dscr(f"MQK{l}", [D, TT + 4], BF16) for l in range(L)]
            self.ATT = [self.dscr(f"ATT{l}", [64, 8, TT], BF16) for l in range(L)]
            self.OG = [self.dscr(f"OG{l}", [2, TT, 512]) for l in range(L)]
            self.OM = [self.dscr(f"OM{l}", [2, TT, 512]) for l in range(L)]
            self.MROWS = [self.dscr(f"MROWS{l}", [2, 5, 4, TT]) for l in range(L)]
            self.MQC = [self.dscr(f"MQC{l}", [D, TT], BF16) for l in range(L)]
            self.gla_norm = self.din("gla_norm", [L, 128])
            self.mlstm_norm = self.din("mlstm_norm", [L, 128])
            self.w_out_attn = self.din("w_out_attn", [L, 512, D])
            self.w_out_gla = self.din("w_out_gla", [L, 512, D])
            self.w_out_mlstm = self.din("w_out_mlstm", [L, 512, D])
            self.w_o = self.din("w_o", [L, D, D])
            self.X2 = [self.dscr(f"X2_{l}", [T, D]) for l in range(L)]
            self.C2 = [self.dscr(f"C2_{l}", [LC, D]) for l in range(L)]
            self.X3 = [self.dscr(f"X3_{l}", [T, D]) for l in range(L)]
            self.C3 = [self.dscr(f"C3_{l}", [LC, D]) for l in range(L)]
            self.conv_w = self.din("mlstm_conv_w", [L, 5, D])
            self.conv_b = self.din("mlstm_conv_b", [L, D])
            self.mlstm_ib = self.din("mlstm_ib", [L, 2, 4])
            self.mlstm_fb = self.din("mlstm_fb", [L, 2, 4])
            self.MOD = [self.dscr(f"MOD{l}", [2, NMOD * D]) for l in range(L)]
            self.X1 = [self.dscr(f"X1_{l}", [T, D]) for l in range(L)]
            self.C1 = [self.dscr(f"C1_{l}", [LC, D]) for l in range(L)]
            with ExitStack() as cst:
                self.cur = cst
                self.ident = self.sb("ident", [128, 128], BF16)
                self.identf = self.sb("identf", [128, 128], F32)
                self.eps_col = self.sb("eps_col", [128, 1], F32)
                S.op(S.dve, lambda h: h.memset(self.eps_col[:], EPS), writes=[self.eps_col.res])
                self.make_consts()
                for l in range(L):
                    xin = self.x_in if l == 0 else self.X3[l - 1]
                    cin = self.ctx_in if l == 0 else self.C3[l - 1]
                    with ExitStack() as ph:
                        self.cur = ph
                        self.phase_mod(l)
                        S.barrier()
                    if self.stop == ("mod", l):
                        break
                    with ExitStack() as ph:
                        self.cur = ph
                        self.phase_ffn(l, 0, [("ctx", cin, self.C1[l], LC, 1), ("lat", xin, self.X1[l], T, 0)])
                        S.barrier()
                    if self.stop == ("ffn1", l):
                        break
                    with ExitStack() as lay:
                        self.cur = lay
                        self.EL = self.sb("EL", [64, 4, 2, TT // 64], F32)
                        with ExitStack() as ph:
                            self.cur = ph
                            self.phase_feat(l, [("ctx", self.C1[l], LC, 1, 0, False), ("lat", self.X1[l], T, 0, LC, True)])
                            S.barrier()
                        if self.stop == ("feat", l):
                            break
                        with ExitStack() as ph:
                            self.cur = ph
                            self.phase_attn(l, l < L - 1)
                            S.barrier()
                        if self.stop == ("attn", l):
                            break
                        with ExitStack() as ph:
                            self.cur = ph
                            self.phase_gla(l)
                            S.barrier()
                        if self.stop == ("gla", l):
                            break
                        self.cur = lay
                        self.DEC = self.sb("DEC", [128, 2, 4, TT // 64], F32)
                        self.sel = self.sb("sel", [4, 4, 128], F32)
                        S.op(S.dve, lambda h, sel_t=self.sel: h.tensor_copy(out=sel_t[:], in_=self.identf[0:4, 0:4].unsqueeze(2).to_broadcast([4, 4, 128])),
                             reads=[self.identf.res], writes=[self.sel.res])
                        stop_ml = False
                        for ph_name, ph_fn in (("mlg", self.phase_ml_gates), ("mlc", self.phase_ml_conv), ("mls", self.phase_ml_scan)):
                            with ExitStack() as ph:
                                self.cur = ph
                                ph_fn(l)
                                S.barrier()
                            if self.stop == (ph_name, l):
                                stop_ml = True
                                break
                        if stop_ml:
                            break
                        if self.stop == ("ml", l):
                            break
                    last = (l == L - 1)
                    with ExitStack() as ph:
                        self.cur = ph
                        st = [("lat", self.X1[l], self.X2[l], T, 0, LC)]
                        if not last:
                            st = [("ctx", self.C1[l], self.C2[l], LC, 1, 0)] + st
                        self.phase_merge(l, st)
                        S.barrier()
                    if self.stop == ("merge", l):
                        break
                    with ExitStack() as ph:
                        self.cur = ph
                        xdst = self.y_out if last else self.X3[l]
                        st = [("lat", self.X2[l], xdst, T, 0)]
                        import os
                        if not last and not os.environ.get("NOCTX2"):
                            st = [("ctx", self.C2[l], self.C3[l], LC, 1)] + st
                        self.phase_ffn(l, 1, [(a, b, c, d_, e) for (a, b, c, d_, e) in st])
                        S.barrier()
                    if self.stop == ("ffn2", l):
                        break
                S.finish()
                S.replay()
        return nc

    def make_consts(self):
        S = self.S
        nc = self.nc
        idf = self.identf
        S.op(S.pool, lambda h: h.memset(idf[:], 0.0), writes=[idf.res])
        S.op(S.pool, lambda h: h.affine_select(out=idf[:], in_=idf[:], pattern=[[-1, 128]],
                                                compare_op=ALU.not_equal, fill=1.0, base=0,
                                                channel_multiplier=1),
             reads=[idf.res], writes=[idf.res])
        S.op(S.dve, lambda h: h.tensor_copy(out=self.ident[:], in_=idf[:]), reads=[idf.res], writes=[self.ident.res])

    def phase_mod(self, l):
        S = self.S
        cl = self.sb("cl", [128, 8, 2], F32)
        cs = self.sb("cs", [128, 8, 2], F32)
        S.dma(S.sp, [(cl[:, :, 0], self.c_in.rearrange("(kc p) -> p kc", p=128)),
                     (cl[:, :, 1], self.cctx_in.rearrange("(kc p) -> p kc", p=128))],
              writes=[cl.res], allow_slow_non_contiguous=True)
        S.op(S.act, lambda h: h.activation(out=cs[:], in_=cl[:], func=AF.Silu), reads=[cl.res], writes=[cs.res])
        wm = [self.sb(f"wm{i}", [128, 8, 512], F32) for i in range(2)]
        mb = [self.sb(f"mb{i}", [2, 512], F32) for i in range(2)]
        mo = [self.sb(f"mo{i}", [2, 512], F32) for i in range(2)]
        pm = [self.ps(f"pm{i}", [2, 512]) for i in range(2)]
        mw = self.mod_w[l].rearrange("(kc p) n -> p kc n", p=128)
        for n in range(18):
            i = n % 2
            S.dma(S.sp, [(wm[i][:, 0:4, :], mw[:, 0:4, n * 512:(n + 1) * 512]),
                         (wm[i][:, 4:8, :], mw[:, 4:8, n * 512:(n + 1) * 512])], writes=[wm[i].res])
            mbsrc = self.mod_b[l:l + 1, n * 512:(n + 1) * 512]
            S.dma(S.sp, [(mb[i][0:1, :], mbsrc), (mb[i][1:2, :], mbsrc)], writes=[mb[i].res])
            for kc in range(8):
                S.op(S.pe, lambda h, kc=kc, i=i: h.matmul(pm[i][:], cs[:, kc, :], wm[i][:, kc, :],
                                                           start=(kc == 0), stop=(kc == 7)),
                     reads=[cs.res, wm[i].res], writes=[pm[i].res], inc=(kc == 7))
            S.op(S.dve, lambda h, i=i: h.tensor_tensor(out=mo[i][:], in0=pm[i][:], in1=mb[i][:], op=ALU.add),
                 reads=[pm[i].res, mb[i].res], writes=[mo[i].res])
            S.dma(S.sp, [(self.MOD[l][:, n * 512:(n + 1) * 512], mo[i][:])], reads=[mo[i].res],
                  writes=[self.R("MOD", l)])

    def load_cols(self, dst_ap, src_row_ap, res):
        self.S.dma(self.S.sp, [(dst_ap, src_row_ap.rearrange("(kc p) -> p kc", p=128))], writes=[res],
                   allow_slow_non_contiguous=True)

    def adaln_cols(self, l, j, row, tag):
        S = self.S
        tmp = self.sb(f"adt_{tag}", [128, 3, 8], F32)
        A = self.sb(f"adA_{tag}", [128, 8], F32)
        MODr = self.MOD[l]
        S.dma(S.sp, [(tmp[:, 0, :], MODr[row, (3 * j) * D:(3 * j + 1) * D].rearrange("(kc p) -> p kc", p=128)),
                     (tmp[:, 1, :], MODr[row, (3 * j + 1) * D:(3 * j + 2) * D].rearrange("(kc p) -> p kc", p=128)),
                     (tmp[:, 2, :], self.norm_g[l, j, :].rearrange("(kc p) -> p kc", p=128))],
              reads=[self.R("MOD", l)], writes=[tmp.res], allow_slow_non_contiguous=True)
        S.op(S.dve, lambda h: h.scalar_tensor_tensor(out=A[:], in0=tmp[:, 1, :], scalar=1.0, in1=tmp[:, 2, :],
                                                      op0=ALU.add, op1=ALU.mult),
             reads=[tmp.res], writes=[A.res])
        return A, tmp

    def gate_bc(self, l, j, row, tag, mul):
        S = self.S
        G = self.sb(f"gate_{tag}", [128, D], F32)
        src = self.MOD[l][row:row + 1, (3 * j + 2) * D:(3 * j + 3) * D]
        src_b = dram_ap(src, src.offset, [[0, 128], [1, D]])
        S.dma(S.sp, [(G[:], src_b)], reads=[self.R("MOD", l)], writes=[G.res])
        if mul != 1.0:
            S.op(S.pool, lambda h: h.tensor_scalar(out=G[:], in0=G[:], scalar1=float(mul), scalar2=None, op0=ALU.mult),
                 reads=[G.res], writes=[G.res])
        return G

    def load_weight_bf16(self, dst, src3, nsplit):
        S = self.S
        kcn = dst.t.shape[1]
        step = max(1, kcn // nsplit)
        dst.parts = []
        for k0 in range(0, kcn, step):
            k1 = min(kcn, k0 + step)
            r = Res(f"wpart{k0}")
            dst.parts.append(r)
            S.dma(S.pool, [(dst[:, k0:k1, :], src3[:, k0:k1, :])], writes=[r])

    def norm_part(self, xt, nb, ss, rs, junk=None):
        S = self.S
        if junk is None:
            junk = self.junk
        S.op(S.act, lambda h: h.activation(out=junk[:], in_=xt[:], func=AF.Square, accum_out=ss[:]),
             reads=[xt.res], writes=[junk.res, ss.res])
        S.op(S.act, lambda h: h.activation(out=rs[:], in_=ss[:], func=AF.Sqrt, scale=1.0 / D, bias=self.eps_col[:]),
             reads=[ss.res], writes=[rs.res])
        S.op(S.dve, lambda h: h.reciprocal(out=rs[:], in_=rs[:]), reads=[rs.res], writes=[rs.res])
        S.op(S.dve, lambda h: h.tensor_scalar(out=nb[:], in0=xt[:], scalar1=rs[:], scalar2=None, op0=ALU.mult),
             reads=[xt.res, rs.res], writes=[nb.res])

    def transpose_part(self, nb, pT, hT, col0, A, sh, evac_engs):
        S = self.S
        for kc in range(8):
            S.op(S.pe, lambda h, kc=kc: h.transpose(out=pT[:, kc * 128:(kc + 1) * 128], in_=nb[:, kc * 128:(kc + 1) * 128],
                                                     identity=self.ident[:]),
                 reads=[nb.res, self.ident.res], writes=[pT.res], inc=(kc == 7))
        for kc in range(8):
            e = evac_engs[kc % len(evac_engs)]
            if e is S.act:
                S.op(e, lambda h, kc=kc: h.activation(out=hT[:, kc, col0:col0 + 128], in_=pT[:, kc * 128:(kc + 1) * 128],
                                                      func=AF.Identity, scale=A[:, kc:kc + 1], bias=sh[:, kc:kc + 1]),
                     reads=[pT.res, A.res, self.shres], writes=[hT.res])
            else:
                S.op(e, lambda h, kc=kc: h.tensor_scalar(out=hT[:, kc, col0:col0 + 128], in0=pT[:, kc * 128:(kc + 1) * 128],
                                                         scalar1=A[:, kc:kc + 1], scalar2=sh[:, kc:kc + 1],
                                                         op0=ALU.mult, op1=ALU.add),
                     reads=[pT.res, A.res, self.shres], writes=[hT.res])

    def phase_ffn(self, l, which, streams):
        S = self.S
        j = 0 if which == 0 else 2
        W13 = self.sb("W13", [128, 8, 2 * DFF], BF16)
        W2 = self.sb("W2", [128, 22, D], BF16)
        self.load_weight_bf16(W13, self.ffn_w13[which][l].rearrange("(kc p) n -> p kc n", p=128), 8)
        self.load_weight_bf16(W2, self.ffn_w2[which][l].rearrange("(fc p) n -> p fc n", p=128), 11)
        import os
        if os.environ.get("FFN_WONLY") and which == 1:
            return
        xl = [self.sb(f"xl{i}", [128, D], F32) for i in range(3)]
        xr = [self.sb(f"xr{i}", [128, D], F32) for i in range(2)]
        nb = [self.sb(f"nb{i}", [128, D], BF16) for i in range(4)]
        ss = [self.sb(f"ss{i}", [128, 1], F32) for i in range(4)]
        rs = [self.sb(f"rs{i}", [128, 1], F32) for i in range(4)]
        hT = self.sb("hT", [128, 8, 512], BF16)
        gT = self.sb("gT", [128, 22, 512], BF16)
        sa = [self.sb(f"sa{i}", [128, 512], F32) for i in range(2)]
        tt = [self.sb(f"tt{i}", [128, 512], F32) for i in range(2)]
        pT = [self.ps(f"pT{i}", [128, D], BF16) for i in range(2)]
        pA = [self.ps(f"pA{i}", [128, 512]) for i in range(2)]
        pB = [self.ps(f"pB{i}", [128, 512]) for i in range(2)]
        pY = [self.ps(f"pY{i}", [128, 512]) for i in range(2)]
        cnt = {"xl": 0, "xr": 0, "nb": 0, "pT": 0, "pAB": 0, "sa": 0, "tt": 0}

        for (tag, src, dst, ntok, row) in streams:
            A, tmp = self.adaln_cols(l, j, row, f"{which}{tag}")
            sh = tmp[:, 0, :]
            self.shres = tmp.res
            G = self.gate_bc(l, j, row, f"{which}{tag}", 0.5)
            tiles = [(t0, min(512, ntok - t0)) for t0 in range(0, ntok, 512)]
            rtag = ("xs", l, which, tag)

            def prep_norm(t0, s):
                i = cnt["xl"] % 3
                cnt["xl"] += 1
                k = cnt["nb"] % 4
                cnt["nb"] += 1
                S.dma(S.sp, [(xl[i][:], src[t0 + s * 128:t0 + (s + 1) * 128, :])], reads=[self.R(src.name)],
                      writes=[xl[i].res])
                self.norm_part(xl[i], nb[k], ss[k], rs[k], junk=nb[k])
                return nb[k]

            def prep_tr(nbt, s):
                k = cnt["pT"] % 2
                cnt["pT"] += 1
                self.transpose_part(nbt, pT[k], hT, s * 128, A, sh, [S.act, S.dve])

            def prep(t0, n):
                for s in range(n // 128):
                    nbt = prep_norm(t0, s)
                    prep_tr(nbt, s)

            prep(*tiles[0])
            for ti, (t0, n) in enumerate(tiles):
                nt = n // 128
                for p in range(22):
                    k = cnt["pAB"] % 2
                    cnt["pAB"] += 1
                    for kc in range(8):
                        S.op(S.pe, lambda h, kc=kc, p=p, k=k, n=n: h.matmul(pA[k][:, :n], W13[:, kc, p * 128:(p + 1) * 128], hT[:, kc, :n],
                                                                        start=(kc == 0), stop=(kc == 7)),
                             reads=W13.parts + [hT.res], writes=[pA[k].res], inc=(kc == 7))
                    for kc in range(8):
                        S.op(S.pe, lambda h, kc=kc, p=p, k=k, n=n: h.matmul(pB[k][:, :n], W13[:, kc, DFF + p * 128:DFF + (p + 1) * 128], hT[:, kc, :n],
                                                                        start=(kc == 0), stop=(kc == 7)),
                             reads=W13.parts + [hT.res], writes=[pB[k].res], inc=(kc == 7))
                    q = cnt["sa"] % 2
                    cnt["sa"] += 1
                    S.op(S.act, lambda h, k=k, q=q, n=n: h.activation(out=sa[q][:, :n], in_=pA[k][:, :n], func=AF.Silu),
                         reads=[pA[k].res], writes=[sa[q].res])
                    S.op(S.dve, lambda h, k=k, q=q, p=p, n=n: h.tensor_tensor(out=gT[:, p, :n], in0=sa[q][:, :n], in1=pB[k][:, :n], op=ALU.mult),
                         reads=[sa[q].res, pB[k].res], writes=[gT.res])
                xrs = []
                for s in range(nt):
                    pass
                if ti + 1 < len(tiles):
                    pending = tiles[ti + 1]
                else:
                    pending = None
                nbts = []
                if pending is not None:
                    for s in range(pending[1] // 128):
                        nbts.append(prep_norm(pending[0], s))
                for s in range(nt):
                    i = cnt["xr"] % 2
                    cnt["xr"] += 1
                    S.dma(S.sp, [(xr[i][:], src[t0 + s * 128:t0 + (s + 1) * 128, :])], reads=[self.R(src.name)],
                          writes=[xr[i].res])
                    for dh in range(2):
                        for fc in range(22):
                            S.op(S.pe, lambda h, fc=fc, dh=dh, s=s: h.matmul(pY[dh][:], gT[:, fc, s * 128:(s + 1) * 128], W2[:, fc, dh * 512:(dh + 1) * 512],
                                                                              start=(fc == 0), stop=(fc == 21)),
                                 reads=[gT.res] + W2.parts, writes=[pY[dh].res], inc=(fc == 21))
                    for dh in range(2):
                        q = cnt["tt"] % 2
                        cnt["tt"] += 1
                        S.op(S.dve, lambda h, dh=dh, q=q, G=G: h.tensor_tensor(out=tt[q][:], in0=pY[dh][:], in1=G[:, dh * 512:(dh + 1) * 512], op=ALU.mult),
                             reads=[pY[dh].res, G.res], writes=[tt[q].res])
                        S.op(S.pool, lambda h, dh=dh, q=q, i=i: h.tensor_tensor(out=xr[i][:, dh * 512:(dh + 1) * 512], in0=xr[i][:, dh * 512:(dh + 1) * 512],
                                                                                 in1=tt[q][:], op=ALU.add),
                             reads=[tt[q].res, xr[i].res], writes=[xr[i].res])
                    S.dma(S.sp, [(dst[t0 + s * 128:t0 + (s + 1) * 128, :], xr[i][:])], reads=[xr[i].res],
                          writes=[self.R(dst.name)])
                for s, nbt in enumerate(nbts):
                    prep_tr(nbt, s)


O_AQ, O_AK, O_AV = 0, 512, 640
O_GQ, O_GK, O_GV, O_GR, O_GG = 768, 1024, 1280, 1792, 2304
O_MQ, O_MK, O_MV, O_MO, O_MI, O_MF = 2336, 2848, 3360, 3872, 4384, 4392
O_SA, O_SG, O_SM = 4400, 5424, 6448


def bcast_rows(ap2d, nparts):
    return bass.AP(ap2d.tensor, ap2d.offset, [[0, nparts]] + [list(x) for x in ap2d.ap[1:]])


def rev_last(ap):
    pat = [list(x) for x in ap.ap]
    st, n = pat[-1]
    return bass.AP(ap.tensor, ap.offset + st * (n - 1), pat[:-1] + [[-st, n]])


def phase_feat(self, l, streams):
    S = self.S
    TT = self.TT
    win = self.w_in[l].rearrange("(kc p) n -> p kc n", p=128)
    Wa = self.sb("Wa", [128, 8, 768], BF16)
    Wv = self.sb("Wv", [128, 8, 1024], BF16)
    Wf = self.sb("Wf", [128, 8, 1536], BF16)
    Wg = self.sb("Wg", [128, 8, 48], BF16)
    S.dma(S.pool, [(Wa[:, 0:4, :], win[:, 0:4, 0:768]), (Wa[:, 4:8, :], win[:, 4:8, 0:768])], writes=[Wa.res])
    for k0 in range(0, 8, 2):
        S.dma(S.pool, [(Wv[:, k0:k0 + 2, 0:512], win[:, k0:k0 + 2, O_GV:O_GV + 512]),
                       (Wv[:, k0:k0 + 2, 512:1024], win[:, k0:k0 + 2, O_MV:O_MV + 512])], writes=[Wv.res])
        S.dma(S.pool, [(Wf[:, k0:k0 + 2, 0:512], win[:, k0:k0 + 2, O_GQ:O_GQ + 512]),
                       (Wf[:, k0:k0 + 2, 512:1536], win[:, k0:k0 + 2, O_MQ:O_MQ + 1024])], writes=[Wf.res])
    S.dma(S.pool, [(Wg[:, :, 0:32], win[:, :, O_GG:O_GG + 32]), (Wg[:, :, 32:48], win[:, :, O_MI:O_MI + 16])], writes=[Wg.res])
    W2p = self.sb("W2p", [32, 2, 256], F32)
    S.op(S.dve, lambda h: h.memset(W2p[:], 0.0), writes=[W2p.res])
    S.dma(S.sp, [(W2p[0:16, 0, :], self.gla_w2[l, 0]), (W2p[16:32, 1, :], self.gla_w2[l, 1])], writes=[W2p.res])
    negb = self.sb("negb", [128, 2, 2], F32)
    S.dma(S.sp, [(negb[:, d, :], self.gla_b[l, d, :].rearrange("(c p) -> p c", p=128)) for d in range(2)],
          writes=[negb.res], allow_slow_non_contiguous=True)
    S.op(S.dve, lambda h: h.tensor_scalar(out=negb[:], in0=negb[:], scalar1=-1.0, scalar2=None, op0=ALU.mult),
         reads=[negb.res], writes=[negb.res])
    gain = self.sb("gain", [128, 10, 64], F32)
    qn_src = self.attn_q_norm[l:l + 1, :]
    kn_src = self.attn_k_norm[l:l + 1, :]
    S.dma(S.sp, [(gain[:, 0:8, :], bass.AP(qn_src.tensor, qn_src.offset, [[0, 128], [0, 8], [1, 64]])),
                 (gain[:, 8:10, :], bass.AP(kn_src.tensor, kn_src.offset, [[0, 128], [0, 2], [1, 64]]))],
          writes=[gain.res])
    mask01 = self.sb("mask01", [128, 8, 64], F32)
    S.op(S.pool, lambda h: h.memset(mask01[:], 1.0), writes=[mask01.res])
    S.op(S.pool, lambda h: h.memset(mask01[:, :, 0:1], 0.0), writes=[mask01.res])
    self.junk = self.sb("junk", [128, D], BF16)

    xl = [self.sb(f"xl{i}", [128, D], F32) for i in range(3)]
    nb = [self.sb(f"nb{i}", [128, D], BF16) for i in range(2)]
    ss = [self.sb(f"ss{i}", [128, 1], F32) for i in range(2)]
    rs = [self.sb(f"rs{i}", [128, 1], F32) for i in range(2)]
    hTs = [self.sb(f"hT{i}", [128, 8, 512], BF16) for i in range(2)]
    sqt = self.sb("sqt", [128, 640], F32)
    ssh = self.sb("ssh", [128, 10], F32)
    rinv = self.sb("rinv", [128, 10], F32)
    qn = self.sb("qn", [128, 10, 64], F32)
    rt = [self.sb(f"rt{i}", [128, 10, 32], F32) for i in range(4)]
    cs_t = [self.sb(f"cst{i}", [128, 2, 32], F32) for i in range(3)]
    qr = [self.sb(f"qr{i}", [128, 10, 64], BF16) for i in range(2)]
    vb = [self.sb(f"vb{i}", [128, 128], BF16) for i in range(4)]
    vb2 = [self.sb(f"vb2{i}", [128, 512], BF16) for i in range(4)]
    QTs = self.sb("QTs", [64, 8, 512], BF16)
    KTs = self.sb("KTs", [64, 2, 512], BF16)
    ggT = self.sb("ggT", [32, 512], F32)
    gts = self.sb("gts", [16, 512], F32)
    ex = [self.sb(f"ex{i}", [128, 512], F32) for i in range(2)]
    csum = [self.sb(f"csum{i}", [128, 512], F32) for i in range(2)]
    eb = [[self.sb(f"eb{d}{c}", [128, 512], F32) for c in range(2)] for d in range(2)]
    enb = [[self.sb(f"enb{d}{c}", [128, 512], F32) for c in range(2)] for d in range(2)]
    ebl = [[self.sb(f"ebl{d}{c}", [128, 512], F32) for c in range(2)] for d in range(2)]
    fo = [self.sb(f"fo{i}", [128, 512], BF16) for i in range(8)]
    elcs = [self.sb(f"elc{i}", [128, 8], F32) for i in range(2)]
    pT = self.ps("pT", [128, D], BF16)
    pq = self.ps("pq", [128, 512])
    pkv = self.ps("pkv", [128, 256])
    pqt = self.ps("pqt", [64, 8, 128], BF16)
    pkt = self.ps("pkt", [64, 2, 128], BF16)
    pf = [self.ps(f"pf{i}", [128, 512]) for i in range(2)]
    pz = self.ps("pz", [128, 512])
    cnt = {"xl": 0, "pf": 0, "fo": 0, "qr": 0, "vb": 0, "vb2": 0, "ex": 0, "rt": 0, "cs": 0, "elc": 0}
    H2Tv = self.H2T[l].rearrange("(kc p) t -> p kc t", p=128)

    def nxt(key, n):
        v = cnt[key] % n
        cnt[key] += 1
        return v

    st_info = []
    for (tag, src, ntok, row, uoff, rope) in streams:
        A, tmp = self.adaln_cols(l, 1, row, f"f{tag}")
        st_info.append((A, tmp, src, uoff, rope))
    tiles = []
    for si, (tag, src, ntok, row, uoff, rope) in enumerate(streams):
        for t0 in range(0, ntok, 512):
            tiles.append((si, t0, min(512, ntok - t0)))

    def prep_load(k, s):
        si, t0, n = tiles[k]
        A, tmp, src, uoff, rope = st_info[si]
        i = nxt("xl", 3)
        S.dma(S.sp, [(xl[i][:], src[t0 + s * 128:t0 + (s + 1) * 128, :])], reads=[self.R(src.name)], writes=[xl[i].res])
        cst = None
        return i

    def rope_load(k, s):
        si, t0, n = tiles[k]
        A, tmp, src, uoff, rope = st_info[si]
        if not rope:
            return None
        cst = cs_t[nxt("cs", 3)]
        S.dma(S.sp, [(cst[:], self.rope_cs[t0 + s * 128:t0 + s * 128 + 128, :, :])], writes=[cst.res])
        return cst

    def prep_sub(k, s, i=None):
        si, t0, n = tiles[k]
        A, tmp, src, uoff, rope = st_info[si]
        hT = hTs[k % 2]
        if i is None:
            i = prep_load(k, s)
        self.norm_part(xl[i], nb[i % 2], ss[i % 2], rs[i % 2])
        self.shres = tmp.res
        self.transpose_part(nb[i % 2], pT, hT, s * 128, A, tmp[:, 0, :], [S.act, S.dve])

    def g_part(k):
        si, t0, n = tiles[k]
        A, tmp, src, uoff, rope = st_info[si]
        hT = hTs[k % 2]
        u0 = uoff + t0
        nch = n // 64
        for kc in range(8):
            S.op(S.pe, lambda h, kc=kc, n=n, hT=hT: h.matmul(pz[0:32, :n], Wg[:, kc, 0:32], hT[:, kc, :n], start=(kc == 0), stop=(kc == 7)),
                 reads=[hT.res, Wg.res], writes=[pz.res], inc=(kc == 7))
        S.op(S.act, lambda h, n=n: h.activation(out=ggT[:, :n], in_=pz[0:32, :n], func=AF.Copy), reads=[pz.res], writes=[ggT.res])
        for kc in range(8):
            S.op(S.pe, lambda h, kc=kc, n=n, hT=hT: h.matmul(pz[0:16, :n], Wg[:, kc, 32:48], hT[:, kc, :n], start=(kc == 0), stop=(kc == 7)),
                 reads=[hT.res, Wg.res], writes=[pz.res], inc=(kc == 7))
        S.op(S.act, lambda h, n=n: h.activation(out=gts[:, :n], in_=pz[0:16, :n], func=AF.Copy), reads=[pz.res], writes=[gts.res])
        S.dma(S.sp, [(self.GATES[l][:, u0:u0 + n], gts[:, :n])], reads=[gts.res], writes=[self.R(self.GATES[l].name)])
        for d in range(2):
            for c2 in range(2):
                S.op(S.pe, lambda h, d=d, c2=c2, n=n: h.matmul(pz[:, :n], W2p[:, d, c2 * 128:(c2 + 1) * 128], ggT[:, :n], start=True, stop=True),
                     reads=[W2p.res, ggT.res], writes=[pz.res])
                e_ = ex[nxt("ex", 2)]
                c_ = csum[(cnt["ex"]) % 2]
                S.op(S.act, lambda h, d=d, c2=c2, n=n, e_=e_: h.activation(out=e_[:, :n], in_=pz[:, :n], func=AF.Exp, scale=-1.0, bias=negb[:, d, c2:c2 + 1]),
                     reads=[pz.res, negb.res], writes=[e_.res])
                S.op(S.act, lambda h, n=n, e_=e_: h.activation(out=e_[:, :n], in_=e_[:, :n], func=AF.Ln, bias=1.0), reads=[e_.res], writes=[e_.res])
                m01 = mask01[:].rearrange("p a b -> p (a b)")[:, :n]
                if d == 0:
                    S.op(S.dve, lambda h, n=n, e_=e_, c_=c_, m01=m01: h.tensor_tensor_scan(out=c_[:, :n], data0=m01, data1=e_[:, :n], initial=0.0, op0=ALU.mult, op1=ALU.add),
                         reads=[e_.res, mask01.res], writes=[c_.res])
                    last = 63
                else:
                    S.op(S.dve, lambda h, n=n, e_=e_, c_=c_, m01=m01: h.tensor_tensor_scan(out=rev_last(c_[:, :n]), data0=m01, data1=rev_last(e_[:, :n]), initial=0.0,
                                                                                         op0=ALU.mult, op1=ALU.add),
                         reads=[e_.res, mask01.res], writes=[c_.res])
                    last = 0
                EB, ENB, EBL = eb[d][c2], enb[d][c2], ebl[d][c2]
                S.op(S.act, lambda h, n=n, c_=c_, EB=EB: h.activation(out=EB[:, :n], in_=c_[:, :n], func=AF.Exp, scale=-1.0 / 16), reads=[c_.res], writes=[EB.res])
                S.op(S.act, lambda h, n=n, c_=c_, ENB=ENB: h.activation(out=ENB[:, :n], in_=c_[:, :n], func=AF.Exp, scale=1.0 / 16), reads=[c_.res], writes=[ENB.res])
                c3 = c_[:, :n].rearrange("p (a b) -> p a b", b=64)
                S.op(S.pool, lambda h, n=n, c_=c_, c3=c3, last=last, nch=nch: h.tensor_tensor(out=c3, in0=c3, in1=c3[:, :, last:last + 1].to_broadcast([128, nch, 64]), op=ALU.subtract),
                     reads=[c_.res], writes=[c_.res])
                S.op(S.act, lambda h, n=n, c_=c_, EBL=EBL: h.activation(out=EBL[:, :n], in_=c_[:, :n], func=AF.Exp, scale=1.0 / 16), reads=[c_.res], writes=[EBL.res])
                ch0 = u0 // 64
                elc = elcs[nxt("elc", 2)]
                S.op(S.pool, lambda h, n=n, EB=EB, last=last, elc=elc, nch=nch: h.tensor_copy(
                    out=elc[:, :nch], in_=EB[:, :n].rearrange("p (a b) -> p a b", b=64)[:, :, last]),
                     reads=[EB.res], writes=[elc.res])
                S.dma(S.sp, [(self.EL[:, 2 * c2 + hh2, d, ch0:ch0 + nch], elc[hh2 * 64:(hh2 + 1) * 64, :nch]) for hh2 in range(2)],
                      reads=[elc.res], writes=[self.EL.res])

    def a_mm(k, s):
        hT = hTs[k % 2]
        c0 = s * 128
        for kc in range(8):
            S.op(S.pe, lambda h, kc=kc, c0=c0, hT=hT: h.matmul(pq[:], hT[:, kc, c0:c0 + 128], Wa[:, kc, 0:512], start=(kc == 0), stop=(kc == 7)),
                 reads=[hT.res, Wa.res], writes=[pq.res], inc=(kc == 7))
        for kc in range(8):
            S.op(S.pe, lambda h, kc=kc, c0=c0, hT=hT: h.matmul(pkv[:], hT[:, kc, c0:c0 + 128], Wa[:, kc, 512:768], start=(kc == 0), stop=(kc == 7)),
                 reads=[hT.res, Wa.res], writes=[pkv.res], inc=(kc == 7))

    def a_chain(k, s, cst=None):
        si, t0, n = tiles[k]
        A, tmp, src, uoff, rope = st_info[si]
        u0 = uoff + t0
        c0 = s * 128
        S.op(S.act, lambda h: h.activation(out=sqt[:, 0:512], in_=pq[:], func=AF.Square), reads=[pq.res], writes=[sqt.res])
        S.op(S.act, lambda h: h.activation(out=sqt[:, 512:640], in_=pkv[:, 0:128], func=AF.Square), reads=[pkv.res], writes=[sqt.res])
        vi = nxt("vb", 4)
        S.op(S.act, lambda h, vi=vi: h.activation(out=vb[vi][:], in_=pkv[:, 128:256], func=AF.Copy), reads=[pkv.res], writes=[vb[vi].res])
        S.dma(S.sp, [(self.VA[l][u0 + c0:u0 + c0 + 128, :], vb[vi][:])], reads=[vb[vi].res], writes=[self.R(self.VA[l].name)])
        S.op(S.dve, lambda h: h.tensor_reduce(out=ssh[:], in_=sqt[:].rearrange("p (a b) -> p a b", b=64), axis=AX.X, op=ALU.add),
             reads=[sqt.res], writes=[ssh.res])
        S.op(S.act, lambda h: h.activation(out=rinv[:], in_=ssh[:], func=AF.Sqrt, scale=1.0 / 64, bias=self.eps_col[:]),
             reads=[ssh.res, self.eps_col.res], writes=[rinv.res])
        S.op(S.dve, lambda h: h.reciprocal(out=rinv[:], in_=rinv[:]), reads=[rinv.res], writes=[rinv.res])
        S.op(S.dve, lambda h: h.tensor_tensor(out=qn[:, 0:8, :], in0=pq[:].rearrange("p (a b) -> p a b", b=64),
                                               in1=rinv[:, 0:8].unsqueeze(2).to_broadcast([128, 8, 64]), op=ALU.mult),
             reads=[pq.res, rinv.res], writes=[qn.res])
        S.op(S.dve, lambda h: h.tensor_tensor(out=qn[:, 8:10, :], in0=pkv[:, 0:128].rearrange("p (a b) -> p a b", b=64),
                                               in1=rinv[:, 8:10].unsqueeze(2).to_broadcast([128, 2, 64]), op=ALU.mult),
             reads=[pkv.res, rinv.res], writes=[qn.res])
        S.op(S.pool, lambda h: h.tensor_tensor(out=qn[:], in0=qn[:], in1=gain[:], op=ALU.mult),
             reads=[qn.res, gain.res], writes=[qn.res])
        q_ = qr[nxt("qr", 2)]
        if rope:
            cosb = cst[:, 0:1, :].to_broadcast([128, 10, 32])
            sinb = cst[:, 1:2, :].to_broadcast([128, 10, 32])
            x1 = qn[:, :, 0:32]
            x2 = qn[:, :, 32:64]
            r = [rt[nxt("rt", 4)] for _ in range(4)]
            S.op(S.pool, lambda h, r=r, cosb=cosb, x1=x1: h.tensor_tensor(out=r[0][:], in0=x1, in1=cosb, op=ALU.mult),
                 reads=[qn.res, cst.res], writes=[r[0].res])
            S.op(S.dve, lambda h, r=r, sinb=sinb, x2=x2: h.tensor_tensor(out=r[1][:], in0=x2, in1=sinb, op=ALU.mult),
                 reads=[qn.res, cst.res], writes=[r[1].res])
            S.op(S.dve, lambda h, r=r, q_=q_: h.tensor_tensor(out=q_[:, :, 0:32], in0=r[0][:], in1=r[1][:], op=ALU.subtract),
                 reads=[r[0].res, r[1].res], writes=[q_.res])
            S.op(S.pool, lambda h, r=r, sinb=sinb, x1=x1: h.tensor_tensor(out=r[2][:], in0=x1, in1=sinb, op=ALU.mult),
                 reads=[qn.res, cst.res], writes=[r[2].res])
            S.op(S.dve, lambda h, r=r, cosb=cosb, x2=x2: h.tensor_tensor(out=r[3][:], in0=x2, in1=cosb, op=ALU.mult),
                 reads=[qn.res, cst.res], writes=[r[3].res])
            S.op(S.dve, lambda h, r=r, q_=q_: h.tensor_tensor(out=q_[:, :, 32:64], in0=r[2][:], in1=r[3][:], op=ALU.add),
                 reads=[r[2].res, r[3].res], writes=[q_.res])
        else:
            S.op(S.dve, lambda h, q_=q_: h.tensor_copy(out=q_[:], in_=qn[:]), reads=[qn.res], writes=[q_.res])
        return q_

    def a_tr(q_, s):
        c0 = s * 128
        for hh in range(8):
            S.op(S.pe, lambda h, hh=hh, q_=q_: h.transpose(out=pqt[:, hh, :], in_=q_[:, hh, :], identity=self.ident[:]),
                 reads=[q_.res, self.ident.res], writes=[pqt.res], inc=(hh == 7))
        for hh in range(2):
            S.op(S.pe, lambda h, hh=hh, q_=q_: h.transpose(out=pkt[:, hh, :], in_=q_[:, 8 + hh, :], identity=self.ident[:]),
                 reads=[q_.res, self.ident.res], writes=[pkt.res], inc=(hh == 1))
        S.op(S.act, lambda h, c0=c0: h.activation(out=QTs[:, :, c0:c0 + 128], in_=pqt[:], func=AF.Copy), reads=[pqt.res], writes=[QTs.res])
        S.op(S.dve, lambda h, c0=c0: h.tensor_copy(out=KTs[:, :, c0:c0 + 128], in_=pkt[:]), reads=[pkt.res], writes=[KTs.res])

    def b_part(k, s):
        si, t0, n = tiles[k]
        uoff = st_info[si][3]
        u0 = uoff + t0
        hT = hTs[k % 2]
        c0 = s * 128
        for half in range(2):
            kk = nxt("pf", 2)
            for kc in range(8):
                S.op(S.pe, lambda h, kc=kc, c0=c0, kk=kk, half=half, hT=hT: h.matmul(pf[kk][:], hT[:, kc, c0:c0 + 128], Wv[:, kc, half * 512:(half + 1) * 512],
                                                                                  start=(kc == 0), stop=(kc == 7)),
                     reads=[hT.res, Wv.res], writes=[pf[kk].res], inc=(kc == 7))
            vi = nxt("vb2", 4)
            if half == 0:
                S.op(S.act, lambda h, kk=kk, vi=vi: h.activation(out=vb2[vi][:], in_=pf[kk][:], func=AF.Copy), reads=[pf[kk].res], writes=[vb2[vi].res])
            else:
                S.op(S.dve, lambda h, kk=kk, vi=vi: h.tensor_copy(out=vb2[vi][:], in_=pf[kk][:]), reads=[pf[kk].res], writes=[vb2[vi].res])
            dstv = self.GV[l] if half == 0 else self.MV[l]
            S.dma(S.sp, [(dstv[u0 + c0:u0 + c0 + 128, :], vb2[vi][:])], reads=[vb2[vi].res], writes=[self.R(dstv.name)])

    def c_part(k, fcs):
        si, t0, n = tiles[k]
        uoff = st_info[si][3]
        u0 = uoff + t0
        hT = hTs[k % 2]
        for fc in fcs:
            kk = nxt("pf", 2)
            for kc in range(8):
                S.op(S.pe, lambda h, kc=kc, fc=fc, kk=kk, n=n, hT=hT: h.matmul(pf[kk][:, :n], Wf[:, kc, fc * 128:(fc + 1) * 128], hT[:, kc, :n], start=(kc == 0), stop=(kc == 7)),
                     reads=[hT.res, Wf.res], writes=[pf[kk].res], inc=(kc == 7))
            if fc < 2:
                for d in range(2):
                    o_ = fo[nxt("fo", 8)]
                    S.op(S.dve, lambda h, kk=kk, n=n, d=d, fc=fc, o_=o_: h.scalar_tensor_tensor(out=o_[:, :n], in0=pf[kk][:, :n], scalar=0.125, in1=eb[d][fc][:, :n],
                                                                                           op0=ALU.mult, op1=ALU.mult),
                         reads=[pf[kk].res, eb[d][fc].res], writes=[o_.res])
                    S.dma(S.sp, [(self.QG[l][d, :, 2 * fc + hh2, u0:u0 + n], o_[hh2 * 64:(hh2 + 1) * 64, :n]) for hh2 in range(2)], reads=[o_.res], writes=[self.R(self.QG[l].name)])
            elif fc < 4:
                c2 = fc - 2
                for d in range(2):
                    o_ = fo[nxt("fo", 8)]
                    S.op(S.dve, lambda h, kk=kk, n=n, d=d, c2=c2, o_=o_: h.tensor_tensor(out=o_[:, :n], in0=pf[kk][:, :n], in1=enb[d][c2][:, :n], op=ALU.mult),
                         reads=[pf[kk].res, enb[d][c2].res], writes=[o_.res])
                    S.dma(S.sp, [(self.KG[l][d, :, 2 * c2 + hh2, u0:u0 + n], o_[hh2 * 64:(hh2 + 1) * 64, :n]) for hh2 in range(2)], reads=[o_.res], writes=[self.R(self.KG[l].name)])
                    o_ = fo[nxt("fo", 8)]
                    S.op(S.dve, lambda h, kk=kk, n=n, d=d, c2=c2, o_=o_: h.tensor_tensor(out=o_[:, :n], in0=pf[kk][:, :n], in1=ebl[d][c2][:, :n], op=ALU.mult),
                         reads=[pf[kk].res, ebl[d][c2].res], writes=[o_.res])
                    S.dma(S.sp, [(self.KH[l][d, :, 2 * c2 + hh2, u0:u0 + n], o_[hh2 * 64:(hh2 + 1) * 64, :n]) for hh2 in range(2)], reads=[o_.res], writes=[self.R(self.KH[l].name)])
            else:
                o_ = fo[nxt("fo", 8)]
                S.op(S.act, lambda h, kk=kk, n=n, o_=o_: h.activation(out=o_[:, :n], in_=pf[kk][:, :n], func=AF.Copy), reads=[pf[kk].res], writes=[o_.res])
                r0 = (fc - 4) * 128
                S.dma(S.sp, [(self.MQK[l][r0:r0 + 128, 2 + u0:2 + u0 + n], o_[:, :n])], reads=[o_.res], writes=[self.R(self.MQK[l].name)])

    for s in range(tiles[0][2] // 128):
        prep_sub(0, s)
    for k, (si, t0, n) in enumerate(tiles):
        A, tmp, src, uoff, rope_ = st_info[si]
        rope = rope_
        nt = n // 128
        u0 = uoff + t0
        hT = hTs[k % 2]
        S.dma(S.sp, [(H2Tv[:, :, u0:u0 + n], hT[:, :, :n])], reads=[hT.res], writes=[self.R(self.H2T[l].name)])
        g_part(k)
        order = [4, 5, 6, 7, 8, 9, 10, 11, 0, 1, 2, 3]
        per = (12 + nt - 1) // nt
        pend = None
        nxt_nt = tiles[k + 1][2] // 128 if k + 1 < len(tiles) else 0
        for s in range(nt):
            xi = prep_load(k + 1, s) if s < nxt_nt else None
            cst = rope_load(k, s)
            a_mm(k, s)
            q_ = a_chain(k, s, cst)
            if pend is not None:
                a_tr(*pend)
            pend = (q_, s)
            b_part(k, s)
            c_part(k, order[s * per:(s + 1) * per])
            if s < nxt_nt:
                prep_sub(k + 1, s, xi)
        a_tr(*pend)
        for s in range(nt, nxt_nt):
            prep_sub(k + 1, s)
        S.dma(S.sp, [(self.QT[l][:, :, u0:u0 + n], QTs[:, :, :n])], reads=[QTs.res], writes=[self.R(self.QT[l].name)])
        S.dma(S.sp, [(self.KT[l][:, :, u0:u0 + n], KTs[:, :, :n])], reads=[KTs.res], writes=[self.R(self.KT[l].name)])


Builder.phase_feat = phase_feat


def phase_attn(self, l, do_ctx):
    S = self.S
    T = self.T
    nbk = T // 128
    ones = self.sb("ones", [128, 128], F32)
    S.op(S.pool, lambda h: h.memset(ones[:], 1.0), writes=[ones.res])
    mP = self.sb("mP", [128, 4, 128], BF16)
    mN = self.sb("mN", [128, 4, 128], BF16)
    mtmp = self.sb("mtmp", [128, 128], F32)
    zer = self.sb("zer", [128, 128], F32)
    S.op(S.pool, lambda h: h.memset(zer[:], 0.0), writes=[zer.res])
    for (m_, sgn) in ((mP, 1), (mN, -1)):
        S.op(S.pool, lambda h, sgn=sgn: h.affine_select(out=mtmp[:], in_=zer[:], pattern=[[-sgn, 128]], compare_op=ALU.is_ge, fill=-30000.0,
                                                         base=0, channel_multiplier=sgn), reads=[zer.res], writes=[mtmp.res])
        S.op(S.pool, lambda h, m_=m_: h.tensor_copy(out=m_[:], in_=mtmp[:].unsqueeze(1).to_broadcast([128, 4, 128])), reads=[mtmp.res], writes=[m_.res])
    esk = self.sb("esk", [128, 2, 4, 128], F32)
    sk8 = self.sb("sk8", [128, 8], F32)
    S.dma(S.sp, [(sk8[64:65, :], self.attn_sink[l:l + 1, :])], writes=[sk8.res])
    S.op(S.act, lambda h: h.activation(out=sk8[64:65, :], in_=sk8[64:65, :], func=AF.Exp), reads=[sk8.res], writes=[sk8.res])
    S.op(S.dve, lambda h: h.tensor_copy(out=esk[64:65].rearrange("p g a b -> p (g a) b"), in_=sk8[64:65, :].unsqueeze(2).to_broadcast([1, 8, 128])),
         reads=[sk8.res], writes=[esk.res])
    KTc = self.sb("KTc", [64, 2, 256], BF16)
    S.dma(S.sp, [(KTc[:], self.KT[l][:, :, 0:256])], reads=[self.R(self.KT[l].name)], writes=[KTc.res])
    Vc = [self.sb(f"Vc{j}", [128, 2, 65], BF16) for j in range(2)]
    Vb = [self.sb(f"Vb{j}", [128, 2, 65], BF16) for j in range(4)]
    KTb = [self.sb(f"KTb{j}", [64, 2, 128], BF16) for j in range(4)]
    for v in Vc + Vb:
        S.op(S.pool, lambda h, v=v: h.memset(v[:], 1.0), writes=[v.res])
    for j in range(2):
        S.dma(S.sp, [(Vc[j][:, :, 0:64], self.VA[l][j * 128:(j + 1) * 128, :].rearrange("p (g d) -> p g d", d=64))],
              reads=[self.R(self.VA[l].name)], writes=[Vc[j].res])
    QTb = [self.sb(f"QTb{j}", [64, 8, 128], BF16) for j in range(2)]
    E = [self.sb(f"E{j}", [128, 4, 128], BF16) for j in range(4)]
    dn = [self.sb(f"dn{j}", [128, 512], F32) for j in range(2)]
    bcs = [self.sb(f"bcs{j}", [64, 512], F32) for j in range(2)]
    aT = [self.sb(f"aT{j}", [64, 4, 128], BF16) for j in range(2)]
    pS = [self.ps(f"pS{j}", [128, 512]) for j in range(3)]
    pO = [self.ps(f"pO{j}", [128, 512]) for j in range(2)]
    pB = [self.ps(f"pB{j}", [64, 512]) for j in range(2)]
    cnt = {}

    def nxt(key, n):
        v = cnt.get(key, 0)
        cnt[key] = v + 1
        return v % n

    def load_kb(m):
        i = m % 4
        u = LC + m * 128
        S.dma(S.sp, [(KTb[i][:], self.KT[l][:, :, u:u + 128])], reads=[self.R(self.KT[l].name)], writes=[KTb[i].res])
        S.dma(S.sp, [(Vb[i][:, :, 0:64], self.VA[l][u:u + 128, :].rearrange("p (g d) -> p g d", d=64))],
              reads=[self.R(self.VA[l].name)], writes=[Vb[i].res])

    pending = []

    def norm(g, po, u0):
        d_ = dn[nxt("dn", 2)]
        S.op(S.dve, lambda h, d_=d_, po=po, g=g: h.tensor_tensor(out=d_[64:65, :], in0=po[64:65, :], in1=esk[64:65, g].rearrange("p a b -> p (a b)"), op=ALU.add),
             reads=[po.res, esk.res], writes=[d_.res])
        S.op(S.dve, lambda h, d_=d_: h.reciprocal(out=d_[64:65, :], in_=d_[64:65, :]), reads=[d_.res], writes=[d_.res])
        pb = pB[nxt("pb", 2)]
        S.op(S.pe, lambda h, d_=d_, pb=pb: h.matmul(pb[:], ones[64:65, 0:64], d_[64:65, :], start=True, stop=True),
             reads=[d_.res, ones.res], writes=[pb.res])
        b_ = bcs[nxt("bcs", 2)]
        S.op(S.act, lambda h, b_=b_, pb=pb: h.activation(out=b_[:], in_=pb[:], func=AF.Copy), reads=[pb.res], writes=[b_.res])
        a_ = aT[nxt("aT", 2)]
        S.op(S.dve, lambda h, a_=a_, b_=b_, po=po: h.tensor_tensor(out=a_[:].rearrange("p a b -> p (a b)"), in0=po[0:64, :], in1=b_[:], op=ALU.mult),
             reads=[po.res, b_.res], writes=[a_.res])
        S.dma(S.sp, [(self.ATT[l][:, 4 * g:4 * g + 4, u0:u0 + 128], a_[:])], reads=[a_.res], writes=[self.R(self.ATT[l].name)])

    def qblock(u0, kbs):
        qi = nxt("q", 2)
        Q = QTb[qi]
        S.dma(S.sp, [(Q[:], self.QT[l][:, :, u0:u0 + 128])], reads=[self.R(self.QT[l].name)], writes=[Q.res])
        for g in range(2):
            po = pO[nxt("po", 2)]
            rhsq = Q[:, 4 * g:4 * g + 4, :].rearrange("p a b -> p (a b)")

            def score(idx, g=g, rhsq=rhsq):
                kt, vt, msk = kbs[idx]
                p = pS[nxt("ps", 3)]
                S.op(S.pe, lambda h, kt=kt, p=p, g=g, rhsq=rhsq, msk=msk: h.matmul(p[:], kt[0][:, g, kt[1]:kt[1] + 128], rhsq, start=True, stop=(msk is None)),
                     reads=[kt[0].res, Q.res], writes=[p.res], inc=(msk is None))
                if msk is not None:
                    S.op(S.pe, lambda h, p=p, msk=msk: h.matmul(p[:], self.ident[:], msk[:].rearrange("p a b -> p (a b)"), start=False, stop=True),
                         reads=[self.ident.res, msk.res], writes=[p.res])
                return p
            ps_list = [score(0)]
            for idx in range(len(kbs)):
                kt, vt, msk = kbs[idx]
                if idx + 1 < len(kbs):
                    ps_list.append(score(idx + 1))
                p = ps_list[idx]
                e = E[nxt("e", 4)]
                S.op(S.act, lambda h, p=p, e=e: h.activation(out=e[:].rearrange("p a b -> p (a b)"), in_=p[:], func=AF.Exp, scale=0.125),
                     reads=[p.res], writes=[e.res])
                S.op(S.pe, lambda h, e=e, vt=vt, po=po, idx=idx, g=g, kbs=kbs: h.matmul(po[0:65, :], vt[:, g, :], e[:].rearrange("p a b -> p (a b)"),
                                                                                      start=(idx == 0), stop=(idx == len(kbs) - 1)),
                     reads=[e.res, vt.res], writes=[po.res], inc=(idx == len(kbs) - 1))
            pending.append((g, po, u0))
            if len(pending) > 1:
                norm(*pending.pop(0))

    ckb = [((KTc, 0), Vc[0], None), ((KTc, 128), Vc[1], None)]
    if do_ctx:
        for n in range(2):
            qblock(n * 128, ckb)
    load_kb(0)
    for n in range(nbk):
        if n + 1 < nbk:
            load_kb(n + 1)
        kbs = []
        if n - 1 >= 0:
            kbs.append(((KTb[(n - 1) % 4], 0), Vb[(n - 1) % 4], mP))
        kbs.append(((KTb[n % 4], 0), Vb[n % 4], None))
        if n + 1 < nbk:
            kbs.append(((KTb[(n + 1) % 4], 0), Vb[(n + 1) % 4], mN))
        qblock(LC + n * 128, kbs + ckb)
    while pending:
        norm(*pending.pop(0))


Builder.phase_attn = phase_attn


def scan_groups(T):
    return [(0, LC)] + [(LC + t0, min(512, T - t0)) for t0 in range(0, T, 512)]


def scan_order(T, d):
    groups = scan_groups(T)
    order = []
    if d == 0:
        for gi, (u0, n) in enumerate(groups):
            for c in range(n // 64):
                order.append((gi, c))
    else:
        gis = [0] + list(range(len(groups) - 1, 0, -1))
        for gi in gis:
            u0, n = groups[gi]
            for c in range(n // 64 - 1, -1, -1):
                order.append((gi, c))
    return order


def phase_gla(self, l):
    S = self.S
    T = self.T
    EL = self.EL
    groups = scan_groups(T)
    ones = self.sb("ones", [64, 64], F32)
    S.op(S.pool, lambda h: h.memset(ones[:], 1.0), writes=[ones.res])
    mtmp = self.sb("mtmp", [64, 64], F32)
    msk = [self.sb(f"msk{d}", [64, 64], BF16) for d in range(2)]
    for d, sgn in ((0, -1), (1, 1)):
        S.op(S.pool, lambda h, sgn=sgn: h.affine_select(out=mtmp[:], in_=ones[:], pattern=[[-sgn, 64]], compare_op=ALU.is_ge, fill=0.0,
                                                         base=0, channel_multiplier=sgn), reads=[ones.res], writes=[mtmp.res])
        S.op(S.pool, lambda h, d=d: h.tensor_copy(out=msk[d][:], in_=mtmp[:]), reads=[mtmp.res], writes=[msk[d].res])
    Sf = [self.sb(f"Sf{d}", [64, 4, 128], F32) for d in range(2)]
    Sb = [self.sb(f"Sb{d}", [64, 4, 128], BF16) for d in range(2)]
    for d in range(2):
        S.op(S.pool, lambda h, d=d: h.memset(Sf[d][:], 0.0), writes=[Sf[d].res])
        S.op(S.pool, lambda h, d=d: h.memset(Sb[d][:], 0.0), writes=[Sb[d].res])
    qg = [[self.sb(f"qg{d}{i}", [64, 4, 512], BF16) for i in range(2)] for d in range(2)]
    kg = [[self.sb(f"kg{d}{i}", [64, 4, 512], BF16) for i in range(2)] for d in range(2)]
    kh = [[self.sb(f"kh{d}{i}", [64, 4, 512], BF16) for i in range(2)] for d in range(2)]
    vg = [[self.sb(f"vg{d}{i}", [64, 8, 512], BF16) for i in range(2)] for d in range(2)]
    am = [[self.sb(f"am{d}{i}", [64, 4, 64], BF16) for i in range(2)] for d in range(2)]
    kt = [[self.sb(f"kt{d}{i}", [64, 4, 64], BF16) for i in range(2)] for d in range(2)]
    ob = [[self.sb(f"ob{d}{i}", [64, 512], F32) for i in range(2)] for d in range(2)]
    pA = [self.ps(f"pA{d}", [64, 256]) for d in range(2)]
    pK = [self.ps(f"pK{d}", [64, 256], BF16) for d in range(2)]
    pO = [self.ps(f"pO{d}", [64, 512]) for d in range(2)]
    pN = [self.ps(f"pN{d}", [64, 512]) for d in range(2)]
    orders = [scan_order(T, d) for d in range(2)]
    nsteps = len(orders[0])
    gcount = [0, 0]
    cur = [None, None]

    def load_group(d, gi):
        i = gcount[d] % 2
        gcount[d] += 1
        u0, n = groups[gi]
        nch = n // 64
        for (dst, srcT) in ((qg[d][i], self.QG[l]), (kg[d][i], self.KG[l]), (kh[d][i], self.KH[l])):
            S.dma(S.sp, [(dst[:, :, :n], srcT[d, :, :, u0:u0 + n])], reads=[self.R(srcT.name)], writes=[dst.res])
        S.dma(S.sp, [(vg[d][i][:, :nch, :], self.GV[l][u0:u0 + n, :].rearrange("(c p) f -> p c f", p=64))], reads=[self.R(self.GV[l].name)],
              writes=[vg[d][i].res])
        return i

    for step in range(nsteps):
        for d in range(2):
            gi, c = orders[d][step]
            if cur[d] is None or cur[d][0] != gi:
                cur[d] = (gi, load_group(d, gi))
            bi = cur[d][1]
            u0, n = groups[gi]
            o = c * 64
            chunk = (u0 + o) // 64
            Q, Kg, Kh, V = qg[d][bi], kg[d][bi], kh[d][bi], vg[d][bi]
            k2 = step % 2
            AM, KTt, OB = am[d][k2], kt[d][k2], ob[d][k2]
            for hh in range(4):
                S.op(S.pe, lambda h, hh=hh, d=d, Kg=Kg, Q=Q, o=o: h.matmul(pA[d][:, hh * 64:(hh + 1) * 64], Kg[:, hh, o:o + 64], Q[:, hh, o:o + 64], start=True, stop=True),
                     reads=[Kg.res, Q.res], writes=[pA[d].res], inc=(hh == 3))
            for hh in range(4):
                S.op(S.pe, lambda h, hh=hh, d=d, Kh=Kh, o=o: h.transpose(out=pK[d][:, hh * 64:(hh + 1) * 64], in_=Kh[:, hh, o:o + 64], identity=self.ident[0:64, 0:64]),
                     reads=[Kh.res, self.ident.res], writes=[pK[d].res], inc=(hh == 3))
            S.op(S.dve, lambda h, d=d, AM=AM: h.tensor_tensor(out=AM[:], in0=pA[d][:].rearrange("p (a b) -> p a b", b=64),
                                                              in1=msk[d][:].unsqueeze(1).to_broadcast([64, 4, 64]), op=ALU.mult),
                 reads=[pA[d].res, msk[d].res], writes=[AM.res])
            S.op(S.act, lambda h, d=d, KTt=KTt: h.activation(out=KTt[:].rearrange("p a b -> p (a b)"), in_=pK[d][:], func=AF.Copy), reads=[pK[d].res], writes=[KTt.res])
            for hh in range(4):
                S.op(S.pe, lambda h, hh=hh, d=d, AM=AM, V=V, c=c: h.matmul(pO[d][:, hh * 128:(hh + 1) * 128], AM[:, hh, :], V[:, c, hh * 128:(hh + 1) * 128], start=True, stop=False),
                     reads=[AM.res, V.res], writes=[pO[d].res], inc=False)
                S.op(S.pe, lambda h, hh=hh, d=d, Q=Q, o=o: h.matmul(pO[d][:, hh * 128:(hh + 1) * 128], Q[:, hh, o:o + 64], Sb[d][:, hh, :], start=False, stop=True),
                     reads=[Q.res, Sb[d].res], writes=[pO[d].res], inc=(hh == 3))
            for hh in range(4):
                S.op(S.pe, lambda h, hh=hh, d=d, KTt=KTt, V=V, c=c: h.matmul(pN[d][:, hh * 128:(hh + 1) * 128], KTt[:, hh, :], V[:, c, hh * 128:(hh + 1) * 128], start=True, stop=True),
                     reads=[KTt.res, V.res], writes=[pN[d].res], inc=(hh == 3))
            S.op(S.act, lambda h, d=d, OB=OB: h.activation(out=OB[:], in_=pO[d][:], func=AF.Copy), reads=[pO[d].res], writes=[OB.res])
            S.dma(S.sp, [(self.OG[l][d, u0 + o:u0 + o + 64, :], OB[:])], reads=[OB.res], writes=[self.R(self.OG[l].name)])
            S.op(S.dve, lambda h, d=d, chunk=chunk: h.tensor_tensor(out=Sf[d][:], in0=Sf[d][:], in1=EL[:, :, d, chunk:chunk + 1].to_broadcast([64, 4, 128]), op=ALU.mult),
                 reads=[Sf[d].res, self.EL.res], writes=[Sf[d].res])
            S.op(S.dve, lambda h, d=d: h.tensor_tensor(out=Sf[d][:].rearrange("p a b -> p (a b)"), in0=Sf[d][:].rearrange("p a b -> p (a b)"), in1=pN[d][:], op=ALU.add),
                 reads=[Sf[d].res, pN[d].res], writes=[Sf[d].res])
            S.op(S.act, lambda h, d=d: h.activation(out=Sb[d][:], in_=Sf[d][:], func=AF.Copy), reads=[Sf[d].res], writes=[Sb[d].res])


Builder.phase_gla = phase_gla


LN_KS = float(-0.5 * np.log(128.0))


def phase_ml_gates(self, l):
    S = self.S
    sel = self.sel
    DEC = self.DEC
    T = self.T
    TT = self.TT
    nch = TT // 64
    bA = self.sb("bA", [4, TT], F32)
    bL = self.sb("bL", [4, TT], F32)
    bC = self.sb("bC", [4, TT], F32)
    bG = self.sb("bG", [4, TT], F32)
    bX = self.sb("bX", [4, TT], F32)
    onesr = self.sb("onesr", [4, TT], BF16)
    S.op(S.pool, lambda h: h.memset(onesr[:], 1.0), writes=[onesr.res])
    gl = self.sb("gl", [4, nch], F32)
    gp = self.sb("gp", [4, nch], F32)
    dd = self.sb("dd", [4, nch], F32)
    ibc = self.sb("ibc", [4, 2], F32)
    pD = self.ps("pD", [128, 512])
    for d in range(2):
        S.dma(S.sp, [(bA[:], self.GATES[l][d * 4:(d + 1) * 4, :])], reads=[self.R(self.GATES[l].name)], writes=[bA.res])
        S.dma(S.sp, [(bL[:], self.GATES[l][8 + d * 4:8 + (d + 1) * 4, :])], reads=[self.R(self.GATES[l].name)], writes=[bL.res])
        S.dma(S.sp, [(ibc[:, 0:1], self.mlstm_ib[l, d, :].rearrange("(h o) -> h o", o=1)), (ibc[:, 1:2], self.mlstm_fb[l, d, :].rearrange("(h o) -> h o", o=1))],
              writes=[ibc.res])
        S.op(S.dve, lambda h: h.tensor_scalar(out=ibc[:, 1:2], in0=ibc[:, 1:2], scalar1=-1.0, scalar2=None, op0=ALU.mult), reads=[ibc.res], writes=[ibc.res])
        S.op(S.act, lambda h: h.activation(out=bL[:], in_=bL[:], func=AF.Exp, scale=-1.0, bias=ibc[:, 1:2]), reads=[bL.res, ibc.res], writes=[bL.res])
        S.op(S.act, lambda h: h.activation(out=bL[:], in_=bL[:], func=AF.Ln, bias=1.0), reads=[bL.res], writes=[bL.res])

        def scan(out, src, op1):
            if d == 0:
                S.op(S.dve, lambda h: h.tensor_tensor_scan(out=out[:], data0=onesr[:], data1=src[:], initial=0.0, op0=ALU.mult, op1=op1),
                     reads=[src.res, onesr.res], writes=[out.res])
            else:
                S.op(S.dve, lambda h: h.tensor_tensor_scan(out=rev_last(out[:, 0:LC]), data0=onesr[:, 0:LC], data1=rev_last(src[:, 0:LC]), initial=0.0, op0=ALU.mult, op1=op1),
                     reads=[src.res, onesr.res], writes=[out.res])
                S.op(S.dve, lambda h: h.tensor_tensor_scan(out=rev_last(out[:, LC:TT]), data0=onesr[:, LC:TT], data1=rev_last(src[:, LC:TT]), initial=out[:, 0:1], op0=ALU.mult, op1=op1),
                     reads=[src.res, onesr.res, out.res], writes=[out.res])
        scan(bC, bL, ALU.add)
        S.op(S.dve, lambda h: h.scalar_tensor_tensor(out=bA[:], in0=bA[:], scalar=ibc[:, 0:1], in1=bC[:], op0=ALU.add, op1=ALU.add),
             reads=[bA.res, ibc.res, bC.res], writes=[bA.res])
        scan(bG, bA, ALU.max)
        G3 = bG[:].rearrange("p (c b) -> p c b", b=64)
        lastpos = 63 if d == 0 else 0
        S.op(S.dve, lambda h, lastpos=lastpos, G3=G3: h.tensor_copy(out=gl[:], in_=G3[:, :, lastpos]), reads=[bG.res], writes=[gl.res])
        S.op(S.dve, lambda h: h.memset(gp[:], 0.0), writes=[gp.res])
        if d == 0:
            S.op(S.dve, lambda h: h.tensor_copy(out=gp[:, 1:nch], in_=gl[:, 0:nch - 1]), reads=[gl.res], writes=[gp.res])
        else:
            S.op(S.dve, lambda h: h.tensor_copy(out=gp[:, 0:3], in_=gl[:, 1:4]), reads=[gl.res], writes=[gp.res])
            S.op(S.dve, lambda h: h.tensor_copy(out=gp[:, 4:nch - 1], in_=gl[:, 5:nch]), reads=[gl.res], writes=[gp.res])
            S.op(S.dve, lambda h: h.tensor_copy(out=gp[:, nch - 1:nch], in_=gl[:, 0:1]), reads=[gl.res], writes=[gp.res])
        L3 = bL[:].rearrange("p (c b) -> p c b", b=64)
        X3 = bX[:].rearrange("p (c b) -> p c b", b=64)
        A3 = bA[:].rearrange("p (c b) -> p c b", b=64)
        S.op(S.dve, lambda h, L3=L3, G3=G3: h.tensor_tensor(out=L3, in0=gp[:].unsqueeze(2).to_broadcast([4, nch, 64]), in1=G3, op=ALU.subtract),
             reads=[gp.res, bG.res], writes=[bL.res])
        S.op(S.act, lambda h: h.activation(out=bL[:], in_=bL[:], func=AF.Exp), reads=[bL.res], writes=[bL.res])
        S.op(S.dve, lambda h: h.tensor_tensor(out=bC[:], in0=bC[:], in1=bG[:], op=ALU.subtract), reads=[bC.res, bG.res], writes=[bC.res])
        S.op(S.act, lambda h: h.activation(out=bC[:], in_=bC[:], func=AF.Exp), reads=[bC.res], writes=[bC.res])
        S.op(S.dve, lambda h, X3=X3, A3=A3: h.tensor_tensor(out=X3, in0=A3, in1=gl[:].unsqueeze(2).to_broadcast([4, nch, 64]), op=ALU.subtract),
             reads=[bA.res, gl.res], writes=[bX.res])
        S.op(S.dve, lambda h: h.tensor_scalar(out=bX[:], in0=bX[:], scalar1=LN_KS, scalar2=None, op0=ALU.add), reads=[bX.res], writes=[bX.res])
        S.op(S.act, lambda h: h.activation(out=bX[:], in_=bX[:], func=AF.Exp), reads=[bX.res], writes=[bX.res])
        S.op(S.dve, lambda h: h.tensor_tensor(out=dd[:], in0=gp[:], in1=gl[:], op=ALU.subtract), reads=[gp.res, gl.res], writes=[dd.res])
        S.op(S.act, lambda h: h.activation(out=dd[:], in_=dd[:], func=AF.Exp), reads=[dd.res], writes=[dd.res])
        for hh in range(4):
            S.op(S.pe, lambda h, hh=hh: h.matmul(pD[:, :nch], sel[:, hh, :], dd[:], start=True, stop=True), reads=[self.sel.res, dd.res], writes=[pD.res])
            S.op(S.dve, lambda h, hh=hh, d=d: h.tensor_copy(out=DEC[:, d, hh, :], in_=pD[:, :nch]), reads=[pD.res], writes=[self.DEC.res])
        for qi, buf in enumerate((bA, bG, bL, bC, bX)):
            S.dma(S.sp, [(self.MROWS[l][d, qi, :, :], buf[:])], reads=[buf.res], writes=[self.R(self.MROWS[l].name)])


def phase_ml_conv(self, l):
    S = self.S
    T = self.T
    TT = self.TT
    wcol = self.sb("wcol", [128, 8, 5], F32)
    cb = self.sb("cb", [128, 8], F32)
    S.dma(S.sp, [(wcol[:, :, k], self.conv_w[l, k, :].rearrange("(fc p) -> p fc", p=128)) for k in range(5)], writes=[wcol.res], allow_slow_non_contiguous=True)
    S.dma(S.sp, [(cb[:], self.conv_b[l, :].rearrange("(fc p) -> p fc", p=128))], writes=[cb.res], allow_slow_non_contiguous=True)
    diagw = self.sb("diagw", [128, 8, 5, 128], BF16)
    for fc in range(8):
        for k in range(5):
            e = S.dve if (fc * 5 + k) % 2 == 0 else S.pool
            S.op(e, lambda h, fc=fc, k=k: h.tensor_scalar(out=diagw[:, fc, k, :], in0=self.identf[:], scalar1=wcol[:, fc, k:k + 1], scalar2=None, op0=ALU.mult),
                 reads=[self.identf.res, wcol.res], writes=[diagw.res])
    xq = [self.sb(f"xq{i}", [128, 8, 516], BF16) for i in range(2)]
    oc = [self.sb(f"oc{i}", [128, 512], BF16) for i in range(3)]
    pc = [self.ps(f"pc{i}", [128, 512]) for i in range(2)]
    MQKv = self.MQK[l].rearrange("(c p) t -> p c t", p=128)
    k2 = 0
    for gi, (u0, n) in enumerate(scan_groups(T)):
        X = xq[gi % 2]
        S.dma(S.sp, [(X[:, 0:4, 0:n + 4], MQKv[:, 0:4, u0:u0 + n + 4]), (X[:, 4:8, 0:n + 4], MQKv[:, 4:8, u0:u0 + n + 4])], reads=[self.R(self.MQK[l].name)], writes=[X.res])
        if u0 == 0 or u0 == LC:
            S.op(S.pool, lambda h, X=X: h.memset(X[:, :, 0:2], 0.0), writes=[X.res])
        if u0 + n == LC or u0 + n == TT:
            S.op(S.pool, lambda h, X=X, n=n: h.memset(X[:, :, n + 2:n + 4], 0.0), writes=[X.res])
        for fc in range(8):
            p = pc[k2 % 2]
            o_ = oc[k2 % 3]
            k2 += 1
            for k in range(5):
                S.op(S.pe, lambda h, fc=fc, k=k, p=p, X=X, n=n: h.matmul(p[:, :n], diagw[:, fc, k, :], X[:, fc, k:k + n], start=(k == 0), stop=(k == 4)),
                     reads=[diagw.res, X.res], writes=[p.res], inc=(k == 4))
            S.op(S.act, lambda h, fc=fc, p=p, o_=o_, n=n: h.activation(out=o_[:, :n], in_=p[:, :n], func=AF.Silu, bias=cb[:, fc:fc + 1]), reads=[p.res, cb.res], writes=[o_.res])
            S.dma(S.sp, [(self.MQC[l][fc * 128:(fc + 1) * 128, u0:u0 + n], o_[:, :n])], reads=[o_.res], writes=[self.R(self.MQC[l].name)])


def phase_ml_scan(self, l):
    S = self.S
    T = self.T
    sel = self.sel
    DEC = self.DEC
    groups = scan_groups(T)
    cfill = self.sb("cfill", [64, 64], F32)
    S.op(S.pool, lambda h: h.memset(cfill[:], LN_KS), writes=[cfill.res])
    mb = [self.sb(f"mb{d}", [64, 64], F32) for d in range(2)]
    for d, sgn in ((0, -1), (1, 1)):
        S.op(S.pool, lambda h, sgn=sgn, d=d: h.affine_select(out=mb[d][:], in_=cfill[:], pattern=[[-sgn, 64]], compare_op=ALU.is_ge, fill=-30000.0,
                                                              base=0, channel_multiplier=sgn), reads=[cfill.res], writes=[mb[d].res])
    negones = self.sb("negones", [4, 128], F32)
    S.op(S.pool, lambda h: h.memset(negones[:], -1.0), writes=[negones.res])
    posones = self.sb("posones", [4, 128], F32)
    S.op(S.pool, lambda h: h.memset(posones[:], 1.0), writes=[posones.res])
    mbr = [self.sb(f"mbr{d}", [64, 4, 64], F32) for d in range(2)]
    for d in range(2):
        S.op(S.pool, lambda h, d=d: h.tensor_copy(out=mbr[d][:], in_=mb[d][:].unsqueeze(1).to_broadcast([64, 4, 64])), reads=[mb[d].res], writes=[mbr[d].res])
    Dg = [self.sb(f"Dg{i}", [4, 3, 4, 512], F32) for i in range(2)]
    DgH = [self.sb(f"DgH{i}", [4, 2, 4, 512], BF16) for i in range(2)]
    DgL = [self.sb(f"DgL{i}", [4, 2, 4, 512], BF16) for i in range(2)]
    posb = self.sb("posb", [4, 128], BF16)
    S.op(S.pool, lambda h: h.memset(posb[:], 1.0), writes=[posb.res])
    Cf = self.sb("Cf", [128, 4, 129], F32)
    Cb = self.sb("Cb", [128, 4, 129], BF16)
    qk = [self.sb(f"qk{i}", [128, 8, 512], BF16) for i in range(2)]
    vg = [self.sb(f"vgm{i}", [64, 8, 4, 129], BF16) for i in range(2)]
    rows = [self.sb(f"rows{i}", [4, 5, 512], F32) for i in range(2)]
    for v in vg:
        S.op(S.pool, lambda h, v=v: h.memset(v[:], 1.0), writes=[v.res])
    wT = [self.sb(f"wT{i}", [64, 256], F32) for i in range(3)]
    sT = [self.sb(f"sT{i}", [64, 4, 64], BF16) for i in range(3)]
    qks = [self.sb(f"qks{i}", [128, 8, 64], BF16) for i in range(3)]
    khat = [self.sb(f"khat{i}", [64, 4, 128], BF16) for i in range(3)]
    enm = [self.sb(f"enm{i}", [64, 4], F32) for i in range(3)]
    rr = [self.sb(f"rr{i}", [64, 4], F32) for i in range(3)]
    ho = [self.sb(f"ho{i}", [64, 4, 128], F32) for i in range(3)]
    pWS = self.ps("pWS", [64, 512])
    pB = self.ps("pB", [128, 8, 64])
    pK = self.ps("pK", [64, 4, 128], BF16)
    pO = self.ps("pO", [64, 1024])
    pN = self.ps("pN", [128, 1024])
    pO3 = pO[:].rearrange("p (h e) -> p h e", e=256)
    pN3 = pN[:].rearrange("p (h e) -> p h e", e=256)
    gcount = [0]

    def load_group(d, gi):
        i = gcount[0] % 2
        gcount[0] += 1
        u0, n = groups[gi]
        nchg = n // 64
        S.dma(S.sp, [(qk[i][:, 0:4, :n], self.MQC[l].rearrange("(c p) t -> p c t", p=128)[:, 0:4, u0:u0 + n]),
                     (qk[i][:, 4:8, :n], self.MQC[l].rearrange("(c p) t -> p c t", p=128)[:, 4:8, u0:u0 + n])], reads=[self.R(self.MQC[l].name)], writes=[qk[i].res])
        S.dma(S.sp, [(vg[i][:, c, :, 0:128], self.MV[l][u0 + c * 64:u0 + (c + 1) * 64, :].rearrange("p (h e) -> p h e", e=128)) for c in range(nchg)],
              reads=[self.R(self.MV[l].name)], writes=[vg[i].res])
        S.dma(S.sp, [(rows[i][:, :, :n], self.MROWS[l][d, :, :, u0:u0 + n].rearrange("q h t -> h q t"))], reads=[self.R(self.MROWS[l].name)], writes=[rows[i].res])
        for qd, qs in enumerate((1, 2, 4)):
            S.op(S.pool, lambda h, i=i, qd=qd, qs=qs, n=n: h.tensor_tensor(out=Dg[i][:, qd, :, :n], in0=rows[i][:, qs:qs + 1, :n].to_broadcast([4, 4, n]),
                                                                       in1=self.identf[0:4, 0:4].unsqueeze(2).to_broadcast([4, 4, n]), op=ALU.mult),
                 reads=[rows[i].res, self.identf.res], writes=[Dg[i].res])
        return i

    for d in range(2):
        S.op(S.pool, lambda h: h.memset(Cf[:], 0.0), writes=[Cf.res])
        S.op(S.pool, lambda h: h.memset(Cb[:], 0.0), writes=[Cb.res])
        order = scan_order(T, d)
        cur = None
        info = []
        for (gi, c) in order:
            if cur is None or cur[0] != gi:
                cur = (gi, None)
            info.append((gi, c))
        bufof = {}

        def stageA(step):
            gi, c = order[step]
            if gi not in bufof:
                bufof.clear()
                bufof[gi] = load_group(d, gi)
            bi = bufof[gi]
            o = c * 64
            k2 = step % 3
            QK, R_ = qk[bi], rows[bi]
            DG = Dg[bi]
            S.op(S.pe, lambda h, R_=R_, o=o: h.matmul(pWS[:, 0:256], R_[:, 0, o:o + 64], sel[:, :, 0:64], start=True, stop=False),
                 reads=[R_.res, sel.res], writes=[pWS.res], inc=False)
            S.op(S.pe, lambda h, DG=DG, o=o: h.matmul(pWS[:, 0:256], negones[:, 0:64], DG[:, 0, :, o:o + 64], start=False, stop=False),
                 reads=[DG.res, negones.res], writes=[pWS.res], inc=False)
            S.op(S.pe, lambda h, d=d: h.matmul(pWS[:, 0:256], self.identf[0:64, 0:64], mbr[d][:], start=False, stop=True),
                 reads=[self.identf.res, mbr[d].res], writes=[pWS.res], inc=False)
            for hh in range(4):
                S.op(S.pe, lambda h, hh=hh, QK=QK, o=o: h.matmul(pWS[:, 256 + hh * 64:256 + (hh + 1) * 64], QK[:, 4 + hh, o:o + 64], QK[:, hh, o:o + 64], start=True, stop=True),
                     reads=[QK.res], writes=[pWS.res], inc=(hh == 3))
            S.op(S.act, lambda h, k2=k2: h.activation(out=wT[k2][:], in_=pWS[:, 0:256], func=AF.Exp), reads=[pWS.res], writes=[wT[k2].res])
            S.op(S.dve, lambda h, k2=k2: h.tensor_tensor(out=sT[k2][:].rearrange("p a b -> p (a b)"), in0=pWS[:, 256:512], in1=wT[k2][:], op=ALU.mult),
                 reads=[pWS.res, wT[k2].res], writes=[sT[k2].res])
            S.op(S.pe, lambda h, DG=DG, o=o: h.matmul(pB[:], posones[:, :], DG[:, 1:3, :, o:o + 64], start=True, stop=True),
                 reads=[DG.res, posones.res], writes=[pB.res])
            S.op(S.dve, lambda h, k2=k2, QK=QK, o=o: h.tensor_tensor(out=qks[k2][:], in0=QK[:, :, o:o + 64], in1=pB[:], op=ALU.mult),
                 reads=[QK.res, pB.res], writes=[qks[k2].res])

        def stageA2(step):
            k2 = step % 3
            for hh in range(4):
                S.op(S.pe, lambda h, hh=hh, k2=k2: h.transpose(out=pK[:, hh, :], in_=qks[k2][:, 4 + hh, :], identity=self.ident[:]),
                     reads=[qks[k2].res, self.ident.res], writes=[pK.res], inc=(hh == 3))
            S.op(S.act, lambda h, k2=k2: h.activation(out=khat[k2][:], in_=pK[:], func=AF.Copy), reads=[pK.res], writes=[khat[k2].res])

        def stageB(step):
            gi, c = order[step]
            u0, n = groups[gi]
            o = c * 64
            chunk = (u0 + o) // 64
            k2 = step % 3
            V, R_ = vgbuf[step], rowbuf[step]
            for hh in range(4):
                S.op(S.pe, lambda h, hh=hh, k2=k2, V=V, c=c: h.matmul(pN[:, hh * 256:hh * 256 + 129], khat[k2][:, hh, :], V[:, c, hh, :], start=True, stop=True),
                     reads=[khat[k2].res, V.res], writes=[pN.res], inc=(hh == 3))
            S.op(S.pe, lambda h, R_=R_, o=o: h.matmul(pO[:, 200:204], R_[:, 3, o:o + 64], self.identf[0:4, 0:4], start=True, stop=True),
                 reads=[R_.res, self.identf.res], writes=[pO.res], inc=False)
            for hh in range(4):
                S.op(S.pe, lambda h, hh=hh, k2=k2, V=V, c=c: h.matmul(pO[:, hh * 256:hh * 256 + 129], sT[k2][:, hh, :], V[:, c, hh, :], start=True, stop=False),
                     reads=[sT[k2].res, V.res], writes=[pO.res], inc=False)
                S.op(S.pe, lambda h, hh=hh, k2=k2: h.matmul(pO[:, hh * 256:hh * 256 + 129], qks[k2][:, hh, :], Cb[:, hh, :], start=False, stop=True),
                     reads=[qks[k2].res, Cb.res], writes=[pO.res], inc=(hh == 3))
            S.op(S.act, lambda h, k2=k2: h.activation(out=enm[k2][:], in_=pO[:, 200:204], func=AF.Copy), reads=[pO.res], writes=[enm[k2].res])
            S.op(S.act, lambda h, k2=k2: h.activation(out=rr[k2][:], in_=pO3[:, :, 128], func=AF.Abs), reads=[pO.res], writes=[rr[k2].res])
            S.op(S.pool, lambda h, d=d, chunk=chunk: h.tensor_tensor(out=Cf[:], in0=Cf[:], in1=DEC[:, d, :, chunk:chunk + 1].to_broadcast([128, 4, 129]), op=ALU.mult),
                 reads=[Cf.res, DEC.res], writes=[Cf.res])
            S.op(S.dve, lambda h: h.tensor_tensor(out=Cf[:], in0=Cf[:], in1=pN3[:, :, 0:129], op=ALU.add), reads=[Cf.res, pN.res], writes=[Cf.res])
            S.op(S.act, lambda h: h.activation(out=Cb[:], in_=Cf[:], func=AF.Copy), reads=[Cf.res], writes=[Cb.res])
            S.op(S.dve, lambda h, k2=k2: h.tensor_tensor(out=rr[k2][:], in0=rr[k2][:], in1=enm[k2][:], op=ALU.max), reads=[rr[k2].res, enm[k2].res], writes=[rr[k2].res])
            S.op(S.dve, lambda h, k2=k2: h.reciprocal(out=rr[k2][:], in_=rr[k2][:]), reads=[rr[k2].res], writes=[rr[k2].res])
            S.op(S.dve, lambda h, k2=k2: h.tensor_tensor(out=ho[k2][:], in0=pO3[:, :, 0:128], in1=rr[k2][:].unsqueeze(2).to_broadcast([64, 4, 128]), op=ALU.mult),
                 reads=[pO.res, rr[k2].res], writes=[ho[k2].res])
            S.dma(S.sp, [(self.OM[l][d, u0 + o:u0 + o + 64, :], ho[k2][:].rearrange("p a b -> p (a b)"))], reads=[ho[k2].res], writes=[self.R(self.OM[l].name)])

        vgbuf = {}
        rowbuf = {}

        def A(step):
            stageA(step)
            gi, c = order[step]
            vgbuf[step] = vg[bufof[gi]]
            rowbuf[step] = rows[bufof[gi]]
        A(0)
        if len(order) > 1:
            A(1)
        stageA2(0)
        for step in range(len(order)):
            if step + 2 < len(order):
                A(step + 2)
            if step + 1 < len(order):
                stageA2(step + 1)
            stageB(step)


Builder.phase_ml_gates = phase_ml_gates
Builder.phase_ml_conv = phase_ml_conv
Builder.phase_ml_scan = phase_ml_scan


def phase_merge(self, l, streams):
    S = self.S
    win = self.w_in[l].rearrange("(kc p) n -> p kc n", p=128)
    Wm = self.sb("Wm", [128, 8, 4096], BF16)
    Wmr = [Res(f"Wm{k}") for k in range(4)]
    for ki, k0 in enumerate(range(0, 8, 2)):
        S.dma(S.pool, [(Wm[:, k0:k0 + 2, 0:512], win[:, k0:k0 + 2, O_GR:O_GR + 512]),
                       (Wm[:, k0:k0 + 2, 512:1024], win[:, k0:k0 + 2, O_MO:O_MO + 512]),
                       (Wm[:, k0:k0 + 2, 1024:4096], win[:, k0:k0 + 2, O_SA:O_SA + 3072])], writes=[Wmr[ki]])
    Woa = self.sb("Woa", [64, 8, D], BF16)
    Wog = self.sb("Wog", [128, 4, D], BF16)
    Wom = self.sb("Wom", [128, 4, D], BF16)
    Wo = self.sb("Wo", [128, 8, D], BF16)
    S.dma(S.pool, [(Woa[:], self.w_out_attn[l].rearrange("(h p) n -> p h n", p=64))], writes=[Woa.res])
    S.dma(S.pool, [(Wog[:], self.w_out_gla[l].rearrange("(c p) n -> p c n", p=128))], writes=[Wog.res])
    S.dma(S.pool, [(Wom[:], self.w_out_mlstm[l].rearrange("(c p) n -> p c n", p=128))], writes=[Wom.res])
    S.dma(S.pool, [(Wo[:, 0:4, :], self.w_o[l].rearrange("(c p) n -> p c n", p=128)[:, 0:4, :]),
                   (Wo[:, 4:8, :], self.w_o[l].rearrange("(c p) n -> p c n", p=128)[:, 4:8, :])], writes=[Wo.res])
    gains = self.sb("gains", [128, 2, 128], F32)
    S.dma(S.sp, [(gains[:, 0, :], bcast_rows(self.gla_norm[l:l + 1, :], 128)), (gains[:, 1, :], bcast_rows(self.mlstm_norm[l:l + 1, :], 128))], writes=[gains.res])
    eps_col = self.eps_col
    hT = [self.sb(f"mhT{i}", [128, 8, 128], BF16) for i in range(2)]
    aTt = [self.sb(f"maT{i}", [64, 8, 128], BF16) for i in range(2)]
    og = [self.sb(f"mog{i}", [128, 2, 512], F32) for i in range(2)]
    om = [self.sb(f"mom{i}", [128, 2, 512], F32) for i in range(2)]
    xr = [self.sb(f"mxr{i}", [128, D], F32) for i in range(2)]
    gts = [self.sb(f"mgt{i}", [128, 8, 512], F32) for i in range(2)]
    sq = self.sb("msq", [128, 512], F32)
    ssqs = [self.sb(f"mssq{i}", [128, 2, 4], F32) for i in range(2)]
    bn = [self.sb(f"mbn{i}", [128, 512], F32) for i in range(2)]
    bbs = [[self.sb(f"mbb{i}{j}", [128, 512], BF16) for j in range(2)] for i in range(2)]
    bTs = [[self.sb(f"mbT{i}{j}", [128, 4, 128], BF16) for j in range(2)] for i in range(2)]
    yb = self.sb("myb", [128, D], BF16)
    yT = self.sb("myT", [128, 8, 128], BF16)
    t1 = [self.sb(f"mt1{i}", [128, 512], F32) for i in range(3)]
    G5 = self.sb("mG5", [128, D], F32)
    pg = [self.ps(f"mpg{i}", [128, 512]) for i in range(2)]
    pT1s = [self.ps(f"mpT1{j}", [128, 512], BF16) for j in range(2)]
    pT2 = self.ps("mpT2", [128, D], BF16)
    py = [self.ps(f"mpy{i}", [128, 512]) for i in range(3)]
    pY = py[0]
    cnt = {}

    def nxt(key, n):
        v = cnt.get(key, 0)
        cnt[key] = v + 1
        return v % n
    H2Tv = self.H2T[l].rearrange("(kc p) t -> p kc t", p=128)
    work = []
    for (tag, src, dst, ntok, row, uoff) in streams:
        for t0 in range(0, ntok, 128):
            work.append((tag, src, dst, row, uoff + t0, t0))

    def stage1(w, i):
        (tag, src, dst, row, u, t0) = work[w]
        H, AT, OGt, OMt, XR, gt, ssq = hT[i], aTt[i], og[i], om[i], xr[i], gts[i], ssqs[i]
        S.dma(S.sp, [(H[:], H2Tv[:, :, u:u + 128])], reads=[self.R(self.H2T[l].name)], writes=[H.res])
        S.dma(S.sp, [(AT[:], self.ATT[l][:, :, u:u + 128])], reads=[self.R(self.ATT[l].name)], writes=[AT.res])
        S.dma(S.sp, [(OGt[:, 0, :], self.OG[l][0, u:u + 128, :]), (OGt[:, 1, :], self.OG[l][1, u:u + 128, :])], reads=[self.R(self.OG[l].name)], writes=[OGt.res])
        S.dma(S.sp, [(OMt[:, 0, :], self.OM[l][0, u:u + 128, :]), (OMt[:, 1, :], self.OM[l][1, u:u + 128, :])], reads=[self.R(self.OM[l].name)], writes=[OMt.res])
        S.dma(S.sp, [(XR[:], src[t0:t0 + 128, :])], reads=[self.R(src.name)], writes=[XR.res])
        for blk in range(8):
            p = pg[nxt("pg", 2)]
            for kc in range(8):
                S.op(S.pe, lambda h, kc=kc, blk=blk, p=p, H=H: h.matmul(p[:], H[:, kc, :], Wm[:, kc, blk * 512:(blk + 1) * 512], start=(kc == 0), stop=(kc == 7)),
                     reads=[H.res] + Wmr, writes=[p.res], inc=(kc == 7))
            fn = AF.Silu if blk == 0 else AF.Sigmoid
            S.op(S.act, lambda h, blk=blk, p=p, fn=fn, gt=gt: h.activation(out=gt[:, blk, :], in_=p[:], func=fn), reads=[p.res], writes=[gt.res])

    def stage1c(w, i):
        OGt, OMt, gt, ssq = og[i], om[i], gts[i], ssqs[i]
        for br, Ot in enumerate((OGt, OMt)):
            S.op(S.pool, lambda h, Ot=Ot: h.tensor_tensor(out=Ot[:, 0, :], in0=Ot[:, 0, :], in1=Ot[:, 1, :], op=ALU.add), reads=[Ot.res], writes=[Ot.res])
            S.op(S.act, lambda h, Ot=Ot: h.activation(out=sq[:], in_=Ot[:, 0, :], func=AF.Square), reads=[Ot.res], writes=[sq.res])
            S.op(S.dve, lambda h, br=br, ssq=ssq: h.tensor_reduce(out=ssq[:, br, :], in_=sq[:].rearrange("p (a b) -> p a b", b=128), axis=AX.X, op=ALU.add),
                 reads=[sq.res], writes=[ssq.res])
        S.op(S.act, lambda h, ssq=ssq: h.activation(out=ssq[:], in_=ssq[:], func=AF.Sqrt, scale=1.0 / 128, bias=eps_col[:]),
             reads=[ssq.res, eps_col.res], writes=[ssq.res])
        S.op(S.dve, lambda h, ssq=ssq: h.reciprocal(out=ssq[:], in_=ssq[:]), reads=[ssq.res], writes=[ssq.res])
        for br, Ot in enumerate((OGt, OMt)):
            B_ = bn[br]
            S.op(S.dve, lambda h, br=br, Ot=Ot, B_=B_, ssq=ssq: h.tensor_tensor(out=B_[:].rearrange("p (a b) -> p a b", b=128), in0=Ot[:, 0, :].rearrange("p (a b) -> p a b", b=128),
                                                                              in1=ssq[:, br, :].unsqueeze(2).to_broadcast([128, 4, 128]), op=ALU.mult),
                 reads=[Ot.res, ssq.res], writes=[B_.res])
            S.op(S.pool, lambda h, br=br, B_=B_: h.tensor_tensor(out=B_[:].rearrange("p (a b) -> p a b", b=128), in0=B_[:].rearrange("p (a b) -> p a b", b=128),
                                                              in1=gains[:, br:br + 1, :].to_broadcast([128, 4, 128]), op=ALU.mult),
                 reads=[B_.res, gains.res], writes=[B_.res])
            BB = bbs[i][br]
            S.op(S.dve, lambda h, br=br, B_=B_, BB=BB, gt=gt: h.tensor_tensor(out=BB[:], in0=B_[:], in1=gt[:, br, :], op=ALU.mult), reads=[B_.res, gt.res], writes=[BB.res])

    def stage1b(w, i):
        for br in range(2):
            BB = bbs[i][br]
            pT1 = pT1s[br]
            for c in range(4):
                S.op(S.pe, lambda h, c=c, BB=BB, pT1=pT1: h.transpose(out=pT1[:, c * 128:(c + 1) * 128], in_=BB[:, c * 128:(c + 1) * 128], identity=self.ident[:]),
                     reads=[BB.res, self.ident.res], writes=[pT1.res], inc=(c == 3))
            BT = bTs[i][br]
            if br == 0:
                S.op(S.act, lambda h, BT=BT, pT1=pT1: h.activation(out=BT[:].rearrange("p a b -> p (a b)"), in_=pT1[:], func=AF.Copy), reads=[pT1.res], writes=[BT.res])
            else:
                S.op(S.dve, lambda h, BT=BT, pT1=pT1: h.tensor_copy(out=BT[:].rearrange("p a b -> p (a b)"), in_=pT1[:]), reads=[pT1.res], writes=[BT.res])

    cur_row = [None]

    def stage2(w, i):
        (tag, src, dst, row, u, t0) = work[w]
        AT, XR, gt = aTt[i], xr[i], gts[i]
        if cur_row[0] != row:
            cur_row[0] = row
            srcg = self.MOD[l][row:row + 1, 5 * D:6 * D]
            S.dma(S.sp, [(G5[:], dram_ap(srcg, srcg.offset, [[0, 128], [1, D]]))], reads=[self.R("MOD", l)], writes=[G5.res])
        for half in range(2):
            cs_ = slice(half * 512, (half + 1) * 512)
            for hh in range(8):
                S.op(S.pe, lambda h, hh=hh, AT=AT, cs_=cs_: h.matmul(py[0][:], AT[:, hh, :], Woa[:, hh, cs_], start=(hh == 0), stop=(hh == 7)),
                     reads=[AT.res, Woa.res], writes=[py[0].res], inc=(hh == 7))
            for c in range(4):
                S.op(S.pe, lambda h, c=c, cs_=cs_, BT=bTs[i][0]: h.matmul(py[1][:], BT[:, c, :], Wog[:, c, cs_], start=(c == 0), stop=(c == 3)),
                     reads=[bTs[i][0].res, Wog.res], writes=[py[1].res], inc=(c == 3))
            for c in range(4):
                S.op(S.pe, lambda h, c=c, cs_=cs_, BT=bTs[i][1]: h.matmul(py[2][:], BT[:, c, :], Wom[:, c, cs_], start=(c == 0), stop=(c == 3)),
                     reads=[bTs[i][1].res, Wom.res], writes=[py[2].res], inc=(c == 3))
            S.op(S.dve, lambda h, half=half, gt=gt: h.tensor_tensor(out=t1[0][:], in0=py[0][:], in1=gt[:, 2 + half, :], op=ALU.mult), reads=[py[0].res, gt.res], writes=[t1[0].res])
            S.op(S.dve, lambda h, half=half, gt=gt: h.tensor_tensor(out=t1[1][:], in0=py[1][:], in1=gt[:, 4 + half, :], op=ALU.mult), reads=[py[1].res, gt.res], writes=[t1[1].res])
            S.op(S.dve, lambda h, half=half, gt=gt: h.tensor_tensor(out=t1[2][:], in0=py[2][:], in1=gt[:, 6 + half, :], op=ALU.mult), reads=[py[2].res, gt.res], writes=[t1[2].res])
            S.op(S.pool, lambda h: h.tensor_tensor(out=t1[0][:], in0=t1[0][:], in1=t1[1][:], op=ALU.add), reads=[t1[0].res, t1[1].res], writes=[t1[0].res])
            S.op(S.pool, lambda h, cs_=cs_: h.tensor_tensor(out=yb[:, cs_], in0=t1[0][:], in1=t1[2][:], op=ALU.add), reads=[t1[0].res, t1[2].res], writes=[yb.res])
        for kc in range(8):
            S.op(S.pe, lambda h, kc=kc: h.transpose(out=pT2[:, kc * 128:(kc + 1) * 128], in_=yb[:, kc * 128:(kc + 1) * 128], identity=self.ident[:]),
                 reads=[yb.res, self.ident.res], writes=[pT2.res], inc=(kc == 7))
        S.op(S.act, lambda h: h.activation(out=yT[:].rearrange("p a b -> p (a b)"), in_=pT2[:], func=AF.Copy), reads=[pT2.res], writes=[yT.res])
        for half in range(2):
            cs_ = slice(half * 512, (half + 1) * 512)
            for kc in range(8):
                S.op(S.pe, lambda h, kc=kc, cs_=cs_: h.matmul(pY[:], yT[:, kc, :], Wo[:, kc, cs_], start=(kc == 0), stop=(kc == 7)),
                     reads=[yT.res, Wo.res], writes=[pY.res], inc=(kc == 7))
            S.op(S.dve, lambda h, cs_=cs_: h.tensor_tensor(out=t1[0][:], in0=pY[:], in1=G5[:, cs_], op=ALU.mult), reads=[pY.res, G5.res], writes=[t1[0].res])
            S.op(S.pool, lambda h, cs_=cs_, XR=XR: h.tensor_tensor(out=XR[:, cs_], in0=XR[:, cs_], in1=t1[0][:], op=ALU.add), reads=[XR.res, t1[0].res], writes=[XR.res])
        S.dma(S.sp, [(dst[t0:t0 + 128, :], XR[:])], reads=[XR.res], writes=[self.R(dst.name)])

    stage1(0, 0)
    stage1c(0, 0)
    stage1b(0, 0)
    for w in range(len(work)):
        if w + 1 < len(work):
            stage1(w + 1, (w + 1) % 2)
        stage2(w, w % 2)
        if w + 1 < len(work):
            stage1c(w + 1, (w + 1) % 2)
            stage1b(w + 1, (w + 1) % 2)


Builder.phase_merge = phase_merge

_NC_CACHE = {}


def kernel(**inputs):
    inp = {k: np.asarray(v) for k, v in inputs.items()}
    Bsz, SEQ, _ = inp["x"].shape
    T = SEQ
    if T not in _NC_CACHE:
        _NC_CACHE[T] = Builder(T).build()
    nc = _NC_CACHE[T]
    in_maps = [make_in_map(inp, b, 0, T) for b in range(Bsz)]
    res = run_bass_kernel_spmd(nc, in_maps, core_ids=list(range(Bsz)))
    out = np.stack([np.asarray(r["y"], dtype=np.float32) for r in res.results], axis=0)
    return out


W_NAMES = ["mod_w", "mod_b", "norm_g", "ffn1_w13", "ffn1_w2", "ffn2_w13", "ffn2_w2", "w_in", "attn_q_norm", "attn_k_norm", "attn_sink", "gla_w2", "gla_b", "mlstm_conv_w", "mlstm_conv_b", "mlstm_ib", "mlstm_fb", "gla_norm", "mlstm_norm", "w_out_attn", "w_out_gla", "w_out_mlstm", "w_o"]


def make_in_map(inp, b, t0, T):
    m = {"x": np.ascontiguousarray(inp["x"][b, t0:t0 + T]), "c": np.ascontiguousarray(inp["c"][b]),
         "ctx": np.ascontiguousarray(inp["ctx"][b]), "c_ctx": np.ascontiguousarray(inp["c_ctx"])}
    for k in W_NAMES:
        m[k] = np.ascontiguousarray(inp[k])
    m["rope_cs"] = rope_table(t0, T)
    return m


def rope_table(t0, T):
    pos = np.arange(t0, t0 + T)
    r = (pos // 64).astype(np.float32)
    col = (pos % 64).astype(np.float32)
    inv = (np.float32(10000.0) ** (-np.arange(16, dtype=np.float32) / np.float32(16))).astype(np.float32)
    ang = np.concatenate([r[:, None] * inv, col[:, None] * inv], axis=-1).astype(np.float32)
    return np.ascontiguousarray(np.stack([np.cos(ang), np.sin(ang)], axis=1).astype(np.float32))
```

```python
import numpy as np
from contextlib import ExitStack
import concourse.bass as bass
import concourse.mybir as mybir
from concourse.bass_utils import run_bass_kernel_spmd

F32 = mybir.dt.float32
BF16 = mybir.dt.bfloat16
AF = mybir.ActivationFunctionType
ALU = mybir.AluOpType
AX = mybir.AxisListType

D = 1024
DFF = 2816
NMOD = 9
LC = 256
EPS = 1e-6
DEPTH = 2
D_IN = 7472


class Res:
    __slots__ = ("name", "w", "r")

    def __init__(self, name=""):
        self.name = name
        self.w = None
        self.r = []


class Eng:
    def __init__(self, name, is_pe=False):
        self.name = name
        self.is_pe = is_pe
        self.ops = []
        self.sems = []
        self.si = 0
        self.cnt = 0
        self.seen = {}
        self.pend_r = []
        self.pend_w = []
        self.pool = []
        self.pi = 0


ROT = 30000


class Sched:
    def __init__(self, nc, es):
        self.nc = nc
        self.es = es
        self.pe = Eng("pe", True)
        self.act = Eng("act")
        self.dve = Eng("dve")
        self.pool = Eng("pool")
        self.sp = Eng("sp")
        self.engs = [self.pe, self.act, self.dve, self.pool, self.sp]
        self.semid = {}
        n_rot = {"pe": 6, "act": 3, "dve": 3, "pool": 3, "sp": 1}
        for e in self.engs:
            for i in range(n_rot[e.name]):
                s = es.enter_context(nc.semaphore(f"s_{e.name}{i}"))
                e.sems.append(s)
        for e, n in ((self.sp, 20), (self.pool, 10), (self.act, 4)):
            for i in range(n):
                s = es.enter_context(nc.semaphore(f"d_{e.name}{i}"))
                e.pool.append([s, 0])
        self.n_ops = 0

    def _need(self, eng, tok, raw):
        if tok is None:
            return None
        sem, val, owner = tok
        if owner == eng.name:
            if eng.is_pe:
                return None
            if not raw:
                return None
        key = id(sem)
        if eng.seen.get(key, 0) >= val:
            return None
        eng.seen[key] = val
        return (sem, val)

    def _waits(self, eng, reads, writes):
        ws = []
        for r in reads:
            w = self._need(eng, r.w, True)
            if w:
                ws.append(w)
        for wr in writes:
            w = self._need(eng, wr.w, False)
            if w:
                ws.append(w)
            for t in wr.r:
                w = self._need(eng, t, False)
                if w:
                    ws.append(w)
        for (sem, val) in ws:
            eng.ops.append(lambda h, sem=sem, val=val: h.wait_ge(sem, val))

    def _record(self, tok, reads, writes):
        for r in reads:
            r.r = [t for t in r.r if t[2] != tok[2] or t[0] is not tok[0]] + [tok]
        for w in writes:
            w.w = tok
            w.r = []

    def op(self, eng, fn, reads=(), writes=(), inc=True):
        self.n_ops += 1
        reads = list(reads)
        writes = list(writes)
        self._waits(eng, reads, writes)
        if not inc:
            eng.ops.append(lambda h, fn=fn: fn(h))
            eng.pend_r += reads
            eng.pend_w += writes
            return
        if eng.cnt >= ROT:
            eng.si += 1
            eng.cnt = 0
        eng.cnt += 1
        sem = eng.sems[eng.si]
        tok = (sem, eng.cnt, eng.name)
        eng.ops.append(lambda h, fn=fn, sem=sem: fn(h).then_inc(sem, 1))
        self._record(tok, reads + eng.pend_r, writes + eng.pend_w)
        eng.pend_r = []
        eng.pend_w = []

    def dma(self, eng, pairs, reads=(), writes=(), **kw):
        self.n_ops += 1
        reads = list(reads)
        writes = list(writes)
        self._waits(eng, reads, writes)
        ent = eng.pool[eng.pi]
        eng.pi = (eng.pi + 1) % len(eng.pool)
        sem = ent[0]
        if ent[1] > 0 and eng.seen.get(id(sem), 0) < ent[1]:
            v = ent[1]
            eng.ops.append(lambda h, sem=sem, v=v: h.wait_ge(sem, v))
            eng.seen[id(sem)] = v
        for (o, i) in pairs:
            ent[1] += 16
            eng.ops.append(lambda h, o=o, i=i, sem=sem: h.dma_start(out=o, in_=i, **kw).then_inc(sem, 16))
        tok = (sem, ent[1], "dma_" + eng.name + str(id(sem)))
        self._record(tok, reads, writes)

    def barrier(self):
        toks = []
        for e in self.engs:
            assert not e.pend_r and not e.pend_w, e.name
            for i in range(e.si + 1):
                v = ROT if i < e.si else e.cnt
                if v > 0:
                    toks.append((e, e.sems[i], v))
            for ent in e.pool:
                if ent[1] > 0:
                    toks.append((None, ent[0], ent[1]))
        for e in self.engs:
            for (own, sem, v) in toks:
                if own is e:
                    continue
                if e.seen.get(id(sem), 0) >= v:
                    continue
                e.seen[id(sem)] = v
                e.ops.append(lambda h, sem=sem, v=v: h.wait_ge(sem, v))

    def finish(self):
        for e in (self.sp, self.pool, self.act):
            for ent in e.pool:
                if ent[1] > 0:
                    self.sp.ops.append(lambda h, sem=ent[0], v=ent[1]: h.wait_ge(sem, v))

    def replay(self):
        nc = self.nc
        with nc.Block() as block:
            @block.tensor
            def _(h):
                for f in self.pe.ops:
                    f(h)

            @block.scalar
            def _(h):
                for f in self.act.ops:
                    f(h)

            @block.vector
            def _(h):
                for f in self.dve.ops:
                    f(h)

            @block.gpsimd
            def _(h):
                for f in self.pool.ops:
                    f(h)

            @block.sync
            def _(h):
                for f in self.sp.ops:
                    f(h)


class Tile:
    def __init__(self, t, name):
        self.t = t
        self.res = Res(name)

    def __getitem__(self, k):
        return self.t[k]


def dram_ap(t, offset, pattern):
    return bass.AP(t.tensor, offset, pattern)


class Builder:
    def __init__(self, T, depth=DEPTH, stop=None, dbg=()):
        self.T = T
        self.depth = depth
        self.stop = stop
        self.dbg = dbg
        self.nc = bass.Bass("TRN2", target_bir_lowering=False)
        self.es = ExitStack()
        self.S = None
        self.dres = {}
        self.rr = {}

    def din(self, name, shape):
        return self.nc.dram_tensor(name, list(shape), F32, kind="ExternalInput").ap()

    def dout(self, name, shape, dt=F32):
        return self.nc.dram_tensor(name, list(shape), dt, kind="ExternalOutput").ap()

    def dscr(self, name, shape, dt=F32):
        if name in self.dbg:
            return self.nc.dram_tensor(name, list(shape), dt, kind="ExternalOutput").ap()
        return self.nc.dram_tensor(name, list(shape), dt).ap()

    def R(self, *key):
        if key not in self.dres:
            self.dres[key] = Res(str(key))
        return self.dres[key]

    def sb(self, name, shape, dt):
        self.uid = getattr(self, "uid", 0) + 1
        name = f"{name}_{self.uid}"
        t = self.cur.enter_context(self.nc.sbuf_tensor(name, list(shape), dt))
        return Tile(t, name)

    def ps(self, name, shape, dt=F32):
        self.uid = getattr(self, "uid", 0) + 1
        name = f"{name}_{self.uid}"
        t = self.cur.enter_context(self.nc.psum_tensor(name, list(shape), dt))
        return Tile(t, name)

    def build(self):
        nc = self.nc
        T = self.T
        L = self.depth
        with self.es as es:
            self.S = S = Sched(nc, es)
            self.x_in = self.din("x", [T, D])
            self.c_in = self.din("c", [D])
            self.ctx_in = self.din("ctx", [LC, D])
            self.cctx_in = self.din("c_ctx", [D])
            self.mod_w = self.din("mod_w", [L, D, NMOD * D])
            self.mod_b = self.din("mod_b", [L, NMOD * D])
            self.norm_g = self.din("norm_g", [L, 3, D])
            self.ffn_w13 = [self.din("ffn1_w13", [L, D, 2 * DFF]), self.din("ffn2_w13", [L, D, 2 * DFF])]
            self.ffn_w2 = [self.din("ffn1_w2", [L, DFF, D]), self.din("ffn2_w2", [L, DFF, D])]
            self.w_in = self.din("w_in", [L, D, D_IN])
            self.attn_q_norm = self.din("attn_q_norm", [L, 64])
            self.attn_k_norm = self.din("attn_k_norm", [L, 64])
            self.attn_sink = self.din("attn_sink", [L, 8])
            self.gla_w2 = self.din("gla_w2", [L, 2, 16, 256])
            self.gla_b = self.din("gla_b", [L, 2, 256])
            self.rope_cs = self.din("rope_cs", [T, 2, 32])
            self.y_out = self.dout("y", [T, D])
            TT = self.TT = LC + T
            self.H2T = [self.dscr(f"H2T{l}", [D, TT], BF16) for l in range(L)]
            self.QT = [self.dscr(f"QT{l}", [64, 8, TT], BF16) for l in range(L)]
            self.KT = [self.dscr(f"KT{l}", [64, 2, TT], BF16) for l in range(L)]
            self.VA = [self.dscr(f"VA{l}", [TT, 128], BF16) for l in range(L)]
            self.GV = [self.dscr(f"GV{l}", [TT, 512], BF16) for l in range(L)]
            self.MV = [self.dscr(f"MV{l}", [TT, 512], BF16) for l in range(L)]
            self.GATES = [self.dscr(f"GATES{l}", [16, TT]) for l in range(L)]
            self.QG = [self.dscr(f"QG{l}", [2, 64, 4, TT], BF16) for l in range(L)]
            self.KG = [self.dscr(f"KG{l}", [2, 64, 4, TT], BF16) for l in range(L)]
            self.KH = [self.dscr(f"KH{l}", [2, 64, 4, TT], BF16) for l in range(L)]
            self.MQK = [self.dscr(f"MQK{l}", [D, TT + 4], BF16) for l in range(L)]
            self.ATT = [self.dscr(f"ATT{l}", [64, 8, TT], BF16) for l in range(L)]
            self.OG = [self.dscr(f"OG{l}", [2, TT, 512]) for l in range(L)]
            self.OM = [self.dscr(f"OM{l}", [2, TT, 512]) for l in range(L)]
            self.MROWS = [self.dscr(f"MROWS{l}", [2, 5, 4, TT]) for l in range(L)]
            self.MQC = [self.dscr(f"MQC{l}", [D, TT], BF16) for l in range(L)]
            self.gla_norm = self.din("gla_norm", [L, 128])
            self.mlstm_norm = self.din("mlstm_norm", [L, 128])
            self.w_out_attn = self.din("w_out_attn", [L, 512, D])
            self.w_out_gla = self.din("w_out_gla", [L, 512, D])
            self.w_out_mlstm = self.din("w_out_mlstm", [L, 512, D])
            self.w_o = self.din("w_o", [L, D, D])
            self.X2 = [self.dscr(f"X2_{l}", [T, D]) for l in range(L)]
            self.C2 = [self.dscr(f"C2_{l}", [LC, D]) for l in range(L)]
            self.X3 = [self.dscr(f"X3_{l}", [T, D]) for l in range(L)]
            self.C3 = [self.dscr(f"C3_{l}", [LC, D]) for l in range(L)]
            self.conv_w = self.din("mlstm_conv_w", [L, 5, D])
            self.conv_b = self.din("mlstm_conv_b", [L, D])
            self.mlstm_ib = self.din("mlstm_ib", [L, 2, 4])
            self.mlstm_fb = self.din("mlstm_fb", [L, 2, 4])
            self.MOD = [self.dscr(f"MOD{l}", [2, NMOD * D]) for l in range(L)]
            self.X1 = [self.dscr(f"X1_{l}", [T, D]) for l in range(L)]
            self.C1 = [self.dscr(f"C1_{l}", [LC, D]) for l in range(L)]
            with ExitStack() as cst:
                self.cur = cst
                self.ident = self.sb("ident", [128, 128], BF16)
                self.identf = self.sb("identf", [128, 128], F32)
                self.eps_col = self.sb("eps_col", [128, 1], F32)
                S.op(S.dve, lambda h: h.memset(self.eps_col[:], EPS), writes=[self.eps_col.res])
                self.make_consts()
                for l in range(L):
                    xin = self.x_in if l == 0 else self.X3[l - 1]
                    cin = self.ctx_in if l == 0 else self.C3[l - 1]
                    with ExitStack() as ph:
                        self.cur = ph
                        self.phase_mod(l)
                        S.barrier()
                    if self.stop == ("mod", l):
                        break
                    with ExitStack() as ph:
                        self.cur = ph
                        self.phase_ffn(l, 0, [("ctx", cin, self.C1[l], LC, 1), ("lat", xin, self.X1[l], T, 0)])
                        S.barrier()
                    if self.stop == ("ffn1", l):
                        break
                    with ExitStack() as lay:
                        self.cur = lay
                        self.EL = self.sb("EL", [64, 4, 2, TT // 64], F32)
                        with ExitStack() as ph:
                            self.cur = ph
                            self.phase_feat(l, [("ctx", self.C1[l], LC, 1, 0, False), ("lat", self.X1[l], T, 0, LC, True)])
                            S.barrier()
                        if self.stop == ("feat", l):
                            break
                        with ExitStack() as ph:
                            self.cur = ph
                            self.phase_attn(l, l < L - 1)
                            S.barrier()
                        if self.stop == ("attn", l):
                            break
                        with ExitStack() as ph:
                            self.cur = ph
                            self.phase_gla(l)
                            S.barrier()
                        if self.stop == ("gla", l):
                            break
                        self.cur = lay
                        self.DEC = self.sb("DEC", [128, 2, 4, TT // 64], F32)
                        self.sel = self.sb("sel", [4, 4, 128], F32)
                        S.op(S.dve, lambda h, sel_t=self.sel: h.tensor_copy(out=sel_t[:], in_=self.identf[0:4, 0:4].unsqueeze(2).to_broadcast([4, 4, 128])),
                             reads=[self.identf.res], writes=[self.sel.res])
                        stop_ml = False
                        for ph_name, ph_fn in (("mlg", self.phase_ml_gates), ("mlc", self.phase_ml_conv), ("mls", self.phase_ml_scan)):
                            with ExitStack() as ph:
                                self.cur = ph
                                ph_fn(l)
                                S.barrier()
                            if self.stop == (ph_name, l):
                                stop_ml = True
                                break
                        if stop_ml:
                            break
                        if self.stop == ("ml", l):
                            break
                    last = (l == L - 1)
                    with ExitStack() as ph:
                        self.cur = ph
                        st = [("lat", self.X1[l], self.X2[l], T, 0, LC)]
                        if not last:
                            st = [("ctx", self.C1[l], self.C2[l], LC, 1, 0)] + st
                        self.phase_merge(l, st)
                        S.barrier()
                    if self.stop == ("merge", l):
                        break
                    with ExitStack() as ph:
                        self.cur = ph
                        xdst = self.y_out if last else self.X3[l]
                        st = [("lat", self.X2[l], xdst, T, 0)]
                        import os
                        if not last and not os.environ.get("NOCTX2"):
                            st = [("ctx", self.C2[l], self.C3[l], LC, 1)] + st
                        self.phase_ffn(l, 1, [(a, b, c, d_, e) for (a, b, c, d_, e) in st])
                        S.barrier()
                    if self.stop == ("ffn2", l):
                        break
                S.finish()
                S.replay()
        return nc

    def make_consts(self):
        S = self.S
        nc = self.nc
        idf = self.identf
        S.op(S.pool, lambda h: h.memset(idf[:], 0.0), writes=[idf.res])
        S.op(S.pool, lambda h: h.affine_select(out=idf[:], in_=idf[:], pattern=[[-1, 128]],
                                                compare_op=ALU.not_equal, fill=1.0, base=0,
                                                channel_multiplier=1),
             reads=[idf.res], writes=[idf.res])
        S.op(S.dve, lambda h: h.tensor_copy(out=self.ident[:], in_=idf[:]), reads=[idf.res], writes=[self.ident.res])

    def phase_mod(self, l):
        S = self.S
        cl = self.sb("cl", [128, 8, 2], F32)
        cs = self.sb("cs", [128, 8, 2], F32)
        S.dma(S.sp, [(cl[:, :, 0], self.c_in.rearrange("(kc p) -> p kc", p=128)),
                     (cl[:, :, 1], self.cctx_in.rearrange("(kc p) -> p kc", p=128))],
              writes=[cl.res], allow_slow_non_contiguous=True)
        S.op(S.act, lambda h: h.activation(out=cs[:], in_=cl[:], func=AF.Silu), reads=[cl.res], writes=[cs.res])
        wm = [self.sb(f"wm{i}", [128, 8, 512], F32) for i in range(2)]
        mb = [self.sb(f"mb{i}", [2, 512], F32) for i in range(2)]
        mo = [self.sb(f"mo{i}", [2, 512], F32) for i in range(2)]
        pm = [self.ps(f"pm{i}", [2, 512]) for i in range(2)]
        mw = self.mod_w[l].rearrange("(kc p) n -> p kc n", p=128)
        for n in range(18):
            i = n % 2
            S.dma(S.sp, [(wm[i][:, 0:4, :], mw[:, 0:4, n * 512:(n + 1) * 512]),
                         (wm[i][:, 4:8, :], mw[:, 4:8, n * 512:(n + 1) * 512])], writes=[wm[i].res])
            mbsrc = self.mod_b[l:l + 1, n * 512:(n + 1) * 512]
            S.dma(S.sp, [(mb[i][0:1, :], mbsrc), (mb[i][1:2, :], mbsrc)], writes=[mb[i].res])
            for kc in range(8):
                S.op(S.pe, lambda h, kc=kc, i=i: h.matmul(pm[i][:], cs[:, kc, :], wm[i][:, kc, :],
                                                           start=(kc == 0), stop=(kc == 7)),
                     reads=[cs.res, wm[i].res], writes=[pm[i].res], inc=(kc == 7))
            S.op(S.dve, lambda h, i=i: h.tensor_tensor(out=mo[i][:], in0=pm[i][:], in1=mb[i][:], op=ALU.add),
                 reads=[pm[i].res, mb[i].res], writes=[mo[i].res])
            S.dma(S.sp, [(self.MOD[l][:, n * 512:(n + 1) * 512], mo[i][:])], reads=[mo[i].res],
                  writes=[self.R("MOD", l)])

    def load_cols(self, dst_ap, src_row_ap, res):
        self.S.dma(self.S.sp, [(dst_ap, src_row_ap.rearrange("(kc p) -> p kc", p=128))], writes=[res],
                   allow_slow_non_contiguous=True)

    def adaln_cols(self, l, j, row, tag):
        S = self.S
        tmp = self.sb(f"adt_{tag}", [128, 3, 8], F32)
        A = self.sb(f"adA_{tag}", [128, 8], F32)
        MODr = self.MOD[l]
        S.dma(S.sp, [(tmp[:, 0, :], MODr[row, (3 * j) * D:(3 * j + 1) * D].rearrange("(kc p) -> p kc", p=128)),
                     (tmp[:, 1, :], MODr[row, (3 * j + 1) * D:(3 * j + 2) * D].rearrange("(kc p) -> p kc", p=128)),
                     (tmp[:, 2, :], self.norm_g[l, j, :].rearrange("(kc p) -> p kc", p=128))],
              reads=[self.R("MOD", l)], writes=[tmp.res], allow_slow_non_contiguous=True)
        S.op(S.dve, lambda h: h.scalar_tensor_tensor(out=A[:], in0=tmp[:, 1, :], scalar=1.0, in1=tmp[:, 2, :],
                                                      op0=ALU.add, op1=ALU.mult),
             reads=[tmp.res], writes=[A.res])
        return A, tmp

    def gate_bc(self, l, j, row, tag, mul):
        S = self.S
        G = self.sb(f"gate_{tag}", [128, D], F32)
        src = self.MOD[l][row:row + 1, (3 * j + 2) * D:(3 * j + 3) * D]
        src_b = dram_ap(src, src.offset, [[0, 128], [1, D]])
        S.dma(S.sp, [(G[:], src_b)], reads=[self.R("MOD", l)], writes=[G.res])
        if mul != 1.0:
            S.op(S.pool, lambda h: h.tensor_scalar(out=G[:], in0=G[:], scalar1=float(mul), scalar2=None, op0=ALU.mult),
                 reads=[G.res], writes=[G.res])
        return G

    def load_weight_bf16(self, dst, src3, nsplit):
        S = self.S
        kcn = dst.t.shape[1]
        step = max(1, kcn // nsplit)
        dst.parts = []
        for k0 in range(0, kcn, step):
            k1 = min(kcn, k0 + step)
            r = Res(f"wpart{k0}")
            dst.parts.append(r)
            S.dma(S.pool, [(dst[:, k0:k1, :], src3[:, k0:k1, :])], writes=[r])

    def norm_part(self, xt, nb, ss, rs, junk=None):
        S = self.S
        if junk is None:
            junk = self.junk
        S.op(S.act, lambda h: h.activation(out=junk[:], in_=xt[:], func=AF.Square, accum_out=ss[:]),
             reads=[xt.res], writes=[junk.res, ss.res])
        S.op(S.act, lambda h: h.activation(out=rs[:], in_=ss[:], func=AF.Sqrt, scale=1.0 / D, bias=self.eps_col[:]),
             reads=[ss.res], writes=[rs.res])
        S.op(S.dve, lambda h: h.reciprocal(out=rs[:], in_=rs[:]), reads=[rs.res], writes=[rs.res])
        S.op(S.dve, lambda h: h.tensor_scalar(out=nb[:], in0=xt[:], scalar1=rs[:], scalar2=None, op0=ALU.mult),
             reads=[xt.res, rs.res], writes=[nb.res])

    def transpose_part(self, nb, pT, hT, col0, A, sh, evac_engs):
        S = self.S
        for kc in range(8):
            S.op(S.pe, lambda h, kc=kc: h.transpose(out=pT[:, kc * 128:(kc + 1) * 128], in_=nb[:, kc * 128:(kc + 1) * 128],
                                                     identity=self.ident[:]),
                 reads=[nb.res, self.ident.res], writes=[pT.res], inc=(kc == 7))
        for kc in range(8):
            e = evac_engs[kc % len(evac_engs)]
            if e is S.act:
                S.op(e, lambda h, kc=kc: h.activation(out=hT[:, kc, col0:col0 + 128], in_=pT[:, kc * 128:(kc + 1) * 128],
                                                      func=AF.Identity, scale=A[:, kc:kc + 1], bias=sh[:, kc:kc + 1]),
                     reads=[pT.res, A.res, self.shres], writes=[hT.res])
            else:
                S.op(e, lambda h, kc=kc: h.tensor_scalar(out=hT[:, kc, col0:col0 + 128], in0=pT[:, kc * 128:(kc + 1) * 128],
                                                         scalar1=A[:, kc:kc + 1], scalar2=sh[:, kc:kc + 1],
                                                         op0=ALU.mult, op1=ALU.add),
                     reads=[pT.res, A.res, self.shres], writes=[hT.res])

    def phase_ffn(self, l, which, streams):
        S = self.S
        j = 0 if which == 0 else 2
        W13 = self.sb("W13", [128, 8, 2 * DFF], BF16)
        W2 = self.sb("W2", [128, 22, D], BF16)
        self.load_weight_bf16(W13, self.ffn_w13[which][l].rearrange("(kc p) n -> p kc n", p=128), 8)
        self.load_weight_bf16(W2, self.ffn_w2[which][l].rearrange("(fc p) n -> p fc n", p=128), 11)
        import os
        if os.environ.get("FFN_WONLY") and which == 1:
            return
        xl = [self.sb(f"xl{i}", [128, D], F32) for i in range(3)]
        xr = [self.sb(f"xr{i}", [128, D], F32) for i in range(2)]
        nb = [self.sb(f"nb{i}", [128, D], BF16) for i in range(4)]
        ss = [self.sb(f"ss{i}", [128, 1], F32) for i in range(4)]
        rs = [self.sb(f"rs{i}", [128, 1], F32) for i in range(4)]
        hT = self.sb("hT", [128, 8, 512], BF16)
        gT = self.sb("gT", [128, 22, 512], BF16)
        sa = [self.sb(f"sa{i}", [128, 512], F32) for i in range(2)]
        tt = [self.sb(f"tt{i}", [128, 512], F32) for i in range(2)]
        pT = [self.ps(f"pT{i}", [128, D], BF16) for i in range(2)]
        pA = [self.ps(f"pA{i}", [128, 512]) for i in range(2)]
        pB = [self.ps(f"pB{i}", [128, 512]) for i in range(2)]
        pY = [self.ps(f"pY{i}", [128, 512]) for i in range(2)]
        cnt = {"xl": 0, "xr": 0, "nb": 0, "pT": 0, "pAB": 0, "sa": 0, "tt": 0}

        for (tag, src, dst, ntok, row) in streams:
            A, tmp = self.adaln_cols(l, j, row, f"{which}{tag}")
            sh = tmp[:, 0, :]
            self.shres = tmp.res
            G = self.gate_bc(l, j, row, f"{which}{tag}", 0.5)
            tiles = [(t0, min(512, ntok - t0)) for t0 in range(0, ntok, 512)]
            rtag = ("xs", l, which, tag)

            def prep_norm(t0, s):
                i = cnt["xl"] % 3
                cnt["xl"] += 1
                k = cnt["nb"] % 4
                cnt["nb"] += 1
                S.dma(S.sp, [(xl[i][:], src[t0 + s * 128:t0 + (s + 1) * 128, :])], reads=[self.R(src.name)],
                      writes=[xl[i].res])
                self.norm_part(xl[i], nb[k], ss[k], rs[k], junk=nb[k])
                return nb[k]

            def prep_tr(nbt, s):
                k = cnt["pT"] % 2
                cnt["pT"] += 1
                self.transpose_part(nbt, pT[k], hT, s * 128, A, sh, [S.act, S.dve])

            def prep(t0, n):
                for s in range(n // 128):
                    nbt = prep_norm(t0, s)
                    prep_tr(nbt, s)

            prep(*tiles[0])
            for ti, (t0, n) in enumerate(tiles):
                nt = n // 128
                for p in range(22):
                    k = cnt["pAB"] % 2
                    cnt["pAB"] += 1
                    for kc in range(8):
                        S.op(S.pe, lambda h, kc=kc, p=p, k=k, n=n: h.matmul(pA[k][:, :n], W13[:, kc, p * 128:(p + 1) * 128], hT[:, kc, :n],
                                                                        start=(kc == 0), stop=(kc == 7)),
                             reads=W13.parts + [hT.res], writes=[pA[k].res], inc=(kc == 7))
                    for kc in range(8):
                        S.op(S.pe, lambda h, kc=kc, p=p, k=k, n=n: h.matmul(pB[k][:, :n], W13[:, kc, DFF + p * 128:DFF + (p + 1) * 128], hT[:, kc, :n],
                                                                        start=(kc == 0), stop=(kc == 7)),
                             reads=W13.parts + [hT.res], writes=[pB[k].res], inc=(kc == 7))
                    q = cnt["sa"] % 2
                    cnt["sa"] += 1
                    S.op(S.act, lambda h, k=k, q=q, n=n: h.activation(out=sa[q][:, :n], in_=pA[k][:, :n], func=AF.Silu),
                         reads=[pA[k].res], writes=[sa[q].res])
                    S.op(S.dve, lambda h, k=k, q=q, p=p, n=n: h.tensor_tensor(out=gT[:, p, :n], in0=sa[q][:, :n], in1=pB[k][:, :n], op=ALU.mult),
                         reads=[sa[q].res, pB[k].res], writes=[gT.res])
                xrs = []
                for s in range(nt):
                    pass
                if ti + 1 < len(tiles):
                    pending = tiles[ti + 1]
                else:
                    pending = None
                nbts = []
                if pending is not None:
                    for s in range(pending[1] // 128):
                        nbts.append(prep_norm(pending[0], s))
                for s in range(nt):
                    i = cnt["xr"] % 2
                    cnt["xr"] += 1
                    S.dma(S.sp, [(xr[i][:], src[t0 + s * 128:t0 + (s + 1) * 128, :])], reads=[self.R(src.name)],
                          writes=[xr[i].res])
                    for dh in range(2):
                        for fc in range(22):
                            S.op(S.pe, lambda h, fc=fc, dh=dh, s=s: h.matmul(pY[dh][:], gT[:, fc, s * 128:(s + 1) * 128], W2[:, fc, dh * 512:(dh + 1) * 512],
                                                                              start=(fc == 0), stop=(fc == 21)),
                                 reads=[gT.res] + W2.parts, writes=[pY[dh].res], inc=(fc == 21))
                    for dh in range(2):
                        q = cnt["tt"] % 2
                        cnt["tt"] += 1
                        S.op(S.dve, lambda h, dh=dh, q=q, G=G: h.tensor_tensor(out=tt[q][:], in0=pY[dh][:], in1=G[:, dh * 512:(dh + 1) * 512], op=ALU.mult),
                             reads=[pY[dh].res, G.res], writes=[tt[q].res])
                        S.op(S.pool, lambda h, dh=dh, q=q, i=i: h.tensor_tensor(out=xr[i][:, dh * 512:(dh + 1) * 512], in0=xr[i][:, dh * 512:(dh + 1) * 512],
                                                                                 in1=tt[q][:], op=ALU.add),
                             reads=[tt[q].res, xr[i].res], writes=[xr[i].res])
                    S.dma(S.pool, [(dst[t0 + s * 128:t0 + (s + 1) * 128, :], xr[i][:])], reads=[xr[i].res],
                          writes=[self.R(dst.name)])
                for s, nbt in enumerate(nbts):
                    prep_tr(nbt, s)


O_AQ, O_AK, O_AV = 0, 512, 640
O_GQ, O_GK, O_GV, O_GR, O_GG = 768, 1024, 1280, 1792, 2304
O_MQ, O_MK, O_MV, O_MO, O_MI, O_MF = 2336, 2848, 3360, 3872, 4384, 4392
O_SA, O_SG, O_SM = 4400, 5424, 6448


def bcast_rows(ap2d, nparts):
    return bass.AP(ap2d.tensor, ap2d.offset, [[0, nparts]] + [list(x) for x in ap2d.ap[1:]])


def rev_last(ap):
    pat = [list(x) for x in ap.ap]
    st, n = pat[-1]
    return bass.AP(ap.tensor, ap.offset + st * (n - 1), pat[:-1] + [[-st, n]])


def phase_feat(self, l, streams):
    S = self.S
    TT = self.TT
    win = self.w_in[l].rearrange("(kc p) n -> p kc n", p=128)
    Wa = self.sb("Wa", [128, 8, 768], BF16)
    Wv = self.sb("Wv", [128, 8, 1024], BF16)
    Wf = self.sb("Wf", [128, 8, 1536], BF16)
    Wg = self.sb("Wg", [128, 8, 48], BF16)
    S.dma(S.pool, [(Wa[:, 0:4, :], win[:, 0:4, 0:768]), (Wa[:, 4:8, :], win[:, 4:8, 0:768])], writes=[Wa.res])
    for k0 in range(0, 8, 2):
        S.dma(S.pool, [(Wv[:, k0:k0 + 2, 0:512], win[:, k0:k0 + 2, O_GV:O_GV + 512]),
                       (Wv[:, k0:k0 + 2, 512:1024], win[:, k0:k0 + 2, O_MV:O_MV + 512])], writes=[Wv.res])
        S.dma(S.pool, [(Wf[:, k0:k0 + 2, 0:512], win[:, k0:k0 + 2, O_GQ:O_GQ + 512]),
                       (Wf[:, k0:k0 + 2, 512:1536], win[:, k0:k0 + 2, O_MQ:O_MQ + 1024])], writes=[Wf.res])
    S.dma(S.pool, [(Wg[:, :, 0:32], win[:, :, O_GG:O_GG + 32]), (Wg[:, :, 32:48], win[:, :, O_MI:O_MI + 16])], writes=[Wg.res])
    W2p = self.sb("W2p", [32, 2, 256], F32)
    S.op(S.dve, lambda h: h.memset(W2p[:], 0.0), writes=[W2p.res])
    S.dma(S.sp, [(W2p[0:16, 0, :], self.gla_w2[l, 0]), (W2p[16:32, 1, :], self.gla_w2[l, 1])], writes=[W2p.res])
    negb = self.sb("negb", [128, 2, 2], F32)
    S.dma(S.sp, [(negb[:, d, :], self.gla_b[l, d, :].rearrange("(c p) -> p c", p=128)) for d in range(2)],
          writes=[negb.res], allow_slow_non_contiguous=True)
    S.op(S.dve, lambda h: h.tensor_scalar(out=negb[:], in0=negb[:], scalar1=-1.0, scalar2=None, op0=ALU.mult),
         reads=[negb.res], writes=[negb.res])
    gain = self.sb("gain", [128, 10, 64], F32)
    qn_src = self.attn_q_norm[l:l + 1, :]
    kn_src = self.attn_k_norm[l:l + 1, :]
    S.dma(S.sp, [(gain[:, 0:8, :], bass.AP(qn_src.tensor, qn_src.offset, [[0, 128], [0, 8], [1, 64]])),
                 (gain[:, 8:10, :], bass.AP(kn_src.tensor, kn_src.offset, [[0, 128], [0, 2], [1, 64]]))],
          writes=[gain.res])
    mask01 = self.sb("mask01", [128, 8, 64], F32)
    S.op(S.pool, lambda h: h.memset(mask01[:], 1.0), writes=[mask01.res])
    S.op(S.pool, lambda h: h.memset(mask01[:, :, 0:1], 0.0), writes=[mask01.res])
    self.junk = self.sb("junk", [128, D], BF16)

    xl = [self.sb(f"xl{i}", [128, D], F32) for i in range(3)]
    nb = [self.sb(f"nb{i}", [128, D], BF16) for i in range(2)]
    ss = [self.sb(f"ss{i}", [128, 1], F32) for i in range(2)]
    rs = [self.sb(f"rs{i}", [128, 1], F32) for i in range(2)]
    hTs = [self.sb(f"hT{i}", [128, 8, 512], BF16) for i in range(2)]
    sqt = self.sb("sqt", [128, 640], F32)
    ssh = self.sb("ssh", [128, 10], F32)
    rinv = self.sb("rinv", [128, 10], F32)
    qn = self.sb("qn", [128, 10, 64], F32)
    rt = [self.sb(f"rt{i}", [128, 10, 32], F32) for i in range(4)]
    cs_t = [self.sb(f"cst{i}", [128, 2, 32], F32) for i in range(3)]
    qr = [self.sb(f"qr{i}", [128, 10, 64], BF16) for i in range(2)]
    vb = [self.sb(f"vb{i}", [128, 128], BF16) for i in range(4)]
    vb2 = [self.sb(f"vb2{i}", [128, 512], BF16) for i in range(4)]
    QTs = self.sb("QTs", [64, 8, 512], BF16)
    KTs = self.sb("KTs", [64, 2, 512], BF16)
    ggT = self.sb("ggT", [32, 512], F32)
    gts = self.sb("gts", [16, 512], F32)
    ex = [self.sb(f"ex{i}", [128, 512], F32) for i in range(2)]
    csum = [self.sb(f"csum{i}", [128, 512], F32) for i in range(2)]
    eb = [[self.sb(f"eb{d}{c}", [128, 512], F32) for c in range(2)] for d in range(2)]
    enb = [[self.sb(f"enb{d}{c}", [128, 512], F32) for c in range(2)] for d in range(2)]
    ebl = [[self.sb(f"ebl{d}{c}", [128, 512], F32) for c in range(2)] for d in range(2)]
    fo = [self.sb(f"fo{i}", [128, 512], BF16) for i in range(8)]
    elcs = [self.sb(f"elc{i}", [128, 8], F32) for i in range(2)]
    pT = self.ps("pT", [128, D], BF16)
    pq = self.ps("pq", [128, 512])
    pkv = self.ps("pkv", [128, 256])
    pqt = self.ps("pqt", [64, 8, 128], BF16)
    pkt = self.ps("pkt", [64, 2, 128], BF16)
    pf = [self.ps(f"pf{i}", [128, 512]) for i in range(2)]
    pz = self.ps("pz", [128, 512])
    cnt = {"xl": 0, "pf": 0, "fo": 0, "qr": 0, "vb": 0, "vb2": 0, "ex": 0, "rt": 0, "cs": 0, "elc": 0}
    H2Tv = self.H2T[l].rearrange("(kc p) t -> p kc t", p=128)

    def nxt(key, n):
        v = cnt[key] % n
        cnt[key] += 1
        return v

    st_info = []
    for (tag, src, ntok, row, uoff, rope) in streams:
        A, tmp = self.adaln_cols(l, 1, row, f"f{tag}")
        st_info.append((A, tmp, src, uoff, rope))
    tiles = []
    for si, (tag, src, ntok, row, uoff, rope) in enumerate(streams):
        for t0 in range(0, ntok, 512):
            tiles.append((si, t0, min(512, ntok - t0)))

    def prep_load(k, s):
        si, t0, n = tiles[k]
        A, tmp, src, uoff, rope = st_info[si]
        i = nxt("xl", 3)
        S.dma(S.sp, [(xl[i][:], src[t0 + s * 128:t0 + (s + 1) * 128, :])], reads=[self.R(src.name)], writes=[xl[i].res])
        cst = None
        return i

    def rope_load(k, s):
        si, t0, n = tiles[k]
        A, tmp, src, uoff, rope = st_info[si]
        if not rope:
            return None
        cst = cs_t[nxt("cs", 3)]
        S.dma(S.sp, [(cst[:], self.rope_cs[t0 + s * 128:t0 + s * 128 + 128, :, :])], writes=[cst.res])
        return cst

    def prep_sub(k, s, i=None):
        si, t0, n = tiles[k]
        A, tmp, src, uoff, rope = st_info[si]
        hT = hTs[k % 2]
        if i is None:
            i = prep_load(k, s)
        self.norm_part(xl[i], nb[i % 2], ss[i % 2], rs[i % 2])
        self.shres = tmp.res
        self.transpose_part(nb[i % 2], pT, hT, s * 128, A, tmp[:, 0, :], [S.act, S.dve])

    def g_part(k):
        si, t0, n = tiles[k]
        A, tmp, src, uoff, rope = st_info[si]
        hT = hTs[k % 2]
        u0 = uoff + t0
        nch = n // 64
        for kc in range(8):
            S.op(S.pe, lambda h, kc=kc, n=n, hT=hT: h.matmul(pz[0:32, :n], Wg[:, kc, 0:32], hT[:, kc, :n], start=(kc == 0), stop=(kc == 7)),
                 reads=[hT.res, Wg.res], writes=[pz.res], inc=(kc == 7))
        S.op(S.act, lambda h, n=n: h.activation(out=ggT[:, :n], in_=pz[0:32, :n], func=AF.Copy), reads=[pz.res], writes=[ggT.res])
        for kc in range(8):
            S.op(S.pe, lambda h, kc=kc, n=n, hT=hT: h.matmul(pz[0:16, :n], Wg[:, kc, 32:48], hT[:, kc, :n], start=(kc == 0), stop=(kc == 7)),
                 reads=[hT.res, Wg.res], writes=[pz.res], inc=(kc == 7))
        S.op(S.act, lambda h, n=n: h.activation(out=gts[:, :n], in_=pz[0:16, :n], func=AF.Copy), reads=[pz.res], writes=[gts.res])
        S.dma(S.sp, [(self.GATES[l][:, u0:u0 + n], gts[:, :n])], reads=[gts.res], writes=[self.R(self.GATES[l].name)])
        for d in range(2):
            for c2 in range(2):
                S.op(S.pe, lambda h, d=d, c2=c2, n=n: h.matmul(pz[:, :n], W2p[:, d, c2 * 128:(c2 + 1) * 128], ggT[:, :n], start=True, stop=True),
                     reads=[W2p.res, ggT.res], writes=[pz.res])
                e_ = ex[nxt("ex", 2)]
                c_ = csum[(cnt["ex"]) % 2]
                S.op(S.act, lambda h, d=d, c2=c2, n=n, e_=e_: h.activation(out=e_[:, :n], in_=pz[:, :n], func=AF.Exp, scale=-1.0, bias=negb[:, d, c2:c2 + 1]),
                     reads=[pz.res, negb.res], writes=[e_.res])
                S.op(S.act, lambda h, n=n, e_=e_: h.activation(out=e_[:, :n], in_=e_[:, :n], func=AF.Ln, bias=1.0), reads=[e_.res], writes=[e_.res])
                m01 = mask01[:].rearrange("p a b -> p (a b)")[:, :n]
                if d == 0:
                    S.op(S.dve, lambda h, n=n, e_=e_, c_=c_, m01=m01: h.tensor_tensor_scan(out=c_[:, :n], data0=m01, data1=e_[:, :n], initial=0.0, op0=ALU.mult, op1=ALU.add),
                         reads=[e_.res, mask01.res], writes=[c_.res])
                    last = 63
                else:
                    S.op(S.dve, lambda h, n=n, e_=e_, c_=c_, m01=m01: h.tensor_tensor_scan(out=rev_last(c_[:, :n]), data0=m01, data1=rev_last(e_[:, :n]), initial=0.0,
                                                                                         op0=ALU.mult, op1=ALU.add),
                         reads=[e_.res, mask01.res], writes=[c_.res])
                    last = 0
                EB, ENB, EBL = eb[d][c2], enb[d][c2], ebl[d][c2]
                S.op(S.act, lambda h, n=n, c_=c_, EB=EB: h.activation(out=EB[:, :n], in_=c_[:, :n], func=AF.Exp, scale=-1.0 / 16), reads=[c_.res], writes=[EB.res])
                S.op(S.act, lambda h, n=n, c_=c_, ENB=ENB: h.activation(out=ENB[:, :n], in_=c_[:, :n], func=AF.Exp, scale=1.0 / 16), reads=[c_.res], writes=[ENB.res])
                c3 = c_[:, :n].rearrange("p (a b) -> p a b", b=64)
                S.op(S.pool, lambda h, n=n, c_=c_, c3=c3, last=last, nch=nch: h.tensor_tensor(out=c3, in0=c3, in1=c3[:, :, last:last + 1].to_broadcast([128, nch, 64]), op=ALU.subtract),
                     reads=[c_.res], writes=[c_.res])
                S.op(S.act, lambda h, n=n, c_=c_, EBL=EBL: h.activation(out=EBL[:, :n], in_=c_[:, :n], func=AF.Exp, scale=1.0 / 16), reads=[c_.res], writes=[EBL.res])
                ch0 = u0 // 64
                elc = elcs[nxt("elc", 2)]
                S.op(S.pool, lambda h, n=n, EB=EB, last=last, elc=elc, nch=nch: h.tensor_copy(
                    out=elc[:, :nch], in_=EB[:, :n].rearrange("p (a b) -> p a b", b=64)[:, :, last]),
                     reads=[EB.res], writes=[elc.res])
                S.dma(S.sp, [(self.EL[:, 2 * c2 + hh2, d, ch0:ch0 + nch], elc[hh2 * 64:(hh2 + 1) * 64, :nch]) for hh2 in range(2)],
                      reads=[elc.res], writes=[self.EL.res])

    def a_mm(k, s):
        hT = hTs[k % 2]
        c0 = s * 128
        for kc in range(8):
            S.op(S.pe, lambda h, kc=kc, c0=c0, hT=hT: h.matmul(pq[:], hT[:, kc, c0:c0 + 128], Wa[:, kc, 0:512], start=(kc == 0), stop=(kc == 7)),
                 reads=[hT.res, Wa.res], writes=[pq.res], inc=(kc == 7))
        for kc in range(8):
            S.op(S.pe, lambda h, kc=kc, c0=c0, hT=hT: h.matmul(pkv[:], hT[:, kc, c0:c0 + 128], Wa[:, kc, 512:768], start=(kc == 0), stop=(kc == 7)),
                 reads=[hT.res, Wa.res], writes=[pkv.res], inc=(kc == 7))

    def a_chain(k, s, cst=None):
        si, t0, n = tiles[k]
        A, tmp, src, uoff, rope = st_info[si]
        u0 = uoff + t0
        c0 = s * 128
        S.op(S.act, lambda h: h.activation(out=sqt[:, 0:512], in_=pq[:], func=AF.Square), reads=[pq.res], writes=[sqt.res])
        S.op(S.act, lambda h: h.activation(out=sqt[:, 512:640], in_=pkv[:, 0:128], func=AF.Square), reads=[pkv.res], writes=[sqt.res])
        vi = nxt("vb", 4)
        S.op(S.act, lambda h, vi=vi: h.activation(out=vb[vi][:], in_=pkv[:, 128:256], func=AF.Copy), reads=[pkv.res], writes=[vb[vi].res])
        S.dma(S.sp, [(self.VA[l][u0 + c0:u0 + c0 + 128, :], vb[vi][:])], reads=[vb[vi].res], writes=[self.R(self.VA[l].name)])
        S.op(S.dve, lambda h: h.tensor_reduce(out=ssh[:], in_=sqt[:].rearrange("p (a b) -> p a b", b=64), axis=AX.X, op=ALU.add),
             reads=[sqt.res], writes=[ssh.res])
        S.op(S.act, lambda h: h.activation(out=rinv[:], in_=ssh[:], func=AF.Sqrt, scale=1.0 / 64, bias=self.eps_col[:]),
             reads=[ssh.res, self.eps_col.res], writes=[rinv.res])
        S.op(S.dve, lambda h: h.reciprocal(out=rinv[:], in_=rinv[:]), reads=[rinv.res], writes=[rinv.res])
        S.op(S.dve, lambda h: h.tensor_tensor(out=qn[:, 0:8, :], in0=pq[:].rearrange("p (a b) -> p a b", b=64),
                                               in1=rinv[:, 0:8].unsqueeze(2).to_broadcast([128, 8, 64]), op=ALU.mult),
             reads=[pq.res, rinv.res], writes=[qn.res])
        S.op(S.dve, lambda h: h.tensor_tensor(out=qn[:, 8:10, :], in0=pkv[:, 0:128].rearrange("p (a b) -> p a b", b=64),
                                               in1=rinv[:, 8:10].unsqueeze(2).to_broadcast([128, 2, 64]), op=ALU.mult),
             reads=[pkv.res, rinv.res], writes=[qn.res])
        S.op(S.pool, lambda h: h.tensor_tensor(out=qn[:], in0=qn[:], in1=gain[:], op=ALU.mult),
             reads=[qn.res, gain.res], writes=[qn.res])
        q_ = qr[nxt("qr", 2)]
        if rope:
            cosb = cst[:, 0:1, :].to_broadcast([128, 10, 32])
            sinb = cst[:, 1:2, :].to_broadcast([128, 10, 32])
            x1 = qn[:, :, 0:32]
            x2 = qn[:, :, 32:64]
            r = [rt[nxt("rt", 4)] for _ in range(4)]
            S.op(S.pool, lambda h, r=r, cosb=cosb, x1=x1: h.tensor_tensor(out=r[0][:], in0=x1, in1=cosb, op=ALU.mult),
                 reads=[qn.res, cst.res], writes=[r[0].res])
            S.op(S.dve, lambda h, r=r, sinb=sinb, x2=x2: h.tensor_tensor(out=r[1][:], in0=x2, in1=sinb, op=ALU.mult),
                 reads=[qn.res, cst.res], writes=[r[1].res])
            S.op(S.dve, lambda h, r=r, q_=q_: h.tensor_tensor(out=q_[:, :, 0:32], in0=r[0][:], in1=r[1][:], op=ALU.subtract),
                 reads=[r[0].res, r[1].res], writes=[q_.res])
            S.op(S.pool, lambda h, r=r, sinb=sinb, x1=x1: h.tensor_tensor(out=r[2][:], in0=x1, in1=sinb, op=ALU.mult),
                 reads=[qn.res, cst.res], writes=[r[2].res])
            S.op(S.dve, lambda h, r=r, cosb=cosb, x2=x2: h.tensor_tensor(out=r[3][:], in0=x2, in1=cosb, op=ALU.mult),
                 reads=[qn.res, cst.res], writes=[r[3].res])
            S.op(S.dve, lambda h, r=r, q_=q_: h.tensor_tensor(out=q_[:, :, 32:64], in0=r[2][:], in1=r[3][:], op=ALU.add),
                 reads=[r[2].res, r[3].res], writes=[q_.res])
        else:
            S.op(S.dve, lambda h, q_=q_: h.tensor_copy(out=q_[:], in_=qn[:]), reads=[qn.res], writes=[q_.res])
        return q_

    def a_tr(q_, s):
        c0 = s * 128
        for hh in range(8):
            S.op(S.pe, lambda h, hh=hh, q_=q_: h.transpose(out=pqt[:, hh, :], in_=q_[:, hh, :], identity=self.ident[:]),
                 reads=[q_.res, self.ident.res], writes=[pqt.res], inc=(hh == 7))
        for hh in range(2):
            S.op(S.pe, lambda h, hh=hh, q_=q_: h.transpose(out=pkt[:, hh, :], in_=q_[:, 8 + hh, :], identity=self.ident[:]),
                 reads=[q_.res, self.ident.res], writes=[pkt.res], inc=(hh == 1))
        S.op(S.act, lambda h, c0=c0: h.activation(out=QTs[:, :, c0:c0 + 128], in_=pqt[:], func=AF.Copy), reads=[pqt.res], writes=[QTs.res])
        S.op(S.dve, lambda h, c0=c0: h.tensor_copy(out=KTs[:, :, c0:c0 + 128], in_=pkt[:]), reads=[pkt.res], writes=[KTs.res])

    def b_part(k, s):
        si, t0, n = tiles[k]
        uoff = st_info[si][3]
        u0 = uoff + t0
        hT = hTs[k % 2]
        c0 = s * 128
        for half in range(2):
            kk = nxt("pf", 2)
            for kc in range(8):
                S.op(S.pe, lambda h, kc=kc, c0=c0, kk=kk, half=half, hT=hT: h.matmul(pf[kk][:], hT[:, kc, c0:c0 + 128], Wv[:, kc, half * 512:(half + 1) * 512],
                                                                                  start=(kc == 0), stop=(kc == 7)),
                     reads=[hT.res, Wv.res], writes=[pf[kk].res], inc=(kc == 7))
            vi = nxt("vb2", 4)
            if half == 0:
                S.op(S.act, lambda h, kk=kk, vi=vi: h.activation(out=vb2[vi][:], in_=pf[kk][:], func=AF.Copy), reads=[pf[kk].res], writes=[vb2[vi].res])
            else:
                S.op(S.dve, lambda h, kk=kk, vi=vi: h.tensor_copy(out=vb2[vi][:], in_=pf[kk][:]), reads=[pf[kk].res], writes=[vb2[vi].res])
            dstv = self.GV[l] if half == 0 else self.MV[l]
            S.dma(S.sp, [(dstv[u0 + c0:u0 + c0 + 128, :], vb2[vi][:])], reads=[vb2[vi].res], writes=[self.R(dstv.name)])

    def c_part(k, fcs):
        si, t0, n = tiles[k]
        uoff = st_info[si][3]
        u0 = uoff + t0
        hT = hTs[k % 2]
        for fc in fcs:
            kk = nxt("pf", 2)
            for kc in range(8):
                S.op(S.pe, lambda h, kc=kc, fc=fc, kk=kk, n=n, hT=hT: h.matmul(pf[kk][:, :n], Wf[:, kc, fc * 128:(fc + 1) * 128], hT[:, kc, :n], start=(kc == 0), stop=(kc == 7)),
                     reads=[hT.res, Wf.res], writes=[pf[kk].res], inc=(kc == 7))
            if fc < 2:
                for d in range(2):
                    o_ = fo[nxt("fo", 8)]
                    S.op(S.dve, lambda h, kk=kk, n=n, d=d, fc=fc, o_=o_: h.scalar_tensor_tensor(out=o_[:, :n], in0=pf[kk][:, :n], scalar=0.125, in1=eb[d][fc][:, :n],
                                                                                           op0=ALU.mult, op1=ALU.mult),
                         reads=[pf[kk].res, eb[d][fc].res], writes=[o_.res])
                    S.dma(S.sp, [(self.QG[l][d, :, 2 * fc + hh2, u0:u0 + n], o_[hh2 * 64:(hh2 + 1) * 64, :n]) for hh2 in range(2)], reads=[o_.res], writes=[self.R(self.QG[l].name)])
            elif fc < 4:
                c2 = fc - 2
                for d in range(2):
                    o_ = fo[nxt("fo", 8)]
                    S.op(S.dve, lambda h, kk=kk, n=n, d=d, c2=c2, o_=o_: h.tensor_tensor(out=o_[:, :n], in0=pf[kk][:, :n], in1=enb[d][c2][:, :n], op=ALU.mult),
                         reads=[pf[kk].res, enb[d][c2].res], writes=[o_.res])
                    S.dma(S.sp, [(self.KG[l][d, :, 2 * c2 + hh2, u0:u0 + n], o_[hh2 * 64:(hh2 + 1) * 64, :n]) for hh2 in range(2)], reads=[o_.res], writes=[self.R(self.KG[l].name)])
                    o_ = fo[nxt("fo", 8)]
                    S.op(S.dve, lambda h, kk=kk, n=n, d=d, c2=c2, o_=o_: h.tensor_tensor(out=o_[:, :n], in0=pf[kk][:, :n], in1=ebl[d][c2][:, :n], op=ALU.mult),
                         reads=[pf[kk].res, ebl[d][c2].res], writes=[o_.res])
                    S.dma(S.sp, [(self.KH[l][d, :, 2 * c2 + hh2, u0:u0 + n], o_[hh2 * 64:(hh2 + 1) * 64, :n]) for hh2 in range(2)], reads=[o_.res], writes=[self.R(self.KH[l].name)])
            else:
                o_ = fo[nxt("fo", 8)]
                S.op(S.act, lambda h, kk=kk, n=n, o_=o_: h.activation(out=o_[:, :n], in_=pf[kk][:, :n], func=AF.Copy), reads=[pf[kk].res], writes=[o_.res])
                r0 = (fc - 4) * 128
                S.dma(S.sp, [(self.MQK[l][r0:r0 + 128, 2 + u0:2 + u0 + n], o_[:, :n])], reads=[o_.res], writes=[self.R(self.MQK[l].name)])

    for s in range(tiles[0][2] // 128):
        prep_sub(0, s)
    for k, (si, t0, n) in enumerate(tiles):
        A, tmp, src, uoff, rope_ = st_info[si]
        rope = rope_
        nt = n // 128
        u0 = uoff + t0
        hT = hTs[k % 2]
        S.dma(S.sp, [(H2Tv[:, :, u0:u0 + n], hT[:, :, :n])], reads=[hT.res], writes=[self.R(self.H2T[l].name)])
        g_part(k)
        order = [4, 5, 6, 7, 8, 9, 10, 11, 0, 1, 2, 3]
        per = (12 + nt - 1) // nt
        pend = None
        nxt_nt = tiles[k + 1][2] // 128 if k + 1 < len(tiles) else 0
        for s in range(nt):
            xi = prep_load(k + 1, s) if s < nxt_nt else None
            cst = rope_load(k, s)
            a_mm(k, s)
            q_ = a_chain(k, s, cst)
            if pend is not None:
                a_tr(*pend)
            pend = (q_, s)
            b_part(k, s)
            c_part(k, order[s * per:(s + 1) * per])
            if s < nxt_nt:
                prep_sub(k + 1, s, xi)
        a_tr(*pend)
        for s in range(nt, nxt_nt):
            prep_sub(k + 1, s)
        S.dma(S.sp, [(self.QT[l][:, :, u0:u0 + n], QTs[:, :, :n])], reads=[QTs.res], writes=[self.R(self.QT[l].name)])
        S.dma(S.sp, [(self.KT[l][:, :, u0:u0 + n], KTs[:, :, :n])], reads=[KTs.res], writes=[self.R(self.KT[l].name)])


Builder.phase_feat = phase_feat


def phase_attn(self, l, do_ctx):
    S = self.S
    T = self.T
    nbk = T // 128
    ones = self.sb("ones", [128, 128], F32)
    S.op(S.pool, lambda h: h.memset(ones[:], 1.0), writes=[ones.res])
    mP = self.sb("mP", [128, 4, 128], BF16)
    mN = self.sb("mN", [128, 4, 128], BF16)
    mtmp = self.sb("mtmp", [128, 128], F32)
    zer = self.sb("zer", [128, 128], F32)
    S.op(S.pool, lambda h: h.memset(zer[:], 0.0), writes=[zer.res])
    for (m_, sgn) in ((mP, 1), (mN, -1)):
        S.op(S.pool, lambda h, sgn=sgn: h.affine_select(out=mtmp[:], in_=zer[:], pattern=[[-sgn, 128]], compare_op=ALU.is_ge, fill=-30000.0,
                                                         base=0, channel_multiplier=sgn), reads=[zer.res], writes=[mtmp.res])
        S.op(S.pool, lambda h, m_=m_: h.tensor_copy(out=m_[:], in_=mtmp[:].unsqueeze(1).to_broadcast([128, 4, 128])), reads=[mtmp.res], writes=[m_.res])
    esk = self.sb("esk", [128, 2, 4, 128], F32)
    sk8 = self.sb("sk8", [128, 8], F32)
    S.dma(S.sp, [(sk8[64:65, :], self.attn_sink[l:l + 1, :])], writes=[sk8.res])
    S.op(S.act, lambda h: h.activation(out=sk8[64:65, :], in_=sk8[64:65, :], func=AF.Exp), reads=[sk8.res], writes=[sk8.res])
    S.op(S.dve, lambda h: h.tensor_copy(out=esk[64:65].rearrange("p g a b -> p (g a) b"), in_=sk8[64:65, :].unsqueeze(2).to_broadcast([1, 8, 128])),
         reads=[sk8.res], writes=[esk.res])
    KTc = self.sb("KTc", [64, 2, 256], BF16)
    S.dma(S.sp, [(KTc[:], self.KT[l][:, :, 0:256])], reads=[self.R(self.KT[l].name)], writes=[KTc.res])
    Vc = [self.sb(f"Vc{j}", [128, 2, 65], BF16) for j in range(2)]
    Vb = [self.sb(f"Vb{j}", [128, 2, 65], BF16) for j in range(4)]
    KTb = [self.sb(f"KTb{j}", [64, 2, 128], BF16) for j in range(4)]
    for v in Vc + Vb:
        S.op(S.pool, lambda h, v=v: h.memset(v[:], 1.0), writes=[v.res])
    for j in range(2):
        S.dma(S.sp, [(Vc[j][:, :, 0:64], self.VA[l][j * 128:(j + 1) * 128, :].rearrange("p (g d) -> p g d", d=64))],
              reads=[self.R(self.VA[l].name)], writes=[Vc[j].res])
    QTb = [self.sb(f"QTb{j}", [64, 8, 128], BF16) for j in range(2)]
    E = [self.sb(f"E{j}", [128, 4, 128], BF16) for j in range(4)]
    dn = [self.sb(f"dn{j}", [128, 512], F32) for j in range(2)]
    bcs = [self.sb(f"bcs{j}", [64, 512], F32) for j in range(2)]
    aT = [self.sb(f"aT{j}", [64, 4, 128], BF16) for j in range(2)]
    pS = [self.ps(f"pS{j}", [128, 512]) for j in range(3)]
    pO = [self.ps(f"pO{j}", [128, 512]) for j in range(2)]
    pB = [self.ps(f"pB{j}", [64, 512]) for j in range(2)]
    cnt = {}

    def nxt(key, n):
        v = cnt.get(key, 0)
        cnt[key] = v + 1
        return v % n

    def load_kb(m):
        i = m % 4
        u = LC + m * 128
        S.dma(S.sp, [(KTb[i][:], self.KT[l][:, :, u:u + 128])], reads=[self.R(self.KT[l].name)], writes=[KTb[i].res])
        S.dma(S.sp, [(Vb[i][:, :, 0:64], self.VA[l][u:u + 128, :].rearrange("p (g d) -> p g d", d=64))],
              reads=[self.R(self.VA[l].name)], writes=[Vb[i].res])

    pending = []

    def norm(g, po, u0):
        d_ = dn[nxt("dn", 2)]
        S.op(S.dve, lambda h, d_=d_, po=po, g=g: h.tensor_tensor(out=d_[64:65, :], in0=po[64:65, :], in1=esk[64:65, g].rearrange("p a b -> p (a b)"), op=ALU.add),
             reads=[po.res, esk.res], writes=[d_.res])
        S.op(S.dve, lambda h, d_=d_: h.reciprocal(out=d_[64:65, :], in_=d_[64:65, :]), reads=[d_.res], writes=[d_.res])
        pb = pB[nxt("pb", 2)]
        S.op(S.pe, lambda h, d_=d_, pb=pb: h.matmul(pb[:], ones[64:65, 0:64], d_[64:65, :], start=True, stop=True),
             reads=[d_.res, ones.res], writes=[pb.res])
        b_ = bcs[nxt("bcs", 2)]
        S.op(S.act, lambda h, b_=b_, pb=pb: h.activation(out=b_[:], in_=pb[:], func=AF.Copy), reads=[pb.res], writes=[b_.res])
        a_ = aT[nxt("aT", 2)]
        S.op(S.dve, lambda h, a_=a_, b_=b_, po=po: h.tensor_tensor(out=a_[:].rearrange("p a b -> p (a b)"), in0=po[0:64, :], in1=b_[:], op=ALU.mult),
             reads=[po.res, b_.res], writes=[a_.res])
        S.dma(S.sp, [(self.ATT[l][:, 4 * g:4 * g + 4, u0:u0 + 128], a_[:])], reads=[a_.res], writes=[self.R(self.ATT[l].name)])

    def qblock(u0, kbs):
        qi = nxt("q", 2)
        Q = QTb[qi]
        S.dma(S.sp, [(Q[:], self.QT[l][:, :, u0:u0 + 128])], reads=[self.R(self.QT[l].name)], writes=[Q.res])
        for g in range(2):
            po = pO[nxt("po", 2)]
            rhsq = Q[:, 4 * g:4 * g + 4, :].rearrange("p a b -> p (a b)")

            def score(idx, g=g, rhsq=rhsq):
                kt, vt, msk = kbs[idx]
                p = pS[nxt("ps", 3)]
                S.op(S.pe, lambda h, kt=kt, p=p, g=g, rhsq=rhsq, msk=msk: h.matmul(p[:], kt[0][:, g, kt[1]:kt[1] + 128], rhsq, start=True, stop=(msk is None)),
                     reads=[kt[0].res, Q.res], writes=[p.res], inc=(msk is None))
                if msk is not None:
                    S.op(S.pe, lambda h, p=p, msk=msk: h.matmul(p[:], self.ident[:], msk[:].rearrange("p a b -> p (a b)"), start=False, stop=True),
                         reads=[self.ident.res, msk.res], writes=[p.res])
                return p
            ps_list = [score(0)]
            for idx in range(len(kbs)):
                kt, vt, msk = kbs[idx]
                if idx + 1 < len(kbs):
                    ps_list.append(score(idx + 1))
                p = ps_list[idx]
                e = E[nxt("e", 4)]
                S.op(S.act, lambda h, p=p, e=e: h.activation(out=e[:].rearrange("p a b -> p (a b)"), in_=p[:], func=AF.Exp, scale=0.125),
                     reads=[p.res], writes=[e.res])
                S.op(S.pe, lambda h, e=e, vt=vt, po=po, idx=idx, g=g, kbs=kbs: h.matmul(po[0:65, :], vt[:, g, :], e[:].rearrange("p a b -> p (a b)"),
                                                                                      start=(idx == 0), stop=(idx == len(kbs) - 1)),
                     reads=[e.res, vt.res], writes=[po.res], inc=(idx == len(kbs) - 1))
            pending.append((g, po, u0))
            if len(pending) > 1:
                norm(*pending.pop(0))

    ckb = [((KTc, 0), Vc[0], None), ((KTc, 128), Vc[1], None)]
    if do_ctx:
        for n in range(2):
            qblock(n * 128, ckb)
    load_kb(0)
    for n in range(nbk):
        if n + 1 < nbk:
            load_kb(n + 1)
        kbs = []
        if n - 1 >= 0:
            kbs.append(((KTb[(n - 1) % 4], 0), Vb[(n - 1) % 4], mP))
        kbs.append(((KTb[n % 4], 0), Vb[n % 4], None))
        if n + 1 < nbk:
            kbs.append(((KTb[(n + 1) % 4], 0), Vb[(n + 1) % 4], mN))
        qblock(LC + n * 128, kbs + ckb)
    while pending:
        norm(*pending.pop(0))


Builder.phase_attn = phase_attn


def scan_groups(T):
    return [(0, LC)] + [(LC + t0, min(512, T - t0)) for t0 in range(0, T, 512)]


def scan_order(T, d):
    groups = scan_groups(T)
    order = []
    if d == 0:
        for gi, (u0, n) in enumerate(groups):
            for c in range(n // 64):
                order.append((gi, c))
    else:
        gis = [0] + list(range(len(groups) - 1, 0, -1))
        for gi in gis:
            u0, n = groups[gi]
            for c in range(n // 64 - 1, -1, -1):
                order.append((gi, c))
    return order


def phase_gla(self, l):
    S = self.S
    T = self.T
    EL = self.EL
    groups = scan_groups(T)
    ones = self.sb("ones", [64, 64], F32)
    S.op(S.pool, lambda h: h.memset(ones[:], 1.0), writes=[ones.res])
    mtmp = self.sb("mtmp", [64, 64], F32)
    msk = [self.sb(f"msk{d}", [64, 64], BF16) for d in range(2)]
    for d, sgn in ((0, -1), (1, 1)):
        S.op(S.pool, lambda h, sgn=sgn: h.affine_select(out=mtmp[:], in_=ones[:], pattern=[[-sgn, 64]], compare_op=ALU.is_ge, fill=0.0,
                                                         base=0, channel_multiplier=sgn), reads=[ones.res], writes=[mtmp.res])
        S.op(S.pool, lambda h, d=d: h.tensor_copy(out=msk[d][:], in_=mtmp[:]), reads=[mtmp.res], writes=[msk[d].res])
    Sf = [self.sb(f"Sf{d}", [64, 4, 128], F32) for d in range(2)]
    Sb = [self.sb(f"Sb{d}", [64, 4, 128], BF16) for d in range(2)]
    for d in range(2):
        S.op(S.pool, lambda h, d=d: h.memset(Sf[d][:], 0.0), writes=[Sf[d].res])
        S.op(S.pool, lambda h, d=d: h.memset(Sb[d][:], 0.0), writes=[Sb[d].res])
    qg = [[self.sb(f"qg{d}{i}", [64, 4, 512], BF16) for i in range(2)] for d in range(2)]
    kg = [[self.sb(f"kg{d}{i}", [64, 4, 512], BF16) for i in range(2)] for d in range(2)]
    kh = [[self.sb(f"kh{d}{i}", [64, 4, 512], BF16) for i in range(2)] for d in range(2)]
    vg = [[self.sb(f"vg{d}{i}", [64, 8, 512], BF16) for i in range(2)] for d in range(2)]
    am = [[self.sb(f"am{d}{i}", [64, 4, 64], BF16) for i in range(2)] for d in range(2)]
    kt = [[self.sb(f"kt{d}{i}", [64, 4, 64], BF16) for i in range(2)] for d in range(2)]
    ob = [[self.sb(f"ob{d}{i}", [64, 512], F32) for i in range(2)] for d in range(2)]
    pA = [self.ps(f"pA{d}", [64, 256]) for d in range(2)]
    pK = [self.ps(f"pK{d}", [64, 256], BF16) for d in range(2)]
    pO = [self.ps(f"pO{d}", [64, 512]) for d in range(2)]
    pN = [self.ps(f"pN{d}", [64, 512]) for d in range(2)]
    orders = [scan_order(T, d) for d in range(2)]
    nsteps = len(orders[0])
    gcount = [0, 0]
    cur = [None, None]

    def load_group(d, gi):
        i = gcount[d] % 2
        gcount[d] += 1
        u0, n = groups[gi]
        nch = n // 64
        for (dst, srcT) in ((qg[d][i], self.QG[l]), (kg[d][i], self.KG[l]), (kh[d][i], self.KH[l])):
            S.dma(S.sp, [(dst[:, :, :n], srcT[d, :, :, u0:u0 + n])], reads=[self.R(srcT.name)], writes=[dst.res])
        S.dma(S.sp, [(vg[d][i][:, :nch, :], self.GV[l][u0:u0 + n, :].rearrange("(c p) f -> p c f", p=64))], reads=[self.R(self.GV[l].name)],
              writes=[vg[d][i].res])
        return i

    for step in range(nsteps):
        for d in range(2):
            gi, c = orders[d][step]
            if cur[d] is None or cur[d][0] != gi:
                cur[d] = (gi, load_group(d, gi))
            bi = cur[d][1]
            u0, n = groups[gi]
            o = c * 64
            chunk = (u0 + o) // 64
            Q, Kg, Kh, V = qg[d][bi], kg[d][bi], kh[d][bi], vg[d][bi]
            k2 = step % 2
            AM, KTt, OB = am[d][k2], kt[d][k2], ob[d][k2]
            for hh in range(4):
                S.op(S.pe, lambda h, hh=hh, d=d, Kg=Kg, Q=Q, o=o: h.matmul(pA[d][:, hh * 64:(hh + 1) * 64], Kg[:, hh, o:o + 64], Q[:, hh, o:o + 64], start=True, stop=True),
                     reads=[Kg.res, Q.res], writes=[pA[d].res], inc=(hh == 3))
            for hh in range(4):
                S.op(S.pe, lambda h, hh=hh, d=d, Kh=Kh, o=o: h.transpose(out=pK[d][:, hh * 64:(hh + 1) * 64], in_=Kh[:, hh, o:o + 64], identity=self.ident[0:64, 0:64]),
                     reads=[Kh.res, self.ident.res], writes=[pK[d].res], inc=(hh == 3))
            S.op(S.dve, lambda h, d=d, AM=AM: h.tensor_tensor(out=AM[:], in0=pA[d][:].rearrange("p (a b) -> p a b", b=64),
                                                              in1=msk[d][:].unsqueeze(1).to_broadcast([64, 4, 64]), op=ALU.mult),
                 reads=[pA[d].res, msk[d].res], writes=[AM.res])
            S.op(S.act, lambda h, d=d, KTt=KTt: h.activation(out=KTt[:].rearrange("p a b -> p (a b)"), in_=pK[d][:], func=AF.Copy), reads=[pK[d].res], writes=[KTt.res])
            for hh in range(4):
                S.op(S.pe, lambda h, hh=hh, d=d, AM=AM, V=V, c=c: h.matmul(pO[d][:, hh * 128:(hh + 1) * 128], AM[:, hh, :], V[:, c, hh * 128:(hh + 1) * 128], start=True, stop=False),
                     reads=[AM.res, V.res], writes=[pO[d].res], inc=False)
                S.op(S.pe, lambda h, hh=hh, d=d, Q=Q, o=o: h.matmul(pO[d][:, hh * 128:(hh + 1) * 128], Q[:, hh, o:o + 64], Sb[d][:, hh, :], start=False, stop=True),
                     reads=[Q.res, Sb[d].res], writes=[pO[d].res], inc=(hh == 3))
            for hh in range(4):
                S.op(S.pe, lambda h, hh=hh, d=d, KTt=KTt, V=V, c=c: h.matmul(pN[d][:, hh * 128:(hh + 1) * 128], KTt[:, hh, :], V[:, c, hh * 128:(hh + 1) * 128], start=True, stop=True),
                     reads=[KTt.res, V.res], writes=[pN[d].res], inc=(hh == 3))
            S.op(S.act, lambda h, d=d, OB=OB: h.activation(out=OB[:], in_=pO[d][:], func=AF.Copy), reads=[pO[d].res], writes=[OB.res])
            S.dma(S.sp, [(self.OG[l][d, u0 + o:u0 + o + 64, :], OB[:])], reads=[OB.res], writes=[self.R(self.OG[l].name)])
            S.op(S.dve, lambda h, d=d, chunk=chunk: h.tensor_tensor(out=Sf[d][:], in0=Sf[d][:], in1=EL[:, :, d, chunk:chunk + 1].to_broadcast([64, 4, 128]), op=ALU.mult),
                 reads=[Sf[d].res, self.EL.res], writes=[Sf[d].res])
            S.op(S.dve, lambda h, d=d: h.tensor_tensor(out=Sf[d][:].rearrange("p a b -> p (a b)"), in0=Sf[d][:].rearrange("p a b -> p (a b)"), in1=pN[d][:], op=ALU.add),
                 reads=[Sf[d].res, pN[d].res], writes=[Sf[d].res])
            S.op(S.act, lambda h, d=d: h.activation(out=Sb[d][:], in_=Sf[d][:], func=AF.Copy), reads=[Sf[d].res], writes=[Sb[d].res])


Builder.phase_gla = phase_gla


LN_KS = float(-0.5 * np.log(128.0))


def phase_ml_gates(self, l):
    S = self.S
    sel = self.sel
    DEC = self.DEC
    T = self.T
    TT = self.TT
    nch = TT // 64
    bA = self.sb("bA", [4, TT], F32)
    bL = self.sb("bL", [4, TT], F32)
    bC = self.sb("bC", [4, TT], F32)
    bG = self.sb("bG", [4, TT], F32)
    bX = self.sb("bX", [4, TT], F32)
    onesr = self.sb("onesr", [4, TT], BF16)
    S.op(S.pool, lambda h: h.memset(onesr[:], 1.0), writes=[onesr.res])
    gl = self.sb("gl", [4, nch], F32)
    gp = self.sb("gp", [4, nch], F32)
    dd = self.sb("dd", [4, nch], F32)
    ibc = self.sb("ibc", [4, 2], F32)
    pD = self.ps("pD", [128, 512])
    for d in range(2):
        S.dma(S.sp, [(bA[:], self.GATES[l][d * 4:(d + 1) * 4, :])], reads=[self.R(self.GATES[l].name)], writes=[bA.res])
        S.dma(S.sp, [(bL[:], self.GATES[l][8 + d * 4:8 + (d + 1) * 4, :])], reads=[self.R(self.GATES[l].name)], writes=[bL.res])
        S.dma(S.sp, [(ibc[:, 0:1], self.mlstm_ib[l, d, :].rearrange("(h o) -> h o", o=1)), (ibc[:, 1:2], self.mlstm_fb[l, d, :].rearrange("(h o) -> h o", o=1))],
              writes=[ibc.res])
        S.op(S.dve, lambda h: h.tensor_scalar(out=ibc[:, 1:2], in0=ibc[:, 1:2], scalar1=-1.0, scalar2=None, op0=ALU.mult), reads=[ibc.res], writes=[ibc.res])
        S.op(S.act, lambda h: h.activation(out=bL[:], in_=bL[:], func=AF.Exp, scale=-1.0, bias=ibc[:, 1:2]), reads=[bL.res, ibc.res], writes=[bL.res])
        S.op(S.act, lambda h: h.activation(out=bL[:], in_=bL[:], func=AF.Ln, bias=1.0), reads=[bL.res], writes=[bL.res])

        def scan(out, src, op1):
            if d == 0:
                S.op(S.dve, lambda h: h.tensor_tensor_scan(out=out[:], data0=onesr[:], data1=src[:], initial=0.0, op0=ALU.mult, op1=op1),
                     reads=[src.res, onesr.res], writes=[out.res])
            else:
                S.op(S.dve, lambda h: h.tensor_tensor_scan(out=rev_last(out[:, 0:LC]), data0=onesr[:, 0:LC], data1=rev_last(src[:, 0:LC]), initial=0.0, op0=ALU.mult, op1=op1),
                     reads=[src.res, onesr.res], writes=[out.res])
                S.op(S.dve, lambda h: h.tensor_tensor_scan(out=rev_last(out[:, LC:TT]), data0=onesr[:, LC:TT], data1=rev_last(src[:, LC:TT]), initial=out[:, 0:1], op0=ALU.mult, op1=op1),
                     reads=[src.res, onesr.res, out.res], writes=[out.res])
        scan(bC, bL, ALU.add)
        S.op(S.dve, lambda h: h.scalar_tensor_tensor(out=bA[:], in0=bA[:], scalar=ibc[:, 0:1], in1=bC[:], op0=ALU.add, op1=ALU.add),
             reads=[bA.res, ibc.res, bC.res], writes=[bA.res])
        scan(bG, bA, ALU.max)
        G3 = bG[:].rearrange("p (c b) -> p c b", b=64)
        lastpos = 63 if d == 0 else 0
        S.op(S.dve, lambda h, lastpos=lastpos, G3=G3: h.tensor_copy(out=gl[:], in_=G3[:, :, lastpos]), reads=[bG.res], writes=[gl.res])
        S.op(S.dve, lambda h: h.memset(gp[:], 0.0), writes=[gp.res])
        if d == 0:
            S.op(S.dve, lambda h: h.tensor_copy(out=gp[:, 1:nch], in_=gl[:, 0:nch - 1]), reads=[gl.res], writes=[gp.res])
        else:
            S.op(S.dve, lambda h: h.tensor_copy(out=gp[:, 0:3], in_=gl[:, 1:4]), reads=[gl.res], writes=[gp.res])
            S.op(S.dve, lambda h: h.tensor_copy(out=gp[:, 4:nch - 1], in_=gl[:, 5:nch]), reads=[gl.res], writes=[gp.res])
            S.op(S.dve, lambda h: h.tensor_copy(out=gp[:, nch - 1:nch], in_=gl[:, 0:1]), reads=[gl.res], writes=[gp.res])
        L3 = bL[:].rearrange("p (c b) -> p c b", b=64)
        X3 = bX[:].rearrange("p (c b) -> p c b", b=64)
        A3 = bA[:].rearrange("p (c b) -> p c b", b=64)
        S.op(S.dve, lambda h, L3=L3, G3=G3: h.tensor_tensor(out=L3, in0=gp[:].unsqueeze(2).to_broadcast([4, nch, 64]), in1=G3, op=ALU.subtract),
             reads=[gp.res, bG.res], writes=[bL.res])
        S.op(S.act, lambda h: h.activation(out=bL[:], in_=bL[:], func=AF.Exp), reads=[bL.res], writes=[bL.res])
        S.op(S.dve, lambda h: h.tensor_tensor(out=bC[:], in0=bC[:], in1=bG[:], op=ALU.subtract), reads=[bC.res, bG.res], writes=[bC.res])
        S.op(S.act, lambda h: h.activation(out=bC[:], in_=bC[:], func=AF.Exp), reads=[bC.res], writes=[bC.res])
        S.op(S.dve, lambda h, X3=X3, A3=A3: h.tensor_tensor(out=X3, in0=A3, in1=gl[:].unsqueeze(2).to_broadcast([4, nch, 64]), op=ALU.subtract),
             reads=[bA.res, gl.res], writes=[bX.res])
        S.op(S.dve, lambda h: h.tensor_scalar(out=bX[:], in0=bX[:], scalar1=LN_KS, scalar2=None, op0=ALU.add), reads=[bX.res], writes=[bX.res])
        S.op(S.act, lambda h: h.activation(out=bX[:], in_=bX[:], func=AF.Exp), reads=[bX.res], writes=[bX.res])
        S.op(S.dve, lambda h: h.tensor_tensor(out=dd[:], in0=gp[:], in1=gl[:], op=ALU.subtract), reads=[gp.res, gl.res], writes=[dd.res])
        S.op(S.act, lambda h: h.activation(out=dd[:], in_=dd[:], func=AF.Exp), reads=[dd.res], writes=[dd.res])
        for hh in range(4):
            S.op(S.pe, lambda h, hh=hh: h.matmul(pD[:, :nch], sel[:, hh, :], dd[:], start=True, stop=True), reads=[self.sel.res, dd.res], writes=[pD.res])
            S.op(S.dve, lambda h, hh=hh, d=d: h.tensor_copy(out=DEC[:, d, hh, :], in_=pD[:, :nch]), reads=[pD.res], writes=[self.DEC.res])
        for qi, buf in enumerate((bA, bG, bL, bC, bX)):
            S.dma(S.sp, [(self.MROWS[l][d, qi, :, :], buf[:])], reads=[buf.res], writes=[self.R(self.MROWS[l].name)])


def phase_ml_conv(self, l):
    S = self.S
    T = self.T
    TT = self.TT
    wcol = self.sb("wcol", [128, 8, 5], F32)
    cb = self.sb("cb", [128, 8], F32)
    S.dma(S.sp, [(wcol[:, :, k], self.conv_w[l, k, :].rearrange("(fc p) -> p fc", p=128)) for k in range(5)], writes=[wcol.res], allow_slow_non_contiguous=True)
    S.dma(S.sp, [(cb[:], self.conv_b[l, :].rearrange("(fc p) -> p fc", p=128))], writes=[cb.res], allow_slow_non_contiguous=True)
    diagw = self.sb("diagw", [128, 8, 5, 128], BF16)
    for fc in range(8):
        for k in range(5):
            e = S.dve if (fc * 5 + k) % 2 == 0 else S.pool
            S.op(e, lambda h, fc=fc, k=k: h.tensor_scalar(out=diagw[:, fc, k, :], in0=self.identf[:], scalar1=wcol[:, fc, k:k + 1], scalar2=None, op0=ALU.mult),
                 reads=[self.identf.res, wcol.res], writes=[diagw.res])
    xq = [self.sb(f"xq{i}", [128, 8, 516], BF16) for i in range(2)]
    oc = [self.sb(f"oc{i}", [128, 512], BF16) for i in range(3)]
    pc = [self.ps(f"pc{i}", [128, 512]) for i in range(2)]
    MQKv = self.MQK[l].rearrange("(c p) t -> p c t", p=128)
    k2 = 0
    for gi, (u0, n) in enumerate(scan_groups(T)):
        X = xq[gi % 2]
        S.dma(S.sp, [(X[:, 0:4, 0:n + 4], MQKv[:, 0:4, u0:u0 + n + 4]), (X[:, 4:8, 0:n + 4], MQKv[:, 4:8, u0:u0 + n + 4])], reads=[self.R(self.MQK[l].name)], writes=[X.res])
        if u0 == 0 or u0 == LC:
            S.op(S.pool, lambda h, X=X: h.memset(X[:, :, 0:2], 0.0), writes=[X.res])
        if u0 + n == LC or u0 + n == TT:
            S.op(S.pool, lambda h, X=X, n=n: h.memset(X[:, :, n + 2:n + 4], 0.0), writes=[X.res])
        for fc in range(8):
            p = pc[k2 % 2]
            o_ = oc[k2 % 3]
            k2 += 1
            for k in range(5):
                S.op(S.pe, lambda h, fc=fc, k=k, p=p, X=X, n=n: h.matmul(p[:, :n], diagw[:, fc, k, :], X[:, fc, k:k + n], start=(k == 0), stop=(k == 4)),
                     reads=[diagw.res, X.res], writes=[p.res], inc=(k == 4))
            S.op(S.act, lambda h, fc=fc, p=p, o_=o_, n=n: h.activation(out=o_[:, :n], in_=p[:, :n], func=AF.Silu, bias=cb[:, fc:fc + 1]), reads=[p.res, cb.res], writes=[o_.res])
            S.dma(S.sp, [(self.MQC[l][fc * 128:(fc + 1) * 128, u0:u0 + n], o_[:, :n])], reads=[o_.res], writes=[self.R(self.MQC[l].name)])


def phase_ml_scan(self, l):
    S = self.S
    T = self.T
    sel = self.sel
    DEC = self.DEC
    groups = scan_groups(T)
    cfill = self.sb("cfill", [64, 64], F32)
    S.op(S.pool, lambda h: h.memset(cfill[:], LN_KS), writes=[cfill.res])
    mb = [self.sb(f"mb{d}", [64, 64], F32) for d in range(2)]
    for d, sgn in ((0, -1), (1, 1)):
        S.op(S.pool, lambda h, sgn=sgn, d=d: h.affine_select(out=mb[d][:], in_=cfill[:], pattern=[[-sgn, 64]], compare_op=ALU.is_ge, fill=-30000.0,
                                                              base=0, channel_multiplier=sgn), reads=[cfill.res], writes=[mb[d].res])
    negones = self.sb("negones", [4, 128], F32)
    S.op(S.pool, lambda h: h.memset(negones[:], -1.0), writes=[negones.res])
    posones = self.sb("posones", [4, 128], F32)
    S.op(S.pool, lambda h: h.memset(posones[:], 1.0), writes=[posones.res])
    mbr = [self.sb(f"mbr{d}", [64, 4, 64], F32) for d in range(2)]
    for d in range(2):
        S.op(S.pool, lambda h, d=d: h.tensor_copy(out=mbr[d][:], in_=mb[d][:].unsqueeze(1).to_broadcast([64, 4, 64])), reads=[mb[d].res], writes=[mbr[d].res])
    Dg = [self.sb(f"Dg{i}", [4, 3, 4, 512], F32) for i in range(2)]
    DgH = [self.sb(f"DgH{i}", [4, 2, 4, 512], BF16) for i in range(2)]
    DgL = [self.sb(f"DgL{i}", [4, 2, 4, 512], BF16) for i in range(2)]
    posb = self.sb("posb", [4, 128], BF16)
    S.op(S.pool, lambda h: h.memset(posb[:], 1.0), writes=[posb.res])
    Cf = self.sb("Cf", [128, 4, 129], F32)
    Cb = self.sb("Cb", [128, 4, 129], BF16)
    qk = [self.sb(f"qk{i}", [128, 8, 512], BF16) for i in range(2)]
    vg = [self.sb(f"vgm{i}", [64, 8, 4, 129], BF16) for i in range(2)]
    rows = [self.sb(f"rows{i}", [4, 5, 512], F32) for i in range(2)]
    for v in vg:
        S.op(S.pool, lambda h, v=v: h.memset(v[:], 1.0), writes=[v.res])
    wT = [self.sb(f"wT{i}", [64, 256], F32) for i in range(3)]
    sT = [self.sb(f"sT{i}", [64, 4, 64], BF16) for i in range(3)]
    qks = [self.sb(f"qks{i}", [128, 8, 64], BF16) for i in range(3)]
    khat = [self.sb(f"khat{i}", [64, 4, 128], BF16) for i in range(3)]
    enm = [self.sb(f"enm{i}", [64, 4], F32) for i in range(3)]
    rr = [self.sb(f"rr{i}", [64, 4], F32) for i in range(3)]
    ho = [self.sb(f"ho{i}", [64, 4, 128], F32) for i in range(3)]
    pWS = self.ps("pWS", [64, 512])
    pB = self.ps("pB", [128, 8, 64])
    pK = self.ps("pK", [64, 4, 128], BF16)
    pO = self.ps("pO", [64, 1024])
    pN = self.ps("pN", [128, 1024])
    pO3 = pO[:].rearrange("p (h e) -> p h e", e=256)
    pN3 = pN[:].rearrange("p (h e) -> p h e", e=256)
    gcount = [0]

    def load_group(d, gi):
        i = gcount[0] % 2
        gcount[0] += 1
        u0, n = groups[gi]
        nchg = n // 64
        S.dma(S.sp, [(qk[i][:, 0:4, :n], self.MQC[l].rearrange("(c p) t -> p c t", p=128)[:, 0:4, u0:u0 + n]),
                     (qk[i][:, 4:8, :n], self.MQC[l].rearrange("(c p) t -> p c t", p=128)[:, 4:8, u0:u0 + n])], reads=[self.R(self.MQC[l].name)], writes=[qk[i].res])
        S.dma(S.sp, [(vg[i][:, c, :, 0:128], self.MV[l][u0 + c * 64:u0 + (c + 1) * 64, :].rearrange("p (h e) -> p h e", e=128)) for c in range(nchg)],
              reads=[self.R(self.MV[l].name)], writes=[vg[i].res])
        S.dma(S.sp, [(rows[i][:, :, :n], self.MROWS[l][d, :, :, u0:u0 + n].rearrange("q h t -> h q t"))], reads=[self.R(self.MROWS[l].name)], writes=[rows[i].res])
        for qd, qs in enumerate((1, 2, 4)):
            S.op(S.pool, lambda h, i=i, qd=qd, qs=qs, n=n: h.tensor_tensor(out=Dg[i][:, qd, :, :n], in0=rows[i][:, qs:qs + 1, :n].to_broadcast([4, 4, n]),
                                                                       in1=self.identf[0:4, 0:4].unsqueeze(2).to_broadcast([4, 4, n]), op=ALU.mult),
                 reads=[rows[i].res, self.identf.res], writes=[Dg[i].res])
        return i

    for d in range(2):
        S.op(S.pool, lambda h: h.memset(Cf[:], 0.0), writes=[Cf.res])
        S.op(S.pool, lambda h: h.memset(Cb[:], 0.0), writes=[Cb.res])
        order = scan_order(T, d)
        cur = None
        info = []
        for (gi, c) in order:
            if cur is None or cur[0] != gi:
                cur = (gi, None)
            info.append((gi, c))
        bufof = {}

        def stageA(step):
            gi, c = order[step]
            if gi not in bufof:
                bufof.clear()
                bufof[gi] = load_group(d, gi)
            bi = bufof[gi]
            o = c * 64
            k2 = step % 3
            QK, R_ = qk[bi], rows[bi]
            DG = Dg[bi]
            S.op(S.pe, lambda h, R_=R_, o=o: h.matmul(pWS[:, 0:256], R_[:, 0, o:o + 64], sel[:, :, 0:64], start=True, stop=False),
                 reads=[R_.res, sel.res], writes=[pWS.res], inc=False)
            S.op(S.pe, lambda h, DG=DG, o=o: h.matmul(pWS[:, 0:256], negones[:, 0:64], DG[:, 0, :, o:o + 64], start=False, stop=False),
                 reads=[DG.res, negones.res], writes=[pWS.res], inc=False)
            S.op(S.pe, lambda h, d=d: h.matmul(pWS[:, 0:256], self.identf[0:64, 0:64], mbr[d][:], start=False, stop=True),
                 reads=[self.identf.res, mbr[d].res], writes=[pWS.res], inc=False)
            for hh in range(4):
                S.op(S.pe, lambda h, hh=hh, QK=QK, o=o: h.matmul(pWS[:, 256 + hh * 64:256 + (hh + 1) * 64], QK[:, 4 + hh, o:o + 64], QK[:, hh, o:o + 64], start=True, stop=True),
                     reads=[QK.res], writes=[pWS.res], inc=(hh == 3))
            S.op(S.act, lambda h, k2=k2: h.activation(out=wT[k2][:], in_=pWS[:, 0:256], func=AF.Exp), reads=[pWS.res], writes=[wT[k2].res])
            S.op(S.dve, lambda h, k2=k2: h.tensor_tensor(out=sT[k2][:].rearrange("p a b -> p (a b)"), in0=pWS[:, 256:512], in1=wT[k2][:], op=ALU.mult),
                 reads=[pWS.res, wT[k2].res], writes=[sT[k2].res])
            S.op(S.pe, lambda h, DG=DG, o=o: h.matmul(pB[:], posones[:, :], DG[:, 1:3, :, o:o + 64], start=True, stop=True),
                 reads=[DG.res, posones.res], writes=[pB.res])
            S.op(S.dve, lambda h, k2=k2, QK=QK, o=o: h.tensor_tensor(out=qks[k2][:], in0=QK[:, :, o:o + 64], in1=pB[:], op=ALU.mult),
                 reads=[QK.res, pB.res], writes=[qks[k2].res])

        def stageA2(step):
            k2 = step % 3
            for hh in range(4):
                S.op(S.pe, lambda h, hh=hh, k2=k2: h.transpose(out=pK[:, hh, :], in_=qks[k2][:, 4 + hh, :], identity=self.ident[:]),
                     reads=[qks[k2].res, self.ident.res], writes=[pK.res], inc=(hh == 3))
            S.op(S.act, lambda h, k2=k2: h.activation(out=khat[k2][:], in_=pK[:], func=AF.Copy), reads=[pK.res], writes=[khat[k2].res])

        def stageB(step):
            gi, c = order[step]
            u0, n = groups[gi]
            o = c * 64
            chunk = (u0 + o) // 64
            k2 = step % 3
            V, R_ = vgbuf[step], rowbuf[step]
            for hh in range(4):
                S.op(S.pe, lambda h, hh=hh, k2=k2, V=V, c=c: h.matmul(pN[:, hh * 256:hh * 256 + 129], khat[k2][:, hh, :], V[:, c, hh, :], start=True, stop=True),
                     reads=[khat[k2].res, V.res], writes=[pN.res], inc=(hh == 3))
            S.op(S.pe, lambda h, R_=R_, o=o: h.matmul(pO[:, 200:204], R_[:, 3, o:o + 64], self.identf[0:4, 0:4], start=True, stop=True),
                 reads=[R_.res, self.identf.res], writes=[pO.res], inc=False)
            for hh in range(4):
                S.op(S.pe, lambda h, hh=hh, k2=k2, V=V, c=c: h.matmul(pO[:, hh * 256:hh * 256 + 129], sT[k2][:, hh, :], V[:, c, hh, :], start=True, stop=False),
                     reads=[sT[k2].res, V.res], writes=[pO.res], inc=False)
                S.op(S.pe, lambda h, hh=hh, k2=k2: h.matmul(pO[:, hh * 256:hh * 256 + 129], qks[k2][:, hh, :], Cb[:, hh, :], start=False, stop=True),
                     reads=[qks[k2].res, Cb.res], writes=[pO.res], inc=(hh == 3))
            S.op(S.act, lambda h, k2=k2: h.activation(out=enm[k2][:], in_=pO[:, 200:204], func=AF.Copy), reads=[pO.res], writes=[enm[k2].res])
            S.op(S.act, lambda h, k2=k2: h.activation(out=rr[k2][:], in_=pO3[:, :, 128], func=AF.Abs), reads=[pO.res], writes=[rr[k2].res])
            S.op(S.pool, lambda h, d=d, chunk=chunk: h.tensor_tensor(out=Cf[:], in0=Cf[:], in1=DEC[:, d, :, chunk:chunk + 1].to_broadcast([128, 4, 129]), op=ALU.mult),
                 reads=[Cf.res, DEC.res], writes=[Cf.res])
            S.op(S.dve, lambda h: h.tensor_tensor(out=Cf[:], in0=Cf[:], in1=pN3[:, :, 0:129], op=ALU.add), reads=[Cf.res, pN.res], writes=[Cf.res])
            S.op(S.act, lambda h: h.activation(out=Cb[:], in_=Cf[:], func=AF.Copy), reads=[Cf.res], writes=[Cb.res])
            S.op(S.dve, lambda h, k2=k2: h.tensor_tensor(out=rr[k2][:], in0=rr[k2][:], in1=enm[k2][:], op=ALU.max), reads=[rr[k2].res, enm[k2].res], writes=[rr[k2].res])
            S.op(S.dve, lambda h, k2=k2: h.reciprocal(out=rr[k2][:], in_=rr[k2][:]), reads=[rr[k2].res], writes=[rr[k2].res])
            S.op(S.dve, lambda h, k2=k2: h.tensor_tensor(out=ho[k2][:], in0=pO3[:, :, 0:128], in1=rr[k2][:].unsqueeze(2).to_broadcast([64, 4, 128]), op=ALU.mult),
                 reads=[pO.res, rr[k2].res], writes=[ho[k2].res])
            S.dma(S.sp, [(self.OM[l][d, u0 + o:u0 + o + 64, :], ho[k2][:].rearrange("p a b -> p (a b)"))], reads=[ho[k2].res], writes=[self.R(self.OM[l].name)])

        vgbuf = {}
        rowbuf = {}

        def A(step):
            stageA(step)
            gi, c = order[step]
            vgbuf[step] = vg[bufof[gi]]
            rowbuf[step] = rows[bufof[gi]]
        A(0)
        if len(order) > 1:
            A(1)
        stageA2(0)
        for step in range(len(order)):
            if step + 2 < len(order):
                A(step + 2)
            if step + 1 < len(order):
                stageA2(step + 1)
            stageB(step)


Builder.phase_ml_gates = phase_ml_gates
Builder.phase_ml_conv = phase_ml_conv
Builder.phase_ml_scan = phase_ml_scan


def phase_merge(self, l, streams):
    S = self.S
    win = self.w_in[l].rearrange("(kc p) n -> p kc n", p=128)
    Wm = self.sb("Wm", [128, 8, 4096], BF16)
    Wmr = [Res(f"Wm{k}") for k in range(4)]
    for ki, k0 in enumerate(range(0, 8, 2)):
        S.dma(S.pool, [(Wm[:, k0:k0 + 2, 0:512], win[:, k0:k0 + 2, O_GR:O_GR + 512]),
                       (Wm[:, k0:k0 + 2, 512:1024], win[:, k0:k0 + 2, O_MO:O_MO + 512]),
                       (Wm[:, k0:k0 + 2, 1024:4096], win[:, k0:k0 + 2, O_SA:O_SA + 3072])], writes=[Wmr[ki]])
    Woa = self.sb("Woa", [64, 8, D], BF16)
    Wog = self.sb("Wog", [128, 4, D], BF16)
    Wom = self.sb("Wom", [128, 4, D], BF16)
    Wo = self.sb("Wo", [128, 8, D], BF16)
    S.dma(S.pool, [(Woa[:], self.w_out_attn[l].rearrange("(h p) n -> p h n", p=64))], writes=[Woa.res])
    S.dma(S.pool, [(Wog[:], self.w_out_gla[l].rearrange("(c p) n -> p c n", p=128))], writes=[Wog.res])
    S.dma(S.pool, [(Wom[:], self.w_out_mlstm[l].rearrange("(c p) n -> p c n", p=128))], writes=[Wom.res])
    S.dma(S.pool, [(Wo[:, 0:4, :], self.w_o[l].rearrange("(c p) n -> p c n", p=128)[:, 0:4, :]),
                   (Wo[:, 4:8, :], self.w_o[l].rearrange("(c p) n -> p c n", p=128)[:, 4:8, :])], writes=[Wo.res])
    gains = self.sb("gains", [128, 2, 128], F32)
    S.dma(S.sp, [(gains[:, 0, :], bcast_rows(self.gla_norm[l:l + 1, :], 128)), (gains[:, 1, :], bcast_rows(self.mlstm_norm[l:l + 1, :], 128))], writes=[gains.res])
    eps_col = self.eps_col
    hT = [self.sb(f"mhT{i}", [128, 8, 128], BF16) for i in range(2)]
    aTt = [self.sb(f"maT{i}", [64, 8, 128], BF16) for i in range(2)]
    og = [self.sb(f"mog{i}", [128, 2, 512], F32) for i in range(2)]
    om = [self.sb(f"mom{i}", [128, 2, 512], F32) for i in range(2)]
    xr = [self.sb(f"mxr{i}", [128, D], F32) for i in range(2)]
    gts = [self.sb(f"mgt{i}", [128, 8, 512], F32) for i in range(2)]
    sq = self.sb("msq", [128, 512], F32)
    ssqs = [self.sb(f"mssq{i}", [128, 2, 4], F32) for i in range(2)]
    bn = [self.sb(f"mbn{i}", [128, 512], F32) for i in range(2)]
    bbs = [[self.sb(f"mbb{i}{j}", [128, 512], BF16) for j in range(2)] for i in range(2)]
    bTs = [[self.sb(f"mbT{i}{j}", [128, 4, 128], BF16) for j in range(2)] for i in range(2)]
    yb = self.sb("myb", [128, D], BF16)
    yT = self.sb("myT", [128, 8, 128], BF16)
    t1 = [self.sb(f"mt1{i}", [128, 512], F32) for i in range(3)]
    G5 = self.sb("mG5", [128, D], F32)
    pg = [self.ps(f"mpg{i}", [128, 512]) for i in range(2)]
    pT1s = [self.ps(f"mpT1{j}", [128, 512], BF16) for j in range(2)]
    pT2 = self.ps("mpT2", [128, D], BF16)
    py = [self.ps(f"mpy{i}", [128, 512]) for i in range(3)]
    pY = py[0]
    cnt = {}

    def nxt(key, n):
        v = cnt.get(key, 0)
        cnt[key] = v + 1
        return v % n
    H2Tv = self.H2T[l].rearrange("(kc p) t -> p kc t", p=128)
    work = []
    for (tag, src, dst, ntok, row, uoff) in streams:
        for t0 in range(0, ntok, 128):
            work.append((tag, src, dst, row, uoff + t0, t0))

    def stage1(w, i):
        (tag, src, dst, row, u, t0) = work[w]
        H, AT, OGt, OMt, XR, gt, ssq = hT[i], aTt[i], og[i], om[i], xr[i], gts[i], ssqs[i]
        S.dma(S.sp, [(H[:], H2Tv[:, :, u:u + 128])], reads=[self.R(self.H2T[l].name)], writes=[H.res])
        S.dma(S.sp, [(AT[:], self.ATT[l][:, :, u:u + 128])], reads=[self.R(self.ATT[l].name)], writes=[AT.res])
        S.dma(S.sp, [(OGt[:, 0, :], self.OG[l][0, u:u + 128, :]), (OGt[:, 1, :], self.OG[l][1, u:u + 128, :])], reads=[self.R(self.OG[l].name)], writes=[OGt.res])
        S.dma(S.sp, [(OMt[:, 0, :], self.OM[l][0, u:u + 128, :]), (OMt[:, 1, :], self.OM[l][1, u:u + 128, :])], reads=[self.R(self.OM[l].name)], writes=[OMt.res])
        S.dma(S.sp, [(XR[:], src[t0:t0 + 128, :])], reads=[self.R(src.name)], writes=[XR.res])

    def stage1g(w, i):
        H, gt = hT[i], gts[i]
        for blk in range(8):
            p = pg[nxt("pg", 2)]
            for kc in range(8):
                S.op(S.pe, lambda h, kc=kc, blk=blk, p=p, H=H: h.matmul(p[:], H[:, kc, :], Wm[:, kc, blk * 512:(blk + 1) * 512], start=(kc == 0), stop=(kc == 7)),
                     reads=[H.res] + Wmr, writes=[p.res], inc=(kc == 7))
            fn = AF.Silu if blk == 0 else AF.Sigmoid
            S.op(S.act, lambda h, blk=blk, p=p, fn=fn, gt=gt: h.activation(out=gt[:, blk, :], in_=p[:], func=fn), reads=[p.res], writes=[gt.res])

    def stage1c(w, i):
        OGt, OMt, gt, ssq = og[i], om[i], gts[i], ssqs[i]
        for br, Ot in enumerate((OGt, OMt)):
            S.op(S.pool, lambda h, Ot=Ot: h.tensor_tensor(out=Ot[:, 0, :], in0=Ot[:, 0, :], in1=Ot[:, 1, :], op=ALU.add), reads=[Ot.res], writes=[Ot.res])
            S.op(S.act, lambda h, Ot=Ot: h.activation(out=sq[:], in_=Ot[:, 0, :], func=AF.Square), reads=[Ot.res], writes=[sq.res])
            S.op(S.dve, lambda h, br=br, ssq=ssq: h.tensor_reduce(out=ssq[:, br, :], in_=sq[:].rearrange("p (a b) -> p a b", b=128), axis=AX.X, op=ALU.add),
                 reads=[sq.res], writes=[ssq.res])
        S.op(S.act, lambda h, ssq=ssq: h.activation(out=ssq[:], in_=ssq[:], func=AF.Sqrt, scale=1.0 / 128, bias=eps_col[:]),
             reads=[ssq.res, eps_col.res], writes=[ssq.res])
        S.op(S.dve, lambda h, ssq=ssq: h.reciprocal(out=ssq[:], in_=ssq[:]), reads=[ssq.res], writes=[ssq.res])
        for br, Ot in enumerate((OGt, OMt)):
            B_ = bn[br]
            S.op(S.dve, lambda h, br=br, Ot=Ot, B_=B_, ssq=ssq: h.tensor_tensor(out=B_[:].rearrange("p (a b) -> p a b", b=128), in0=Ot[:, 0, :].rearrange("p (a b) -> p a b", b=128),
                                                                              in1=ssq[:, br, :].unsqueeze(2).to_broadcast([128, 4, 128]), op=ALU.mult),
                 reads=[Ot.res, ssq.res], writes=[B_.res])
            S.op(S.pool, lambda h, br=br, B_=B_: h.tensor_tensor(out=B_[:].rearrange("p (a b) -> p a b", b=128), in0=B_[:].rearrange("p (a b) -> p a b", b=128),
                                                              in1=gains[:, br:br + 1, :].to_broadcast([128, 4, 128]), op=ALU.mult),
                 reads=[B_.res, gains.res], writes=[B_.res])

    def stage1d(w, i):
        gt = gts[i]
        for br in range(2):
            B_ = bn[br]
            BB = bbs[i][br]
            S.op(S.dve, lambda h, br=br, B_=B_, BB=BB, gt=gt: h.tensor_tensor(out=BB[:], in0=B_[:], in1=gt[:, br, :], op=ALU.mult), reads=[B_.res, gt.res], writes=[BB.res])

    def stage1b(w, i):
        for br in range(2):
            BB = bbs[i][br]
            pT1 = pT1s[br]
            for c in range(4):
                S.op(S.pe, lambda h, c=c, BB=BB, pT1=pT1: h.transpose(out=pT1[:, c * 128:(c + 1) * 128], in_=BB[:, c * 128:(c + 1) * 128], identity=self.ident[:]),
                     reads=[BB.res, self.ident.res], writes=[pT1.res], inc=(c == 3))
            BT = bTs[i][br]
            if br == 0:
                S.op(S.act, lambda h, BT=BT, pT1=pT1: h.activation(out=BT[:].rearrange("p a b -> p (a b)"), in_=pT1[:], func=AF.Copy), reads=[pT1.res], writes=[BT.res])
            else:
                S.op(S.dve, lambda h, BT=BT, pT1=pT1: h.tensor_copy(out=BT[:].rearrange("p a b -> p (a b)"), in_=pT1[:]), reads=[pT1.res], writes=[BT.res])

    cur_row = [None]

    def stage2(w, i):
        (tag, src, dst, row, u, t0) = work[w]
        AT, XR, gt = aTt[i], xr[i], gts[i]
        if cur_row[0] != row:
            cur_row[0] = row
            srcg = self.MOD[l][row:row + 1, 5 * D:6 * D]
            S.dma(S.sp, [(G5[:], dram_ap(srcg, srcg.offset, [[0, 128], [1, D]]))], reads=[self.R("MOD", l)], writes=[G5.res])
        for half in range(2):
            cs_ = slice(half * 512, (half + 1) * 512)
            for hh in range(8):
                S.op(S.pe, lambda h, hh=hh, AT=AT, cs_=cs_: h.matmul(py[0][:], AT[:, hh, :], Woa[:, hh, cs_], start=(hh == 0), stop=(hh == 7)),
                     reads=[AT.res, Woa.res], writes=[py[0].res], inc=(hh == 7))
            for c in range(4):
                S.op(S.pe, lambda h, c=c, cs_=cs_, BT=bTs[i][0]: h.matmul(py[1][:], BT[:, c, :], Wog[:, c, cs_], start=(c == 0), stop=(c == 3)),
                     reads=[bTs[i][0].res, Wog.res], writes=[py[1].res], inc=(c == 3))
            for c in range(4):
                S.op(S.pe, lambda h, c=c, cs_=cs_, BT=bTs[i][1]: h.matmul(py[2][:], BT[:, c, :], Wom[:, c, cs_], start=(c == 0), stop=(c == 3)),
                     reads=[bTs[i][1].res, Wom.res], writes=[py[2].res], inc=(c == 3))
            S.op(S.dve, lambda h, half=half, gt=gt: h.tensor_tensor(out=t1[0][:], in0=py[0][:], in1=gt[:, 2 + half, :], op=ALU.mult), reads=[py[0].res, gt.res], writes=[t1[0].res])
            S.op(S.dve, lambda h, half=half, gt=gt: h.tensor_tensor(out=t1[1][:], in0=py[1][:], in1=gt[:, 4 + half, :], op=ALU.mult), reads=[py[1].res, gt.res], writes=[t1[1].res])
            S.op(S.dve, lambda h, half=half, gt=gt: h.tensor_tensor(out=t1[2][:], in0=py[2][:], in1=gt[:, 6 + half, :], op=ALU.mult), reads=[py[2].res, gt.res], writes=[t1[2].res])
            S.op(S.dve, lambda h: h.tensor_tensor(out=t1[0][:], in0=t1[0][:], in1=t1[1][:], op=ALU.add), reads=[t1[0].res, t1[1].res], writes=[t1[0].res])
            S.op(S.dve, lambda h, cs_=cs_: h.tensor_tensor(out=yb[:, cs_], in0=t1[0][:], in1=t1[2][:], op=ALU.add), reads=[t1[0].res, t1[2].res], writes=[yb.res])
        for kc in range(8):
            S.op(S.pe, lambda h, kc=kc: h.transpose(out=pT2[:, kc * 128:(kc + 1) * 128], in_=yb[:, kc * 128:(kc + 1) * 128], identity=self.ident[:]),
                 reads=[yb.res, self.ident.res], writes=[pT2.res], inc=(kc == 7))
        S.op(S.act, lambda h: h.activation(out=yT[:].rearrange("p a b -> p (a b)"), in_=pT2[:], func=AF.Copy), reads=[pT2.res], writes=[yT.res])
        for half in range(2):
            cs_ = slice(half * 512, (half + 1) * 512)
            for kc in range(8):
                S.op(S.pe, lambda h, kc=kc, cs_=cs_: h.matmul(pY[:], yT[:, kc, :], Wo[:, kc, cs_], start=(kc == 0), stop=(kc == 7)),
                     reads=[yT.res, Wo.res], writes=[pY.res], inc=(kc == 7))
            tq = t1[1 + half]
            S.op(S.dve, lambda h, cs_=cs_, tq=tq: h.tensor_tensor(out=tq[:], in0=pY[:], in1=G5[:, cs_], op=ALU.mult), reads=[pY.res, G5.res], writes=[tq.res])
            S.op(S.pool, lambda h, cs_=cs_, XR=XR, tq=tq: h.tensor_tensor(out=XR[:, cs_], in0=XR[:, cs_], in1=tq[:], op=ALU.add), reads=[XR.res, tq.res], writes=[XR.res])
        S.dma(S.pool, [(dst[t0:t0 + 128, :], XR[:])], reads=[XR.res], writes=[self.R(dst.name)])

    def stage1all(w, i):
        stage1(w, i)
        stage1c(w, i)
        stage1g(w, i)
        stage1d(w, i)
        stage1b(w, i)

    stage1all(0, 0)
    for w in range(len(work)):
        if w + 1 < len(work):
            stage1all(w + 1, (w + 1) % 2)
        stage2(w, w % 2)


Builder.phase_merge = phase_merge

_NC_CACHE = {}


def kernel(**inputs):
    inp = {k: np.asarray(v) for k, v in inputs.items()}
    Bsz, SEQ, _ = inp["x"].shape
    T = SEQ
    if T not in _NC_CACHE:
        _NC_CACHE[T] = Builder(T).build()
    nc = _NC_CACHE[T]
    in_maps = [make_in_map(inp, b, 0, T) for b in range(Bsz)]
    res = run_bass_kernel_spmd(nc, in_maps, core_ids=list(range(Bsz)))
    out = np.stack([np.asarray(r["y"], dtype=np.float32) for r in res.results], axis=0)
    return out


W_NAMES = ["mod_w", "mod_b", "norm_g", "ffn1_w13", "ffn1_w2", "ffn2_w13", "ffn2_w2", "w_in", "attn_q_norm", "attn_k_norm", "attn_sink", "gla_w2", "gla_b", "mlstm_conv_w", "mlstm_conv_b", "mlstm_ib", "mlstm_fb", "gla_norm", "mlstm_norm", "w_out_attn", "w_out_gla", "w_out_mlstm", "w_o"]


def make_in_map(inp, b, t0, T):
    m = {"x": np.ascontiguousarray(inp["x"][b, t0:t0 + T]), "c": np.ascontiguousarray(inp["c"][b]),
         "ctx": np.ascontiguousarray(inp["ctx"][b]), "c_ctx": np.ascontiguousarray(inp["c_ctx"])}
    for k in W_NAMES:
        m[k] = np.ascontiguousarray(inp[k])
    m["rope_cs"] = rope_table(t0, T)
    return m


def rope_table(t0, T):
    pos = np.arange(t0, t0 + T)
    r = (pos // 64).astype(np.float32)
    col = (pos % 64).astype(np.float32)
    inv = (np.float32(10000.0) ** (-np.arange(16, dtype=np.float32) / np.float32(16))).astype(np.float32)
    ang = np.concatenate([r[:, None] * inv, col[:, None] * inv], axis=-1).astype(np.float32)
    return np.ascontiguousarray(np.stack([np.cos(ang), np.sin(ang)], axis=1).astype(np.float32))
```

```python
import numpy as np
from contextlib import ExitStack
import concourse.bass as bass
import concourse.mybir as mybir
from concourse.bass_utils import run_bass_kernel_spmd

F32 = mybir.dt.float32
BF16 = mybir.dt.bfloat16
AF = mybir.ActivationFunctionType
ALU = mybir.AluOpType
AX = mybir.AxisListType

D = 1024
DFF = 2816
NMOD = 9
LC = 256
EPS = 1e-6
DEPTH = 2
D_IN = 7472


class Res:
    __slots__ = ("name", "w", "r")

    def __init__(self, name=""):
        self.name = name
        self.w = None
        self.r = []


class Eng:
    def __init__(self, name, is_pe=False):
        self.name = name
        self.is_pe = is_pe
        self.ops = []
        self.sems = []
        self.si = 0
        self.cnt = 0
        self.seen = {}
        self.pend_r = []
        self.pend_w = []
        self.pool = []
        self.pi = 0


ROT = 30000


class Sched:
    def __init__(self, nc, es):
        self.nc = nc
        self.es = es
        self.pe = Eng("pe", True)
        self.act = Eng("act")
        self.dve = Eng("dve")
        self.pool = Eng("pool")
        self.sp = Eng("sp")
        self.engs = [self.pe, self.act, self.dve, self.pool, self.sp]
        self.semid = {}
        n_rot = {"pe": 6, "act": 3, "dve": 3, "pool": 3, "sp": 1}
        for e in self.engs:
            for i in range(n_rot[e.name]):
                s = es.enter_context(nc.semaphore(f"s_{e.name}{i}"))
                e.sems.append(s)
        for e, n in ((self.sp, 20), (self.pool, 10), (self.act, 4)):
            for i in range(n):
                s = es.enter_context(nc.semaphore(f"d_{e.name}{i}"))
                e.pool.append([s, 0])
        self.n_ops = 0

    def _need(self, eng, tok, raw):
        if tok is None:
            return None
        sem, val, owner = tok
        if owner == eng.name:
            if eng.is_pe:
                return None
            if not raw:
                return None
        key = id(sem)
        if eng.seen.get(key, 0) >= val:
            return None
        eng.seen[key] = val
        return (sem, val)

    def _waits(self, eng, reads, writes):
        ws = []
        for r in reads:
            w = self._need(eng, r.w, True)
            if w:
                ws.append(w)
        for wr in writes:
            w = self._need(eng, wr.w, False)
            if w:
                ws.append(w)
            for t in wr.r:
                w = self._need(eng, t, False)
                if w:
                    ws.append(w)
        for (sem, val) in ws:
            eng.ops.append(lambda h, sem=sem, val=val: h.wait_ge(sem, val))

    def _record(self, tok, reads, writes):
        for r in reads:
            r.r = [t for t in r.r if t[2] != tok[2] or t[0] is not tok[0]] + [tok]
        for w in writes:
            w.w = tok
            w.r = []

    def op(self, eng, fn, reads=(), writes=(), inc=True):
        self.n_ops += 1
        reads = list(reads)
        writes = list(writes)
        self._waits(eng, reads, writes)
        if not inc:
            eng.ops.append(lambda h, fn=fn: fn(h))
            eng.pend_r += reads
            eng.pend_w += writes
            return
        if eng.cnt >= ROT:
            eng.si += 1
            eng.cnt = 0
        eng.cnt += 1
        sem = eng.sems[eng.si]
        tok = (sem, eng.cnt, eng.name)
        eng.ops.append(lambda h, fn=fn, sem=sem: fn(h).then_inc(sem, 1))
        self._record(tok, reads + eng.pend_r, writes + eng.pend_w)
        eng.pend_r = []
        eng.pend_w = []

    def dma(self, eng, pairs, reads=(), writes=(), **kw):
        self.n_ops += 1
        reads = list(reads)
        writes = list(writes)
        self._waits(eng, reads, writes)
        ent = eng.pool[eng.pi]
        eng.pi = (eng.pi + 1) % len(eng.pool)
        sem = ent[0]
        if ent[1] > 0 and eng.seen.get(id(sem), 0) < ent[1]:
            v = ent[1]
            eng.ops.append(lambda h, sem=sem, v=v: h.wait_ge(sem, v))
            eng.seen[id(sem)] = v
        for (o, i) in pairs:
            ent[1] += 16
            eng.ops.append(lambda h, o=o, i=i, sem=sem: h.dma_start(out=o, in_=i, **kw).then_inc(sem, 16))
        tok = (sem, ent[1], "dma_" + eng.name + str(id(sem)))
        self._record(tok, reads, writes)

    def barrier(self):
        toks = []
        for e in self.engs:
            assert not e.pend_r and not e.pend_w, e.name
            for i in range(e.si + 1):
                v = ROT if i < e.si else e.cnt
                if v > 0:
                    toks.append((e, e.sems[i], v))
            for ent in e.pool:
                if ent[1] > 0:
                    toks.append((None, ent[0], ent[1]))
        for e in self.engs:
            for (own, sem, v) in toks:
                if own is e:
                    continue
                if e.seen.get(id(sem), 0) >= v:
                    continue
                e.seen[id(sem)] = v
                e.ops.append(lambda h, sem=sem, v=v: h.wait_ge(sem, v))

    def finish(self):
        for e in (self.sp, self.pool, self.act):
            for ent in e.pool:
                if ent[1] > 0:
                    self.sp.ops.append(lambda h, sem=ent[0], v=ent[1]: h.wait_ge(sem, v))

    def replay(self):
        nc = self.nc
        with nc.Block() as block:
            @block.tensor
            def _(h):
                for f in self.pe.ops:
                    f(h)

            @block.scalar
            def _(h):
                for f in self.act.ops:
                    f(h)

            @block.vector
            def _(h):
                for f in self.dve.ops:
                    f(h)

            @block.gpsimd
            def _(h):
                for f in self.pool.ops:
                    f(h)

            @block.sync
            def _(h):
                for f in self.sp.ops:
                    f(h)


class Tile:
    def __init__(self, t, name):
        self.t = t
        self.res = Res(name)

    def __getitem__(self, k):
        return self.t[k]


def dram_ap(t, offset, pattern):
    return bass.AP(t.tensor, offset, pattern)


class Builder:
    def __init__(self, T, depth=DEPTH, stop=None, dbg=()):
        self.T = T
        self.depth = depth
        self.stop = stop
        self.dbg = dbg
        self.nc = bass.Bass("TRN2", target_bir_lowering=False)
        self.es = ExitStack()
        self.S = None
        self.dres = {}
        self.rr = {}

    def din(self, name, shape):
        return self.nc.dram_tensor(name, list(shape), F32, kind="ExternalInput").ap()

    def dout(self, name, shape, dt=F32):
        return self.nc.dram_tensor(name, list(shape), dt, kind="ExternalOutput").ap()

    def dscr(self, name, shape, dt=F32):
        if name in self.dbg:
            return self.nc.dram_tensor(name, list(shape), dt, kind="ExternalOutput").ap()
        return self.nc.dram_tensor(name, list(shape), dt).ap()

    def R(self, *key):
        if key not in self.dres:
            self.dres[key] = Res(str(key))
        return self.dres[key]

    def sb(self, name, shape, dt):
        self.uid = getattr(self, "uid", 0) + 1
        name = f"{name}_{self.uid}"
        t = self.cur.enter_context(self.nc.sbuf_tensor(name, list(shape), dt))
        return Tile(t, name)

    def ps(self, name, shape, dt=F32):
        self.uid = getattr(self, "uid", 0) + 1
        name = f"{name}_{self.uid}"
        t = self.cur.enter_context(self.nc.psum_tensor(name, list(shape), dt))
        return Tile(t, name)

    def build(self):
        nc = self.nc
        T = self.T
        L = self.depth
        with self.es as es:
            self.S = S = Sched(nc, es)
            self.x_in = self.din("x", [T, D])
            self.c_in = self.din("c", [D])
            self.ctx_in = self.din("ctx", [LC, D])
            self.cctx_in = self.din("c_ctx", [D])
            self.mod_w = self.din("mod_w", [L, D, NMOD * D])
            self.mod_b = self.din("mod_b", [L, NMOD * D])
            self.norm_g = self.din("norm_g", [L, 3, D])
            self.ffn_w13 = [self.din("ffn1_w13", [L, D, 2 * DFF]), self.din("ffn2_w13", [L, D, 2 * DFF])]
            self.ffn_w2 = [self.din("ffn1_w2", [L, DFF, D]), self.din("ffn2_w2", [L, DFF, D])]
            self.w_in = self.din("w_in", [L, D, D_IN])
            self.attn_q_norm = self.din("attn_q_norm", [L, 64])
            self.attn_k_norm = self.din("attn_k_norm", [L, 64])
            self.attn_sink = self.din("attn_sink", [L, 8])
            self.gla_w2 = self.din("gla_w2", [L, 2, 16, 256])
            self.gla_b = self.din("gla_b", [L, 2, 256])
            self.rope_cs = self.din("rope_cs", [T, 2, 32])
            self.y_out = self.dout("y", [T, D])
            TT = self.TT = LC + T
            self.H2T = [self.dscr(f"H2T{l}", [D, TT], BF16) for l in range(L)]
            self.QT = [self.dscr(f"QT{l}", [64, 8, TT], BF16) for l in range(L)]
            self.KT = [self.dscr(f"KT{l}", [64, 2, TT], BF16) for l in range(L)]
            self.VA = [self.dscr(f"VA{l}", [TT, 128], BF16) for l in range(L)]
            self.GV = [self.dscr(f"GV{l}", [TT, 512], BF16) for l in range(L)]
            self.MV = [self.dscr(f"MV{l}", [TT, 512], BF16) for l in range(L)]
            self.GATES = [self.dscr(f"GATES{l}", [16, TT]) for l in range(L)]
            self.QG = [self.dscr(f"QG{l}", [2, 64, 4, TT], BF16) for l in range(L)]
            self.KG = [self.dscr(f"KG{l}", [2, 64, 4, TT], BF16) for l in range(L)]
            self.KH = [self.dscr(f"KH{l}", [2, 64, 4, TT], BF16) for l in range(L)]
            self.MQK = [self.dscr(f"MQK{l}", [D, TT + 4], BF16) for l in range(L)]
            self.ATT = [self.dscr(f"ATT{l}", [64, 8, TT], BF16) for l in range(L)]
            self.OG = [self.dscr(f"OG{l}", [2, TT, 512]) for l in range(L)]
            self.OM = [self.dscr(f"OM{l}", [2, TT, 512]) for l in range(L)]
            self.MROWS = [self.dscr(f"MROWS{l}", [2, 5, 4, TT]) for l in range(L)]
            self.MQC = [self.dscr(f"MQC{l}", [D, TT], BF16) for l in range(L)]
            self.gla_norm = self.din("gla_norm", [L, 128])
            self.mlstm_norm = self.din("mlstm_norm", [L, 128])
            self.w_out_attn = self.din("w_out_attn", [L, 512, D])
            self.w_out_gla = self.din("w_out_gla", [L, 512, D])
            self.w_out_mlstm = self.din("w_out_mlstm", [L, 512, D])
            self.w_o = self.din("w_o", [L, D, D])
            self.X2 = [self.dscr(f"X2_{l}", [T, D]) for l in range(L)]
            self.C2 = [self.dscr(f"C2_{l}", [LC, D]) for l in range(L)]
            self.X3 = [self.dscr(f"X3_{l}", [T, D]) for l in range(L)]
            self.C3 = [self.dscr(f"C3_{l}", [LC, D]) for l in range(L)]
            self.conv_w = self.din("mlstm_conv_w", [L, 5, D])
            self.conv_b = self.din("mlstm_conv_b", [L, D])
            self.mlstm_ib = self.din("mlstm_ib", [L, 2, 4])
            self.mlstm_fb = self.din("mlstm_fb", [L, 2, 4])
            self.MOD = [self.dscr(f"MOD{l}", [2, NMOD * D]) for l in range(L)]
            self.X1 = [self.dscr(f"X1_{l}", [T, D]) for l in range(L)]
            self.C1 = [self.dscr(f"C1_{l}", [LC, D]) for l in range(L)]
            with ExitStack() as cst:
                self.cur = cst
                self.ident = self.sb("ident", [128, 128], BF16)
                self.identf = self.sb("identf", [128, 128], F32)
                self.eps_col = self.sb("eps_col", [128, 1], F32)
                S.op(S.dve, lambda h: h.memset(self.eps_col[:], EPS), writes=[self.eps_col.res])
                self.make_consts()
                for l in range(L):
                    xin = self.x_in if l == 0 else self.X3[l - 1]
                    cin = self.ctx_in if l == 0 else self.C3[l - 1]
                    with ExitStack() as ph:
                        self.cur = ph
                        self.phase_mod(l)
                        S.barrier()
                    if self.stop == ("mod", l):
                        break
                    with ExitStack() as ph:
                        self.cur = ph
                        self.phase_ffn(l, 0, [("ctx", cin, self.C1[l], LC, 1), ("lat", xin, self.X1[l], T, 0)])
                        S.barrier()
                    if self.stop == ("ffn1", l):
                        break
                    with ExitStack() as lay:
                        self.cur = lay
                        self.EL = self.sb("EL", [64, 4, 2, TT // 64], F32)
                        with ExitStack() as ph:
                            self.cur = ph
                            self.phase_feat(l, [("ctx", self.C1[l], LC, 1, 0, False), ("lat", self.X1[l], T, 0, LC, True)])
                            S.barrier()
                        if self.stop == ("feat", l):
                            break
                        with ExitStack() as ph:
                            self.cur = ph
                            self.phase_attn(l, l < L - 1)
                            S.barrier()
                        if self.stop == ("attn", l):
                            break
                        with ExitStack() as ph:
                            self.cur = ph
                            self.phase_gla(l)
                            S.barrier()
                        if self.stop == ("gla", l):
                            break
                        self.cur = lay
                        self.DEC = self.sb("DEC", [128, 2, 4, TT // 64], F32)
                        self.sel = self.sb("sel", [4, 4, 128], F32)
                        S.op(S.dve, lambda h, sel_t=self.sel: h.tensor_copy(out=sel_t[:], in_=self.identf[0:4, 0:4].unsqueeze(2).to_broadcast([4, 4, 128])),
                             reads=[self.identf.res], writes=[self.sel.res])
                        stop_ml = False
                        for ph_name, ph_fn in (("mlg", self.phase_ml_gates), ("mlc", self.phase_ml_conv), ("mls", self.phase_ml_scan)):
                            with ExitStack() as ph:
                                self.cur = ph
                                ph_fn(l)
                                S.barrier()
                            if self.stop == (ph_name, l):
                                stop_ml = True
                                break
                        if stop_ml:
                            break
                        if self.stop == ("ml", l):
                            break
                    last = (l == L - 1)
                    with ExitStack() as ph:
                        self.cur = ph
                        st = [("lat", self.X1[l], self.X2[l], T, 0, LC)]
                        if not last:
                            st = [("ctx", self.C1[l], self.C2[l], LC, 1, 0)] + st
                        self.phase_merge(l, st)
                        S.barrier()
                    if self.stop == ("merge", l):
                        break
                    with ExitStack() as ph:
                        self.cur = ph
                        xdst = self.y_out if last else self.X3[l]
                        st = [("lat", self.X2[l], xdst, T, 0)]
                        import os
                        if not last and not os.environ.get("NOCTX2"):
                            st = [("ctx", self.C2[l], self.C3[l], LC, 1)] + st
                        self.phase_ffn(l, 1, [(a, b, c, d_, e) for (a, b, c, d_, e) in st])
                        S.barrier()
                    if self.stop == ("ffn2", l):
                        break
                S.finish()
                S.replay()
        return nc

    def make_consts(self):
        S = self.S
        nc = self.nc
        idf = self.identf
        S.op(S.pool, lambda h: h.memset(idf[:], 0.0), writes=[idf.res])
        S.op(S.pool, lambda h: h.affine_select(out=idf[:], in_=idf[:], pattern=[[-1, 128]],
                                                compare_op=ALU.not_equal, fill=1.0, base=0,
                                                channel_multiplier=1),
             reads=[idf.res], writes=[idf.res])
        S.op(S.dve, lambda h: h.tensor_copy(out=self.ident[:], in_=idf[:]), reads=[idf.res], writes=[self.ident.res])

    def phase_mod(self, l):
        S = self.S
        cl = self.sb("cl", [128, 8, 2], F32)
        cs = self.sb("cs", [128, 8, 2], F32)
        S.dma(S.sp, [(cl[:, :, 0], self.c_in.rearrange("(kc p) -> p kc", p=128)),
                     (cl[:, :, 1], self.cctx_in.rearrange("(kc p) -> p kc", p=128))],
              writes=[cl.res], allow_slow_non_contiguous=True)
        S.op(S.act, lambda h: h.activation(out=cs[:], in_=cl[:], func=AF.Silu), reads=[cl.res], writes=[cs.res])
        wm = [self.sb(f"wm{i}", [128, 8, 512], F32) for i in range(2)]
        mb = [self.sb(f"mb{i}", [2, 512], F32) for i in range(2)]
        mo = [self.sb(f"mo{i}", [2, 512], F32) for i in range(2)]
        pm = [self.ps(f"pm{i}", [2, 512]) for i in range(2)]
        mw = self.mod_w[l].rearrange("(kc p) n -> p kc n", p=128)
        for n in range(18):
            i = n % 2
            S.dma(S.sp, [(wm[i][:, 0:4, :], mw[:, 0:4, n * 512:(n + 1) * 512]),
                         (wm[i][:, 4:8, :], mw[:, 4:8, n * 512:(n + 1) * 512])], writes=[wm[i].res])
            mbsrc = self.mod_b[l:l + 1, n * 512:(n + 1) * 512]
            S.dma(S.sp, [(mb[i][0:1, :], mbsrc), (mb[i][1:2, :], mbsrc)], writes=[mb[i].res])
            for kc in range(8):
                S.op(S.pe, lambda h, kc=kc, i=i: h.matmul(pm[i][:], cs[:, kc, :], wm[i][:, kc, :],
                                                           start=(kc == 0), stop=(kc == 7)),
                     reads=[cs.res, wm[i].res], writes=[pm[i].res], inc=(kc == 7))
            S.op(S.dve, lambda h, i=i: h.tensor_tensor(out=mo[i][:], in0=pm[i][:], in1=mb[i][:], op=ALU.add),
                 reads=[pm[i].res, mb[i].res], writes=[mo[i].res])
            S.dma(S.sp, [(self.MOD[l][:, n * 512:(n + 1) * 512], mo[i][:])], reads=[mo[i].res],
                  writes=[self.R("MOD", l)])

    def load_cols(self, dst_ap, src_row_ap, res):
        self.S.dma(self.S.sp, [(dst_ap, src_row_ap.rearrange("(kc p) -> p kc", p=128))], writes=[res],
                   allow_slow_non_contiguous=True)

    def adaln_cols(self, l, j, row, tag):
        S = self.S
        tmp = self.sb(f"adt_{tag}", [128, 3, 8], F32)
        A = self.sb(f"adA_{tag}", [128, 8], F32)
        MODr = self.MOD[l]
        S.dma(S.sp, [(tmp[:, 0, :], MODr[row, (3 * j) * D:(3 * j + 1) * D].rearrange("(kc p) -> p kc", p=128)),
                     (tmp[:, 1, :], MODr[row, (3 * j + 1) * D:(3 * j + 2) * D].rearrange("(kc p) -> p kc", p=128)),
                     (tmp[:, 2, :], self.norm_g[l, j, :].rearrange("(kc p) -> p kc", p=128))],
              reads=[self.R("MOD", l)], writes=[tmp.res], allow_slow_non_contiguous=True)
        S.op(S.dve, lambda h: h.scalar_tensor_tensor(out=A[:], in0=tmp[:, 1, :], scalar=1.0, in1=tmp[:, 2, :],
                                                      op0=ALU.add, op1=ALU.mult),
             reads=[tmp.res], writes=[A.res])
        return A, tmp

    def gate_bc(self, l, j, row, tag, mul):
        S = self.S
        G = self.sb(f"gate_{tag}", [128, D], F32)
        src = self.MOD[l][row:row + 1, (3 * j + 2) * D:(3 * j + 3) * D]
        src_b = dram_ap(src, src.offset, [[0, 128], [1, D]])
        S.dma(S.sp, [(G[:], src_b)], reads=[self.R("MOD", l)], writes=[G.res])
        if mul != 1.0:
            S.op(S.pool, lambda h: h.tensor_scalar(out=G[:], in0=G[:], scalar1=float(mul), scalar2=None, op0=ALU.mult),
                 reads=[G.res], writes=[G.res])
        return G

    def load_weight_bf16(self, dst, src3, nsplit):
        S = self.S
        kcn = dst.t.shape[1]
        step = max(1, kcn // nsplit)
        dst.parts = []
        for k0 in range(0, kcn, step):
            k1 = min(kcn, k0 + step)
            r = Res(f"wpart{k0}")
            dst.parts.append(r)
            S.dma(S.pool, [(dst[:, k0:k1, :], src3[:, k0:k1, :])], writes=[r])

    def norm_part(self, xt, nb, ss, rs, junk=None):
        S = self.S
        if junk is None:
            junk = self.junk
        S.op(S.act, lambda h: h.activation(out=junk[:], in_=xt[:], func=AF.Square, accum_out=ss[:]),
             reads=[xt.res], writes=[junk.res, ss.res])
        S.op(S.act, lambda h: h.activation(out=rs[:], in_=ss[:], func=AF.Sqrt, scale=1.0 / D, bias=self.eps_col[:]),
             reads=[ss.res], writes=[rs.res])
        S.op(S.dve, lambda h: h.reciprocal(out=rs[:], in_=rs[:]), reads=[rs.res], writes=[rs.res])
        S.op(S.dve, lambda h: h.tensor_scalar(out=nb[:], in0=xt[:], scalar1=rs[:], scalar2=None, op0=ALU.mult),
             reads=[xt.res, rs.res], writes=[nb.res])

    def transpose_part(self, nb, pT, hT, col0, A, sh, evac_engs):
        S = self.S
        for kc in range(8):
            S.op(S.pe, lambda h, kc=kc: h.transpose(out=pT[:, kc * 128:(kc + 1) * 128], in_=nb[:, kc * 128:(kc + 1) * 128],
                                                     identity=self.ident[:]),
                 reads=[nb.res, self.ident.res], writes=[pT.res], inc=(kc == 7))
        for kc in range(8):
            e = evac_engs[kc % len(evac_engs)]
            if e is S.act:
                S.op(e, lambda h, kc=kc: h.activation(out=hT[:, kc, col0:col0 + 128], in_=pT[:, kc * 128:(kc + 1) * 128],
                                                      func=AF.Identity, scale=A[:, kc:kc + 1], bias=sh[:, kc:kc + 1]),
                     reads=[pT.res, A.res, self.shres], writes=[hT.res])
            else:
                S.op(e, lambda h, kc=kc: h.tensor_scalar(out=hT[:, kc, col0:col0 + 128], in0=pT[:, kc * 128:(kc + 1) * 128],
                                                         scalar1=A[:, kc:kc + 1], scalar2=sh[:, kc:kc + 1],
                                                         op0=ALU.mult, op1=ALU.add),
                     reads=[pT.res, A.res, self.shres], writes=[hT.res])

    def phase_ffn(self, l, which, streams):
        S = self.S
        j = 0 if which == 0 else 2
        W13 = self.sb("W13", [128, 8, 2 * DFF], BF16)
        W2 = self.sb("W2", [128, 22, D], BF16)
        self.load_weight_bf16(W13, self.ffn_w13[which][l].rearrange("(kc p) n -> p kc n", p=128), 8)
        self.load_weight_bf16(W2, self.ffn_w2[which][l].rearrange("(fc p) n -> p fc n", p=128), 11)
        import os
        if os.environ.get("FFN_WONLY") and which == 1:
            return
        xl = [self.sb(f"xl{i}", [128, D], F32) for i in range(3)]
        xr = [self.sb(f"xr{i}", [128, D], F32) for i in range(2)]
        nb = [self.sb(f"nb{i}", [128, D], BF16) for i in range(4)]
        ss = [self.sb(f"ss{i}", [128, 1], F32) for i in range(4)]
        rs = [self.sb(f"rs{i}", [128, 1], F32) for i in range(4)]
        hT = self.sb("hT", [128, 8, 512], BF16)
        gT = self.sb("gT", [128, 22, 512], BF16)
        sa = [self.sb(f"sa{i}", [128, 512], F32) for i in range(2)]
        tt = [self.sb(f"tt{i}", [128, 512], F32) for i in range(2)]
        pT = [self.ps(f"pT{i}", [128, D], BF16) for i in range(2)]
        pA = [self.ps(f"pA{i}", [128, 512]) for i in range(2)]
        pB = [self.ps(f"pB{i}", [128, 512]) for i in range(2)]
        pY = [self.ps(f"pY{i}", [128, 512]) for i in range(2)]
        cnt = {"xl": 0, "xr": 0, "nb": 0, "pT": 0, "pAB": 0, "sa": 0, "tt": 0}

        for (tag, src, dst, ntok, row) in streams:
            A, tmp = self.adaln_cols(l, j, row, f"{which}{tag}")
            sh = tmp[:, 0, :]
            self.shres = tmp.res
            G = self.gate_bc(l, j, row, f"{which}{tag}", 0.5)
            tiles = [(t0, min(512, ntok - t0)) for t0 in range(0, ntok, 512)]
            rtag = ("xs", l, which, tag)

            def prep_norm(t0, s):
                i = cnt["xl"] % 3
                cnt["xl"] += 1
                k = cnt["nb"] % 4
                cnt["nb"] += 1
                S.dma(S.sp, [(xl[i][:], src[t0 + s * 128:t0 + (s + 1) * 128, :])], reads=[self.R(src.name)],
                      writes=[xl[i].res])
                self.norm_part(xl[i], nb[k], ss[k], rs[k], junk=nb[k])
                return nb[k]

            def prep_tr(nbt, s):
                k = cnt["pT"] % 2
                cnt["pT"] += 1
                self.transpose_part(nbt, pT[k], hT, s * 128, A, sh, [S.act, S.dve])

            def prep(t0, n):
                for s in range(n // 128):
                    nbt = prep_norm(t0, s)
                    prep_tr(nbt, s)

            prep(*tiles[0])
            for ti, (t0, n) in enumerate(tiles):
                nt = n // 128
                for p in range(22):
                    k = cnt["pAB"] % 2
                    cnt["pAB"] += 1
                    for kc in range(8):
                        S.op(S.pe, lambda h, kc=kc, p=p, k=k, n=n: h.matmul(pA[k][:, :n], W13[:, kc, p * 128:(p + 1) * 128], hT[:, kc, :n],
                                                                        start=(kc == 0), stop=(kc == 7)),
                             reads=W13.parts + [hT.res], writes=[pA[k].res], inc=(kc == 7))
                    for kc in range(8):
                        S.op(S.pe, lambda h, kc=kc, p=p, k=k, n=n: h.matmul(pB[k][:, :n], W13[:, kc, DFF + p * 128:DFF + (p + 1) * 128], hT[:, kc, :n],
                                                                        start=(kc == 0), stop=(kc == 7)),
                             reads=W13.parts + [hT.res], writes=[pB[k].res], inc=(kc == 7))
                    q = cnt["sa"] % 2
                    cnt["sa"] += 1
                    S.op(S.act, lambda h, k=k, q=q, n=n: h.activation(out=sa[q][:, :n], in_=pA[k][:, :n], func=AF.Silu),
                         reads=[pA[k].res], writes=[sa[q].res])
                    S.op(S.dve, lambda h, k=k, q=q, p=p, n=n: h.tensor_tensor(out=gT[:, p, :n], in0=sa[q][:, :n], in1=pB[k][:, :n], op=ALU.mult),
                         reads=[sa[q].res, pB[k].res], writes=[gT.res])
                xrs = []
                for s in range(nt):
                    pass
                if ti + 1 < len(tiles):
                    pending = tiles[ti + 1]
                else:
                    pending = None
                nbts = []
                if pending is not None:
                    for s in range(pending[1] // 128):
                        nbts.append(prep_norm(pending[0], s))
                for s in range(nt):
                    i = cnt["xr"] % 2
                    cnt["xr"] += 1
                    S.dma(S.sp, [(xr[i][:], src[t0 + s * 128:t0 + (s + 1) * 128, :])], reads=[self.R(src.name)],
                          writes=[xr[i].res])
                    for dh in range(2):
                        for fc in range(22):
                            S.op(S.pe, lambda h, fc=fc, dh=dh, s=s: h.matmul(pY[dh][:], gT[:, fc, s * 128:(s + 1) * 128], W2[:, fc, dh * 512:(dh + 1) * 512],
                                                                              start=(fc == 0), stop=(fc == 21)),
                                 reads=[gT.res] + W2.parts, writes=[pY[dh].res], inc=(fc == 21))
                    for dh in range(2):
                        q = cnt["tt"] % 2
                        cnt["tt"] += 1
                        S.op(S.dve, lambda h, dh=dh, q=q, G=G: h.tensor_tensor(out=tt[q][:], in0=pY[dh][:], in1=G[:, dh * 512:(dh + 1) * 512], op=ALU.mult),
                             reads=[pY[dh].res, G.res], writes=[tt[q].res])
                        S.op(S.pool, lambda h, dh=dh, q=q, i=i: h.tensor_tensor(out=xr[i][:, dh * 512:(dh + 1) * 512], in0=xr[i][:, dh * 512:(dh + 1) * 512],
                                                                                 in1=tt[q][:], op=ALU.add),
                             reads=[tt[q].res, xr[i].res], writes=[xr[i].res])
                    S.dma(S.pool, [(dst[t0 + s * 128:t0 + (s + 1) * 128, :], xr[i][:])], reads=[xr[i].res],
                          writes=[self.R(dst.name)])
                for s, nbt in enumerate(nbts):
                    prep_tr(nbt, s)


O_AQ, O_AK, O_AV = 0, 512, 640
O_GQ, O_GK, O_GV, O_GR, O_GG = 768, 1024, 1280, 1792, 2304
O_MQ, O_MK, O_MV, O_MO, O_MI, O_MF = 2336, 2848, 3360, 3872, 4384, 4392
O_SA, O_SG, O_SM = 4400, 5424, 6448


def bcast_rows(ap2d, nparts):
    return bass.AP(ap2d.tensor, ap2d.offset, [[0, nparts]] + [list(x) for x in ap2d.ap[1:]])


def rev_last(ap):
    pat = [list(x) for x in ap.ap]
    st, n = pat[-1]
    return bass.AP(ap.tensor, ap.offset + st * (n - 1), pat[:-1] + [[-st, n]])


def phase_feat(self, l, streams):
    S = self.S
    TT = self.TT
    win = self.w_in[l].rearrange("(kc p) n -> p kc n", p=128)
    Wa = self.sb("Wa", [128, 8, 768], BF16)
    Wv = self.sb("Wv", [128, 8, 1024], BF16)
    Wf = self.sb("Wf", [128, 8, 1536], BF16)
    Wg = self.sb("Wg", [128, 8, 48], BF16)
    S.dma(S.pool, [(Wa[:, 0:4, :], win[:, 0:4, 0:768]), (Wa[:, 4:8, :], win[:, 4:8, 0:768])], writes=[Wa.res])
    for k0 in range(0, 8, 2):
        S.dma(S.pool, [(Wv[:, k0:k0 + 2, 0:512], win[:, k0:k0 + 2, O_GV:O_GV + 512]),
                       (Wv[:, k0:k0 + 2, 512:1024], win[:, k0:k0 + 2, O_MV:O_MV + 512])], writes=[Wv.res])
        S.dma(S.pool, [(Wf[:, k0:k0 + 2, 0:512], win[:, k0:k0 + 2, O_GQ:O_GQ + 512]),
                       (Wf[:, k0:k0 + 2, 512:1536], win[:, k0:k0 + 2, O_MQ:O_MQ + 1024])], writes=[Wf.res])
    S.dma(S.pool, [(Wg[:, :, 0:32], win[:, :, O_GG:O_GG + 32]), (Wg[:, :, 32:48], win[:, :, O_MI:O_MI + 16])], writes=[Wg.res])
    W2p = self.sb("W2p", [32, 2, 256], F32)
    S.op(S.dve, lambda h: h.memset(W2p[:], 0.0), writes=[W2p.res])
    S.dma(S.sp, [(W2p[0:16, 0, :], self.gla_w2[l, 0]), (W2p[16:32, 1, :], self.gla_w2[l, 1])], writes=[W2p.res])
    negb = self.sb("negb", [128, 2, 2], F32)
    S.dma(S.sp, [(negb[:, d, :], self.gla_b[l, d, :].rearrange("(c p) -> p c", p=128)) for d in range(2)],
          writes=[negb.res], allow_slow_non_contiguous=True)
    S.op(S.dve, lambda h: h.tensor_scalar(out=negb[:], in0=negb[:], scalar1=-1.0, scalar2=None, op0=ALU.mult),
         reads=[negb.res], writes=[negb.res])
    gain = self.sb("gain", [128, 10, 64], F32)
    qn_src = self.attn_q_norm[l:l + 1, :]
    kn_src = self.attn_k_norm[l:l + 1, :]
    S.dma(S.sp, [(gain[:, 0:8, :], bass.AP(qn_src.tensor, qn_src.offset, [[0, 128], [0, 8], [1, 64]])),
                 (gain[:, 8:10, :], bass.AP(kn_src.tensor, kn_src.offset, [[0, 128], [0, 2], [1, 64]]))],
          writes=[gain.res])
    mask01 = self.sb("mask01", [128, 8, 64], F32)
    S.op(S.pool, lambda h: h.memset(mask01[:], 1.0), writes=[mask01.res])
    S.op(S.pool, lambda h: h.memset(mask01[:, :, 0:1], 0.0), writes=[mask01.res])
    self.junk = self.sb("junk", [128, D], BF16)

    xl = [self.sb(f"xl{i}", [128, D], F32) for i in range(3)]
    nb = [self.sb(f"nb{i}", [128, D], BF16) for i in range(2)]
    ss = [self.sb(f"ss{i}", [128, 1], F32) for i in range(2)]
    rs = [self.sb(f"rs{i}", [128, 1], F32) for i in range(2)]
    hTs = [self.sb(f"hT{i}", [128, 8, 512], BF16) for i in range(2)]
    sqt = self.sb("sqt", [128, 640], F32)
    ssh = self.sb("ssh", [128, 10], F32)
    rinv = self.sb("rinv", [128, 10], F32)
    qn = self.sb("qn", [128, 10, 64], F32)
    rt = [self.sb(f"rt{i}", [128, 10, 32], F32) for i in range(4)]
    cs_t = [self.sb(f"cst{i}", [128, 2, 32], F32) for i in range(3)]
    qr = [self.sb(f"qr{i}", [128, 10, 64], BF16) for i in range(2)]
    vb = [self.sb(f"vb{i}", [128, 128], BF16) for i in range(4)]
    vb2 = [self.sb(f"vb2{i}", [128, 512], BF16) for i in range(4)]
    QTs = self.sb("QTs", [64, 8, 512], BF16)
    KTs = self.sb("KTs", [64, 2, 512], BF16)
    ggT = self.sb("ggT", [32, 512], F32)
    gts = self.sb("gts", [16, 512], F32)
    ex = [self.sb(f"ex{i}", [128, 512], F32) for i in range(2)]
    csum = [self.sb(f"csum{i}", [128, 512], F32) for i in range(2)]
    eb = [[self.sb(f"eb{d}{c}", [128, 512], F32) for c in range(2)] for d in range(2)]
    enb = [[self.sb(f"enb{d}{c}", [128, 512], F32) for c in range(2)] for d in range(2)]
    ebl = [[self.sb(f"ebl{d}{c}", [128, 512], F32) for c in range(2)] for d in range(2)]
    fo = [self.sb(f"fo{i}", [128, 512], BF16) for i in range(8)]
    elcs = [self.sb(f"elc{i}", [128, 8], F32) for i in range(2)]
    pT = self.ps("pT", [128, D], BF16)
    pq = self.ps("pq", [128, 512])
    pkv = self.ps("pkv", [128, 256])
    pqt = self.ps("pqt", [64, 8, 128], BF16)
    pkt = self.ps("pkt", [64, 2, 128], BF16)
    pf = [self.ps(f"pf{i}", [128, 512]) for i in range(2)]
    pz = self.ps("pz", [128, 512])
    cnt = {"xl": 0, "pf": 0, "fo": 0, "qr": 0, "vb": 0, "vb2": 0, "ex": 0, "rt": 0, "cs": 0, "elc": 0}
    H2Tv = self.H2T[l].rearrange("(kc p) t -> p kc t", p=128)

    def nxt(key, n):
        v = cnt[key] % n
        cnt[key] += 1
        return v

    st_info = []
    for (tag, src, ntok, row, uoff, rope) in streams:
        A, tmp = self.adaln_cols(l, 1, row, f"f{tag}")
        st_info.append((A, tmp, src, uoff, rope))
    tiles = []
    for si, (tag, src, ntok, row, uoff, rope) in enumerate(streams):
        for t0 in range(0, ntok, 512):
            tiles.append((si, t0, min(512, ntok - t0)))

    def prep_load(k, s):
        si, t0, n = tiles[k]
        A, tmp, src, uoff, rope = st_info[si]
        i = nxt("xl", 3)
        S.dma(S.sp, [(xl[i][:], src[t0 + s * 128:t0 + (s + 1) * 128, :])], reads=[self.R(src.name)], writes=[xl[i].res])
        cst = None
        return i

    def rope_load(k, s):
        si, t0, n = tiles[k]
        A, tmp, src, uoff, rope = st_info[si]
        if not rope:
            return None
        cst = cs_t[nxt("cs", 3)]
        S.dma(S.sp, [(cst[:], self.rope_cs[t0 + s * 128:t0 + s * 128 + 128, :, :])], writes=[cst.res])
        return cst

    def prep_sub(k, s, i=None):
        si, t0, n = tiles[k]
        A, tmp, src, uoff, rope = st_info[si]
        hT = hTs[k % 2]
        if i is None:
            i = prep_load(k, s)
        self.norm_part(xl[i], nb[i % 2], ss[i % 2], rs[i % 2])
        self.shres = tmp.res
        self.transpose_part(nb[i % 2], pT, hT, s * 128, A, tmp[:, 0, :], [S.act, S.dve])

    def g_part(k):
        si, t0, n = tiles[k]
        A, tmp, src, uoff, rope = st_info[si]
        hT = hTs[k % 2]
        u0 = uoff + t0
        nch = n // 64
        for kc in range(8):
            S.op(S.pe, lambda h, kc=kc, n=n, hT=hT: h.matmul(pz[0:32, :n], Wg[:, kc, 0:32], hT[:, kc, :n], start=(kc == 0), stop=(kc == 7)),
                 reads=[hT.res, Wg.res], writes=[pz.res], inc=(kc == 7))
        S.op(S.act, lambda h, n=n: h.activation(out=ggT[:, :n], in_=pz[0:32, :n], func=AF.Copy), reads=[pz.res], writes=[ggT.res])
        for kc in range(8):
            S.op(S.pe, lambda h, kc=kc, n=n, hT=hT: h.matmul(pz[0:16, :n], Wg[:, kc, 32:48], hT[:, kc, :n], start=(kc == 0), stop=(kc == 7)),
                 reads=[hT.res, Wg.res], writes=[pz.res], inc=(kc == 7))
        S.op(S.act, lambda h, n=n: h.activation(out=gts[:, :n], in_=pz[0:16, :n], func=AF.Copy), reads=[pz.res], writes=[gts.res])
        S.dma(S.sp, [(self.GATES[l][:, u0:u0 + n], gts[:, :n])], reads=[gts.res], writes=[self.R(self.GATES[l].name)])
        for d in range(2):
            for c2 in range(2):
                S.op(S.pe, lambda h, d=d, c2=c2, n=n: h.matmul(pz[:, :n], W2p[:, d, c2 * 128:(c2 + 1) * 128], ggT[:, :n], start=True, stop=True),
                     reads=[W2p.res, ggT.res], writes=[pz.res])
                e_ = ex[nxt("ex", 2)]
                c_ = csum[(cnt["ex"]) % 2]
                S.op(S.act, lambda h, d=d, c2=c2, n=n, e_=e_: h.activation(out=e_[:, :n], in_=pz[:, :n], func=AF.Exp, scale=-1.0, bias=negb[:, d, c2:c2 + 1]),
                     reads=[pz.res, negb.res], writes=[e_.res])
                S.op(S.act, lambda h, n=n, e_=e_: h.activation(out=e_[:, :n], in_=e_[:, :n], func=AF.Ln, bias=1.0), reads=[e_.res], writes=[e_.res])
                m01 = mask01[:].rearrange("p a b -> p (a b)")[:, :n]
                if d == 0:
                    S.op(S.dve, lambda h, n=n, e_=e_, c_=c_, m01=m01: h.tensor_tensor_scan(out=c_[:, :n], data0=m01, data1=e_[:, :n], initial=0.0, op0=ALU.mult, op1=ALU.add),
                         reads=[e_.res, mask01.res], writes=[c_.res])
                    last = 63
                else:
                    S.op(S.dve, lambda h, n=n, e_=e_, c_=c_, m01=m01: h.tensor_tensor_scan(out=rev_last(c_[:, :n]), data0=m01, data1=rev_last(e_[:, :n]), initial=0.0,
                                                                                         op0=ALU.mult, op1=ALU.add),
                         reads=[e_.res, mask01.res], writes=[c_.res])
                    last = 0
                EB, ENB, EBL = eb[d][c2], enb[d][c2], ebl[d][c2]
                S.op(S.act, lambda h, n=n, c_=c_, EB=EB: h.activation(out=EB[:, :n], in_=c_[:, :n], func=AF.Exp, scale=-1.0 / 16), reads=[c_.res], writes=[EB.res])
                S.op(S.act, lambda h, n=n, c_=c_, ENB=ENB: h.activation(out=ENB[:, :n], in_=c_[:, :n], func=AF.Exp, scale=1.0 / 16), reads=[c_.res], writes=[ENB.res])
                c3 = c_[:, :n].rearrange("p (a b) -> p a b", b=64)
                S.op(S.pool, lambda h, n=n, c_=c_, c3=c3, last=last, nch=nch: h.tensor_tensor(out=c3, in0=c3, in1=c3[:, :, last:last + 1].to_broadcast([128, nch, 64]), op=ALU.subtract),
                     reads=[c_.res], writes=[c_.res])
                S.op(S.act, lambda h, n=n, c_=c_, EBL=EBL: h.activation(out=EBL[:, :n], in_=c_[:, :n], func=AF.Exp, scale=1.0 / 16), reads=[c_.res], writes=[EBL.res])
                ch0 = u0 // 64
                elc = elcs[nxt("elc", 2)]
                S.op(S.pool, lambda h, n=n, EB=EB, last=last, elc=elc, nch=nch: h.tensor_copy(
                    out=elc[:, :nch], in_=EB[:, :n].rearrange("p (a b) -> p a b", b=64)[:, :, last]),
                     reads=[EB.res], writes=[elc.res])
                S.dma(S.sp, [(self.EL[:, 2 * c2 + hh2, d, ch0:ch0 + nch], elc[hh2 * 64:(hh2 + 1) * 64, :nch]) for hh2 in range(2)],
                      reads=[elc.res], writes=[self.EL.res])

    def a_mm(k, s):
        hT = hTs[k % 2]
        c0 = s * 128
        for kc in range(8):
            S.op(S.pe, lambda h, kc=kc, c0=c0, hT=hT: h.matmul(pq[:], hT[:, kc, c0:c0 + 128], Wa[:, kc, 0:512], start=(kc == 0), stop=(kc == 7)),
                 reads=[hT.res, Wa.res], writes=[pq.res], inc=(kc == 7))
        for kc in range(8):
            S.op(S.pe, lambda h, kc=kc, c0=c0, hT=hT: h.matmul(pkv[:], hT[:, kc, c0:c0 + 128], Wa[:, kc, 512:768], start=(kc == 0), stop=(kc == 7)),
                 reads=[hT.res, Wa.res], writes=[pkv.res], inc=(kc == 7))

    def a_chain(k, s, cst=None):
        si, t0, n = tiles[k]
        A, tmp, src, uoff, rope = st_info[si]
        u0 = uoff + t0
        c0 = s * 128
        S.op(S.act, lambda h: h.activation(out=sqt[:, 0:512], in_=pq[:], func=AF.Square), reads=[pq.res], writes=[sqt.res])
        S.op(S.act, lambda h: h.activation(out=sqt[:, 512:640], in_=pkv[:, 0:128], func=AF.Square), reads=[pkv.res], writes=[sqt.res])
        vi = nxt("vb", 4)
        S.op(S.act, lambda h, vi=vi: h.activation(out=vb[vi][:], in_=pkv[:, 128:256], func=AF.Copy), reads=[pkv.res], writes=[vb[vi].res])
        S.dma(S.sp, [(self.VA[l][u0 + c0:u0 + c0 + 128, :], vb[vi][:])], reads=[vb[vi].res], writes=[self.R(self.VA[l].name)])
        S.op(S.dve, lambda h: h.tensor_reduce(out=ssh[:], in_=sqt[:].rearrange("p (a b) -> p a b", b=64), axis=AX.X, op=ALU.add),
             reads=[sqt.res], writes=[ssh.res])
        S.op(S.act, lambda h: h.activation(out=rinv[:], in_=ssh[:], func=AF.Sqrt, scale=1.0 / 64, bias=self.eps_col[:]),
             reads=[ssh.res, self.eps_col.res], writes=[rinv.res])
        S.op(S.dve, lambda h: h.reciprocal(out=rinv[:], in_=rinv[:]), reads=[rinv.res], writes=[rinv.res])
        S.op(S.dve, lambda h: h.tensor_tensor(out=qn[:, 0:8, :], in0=pq[:].rearrange("p (a b) -> p a b", b=64),
                                               in1=rinv[:, 0:8].unsqueeze(2).to_broadcast([128, 8, 64]), op=ALU.mult),
             reads=[pq.res, rinv.res], writes=[qn.res])
        S.op(S.dve, lambda h: h.tensor_tensor(out=qn[:, 8:10, :], in0=pkv[:, 0:128].rearrange("p (a b) -> p a b", b=64),
                                               in1=rinv[:, 8:10].unsqueeze(2).to_broadcast([128, 2, 64]), op=ALU.mult),
             reads=[pkv.res, rinv.res], writes=[qn.res])
        S.op(S.pool, lambda h: h.tensor_tensor(out=qn[:], in0=qn[:], in1=gain[:], op=ALU.mult),
             reads=[qn.res, gain.res], writes=[qn.res])
        q_ = qr[nxt("qr", 2)]
        if rope:
            cosb = cst[:, 0:1, :].to_broadcast([128, 10, 32])
            sinb = cst[:, 1:2, :].to_broadcast([128, 10, 32])
            x1 = qn[:, :, 0:32]
            x2 = qn[:, :, 32:64]
            r = [rt[nxt("rt", 4)] for _ in range(4)]
            S.op(S.pool, lambda h, r=r, cosb=cosb, x1=x1: h.tensor_tensor(out=r[0][:], in0=x1, in1=cosb, op=ALU.mult),
                 reads=[qn.res, cst.res], writes=[r[0].res])
            S.op(S.dve, lambda h, r=r, sinb=sinb, x2=x2: h.tensor_tensor(out=r[1][:], in0=x2, in1=sinb, op=ALU.mult),
                 reads=[qn.res, cst.res], writes=[r[1].res])
            S.op(S.dve, lambda h, r=r, q_=q_: h.tensor_tensor(out=q_[:, :, 0:32], in0=r[0][:], in1=r[1][:], op=ALU.subtract),
                 reads=[r[0].res, r[1].res], writes=[q_.res])
            S.op(S.pool, lambda h, r=r, sinb=sinb, x1=x1: h.tensor_tensor(out=r[2][:], in0=x1, in1=sinb, op=ALU.mult),
                 reads=[qn.res, cst.res], writes=[r[2].res])
            S.op(S.dve, lambda h, r=r, cosb=cosb, x2=x2: h.tensor_tensor(out=r[3][:], in0=x2, in1=cosb, op=ALU.mult),
                 reads=[qn.res, cst.res], writes=[r[3].res])
            S.op(S.dve, lambda h, r=r, q_=q_: h.tensor_tensor(out=q_[:, :, 32:64], in0=r[2][:], in1=r[3][:], op=ALU.add),
                 reads=[r[2].res, r[3].res], writes=[q_.res])
        else:
            S.op(S.dve, lambda h, q_=q_: h.tensor_copy(out=q_[:], in_=qn[:]), reads=[qn.res], writes=[q_.res])
        return q_

    def a_tr(q_, s):
        c0 = s * 128
        for hh in range(8):
            S.op(S.pe, lambda h, hh=hh, q_=q_: h.transpose(out=pqt[:, hh, :], in_=q_[:, hh, :], identity=self.ident[:]),
                 reads=[q_.res, self.ident.res], writes=[pqt.res], inc=(hh == 7))
        for hh in range(2):
            S.op(S.pe, lambda h, hh=hh, q_=q_: h.transpose(out=pkt[:, hh, :], in_=q_[:, 8 + hh, :], identity=self.ident[:]),
                 reads=[q_.res, self.ident.res], writes=[pkt.res], inc=(hh == 1))
        S.op(S.act, lambda h, c0=c0: h.activation(out=QTs[:, :, c0:c0 + 128], in_=pqt[:], func=AF.Copy), reads=[pqt.res], writes=[QTs.res])
        S.op(S.dve, lambda h, c0=c0: h.tensor_copy(out=KTs[:, :, c0:c0 + 128], in_=pkt[:]), reads=[pkt.res], writes=[KTs.res])

    def b_part(k, s):
        si, t0, n = tiles[k]
        uoff = st_info[si][3]
        u0 = uoff + t0
        hT = hTs[k % 2]
        c0 = s * 128
        for half in range(2):
            kk = nxt("pf", 2)
            for kc in range(8):
                S.op(S.pe, lambda h, kc=kc, c0=c0, kk=kk, half=half, hT=hT: h.matmul(pf[kk][:], hT[:, kc, c0:c0 + 128], Wv[:, kc, half * 512:(half + 1) * 512],
                                                                                  start=(kc == 0), stop=(kc == 7)),
                     reads=[hT.res, Wv.res], writes=[pf[kk].res], inc=(kc == 7))
            vi = nxt("vb2", 4)
            if half == 0:
                S.op(S.act, lambda h, kk=kk, vi=vi: h.activation(out=vb2[vi][:], in_=pf[kk][:], func=AF.Copy), reads=[pf[kk].res], writes=[vb2[vi].res])
            else:
                S.op(S.dve, lambda h, kk=kk, vi=vi: h.tensor_copy(out=vb2[vi][:], in_=pf[kk][:]), reads=[pf[kk].res], writes=[vb2[vi].res])
            dstv = self.GV[l] if half == 0 else self.MV[l]
            S.dma(S.sp, [(dstv[u0 + c0:u0 + c0 + 128, :], vb2[vi][:])], reads=[vb2[vi].res], writes=[self.R(dstv.name)])

    def c_part(k, fcs):
        si, t0, n = tiles[k]
        uoff = st_info[si][3]
        u0 = uoff + t0
        hT = hTs[k % 2]
        for fc in fcs:
            kk = nxt("pf", 2)
            for kc in range(8):
                S.op(S.pe, lambda h, kc=kc, fc=fc, kk=kk, n=n, hT=hT: h.matmul(pf[kk][:, :n], Wf[:, kc, fc * 128:(fc + 1) * 128], hT[:, kc, :n], start=(kc == 0), stop=(kc == 7)),
                     reads=[hT.res, Wf.res], writes=[pf[kk].res], inc=(kc == 7))
            if fc < 2:
                for d in range(2):
                    o_ = fo[nxt("fo", 8)]
                    S.op(S.dve, lambda h, kk=kk, n=n, d=d, fc=fc, o_=o_: h.scalar_tensor_tensor(out=o_[:, :n], in0=pf[kk][:, :n], scalar=0.125, in1=eb[d][fc][:, :n],
                                                                                           op0=ALU.mult, op1=ALU.mult),
                         reads=[pf[kk].res, eb[d][fc].res], writes=[o_.res])
                    S.dma(S.pool, [(self.QG[l][d, :, 2 * fc + hh2, u0:u0 + n], o_[hh2 * 64:(hh2 + 1) * 64, :n]) for hh2 in range(2)], reads=[o_.res], writes=[self.R(self.QG[l].name)])
            elif fc < 4:
                c2 = fc - 2
                for d in range(2):
                    o_ = fo[nxt("fo", 8)]
                    S.op(S.dve, lambda h, kk=kk, n=n, d=d, c2=c2, o_=o_: h.tensor_tensor(out=o_[:, :n], in0=pf[kk][:, :n], in1=enb[d][c2][:, :n], op=ALU.mult),
                         reads=[pf[kk].res, enb[d][c2].res], writes=[o_.res])
                    S.dma(S.pool, [(self.KG[l][d, :, 2 * c2 + hh2, u0:u0 + n], o_[hh2 * 64:(hh2 + 1) * 64, :n]) for hh2 in range(2)], reads=[o_.res], writes=[self.R(self.KG[l].name)])
                    o_ = fo[nxt("fo", 8)]
                    S.op(S.dve, lambda h, kk=kk, n=n, d=d, c2=c2, o_=o_: h.tensor_tensor(out=o_[:, :n], in0=pf[kk][:, :n], in1=ebl[d][c2][:, :n], op=ALU.mult),
                         reads=[pf[kk].res, ebl[d][c2].res], writes=[o_.res])
                    S.dma(S.pool, [(self.KH[l][d, :, 2 * c2 + hh2, u0:u0 + n], o_[hh2 * 64:(hh2 + 1) * 64, :n]) for hh2 in range(2)], reads=[o_.res], writes=[self.R(self.KH[l].name)])
            else:
                o_ = fo[nxt("fo", 8)]
                S.op(S.act, lambda h, kk=kk, n=n, o_=o_: h.activation(out=o_[:, :n], in_=pf[kk][:, :n], func=AF.Copy), reads=[pf[kk].res], writes=[o_.res])
                r0 = (fc - 4) * 128
                S.dma(S.pool, [(self.MQK[l][r0:r0 + 128, 2 + u0:2 + u0 + n], o_[:, :n])], reads=[o_.res], writes=[self.R(self.MQK[l].name)])

    for s in range(tiles[0][2] // 128):
        prep_sub(0, s)
    for k, (si, t0, n) in enumerate(tiles):
        A, tmp, src, uoff, rope_ = st_info[si]
        rope = rope_
        nt = n // 128
        u0 = uoff + t0
        hT = hTs[k % 2]
        S.dma(S.sp, [(H2Tv[:, :, u0:u0 + n], hT[:, :, :n])], reads=[hT.res], writes=[self.R(self.H2T[l].name)])
        g_part(k)
        order = [4, 5, 6, 7, 8, 9, 10, 11, 0, 1, 2, 3]
        per = (12 + nt - 1) // nt
        pend = None
        nxt_nt = tiles[k + 1][2] // 128 if k + 1 < len(tiles) else 0
        for s in range(nt):
            xi = prep_load(k + 1, s) if s < nxt_nt else None
            cst = rope_load(k, s)
            a_mm(k, s)
            q_ = a_chain(k, s, cst)
            if pend is not None:
                a_tr(*pend)
            pend = (q_, s)
            b_part(k, s)
            c_part(k, order[s * per:(s + 1) * per])
            if s < nxt_nt:
                prep_sub(k + 1, s, xi)
        a_tr(*pend)
        for s in range(nt, nxt_nt):
            prep_sub(k + 1, s)
        S.dma(S.sp, [(self.QT[l][:, :, u0:u0 + n], QTs[:, :, :n])], reads=[QTs.res], writes=[self.R(self.QT[l].name)])
        S.dma(S.sp, [(self.KT[l][:, :, u0:u0 + n], KTs[:, :, :n])], reads=[KTs.res], writes=[self.R(self.KT[l].name)])


Builder.phase_feat = phase_feat


def phase_attn(self, l, do_ctx):
    S = self.S
    T = self.T
    nbk = T // 128
    ones = self.sb("ones", [128, 128], F32)
    S.op(S.pool, lambda h: h.memset(ones[:], 1.0), writes=[ones.res])
    mP = self.sb("mP", [128, 4, 128], BF16)
    mN = self.sb("mN", [128, 4, 128], BF16)
    mtmp = self.sb("mtmp", [128, 128], F32)
    zer = self.sb("zer", [128, 128], F32)
    S.op(S.pool, lambda h: h.memset(zer[:], 0.0), writes=[zer.res])
    for (m_, sgn) in ((mP, 1), (mN, -1)):
        S.op(S.pool, lambda h, sgn=sgn: h.affine_select(out=mtmp[:], in_=zer[:], pattern=[[-sgn, 128]], compare_op=ALU.is_ge, fill=-30000.0,
                                                         base=0, channel_multiplier=sgn), reads=[zer.res], writes=[mtmp.res])
        S.op(S.pool, lambda h, m_=m_: h.tensor_copy(out=m_[:], in_=mtmp[:].unsqueeze(1).to_broadcast([128, 4, 128])), reads=[mtmp.res], writes=[m_.res])
    esk = self.sb("esk", [128, 2, 4, 128], F32)
    sk8 = self.sb("sk8", [128, 8], F32)
    S.dma(S.sp, [(sk8[64:65, :], self.attn_sink[l:l + 1, :])], writes=[sk8.res])
    S.op(S.act, lambda h: h.activation(out=sk8[64:65, :], in_=sk8[64:65, :], func=AF.Exp), reads=[sk8.res], writes=[sk8.res])
    S.op(S.dve, lambda h: h.tensor_copy(out=esk[64:65].rearrange("p g a b -> p (g a) b"), in_=sk8[64:65, :].unsqueeze(2).to_broadcast([1, 8, 128])),
         reads=[sk8.res], writes=[esk.res])
    KTc = self.sb("KTc", [64, 2, 256], BF16)
    S.dma(S.sp, [(KTc[:], self.KT[l][:, :, 0:256])], reads=[self.R(self.KT[l].name)], writes=[KTc.res])
    Vc = [self.sb(f"Vc{j}", [128, 2, 65], BF16) for j in range(2)]
    Vb = [self.sb(f"Vb{j}", [128, 2, 65], BF16) for j in range(4)]
    KTb = [self.sb(f"KTb{j}", [64, 2, 128], BF16) for j in range(4)]
    for v in Vc + Vb:
        S.op(S.pool, lambda h, v=v: h.memset(v[:], 1.0), writes=[v.res])
    for j in range(2):
        S.dma(S.sp, [(Vc[j][:, :, 0:64], self.VA[l][j * 128:(j + 1) * 128, :].rearrange("p (g d) -> p g d", d=64))],
              reads=[self.R(self.VA[l].name)], writes=[Vc[j].res])
    QTb = [self.sb(f"QTb{j}", [64, 8, 128], BF16) for j in range(2)]
    E = [self.sb(f"E{j}", [128, 4, 128], BF16) for j in range(4)]
    dn = [self.sb(f"dn{j}", [128, 512], F32) for j in range(2)]
    bcs = [self.sb(f"bcs{j}", [64, 512], F32) for j in range(2)]
    aT = [self.sb(f"aT{j}", [64, 4, 128], BF16) for j in range(2)]
    pS = [self.ps(f"pS{j}", [128, 512]) for j in range(3)]
    pO = [self.ps(f"pO{j}", [128, 512]) for j in range(2)]
    pB = [self.ps(f"pB{j}", [64, 512]) for j in range(2)]
    cnt = {}

    def nxt(key, n):
        v = cnt.get(key, 0)
        cnt[key] = v + 1
        return v % n

    def load_kb(m):
        i = m % 4
        u = LC + m * 128
        S.dma(S.sp, [(KTb[i][:], self.KT[l][:, :, u:u + 128])], reads=[self.R(self.KT[l].name)], writes=[KTb[i].res])
        S.dma(S.sp, [(Vb[i][:, :, 0:64], self.VA[l][u:u + 128, :].rearrange("p (g d) -> p g d", d=64))],
              reads=[self.R(self.VA[l].name)], writes=[Vb[i].res])

    pending = []

    def norm(g, po, u0):
        d_ = dn[nxt("dn", 2)]
        S.op(S.dve, lambda h, d_=d_, po=po, g=g: h.tensor_tensor(out=d_[64:65, :], in0=po[64:65, :], in1=esk[64:65, g].rearrange("p a b -> p (a b)"), op=ALU.add),
             reads=[po.res, esk.res], writes=[d_.res])
        S.op(S.dve, lambda h, d_=d_: h.reciprocal(out=d_[64:65, :], in_=d_[64:65, :]), reads=[d_.res], writes=[d_.res])
        pb = pB[nxt("pb", 2)]
        S.op(S.pe, lambda h, d_=d_, pb=pb: h.matmul(pb[:], ones[64:65, 0:64], d_[64:65, :], start=True, stop=True),
             reads=[d_.res, ones.res], writes=[pb.res])
        b_ = bcs[nxt("bcs", 2)]
        S.op(S.act, lambda h, b_=b_, pb=pb: h.activation(out=b_[:], in_=pb[:], func=AF.Copy), reads=[pb.res], writes=[b_.res])
        a_ = aT[nxt("aT", 2)]
        S.op(S.dve, lambda h, a_=a_, b_=b_, po=po: h.tensor_tensor(out=a_[:].rearrange("p a b -> p (a b)"), in0=po[0:64, :], in1=b_[:], op=ALU.mult),
             reads=[po.res, b_.res], writes=[a_.res])
        S.dma(S.pool, [(self.ATT[l][:, 4 * g:4 * g + 4, u0:u0 + 128], a_[:])], reads=[a_.res], writes=[self.R(self.ATT[l].name)])

    def qblock(u0, kbs):
        qi = nxt("q", 2)
        Q = QTb[qi]
        S.dma(S.sp, [(Q[:], self.QT[l][:, :, u0:u0 + 128])], reads=[self.R(self.QT[l].name)], writes=[Q.res])
        for g in range(2):
            po = pO[nxt("po", 2)]
            rhsq = Q[:, 4 * g:4 * g + 4, :].rearrange("p a b -> p (a b)")

            def score(idx, g=g, rhsq=rhsq):
                kt, vt, msk = kbs[idx]
                p = pS[nxt("ps", 3)]
                S.op(S.pe, lambda h, kt=kt, p=p, g=g, rhsq=rhsq, msk=msk: h.matmul(p[:], kt[0][:, g, kt[1]:kt[1] + 128], rhsq, start=True, stop=(msk is None)),
                     reads=[kt[0].res, Q.res], writes=[p.res], inc=(msk is None))
                if msk is not None:
                    S.op(S.pe, lambda h, p=p, msk=msk: h.matmul(p[:], self.ident[:], msk[:].rearrange("p a b -> p (a b)"), start=False, stop=True),
                         reads=[self.ident.res, msk.res], writes=[p.res])
                return p
            ps_list = [score(0)]
            for idx in range(len(kbs)):
                kt, vt, msk = kbs[idx]
                if idx + 1 < len(kbs):
                    ps_list.append(score(idx + 1))
                p = ps_list[idx]
                e = E[nxt("e", 4)]
                S.op(S.act, lambda h, p=p, e=e: h.activation(out=e[:].rearrange("p a b -> p (a b)"), in_=p[:], func=AF.Exp, scale=0.125),
                     reads=[p.res], writes=[e.res])
                S.op(S.pe, lambda h, e=e, vt=vt, po=po, idx=idx, g=g, kbs=kbs: h.matmul(po[0:65, :], vt[:, g, :], e[:].rearrange("p a b -> p (a b)"),
                                                                                      start=(idx == 0), stop=(idx == len(kbs) - 1)),
                     reads=[e.res, vt.res], writes=[po.res], inc=(idx == len(kbs) - 1))
            pending.append((g, po, u0))
            if len(pending) > 1:
                norm(*pending.pop(0))

    ckb = [((KTc, 0), Vc[0], None), ((KTc, 128), Vc[1], None)]
    if do_ctx:
        for n in range(2):
            qblock(n * 128, ckb)
    load_kb(0)
    for n in range(nbk):
        if n + 1 < nbk:
            load_kb(n + 1)
        kbs = []
        if n - 1 >= 0:
            kbs.append(((KTb[(n - 1) % 4], 0), Vb[(n - 1) % 4], mP))
        kbs.append(((KTb[n % 4], 0), Vb[n % 4], None))
        if n + 1 < nbk:
            kbs.append(((KTb[(n + 1) % 4], 0), Vb[(n + 1) % 4], mN))
        qblock(LC + n * 128, kbs + ckb)
    while pending:
        norm(*pending.pop(0))


Builder.phase_attn = phase_attn


def scan_groups(T):
    return [(0, LC)] + [(LC + t0, min(512, T - t0)) for t0 in range(0, T, 512)]


def scan_order(T, d):
    groups = scan_groups(T)
    order = []
    if d == 0:
        for gi, (u0, n) in enumerate(groups):
            for c in range(n // 64):
                order.append((gi, c))
    else:
        gis = [0] + list(range(len(groups) - 1, 0, -1))
        for gi in gis:
            u0, n = groups[gi]
            for c in range(n // 64 - 1, -1, -1):
                order.append((gi, c))
    return order


def phase_gla(self, l):
    S = self.S
    T = self.T
    EL = self.EL
    groups = scan_groups(T)
    ones = self.sb("ones", [64, 64], F32)
    S.op(S.pool, lambda h: h.memset(ones[:], 1.0), writes=[ones.res])
    mtmp = self.sb("mtmp", [64, 64], F32)
    msk = [self.sb(f"msk{d}", [64, 64], BF16) for d in range(2)]
    for d, sgn in ((0, -1), (1, 1)):
        S.op(S.pool, lambda h, sgn=sgn: h.affine_select(out=mtmp[:], in_=ones[:], pattern=[[-sgn, 64]], compare_op=ALU.is_ge, fill=0.0,
                                                         base=0, channel_multiplier=sgn), reads=[ones.res], writes=[mtmp.res])
        S.op(S.pool, lambda h, d=d: h.tensor_copy(out=msk[d][:], in_=mtmp[:]), reads=[mtmp.res], writes=[msk[d].res])
    Sf = [self.sb(f"Sf{d}", [64, 4, 128], F32) for d in range(2)]
    Sb = [self.sb(f"Sb{d}", [64, 4, 128], BF16) for d in range(2)]
    for d in range(2):
        S.op(S.pool, lambda h, d=d: h.memset(Sf[d][:], 0.0), writes=[Sf[d].res])
        S.op(S.pool, lambda h, d=d: h.memset(Sb[d][:], 0.0), writes=[Sb[d].res])
    qg = [[self.sb(f"qg{d}{i}", [64, 4, 512], BF16) for i in range(2)] for d in range(2)]
    kg = [[self.sb(f"kg{d}{i}", [64, 4, 512], BF16) for i in range(2)] for d in range(2)]
    kh = [[self.sb(f"kh{d}{i}", [64, 4, 512], BF16) for i in range(2)] for d in range(2)]
    vg = [[self.sb(f"vg{d}{i}", [64, 8, 512], BF16) for i in range(2)] for d in range(2)]
    am = [[self.sb(f"am{d}{i}", [64, 4, 64], BF16) for i in range(2)] for d in range(2)]
    kt = [[self.sb(f"kt{d}{i}", [64, 4, 64], BF16) for i in range(2)] for d in range(2)]
    ob = [[self.sb(f"ob{d}{i}", [64, 512], F32) for i in range(2)] for d in range(2)]
    pA = [self.ps(f"pA{d}", [64, 256]) for d in range(2)]
    pK = [self.ps(f"pK{d}", [64, 256], BF16) for d in range(2)]
    pO = [self.ps(f"pO{d}", [64, 512]) for d in range(2)]
    pN = [self.ps(f"pN{d}", [64, 512]) for d in range(2)]
    orders = [scan_order(T, d) for d in range(2)]
    nsteps = len(orders[0])
    gcount = [0, 0]
    cur = [None, None]

    def load_group(d, gi):
        i = gcount[d] % 2
        gcount[d] += 1
        u0, n = groups[gi]
        nch = n // 64
        for (dst, srcT) in ((qg[d][i], self.QG[l]), (kg[d][i], self.KG[l]), (kh[d][i], self.KH[l])):
            S.dma(S.sp, [(dst[:, :, :n], srcT[d, :, :, u0:u0 + n])], reads=[self.R(srcT.name)], writes=[dst.res])
        S.dma(S.sp, [(vg[d][i][:, :nch, :], self.GV[l][u0:u0 + n, :].rearrange("(c p) f -> p c f", p=64))], reads=[self.R(self.GV[l].name)],
              writes=[vg[d][i].res])
        return i

    ctxs = {}

    def stageA(step, d):
        gi, c = orders[d][step]
        if cur[d] is None or cur[d][0] != gi:
            cur[d] = (gi, load_group(d, gi))
        bi = cur[d][1]
        u0, n = groups[gi]
        o = c * 64
        Q, Kg, Kh, V = qg[d][bi], kg[d][bi], kh[d][bi], vg[d][bi]
        k2 = step % 2
        AM, KTt = am[d][k2], kt[d][k2]
        ctxs[(step, d)] = (Q, V, AM, KTt, u0, o, c)
        for hh in range(4):
            S.op(S.pe, lambda h, hh=hh, d=d, Kg=Kg, Q=Q, o=o: h.matmul(pA[d][:, hh * 64:(hh + 1) * 64], Kg[:, hh, o:o + 64], Q[:, hh, o:o + 64], start=True, stop=True),
                 reads=[Kg.res, Q.res], writes=[pA[d].res], inc=(hh == 3))
        for hh in range(4):
            S.op(S.pe, lambda h, hh=hh, d=d, Kh=Kh, o=o: h.transpose(out=pK[d][:, hh * 64:(hh + 1) * 64], in_=Kh[:, hh, o:o + 64], identity=self.ident[0:64, 0:64]),
                 reads=[Kh.res, self.ident.res], writes=[pK[d].res], inc=(hh == 3))
        S.op(S.dve, lambda h, d=d, AM=AM: h.tensor_tensor(out=AM[:], in0=pA[d][:].rearrange("p (a b) -> p a b", b=64),
                                                          in1=msk[d][:].unsqueeze(1).to_broadcast([64, 4, 64]), op=ALU.mult),
             reads=[pA[d].res, msk[d].res], writes=[AM.res])
        S.op(S.act, lambda h, d=d, KTt=KTt: h.activation(out=KTt[:].rearrange("p a b -> p (a b)"), in_=pK[d][:], func=AF.Copy), reads=[pK[d].res], writes=[KTt.res])

    def stageB(step, d):
        Q, V, AM, KTt, u0, o, c = ctxs.pop((step, d))
        chunk = (u0 + o) // 64
        OB = ob[d][step % 2]
        for hh in range(4):
            S.op(S.pe, lambda h, hh=hh, d=d, KTt=KTt, V=V, c=c: h.matmul(pN[d][:, hh * 128:(hh + 1) * 128], KTt[:, hh, :], V[:, c, hh * 128:(hh + 1) * 128], start=True, stop=True),
                 reads=[KTt.res, V.res], writes=[pN[d].res], inc=(hh == 3))
        for hh in range(4):
            S.op(S.pe, lambda h, hh=hh, d=d, AM=AM, V=V, c=c: h.matmul(pO[d][:, hh * 128:(hh + 1) * 128], AM[:, hh, :], V[:, c, hh * 128:(hh + 1) * 128], start=True, stop=False),
                 reads=[AM.res, V.res], writes=[pO[d].res], inc=False)
            S.op(S.pe, lambda h, hh=hh, d=d, Q=Q, o=o: h.matmul(pO[d][:, hh * 128:(hh + 1) * 128], Q[:, hh, o:o + 64], Sb[d][:, hh, :], start=False, stop=True),
                 reads=[Q.res, Sb[d].res], writes=[pO[d].res], inc=(hh == 3))
        S.op(S.act, lambda h, d=d, OB=OB: h.activation(out=OB[:], in_=pO[d][:], func=AF.Copy), reads=[pO[d].res], writes=[OB.res])
        S.dma(S.sp, [(self.OG[l][d, u0 + o:u0 + o + 64, :], OB[:])], reads=[OB.res], writes=[self.R(self.OG[l].name)])
        S.op(S.dve, lambda h, d=d, chunk=chunk: h.tensor_tensor(out=Sf[d][:], in0=Sf[d][:], in1=EL[:, :, d, chunk:chunk + 1].to_broadcast([64, 4, 128]), op=ALU.mult),
             reads=[Sf[d].res, self.EL.res], writes=[Sf[d].res])
        S.op(S.dve, lambda h, d=d: h.tensor_tensor(out=Sf[d][:].rearrange("p a b -> p (a b)"), in0=Sf[d][:].rearrange("p a b -> p (a b)"), in1=pN[d][:], op=ALU.add),
             reads=[Sf[d].res, pN[d].res], writes=[Sf[d].res])
        S.op(S.act, lambda h, d=d: h.activation(out=Sb[d][:], in_=Sf[d][:], func=AF.Copy), reads=[Sf[d].res], writes=[Sb[d].res])

    for d in range(2):
        stageA(0, d)
    for step in range(nsteps):
        if step + 1 < nsteps:
            for d in range(2):
                stageA(step + 1, d)
        for d in range(2):
            stageB(step, d)


Builder.phase_gla = phase_gla


LN_KS = float(-0.5 * np.log(128.0))


def phase_ml_gates(self, l):
    S = self.S
    sel = self.sel
    DEC = self.DEC
    T = self.T
    TT = self.TT
    nch = TT // 64
    bA = self.sb("bA", [4, TT], F32)
    bL = self.sb("bL", [4, TT], F32)
    bC = self.sb("bC", [4, TT], F32)
    bG = self.sb("bG", [4, TT], F32)
    bX = self.sb("bX", [4, TT], F32)
    onesr = self.sb("onesr", [4, TT], BF16)
    S.op(S.pool, lambda h: h.memset(onesr[:], 1.0), writes=[onesr.res])
    gl = self.sb("gl", [4, nch], F32)
    gp = self.sb("gp", [4, nch], F32)
    dd = self.sb("dd", [4, nch], F32)
    ibc = self.sb("ibc", [4, 2], F32)
    pD = self.ps("pD", [128, 512])
    for d in range(2):
        S.dma(S.sp, [(bA[:], self.GATES[l][d * 4:(d + 1) * 4, :])], reads=[self.R(self.GATES[l].name)], writes=[bA.res])
        S.dma(S.sp, [(bL[:], self.GATES[l][8 + d * 4:8 + (d + 1) * 4, :])], reads=[self.R(self.GATES[l].name)], writes=[bL.res])
        S.dma(S.sp, [(ibc[:, 0:1], self.mlstm_ib[l, d, :].rearrange("(h o) -> h o", o=1)), (ibc[:, 1:2], self.mlstm_fb[l, d, :].rearrange("(h o) -> h o", o=1))],
              writes=[ibc.res])
        S.op(S.dve, lambda h: h.tensor_scalar(out=ibc[:, 1:2], in0=ibc[:, 1:2], scalar1=-1.0, scalar2=None, op0=ALU.mult), reads=[ibc.res], writes=[ibc.res])
        S.op(S.act, lambda h: h.activation(out=bL[:], in_=bL[:], func=AF.Exp, scale=-1.0, bias=ibc[:, 1:2]), reads=[bL.res, ibc.res], writes=[bL.res])
        S.op(S.act, lambda h: h.activation(out=bL[:], in_=bL[:], func=AF.Ln, bias=1.0), reads=[bL.res], writes=[bL.res])

        def scan(out, src, op1):
            if d == 0:
                S.op(S.dve, lambda h: h.tensor_tensor_scan(out=out[:], data0=onesr[:], data1=src[:], initial=0.0, op0=ALU.mult, op1=op1),
                     reads=[src.res, onesr.res], writes=[out.res])
            else:
                S.op(S.dve, lambda h: h.tensor_tensor_scan(out=rev_last(out[:, 0:LC]), data0=onesr[:, 0:LC], data1=rev_last(src[:, 0:LC]), initial=0.0, op0=ALU.mult, op1=op1),
                     reads=[src.res, onesr.res], writes=[out.res])
                S.op(S.dve, lambda h: h.tensor_tensor_scan(out=rev_last(out[:, LC:TT]), data0=onesr[:, LC:TT], data1=rev_last(src[:, LC:TT]), initial=out[:, 0:1], op0=ALU.mult, op1=op1),
                     reads=[src.res, onesr.res, out.res], writes=[out.res])
        scan(bC, bL, ALU.add)
        S.op(S.dve, lambda h: h.scalar_tensor_tensor(out=bA[:], in0=bA[:], scalar=ibc[:, 0:1], in1=bC[:], op0=ALU.add, op1=ALU.add),
             reads=[bA.res, ibc.res, bC.res], writes=[bA.res])
        scan(bG, bA, ALU.max)
        G3 = bG[:].rearrange("p (c b) -> p c b", b=64)
        lastpos = 63 if d == 0 else 0
        S.op(S.dve, lambda h, lastpos=lastpos, G3=G3: h.tensor_copy(out=gl[:], in_=G3[:, :, lastpos]), reads=[bG.res], writes=[gl.res])
        S.op(S.dve, lambda h: h.memset(gp[:], 0.0), writes=[gp.res])
        if d == 0:
            S.op(S.dve, lambda h: h.tensor_copy(out=gp[:, 1:nch], in_=gl[:, 0:nch - 1]), reads=[gl.res], writes=[gp.res])
        else:
            S.op(S.dve, lambda h: h.tensor_copy(out=gp[:, 0:3], in_=gl[:, 1:4]), reads=[gl.res], writes=[gp.res])
            S.op(S.dve, lambda h: h.tensor_copy(out=gp[:, 4:nch - 1], in_=gl[:, 5:nch]), reads=[gl.res], writes=[gp.res])
            S.op(S.dve, lambda h: h.tensor_copy(out=gp[:, nch - 1:nch], in_=gl[:, 0:1]), reads=[gl.res], writes=[gp.res])
        L3 = bL[:].rearrange("p (c b) -> p c b", b=64)
        X3 = bX[:].rearrange("p (c b) -> p c b", b=64)
        A3 = bA[:].rearrange("p (c b) -> p c b", b=64)
        S.op(S.dve, lambda h, L3=L3, G3=G3: h.tensor_tensor(out=L3, in0=gp[:].unsqueeze(2).to_broadcast([4, nch, 64]), in1=G3, op=ALU.subtract),
             reads=[gp.res, bG.res], writes=[bL.res])
        S.op(S.act, lambda h: h.activation(out=bL[:], in_=bL[:], func=AF.Exp), reads=[bL.res], writes=[bL.res])
        S.op(S.dve, lambda h: h.tensor_tensor(out=bC[:], in0=bC[:], in1=bG[:], op=ALU.subtract), reads=[bC.res, bG.res], writes=[bC.res])
        S.op(S.act, lambda h: h.activation(out=bC[:], in_=bC[:], func=AF.Exp), reads=[bC.res], writes=[bC.res])
        S.op(S.dve, lambda h, X3=X3, A3=A3: h.tensor_tensor(out=X3, in0=A3, in1=gl[:].unsqueeze(2).to_broadcast([4, nch, 64]), op=ALU.subtract),
             reads=[bA.res, gl.res], writes=[bX.res])
        S.op(S.dve, lambda h: h.tensor_scalar(out=bX[:], in0=bX[:], scalar1=LN_KS, scalar2=None, op0=ALU.add), reads=[bX.res], writes=[bX.res])
        S.op(S.act, lambda h: h.activation(out=bX[:], in_=bX[:], func=AF.Exp), reads=[bX.res], writes=[bX.res])
        S.op(S.dve, lambda h: h.tensor_tensor(out=dd[:], in0=gp[:], in1=gl[:], op=ALU.subtract), reads=[gp.res, gl.res], writes=[dd.res])
        S.op(S.act, lambda h: h.activation(out=dd[:], in_=dd[:], func=AF.Exp), reads=[dd.res], writes=[dd.res])
        for hh in range(4):
            S.op(S.pe, lambda h, hh=hh: h.matmul(pD[:, :nch], sel[:, hh, :], dd[:], start=True, stop=True), reads=[self.sel.res, dd.res], writes=[pD.res])
            S.op(S.dve, lambda h, hh=hh, d=d: h.tensor_copy(out=DEC[:, d, hh, :], in_=pD[:, :nch]), reads=[pD.res], writes=[self.DEC.res])
        for qi, buf in enumerate((bA, bG, bL, bC, bX)):
            S.dma(S.sp, [(self.MROWS[l][d, qi, :, :], buf[:])], reads=[buf.res], writes=[self.R(self.MROWS[l].name)])


def phase_ml_conv(self, l):
    S = self.S
    T = self.T
    TT = self.TT
    wcol = self.sb("wcol", [128, 8, 5], F32)
    cb = self.sb("cb", [128, 8], F32)
    S.dma(S.sp, [(wcol[:, :, k], self.conv_w[l, k, :].rearrange("(fc p) -> p fc", p=128)) for k in range(5)], writes=[wcol.res], allow_slow_non_contiguous=True)
    S.dma(S.sp, [(cb[:], self.conv_b[l, :].rearrange("(fc p) -> p fc", p=128))], writes=[cb.res], allow_slow_non_contiguous=True)
    diagw = self.sb("diagw", [128, 8, 5, 128], BF16)
    for fc in range(8):
        for k in range(5):
            e = S.dve if (fc * 5 + k) % 2 == 0 else S.pool
            S.op(e, lambda h, fc=fc, k=k: h.tensor_scalar(out=diagw[:, fc, k, :], in0=self.identf[:], scalar1=wcol[:, fc, k:k + 1], scalar2=None, op0=ALU.mult),
                 reads=[self.identf.res, wcol.res], writes=[diagw.res])
    xq = [self.sb(f"xq{i}", [128, 8, 516], BF16) for i in range(2)]
    oc = [self.sb(f"oc{i}", [128, 512], BF16) for i in range(3)]
    pc = [self.ps(f"pc{i}", [128, 512]) for i in range(2)]
    MQKv = self.MQK[l].rearrange("(c p) t -> p c t", p=128)
    k2 = 0
    for gi, (u0, n) in enumerate(scan_groups(T)):
        X = xq[gi % 2]
        S.dma(S.sp, [(X[:, 0:4, 0:n + 4], MQKv[:, 0:4, u0:u0 + n + 4]), (X[:, 4:8, 0:n + 4], MQKv[:, 4:8, u0:u0 + n + 4])], reads=[self.R(self.MQK[l].name)], writes=[X.res])
        if u0 == 0 or u0 == LC:
            S.op(S.pool, lambda h, X=X: h.memset(X[:, :, 0:2], 0.0), writes=[X.res])
        if u0 + n == LC or u0 + n == TT:
            S.op(S.pool, lambda h, X=X, n=n: h.memset(X[:, :, n + 2:n + 4], 0.0), writes=[X.res])
        for fc in range(8):
            p = pc[k2 % 2]
            o_ = oc[k2 % 3]
            k2 += 1
            for k in range(5):
                S.op(S.pe, lambda h, fc=fc, k=k, p=p, X=X, n=n: h.matmul(p[:, :n], diagw[:, fc, k, :], X[:, fc, k:k + n], start=(k == 0), stop=(k == 4)),
                     reads=[diagw.res, X.res], writes=[p.res], inc=(k == 4))
            S.op(S.act, lambda h, fc=fc, p=p, o_=o_, n=n: h.activation(out=o_[:, :n], in_=p[:, :n], func=AF.Silu, bias=cb[:, fc:fc + 1]), reads=[p.res, cb.res], writes=[o_.res])
            S.dma(S.sp, [(self.MQC[l][fc * 128:(fc + 1) * 128, u0:u0 + n], o_[:, :n])], reads=[o_.res], writes=[self.R(self.MQC[l].name)])


def phase_ml_scan(self, l):
    S = self.S
    T = self.T
    sel = self.sel
    DEC = self.DEC
    groups = scan_groups(T)
    cfill = self.sb("cfill", [64, 64], F32)
    S.op(S.pool, lambda h: h.memset(cfill[:], LN_KS), writes=[cfill.res])
    mb = [self.sb(f"mb{d}", [64, 64], F32) for d in range(2)]
    for d, sgn in ((0, -1), (1, 1)):
        S.op(S.pool, lambda h, sgn=sgn, d=d: h.affine_select(out=mb[d][:], in_=cfill[:], pattern=[[-sgn, 64]], compare_op=ALU.is_ge, fill=-30000.0,
                                                              base=0, channel_multiplier=sgn), reads=[cfill.res], writes=[mb[d].res])
    negones = self.sb("negones", [4, 128], F32)
    S.op(S.pool, lambda h: h.memset(negones[:], -1.0), writes=[negones.res])
    posones = self.sb("posones", [4, 128], F32)
    S.op(S.pool, lambda h: h.memset(posones[:], 1.0), writes=[posones.res])
    mbr = [self.sb(f"mbr{d}", [64, 4, 64], F32) for d in range(2)]
    for d in range(2):
        S.op(S.pool, lambda h, d=d: h.tensor_copy(out=mbr[d][:], in_=mb[d][:].unsqueeze(1).to_broadcast([64, 4, 64])), reads=[mb[d].res], writes=[mbr[d].res])
    Dg = [self.sb(f"Dg{i}", [4, 3, 4, 512], F32) for i in range(2)]
    DgH = [self.sb(f"DgH{i}", [4, 2, 4, 512], BF16) for i in range(2)]
    DgL = [self.sb(f"DgL{i}", [4, 2, 4, 512], BF16) for i in range(2)]
    posb = self.sb("posb", [4, 128], BF16)
    S.op(S.pool, lambda h: h.memset(posb[:], 1.0), writes=[posb.res])
    Cf = self.sb("Cf", [128, 4, 129], F32)
    Cb = self.sb("Cb", [128, 4, 129], BF16)
    qk = [self.sb(f"qk{i}", [128, 8, 512], BF16) for i in range(2)]
    vg = [self.sb(f"vgm{i}", [64, 8, 4, 129], BF16) for i in range(2)]
    rows = [self.sb(f"rows{i}", [4, 5, 512], F32) for i in range(2)]
    for v in vg:
        S.op(S.pool, lambda h, v=v: h.memset(v[:], 1.0), writes=[v.res])
    wT = [self.sb(f"wT{i}", [64, 256], F32) for i in range(3)]
    sT = [self.sb(f"sT{i}", [64, 4, 64], BF16) for i in range(3)]
    qks = [self.sb(f"qks{i}", [128, 8, 64], BF16) for i in range(3)]
    khat = [self.sb(f"khat{i}", [64, 4, 128], BF16) for i in range(3)]
    enm = [self.sb(f"enm{i}", [64, 4], F32) for i in range(3)]
    rr = [self.sb(f"rr{i}", [64, 4], F32) for i in range(3)]
    ho = [self.sb(f"ho{i}", [64, 4, 128], F32) for i in range(3)]
    pWS = self.ps("pWS", [64, 512])
    pB = self.ps("pB", [128, 8, 64])
    pK = self.ps("pK", [64, 4, 128], BF16)
    pO = self.ps("pO", [64, 1024])
    pN = self.ps("pN", [128, 1024])
    pO3 = pO[:].rearrange("p (h e) -> p h e", e=256)
    pN3 = pN[:].rearrange("p (h e) -> p h e", e=256)
    gcount = [0]

    def load_group(d, gi):
        i = gcount[0] % 2
        gcount[0] += 1
        u0, n = groups[gi]
        nchg = n // 64
        S.dma(S.sp, [(qk[i][:, 0:4, :n], self.MQC[l].rearrange("(c p) t -> p c t", p=128)[:, 0:4, u0:u0 + n]),
                     (qk[i][:, 4:8, :n], self.MQC[l].rearrange("(c p) t -> p c t", p=128)[:, 4:8, u0:u0 + n])], reads=[self.R(self.MQC[l].name)], writes=[qk[i].res])
        S.dma(S.sp, [(vg[i][:, c, :, 0:128], self.MV[l][u0 + c * 64:u0 + (c + 1) * 64, :].rearrange("p (h e) -> p h e", e=128)) for c in range(nchg)],
              reads=[self.R(self.MV[l].name)], writes=[vg[i].res])
        S.dma(S.sp, [(rows[i][:, :, :n], self.MROWS[l][d, :, :, u0:u0 + n].rearrange("q h t -> h q t"))], reads=[self.R(self.MROWS[l].name)], writes=[rows[i].res])
        for qd, qs in enumerate((1, 2, 4)):
            S.op(S.pool, lambda h, i=i, qd=qd, qs=qs, n=n: h.tensor_tensor(out=Dg[i][:, qd, :, :n], in0=rows[i][:, qs:qs + 1, :n].to_broadcast([4, 4, n]),
                                                                       in1=self.identf[0:4, 0:4].unsqueeze(2).to_broadcast([4, 4, n]), op=ALU.mult),
                 reads=[rows[i].res, self.identf.res], writes=[Dg[i].res])
        return i

    for d in range(2):
        S.op(S.pool, lambda h: h.memset(Cf[:], 0.0), writes=[Cf.res])
        S.op(S.pool, lambda h: h.memset(Cb[:], 0.0), writes=[Cb.res])
        order = scan_order(T, d)
        cur = None
        info = []
        for (gi, c) in order:
            if cur is None or cur[0] != gi:
                cur = (gi, None)
            info.append((gi, c))
        bufof = {}

        def stageA(step):
            gi, c = order[step]
            if gi not in bufof:
                bufof.clear()
                bufof[gi] = load_group(d, gi)
            bi = bufof[gi]
            o = c * 64
            k2 = step % 3
            QK, R_ = qk[bi], rows[bi]
            DG = Dg[bi]
            S.op(S.pe, lambda h, R_=R_, o=o: h.matmul(pWS[:, 0:256], R_[:, 0, o:o + 64], sel[:, :, 0:64], start=True, stop=False),
                 reads=[R_.res, sel.res], writes=[pWS.res], inc=False)
            S.op(S.pe, lambda h, DG=DG, o=o: h.matmul(pWS[:, 0:256], negones[:, 0:64], DG[:, 0, :, o:o + 64], start=False, stop=False),
                 reads=[DG.res, negones.res], writes=[pWS.res], inc=False)
            S.op(S.pe, lambda h, d=d: h.matmul(pWS[:, 0:256], self.identf[0:64, 0:64], mbr[d][:], start=False, stop=True),
                 reads=[self.identf.res, mbr[d].res], writes=[pWS.res], inc=False)
            for hh in range(4):
                S.op(S.pe, lambda h, hh=hh, QK=QK, o=o: h.matmul(pWS[:, 256 + hh * 64:256 + (hh + 1) * 64], QK[:, 4 + hh, o:o + 64], QK[:, hh, o:o + 64], start=True, stop=True),
                     reads=[QK.res], writes=[pWS.res], inc=(hh == 3))
            S.op(S.act, lambda h, k2=k2: h.activation(out=wT[k2][:], in_=pWS[:, 0:256], func=AF.Exp), reads=[pWS.res], writes=[wT[k2].res])
            S.op(S.dve, lambda h, k2=k2: h.tensor_tensor(out=sT[k2][:].rearrange("p a b -> p (a b)"), in0=pWS[:, 256:512], in1=wT[k2][:], op=ALU.mult),
                 reads=[pWS.res, wT[k2].res], writes=[sT[k2].res])
            S.op(S.pe, lambda h, DG=DG, o=o: h.matmul(pB[:], posones[:, :], DG[:, 1:3, :, o:o + 64], start=True, stop=True),
                 reads=[DG.res, posones.res], writes=[pB.res])
            S.op(S.dve, lambda h, k2=k2, QK=QK, o=o: h.tensor_tensor(out=qks[k2][:], in0=QK[:, :, o:o + 64], in1=pB[:], op=ALU.mult),
                 reads=[QK.res, pB.res], writes=[qks[k2].res])

        def stageA2(step):
            k2 = step % 3
            for hh in range(4):
                S.op(S.pe, lambda h, hh=hh, k2=k2: h.transpose(out=pK[:, hh, :], in_=qks[k2][:, 4 + hh, :], identity=self.ident[:]),
                     reads=[qks[k2].res, self.ident.res], writes=[pK.res], inc=(hh == 3))
            S.op(S.act, lambda h, k2=k2: h.activation(out=khat[k2][:], in_=pK[:], func=AF.Copy), reads=[pK.res], writes=[khat[k2].res])

        def stageB(step):
            gi, c = order[step]
            u0, n = groups[gi]
            o = c * 64
            chunk = (u0 + o) // 64
            k2 = step % 3
            V, R_ = vgbuf[step], rowbuf[step]
            for hh in range(4):
                S.op(S.pe, lambda h, hh=hh, k2=k2, V=V, c=c: h.matmul(pN[:, hh * 256:hh * 256 + 129], khat[k2][:, hh, :], V[:, c, hh, :], start=True, stop=True),
                     reads=[khat[k2].res, V.res], writes=[pN.res], inc=(hh == 3))
            S.op(S.pe, lambda h, R_=R_, o=o: h.matmul(pO[:, 200:204], R_[:, 3, o:o + 64], self.identf[0:4, 0:4], start=True, stop=True),
                 reads=[R_.res, self.identf.res], writes=[pO.res], inc=False)
            for hh in range(4):
                S.op(S.pe, lambda h, hh=hh, k2=k2, V=V, c=c: h.matmul(pO[:, hh * 256:hh * 256 + 129], sT[k2][:, hh, :], V[:, c, hh, :], start=True, stop=False),
                     reads=[sT[k2].res, V.res], writes=[pO.res], inc=False)
                S.op(S.pe, lambda h, hh=hh, k2=k2: h.matmul(pO[:, hh * 256:hh * 256 + 129], qks[k2][:, hh, :], Cb[:, hh, :], start=False, stop=True),
                     reads=[qks[k2].res, Cb.res], writes=[pO.res], inc=(hh == 3))
            S.op(S.act, lambda h, k2=k2: h.activation(out=enm[k2][:], in_=pO[:, 200:204], func=AF.Copy), reads=[pO.res], writes=[enm[k2].res])
            S.op(S.act, lambda h, k2=k2: h.activation(out=rr[k2][:], in_=pO3[:, :, 128], func=AF.Abs), reads=[pO.res], writes=[rr[k2].res])
            S.op(S.pool, lambda h, d=d, chunk=chunk: h.tensor_tensor(out=Cf[:], in0=Cf[:], in1=DEC[:, d, :, chunk:chunk + 1].to_broadcast([128, 4, 129]), op=ALU.mult),
                 reads=[Cf.res, DEC.res], writes=[Cf.res])
            S.op(S.dve, lambda h: h.tensor_tensor(out=Cf[:], in0=Cf[:], in1=pN3[:, :, 0:129], op=ALU.add), reads=[Cf.res, pN.res], writes=[Cf.res])
            S.op(S.act, lambda h: h.activation(out=Cb[:], in_=Cf[:], func=AF.Copy), reads=[Cf.res], writes=[Cb.res])
            S.op(S.dve, lambda h, k2=k2: h.tensor_tensor(out=rr[k2][:], in0=rr[k2][:], in1=enm[k2][:], op=ALU.max), reads=[rr[k2].res, enm[k2].res], writes=[rr[k2].res])
            S.op(S.dve, lambda h, k2=k2: h.reciprocal(out=rr[k2][:], in_=rr[k2][:]), reads=[rr[k2].res], writes=[rr[k2].res])
            S.op(S.dve, lambda h, k2=k2: h.tensor_tensor(out=ho[k2][:], in0=pO3[:, :, 0:128], in1=rr[k2][:].unsqueeze(2).to_broadcast([64, 4, 128]), op=ALU.mult),
                 reads=[pO.res, rr[k2].res], writes=[ho[k2].res])
            S.dma(S.sp, [(self.OM[l][d, u0 + o:u0 + o + 64, :], ho[k2][:].rearrange("p a b -> p (a b)"))], reads=[ho[k2].res], writes=[self.R(self.OM[l].name)])

        vgbuf = {}
        rowbuf = {}

        def A(step):
            stageA(step)
            gi, c = order[step]
            vgbuf[step] = vg[bufof[gi]]
            rowbuf[step] = rows[bufof[gi]]
        A(0)
        if len(order) > 1:
            A(1)
        stageA2(0)
        for step in range(len(order)):
            if step + 2 < len(order):
                A(step + 2)
            if step + 1 < len(order):
                stageA2(step + 1)
            stageB(step)


Builder.phase_ml_gates = phase_ml_gates
Builder.phase_ml_conv = phase_ml_conv
Builder.phase_ml_scan = phase_ml_scan


def phase_merge(self, l, streams):
    S = self.S
    win = self.w_in[l].rearrange("(kc p) n -> p kc n", p=128)
    Wm = self.sb("Wm", [128, 8, 4096], BF16)
    Wmr = [Res(f"Wm{k}") for k in range(4)]
    for ki, k0 in enumerate(range(0, 8, 2)):
        S.dma(S.pool, [(Wm[:, k0:k0 + 2, 0:512], win[:, k0:k0 + 2, O_GR:O_GR + 512]),
                       (Wm[:, k0:k0 + 2, 512:1024], win[:, k0:k0 + 2, O_MO:O_MO + 512]),
                       (Wm[:, k0:k0 + 2, 1024:4096], win[:, k0:k0 + 2, O_SA:O_SA + 3072])], writes=[Wmr[ki]])
    Woa = self.sb("Woa", [64, 8, D], BF16)
    Wog = self.sb("Wog", [128, 4, D], BF16)
    Wom = self.sb("Wom", [128, 4, D], BF16)
    Wo = self.sb("Wo", [128, 8, D], BF16)
    S.dma(S.pool, [(Woa[:], self.w_out_attn[l].rearrange("(h p) n -> p h n", p=64))], writes=[Woa.res])
    S.dma(S.pool, [(Wog[:], self.w_out_gla[l].rearrange("(c p) n -> p c n", p=128))], writes=[Wog.res])
    S.dma(S.pool, [(Wom[:], self.w_out_mlstm[l].rearrange("(c p) n -> p c n", p=128))], writes=[Wom.res])
    S.dma(S.pool, [(Wo[:, 0:4, :], self.w_o[l].rearrange("(c p) n -> p c n", p=128)[:, 0:4, :]),
                   (Wo[:, 4:8, :], self.w_o[l].rearrange("(c p) n -> p c n", p=128)[:, 4:8, :])], writes=[Wo.res])
    gains = self.sb("gains", [128, 2, 128], F32)
    S.dma(S.sp, [(gains[:, 0, :], bcast_rows(self.gla_norm[l:l + 1, :], 128)), (gains[:, 1, :], bcast_rows(self.mlstm_norm[l:l + 1, :], 128))], writes=[gains.res])
    eps_col = self.eps_col
    hT = [self.sb(f"mhT{i}", [128, 8, 128], BF16) for i in range(2)]
    aTt = [self.sb(f"maT{i}", [64, 8, 128], BF16) for i in range(2)]
    og = [self.sb(f"mog{i}", [128, 2, 512], F32) for i in range(2)]
    om = [self.sb(f"mom{i}", [128, 2, 512], F32) for i in range(2)]
    xr = [self.sb(f"mxr{i}", [128, D], F32) for i in range(2)]
    gts = [self.sb(f"mgt{i}", [128, 8, 512], F32) for i in range(2)]
    sq = self.sb("msq", [128, 512], F32)
    ssqs = [self.sb(f"mssq{i}", [128, 2, 4], F32) for i in range(2)]
    bn = [self.sb(f"mbn{i}", [128, 512], F32) for i in range(2)]
    bbs = [[self.sb(f"mbb{i}{j}", [128, 512], BF16) for j in range(2)] for i in range(2)]
    bTs = [[self.sb(f"mbT{i}{j}", [128, 4, 128], BF16) for j in range(2)] for i in range(2)]
    yb = self.sb("myb", [128, D], BF16)
    yT = self.sb("myT", [128, 8, 128], BF16)
    t1 = [self.sb(f"mt1{i}", [128, 512], F32) for i in range(3)]
    G5 = self.sb("mG5", [128, D], F32)
    pg = [self.ps(f"mpg{i}", [128, 512]) for i in range(2)]
    pT1s = [self.ps(f"mpT1{j}", [128, 512], BF16) for j in range(2)]
    pT2 = self.ps("mpT2", [128, D], BF16)
    py = [self.ps(f"mpy{i}", [128, 512]) for i in range(3)]
    pY = py[0]
    cnt = {}

    def nxt(key, n):
        v = cnt.get(key, 0)
        cnt[key] = v + 1
        return v % n
    H2Tv = self.H2T[l].rearrange("(kc p) t -> p kc t", p=128)
    work = []
    for (tag, src, dst, ntok, row, uoff) in streams:
        for t0 in range(0, ntok, 128):
            work.append((tag, src, dst, row, uoff + t0, t0))

    def stage1(w, i):
        (tag, src, dst, row, u, t0) = work[w]
        H, AT, OGt, OMt, XR, gt, ssq = hT[i], aTt[i], og[i], om[i], xr[i], gts[i], ssqs[i]
        S.dma(S.sp, [(H[:], H2Tv[:, :, u:u + 128])], reads=[self.R(self.H2T[l].name)], writes=[H.res])
        S.dma(S.sp, [(AT[:], self.ATT[l][:, :, u:u + 128])], reads=[self.R(self.ATT[l].name)], writes=[AT.res])
        S.dma(S.sp, [(OGt[:, 0, :], self.OG[l][0, u:u + 128, :]), (OGt[:, 1, :], self.OG[l][1, u:u + 128, :])], reads=[self.R(self.OG[l].name)], writes=[OGt.res])
        S.dma(S.sp, [(OMt[:, 0, :], self.OM[l][0, u:u + 128, :]), (OMt[:, 1, :], self.OM[l][1, u:u + 128, :])], reads=[self.R(self.OM[l].name)], writes=[OMt.res])
        S.dma(S.sp, [(XR[:], src[t0:t0 + 128, :])], reads=[self.R(src.name)], writes=[XR.res])

    def stage1g(w, i):
        H, gt = hT[i], gts[i]
        for blk in range(8):
            p = pg[nxt("pg", 2)]
            for kc in range(8):
                S.op(S.pe, lambda h, kc=kc, blk=blk, p=p, H=H: h.matmul(p[:], H[:, kc, :], Wm[:, kc, blk * 512:(blk + 1) * 512], start=(kc == 0), stop=(kc == 7)),
                     reads=[H.res] + Wmr, writes=[p.res], inc=(kc == 7))
            fn = AF.Silu if blk == 0 else AF.Sigmoid
            S.op(S.act, lambda h, blk=blk, p=p, fn=fn, gt=gt: h.activation(out=gt[:, blk, :], in_=p[:], func=fn), reads=[p.res], writes=[gt.res])

    def stage1c(w, i):
        OGt, OMt, gt, ssq = og[i], om[i], gts[i], ssqs[i]
        for br, Ot in enumerate((OGt, OMt)):
            S.op(S.pool, lambda h, Ot=Ot: h.tensor_tensor(out=Ot[:, 0, :], in0=Ot[:, 0, :], in1=Ot[:, 1, :], op=ALU.add), reads=[Ot.res], writes=[Ot.res])
            S.op(S.act, lambda h, Ot=Ot: h.activation(out=sq[:], in_=Ot[:, 0, :], func=AF.Square), reads=[Ot.res], writes=[sq.res])
            S.op(S.dve, lambda h, br=br, ssq=ssq: h.tensor_reduce(out=ssq[:, br, :], in_=sq[:].rearrange("p (a b) -> p a b", b=128), axis=AX.X, op=ALU.add),
                 reads=[sq.res], writes=[ssq.res])
        S.op(S.act, lambda h, ssq=ssq: h.activation(out=ssq[:], in_=ssq[:], func=AF.Sqrt, scale=1.0 / 128, bias=eps_col[:]),
             reads=[ssq.res, eps_col.res], writes=[ssq.res])
        S.op(S.dve, lambda h, ssq=ssq: h.reciprocal(out=ssq[:], in_=ssq[:]), reads=[ssq.res], writes=[ssq.res])
        for br, Ot in enumerate((OGt, OMt)):
            B_ = bn[br]
            S.op(S.dve, lambda h, br=br, Ot=Ot, B_=B_, ssq=ssq: h.tensor_tensor(out=B_[:].rearrange("p (a b) -> p a b", b=128), in0=Ot[:, 0, :].rearrange("p (a b) -> p a b", b=128),
                                                                              in1=ssq[:, br, :].unsqueeze(2).to_broadcast([128, 4, 128]), op=ALU.mult),
                 reads=[Ot.res, ssq.res], writes=[B_.res])
            S.op(S.pool, lambda h, br=br, B_=B_: h.tensor_tensor(out=B_[:].rearrange("p (a b) -> p a b", b=128), in0=B_[:].rearrange("p (a b) -> p a b", b=128),
                                                              in1=gains[:, br:br + 1, :].to_broadcast([128, 4, 128]), op=ALU.mult),
                 reads=[B_.res, gains.res], writes=[B_.res])

    def stage1d(w, i):
        gt = gts[i]
        for br in range(2):
            B_ = bn[br]
            BB = bbs[i][br]
            S.op(S.dve, lambda h, br=br, B_=B_, BB=BB, gt=gt: h.tensor_tensor(out=BB[:], in0=B_[:], in1=gt[:, br, :], op=ALU.mult), reads=[B_.res, gt.res], writes=[BB.res])

    def stage1b(w, i):
        for br in range(2):
            BB = bbs[i][br]
            pT1 = pT1s[br]
            for c in range(4):
                S.op(S.pe, lambda h, c=c, BB=BB, pT1=pT1: h.transpose(out=pT1[:, c * 128:(c + 1) * 128], in_=BB[:, c * 128:(c + 1) * 128], identity=self.ident[:]),
                     reads=[BB.res, self.ident.res], writes=[pT1.res], inc=(c == 3))
            BT = bTs[i][br]
            if br == 0:
                S.op(S.act, lambda h, BT=BT, pT1=pT1: h.activation(out=BT[:].rearrange("p a b -> p (a b)"), in_=pT1[:], func=AF.Copy), reads=[pT1.res], writes=[BT.res])
            else:
                S.op(S.dve, lambda h, BT=BT, pT1=pT1: h.tensor_copy(out=BT[:].rearrange("p a b -> p (a b)"), in_=pT1[:]), reads=[pT1.res], writes=[BT.res])

    cur_row = [None]

    def stage2(w, i):
        (tag, src, dst, row, u, t0) = work[w]
        AT, XR, gt = aTt[i], xr[i], gts[i]
        if cur_row[0] != row:
            cur_row[0] = row
            srcg = self.MOD[l][row:row + 1, 5 * D:6 * D]
            S.dma(S.sp, [(G5[:], dram_ap(srcg, srcg.offset, [[0, 128], [1, D]]))], reads=[self.R("MOD", l)], writes=[G5.res])
        for half in range(2):
            cs_ = slice(half * 512, (half + 1) * 512)
            for hh in range(8):
                S.op(S.pe, lambda h, hh=hh, AT=AT, cs_=cs_: h.matmul(py[0][:], AT[:, hh, :], Woa[:, hh, cs_], start=(hh == 0), stop=(hh == 7)),
                     reads=[AT.res, Woa.res], writes=[py[0].res], inc=(hh == 7))
            for c in range(4):
                S.op(S.pe, lambda h, c=c, cs_=cs_, BT=bTs[i][0]: h.matmul(py[1][:], BT[:, c, :], Wog[:, c, cs_], start=(c == 0), stop=(c == 3)),
                     reads=[bTs[i][0].res, Wog.res], writes=[py[1].res], inc=(c == 3))
            for c in range(4):
                S.op(S.pe, lambda h, c=c, cs_=cs_, BT=bTs[i][1]: h.matmul(py[2][:], BT[:, c, :], Wom[:, c, cs_], start=(c == 0), stop=(c == 3)),
                     reads=[bTs[i][1].res, Wom.res], writes=[py[2].res], inc=(c == 3))
            S.op(S.dve, lambda h, half=half, gt=gt: h.tensor_tensor(out=t1[0][:], in0=py[0][:], in1=gt[:, 2 + half, :], op=ALU.mult), reads=[py[0].res, gt.res], writes=[t1[0].res])
            S.op(S.dve, lambda h, half=half, gt=gt: h.tensor_tensor(out=t1[1][:], in0=py[1][:], in1=gt[:, 4 + half, :], op=ALU.mult), reads=[py[1].res, gt.res], writes=[t1[1].res])
            S.op(S.dve, lambda h, half=half, gt=gt: h.tensor_tensor(out=t1[2][:], in0=py[2][:], in1=gt[:, 6 + half, :], op=ALU.mult), reads=[py[2].res, gt.res], writes=[t1[2].res])
            S.op(S.dve, lambda h: h.tensor_tensor(out=t1[0][:], in0=t1[0][:], in1=t1[1][:], op=ALU.add), reads=[t1[0].res, t1[1].res], writes=[t1[0].res])
            S.op(S.dve, lambda h, cs_=cs_: h.tensor_tensor(out=yb[:, cs_], in0=t1[0][:], in1=t1[2][:], op=ALU.add), reads=[t1[0].res, t1[2].res], writes=[yb.res])
        for kc in range(8):
            S.op(S.pe, lambda h, kc=kc: h.transpose(out=pT2[:, kc * 128:(kc + 1) * 128], in_=yb[:, kc * 128:(kc + 1) * 128], identity=self.ident[:]),
                 reads=[yb.res, self.ident.res], writes=[pT2.res], inc=(kc == 7))
        S.op(S.act, lambda h: h.activation(out=yT[:].rearrange("p a b -> p (a b)"), in_=pT2[:], func=AF.Copy), reads=[pT2.res], writes=[yT.res])
        for half in range(2):
            cs_ = slice(half * 512, (half + 1) * 512)
            for kc in range(8):
                S.op(S.pe, lambda h, kc=kc, cs_=cs_: h.matmul(pY[:], yT[:, kc, :], Wo[:, kc, cs_], start=(kc == 0), stop=(kc == 7)),
                     reads=[yT.res, Wo.res], writes=[pY.res], inc=(kc == 7))
            tq = t1[1 + half]
            S.op(S.dve, lambda h, cs_=cs_, tq=tq: h.tensor_tensor(out=tq[:], in0=pY[:], in1=G5[:, cs_], op=ALU.mult), reads=[pY.res, G5.res], writes=[tq.res])
            S.op(S.pool, lambda h, cs_=cs_, XR=XR, tq=tq: h.tensor_tensor(out=XR[:, cs_], in0=XR[:, cs_], in1=tq[:], op=ALU.add), reads=[XR.res, tq.res], writes=[XR.res])
        S.dma(S.pool, [(dst[t0:t0 + 128, :], XR[:])], reads=[XR.res], writes=[self.R(dst.name)])

    def stage1all(w, i):
        stage1(w, i)
        stage1c(w, i)
        stage1g(w, i)
        stage1d(w, i)
        stage1b(w, i)

    stage1all(0, 0)
    for w in range(len(work)):
        if w + 1 < len(work):
            stage1all(w + 1, (w + 1) % 2)
        stage2(w, w % 2)


Builder.phase_merge = phase_merge

_NC_CACHE = {}


def kernel(**inputs):
    inp = {k: np.asarray(v) for k, v in inputs.items()}
    Bsz, SEQ, _ = inp["x"].shape
    T = SEQ
    if T not in _NC_CACHE:
        _NC_CACHE[T] = Builder(T).build()
    nc = _NC_CACHE[T]
    in_maps = [make_in_map(inp, b, 0, T) for b in range(Bsz)]
    res = run_bass_kernel_spmd(nc, in_maps, core_ids=list(range(Bsz)))
    out = np.stack([np.asarray(r["y"], dtype=np.float32) for r in res.results], axis=0)
    return out


W_NAMES = ["mod_w", "mod_b", "norm_g", "ffn1_w13", "ffn1_w2", "ffn2_w13", "ffn2_w2", "w_in", "attn_q_norm", "attn_k_norm", "attn_sink", "gla_w2", "gla_b", "mlstm_conv_w", "mlstm_conv_b", "mlstm_ib", "mlstm_fb", "gla_norm", "mlstm_norm", "w_out_attn", "w_out_gla", "w_out_mlstm", "w_o"]


def make_in_map(inp, b, t0, T):
    m = {"x": np.ascontiguousarray(inp["x"][b, t0:t0 + T]), "c": np.ascontiguousarray(inp["c"][b]),
         "ctx": np.ascontiguousarray(inp["ctx"][b]), "c_ctx": np.ascontiguousarray(inp["c_ctx"])}
    for k in W_NAMES:
        m[k] = np.ascontiguousarray(inp[k])
    m["rope_cs"] = rope_table(t0, T)
    return m


def rope_table(t0, T):
    pos = np.arange(t0, t0 + T)
    r = (pos // 64).astype(np.float32)
    col = (pos % 64).astype(np.float32)
    inv = (np.float32(10000.0) ** (-np.arange(16, dtype=np.float32) / np.float32(16))).astype(np.float32)
    ang = np.concatenate([r[:, None] * inv, col[:, None] * inv], axis=-1).astype(np.float32)
    return np.ascontiguousarray(np.stack([np.cos(ang), np.sin(ang)], axis=1).astype(np.float32))
```

```python
import numpy as np
from contextlib import ExitStack
import concourse.bass as bass
import concourse.mybir as mybir
from concourse.bass_utils import run_bass_kernel_spmd

F32 = mybir.dt.float32
BF16 = mybir.dt.bfloat16
AF = mybir.ActivationFunctionType
ALU = mybir.AluOpType
AX = mybir.AxisListType

D = 1024
DFF = 2816
NMOD = 9
LC = 256
EPS = 1e-6
DEPTH = 2
D_IN = 7472


class Res:
    __slots__ = ("name", "w", "r")

    def __init__(self, name=""):
        self.name = name
        self.w = None
        self.r = []


class Eng:
    def __init__(self, name, is_pe=False):
        self.name = name
        self.is_pe = is_pe
        self.ops = []
        self.sems = []
        self.si = 0
        self.cnt = 0
        self.seen = {}
        self.pend_r = []
        self.pend_w = []
        self.pool = []
        self.pi = 0


ROT = 30000


class Sched:
    def __init__(self, nc, es):
        self.nc = nc
        self.es = es
        self.pe = Eng("pe", True)
        self.act = Eng("act")
        self.dve = Eng("dve")
        self.pool = Eng("pool")
        self.sp = Eng("sp")
        self.engs = [self.pe, self.act, self.dve, self.pool, self.sp]
        self.semid = {}
        n_rot = {"pe": 6, "act": 3, "dve": 3, "pool": 3, "sp": 1}
        for e in self.engs:
            for i in range(n_rot[e.name]):
                s = es.enter_context(nc.semaphore(f"s_{e.name}{i}"))
                e.sems.append(s)
        for e, n in ((self.sp, 20), (self.pool, 10), (self.act, 4)):
            for i in range(n):
                s = es.enter_context(nc.semaphore(f"d_{e.name}{i}"))
                e.pool.append([s, 0])
        self.n_ops = 0

    def _need(self, eng, tok, raw):
        if tok is None:
            return None
        sem, val, owner = tok
        if owner == eng.name:
            if eng.is_pe:
                return None
            if not raw:
                return None
        key = id(sem)
        if eng.seen.get(key, 0) >= val:
            return None
        eng.seen[key] = val
        return (sem, val)

    def _waits(self, eng, reads, writes):
        ws = []
        for r in reads:
            w = self._need(eng, r.w, True)
            if w:
                ws.append(w)
        for wr in writes:
            w = self._need(eng, wr.w, False)
            if w:
                ws.append(w)
            for t in wr.r:
                w = self._need(eng, t, False)
                if w:
                    ws.append(w)
        for (sem, val) in ws:
            eng.ops.append(lambda h, sem=sem, val=val: h.wait_ge(sem, val))

    def _record(self, tok, reads, writes):
        for r in reads:
            r.r = [t for t in r.r if t[2] != tok[2] or t[0] is not tok[0]] + [tok]
        for w in writes:
            w.w = tok
            w.r = []

    def op(self, eng, fn, reads=(), writes=(), inc=True):
        self.n_ops += 1
        reads = list(reads)
        writes = list(writes)
        self._waits(eng, reads, writes)
        if not inc:
            eng.ops.append(lambda h, fn=fn: fn(h))
            eng.pend_r += reads
            eng.pend_w += writes
            return
        if eng.cnt >= ROT:
            eng.si += 1
            eng.cnt = 0
        eng.cnt += 1
        sem = eng.sems[eng.si]
        tok = (sem, eng.cnt, eng.name)
        eng.ops.append(lambda h, fn=fn, sem=sem: fn(h).then_inc(sem, 1))
        self._record(tok, reads + eng.pend_r, writes + eng.pend_w)
        eng.pend_r = []
        eng.pend_w = []

    def dma(self, eng, pairs, reads=(), writes=(), **kw):
        self.n_ops += 1
        reads = list(reads)
        writes = list(writes)
        self._waits(eng, reads, writes)
        ent = eng.pool[eng.pi]
        eng.pi = (eng.pi + 1) % len(eng.pool)
        sem = ent[0]
        if ent[1] > 0 and eng.seen.get(id(sem), 0) < ent[1]:
            v = ent[1]
            eng.ops.append(lambda h, sem=sem, v=v: h.wait_ge(sem, v))
            eng.seen[id(sem)] = v
        for (o, i) in pairs:
            ent[1] += 16
            eng.ops.append(lambda h, o=o, i=i, sem=sem: h.dma_start(out=o, in_=i, **kw).then_inc(sem, 16))
        tok = (sem, ent[1], "dma_" + eng.name + str(id(sem)))
        self._record(tok, reads, writes)

    def barrier(self):
        toks = []
        for e in self.engs:
            assert not e.pend_r and not e.pend_w, e.name
            for i in range(e.si + 1):
                v = ROT if i < e.si else e.cnt
                if v > 0:
                    toks.append((e, e.sems[i], v))
            for ent in e.pool:
                if ent[1] > 0:
                    toks.append((None, ent[0], ent[1]))
        for e in self.engs:
            for (own, sem, v) in toks:
                if own is e:
                    continue
                if e.seen.get(id(sem), 0) >= v:
                    continue
                e.seen[id(sem)] = v
                e.ops.append(lambda h, sem=sem, v=v: h.wait_ge(sem, v))

    def finish(self):
        for e in (self.sp, self.pool, self.act):
            for ent in e.pool:
                if ent[1] > 0:
                    self.sp.ops.append(lambda h, sem=ent[0], v=ent[1]: h.wait_ge(sem, v))

    def replay(self):
        nc = self.nc
        with nc.Block() as block:
            @block.tensor
            def _(h):
                for f in self.pe.ops:
                    f(h)

            @block.scalar
            def _(h):
                for f in self.act.ops:
                    f(h)

            @block.vector
            def _(h):
                for f in self.dve.ops:
                    f(h)

            @block.gpsimd
            def _(h):
                for f in self.pool.ops:
                    f(h)

            @block.sync
            def _(h):
                for f in self.sp.ops:
                    f(h)


class Tile:
    def __init__(self, t, name):
        self.t = t
        self.res = Res(name)

    def __getitem__(self, k):
        return self.t[k]


def dram_ap(t, offset, pattern):
    return bass.AP(t.tensor, offset, pattern)


class Builder:
    def __init__(self, T, depth=DEPTH, stop=None, dbg=()):
        self.T = T
        self.depth = depth
        self.stop = stop
        self.dbg = dbg
        self.nc = bass.Bass("TRN2", target_bir_lowering=False)
        self.es = ExitStack()
        self.S = None
        self.dres = {}
        self.rr = {}

    def din(self, name, shape):
        return self.nc.dram_tensor(name, list(shape), F32, kind="ExternalInput").ap()

    def dout(self, name, shape, dt=F32):
        return self.nc.dram_tensor(name, list(shape), dt, kind="ExternalOutput").ap()

    def dscr(self, name, shape, dt=F32):
        if name in self.dbg:
            return self.nc.dram_tensor(name, list(shape), dt, kind="ExternalOutput").ap()
        return self.nc.dram_tensor(name, list(shape), dt).ap()

    def R(self, *key):
        if key not in self.dres:
            self.dres[key] = Res(str(key))
        return self.dres[key]

    def sb(self, name, shape, dt):
        self.uid = getattr(self, "uid", 0) + 1
        name = f"{name}_{self.uid}"
        t = self.cur.enter_context(self.nc.sbuf_tensor(name, list(shape), dt))
        return Tile(t, name)

    def ps(self, name, shape, dt=F32):
        self.uid = getattr(self, "uid", 0) + 1
        name = f"{name}_{self.uid}"
        t = self.cur.enter_context(self.nc.psum_tensor(name, list(shape), dt))
        return Tile(t, name)

    def build(self):
        nc = self.nc
        T = self.T
        L = self.depth
        with self.es as es:
            self.S = S = Sched(nc, es)
            self.x_in = self.din("x", [T, D])
            self.c_in = self.din("c", [D])
            self.ctx_in = self.din("ctx", [LC, D])
            self.cctx_in = self.din("c_ctx", [D])
            self.mod_w = self.din("mod_w", [L, D, NMOD * D])
            self.mod_b = self.din("mod_b", [L, NMOD * D])
            self.norm_g = self.din("norm_g", [L, 3, D])
            self.ffn_w13 = [self.din("ffn1_w13", [L, D, 2 * DFF]), self.din("ffn2_w13", [L, D, 2 * DFF])]
            self.ffn_w2 = [self.din("ffn1_w2", [L, DFF, D]), self.din("ffn2_w2", [L, DFF, D])]
            self.w_in = self.din("w_in", [L, D, D_IN])
            self.attn_q_norm = self.din("attn_q_norm", [L, 64])
            self.attn_k_norm = self.din("attn_k_norm", [L, 64])
            self.attn_sink = self.din("attn_sink", [L, 8])
            self.gla_w2 = self.din("gla_w2", [L, 2, 16, 256])
            self.gla_b = self.din("gla_b", [L, 2, 256])
            self.rope_cs = self.din("rope_cs", [T, 2, 32])
            self.y_out = self.dout("y", [T, D])
            TT = self.TT = LC + T
            self.H2T = [self.dscr(f"H2T{l}", [D, TT], BF16) for l in range(L)]
            self.QT = [self.dscr(f"QT{l}", [64, 8, TT], BF16) for l in range(L)]
            self.KT = [self.dscr(f"KT{l}", [64, 2, TT], BF16) for l in range(L)]
            self.VA = [self.dscr(f"VA{l}", [TT, 128], BF16) for l in range(L)]
            self.GV = [self.dscr(f"GV{l}", [TT, 512], BF16) for l in range(L)]
            self.MV = [self.dscr(f"MV{l}", [TT, 512], BF16) for l in range(L)]
            self.GATES = [self.dscr(f"GATES{l}", [16, TT]) for l in range(L)]
            self.QG = [self.dscr(f"QG{l}", [2, 64, 4, TT], BF16) for l in range(L)]
            self.KG = [self.dscr(f"KG{l}", [2, 64, 4, TT], BF16) for l in range(L)]
            self.KH = [self.dscr(f"KH{l}", [2, 64, 4, TT], BF16) for l in range(L)]
            self.MQK = [self.dscr(f"MQK{l}", [D, TT + 4], BF16) for l in range(L)]
            self.ATT = [self.dscr(f"ATT{l}", [64, 8, TT], BF16) for l in range(L)]
            self.OG = [self.dscr(f"OG{l}", [2, TT, 512]) for l in range(L)]
            self.OM = [self.dscr(f"OM{l}", [2, TT, 512]) for l in range(L)]
            self.MROWS = [self.dscr(f"MROWS{l}", [2, 5, 4, TT]) for l in range(L)]
            self.MQC = [self.dscr(f"MQC{l}", [D, TT], BF16) for l in range(L)]
            self.gla_norm = self.din("gla_norm", [L, 128])
            self.mlstm_norm = self.din("mlstm_norm", [L, 128])
            self.w_out_attn = self.din("w_out_attn", [L, 512, D])
            self.w_out_gla = self.din("w_out_gla", [L, 512, D])
            self.w_out_mlstm = self.din("w_out_mlstm", [L, 512, D])
            self.w_o = self.din("w_o", [L, D, D])
            self.X2 = [self.dscr(f"X2_{l}", [T, D]) for l in range(L)]
            self.C2 = [self.dscr(f"C2_{l}", [LC, D]) for l in range(L)]
            self.X3 = [self.dscr(f"X3_{l}", [T, D]) for l in range(L)]
            self.C3 = [self.dscr(f"C3_{l}", [LC, D]) for l in range(L)]
            self.conv_w = self.din("mlstm_conv_w", [L, 5, D])
            self.conv_b = self.din("mlstm_conv_b", [L, D])
            self.mlstm_ib = self.din("mlstm_ib", [L, 2, 4])
            self.mlstm_fb = self.din("mlstm_fb", [L, 2, 4])
            self.MOD = [self.dscr(f"MOD{l}", [2, NMOD * D]) for l in range(L)]
            self.X1 = [self.dscr(f"X1_{l}", [T, D]) for l in range(L)]
            self.C1 = [self.dscr(f"C1_{l}", [LC, D]) for l in range(L)]
            with ExitStack() as cst:
                self.cur = cst
                self.ident = self.sb("ident", [128, 128], BF16)
                self.identf = self.sb("identf", [128, 128], F32)
                self.eps_col = self.sb("eps_col", [128, 1], F32)
                S.op(S.dve, lambda h: h.memset(self.eps_col[:], EPS), writes=[self.eps_col.res])
                self.make_consts()
                for l in range(L):
                    xin = self.x_in if l == 0 else self.X3[l - 1]
                    cin = self.ctx_in if l == 0 else self.C3[l - 1]
                    with ExitStack() as ph:
                        self.cur = ph
                        self.phase_mod(l)
                        S.barrier()
                    if self.stop == ("mod", l):
                        break
                    with ExitStack() as ph:
                        self.cur = ph
                        self.phase_ffn(l, 0, [("ctx", cin, self.C1[l], LC, 1), ("lat", xin, self.X1[l], T, 0)])
                        S.barrier()
                    if self.stop == ("ffn1", l):
                        break
                    with ExitStack() as lay:
                        self.cur = lay
                        self.EL = self.sb("EL", [64, 4, 2, TT // 64], F32)
                        with ExitStack() as ph:
                            self.cur = ph
                            self.phase_feat(l, [("ctx", self.C1[l], LC, 1, 0, False), ("lat", self.X1[l], T, 0, LC, True)])
                            S.barrier()
                        if self.stop == ("feat", l):
                            break
                        with ExitStack() as ph:
                            self.cur = ph
                            self.phase_attn(l, l < L - 1)
                            S.barrier()
                        if self.stop == ("attn", l):
                            break
                        with ExitStack() as ph:
                            self.cur = ph
                            self.phase_gla(l)
                            S.barrier()
                        if self.stop == ("gla", l):
                            break
                        self.cur = lay
                        self.DEC = self.sb("DEC", [128, 2, 4, TT // 64], F32)
                        self.sel = self.sb("sel", [4, 4, 128], F32)
                        S.op(S.dve, lambda h, sel_t=self.sel: h.tensor_copy(out=sel_t[:], in_=self.identf[0:4, 0:4].unsqueeze(2).to_broadcast([4, 4, 128])),
                             reads=[self.identf.res], writes=[self.sel.res])
                        stop_ml = False
                        for ph_name, ph_fn in (("mlg", self.phase_ml_gates), ("mlc", self.phase_ml_conv), ("mls", self.phase_ml_scan)):
                            with ExitStack() as ph:
                                self.cur = ph
                                ph_fn(l)
                                S.barrier()
                            if self.stop == (ph_name, l):
                                stop_ml = True
                                break
                        if stop_ml:
                            break
                        if self.stop == ("ml", l):
                            break
                    last = (l == L - 1)
                    with ExitStack() as ph:
                        self.cur = ph
                        st = [("lat", self.X1[l], self.X2[l], T, 0, LC)]
                        if not last:
                            st = [("ctx", self.C1[l], self.C2[l], LC, 1, 0)] + st
                        self.phase_merge(l, st)
                        S.barrier()
                    if self.stop == ("merge", l):
                        break
                    with ExitStack() as ph:
                        self.cur = ph
                        xdst = self.y_out if last else self.X3[l]
                        st = [("lat", self.X2[l], xdst, T, 0)]
                        import os
                        if not last and not os.environ.get("NOCTX2"):
                            st = [("ctx", self.C2[l], self.C3[l], LC, 1)] + st
                        self.phase_ffn(l, 1, [(a, b, c, d_, e) for (a, b, c, d_, e) in st])
                        S.barrier()
                    if self.stop == ("ffn2", l):
                        break
                S.finish()
                S.replay()
        return nc

    def make_consts(self):
        S = self.S
        nc = self.nc
        idf = self.identf
        S.op(S.pool, lambda h: h.memset(idf[:], 0.0), writes=[idf.res])
        S.op(S.pool, lambda h: h.affine_select(out=idf[:], in_=idf[:], pattern=[[-1, 128]],
                                                compare_op=ALU.not_equal, fill=1.0, base=0,
                                                channel_multiplier=1),
             reads=[idf.res], writes=[idf.res])
        S.op(S.dve, lambda h: h.tensor_copy(out=self.ident[:], in_=idf[:]), reads=[idf.res], writes=[self.ident.res])

    def phase_mod(self, l):
        S = self.S
        cl = self.sb("cl", [128, 8, 2], F32)
        cs = self.sb("cs", [128, 8, 2], F32)
        S.dma(S.sp, [(cl[:, :, 0], self.c_in.rearrange("(kc p) -> p kc", p=128)),
                     (cl[:, :, 1], self.cctx_in.rearrange("(kc p) -> p kc", p=128))],
              writes=[cl.res], allow_slow_non_contiguous=True)
        S.op(S.act, lambda h: h.activation(out=cs[:], in_=cl[:], func=AF.Silu), reads=[cl.res], writes=[cs.res])
        wm = [self.sb(f"wm{i}", [128, 8, 512], F32) for i in range(2)]
        mb = [self.sb(f"mb{i}", [2, 512], F32) for i in range(2)]
        mo = [self.sb(f"mo{i}", [2, 512], F32) for i in range(2)]
        pm = [self.ps(f"pm{i}", [2, 512]) for i in range(2)]
        mw = self.mod_w[l].rearrange("(kc p) n -> p kc n", p=128)
        for n in range(18):
            i = n % 2
            S.dma(S.sp, [(wm[i][:, 0:4, :], mw[:, 0:4, n * 512:(n + 1) * 512]),
                         (wm[i][:, 4:8, :], mw[:, 4:8, n * 512:(n + 1) * 512])], writes=[wm[i].res])
            mbsrc = self.mod_b[l:l + 1, n * 512:(n + 1) * 512]
            S.dma(S.sp, [(mb[i][0:1, :], mbsrc), (mb[i][1:2, :], mbsrc)], writes=[mb[i].res])
            for kc in range(8):
                S.op(S.pe, lambda h, kc=kc, i=i: h.matmul(pm[i][:], cs[:, kc, :], wm[i][:, kc, :],
                                                           start=(kc == 0), stop=(kc == 7)),
                     reads=[cs.res, wm[i].res], writes=[pm[i].res], inc=(kc == 7))
            S.op(S.dve, lambda h, i=i: h.tensor_tensor(out=mo[i][:], in0=pm[i][:], in1=mb[i][:], op=ALU.add),
                 reads=[pm[i].res, mb[i].res], writes=[mo[i].res])
            S.dma(S.sp, [(self.MOD[l][:, n * 512:(n + 1) * 512], mo[i][:])], reads=[mo[i].res],
                  writes=[self.R("MOD", l)])

    def load_cols(self, dst_ap, src_row_ap, res):
        self.S.dma(self.S.sp, [(dst_ap, src_row_ap.rearrange("(kc p) -> p kc", p=128))], writes=[res],
                   allow_slow_non_contiguous=True)

    def adaln_cols(self, l, j, row, tag):
        S = self.S
        tmp = self.sb(f"adt_{tag}", [128, 3, 8], F32)
        A = self.sb(f"adA_{tag}", [128, 8], F32)
        MODr = self.MOD[l]
        S.dma(S.sp, [(tmp[:, 0, :], MODr[row, (3 * j) * D:(3 * j + 1) * D].rearrange("(kc p) -> p kc", p=128)),
                     (tmp[:, 1, :], MODr[row, (3 * j + 1) * D:(3 * j + 2) * D].rearrange("(kc p) -> p kc", p=128)),
                     (tmp[:, 2, :], self.norm_g[l, j, :].rearrange("(kc p) -> p kc", p=128))],
              reads=[self.R("MOD", l)], writes=[tmp.res], allow_slow_non_contiguous=True)
        S.op(S.dve, lambda h: h.scalar_tensor_tensor(out=A[:], in0=tmp[:, 1, :], scalar=1.0, in1=tmp[:, 2, :],
                                                      op0=ALU.add, op1=ALU.mult),
             reads=[tmp.res], writes=[A.res])
        return A, tmp

    def gate_bc(self, l, j, row, tag, mul):
        S = self.S
        G = self.sb(f"gate_{tag}", [128, D], F32)
        src = self.MOD[l][row:row + 1, (3 * j + 2) * D:(3 * j + 3) * D]
        src_b = dram_ap(src, src.offset, [[0, 128], [1, D]])
        S.dma(S.sp, [(G[:], src_b)], reads=[self.R("MOD", l)], writes=[G.res])
        if mul != 1.0:
            S.op(S.pool, lambda h: h.tensor_scalar(out=G[:], in0=G[:], scalar1=float(mul), scalar2=None, op0=ALU.mult),
                 reads=[G.res], writes=[G.res])
        return G

    def load_weight_bf16(self, dst, src3, nsplit):
        S = self.S
        kcn = dst.t.shape[1]
        step = max(1, kcn // nsplit)
        dst.parts = []
        for k0 in range(0, kcn, step):
            k1 = min(kcn, k0 + step)
            r = Res(f"wpart{k0}")
            dst.parts.append(r)
            S.dma(S.pool, [(dst[:, k0:k1, :], src3[:, k0:k1, :])], writes=[r])

    def norm_part(self, xt, nb, ss, rs, junk=None):
        S = self.S
        if junk is None:
            junk = self.junk
        S.op(S.act, lambda h: h.activation(out=junk[:], in_=xt[:], func=AF.Square, accum_out=ss[:]),
             reads=[xt.res], writes=[junk.res, ss.res])
        S.op(S.act, lambda h: h.activation(out=rs[:], in_=ss[:], func=AF.Sqrt, scale=1.0 / D, bias=self.eps_col[:]),
             reads=[ss.res], writes=[rs.res])
        S.op(S.dve, lambda h: h.reciprocal(out=rs[:], in_=rs[:]), reads=[rs.res], writes=[rs.res])
        S.op(S.dve, lambda h: h.tensor_scalar(out=nb[:], in0=xt[:], scalar1=rs[:], scalar2=None, op0=ALU.mult),
             reads=[xt.res, rs.res], writes=[nb.res])

    def transpose_part(self, nb, pT, hT, col0, A, sh, evac_engs):
        S = self.S
        for kc in range(8):
            S.op(S.pe, lambda h, kc=kc: h.transpose(out=pT[:, kc * 128:(kc + 1) * 128], in_=nb[:, kc * 128:(kc + 1) * 128],
                                                     identity=self.ident[:]),
                 reads=[nb.res, self.ident.res], writes=[pT.res], inc=(kc == 7))
        for kc in range(8):
            e = evac_engs[kc % len(evac_engs)]
            if e is S.act:
                S.op(e, lambda h, kc=kc: h.activation(out=hT[:, kc, col0:col0 + 128], in_=pT[:, kc * 128:(kc + 1) * 128],
                                                      func=AF.Identity, scale=A[:, kc:kc + 1], bias=sh[:, kc:kc + 1]),
                     reads=[pT.res, A.res, self.shres], writes=[hT.res])
            else:
                S.op(e, lambda h, kc=kc: h.tensor_scalar(out=hT[:, kc, col0:col0 + 128], in0=pT[:, kc * 128:(kc + 1) * 128],
                                                         scalar1=A[:, kc:kc + 1], scalar2=sh[:, kc:kc + 1],
                                                         op0=ALU.mult, op1=ALU.add),
                     reads=[pT.res, A.res, self.shres], writes=[hT.res])

    def phase_ffn(self, l, which, streams):
        S = self.S
        j = 0 if which == 0 else 2
        W13 = self.sb("W13", [128, 8, 2 * DFF], BF16)
        W2 = self.sb("W2", [128, 22, D], BF16)
        self.load_weight_bf16(W13, self.ffn_w13[which][l].rearrange("(kc p) n -> p kc n", p=128), 8)
        self.load_weight_bf16(W2, self.ffn_w2[which][l].rearrange("(fc p) n -> p fc n", p=128), 11)
        import os
        if os.environ.get("FFN_WONLY") and which == 1:
            return
        xl = [self.sb(f"xl{i}", [128, D], F32) for i in range(3)]
        xr = [self.sb(f"xr{i}", [128, D], F32) for i in range(2)]
        nb = [self.sb(f"nb{i}", [128, D], BF16) for i in range(4)]
        ss = [self.sb(f"ss{i}", [128, 1], F32) for i in range(4)]
        rs = [self.sb(f"rs{i}", [128, 1], F32) for i in range(4)]
        hT = self.sb("hT", [128, 8, 512], BF16)
        gT = self.sb("gT", [128, 22, 512], BF16)
        sa = [self.sb(f"sa{i}", [128, 512], F32) for i in range(2)]
        tt = [self.sb(f"tt{i}", [128, 512], F32) for i in range(2)]
        pT = [self.ps(f"pT{i}", [128, D], BF16) for i in range(2)]
        pA = [self.ps(f"pA{i}", [128, 512]) for i in range(2)]
        pB = [self.ps(f"pB{i}", [128, 512]) for i in range(2)]
        pY = [self.ps(f"pY{i}", [128, 512]) for i in range(2)]
        cnt = {"xl": 0, "xr": 0, "nb": 0, "pT": 0, "pAB": 0, "sa": 0, "tt": 0}

        for (tag, src, dst, ntok, row) in streams:
            A, tmp = self.adaln_cols(l, j, row, f"{which}{tag}")
            sh = tmp[:, 0, :]
            self.shres = tmp.res
            G = self.gate_bc(l, j, row, f"{which}{tag}", 0.5)
            tiles = [(t0, min(512, ntok - t0)) for t0 in range(0, ntok, 512)]
            rtag = ("xs", l, which, tag)

            def prep_norm(t0, s):
                i = cnt["xl"] % 3
                cnt["xl"] += 1
                k = cnt["nb"] % 4
                cnt["nb"] += 1
                S.dma(S.sp, [(xl[i][:], src[t0 + s * 128:t0 + (s + 1) * 128, :])], reads=[self.R(src.name)],
                      writes=[xl[i].res])
                self.norm_part(xl[i], nb[k], ss[k], rs[k], junk=nb[k])
                return nb[k]

            def prep_tr(nbt, s):
                k = cnt["pT"] % 2
                cnt["pT"] += 1
                self.transpose_part(nbt, pT[k], hT, s * 128, A, sh, [S.act, S.dve])

            def prep(t0, n):
                for s in range(n // 128):
                    nbt = prep_norm(t0, s)
                    prep_tr(nbt, s)

            prep(*tiles[0])
            for ti, (t0, n) in enumerate(tiles):
                nt = n // 128
                for p in range(22):
                    k = cnt["pAB"] % 2
                    cnt["pAB"] += 1
                    for kc in range(8):
                        S.op(S.pe, lambda h, kc=kc, p=p, k=k, n=n: h.matmul(pA[k][:, :n], W13[:, kc, p * 128:(p + 1) * 128], hT[:, kc, :n],
                                                                        start=(kc == 0), stop=(kc == 7)),
                             reads=W13.parts + [hT.res], writes=[pA[k].res], inc=(kc == 7))
                    for kc in range(8):
                        S.op(S.pe, lambda h, kc=kc, p=p, k=k, n=n: h.matmul(pB[k][:, :n], W13[:, kc, DFF + p * 128:DFF + (p + 1) * 128], hT[:, kc, :n],
                                                                        start=(kc == 0), stop=(kc == 7)),
                             reads=W13.parts + [hT.res], writes=[pB[k].res], inc=(kc == 7))
                    q = cnt["sa"] % 2
                    cnt["sa"] += 1
                    S.op(S.act, lambda h, k=k, q=q, n=n: h.activation(out=sa[q][:, :n], in_=pA[k][:, :n], func=AF.Silu),
                         reads=[pA[k].res], writes=[sa[q].res])
                    S.op(S.dve, lambda h, k=k, q=q, p=p, n=n: h.tensor_tensor(out=gT[:, p, :n], in0=sa[q][:, :n], in1=pB[k][:, :n], op=ALU.mult),
                         reads=[sa[q].res, pB[k].res], writes=[gT.res])
                xrs = []
                for s in range(nt):
                    pass
                if ti + 1 < len(tiles):
                    pending = tiles[ti + 1]
                else:
                    pending = None
                nbts = []
                if pending is not None:
                    for s in range(pending[1] // 128):
                        nbts.append(prep_norm(pending[0], s))
                for s in range(nt):
                    i = cnt["xr"] % 2
                    cnt["xr"] += 1
                    S.dma(S.sp, [(xr[i][:], src[t0 + s * 128:t0 + (s + 1) * 128, :])], reads=[self.R(src.name)],
                          writes=[xr[i].res])
                    for dh in range(2):
                        for fc in range(22):
                            S.op(S.pe, lambda h, fc=fc, dh=dh, s=s: h.matmul(pY[dh][:], gT[:, fc, s * 128:(s + 1) * 128], W2[:, fc, dh * 512:(dh + 1) * 512],
                                                                              start=(fc == 0), stop=(fc == 21)),
                                 reads=[gT.res] + W2.parts, writes=[pY[dh].res], inc=(fc == 21))
                    for dh in range(2):
                        q = cnt["tt"] % 2
                        cnt["tt"] += 1
                        S.op(S.dve, lambda h, dh=dh, q=q, G=G: h.tensor_tensor(out=tt[q][:], in0=pY[dh][:], in1=G[:, dh * 512:(dh + 1) * 512], op=ALU.mult),
                             reads=[pY[dh].res, G.res], writes=[tt[q].res])
                        S.op(S.pool, lambda h, dh=dh, q=q, i=i: h.tensor_tensor(out=xr[i][:, dh * 512:(dh + 1) * 512], in0=xr[i][:, dh * 512:(dh + 1) * 512],
                                                                                 in1=tt[q][:], op=ALU.add),
                             reads=[tt[q].res, xr[i].res], writes=[xr[i].res])
                    S.dma(S.pool, [(dst[t0 + s * 128:t0 + (s + 1) * 128, :], xr[i][:])], reads=[xr[i].res],
                          writes=[self.R(dst.name)])
                for s, nbt in enumerate(nbts):
                    prep_tr(nbt, s)


O_AQ, O_AK, O_AV = 0, 512, 640
O_GQ, O_GK, O_GV, O_GR, O_GG = 768, 1024, 1280, 1792, 2304
O_MQ, O_MK, O_MV, O_MO, O_MI, O_MF = 2336, 2848, 3360, 3872, 4384, 4392
O_SA, O_SG, O_SM = 4400, 5424, 6448


def bcast_rows(ap2d, nparts):
    return bass.AP(ap2d.tensor, ap2d.offset, [[0, nparts]] + [list(x) for x in ap2d.ap[1:]])


def rev_last(ap):
    pat = [list(x) for x in ap.ap]
    st, n = pat[-1]
    return bass.AP(ap.tensor, ap.offset + st * (n - 1), pat[:-1] + [[-st, n]])


def phase_feat(self, l, streams):
    S = self.S
    TT = self.TT
    win = self.w_in[l].rearrange("(kc p) n -> p kc n", p=128)
    Wa = self.sb("Wa", [128, 8, 768], BF16)
    Wv = self.sb("Wv", [128, 8, 1024], BF16)
    Wf = self.sb("Wf", [128, 8, 1536], BF16)
    Wg = self.sb("Wg", [128, 8, 48], BF16)
    S.dma(S.pool, [(Wa[:, 0:4, :], win[:, 0:4, 0:768]), (Wa[:, 4:8, :], win[:, 4:8, 0:768])], writes=[Wa.res])
    for k0 in range(0, 8, 2):
        S.dma(S.pool, [(Wv[:, k0:k0 + 2, 0:512], win[:, k0:k0 + 2, O_GV:O_GV + 512]),
                       (Wv[:, k0:k0 + 2, 512:1024], win[:, k0:k0 + 2, O_MV:O_MV + 512])], writes=[Wv.res])
        S.dma(S.pool, [(Wf[:, k0:k0 + 2, 0:512], win[:, k0:k0 + 2, O_GQ:O_GQ + 512]),
                       (Wf[:, k0:k0 + 2, 512:1536], win[:, k0:k0 + 2, O_MQ:O_MQ + 1024])], writes=[Wf.res])
    S.dma(S.pool, [(Wg[:, :, 0:32], win[:, :, O_GG:O_GG + 32]), (Wg[:, :, 32:48], win[:, :, O_MI:O_MI + 16])], writes=[Wg.res])
    W2p = self.sb("W2p", [32, 2, 256], F32)
    S.op(S.dve, lambda h: h.memset(W2p[:], 0.0), writes=[W2p.res])
    S.dma(S.sp, [(W2p[0:16, 0, :], self.gla_w2[l, 0]), (W2p[16:32, 1, :], self.gla_w2[l, 1])], writes=[W2p.res])
    negb = self.sb("negb", [128, 2, 2], F32)
    S.dma(S.sp, [(negb[:, d, :], self.gla_b[l, d, :].rearrange("(c p) -> p c", p=128)) for d in range(2)],
          writes=[negb.res], allow_slow_non_contiguous=True)
    S.op(S.dve, lambda h: h.tensor_scalar(out=negb[:], in0=negb[:], scalar1=-1.0, scalar2=None, op0=ALU.mult),
         reads=[negb.res], writes=[negb.res])
    gain = self.sb("gain", [128, 10, 64], F32)
    qn_src = self.attn_q_norm[l:l + 1, :]
    kn_src = self.attn_k_norm[l:l + 1, :]
    S.dma(S.sp, [(gain[:, 0:8, :], bass.AP(qn_src.tensor, qn_src.offset, [[0, 128], [0, 8], [1, 64]])),
                 (gain[:, 8:10, :], bass.AP(kn_src.tensor, kn_src.offset, [[0, 128], [0, 2], [1, 64]]))],
          writes=[gain.res])
    mask01 = self.sb("mask01", [128, 8, 64], F32)
    S.op(S.pool, lambda h: h.memset(mask01[:], 1.0), writes=[mask01.res])
    S.op(S.pool, lambda h: h.memset(mask01[:, :, 0:1], 0.0), writes=[mask01.res])
    self.junk = self.sb("junk", [128, D], BF16)

    xl = [self.sb(f"xl{i}", [128, D], F32) for i in range(3)]
    nb = [self.sb(f"nb{i}", [128, D], BF16) for i in range(2)]
    ss = [self.sb(f"ss{i}", [128, 1], F32) for i in range(2)]
    rs = [self.sb(f"rs{i}", [128, 1], F32) for i in range(2)]
    hTs = [self.sb(f"hT{i}", [128, 8, 512], BF16) for i in range(2)]
    sqt = self.sb("sqt", [128, 640], F32)
    ssh = self.sb("ssh", [128, 10], F32)
    rinv = self.sb("rinv", [128, 10], F32)
    qn = self.sb("qn", [128, 10, 64], F32)
    rt = [self.sb(f"rt{i}", [128, 10, 32], F32) for i in range(4)]
    cs_t = [self.sb(f"cst{i}", [128, 2, 32], F32) for i in range(3)]
    qr = [self.sb(f"qr{i}", [128, 10, 64], BF16) for i in range(2)]
    vb = [self.sb(f"vb{i}", [128, 128], BF16) for i in range(4)]
    vb2 = [self.sb(f"vb2{i}", [128, 512], BF16) for i in range(4)]
    QTs = self.sb("QTs", [64, 8, 512], BF16)
    KTs = self.sb("KTs", [64, 2, 512], BF16)
    ggT = self.sb("ggT", [32, 512], F32)
    gts = self.sb("gts", [16, 512], F32)
    ex = [self.sb(f"ex{i}", [128, 512], F32) for i in range(2)]
    csum = [self.sb(f"csum{i}", [128, 512], F32) for i in range(2)]
    eb = [[self.sb(f"eb{d}{c}", [128, 512], F32) for c in range(2)] for d in range(2)]
    enb = [[self.sb(f"enb{d}{c}", [128, 512], F32) for c in range(2)] for d in range(2)]
    ebl = [[self.sb(f"ebl{d}{c}", [128, 512], F32) for c in range(2)] for d in range(2)]
    fo = [self.sb(f"fo{i}", [128, 512], BF16) for i in range(8)]
    elcs = [self.sb(f"elc{i}", [128, 8], F32) for i in range(2)]
    pT = self.ps("pT", [128, D], BF16)
    pq = self.ps("pq", [128, 512])
    pkv = self.ps("pkv", [128, 256])
    pqt = self.ps("pqt", [64, 8, 128], BF16)
    pkt = self.ps("pkt", [64, 2, 128], BF16)
    pf = [self.ps(f"pf{i}", [128, 512]) for i in range(2)]
    pz = self.ps("pz", [128, 512])
    cnt = {"xl": 0, "pf": 0, "fo": 0, "qr": 0, "vb": 0, "vb2": 0, "ex": 0, "rt": 0, "cs": 0, "elc": 0}
    H2Tv = self.H2T[l].rearrange("(kc p) t -> p kc t", p=128)

    def nxt(key, n):
        v = cnt[key] % n
        cnt[key] += 1
        return v

    st_info = []
    for (tag, src, ntok, row, uoff, rope) in streams:
        A, tmp = self.adaln_cols(l, 1, row, f"f{tag}")
        st_info.append((A, tmp, src, uoff, rope))
    tiles = []
    for si, (tag, src, ntok, row, uoff, rope) in enumerate(streams):
        for t0 in range(0, ntok, 512):
            tiles.append((si, t0, min(512, ntok - t0)))

    def prep_load(k, s):
        si, t0, n = tiles[k]
        A, tmp, src, uoff, rope = st_info[si]
        i = nxt("xl", 3)
        S.dma(S.sp, [(xl[i][:], src[t0 + s * 128:t0 + (s + 1) * 128, :])], reads=[self.R(src.name)], writes=[xl[i].res])
        cst = None
        return i

    def rope_load(k, s):
        si, t0, n = tiles[k]
        A, tmp, src, uoff, rope = st_info[si]
        if not rope:
            return None
        cst = cs_t[nxt("cs", 3)]
        S.dma(S.sp, [(cst[:], self.rope_cs[t0 + s * 128:t0 + s * 128 + 128, :, :])], writes=[cst.res])
        return cst

    def prep_sub(k, s, i=None):
        si, t0, n = tiles[k]
        A, tmp, src, uoff, rope = st_info[si]
        hT = hTs[k % 2]
        if i is None:
            i = prep_load(k, s)
        self.norm_part(xl[i], nb[i % 2], ss[i % 2], rs[i % 2])
        self.shres = tmp.res
        self.transpose_part(nb[i % 2], pT, hT, s * 128, A, tmp[:, 0, :], [S.act, S.dve])

    def g_part(k):
        si, t0, n = tiles[k]
        A, tmp, src, uoff, rope = st_info[si]
        hT = hTs[k % 2]
        u0 = uoff + t0
        nch = n // 64
        for kc in range(8):
            S.op(S.pe, lambda h, kc=kc, n=n, hT=hT: h.matmul(pz[0:32, :n], Wg[:, kc, 0:32], hT[:, kc, :n], start=(kc == 0), stop=(kc == 7)),
                 reads=[hT.res, Wg.res], writes=[pz.res], inc=(kc == 7))
        S.op(S.act, lambda h, n=n: h.activation(out=ggT[:, :n], in_=pz[0:32, :n], func=AF.Copy), reads=[pz.res], writes=[ggT.res])
        for kc in range(8):
            S.op(S.pe, lambda h, kc=kc, n=n, hT=hT: h.matmul(pz[0:16, :n], Wg[:, kc, 32:48], hT[:, kc, :n], start=(kc == 0), stop=(kc == 7)),
                 reads=[hT.res, Wg.res], writes=[pz.res], inc=(kc == 7))
        S.op(S.act, lambda h, n=n: h.activation(out=gts[:, :n], in_=pz[0:16, :n], func=AF.Copy), reads=[pz.res], writes=[gts.res])
        S.dma(S.sp, [(self.GATES[l][:, u0:u0 + n], gts[:, :n])], reads=[gts.res], writes=[self.R(self.GATES[l].name)])
        for d in range(2):
            for c2 in range(2):
                S.op(S.pe, lambda h, d=d, c2=c2, n=n: h.matmul(pz[:, :n], W2p[:, d, c2 * 128:(c2 + 1) * 128], ggT[:, :n], start=True, stop=True),
                     reads=[W2p.res, ggT.res], writes=[pz.res])
                e_ = ex[nxt("ex", 2)]
                c_ = csum[(cnt["ex"]) % 2]
                S.op(S.act, lambda h, d=d, c2=c2, n=n, e_=e_: h.activation(out=e_[:, :n], in_=pz[:, :n], func=AF.Exp, scale=-1.0, bias=negb[:, d, c2:c2 + 1]),
                     reads=[pz.res, negb.res], writes=[e_.res])
                S.op(S.act, lambda h, n=n, e_=e_: h.activation(out=e_[:, :n], in_=e_[:, :n], func=AF.Ln, bias=1.0), reads=[e_.res], writes=[e_.res])
                m01 = mask01[:].rearrange("p a b -> p (a b)")[:, :n]
                if d == 0:
                    S.op(S.dve, lambda h, n=n, e_=e_, c_=c_, m01=m01: h.tensor_tensor_scan(out=c_[:, :n], data0=m01, data1=e_[:, :n], initial=0.0, op0=ALU.mult, op1=ALU.add),
                         reads=[e_.res, mask01.res], writes=[c_.res])
                    last = 63
                else:
                    S.op(S.dve, lambda h, n=n, e_=e_, c_=c_, m01=m01: h.tensor_tensor_scan(out=rev_last(c_[:, :n]), data0=m01, data1=rev_last(e_[:, :n]), initial=0.0,
                                                                                         op0=ALU.mult, op1=ALU.add),
                         reads=[e_.res, mask01.res], writes=[c_.res])
                    last = 0
                EB, ENB, EBL = eb[d][c2], enb[d][c2], ebl[d][c2]
                S.op(S.act, lambda h, n=n, c_=c_, EB=EB: h.activation(out=EB[:, :n], in_=c_[:, :n], func=AF.Exp, scale=-1.0 / 16), reads=[c_.res], writes=[EB.res])
                S.op(S.act, lambda h, n=n, c_=c_, ENB=ENB: h.activation(out=ENB[:, :n], in_=c_[:, :n], func=AF.Exp, scale=1.0 / 16), reads=[c_.res], writes=[ENB.res])
                c3 = c_[:, :n].rearrange("p (a b) -> p a b", b=64)
                S.op(S.pool, lambda h, n=n, c_=c_, c3=c3, last=last, nch=nch: h.tensor_tensor(out=c3, in0=c3, in1=c3[:, :, last:last + 1].to_broadcast([128, nch, 64]), op=ALU.subtract),
                     reads=[c_.res], writes=[c_.res])
                S.op(S.act, lambda h, n=n, c_=c_, EBL=EBL: h.activation(out=EBL[:, :n], in_=c_[:, :n], func=AF.Exp, scale=1.0 / 16), reads=[c_.res], writes=[EBL.res])
                ch0 = u0 // 64
                elc = elcs[nxt("elc", 2)]
                S.op(S.pool, lambda h, n=n, EB=EB, last=last, elc=elc, nch=nch: h.tensor_copy(
                    out=elc[:, :nch], in_=EB[:, :n].rearrange("p (a b) -> p a b", b=64)[:, :, last]),
                     reads=[EB.res], writes=[elc.res])
                S.dma(S.sp, [(self.EL[:, 2 * c2 + hh2, d, ch0:ch0 + nch], elc[hh2 * 64:(hh2 + 1) * 64, :nch]) for hh2 in range(2)],
                      reads=[elc.res], writes=[self.EL.res])

    def a_mm(k, s):
        hT = hTs[k % 2]
        c0 = s * 128
        for kc in range(8):
            S.op(S.pe, lambda h, kc=kc, c0=c0, hT=hT: h.matmul(pq[:], hT[:, kc, c0:c0 + 128], Wa[:, kc, 0:512], start=(kc == 0), stop=(kc == 7)),
                 reads=[hT.res, Wa.res], writes=[pq.res], inc=(kc == 7))
        for kc in range(8):
            S.op(S.pe, lambda h, kc=kc, c0=c0, hT=hT: h.matmul(pkv[:], hT[:, kc, c0:c0 + 128], Wa[:, kc, 512:768], start=(kc == 0), stop=(kc == 7)),
                 reads=[hT.res, Wa.res], writes=[pkv.res], inc=(kc == 7))

    def a_chain(k, s, cst=None):
        si, t0, n = tiles[k]
        A, tmp, src, uoff, rope = st_info[si]
        u0 = uoff + t0
        c0 = s * 128
        S.op(S.act, lambda h: h.activation(out=sqt[:, 0:512], in_=pq[:], func=AF.Square), reads=[pq.res], writes=[sqt.res])
        S.op(S.act, lambda h: h.activation(out=sqt[:, 512:640], in_=pkv[:, 0:128], func=AF.Square), reads=[pkv.res], writes=[sqt.res])
        vi = nxt("vb", 4)
        S.op(S.act, lambda h, vi=vi: h.activation(out=vb[vi][:], in_=pkv[:, 128:256], func=AF.Copy), reads=[pkv.res], writes=[vb[vi].res])
        S.dma(S.sp, [(self.VA[l][u0 + c0:u0 + c0 + 128, :], vb[vi][:])], reads=[vb[vi].res], writes=[self.R(self.VA[l].name)])
        S.op(S.dve, lambda h: h.tensor_reduce(out=ssh[:], in_=sqt[:].rearrange("p (a b) -> p a b", b=64), axis=AX.X, op=ALU.add),
             reads=[sqt.res], writes=[ssh.res])
        S.op(S.act, lambda h: h.activation(out=rinv[:], in_=ssh[:], func=AF.Sqrt, scale=1.0 / 64, bias=self.eps_col[:]),
             reads=[ssh.res, self.eps_col.res], writes=[rinv.res])
        S.op(S.dve, lambda h: h.reciprocal(out=rinv[:], in_=rinv[:]), reads=[rinv.res], writes=[rinv.res])
        S.op(S.dve, lambda h: h.tensor_tensor(out=qn[:, 0:8, :], in0=pq[:].rearrange("p (a b) -> p a b", b=64),
                                               in1=rinv[:, 0:8].unsqueeze(2).to_broadcast([128, 8, 64]), op=ALU.mult),
             reads=[pq.res, rinv.res], writes=[qn.res])
        S.op(S.dve, lambda h: h.tensor_tensor(out=qn[:, 8:10, :], in0=pkv[:, 0:128].rearrange("p (a b) -> p a b", b=64),
                                               in1=rinv[:, 8:10].unsqueeze(2).to_broadcast([128, 2, 64]), op=ALU.mult),
             reads=[pkv.res, rinv.res], writes=[qn.res])
        S.op(S.pool, lambda h: h.tensor_tensor(out=qn[:], in0=qn[:], in1=gain[:], op=ALU.mult),
             reads=[qn.res, gain.res], writes=[qn.res])
        q_ = qr[nxt("qr", 2)]
        if rope:
            cosb = cst[:, 0:1, :].to_broadcast([128, 10, 32])
            sinb = cst[:, 1:2, :].to_broadcast([128, 10, 32])
            x1 = qn[:, :, 0:32]
            x2 = qn[:, :, 32:64]
            r = [rt[nxt("rt", 4)] for _ in range(4)]
            S.op(S.pool, lambda h, r=r, cosb=cosb, x1=x1: h.tensor_tensor(out=r[0][:], in0=x1, in1=cosb, op=ALU.mult),
                 reads=[qn.res, cst.res], writes=[r[0].res])
            S.op(S.dve, lambda h, r=r, sinb=sinb, x2=x2: h.tensor_tensor(out=r[1][:], in0=x2, in1=sinb, op=ALU.mult),
                 reads=[qn.res, cst.res], writes=[r[1].res])
            S.op(S.dve, lambda h, r=r, q_=q_: h.tensor_tensor(out=q_[:, :, 0:32], in0=r[0][:], in1=r[1][:], op=ALU.subtract),
                 reads=[r[0].res, r[1].res], writes=[q_.res])
            S.op(S.pool, lambda h, r=r, sinb=sinb, x1=x1: h.tensor_tensor(out=r[2][:], in0=x1, in1=sinb, op=ALU.mult),
                 reads=[qn.res, cst.res], writes=[r[2].res])
            S.op(S.dve, lambda h, r=r, cosb=cosb, x2=x2: h.tensor_tensor(out=r[3][:], in0=x2, in1=cosb, op=ALU.mult),
                 reads=[qn.res, cst.res], writes=[r[3].res])
            S.op(S.dve, lambda h, r=r, q_=q_: h.tensor_tensor(out=q_[:, :, 32:64], in0=r[2][:], in1=r[3][:], op=ALU.add),
                 reads=[r[2].res, r[3].res], writes=[q_.res])
        else:
            S.op(S.dve, lambda h, q_=q_: h.tensor_copy(out=q_[:], in_=qn[:]), reads=[qn.res], writes=[q_.res])
        return q_

    def a_tr(q_, s):
        c0 = s * 128
        for hh in range(8):
            S.op(S.pe, lambda h, hh=hh, q_=q_: h.transpose(out=pqt[:, hh, :], in_=q_[:, hh, :], identity=self.ident[:]),
                 reads=[q_.res, self.ident.res], writes=[pqt.res], inc=(hh == 7))
        for hh in range(2):
            S.op(S.pe, lambda h, hh=hh, q_=q_: h.transpose(out=pkt[:, hh, :], in_=q_[:, 8 + hh, :], identity=self.ident[:]),
                 reads=[q_.res, self.ident.res], writes=[pkt.res], inc=(hh == 1))
        S.op(S.act, lambda h, c0=c0: h.activation(out=QTs[:, :, c0:c0 + 128], in_=pqt[:], func=AF.Copy), reads=[pqt.res], writes=[QTs.res])
        S.op(S.dve, lambda h, c0=c0: h.tensor_copy(out=KTs[:, :, c0:c0 + 128], in_=pkt[:]), reads=[pkt.res], writes=[KTs.res])

    def b_part(k, s):
        si, t0, n = tiles[k]
        uoff = st_info[si][3]
        u0 = uoff + t0
        hT = hTs[k % 2]
        c0 = s * 128
        for half in range(2):
            kk = nxt("pf", 2)
            for kc in range(8):
                S.op(S.pe, lambda h, kc=kc, c0=c0, kk=kk, half=half, hT=hT: h.matmul(pf[kk][:], hT[:, kc, c0:c0 + 128], Wv[:, kc, half * 512:(half + 1) * 512],
                                                                                  start=(kc == 0), stop=(kc == 7)),
                     reads=[hT.res, Wv.res], writes=[pf[kk].res], inc=(kc == 7))
            vi = nxt("vb2", 4)
            if half == 0:
                S.op(S.act, lambda h, kk=kk, vi=vi: h.activation(out=vb2[vi][:], in_=pf[kk][:], func=AF.Copy), reads=[pf[kk].res], writes=[vb2[vi].res])
            else:
                S.op(S.dve, lambda h, kk=kk, vi=vi: h.tensor_copy(out=vb2[vi][:], in_=pf[kk][:]), reads=[pf[kk].res], writes=[vb2[vi].res])
            dstv = self.GV[l] if half == 0 else self.MV[l]
            S.dma(S.sp, [(dstv[u0 + c0:u0 + c0 + 128, :], vb2[vi][:])], reads=[vb2[vi].res], writes=[self.R(dstv.name)])

    def c_part(k, fcs):
        si, t0, n = tiles[k]
        uoff = st_info[si][3]
        u0 = uoff + t0
        hT = hTs[k % 2]
        for fc in fcs:
            kk = nxt("pf", 2)
            for kc in range(8):
                S.op(S.pe, lambda h, kc=kc, fc=fc, kk=kk, n=n, hT=hT: h.matmul(pf[kk][:, :n], Wf[:, kc, fc * 128:(fc + 1) * 128], hT[:, kc, :n], start=(kc == 0), stop=(kc == 7)),
                     reads=[hT.res, Wf.res], writes=[pf[kk].res], inc=(kc == 7))
            if fc < 2:
                for d in range(2):
                    o_ = fo[nxt("fo", 8)]
                    S.op(S.dve, lambda h, kk=kk, n=n, d=d, fc=fc, o_=o_: h.scalar_tensor_tensor(out=o_[:, :n], in0=pf[kk][:, :n], scalar=0.125, in1=eb[d][fc][:, :n],
                                                                                           op0=ALU.mult, op1=ALU.mult),
                         reads=[pf[kk].res, eb[d][fc].res], writes=[o_.res])
                    S.dma(S.pool, [(self.QG[l][d, :, 2 * fc + hh2, u0:u0 + n], o_[hh2 * 64:(hh2 + 1) * 64, :n]) for hh2 in range(2)], reads=[o_.res], writes=[self.R(self.QG[l].name)])
            elif fc < 4:
                c2 = fc - 2
                for d in range(2):
                    o_ = fo[nxt("fo", 8)]
                    S.op(S.dve, lambda h, kk=kk, n=n, d=d, c2=c2, o_=o_: h.tensor_tensor(out=o_[:, :n], in0=pf[kk][:, :n], in1=enb[d][c2][:, :n], op=ALU.mult),
                         reads=[pf[kk].res, enb[d][c2].res], writes=[o_.res])
                    S.dma(S.pool, [(self.KG[l][d, :, 2 * c2 + hh2, u0:u0 + n], o_[hh2 * 64:(hh2 + 1) * 64, :n]) for hh2 in range(2)], reads=[o_.res], writes=[self.R(self.KG[l].name)])
                    o_ = fo[nxt("fo", 8)]
                    S.op(S.dve, lambda h, kk=kk, n=n, d=d, c2=c2, o_=o_: h.tensor_tensor(out=o_[:, :n], in0=pf[kk][:, :n], in1=ebl[d][c2][:, :n], op=ALU.mult),
                         reads=[pf[kk].res, ebl[d][c2].res], writes=[o_.res])
                    S.dma(S.pool, [(self.KH[l][d, :, 2 * c2 + hh2, u0:u0 + n], o_[hh2 * 64:(hh2 + 1) * 64, :n]) for hh2 in range(2)], reads=[o_.res], writes=[self.R(self.KH[l].name)])
            else:
                o_ = fo[nxt("fo", 8)]
                S.op(S.act, lambda h, kk=kk, n=n, o_=o_: h.activation(out=o_[:, :n], in_=pf[kk][:, :n], func=AF.Copy), reads=[pf[kk].res], writes=[o_.res])
                r0 = (fc - 4) * 128
                S.dma(S.pool, [(self.MQK[l][r0:r0 + 128, 2 + u0:2 + u0 + n], o_[:, :n])], reads=[o_.res], writes=[self.R(self.MQK[l].name)])

    for s in range(tiles[0][2] // 128):
        prep_sub(0, s)
    for k, (si, t0, n) in enumerate(tiles):
        A, tmp, src, uoff, rope_ = st_info[si]
        rope = rope_
        nt = n // 128
        u0 = uoff + t0
        hT = hTs[k % 2]
        S.dma(S.sp, [(H2Tv[:, :, u0:u0 + n], hT[:, :, :n])], reads=[hT.res], writes=[self.R(self.H2T[l].name)])
        g_part(k)
        order = [4, 5, 6, 7, 8, 9, 10, 11, 0, 1, 2, 3]
        per = (12 + nt - 1) // nt
        pend = None
        nxt_nt = tiles[k + 1][2] // 128 if k + 1 < len(tiles) else 0
        for s in range(nt):
            xi = prep_load(k + 1, s) if s < nxt_nt else None
            cst = rope_load(k, s)
            a_mm(k, s)
            q_ = a_chain(k, s, cst)
            if pend is not None:
                a_tr(*pend)
            pend = (q_, s)
            b_part(k, s)
            c_part(k, order[s * per:(s + 1) * per])
            if s < nxt_nt:
                prep_sub(k + 1, s, xi)
        a_tr(*pend)
        for s in range(nt, nxt_nt):
            prep_sub(k + 1, s)
        S.dma(S.sp, [(self.QT[l][:, :, u0:u0 + n], QTs[:, :, :n])], reads=[QTs.res], writes=[self.R(self.QT[l].name)])
        S.dma(S.sp, [(self.KT[l][:, :, u0:u0 + n], KTs[:, :, :n])], reads=[KTs.res], writes=[self.R(self.KT[l].name)])


Builder.phase_feat = phase_feat


def phase_attn(self, l, do_ctx):
    S = self.S
    T = self.T
    nbk = T // 128
    ones = self.sb("ones", [128, 128], F32)
    S.op(S.pool, lambda h: h.memset(ones[:], 1.0), writes=[ones.res])
    mP = self.sb("mP", [128, 4, 128], BF16)
    mN = self.sb("mN", [128, 4, 128], BF16)
    mtmp = self.sb("mtmp", [128, 128], F32)
    zer = self.sb("zer", [128, 128], F32)
    S.op(S.pool, lambda h: h.memset(zer[:], 0.0), writes=[zer.res])
    for (m_, sgn) in ((mP, 1), (mN, -1)):
        S.op(S.pool, lambda h, sgn=sgn: h.affine_select(out=mtmp[:], in_=zer[:], pattern=[[-sgn, 128]], compare_op=ALU.is_ge, fill=-30000.0,
                                                         base=0, channel_multiplier=sgn), reads=[zer.res], writes=[mtmp.res])
        S.op(S.pool, lambda h, m_=m_: h.tensor_copy(out=m_[:], in_=mtmp[:].unsqueeze(1).to_broadcast([128, 4, 128])), reads=[mtmp.res], writes=[m_.res])
    esk = self.sb("esk", [128, 2, 4, 128], F32)
    sk8 = self.sb("sk8", [128, 8], F32)
    S.dma(S.sp, [(sk8[64:65, :], self.attn_sink[l:l + 1, :])], writes=[sk8.res])
    S.op(S.act, lambda h: h.activation(out=sk8[64:65, :], in_=sk8[64:65, :], func=AF.Exp), reads=[sk8.res], writes=[sk8.res])
    S.op(S.dve, lambda h: h.tensor_copy(out=esk[64:65].rearrange("p g a b -> p (g a) b"), in_=sk8[64:65, :].unsqueeze(2).to_broadcast([1, 8, 128])),
         reads=[sk8.res], writes=[esk.res])
    KTc = self.sb("KTc", [64, 2, 256], BF16)
    S.dma(S.sp, [(KTc[:], self.KT[l][:, :, 0:256])], reads=[self.R(self.KT[l].name)], writes=[KTc.res])
    Vc = [self.sb(f"Vc{j}", [128, 2, 65], BF16) for j in range(2)]
    Vb = [self.sb(f"Vb{j}", [128, 2, 65], BF16) for j in range(4)]
    KTb = [self.sb(f"KTb{j}", [64, 2, 128], BF16) for j in range(4)]
    for v in Vc + Vb:
        S.op(S.pool, lambda h, v=v: h.memset(v[:], 1.0), writes=[v.res])
    for j in range(2):
        S.dma(S.sp, [(Vc[j][:, :, 0:64], self.VA[l][j * 128:(j + 1) * 128, :].rearrange("p (g d) -> p g d", d=64))],
              reads=[self.R(self.VA[l].name)], writes=[Vc[j].res])
    QTb = [self.sb(f"QTb{j}", [64, 8, 128], BF16) for j in range(2)]
    E = [self.sb(f"E{j}", [128, 4, 128], BF16) for j in range(4)]
    dn = [self.sb(f"dn{j}", [128, 512], F32) for j in range(2)]
    bcs = [self.sb(f"bcs{j}", [64, 512], F32) for j in range(2)]
    aT = [self.sb(f"aT{j}", [64, 4, 128], BF16) for j in range(2)]
    pS = [self.ps(f"pS{j}", [128, 512]) for j in range(3)]
    pO = [self.ps(f"pO{j}", [128, 512]) for j in range(2)]
    pB = [self.ps(f"pB{j}", [64, 512]) for j in range(2)]
    cnt = {}

    def nxt(key, n):
        v = cnt.get(key, 0)
        cnt[key] = v + 1
        return v % n

    def load_kb(m):
        i = m % 4
        u = LC + m * 128
        S.dma(S.sp, [(KTb[i][:], self.KT[l][:, :, u:u + 128])], reads=[self.R(self.KT[l].name)], writes=[KTb[i].res])
        S.dma(S.sp, [(Vb[i][:, :, 0:64], self.VA[l][u:u + 128, :].rearrange("p (g d) -> p g d", d=64))],
              reads=[self.R(self.VA[l].name)], writes=[Vb[i].res])

    pending = []

    def norm(g, po, u0):
        d_ = dn[nxt("dn", 2)]
        S.op(S.dve, lambda h, d_=d_, po=po, g=g: h.tensor_tensor(out=d_[64:65, :], in0=po[64:65, :], in1=esk[64:65, g].rearrange("p a b -> p (a b)"), op=ALU.add),
             reads=[po.res, esk.res], writes=[d_.res])
        S.op(S.dve, lambda h, d_=d_: h.reciprocal(out=d_[64:65, :], in_=d_[64:65, :]), reads=[d_.res], writes=[d_.res])
        pb = pB[nxt("pb", 2)]
        S.op(S.pe, lambda h, d_=d_, pb=pb: h.matmul(pb[:], ones[64:65, 0:64], d_[64:65, :], start=True, stop=True),
             reads=[d_.res, ones.res], writes=[pb.res])
        b_ = bcs[nxt("bcs", 2)]
        S.op(S.act, lambda h, b_=b_, pb=pb: h.activation(out=b_[:], in_=pb[:], func=AF.Copy), reads=[pb.res], writes=[b_.res])
        a_ = aT[nxt("aT", 2)]
        S.op(S.dve, lambda h, a_=a_, b_=b_, po=po: h.tensor_tensor(out=a_[:].rearrange("p a b -> p (a b)"), in0=po[0:64, :], in1=b_[:], op=ALU.mult),
             reads=[po.res, b_.res], writes=[a_.res])
        S.dma(S.pool, [(self.ATT[l][:, 4 * g:4 * g + 4, u0:u0 + 128], a_[:])], reads=[a_.res], writes=[self.R(self.ATT[l].name)])

    def qblock(u0, kbs):
        qi = nxt("q", 2)
        Q = QTb[qi]
        S.dma(S.sp, [(Q[:], self.QT[l][:, :, u0:u0 + 128])], reads=[self.R(self.QT[l].name)], writes=[Q.res])
        for g in range(2):
            po = pO[nxt("po", 2)]
            rhsq = Q[:, 4 * g:4 * g + 4, :].rearrange("p a b -> p (a b)")

            def score(idx, g=g, rhsq=rhsq):
                kt, vt, msk = kbs[idx]
                p = pS[nxt("ps", 3)]
                S.op(S.pe, lambda h, kt=kt, p=p, g=g, rhsq=rhsq, msk=msk: h.matmul(p[:], kt[0][:, g, kt[1]:kt[1] + 128], rhsq, start=True, stop=(msk is None)),
                     reads=[kt[0].res, Q.res], writes=[p.res], inc=(msk is None))
                if msk is not None:
                    S.op(S.pe, lambda h, p=p, msk=msk: h.matmul(p[:], self.ident[:], msk[:].rearrange("p a b -> p (a b)"), start=False, stop=True),
                         reads=[self.ident.res, msk.res], writes=[p.res])
                return p
            ps_list = [score(0)]
            for idx in range(len(kbs)):
                kt, vt, msk = kbs[idx]
                if idx + 1 < len(kbs):
                    ps_list.append(score(idx + 1))
                p = ps_list[idx]
                e = E[nxt("e", 4)]
                S.op(S.act, lambda h, p=p, e=e: h.activation(out=e[:].rearrange("p a b -> p (a b)"), in_=p[:], func=AF.Exp, scale=0.125),
                     reads=[p.res], writes=[e.res])
                S.op(S.pe, lambda h, e=e, vt=vt, po=po, idx=idx, g=g, kbs=kbs: h.matmul(po[0:65, :], vt[:, g, :], e[:].rearrange("p a b -> p (a b)"),
                                                                                      start=(idx == 0), stop=(idx == len(kbs) - 1)),
                     reads=[e.res, vt.res], writes=[po.res], inc=(idx == len(kbs) - 1))
            pending.append((g, po, u0))
            if len(pending) > 1:
                norm(*pending.pop(0))

    ckb = [((KTc, 0), Vc[0], None), ((KTc, 128), Vc[1], None)]
    if do_ctx:
        for n in range(2):
            qblock(n * 128, ckb)
    load_kb(0)
    for n in range(nbk):
        if n + 1 < nbk:
            load_kb(n + 1)
        kbs = []
        if n - 1 >= 0:
            kbs.append(((KTb[(n - 1) % 4], 0), Vb[(n - 1) % 4], mP))
        kbs.append(((KTb[n % 4], 0), Vb[n % 4], None))
        if n + 1 < nbk:
            kbs.append(((KTb[(n + 1) % 4], 0), Vb[(n + 1) % 4], mN))
        qblock(LC + n * 128, kbs + ckb)
    while pending:
        norm(*pending.pop(0))


Builder.phase_attn = phase_attn


def scan_groups(T):
    return [(0, LC)] + [(LC + t0, min(512, T - t0)) for t0 in range(0, T, 512)]


def scan_order(T, d):
    groups = scan_groups(T)
    order = []
    if d == 0:
        for gi, (u0, n) in enumerate(groups):
            for c in range(n // 64):
                order.append((gi, c))
    else:
        gis = [0] + list(range(len(groups) - 1, 0, -1))
        for gi in gis:
            u0, n = groups[gi]
            for c in range(n // 64 - 1, -1, -1):
                order.append((gi, c))
    return order


def phase_gla(self, l):
    S = self.S
    T = self.T
    EL = self.EL
    groups = scan_groups(T)
    ones = self.sb("ones", [64, 64], F32)
    S.op(S.pool, lambda h: h.memset(ones[:], 1.0), writes=[ones.res])
    mtmp = self.sb("mtmp", [64, 64], F32)
    msk = [self.sb(f"msk{d}", [64, 64], BF16) for d in range(2)]
    for d, sgn in ((0, -1), (1, 1)):
        S.op(S.pool, lambda h, sgn=sgn: h.affine_select(out=mtmp[:], in_=ones[:], pattern=[[-sgn, 64]], compare_op=ALU.is_ge, fill=0.0,
                                                         base=0, channel_multiplier=sgn), reads=[ones.res], writes=[mtmp.res])
        S.op(S.pool, lambda h, d=d: h.tensor_copy(out=msk[d][:], in_=mtmp[:]), reads=[mtmp.res], writes=[msk[d].res])
    Sf = [self.sb(f"Sf{d}", [64, 4, 128], F32) for d in range(2)]
    Sb = [self.sb(f"Sb{d}", [64, 4, 128], BF16) for d in range(2)]
    for d in range(2):
        S.op(S.pool, lambda h, d=d: h.memset(Sf[d][:], 0.0), writes=[Sf[d].res])
        S.op(S.pool, lambda h, d=d: h.memset(Sb[d][:], 0.0), writes=[Sb[d].res])
    qg = [[self.sb(f"qg{d}{i}", [64, 4, 512], BF16) for i in range(2)] for d in range(2)]
    kg = [[self.sb(f"kg{d}{i}", [64, 4, 512], BF16) for i in range(2)] for d in range(2)]
    kh = [[self.sb(f"kh{d}{i}", [64, 4, 512], BF16) for i in range(2)] for d in range(2)]
    vg = [[self.sb(f"vg{d}{i}", [64, 8, 512], BF16) for i in range(2)] for d in range(2)]
    am = [[self.sb(f"am{d}{i}", [64, 4, 64], BF16) for i in range(2)] for d in range(2)]
    kt = [[self.sb(f"kt{d}{i}", [64, 4, 64], BF16) for i in range(2)] for d in range(2)]
    ob = [[self.sb(f"ob{d}{i}", [64, 512], F32) for i in range(2)] for d in range(2)]
    pA = [self.ps(f"pA{d}", [64, 256]) for d in range(2)]
    pK = [self.ps(f"pK{d}", [64, 256], BF16) for d in range(2)]
    pO = [self.ps(f"pO{d}", [64, 512]) for d in range(2)]
    pN = [self.ps(f"pN{d}", [64, 512]) for d in range(2)]
    orders = [scan_order(T, d) for d in range(2)]
    nsteps = len(orders[0])
    gcount = [0, 0]
    cur = [None, None]

    def load_group(d, gi):
        i = gcount[d] % 2
        gcount[d] += 1
        u0, n = groups[gi]
        nch = n // 64
        for (dst, srcT) in ((qg[d][i], self.QG[l]), (kg[d][i], self.KG[l]), (kh[d][i], self.KH[l])):
            S.dma(S.sp, [(dst[:, :, :n], srcT[d, :, :, u0:u0 + n])], reads=[self.R(srcT.name)], writes=[dst.res])
        S.dma(S.sp, [(vg[d][i][:, :nch, :], self.GV[l][u0:u0 + n, :].rearrange("(c p) f -> p c f", p=64))], reads=[self.R(self.GV[l].name)],
              writes=[vg[d][i].res])
        return i

    ctxs = {}

    def stageA(step, d):
        gi, c = orders[d][step]
        if cur[d] is None or cur[d][0] != gi:
            cur[d] = (gi, load_group(d, gi))
        bi = cur[d][1]
        u0, n = groups[gi]
        o = c * 64
        Q, Kg, Kh, V = qg[d][bi], kg[d][bi], kh[d][bi], vg[d][bi]
        k2 = step % 2
        AM, KTt = am[d][k2], kt[d][k2]
        ctxs[(step, d)] = (Q, V, AM, KTt, u0, o, c)
        for hh in range(4):
            S.op(S.pe, lambda h, hh=hh, d=d, Kg=Kg, Q=Q, o=o: h.matmul(pA[d][:, hh * 64:(hh + 1) * 64], Kg[:, hh, o:o + 64], Q[:, hh, o:o + 64], start=True, stop=True),
                 reads=[Kg.res, Q.res], writes=[pA[d].res], inc=(hh == 3))
        for hh in range(4):
            S.op(S.pe, lambda h, hh=hh, d=d, Kh=Kh, o=o: h.transpose(out=pK[d][:, hh * 64:(hh + 1) * 64], in_=Kh[:, hh, o:o + 64], identity=self.ident[0:64, 0:64]),
                 reads=[Kh.res, self.ident.res], writes=[pK[d].res], inc=(hh == 3))
        S.op(S.dve, lambda h, d=d, AM=AM: h.tensor_tensor(out=AM[:], in0=pA[d][:].rearrange("p (a b) -> p a b", b=64),
                                                          in1=msk[d][:].unsqueeze(1).to_broadcast([64, 4, 64]), op=ALU.mult),
             reads=[pA[d].res, msk[d].res], writes=[AM.res])
        S.op(S.act, lambda h, d=d, KTt=KTt: h.activation(out=KTt[:].rearrange("p a b -> p (a b)"), in_=pK[d][:], func=AF.Copy), reads=[pK[d].res], writes=[KTt.res])

    def stageB(step, d):
        Q, V, AM, KTt, u0, o, c = ctxs.pop((step, d))
        chunk = (u0 + o) // 64
        OB = ob[d][step % 2]
        for hh in range(4):
            S.op(S.pe, lambda h, hh=hh, d=d, KTt=KTt, V=V, c=c: h.matmul(pN[d][:, hh * 128:(hh + 1) * 128], KTt[:, hh, :], V[:, c, hh * 128:(hh + 1) * 128], start=True, stop=True),
                 reads=[KTt.res, V.res], writes=[pN[d].res], inc=(hh == 3))
        for hh in range(4):
            S.op(S.pe, lambda h, hh=hh, d=d, AM=AM, V=V, c=c: h.matmul(pO[d][:, hh * 128:(hh + 1) * 128], AM[:, hh, :], V[:, c, hh * 128:(hh + 1) * 128], start=True, stop=False),
                 reads=[AM.res, V.res], writes=[pO[d].res], inc=False)
            S.op(S.pe, lambda h, hh=hh, d=d, Q=Q, o=o: h.matmul(pO[d][:, hh * 128:(hh + 1) * 128], Q[:, hh, o:o + 64], Sb[d][:, hh, :], start=False, stop=True),
                 reads=[Q.res, Sb[d].res], writes=[pO[d].res], inc=(hh == 3))
        S.op(S.act, lambda h, d=d, OB=OB: h.activation(out=OB[:], in_=pO[d][:], func=AF.Copy), reads=[pO[d].res], writes=[OB.res])
        S.dma(S.sp, [(self.OG[l][d, u0 + o:u0 + o + 64, :], OB[:])], reads=[OB.res], writes=[self.R(self.OG[l].name)])
        S.op(S.dve, lambda h, d=d, chunk=chunk: h.tensor_tensor(out=Sf[d][:], in0=Sf[d][:], in1=EL[:, :, d, chunk:chunk + 1].to_broadcast([64, 4, 128]), op=ALU.mult),
             reads=[Sf[d].res, self.EL.res], writes=[Sf[d].res])
        S.op(S.dve, lambda h, d=d: h.tensor_tensor(out=Sf[d][:].rearrange("p a b -> p (a b)"), in0=Sf[d][:].rearrange("p a b -> p (a b)"), in1=pN[d][:], op=ALU.add),
             reads=[Sf[d].res, pN[d].res], writes=[Sf[d].res])
        S.op(S.act, lambda h, d=d: h.activation(out=Sb[d][:], in_=Sf[d][:], func=AF.Copy), reads=[Sf[d].res], writes=[Sb[d].res])

    for d in range(2):
        stageA(0, d)
    for step in range(nsteps):
        if step + 1 < nsteps:
            for d in range(2):
                stageA(step + 1, d)
        for d in range(2):
            stageB(step, d)


Builder.phase_gla = phase_gla


LN_KS = float(-0.5 * np.log(128.0))


def phase_ml_gates(self, l):
    S = self.S
    sel = self.sel
    DEC = self.DEC
    T = self.T
    TT = self.TT
    nch = TT // 64
    bA = self.sb("bA", [4, TT], F32)
    bL = self.sb("bL", [4, TT], F32)
    bC = self.sb("bC", [4, TT], F32)
    bG = self.sb("bG", [4, TT], F32)
    bX = self.sb("bX", [4, TT], F32)
    onesr = self.sb("onesr", [4, TT], BF16)
    S.op(S.pool, lambda h: h.memset(onesr[:], 1.0), writes=[onesr.res])
    gl = self.sb("gl", [4, nch], F32)
    gp = self.sb("gp", [4, nch], F32)
    dd = self.sb("dd", [4, nch], F32)
    ibc = self.sb("ibc", [4, 2], F32)
    pD = self.ps("pD", [128, 512])
    for d in range(2):
        S.dma(S.sp, [(bA[:], self.GATES[l][d * 4:(d + 1) * 4, :])], reads=[self.R(self.GATES[l].name)], writes=[bA.res])
        S.dma(S.sp, [(bL[:], self.GATES[l][8 + d * 4:8 + (d + 1) * 4, :])], reads=[self.R(self.GATES[l].name)], writes=[bL.res])
        S.dma(S.sp, [(ibc[:, 0:1], self.mlstm_ib[l, d, :].rearrange("(h o) -> h o", o=1)), (ibc[:, 1:2], self.mlstm_fb[l, d, :].rearrange("(h o) -> h o", o=1))],
              writes=[ibc.res])
        S.op(S.dve, lambda h: h.tensor_scalar(out=ibc[:, 1:2], in0=ibc[:, 1:2], scalar1=-1.0, scalar2=None, op0=ALU.mult), reads=[ibc.res], writes=[ibc.res])
        S.op(S.act, lambda h: h.activation(out=bL[:], in_=bL[:], func=AF.Exp, scale=-1.0, bias=ibc[:, 1:2]), reads=[bL.res, ibc.res], writes=[bL.res])
        S.op(S.act, lambda h: h.activation(out=bL[:], in_=bL[:], func=AF.Ln, bias=1.0), reads=[bL.res], writes=[bL.res])

        def scan(out, src, op1):
            if d == 0:
                S.op(S.dve, lambda h: h.tensor_tensor_scan(out=out[:], data0=onesr[:], data1=src[:], initial=0.0, op0=ALU.mult, op1=op1),
                     reads=[src.res, onesr.res], writes=[out.res])
            else:
                S.op(S.dve, lambda h: h.tensor_tensor_scan(out=rev_last(out[:, 0:LC]), data0=onesr[:, 0:LC], data1=rev_last(src[:, 0:LC]), initial=0.0, op0=ALU.mult, op1=op1),
                     reads=[src.res, onesr.res], writes=[out.res])
                S.op(S.dve, lambda h: h.tensor_tensor_scan(out=rev_last(out[:, LC:TT]), data0=onesr[:, LC:TT], data1=rev_last(src[:, LC:TT]), initial=out[:, 0:1], op0=ALU.mult, op1=op1),
                     reads=[src.res, onesr.res, out.res], writes=[out.res])
        scan(bC, bL, ALU.add)
        S.op(S.dve, lambda h: h.scalar_tensor_tensor(out=bA[:], in0=bA[:], scalar=ibc[:, 0:1], in1=bC[:], op0=ALU.add, op1=ALU.add),
             reads=[bA.res, ibc.res, bC.res], writes=[bA.res])
        scan(bG, bA, ALU.max)
        G3 = bG[:].rearrange("p (c b) -> p c b", b=64)
        lastpos = 63 if d == 0 else 0
        S.op(S.dve, lambda h, lastpos=lastpos, G3=G3: h.tensor_copy(out=gl[:], in_=G3[:, :, lastpos]), reads=[bG.res], writes=[gl.res])
        S.op(S.dve, lambda h: h.memset(gp[:], 0.0), writes=[gp.res])
        if d == 0:
            S.op(S.dve, lambda h: h.tensor_copy(out=gp[:, 1:nch], in_=gl[:, 0:nch - 1]), reads=[gl.res], writes=[gp.res])
        else:
            S.op(S.dve, lambda h: h.tensor_copy(out=gp[:, 0:3], in_=gl[:, 1:4]), reads=[gl.res], writes=[gp.res])
            S.op(S.dve, lambda h: h.tensor_copy(out=gp[:, 4:nch - 1], in_=gl[:, 5:nch]), reads=[gl.res], writes=[gp.res])
            S.op(S.dve, lambda h: h.tensor_copy(out=gp[:, nch - 1:nch], in_=gl[:, 0:1]), reads=[gl.res], writes=[gp.res])
        L3 = bL[:].rearrange("p (c b) -> p c b", b=64)
        X3 = bX[:].rearrange("p (c b) -> p c b", b=64)
        A3 = bA[:].rearrange("p (c b) -> p c b", b=64)
        S.op(S.dve, lambda h, L3=L3, G3=G3: h.tensor_tensor(out=L3, in0=gp[:].unsqueeze(2).to_broadcast([4, nch, 64]), in1=G3, op=ALU.subtract),
             reads=[gp.res, bG.res], writes=[bL.res])
        S.op(S.act, lambda h: h.activation(out=bL[:], in_=bL[:], func=AF.Exp), reads=[bL.res], writes=[bL.res])
        S.op(S.dve, lambda h: h.tensor_tensor(out=bC[:], in0=bC[:], in1=bG[:], op=ALU.subtract), reads=[bC.res, bG.res], writes=[bC.res])
        S.op(S.act, lambda h: h.activation(out=bC[:], in_=bC[:], func=AF.Exp), reads=[bC.res], writes=[bC.res])
        S.op(S.dve, lambda h, X3=X3, A3=A3: h.tensor_tensor(out=X3, in0=A3, in1=gl[:].unsqueeze(2).to_broadcast([4, nch, 64]), op=ALU.subtract),
             reads=[bA.res, gl.res], writes=[bX.res])
        S.op(S.dve, lambda h: h.tensor_scalar(out=bX[:], in0=bX[:], scalar1=LN_KS, scalar2=None, op0=ALU.add), reads=[bX.res], writes=[bX.res])
        S.op(S.act, lambda h: h.activation(out=bX[:], in_=bX[:], func=AF.Exp), reads=[bX.res], writes=[bX.res])
        S.op(S.dve, lambda h: h.tensor_tensor(out=dd[:], in0=gp[:], in1=gl[:], op=ALU.subtract), reads=[gp.res, gl.res], writes=[dd.res])
        S.op(S.act, lambda h: h.activation(out=dd[:], in_=dd[:], func=AF.Exp), reads=[dd.res], writes=[dd.res])
        for hh in range(4):
            S.op(S.pe, lambda h, hh=hh: h.matmul(pD[:, :nch], sel[:, hh, :], dd[:], start=True, stop=True), reads=[self.sel.res, dd.res], writes=[pD.res])
            S.op(S.dve, lambda h, hh=hh, d=d: h.tensor_copy(out=DEC[:, d, hh, :], in_=pD[:, :nch]), reads=[pD.res], writes=[self.DEC.res])
        for qi, buf in enumerate((bA, bG, bL, bC, bX)):
            S.dma(S.sp, [(self.MROWS[l][d, qi, :, :], buf[:])], reads=[buf.res], writes=[self.R(self.MROWS[l].name)])


def phase_ml_conv(self, l):
    S = self.S
    T = self.T
    TT = self.TT
    wcol = self.sb("wcol", [128, 8, 5], F32)
    cb = self.sb("cb", [128, 8], F32)
    S.dma(S.sp, [(wcol[:, :, k], self.conv_w[l, k, :].rearrange("(fc p) -> p fc", p=128)) for k in range(5)], writes=[wcol.res], allow_slow_non_contiguous=True)
    S.dma(S.sp, [(cb[:], self.conv_b[l, :].rearrange("(fc p) -> p fc", p=128))], writes=[cb.res], allow_slow_non_contiguous=True)
    diagw = self.sb("diagw", [128, 8, 5, 128], BF16)
    for fc in range(8):
        for k in range(5):
            e = S.dve if (fc * 5 + k) % 2 == 0 else S.pool
            S.op(e, lambda h, fc=fc, k=k: h.tensor_scalar(out=diagw[:, fc, k, :], in0=self.identf[:], scalar1=wcol[:, fc, k:k + 1], scalar2=None, op0=ALU.mult),
                 reads=[self.identf.res, wcol.res], writes=[diagw.res])
    xq = [self.sb(f"xq{i}", [128, 8, 516], BF16) for i in range(2)]
    oc = [self.sb(f"oc{i}", [128, 512], BF16) for i in range(3)]
    pc = [self.ps(f"pc{i}", [128, 512]) for i in range(2)]
    MQKv = self.MQK[l].rearrange("(c p) t -> p c t", p=128)
    k2 = 0
    for gi, (u0, n) in enumerate(scan_groups(T)):
        X = xq[gi % 2]
        S.dma(S.sp, [(X[:, 0:4, 0:n + 4], MQKv[:, 0:4, u0:u0 + n + 4]), (X[:, 4:8, 0:n + 4], MQKv[:, 4:8, u0:u0 + n + 4])], reads=[self.R(self.MQK[l].name)], writes=[X.res])
        if u0 == 0 or u0 == LC:
            S.op(S.pool, lambda h, X=X: h.memset(X[:, :, 0:2], 0.0), writes=[X.res])
        if u0 + n == LC or u0 + n == TT:
            S.op(S.pool, lambda h, X=X, n=n: h.memset(X[:, :, n + 2:n + 4], 0.0), writes=[X.res])
        for fc in range(8):
            p = pc[k2 % 2]
            o_ = oc[k2 % 3]
            k2 += 1
            for k in range(5):
                S.op(S.pe, lambda h, fc=fc, k=k, p=p, X=X, n=n: h.matmul(p[:, :n], diagw[:, fc, k, :], X[:, fc, k:k + n], start=(k == 0), stop=(k == 4)),
                     reads=[diagw.res, X.res], writes=[p.res], inc=(k == 4))
            S.op(S.act, lambda h, fc=fc, p=p, o_=o_, n=n: h.activation(out=o_[:, :n], in_=p[:, :n], func=AF.Silu, bias=cb[:, fc:fc + 1]), reads=[p.res, cb.res], writes=[o_.res])
            S.dma(S.sp, [(self.MQC[l][fc * 128:(fc + 1) * 128, u0:u0 + n], o_[:, :n])], reads=[o_.res], writes=[self.R(self.MQC[l].name)])


def phase_ml_scan(self, l):
    S = self.S
    T = self.T
    sel = self.sel
    DEC = self.DEC
    groups = scan_groups(T)
    cfill = self.sb("cfill", [64, 64], F32)
    S.op(S.pool, lambda h: h.memset(cfill[:], LN_KS), writes=[cfill.res])
    mb = [self.sb(f"mb{d}", [64, 64], F32) for d in range(2)]
    for d, sgn in ((0, -1), (1, 1)):
        S.op(S.pool, lambda h, sgn=sgn, d=d: h.affine_select(out=mb[d][:], in_=cfill[:], pattern=[[-sgn, 64]], compare_op=ALU.is_ge, fill=-30000.0,
                                                              base=0, channel_multiplier=sgn), reads=[cfill.res], writes=[mb[d].res])
    negones = self.sb("negones", [4, 128], F32)
    S.op(S.pool, lambda h: h.memset(negones[:], -1.0), writes=[negones.res])
    posones = self.sb("posones", [4, 128], F32)
    S.op(S.pool, lambda h: h.memset(posones[:], 1.0), writes=[posones.res])
    mbr = [self.sb(f"mbr{d}", [64, 4, 64], F32) for d in range(2)]
    for d in range(2):
        S.op(S.pool, lambda h, d=d: h.tensor_copy(out=mbr[d][:], in_=mb[d][:].unsqueeze(1).to_broadcast([64, 4, 64])), reads=[mb[d].res], writes=[mbr[d].res])
    Dg = [self.sb(f"Dg{i}", [4, 3, 4, 512], F32) for i in range(2)]
    DgH = [self.sb(f"DgH{i}", [4, 2, 4, 512], BF16) for i in range(2)]
    DgL = [self.sb(f"DgL{i}", [4, 2, 4, 512], BF16) for i in range(2)]
    WB = [self.sb(f"WB{i}", [128, 2, 4, 512], F32) for i in range(2)]
    posb = self.sb("posb", [4, 128], BF16)
    S.op(S.pool, lambda h: h.memset(posb[:], 1.0), writes=[posb.res])
    Cf = self.sb("Cf", [128, 4, 129], F32)
    Cb = self.sb("Cb", [128, 4, 129], BF16)
    qk = [self.sb(f"qk{i}", [128, 8, 512], BF16) for i in range(2)]
    vg = [self.sb(f"vgm{i}", [64, 8, 4, 129], BF16) for i in range(2)]
    rows = [self.sb(f"rows{i}", [4, 5, 512], F32) for i in range(2)]
    for v in vg:
        S.op(S.pool, lambda h, v=v: h.memset(v[:], 1.0), writes=[v.res])
    wT = [self.sb(f"wT{i}", [64, 256], F32) for i in range(3)]
    sT = [self.sb(f"sT{i}", [64, 4, 64], BF16) for i in range(3)]
    qks = [self.sb(f"qks{i}", [128, 8, 64], BF16) for i in range(3)]
    khat = [self.sb(f"khat{i}", [64, 4, 128], BF16) for i in range(3)]
    enm = [self.sb(f"enm{i}", [64, 4], F32) for i in range(3)]
    rr = [self.sb(f"rr{i}", [64, 4], F32) for i in range(3)]
    ho = [self.sb(f"ho{i}", [64, 4, 128], F32) for i in range(3)]
    pWS = self.ps("pWS", [64, 512])
    pB = self.ps("pB", [128, 8, 64])
    pK = self.ps("pK", [64, 4, 128], BF16)
    pO = self.ps("pO", [64, 1024])
    pN = self.ps("pN", [128, 1024])
    pO3 = pO[:].rearrange("p (h e) -> p h e", e=256)
    pN3 = pN[:].rearrange("p (h e) -> p h e", e=256)
    gcount = [0]

    def load_group(d, gi):
        i = gcount[0] % 2
        gcount[0] += 1
        u0, n = groups[gi]
        nchg = n // 64
        S.dma(S.sp, [(qk[i][:, 0:4, :n], self.MQC[l].rearrange("(c p) t -> p c t", p=128)[:, 0:4, u0:u0 + n]),
                     (qk[i][:, 4:8, :n], self.MQC[l].rearrange("(c p) t -> p c t", p=128)[:, 4:8, u0:u0 + n])], reads=[self.R(self.MQC[l].name)], writes=[qk[i].res])
        S.dma(S.sp, [(vg[i][:, c, :, 0:128], self.MV[l][u0 + c * 64:u0 + (c + 1) * 64, :].rearrange("p (h e) -> p h e", e=128)) for c in range(nchg)],
              reads=[self.R(self.MV[l].name)], writes=[vg[i].res])
        S.dma(S.sp, [(rows[i][:, :, :n], self.MROWS[l][d, :, :, u0:u0 + n].rearrange("q h t -> h q t"))], reads=[self.R(self.MROWS[l].name)], writes=[rows[i].res])
        for qi, qs in enumerate((2, 4)):
            srcr = self.MROWS[l][d, qs, :, u0:u0 + n]
            S.dma(S.sp, [(WB[i][:, qi, :, :n], bass.AP(srcr.tensor, srcr.offset, [[0, 128]] + [list(x) for x in srcr.ap]))],
                  reads=[self.R(self.MROWS[l].name)], writes=[WB[i].res])
        for qd, qs in enumerate((1,)):
            S.op(S.pool, lambda h, i=i, qd=qd, qs=qs, n=n: h.tensor_tensor(out=Dg[i][:, qd, :, :n], in0=rows[i][:, qs:qs + 1, :n].to_broadcast([4, 4, n]),
                                                                       in1=self.identf[0:4, 0:4].unsqueeze(2).to_broadcast([4, 4, n]), op=ALU.mult),
                 reads=[rows[i].res, self.identf.res], writes=[Dg[i].res])
        return i

    for d in range(2):
        S.op(S.pool, lambda h: h.memset(Cf[:], 0.0), writes=[Cf.res])
        S.op(S.pool, lambda h: h.memset(Cb[:], 0.0), writes=[Cb.res])
        order = scan_order(T, d)
        cur = None
        info = []
        for (gi, c) in order:
            if cur is None or cur[0] != gi:
                cur = (gi, None)
            info.append((gi, c))
        bufof = {}

        def stageA(step):
            gi, c = order[step]
            if gi not in bufof:
                bufof.clear()
                bufof[gi] = load_group(d, gi)
            bi = bufof[gi]
            o = c * 64
            k2 = step % 3
            QK, R_ = qk[bi], rows[bi]
            DG = Dg[bi]
            S.op(S.pe, lambda h, R_=R_, o=o: h.matmul(pWS[:, 0:256], R_[:, 0, o:o + 64], sel[:, :, 0:64], start=True, stop=False),
                 reads=[R_.res, sel.res], writes=[pWS.res], inc=False)
            S.op(S.pe, lambda h, DG=DG, o=o: h.matmul(pWS[:, 0:256], negones[:, 0:64], DG[:, 0, :, o:o + 64], start=False, stop=False),
                 reads=[DG.res, negones.res], writes=[pWS.res], inc=False)
            S.op(S.pe, lambda h, d=d: h.matmul(pWS[:, 0:256], self.identf[0:64, 0:64], mbr[d][:], start=False, stop=True),
                 reads=[self.identf.res, mbr[d].res], writes=[pWS.res], inc=False)
            for hh in range(4):
                S.op(S.pe, lambda h, hh=hh, QK=QK, o=o: h.matmul(pWS[:, 256 + hh * 64:256 + (hh + 1) * 64], QK[:, 4 + hh, o:o + 64], QK[:, hh, o:o + 64], start=True, stop=True),
                     reads=[QK.res], writes=[pWS.res], inc=(hh == 3))
            S.op(S.act, lambda h, k2=k2: h.activation(out=wT[k2][:], in_=pWS[:, 0:256], func=AF.Exp), reads=[pWS.res], writes=[wT[k2].res])
            S.op(S.dve, lambda h, k2=k2: h.tensor_tensor(out=sT[k2][:].rearrange("p a b -> p (a b)"), in0=pWS[:, 256:512], in1=wT[k2][:], op=ALU.mult),
                 reads=[pWS.res, wT[k2].res], writes=[sT[k2].res])
            S.op(S.dve, lambda h, k2=k2, QK=QK, o=o, bi=bi: h.tensor_tensor(out=qks[k2][:], in0=QK[:, :, o:o + 64], in1=WB[bi][:, :, :, o:o + 64].rearrange("p a b c -> p (a b) c"), op=ALU.mult),
                 reads=[QK.res, WB[bi].res], writes=[qks[k2].res])

        def stageA2(step):
            k2 = step % 3
            for hh in range(4):
                S.op(S.pe, lambda h, hh=hh, k2=k2: h.transpose(out=pK[:, hh, :], in_=qks[k2][:, 4 + hh, :], identity=self.ident[:]),
                     reads=[qks[k2].res, self.ident.res], writes=[pK.res], inc=(hh == 3))
            S.op(S.act, lambda h, k2=k2: h.activation(out=khat[k2][:], in_=pK[:], func=AF.Copy), reads=[pK.res], writes=[khat[k2].res])

        def stageB(step):
            gi, c = order[step]
            u0, n = groups[gi]
            o = c * 64
            chunk = (u0 + o) // 64
            k2 = step % 3
            V, R_ = vgbuf[step], rowbuf[step]
            for hh in range(4):
                S.op(S.pe, lambda h, hh=hh, k2=k2, V=V, c=c: h.matmul(pN[:, hh * 256:hh * 256 + 129], khat[k2][:, hh, :], V[:, c, hh, :], start=True, stop=True),
                     reads=[khat[k2].res, V.res], writes=[pN.res], inc=(hh == 3))
            S.op(S.pe, lambda h, R_=R_, o=o: h.matmul(pO[:, 200:204], R_[:, 3, o:o + 64], self.identf[0:4, 0:4], start=True, stop=True),
                 reads=[R_.res, self.identf.res], writes=[pO.res], inc=False)
            for hh in range(4):
                S.op(S.pe, lambda h, hh=hh, k2=k2, V=V, c=c: h.matmul(pO[:, hh * 256:hh * 256 + 129], sT[k2][:, hh, :], V[:, c, hh, :], start=True, stop=False),
                     reads=[sT[k2].res, V.res], writes=[pO.res], inc=False)
                S.op(S.pe, lambda h, hh=hh, k2=k2: h.matmul(pO[:, hh * 256:hh * 256 + 129], qks[k2][:, hh, :], Cb[:, hh, :], start=False, stop=True),
                     reads=[qks[k2].res, Cb.res], writes=[pO.res], inc=(hh == 3))
            S.op(S.act, lambda h, k2=k2: h.activation(out=enm[k2][:], in_=pO[:, 200:204], func=AF.Copy), reads=[pO.res], writes=[enm[k2].res])
            S.op(S.act, lambda h, k2=k2: h.activation(out=rr[k2][:], in_=pO3[:, :, 128], func=AF.Abs), reads=[pO.res], writes=[rr[k2].res])
            S.op(S.pool, lambda h, d=d, chunk=chunk: h.tensor_tensor(out=Cf[:], in0=Cf[:], in1=DEC[:, d, :, chunk:chunk + 1].to_broadcast([128, 4, 129]), op=ALU.mult),
                 reads=[Cf.res, DEC.res], writes=[Cf.res])
            S.op(S.dve, lambda h: h.tensor_tensor(out=Cf[:], in0=Cf[:], in1=pN3[:, :, 0:129], op=ALU.add), reads=[Cf.res, pN.res], writes=[Cf.res])
            S.op(S.act, lambda h: h.activation(out=Cb[:], in_=Cf[:], func=AF.Copy), reads=[Cf.res], writes=[Cb.res])
            S.op(S.dve, lambda h, k2=k2: h.tensor_tensor(out=rr[k2][:], in0=rr[k2][:], in1=enm[k2][:], op=ALU.max), reads=[rr[k2].res, enm[k2].res], writes=[rr[k2].res])
            S.op(S.dve, lambda h, k2=k2: h.reciprocal(out=rr[k2][:], in_=rr[k2][:]), reads=[rr[k2].res], writes=[rr[k2].res])
            S.op(S.dve, lambda h, k2=k2: h.tensor_tensor(out=ho[k2][:], in0=pO3[:, :, 0:128], in1=rr[k2][:].unsqueeze(2).to_broadcast([64, 4, 128]), op=ALU.mult),
                 reads=[pO.res, rr[k2].res], writes=[ho[k2].res])
            S.dma(S.sp, [(self.OM[l][d, u0 + o:u0 + o + 64, :], ho[k2][:].rearrange("p a b -> p (a b)"))], reads=[ho[k2].res], writes=[self.R(self.OM[l].name)])

        vgbuf = {}
        rowbuf = {}

        def A(step):
            stageA(step)
            gi, c = order[step]
            vgbuf[step] = vg[bufof[gi]]
            rowbuf[step] = rows[bufof[gi]]
        A(0)
        if len(order) > 1:
            A(1)
        stageA2(0)
        for step in range(len(order)):
            if step + 2 < len(order):
                A(step + 2)
            if step + 1 < len(order):
                stageA2(step + 1)
            stageB(step)


Builder.phase_ml_gates = phase_ml_gates
Builder.phase_ml_conv = phase_ml_conv
Builder.phase_ml_scan = phase_ml_scan


def phase_merge(self, l, streams):
    S = self.S
    win = self.w_in[l].rearrange("(kc p) n -> p kc n", p=128)
    Wm = self.sb("Wm", [128, 8, 4096], BF16)
    Wmr = [Res(f"Wm{k}") for k in range(4)]
    for ki, k0 in enumerate(range(0, 8, 2)):
        S.dma(S.pool, [(Wm[:, k0:k0 + 2, 0:512], win[:, k0:k0 + 2, O_GR:O_GR + 512]),
                       (Wm[:, k0:k0 + 2, 512:1024], win[:, k0:k0 + 2, O_MO:O_MO + 512]),
                       (Wm[:, k0:k0 + 2, 1024:4096], win[:, k0:k0 + 2, O_SA:O_SA + 3072])], writes=[Wmr[ki]])
    Woa = self.sb("Woa", [64, 8, D], BF16)
    Wog = self.sb("Wog", [128, 4, D], BF16)
    Wom = self.sb("Wom", [128, 4, D], BF16)
    Wo = self.sb("Wo", [128, 8, D], BF16)
    S.dma(S.pool, [(Woa[:], self.w_out_attn[l].rearrange("(h p) n -> p h n", p=64))], writes=[Woa.res])
    S.dma(S.pool, [(Wog[:], self.w_out_gla[l].rearrange("(c p) n -> p c n", p=128))], writes=[Wog.res])
    S.dma(S.pool, [(Wom[:], self.w_out_mlstm[l].rearrange("(c p) n -> p c n", p=128))], writes=[Wom.res])
    S.dma(S.pool, [(Wo[:, 0:4, :], self.w_o[l].rearrange("(c p) n -> p c n", p=128)[:, 0:4, :]),
                   (Wo[:, 4:8, :], self.w_o[l].rearrange("(c p) n -> p c n", p=128)[:, 4:8, :])], writes=[Wo.res])
    gains = self.sb("gains", [128, 2, 128], F32)
    S.dma(S.sp, [(gains[:, 0, :], bcast_rows(self.gla_norm[l:l + 1, :], 128)), (gains[:, 1, :], bcast_rows(self.mlstm_norm[l:l + 1, :], 128))], writes=[gains.res])
    eps_col = self.eps_col
    hT = [self.sb(f"mhT{i}", [128, 8, 128], BF16) for i in range(2)]
    aTt = [self.sb(f"maT{i}", [64, 8, 128], BF16) for i in range(2)]
    og = [self.sb(f"mog{i}", [128, 2, 512], F32) for i in range(2)]
    om = [self.sb(f"mom{i}", [128, 2, 512], F32) for i in range(2)]
    xr = [self.sb(f"mxr{i}", [128, D], F32) for i in range(2)]
    gts = [self.sb(f"mgt{i}", [128, 8, 512], F32) for i in range(2)]
    sq = self.sb("msq", [128, 512], F32)
    ssqs = [self.sb(f"mssq{i}", [128, 2, 4], F32) for i in range(2)]
    bn = [self.sb(f"mbn{i}", [128, 512], F32) for i in range(2)]
    bbs = [[self.sb(f"mbb{i}{j}", [128, 512], BF16) for j in range(2)] for i in range(2)]
    bTs = [[self.sb(f"mbT{i}{j}", [128, 4, 128], BF16) for j in range(2)] for i in range(2)]
    yb = self.sb("myb", [128, D], BF16)
    yT = self.sb("myT", [128, 8, 128], BF16)
    t1 = [self.sb(f"mt1{i}", [128, 512], F32) for i in range(3)]
    G5 = self.sb("mG5", [128, D], F32)
    pg = [self.ps(f"mpg{i}", [128, 512]) for i in range(2)]
    pT1s = [self.ps(f"mpT1{j}", [128, 512], BF16) for j in range(2)]
    pT2 = self.ps("mpT2", [128, D], BF16)
    py = [self.ps(f"mpy{i}", [128, 512]) for i in range(3)]
    pY = py[0]
    cnt = {}

    def nxt(key, n):
        v = cnt.get(key, 0)
        cnt[key] = v + 1
        return v % n
    H2Tv = self.H2T[l].rearrange("(kc p) t -> p kc t", p=128)
    work = []
    for (tag, src, dst, ntok, row, uoff) in streams:
        for t0 in range(0, ntok, 128):
            work.append((tag, src, dst, row, uoff + t0, t0))

    def stage1(w, i):
        (tag, src, dst, row, u, t0) = work[w]
        H, AT, OGt, OMt, XR, gt, ssq = hT[i], aTt[i], og[i], om[i], xr[i], gts[i], ssqs[i]
        S.dma(S.sp, [(H[:], H2Tv[:, :, u:u + 128])], reads=[self.R(self.H2T[l].name)], writes=[H.res])
        S.dma(S.sp, [(AT[:], self.ATT[l][:, :, u:u + 128])], reads=[self.R(self.ATT[l].name)], writes=[AT.res])
        S.dma(S.sp, [(OGt[:, 0, :], self.OG[l][0, u:u + 128, :]), (OGt[:, 1, :], self.OG[l][1, u:u + 128, :])], reads=[self.R(self.OG[l].name)], writes=[OGt.res])
        S.dma(S.sp, [(OMt[:, 0, :], self.OM[l][0, u:u + 128, :]), (OMt[:, 1, :], self.OM[l][1, u:u + 128, :])], reads=[self.R(self.OM[l].name)], writes=[OMt.res])
        S.dma(S.sp, [(XR[:], src[t0:t0 + 128, :])], reads=[self.R(src.name)], writes=[XR.res])

    def stage1g(w, i):
        H, gt = hT[i], gts[i]
        for blk in range(8):
            p = pg[nxt("pg", 2)]
            for kc in range(8):
                S.op(S.pe, lambda h, kc=kc, blk=blk, p=p, H=H: h.matmul(p[:], H[:, kc, :], Wm[:, kc, blk * 512:(blk + 1) * 512], start=(kc == 0), stop=(kc == 7)),
                     reads=[H.res] + Wmr, writes=[p.res], inc=(kc == 7))
            fn = AF.Silu if blk == 0 else AF.Sigmoid
            S.op(S.act, lambda h, blk=blk, p=p, fn=fn, gt=gt: h.activation(out=gt[:, blk, :], in_=p[:], func=fn), reads=[p.res], writes=[gt.res])

    def stage1c(w, i):
        OGt, OMt, gt, ssq = og[i], om[i], gts[i], ssqs[i]
        for br, Ot in enumerate((OGt, OMt)):
            S.op(S.pool, lambda h, Ot=Ot: h.tensor_tensor(out=Ot[:, 0, :], in0=Ot[:, 0, :], in1=Ot[:, 1, :], op=ALU.add), reads=[Ot.res], writes=[Ot.res])
            S.op(S.act, lambda h, Ot=Ot: h.activation(out=sq[:], in_=Ot[:, 0, :], func=AF.Square), reads=[Ot.res], writes=[sq.res])
            S.op(S.dve, lambda h, br=br, ssq=ssq: h.tensor_reduce(out=ssq[:, br, :], in_=sq[:].rearrange("p (a b) -> p a b", b=128), axis=AX.X, op=ALU.add),
                 reads=[sq.res], writes=[ssq.res])
        S.op(S.act, lambda h, ssq=ssq: h.activation(out=ssq[:], in_=ssq[:], func=AF.Sqrt, scale=1.0 / 128, bias=eps_col[:]),
             reads=[ssq.res, eps_col.res], writes=[ssq.res])
        S.op(S.dve, lambda h, ssq=ssq: h.reciprocal(out=ssq[:], in_=ssq[:]), reads=[ssq.res], writes=[ssq.res])
        for br, Ot in enumerate((OGt, OMt)):
            B_ = bn[br]
            S.op(S.dve, lambda h, br=br, Ot=Ot, B_=B_, ssq=ssq: h.tensor_tensor(out=B_[:].rearrange("p (a b) -> p a b", b=128), in0=Ot[:, 0, :].rearrange("p (a b) -> p a b", b=128),
                                                                              in1=ssq[:, br, :].unsqueeze(2).to_broadcast([128, 4, 128]), op=ALU.mult),
                 reads=[Ot.res, ssq.res], writes=[B_.res])
            S.op(S.pool, lambda h, br=br, B_=B_: h.tensor_tensor(out=B_[:].rearrange("p (a b) -> p a b", b=128), in0=B_[:].rearrange("p (a b) -> p a b", b=128),
                                                              in1=gains[:, br:br + 1, :].to_broadcast([128, 4, 128]), op=ALU.mult),
                 reads=[B_.res, gains.res], writes=[B_.res])

    def stage1d(w, i):
        gt = gts[i]
        for br in range(2):
            B_ = bn[br]
            BB = bbs[i][br]
            S.op(S.dve, lambda h, br=br, B_=B_, BB=BB, gt=gt: h.tensor_tensor(out=BB[:], in0=B_[:], in1=gt[:, br, :], op=ALU.mult), reads=[B_.res, gt.res], writes=[BB.res])

    def stage1b(w, i):
        for br in range(2):
            BB = bbs[i][br]
            pT1 = pT1s[br]
            for c in range(4):
                S.op(S.pe, lambda h, c=c, BB=BB, pT1=pT1: h.transpose(out=pT1[:, c * 128:(c + 1) * 128], in_=BB[:, c * 128:(c + 1) * 128], identity=self.ident[:]),
                     reads=[BB.res, self.ident.res], writes=[pT1.res], inc=(c == 3))
            BT = bTs[i][br]
            if br == 0:
                S.op(S.act, lambda h, BT=BT, pT1=pT1: h.activation(out=BT[:].rearrange("p a b -> p (a b)"), in_=pT1[:], func=AF.Copy), reads=[pT1.res], writes=[BT.res])
            else:
                S.op(S.dve, lambda h, BT=BT, pT1=pT1: h.tensor_copy(out=BT[:].rearrange("p a b -> p (a b)"), in_=pT1[:]), reads=[pT1.res], writes=[BT.res])

    cur_row = [None]

    def stage2(w, i):
        (tag, src, dst, row, u, t0) = work[w]
        AT, XR, gt = aTt[i], xr[i], gts[i]
        if cur_row[0] != row:
            cur_row[0] = row
            srcg = self.MOD[l][row:row + 1, 5 * D:6 * D]
            S.dma(S.sp, [(G5[:], dram_ap(srcg, srcg.offset, [[0, 128], [1, D]]))], reads=[self.R("MOD", l)], writes=[G5.res])
        for half in range(2):
            cs_ = slice(half * 512, (half + 1) * 512)
            for hh in range(8):
                S.op(S.pe, lambda h, hh=hh, AT=AT, cs_=cs_: h.matmul(py[0][:], AT[:, hh, :], Woa[:, hh, cs_], start=(hh == 0), stop=(hh == 7)),
                     reads=[AT.res, Woa.res], writes=[py[0].res], inc=(hh == 7))
            for c in range(4):
                S.op(S.pe, lambda h, c=c, cs_=cs_, BT=bTs[i][0]: h.matmul(py[1][:], BT[:, c, :], Wog[:, c, cs_], start=(c == 0), stop=(c == 3)),
                     reads=[bTs[i][0].res, Wog.res], writes=[py[1].res], inc=(c == 3))
            for c in range(4):
                S.op(S.pe, lambda h, c=c, cs_=cs_, BT=bTs[i][1]: h.matmul(py[2][:], BT[:, c, :], Wom[:, c, cs_], start=(c == 0), stop=(c == 3)),
                     reads=[bTs[i][1].res, Wom.res], writes=[py[2].res], inc=(c == 3))
            S.op(S.dve, lambda h, half=half, gt=gt: h.tensor_tensor(out=t1[0][:], in0=py[0][:], in1=gt[:, 2 + half, :], op=ALU.mult), reads=[py[0].res, gt.res], writes=[t1[0].res])
            S.op(S.dve, lambda h, half=half, gt=gt: h.tensor_tensor(out=t1[1][:], in0=py[1][:], in1=gt[:, 4 + half, :], op=ALU.mult), reads=[py[1].res, gt.res], writes=[t1[1].res])
            S.op(S.dve, lambda h, half=half, gt=gt: h.tensor_tensor(out=t1[2][:], in0=py[2][:], in1=gt[:, 6 + half, :], op=ALU.mult), reads=[py[2].res, gt.res], writes=[t1[2].res])
            S.op(S.dve, lambda h: h.tensor_tensor(out=t1[0][:], in0=t1[0][:], in1=t1[1][:], op=ALU.add), reads=[t1[0].res, t1[1].res], writes=[t1[0].res])
            S.op(S.dve, lambda h, cs_=cs_: h.tensor_tensor(out=yb[:, cs_], in0=t1[0][:], in1=t1[2][:], op=ALU.add), reads=[t1[0].res, t1[2].res], writes=[yb.res])
        for kc in range(8):
            S.op(S.pe, lambda h, kc=kc: h.transpose(out=pT2[:, kc * 128:(kc + 1) * 128], in_=yb[:, kc * 128:(kc + 1) * 128], identity=self.ident[:]),
                 reads=[yb.res, self.ident.res], writes=[pT2.res], inc=(kc == 7))
        S.op(S.act, lambda h: h.activation(out=yT[:].rearrange("p a b -> p (a b)"), in_=pT2[:], func=AF.Copy), reads=[pT2.res], writes=[yT.res])
        for half in range(2):
            cs_ = slice(half * 512, (half + 1) * 512)
            for kc in range(8):
                S.op(S.pe, lambda h, kc=kc, cs_=cs_: h.matmul(pY[:], yT[:, kc, :], Wo[:, kc, cs_], start=(kc == 0), stop=(kc == 7)),
                     reads=[yT.res, Wo.res], writes=[pY.res], inc=(kc == 7))
            tq = t1[1 + half]
            S.op(S.dve, lambda h, cs_=cs_, tq=tq: h.tensor_tensor(out=tq[:], in0=pY[:], in1=G5[:, cs_], op=ALU.mult), reads=[pY.res, G5.res], writes=[tq.res])
            S.op(S.pool, lambda h, cs_=cs_, XR=XR, tq=tq: h.tensor_tensor(out=XR[:, cs_], in0=XR[:, cs_], in1=tq[:], op=ALU.add), reads=[XR.res, tq.res], writes=[XR.res])
        S.dma(S.pool, [(dst[t0:t0 + 128, :], XR[:])], reads=[XR.res], writes=[self.R(dst.name)])

    def stage1all(w, i):
        stage1(w, i)
        stage1c(w, i)
        stage1g(w, i)
        stage1d(w, i)
        stage1b(w, i)

    stage1all(0, 0)
    for w in range(len(work)):
        if w + 1 < len(work):
            stage1all(w + 1, (w + 1) % 2)
        stage2(w, w % 2)


Builder.phase_merge = phase_merge

_NC_CACHE = {}


def kernel(**inputs):
    inp = {k: np.asarray(v) for k, v in inputs.items()}
    Bsz, SEQ, _ = inp["x"].shape
    T = SEQ
    if T not in _NC_CACHE:
        _NC_CACHE[T] = Builder(T).build()
    nc = _NC_CACHE[T]
    in_maps = [make_in_map(inp, b, 0, T) for b in range(Bsz)]
    res = run_bass_kernel_spmd(nc, in_maps, core_ids=list(range(Bsz)))
    out = np.stack([np.asarray(r["y"], dtype=np.float32) for r in res.results], axis=0)
    return out


W_NAMES = ["mod_w", "mod_b", "norm_g", "ffn1_w13", "ffn1_w2", "ffn2_w13", "ffn2_w2", "w_in", "attn_q_norm", "attn_k_norm", "attn_sink", "gla_w2", "gla_b", "mlstm_conv_w", "mlstm_conv_b", "mlstm_ib", "mlstm_fb", "gla_norm", "mlstm_norm", "w_out_attn", "w_out_gla", "w_out_mlstm", "w_o"]


def make_in_map(inp, b, t0, T):
    m = {"x": np.ascontiguousarray(inp["x"][b, t0:t0 + T]), "c": np.ascontiguousarray(inp["c"][b]),
         "ctx": np.ascontiguousarray(inp["ctx"][b]), "c_ctx": np.ascontiguousarray(inp["c_ctx"])}
    for k in W_NAMES:
        m[k] = np.ascontiguousarray(inp[k])
    m["rope_cs"] = rope_table(t0, T)
    return m


def rope_table(t0, T):
    pos = np.arange(t0, t0 + T)
    r = (pos // 64).astype(np.float32)
    col = (pos % 64).astype(np.float32)
    inv = (np.float32(10000.0) ** (-np.arange(16, dtype=np.float32) / np.float32(16))).astype(np.float32)
    ang = np.concatenate([r[:, None] * inv, col[:, None] * inv], axis=-1).astype(np.float32)
    return np.ascontiguousarray(np.stack([np.cos(ang), np.sin(ang)], axis=1).astype(np.float32))
```

```python
import numpy as np
from contextlib import ExitStack
import concourse.bass as bass
import concourse.mybir as mybir
from concourse.bass_utils import run_bass_kernel_spmd

F32 = mybir.dt.float32
BF16 = mybir.dt.bfloat16
AF = mybir.ActivationFunctionType
ALU = mybir.AluOpType
AX = mybir.AxisListType

D = 1024
DFF = 2816
NMOD = 9
LC = 256
EPS = 1e-6
DEPTH = 2
D_IN = 7472


class Res:
    __slots__ = ("name", "w", "r")

    def __init__(self, name=""):
        self.name = name
        self.w = None
        self.r = []


class Eng:
    def __init__(self, name, is_pe=False):
        self.name = name
        self.is_pe = is_pe
        self.ops = []
        self.sems = []
        self.si = 0
        self.cnt = 0
        self.seen = {}
        self.pend_r = []
        self.pend_w = []
        self.pool = []
        self.pi = 0


ROT = 30000


class Sched:
    def __init__(self, nc, es):
        self.nc = nc
        self.es = es
        self.pe = Eng("pe", True)
        self.act = Eng("act")
        self.dve = Eng("dve")
        self.pool = Eng("pool")
        self.sp = Eng("sp")
        self.engs = [self.pe, self.act, self.dve, self.pool, self.sp]
        self.semid = {}
        n_rot = {"pe": 6, "act": 3, "dve": 3, "pool": 3, "sp": 1}
        for e in self.engs:
            for i in range(n_rot[e.name]):
                s = es.enter_context(nc.semaphore(f"s_{e.name}{i}"))
                e.sems.append(s)
        for e, n in ((self.sp, 20), (self.pool, 10), (self.act, 4)):
            for i in range(n):
                s = es.enter_context(nc.semaphore(f"d_{e.name}{i}"))
                e.pool.append([s, 0])
        self.n_ops = 0

    def _need(self, eng, tok, raw):
        if tok is None:
            return None
        sem, val, owner = tok
        if owner == eng.name:
            if eng.is_pe:
                return None
            if not raw:
                return None
        key = id(sem)
        if eng.seen.get(key, 0) >= val:
            return None
        eng.seen[key] = val
        return (sem, val)

    def _waits(self, eng, reads, writes):
        ws = []
        for r in reads:
            w = self._need(eng, r.w, True)
            if w:
                ws.append(w)
        for wr in writes:
            w = self._need(eng, wr.w, False)
            if w:
                ws.append(w)
            for t in wr.r:
                w = self._need(eng, t, False)
                if w:
                    ws.append(w)
        for (sem, val) in ws:
            eng.ops.append(lambda h, sem=sem, val=val: h.wait_ge(sem, val))

    def _record(self, tok, reads, writes):
        for r in reads:
            r.r = [t for t in r.r if t[2] != tok[2] or t[0] is not tok[0]] + [tok]
        for w in writes:
            w.w = tok
            w.r = []

    def op(self, eng, fn, reads=(), writes=(), inc=True):
        self.n_ops += 1
        reads = list(reads)
        writes = list(writes)
        self._waits(eng, reads, writes)
        if not inc:
            eng.ops.append(lambda h, fn=fn: fn(h))
            eng.pend_r += reads
            eng.pend_w += writes
            return
        if eng.cnt >= ROT:
            eng.si += 1
            eng.cnt = 0
        eng.cnt += 1
        sem = eng.sems[eng.si]
        tok = (sem, eng.cnt, eng.name)
        eng.ops.append(lambda h, fn=fn, sem=sem: fn(h).then_inc(sem, 1))
        self._record(tok, reads + eng.pend_r, writes + eng.pend_w)
        eng.pend_r = []
        eng.pend_w = []

    def dma(self, eng, pairs, reads=(), writes=(), **kw):
        self.n_ops += 1
        reads = list(reads)
        writes = list(writes)
        self._waits(eng, reads, writes)
        ent = eng.pool[eng.pi]
        eng.pi = (eng.pi + 1) % len(eng.pool)
        sem = ent[0]
        if ent[1] > 0 and eng.seen.get(id(sem), 0) < ent[1]:
            v = ent[1]
            eng.ops.append(lambda h, sem=sem, v=v: h.wait_ge(sem, v))
            eng.seen[id(sem)] = v
        for (o, i) in pairs:
            ent[1] += 16
            eng.ops.append(lambda h, o=o, i=i, sem=sem: h.dma_start(out=o, in_=i, **kw).then_inc(sem, 16))
        tok = (sem, ent[1], "dma_" + eng.name + str(id(sem)))
        self._record(tok, reads, writes)

    def barrier(self):
        toks = []
        for e in self.engs:
            assert not e.pend_r and not e.pend_w, e.name
            for i in range(e.si + 1):
                v = ROT if i < e.si else e.cnt
                if v > 0:
                    toks.append((e, e.sems[i], v))
            for ent in e.pool:
                if ent[1] > 0:
                    toks.append((None, ent[0], ent[1]))
        for e in self.engs:
            for (own, sem, v) in toks:
                if own is e:
                    continue
                if e.seen.get(id(sem), 0) >= v:
                    continue
                e.seen[id(sem)] = v
                e.ops.append(lambda h, sem=sem, v=v: h.wait_ge(sem, v))

    def finish(self):
        for e in (self.sp, self.pool, self.act):
            for ent in e.pool:
                if ent[1] > 0:
                    self.sp.ops.append(lambda h, sem=ent[0], v=ent[1]: h.wait_ge(sem, v))

    def replay(self):
        nc = self.nc
        with nc.Block() as block:
            @block.tensor
            def _(h):
                for f in self.pe.ops:
                    f(h)

            @block.scalar
            def _(h):
                for f in self.act.ops:
                    f(h)

            @block.vector
            def _(h):
                for f in self.dve.ops:
                    f(h)

            @block.gpsimd
            def _(h):
                for f in self.pool.ops:
                    f(h)

            @block.sync
            def _(h):
                for f in self.sp.ops:
                    f(h)


class Tile:
    def __init__(self, t, name):
        self.t = t
        self.res = Res(name)

    def __getitem__(self, k):
        return self.t[k]


def dram_ap(t, offset, pattern):
    return bass.AP(t.tensor, offset, pattern)


class Builder:
    def __init__(self, T, depth=DEPTH, stop=None, dbg=()):
        self.T = T
        self.depth = depth
        self.stop = stop
        self.dbg = dbg
        self.nc = bass.Bass("TRN2", target_bir_lowering=False)
        self.es = ExitStack()
        self.S = None
        self.dres = {}
        self.rr = {}

    def din(self, name, shape):
        return self.nc.dram_tensor(name, list(shape), F32, kind="ExternalInput").ap()

    def dout(self, name, shape, dt=F32):
        return self.nc.dram_tensor(name, list(shape), dt, kind="ExternalOutput").ap()

    def dscr(self, name, shape, dt=F32):
        if name in self.dbg:
            return self.nc.dram_tensor(name, list(shape), dt, kind="ExternalOutput").ap()
        return self.nc.dram_tensor(name, list(shape), dt).ap()

    def R(self, *key):
        if key not in self.dres:
            self.dres[key] = Res(str(key))
        return self.dres[key]

    def sb(self, name, shape, dt):
        self.uid = getattr(self, "uid", 0) + 1
        name = f"{name}_{self.uid}"
        t = self.cur.enter_context(self.nc.sbuf_tensor(name, list(shape), dt))
        return Tile(t, name)

    def ps(self, name, shape, dt=F32):
        self.uid = getattr(self, "uid", 0) + 1
        name = f"{name}_{self.uid}"
        t = self.cur.enter_context(self.nc.psum_tensor(name, list(shape), dt))
        return Tile(t, name)

    def build(self):
        nc = self.nc
        T = self.T
        L = self.depth
        with self.es as es:
            self.S = S = Sched(nc, es)
            self.x_in = self.din("x", [T, D])
            self.c_in = self.din("c", [D])
            self.ctx_in = self.din("ctx", [LC, D])
            self.cctx_in = self.din("c_ctx", [D])
            self.mod_w = self.din("mod_w", [L, D, NMOD * D])
            self.mod_b = self.din("mod_b", [L, NMOD * D])
            self.norm_g = self.din("norm_g", [L, 3, D])
            self.ffn_w13 = [self.din("ffn1_w13", [L, D, 2 * DFF]), self.din("ffn2_w13", [L, D, 2 * DFF])]
            self.ffn_w2 = [self.din("ffn1_w2", [L, DFF, D]), self.din("ffn2_w2", [L, DFF, D])]
            self.w_in = self.din("w_in", [L, D, D_IN])
            self.attn_q_norm = self.din("attn_q_norm", [L, 64])
            self.attn_k_norm = self.din("attn_k_norm", [L, 64])
            self.attn_sink = self.din("attn_sink", [L, 8])
            self.gla_w2 = self.din("gla_w2", [L, 2, 16, 256])
            self.gla_b = self.din("gla_b", [L, 2, 256])
            self.rope_cs = self.din("rope_cs", [T, 2, 32])
            self.y_out = self.dout("y", [T, D])
            TT = self.TT = LC + T
            self.H2T = [self.dscr(f"H2T{l}", [D, TT], BF16) for l in range(L)]
            self.QT = [self.dscr(f"QT{l}", [64, 8, TT], BF16) for l in range(L)]
            self.KT = [self.dscr(f"KT{l}", [64, 2, TT], BF16) for l in range(L)]
            self.VA = [self.dscr(f"VA{l}", [TT, 128], BF16) for l in range(L)]
            self.GV = [self.dscr(f"GV{l}", [TT, 512], BF16) for l in range(L)]
            self.MV = [self.dscr(f"MV{l}", [TT, 512], BF16) for l in range(L)]
            self.GATES = [self.dscr(f"GATES{l}", [16, TT]) for l in range(L)]
            self.QG = [self.dscr(f"QG{l}", [2, 64, 4, TT], BF16) for l in range(L)]
            self.KG = [self.dscr(f"KG{l}", [2, 64, 4, TT], BF16) for l in range(L)]
            self.KH = [self.dscr(f"KH{l}", [2, 64, 4, TT], BF16) for l in range(L)]
            self.MQK = [self.dscr(f"MQK{l}", [D, TT + 4], BF16) for l in range(L)]
            self.ATT = [self.dscr(f"ATT{l}", [64, 8, TT], BF16) for l in range(L)]
            self.OG = [self.dscr(f"OG{l}", [2, TT, 512]) for l in range(L)]
            self.OM = [self.dscr(f"OM{l}", [2, TT, 512]) for l in range(L)]
            self.MROWS = [self.dscr(f"MROWS{l}", [2, 5, 4, TT]) for l in range(L)]
            self.MQC = [self.dscr(f"MQC{l}", [D, TT], BF16) for l in range(L)]
            self.gla_norm = self.din("gla_norm", [L, 128])
            self.mlstm_norm = self.din("mlstm_norm", [L, 128])
            self.w_out_attn = self.din("w_out_attn", [L, 512, D])
            self.w_out_gla = self.din("w_out_gla", [L, 512, D])
            self.w_out_mlstm = self.din("w_out_mlstm", [L, 512, D])
            self.w_o = self.din("w_o", [L, D, D])
            self.X2 = [self.dscr(f"X2_{l}", [T, D]) for l in range(L)]
            self.C2 = [self.dscr(f"C2_{l}", [LC, D]) for l in range(L)]
            self.X3 = [self.dscr(f"X3_{l}", [T, D]) for l in range(L)]
            self.C3 = [self.dscr(f"C3_{l}", [LC, D]) for l in range(L)]
            self.conv_w = self.din("mlstm_conv_w", [L, 5, D])
            self.conv_b = self.din("mlstm_conv_b", [L, D])
            self.mlstm_ib = self.din("mlstm_ib", [L, 2, 4])
            self.mlstm_fb = self.din("mlstm_fb", [L, 2, 4])
            self.MOD = [self.dscr(f"MOD{l}", [2, NMOD * D]) for l in range(L)]
            self.X1 = [self.dscr(f"X1_{l}", [T, D]) for l in range(L)]
            self.C1 = [self.dscr(f"C1_{l}", [LC, D]) for l in range(L)]
            with ExitStack() as cst:
                self.cur = cst
                self.ident = self.sb("ident", [128, 128], BF16)
                self.identf = self.sb("identf", [128, 128], F32)
                self.eps_col = self.sb("eps_col", [128, 1], F32)
                S.op(S.dve, lambda h: h.memset(self.eps_col[:], EPS), writes=[self.eps_col.res])
                self.make_consts()
                for l in range(L):
                    xin = self.x_in if l == 0 else self.X3[l - 1]
                    cin = self.ctx_in if l == 0 else self.C3[l - 1]
                    with ExitStack() as ph:
                        self.cur = ph
                        self.phase_mod(l)
                        S.barrier()
                    if self.stop == ("mod", l):
                        break
                    with ExitStack() as ph:
                        self.cur = ph
                        self.phase_ffn(l, 0, [("ctx", cin, self.C1[l], LC, 1), ("lat", xin, self.X1[l], T, 0)])
                        S.barrier()
                    if self.stop == ("ffn1", l):
                        break
                    with ExitStack() as lay:
                        self.cur = lay
                        self.EL = self.sb("EL", [64, 4, 2, TT // 64], F32)
                        with ExitStack() as ph:
                            self.cur = ph
                            self.phase_feat(l, [("ctx", self.C1[l], LC, 1, 0, False), ("lat", self.X1[l], T, 0, LC, True)])
                            S.barrier()
                        if self.stop == ("feat", l):
                            break
                        with ExitStack() as ph:
                            self.cur = ph
                            self.phase_attn(l, l < L - 1)
                            S.barrier()
                        if self.stop == ("attn", l):
                            break
                        with ExitStack() as ph:
                            self.cur = ph
                            self.phase_gla(l)
                            S.barrier()
                        if self.stop == ("gla", l):
                            break
                        self.cur = lay
                        self.DEC = self.sb("DEC", [128, 2, 4, TT // 64], F32)
                        self.sel = self.sb("sel", [4, 4, 128], F32)
                        S.op(S.dve, lambda h, sel_t=self.sel: h.tensor_copy(out=sel_t[:], in_=self.identf[0:4, 0:4].unsqueeze(2).to_broadcast([4, 4, 128])),
                             reads=[self.identf.res], writes=[self.sel.res])
                        stop_ml = False
                        for ph_name, ph_fn in (("mlg", self.phase_ml_gates), ("mlc", self.phase_ml_conv), ("mls", self.phase_ml_scan)):
                            with ExitStack() as ph:
                                self.cur = ph
                                ph_fn(l)
                                S.barrier()
                            if self.stop == (ph_name, l):
                                stop_ml = True
                                break
                        if stop_ml:
                            break
                        if self.stop == ("ml", l):
                            break
                    last = (l == L - 1)
                    with ExitStack() as ph:
                        self.cur = ph
                        st = [("lat", self.X1[l], self.X2[l], T, 0, LC)]
                        if not last:
                            st = [("ctx", self.C1[l], self.C2[l], LC, 1, 0)] + st
                        self.phase_merge(l, st)
                        S.barrier()
                    if self.stop == ("merge", l):
                        break
                    with ExitStack() as ph:
                        self.cur = ph
                        xdst = self.y_out if last else self.X3[l]
                        st = [("lat", self.X2[l], xdst, T, 0)]
                        import os
                        if not last and not os.environ.get("NOCTX2"):
                            st = [("ctx", self.C2[l], self.C3[l], LC, 1)] + st
                        self.phase_ffn(l, 1, [(a, b, c, d_, e) for (a, b, c, d_, e) in st])
                        S.barrier()
                    if self.stop == ("ffn2", l):
                        break
                S.finish()
                S.replay()
        return nc

    def make_consts(self):
        S = self.S
        nc = self.nc
        idf = self.identf
        S.op(S.pool, lambda h: h.memset(idf[:], 0.0), writes=[idf.res])
        S.op(S.pool, lambda h: h.affine_select(out=idf[:], in_=idf[:], pattern=[[-1, 128]],
                                                compare_op=ALU.not_equal, fill=1.0, base=0,
                                                channel_multiplier=1),
             reads=[idf.res], writes=[idf.res])
        S.op(S.dve, lambda h: h.tensor_copy(out=self.ident[:], in_=idf[:]), reads=[idf.res], writes=[self.ident.res])

    def phase_mod(self, l):
        S = self.S
        cl = self.sb("cl", [128, 8, 2], F32)
        cs = self.sb("cs", [128, 8, 2], F32)
        S.dma(S.sp, [(cl[:, :, 0], self.c_in.rearrange("(kc p) -> p kc", p=128)),
                     (cl[:, :, 1], self.cctx_in.rearrange("(kc p) -> p kc", p=128))],
              writes=[cl.res], allow_slow_non_contiguous=True)
        S.op(S.act, lambda h: h.activation(out=cs[:], in_=cl[:], func=AF.Silu), reads=[cl.res], writes=[cs.res])
        wm = [self.sb(f"wm{i}", [128, 8, 512], F32) for i in range(4)]
        mb = [self.sb(f"mb{i}", [2, 512], F32) for i in range(4)]
        mo = [self.sb(f"mo{i}", [2, 512], F32) for i in range(2)]
        pm = [self.ps(f"pm{i}", [2, 512]) for i in range(2)]
        mw = self.mod_w[l].rearrange("(kc p) n -> p kc n", p=128)
        for n in range(18):
            i = n % 4
            S.dma(S.sp if n % 2 == 0 else S.act, [(wm[i][:, 2 * q:2 * q + 2, :], mw[:, 2 * q:2 * q + 2, n * 512:(n + 1) * 512]) for q in range(4)], writes=[wm[i].res])
            mbsrc = self.mod_b[l:l + 1, n * 512:(n + 1) * 512]
            S.dma(S.sp, [(mb[i][0:1, :], mbsrc), (mb[i][1:2, :], mbsrc)], writes=[mb[i].res])
            for kc in range(8):
                S.op(S.pe, lambda h, kc=kc, i=i, j=n % 2: h.matmul(pm[j][:], cs[:, kc, :], wm[i][:, kc, :],
                                                           start=(kc == 0), stop=(kc == 7)),
                     reads=[cs.res, wm[i].res], writes=[pm[n % 2].res], inc=(kc == 7))
            S.op(S.dve, lambda h, i=i, j=n % 2: h.tensor_tensor(out=mo[j][:], in0=pm[j][:], in1=mb[i][:], op=ALU.add),
                 reads=[pm[n % 2].res, mb[i].res], writes=[mo[n % 2].res])
            S.dma(S.pool, [(self.MOD[l][:, n * 512:(n + 1) * 512], mo[n % 2][:])], reads=[mo[n % 2].res],
                  writes=[self.R("MOD", l)])

    def load_cols(self, dst_ap, src_row_ap, res):
        self.S.dma(self.S.sp, [(dst_ap, src_row_ap.rearrange("(kc p) -> p kc", p=128))], writes=[res],
                   allow_slow_non_contiguous=True)

    def adaln_cols(self, l, j, row, tag):
        S = self.S
        tmp = self.sb(f"adt_{tag}", [128, 3, 8], F32)
        A = self.sb(f"adA_{tag}", [128, 8], F32)
        MODr = self.MOD[l]
        S.dma(S.sp, [(tmp[:, 0, :], MODr[row, (3 * j) * D:(3 * j + 1) * D].rearrange("(kc p) -> p kc", p=128)),
                     (tmp[:, 1, :], MODr[row, (3 * j + 1) * D:(3 * j + 2) * D].rearrange("(kc p) -> p kc", p=128)),
                     (tmp[:, 2, :], self.norm_g[l, j, :].rearrange("(kc p) -> p kc", p=128))],
              reads=[self.R("MOD", l)], writes=[tmp.res], allow_slow_non_contiguous=True)
        S.op(S.dve, lambda h: h.scalar_tensor_tensor(out=A[:], in0=tmp[:, 1, :], scalar=1.0, in1=tmp[:, 2, :],
                                                      op0=ALU.add, op1=ALU.mult),
             reads=[tmp.res], writes=[A.res])
        return A, tmp

    def gate_bc(self, l, j, row, tag, mul):
        S = self.S
        G = self.sb(f"gate_{tag}", [128, D], F32)
        src = self.MOD[l][row:row + 1, (3 * j + 2) * D:(3 * j + 3) * D]
        src_b = dram_ap(src, src.offset, [[0, 128], [1, D]])
        S.dma(S.sp, [(G[:], src_b)], reads=[self.R("MOD", l)], writes=[G.res])
        if mul != 1.0:
            S.op(S.pool, lambda h: h.tensor_scalar(out=G[:], in0=G[:], scalar1=float(mul), scalar2=None, op0=ALU.mult),
                 reads=[G.res], writes=[G.res])
        return G

    def load_weight_bf16(self, dst, src3, nsplit):
        S = self.S
        kcn = dst.t.shape[1]
        step = max(1, kcn // nsplit)
        dst.parts = []
        for k0 in range(0, kcn, step):
            k1 = min(kcn, k0 + step)
            r = Res(f"wpart{k0}")
            dst.parts.append(r)
            S.dma(S.pool, [(dst[:, k0:k1, :], src3[:, k0:k1, :])], writes=[r])

    def norm_part(self, xt, nb, ss, rs, junk=None):
        S = self.S
        if junk is None:
            junk = self.junk
        S.op(S.act, lambda h: h.activation(out=junk[:], in_=xt[:], func=AF.Square, accum_out=ss[:]),
             reads=[xt.res], writes=[junk.res, ss.res])
        S.op(S.act, lambda h: h.activation(out=rs[:], in_=ss[:], func=AF.Sqrt, scale=1.0 / D, bias=self.eps_col[:]),
             reads=[ss.res], writes=[rs.res])
        S.op(S.dve, lambda h: h.reciprocal(out=rs[:], in_=rs[:]), reads=[rs.res], writes=[rs.res])
        S.op(S.dve, lambda h: h.tensor_scalar(out=nb[:], in0=xt[:], scalar1=rs[:], scalar2=None, op0=ALU.mult),
             reads=[xt.res, rs.res], writes=[nb.res])

    def transpose_part(self, nb, pT, hT, col0, A, sh, evac_engs):
        S = self.S
        for kc in range(8):
            S.op(S.pe, lambda h, kc=kc: h.transpose(out=pT[:, kc * 128:(kc + 1) * 128], in_=nb[:, kc * 128:(kc + 1) * 128],
                                                     identity=self.ident[:]),
                 reads=[nb.res, self.ident.res], writes=[pT.res], inc=(kc == 7))
        for kc in range(8):
            e = evac_engs[kc % len(evac_engs)]
            if e is S.act:
                S.op(e, lambda h, kc=kc: h.activation(out=hT[:, kc, col0:col0 + 128], in_=pT[:, kc * 128:(kc + 1) * 128],
                                                      func=AF.Identity, scale=A[:, kc:kc + 1], bias=sh[:, kc:kc + 1]),
                     reads=[pT.res, A.res, self.shres], writes=[hT.res])
            else:
                S.op(e, lambda h, kc=kc: h.tensor_scalar(out=hT[:, kc, col0:col0 + 128], in0=pT[:, kc * 128:(kc + 1) * 128],
                                                         scalar1=A[:, kc:kc + 1], scalar2=sh[:, kc:kc + 1],
                                                         op0=ALU.mult, op1=ALU.add),
                     reads=[pT.res, A.res, self.shres], writes=[hT.res])

    def phase_ffn(self, l, which, streams):
        S = self.S
        j = 0 if which == 0 else 2
        W13 = self.sb("W13", [128, 8, 2 * DFF], BF16)
        W2 = self.sb("W2", [128, 22, D], BF16)
        self.load_weight_bf16(W13, self.ffn_w13[which][l].rearrange("(kc p) n -> p kc n", p=128), 8)
        self.load_weight_bf16(W2, self.ffn_w2[which][l].rearrange("(fc p) n -> p fc n", p=128), 11)
        import os
        if os.environ.get("FFN_WONLY") and which == 1:
            return
        xl = [self.sb(f"xl{i}", [128, D], F32) for i in range(3)]
        xr = [self.sb(f"xr{i}", [128, D], F32) for i in range(2)]
        nb = [self.sb(f"nb{i}", [128, D], BF16) for i in range(4)]
        ss = [self.sb(f"ss{i}", [128, 1], F32) for i in range(4)]
        rs = [self.sb(f"rs{i}", [128, 1], F32) for i in range(4)]
        hT = self.sb("hT", [128, 8, 512], BF16)
        gT = self.sb("gT", [128, 22, 512], BF16)
        sa = [self.sb(f"sa{i}", [128, 512], F32) for i in range(2)]
        tt = [self.sb(f"tt{i}", [128, 512], F32) for i in range(2)]
        pT = [self.ps(f"pT{i}", [128, D], BF16) for i in range(2)]
        pA = [self.ps(f"pA{i}", [128, 512]) for i in range(2)]
        pB = [self.ps(f"pB{i}", [128, 512]) for i in range(2)]
        pY = [self.ps(f"pY{i}", [128, 512]) for i in range(2)]
        cnt = {"xl": 0, "xr": 0, "nb": 0, "pT": 0, "pAB": 0, "sa": 0, "tt": 0}

        for (tag, src, dst, ntok, row) in streams:
            A, tmp = self.adaln_cols(l, j, row, f"{which}{tag}")
            sh = tmp[:, 0, :]
            self.shres = tmp.res
            G = self.gate_bc(l, j, row, f"{which}{tag}", 0.5)
            tiles = [(t0, min(512, ntok - t0)) for t0 in range(0, ntok, 512)]
            rtag = ("xs", l, which, tag)

            def prep_norm(t0, s):
                i = cnt["xl"] % 3
                cnt["xl"] += 1
                k = cnt["nb"] % 4
                cnt["nb"] += 1
                S.dma(S.sp, [(xl[i][:], src[t0 + s * 128:t0 + (s + 1) * 128, :])], reads=[self.R(src.name)],
                      writes=[xl[i].res])
                self.norm_part(xl[i], nb[k], ss[k], rs[k], junk=nb[k])
                return nb[k]

            def prep_tr(nbt, s):
                k = cnt["pT"] % 2
                cnt["pT"] += 1
                self.transpose_part(nbt, pT[k], hT, s * 128, A, sh, [S.act, S.dve])

            def prep(t0, n):
                for s in range(n // 128):
                    nbt = prep_norm(t0, s)
                    prep_tr(nbt, s)

            prep(*tiles[0])
            for ti, (t0, n) in enumerate(tiles):
                nt = n // 128
                for p in range(22):
                    k = cnt["pAB"] % 2
                    cnt["pAB"] += 1
                    for kc in range(8):
                        S.op(S.pe, lambda h, kc=kc, p=p, k=k, n=n: h.matmul(pA[k][:, :n], W13[:, kc, p * 128:(p + 1) * 128], hT[:, kc, :n],
                                                                        start=(kc == 0), stop=(kc == 7)),
                             reads=W13.parts + [hT.res], writes=[pA[k].res], inc=(kc == 7))
                    for kc in range(8):
                        S.op(S.pe, lambda h, kc=kc, p=p, k=k, n=n: h.matmul(pB[k][:, :n], W13[:, kc, DFF + p * 128:DFF + (p + 1) * 128], hT[:, kc, :n],
                                                                        start=(kc == 0), stop=(kc == 7)),
                             reads=W13.parts + [hT.res], writes=[pB[k].res], inc=(kc == 7))
                    q = cnt["sa"] % 2
                    cnt["sa"] += 1
                    S.op(S.act, lambda h, k=k, q=q, n=n: h.activation(out=sa[q][:, :n], in_=pA[k][:, :n], func=AF.Silu),
                         reads=[pA[k].res], writes=[sa[q].res])
                    S.op(S.dve, lambda h, k=k, q=q, p=p, n=n: h.tensor_tensor(out=gT[:, p, :n], in0=sa[q][:, :n], in1=pB[k][:, :n], op=ALU.mult),
                         reads=[sa[q].res, pB[k].res], writes=[gT.res])
                xrs = []
                for s in range(nt):
                    pass
                if ti + 1 < len(tiles):
                    pending = tiles[ti + 1]
                else:
                    pending = None
                nbts = []
                if pending is not None:
                    for s in range(pending[1] // 128):
                        nbts.append(prep_norm(pending[0], s))
                for s in range(nt):
                    i = cnt["xr"] % 2
                    cnt["xr"] += 1
                    S.dma(S.sp, [(xr[i][:], src[t0 + s * 128:t0 + (s + 1) * 128, :])], reads=[self.R(src.name)],
                          writes=[xr[i].res])
                    for dh in range(2):
                        for fc in range(22):
                            S.op(S.pe, lambda h, fc=fc, dh=dh, s=s: h.matmul(pY[dh][:], gT[:, fc, s * 128:(s + 1) * 128], W2[:, fc, dh * 512:(dh + 1) * 512],
                                                                              start=(fc == 0), stop=(fc == 21)),
                                 reads=[gT.res] + W2.parts, writes=[pY[dh].res], inc=(fc == 21))
                    for dh in range(2):
                        q = cnt["tt"] % 2
                        cnt["tt"] += 1
                        S.op(S.dve, lambda h, dh=dh, q=q, G=G: h.tensor_tensor(out=tt[q][:], in0=pY[dh][:], in1=G[:, dh * 512:(dh + 1) * 512], op=ALU.mult),
                             reads=[pY[dh].res, G.res], writes=[tt[q].res])
                        S.op(S.pool, lambda h, dh=dh, q=q, i=i: h.tensor_tensor(out=xr[i][:, dh * 512:(dh + 1) * 512], in0=xr[i][:, dh * 512:(dh + 1) * 512],
                                                                                 in1=tt[q][:], op=ALU.add),
                             reads=[tt[q].res, xr[i].res], writes=[xr[i].res])
                    S.dma(S.pool, [(dst[t0 + s * 128:t0 + (s + 1) * 128, :], xr[i][:])], reads=[xr[i].res],
                          writes=[self.R(dst.name)])
                for s, nbt in enumerate(nbts):
                    prep_tr(nbt, s)


O_AQ, O_AK, O_AV = 0, 512, 640
O_GQ, O_GK, O_GV, O_GR, O_GG = 768, 1024, 1280, 1792, 2304
O_MQ, O_MK, O_MV, O_MO, O_MI, O_MF = 2336, 2848, 3360, 3872, 4384, 4392
O_SA, O_SG, O_SM = 4400, 5424, 6448


def bcast_rows(ap2d, nparts):
    return bass.AP(ap2d.tensor, ap2d.offset, [[0, nparts]] + [list(x) for x in ap2d.ap[1:]])


def rev_last(ap):
    pat = [list(x) for x in ap.ap]
    st, n = pat[-1]
    return bass.AP(ap.tensor, ap.offset + st * (n - 1), pat[:-1] + [[-st, n]])


def phase_feat(self, l, streams):
    S = self.S
    TT = self.TT
    win = self.w_in[l].rearrange("(kc p) n -> p kc n", p=128)
    Wa = self.sb("Wa", [128, 8, 768], BF16)
    Wv = self.sb("Wv", [128, 8, 1024], BF16)
    Wf = self.sb("Wf", [128, 8, 1536], BF16)
    Wg = self.sb("Wg", [128, 8, 48], BF16)
    S.dma(S.pool, [(Wa[:, 0:4, :], win[:, 0:4, 0:768]), (Wa[:, 4:8, :], win[:, 4:8, 0:768])], writes=[Wa.res])
    for k0 in range(0, 8, 2):
        S.dma(S.pool, [(Wv[:, k0:k0 + 2, 0:512], win[:, k0:k0 + 2, O_GV:O_GV + 512]),
                       (Wv[:, k0:k0 + 2, 512:1024], win[:, k0:k0 + 2, O_MV:O_MV + 512])], writes=[Wv.res])
        S.dma(S.pool, [(Wf[:, k0:k0 + 2, 0:512], win[:, k0:k0 + 2, O_GQ:O_GQ + 512]),
                       (Wf[:, k0:k0 + 2, 512:1536], win[:, k0:k0 + 2, O_MQ:O_MQ + 1024])], writes=[Wf.res])
    S.dma(S.pool, [(Wg[:, :, 0:32], win[:, :, O_GG:O_GG + 32]), (Wg[:, :, 32:48], win[:, :, O_MI:O_MI + 16])], writes=[Wg.res])
    W2p = self.sb("W2p", [32, 2, 256], F32)
    S.op(S.dve, lambda h: h.memset(W2p[:], 0.0), writes=[W2p.res])
    S.dma(S.sp, [(W2p[0:16, 0, :], self.gla_w2[l, 0]), (W2p[16:32, 1, :], self.gla_w2[l, 1])], writes=[W2p.res])
    negb = self.sb("negb", [128, 2, 2], F32)
    S.dma(S.sp, [(negb[:, d, :], self.gla_b[l, d, :].rearrange("(c p) -> p c", p=128)) for d in range(2)],
          writes=[negb.res], allow_slow_non_contiguous=True)
    S.op(S.dve, lambda h: h.tensor_scalar(out=negb[:], in0=negb[:], scalar1=-1.0, scalar2=None, op0=ALU.mult),
         reads=[negb.res], writes=[negb.res])
    gain = self.sb("gain", [128, 10, 64], F32)
    qn_src = self.attn_q_norm[l:l + 1, :]
    kn_src = self.attn_k_norm[l:l + 1, :]
    S.dma(S.sp, [(gain[:, 0:8, :], bass.AP(qn_src.tensor, qn_src.offset, [[0, 128], [0, 8], [1, 64]])),
                 (gain[:, 8:10, :], bass.AP(kn_src.tensor, kn_src.offset, [[0, 128], [0, 2], [1, 64]]))],
          writes=[gain.res])
    mask01 = self.sb("mask01", [128, 8, 64], F32)
    S.op(S.pool, lambda h: h.memset(mask01[:], 1.0), writes=[mask01.res])
    S.op(S.pool, lambda h: h.memset(mask01[:, :, 0:1], 0.0), writes=[mask01.res])
    self.junk = self.sb("junk", [128, D], BF16)

    xl = [self.sb(f"xl{i}", [128, D], F32) for i in range(3)]
    nb = [self.sb(f"nb{i}", [128, D], BF16) for i in range(2)]
    ss = [self.sb(f"ss{i}", [128, 1], F32) for i in range(2)]
    rs = [self.sb(f"rs{i}", [128, 1], F32) for i in range(2)]
    hTs = [self.sb(f"hT{i}", [128, 8, 512], BF16) for i in range(2)]
    sqt = self.sb("sqt", [128, 640], F32)
    ssh = self.sb("ssh", [128, 10], F32)
    rinv = self.sb("rinv", [128, 10], F32)
    qn = self.sb("qn", [128, 10, 64], F32)
    rt = [self.sb(f"rt{i}", [128, 10, 32], F32) for i in range(4)]
    cs_t = [self.sb(f"cst{i}", [128, 2, 32], F32) for i in range(3)]
    qr = [self.sb(f"qr{i}", [128, 10, 64], BF16) for i in range(2)]
    vb = [self.sb(f"vb{i}", [128, 128], BF16) for i in range(4)]
    vb2 = [self.sb(f"vb2{i}", [128, 512], BF16) for i in range(4)]
    QTs = self.sb("QTs", [64, 8, 512], BF16)
    KTs = self.sb("KTs", [64, 2, 512], BF16)
    ggT = self.sb("ggT", [32, 512], F32)
    gts = self.sb("gts", [16, 512], F32)
    ex = [self.sb(f"ex{i}", [128, 512], F32) for i in range(2)]
    csum = [self.sb(f"csum{i}", [128, 512], F32) for i in range(2)]
    eb = [[self.sb(f"eb{d}{c}", [128, 512], F32) for c in range(2)] for d in range(2)]
    enb = [[self.sb(f"enb{d}{c}", [128, 512], F32) for c in range(2)] for d in range(2)]
    ebl = [[self.sb(f"ebl{d}{c}", [128, 512], F32) for c in range(2)] for d in range(2)]
    fo = [self.sb(f"fo{i}", [128, 512], BF16) for i in range(8)]
    elcs = [self.sb(f"elc{i}", [128, 8], F32) for i in range(2)]
    pT = self.ps("pT", [128, D], BF16)
    pq = self.ps("pq", [128, 512])
    pkv = self.ps("pkv", [128, 256])
    pqt = self.ps("pqt", [64, 8, 128], BF16)
    pkt = self.ps("pkt", [64, 2, 128], BF16)
    pf = [self.ps(f"pf{i}", [128, 512]) for i in range(2)]
    pz = self.ps("pz", [128, 512])
    cnt = {"xl": 0, "pf": 0, "fo": 0, "qr": 0, "vb": 0, "vb2": 0, "ex": 0, "rt": 0, "cs": 0, "elc": 0}
    H2Tv = self.H2T[l].rearrange("(kc p) t -> p kc t", p=128)

    def nxt(key, n):
        v = cnt[key] % n
        cnt[key] += 1
        return v

    st_info = []
    for (tag, src, ntok, row, uoff, rope) in streams:
        A, tmp = self.adaln_cols(l, 1, row, f"f{tag}")
        st_info.append((A, tmp, src, uoff, rope))
    tiles = []
    for si, (tag, src, ntok, row, uoff, rope) in enumerate(streams):
        for t0 in range(0, ntok, 512):
            tiles.append((si, t0, min(512, ntok - t0)))

    def prep_load(k, s):
        si, t0, n = tiles[k]
        A, tmp, src, uoff, rope = st_info[si]
        i = nxt("xl", 3)
        S.dma(S.sp, [(xl[i][:], src[t0 + s * 128:t0 + (s + 1) * 128, :])], reads=[self.R(src.name)], writes=[xl[i].res])
        cst = None
        return i

    def rope_load(k, s):
        si, t0, n = tiles[k]
        A, tmp, src, uoff, rope = st_info[si]
        if not rope:
            return None
        cst = cs_t[nxt("cs", 3)]
        S.dma(S.sp, [(cst[:], self.rope_cs[t0 + s * 128:t0 + s * 128 + 128, :, :])], writes=[cst.res])
        return cst

    def prep_sub(k, s, i=None):
        si, t0, n = tiles[k]
        A, tmp, src, uoff, rope = st_info[si]
        hT = hTs[k % 2]
        if i is None:
            i = prep_load(k, s)
        self.norm_part(xl[i], nb[i % 2], ss[i % 2], rs[i % 2])
        self.shres = tmp.res
        self.transpose_part(nb[i % 2], pT, hT, s * 128, A, tmp[:, 0, :], [S.act, S.dve])

    def g_part(k):
        si, t0, n = tiles[k]
        A, tmp, src, uoff, rope = st_info[si]
        hT = hTs[k % 2]
        u0 = uoff + t0
        nch = n // 64
        for kc in range(8):
            S.op(S.pe, lambda h, kc=kc, n=n, hT=hT: h.matmul(pz[0:32, :n], Wg[:, kc, 0:32], hT[:, kc, :n], start=(kc == 0), stop=(kc == 7)),
                 reads=[hT.res, Wg.res], writes=[pz.res], inc=(kc == 7))
        S.op(S.act, lambda h, n=n: h.activation(out=ggT[:, :n], in_=pz[0:32, :n], func=AF.Copy), reads=[pz.res], writes=[ggT.res])
        for kc in range(8):
            S.op(S.pe, lambda h, kc=kc, n=n, hT=hT: h.matmul(pz[0:16, :n], Wg[:, kc, 32:48], hT[:, kc, :n], start=(kc == 0), stop=(kc == 7)),
                 reads=[hT.res, Wg.res], writes=[pz.res], inc=(kc == 7))
        S.op(S.act, lambda h, n=n: h.activation(out=gts[:, :n], in_=pz[0:16, :n], func=AF.Copy), reads=[pz.res], writes=[gts.res])
        S.dma(S.sp, [(self.GATES[l][:, u0:u0 + n], gts[:, :n])], reads=[gts.res], writes=[self.R(self.GATES[l].name)])
        for d in range(2):
            for c2 in range(2):
                S.op(S.pe, lambda h, d=d, c2=c2, n=n: h.matmul(pz[:, :n], W2p[:, d, c2 * 128:(c2 + 1) * 128], ggT[:, :n], start=True, stop=True),
                     reads=[W2p.res, ggT.res], writes=[pz.res])
                e_ = ex[nxt("ex", 2)]
                c_ = csum[(cnt["ex"]) % 2]
                S.op(S.act, lambda h, d=d, c2=c2, n=n, e_=e_: h.activation(out=e_[:, :n], in_=pz[:, :n], func=AF.Exp, scale=-1.0, bias=negb[:, d, c2:c2 + 1]),
                     reads=[pz.res, negb.res], writes=[e_.res])
                S.op(S.act, lambda h, n=n, e_=e_: h.activation(out=e_[:, :n], in_=e_[:, :n], func=AF.Ln, bias=1.0), reads=[e_.res], writes=[e_.res])
                m01 = mask01[:].rearrange("p a b -> p (a b)")[:, :n]
                if d == 0:
                    S.op(S.dve, lambda h, n=n, e_=e_, c_=c_, m01=m01: h.tensor_tensor_scan(out=c_[:, :n], data0=m01, data1=e_[:, :n], initial=0.0, op0=ALU.mult, op1=ALU.add),
                         reads=[e_.res, mask01.res], writes=[c_.res])
                    last = 63
                else:
                    S.op(S.dve, lambda h, n=n, e_=e_, c_=c_, m01=m01: h.tensor_tensor_scan(out=rev_last(c_[:, :n]), data0=m01, data1=rev_last(e_[:, :n]), initial=0.0,
                                                                                         op0=ALU.mult, op1=ALU.add),
                         reads=[e_.res, mask01.res], writes=[c_.res])
                    last = 0
                EB, ENB, EBL = eb[d][c2], enb[d][c2], ebl[d][c2]
                S.op(S.act, lambda h, n=n, c_=c_, EB=EB: h.activation(out=EB[:, :n], in_=c_[:, :n], func=AF.Exp, scale=-1.0 / 16), reads=[c_.res], writes=[EB.res])
                S.op(S.act, lambda h, n=n, c_=c_, ENB=ENB: h.activation(out=ENB[:, :n], in_=c_[:, :n], func=AF.Exp, scale=1.0 / 16), reads=[c_.res], writes=[ENB.res])
                c3 = c_[:, :n].rearrange("p (a b) -> p a b", b=64)
                S.op(S.pool, lambda h, n=n, c_=c_, c3=c3, last=last, nch=nch: h.tensor_tensor(out=c3, in0=c3, in1=c3[:, :, last:last + 1].to_broadcast([128, nch, 64]), op=ALU.subtract),
                     reads=[c_.res], writes=[c_.res])
                S.op(S.act, lambda h, n=n, c_=c_, EBL=EBL: h.activation(out=EBL[:, :n], in_=c_[:, :n], func=AF.Exp, scale=1.0 / 16), reads=[c_.res], writes=[EBL.res])
                ch0 = u0 // 64
                elc = elcs[nxt("elc", 2)]
                S.op(S.pool, lambda h, n=n, EB=EB, last=last, elc=elc, nch=nch: h.tensor_copy(
                    out=elc[:, :nch], in_=EB[:, :n].rearrange("p (a b) -> p a b", b=64)[:, :, last]),
                     reads=[EB.res], writes=[elc.res])
                S.dma(S.sp, [(self.EL[:, 2 * c2 + hh2, d, ch0:ch0 + nch], elc[hh2 * 64:(hh2 + 1) * 64, :nch]) for hh2 in range(2)],
                      reads=[elc.res], writes=[self.EL.res])

    def a_mm(k, s):
        hT = hTs[k % 2]
        c0 = s * 128
        for kc in range(8):
            S.op(S.pe, lambda h, kc=kc, c0=c0, hT=hT: h.matmul(pq[:], hT[:, kc, c0:c0 + 128], Wa[:, kc, 0:512], start=(kc == 0), stop=(kc == 7)),
                 reads=[hT.res, Wa.res], writes=[pq.res], inc=(kc == 7))
        for kc in range(8):
            S.op(S.pe, lambda h, kc=kc, c0=c0, hT=hT: h.matmul(pkv[:], hT[:, kc, c0:c0 + 128], Wa[:, kc, 512:768], start=(kc == 0), stop=(kc == 7)),
                 reads=[hT.res, Wa.res], writes=[pkv.res], inc=(kc == 7))

    def a_chain(k, s, cst=None):
        si, t0, n = tiles[k]
        A, tmp, src, uoff, rope = st_info[si]
        u0 = uoff + t0
        c0 = s * 128
        S.op(S.act, lambda h: h.activation(out=sqt[:, 0:512], in_=pq[:], func=AF.Square), reads=[pq.res], writes=[sqt.res])
        S.op(S.act, lambda h: h.activation(out=sqt[:, 512:640], in_=pkv[:, 0:128], func=AF.Square), reads=[pkv.res], writes=[sqt.res])
        vi = nxt("vb", 4)
        S.op(S.act, lambda h, vi=vi: h.activation(out=vb[vi][:], in_=pkv[:, 128:256], func=AF.Copy), reads=[pkv.res], writes=[vb[vi].res])
        S.dma(S.sp, [(self.VA[l][u0 + c0:u0 + c0 + 128, :], vb[vi][:])], reads=[vb[vi].res], writes=[self.R(self.VA[l].name)])
        S.op(S.dve, lambda h: h.tensor_reduce(out=ssh[:], in_=sqt[:].rearrange("p (a b) -> p a b", b=64), axis=AX.X, op=ALU.add),
             reads=[sqt.res], writes=[ssh.res])
        S.op(S.act, lambda h: h.activation(out=rinv[:], in_=ssh[:], func=AF.Sqrt, scale=1.0 / 64, bias=self.eps_col[:]),
             reads=[ssh.res, self.eps_col.res], writes=[rinv.res])
        S.op(S.dve, lambda h: h.reciprocal(out=rinv[:], in_=rinv[:]), reads=[rinv.res], writes=[rinv.res])
        S.op(S.dve, lambda h: h.tensor_tensor(out=qn[:, 0:8, :], in0=pq[:].rearrange("p (a b) -> p a b", b=64),
                                               in1=rinv[:, 0:8].unsqueeze(2).to_broadcast([128, 8, 64]), op=ALU.mult),
             reads=[pq.res, rinv.res], writes=[qn.res])
        S.op(S.dve, lambda h: h.tensor_tensor(out=qn[:, 8:10, :], in0=pkv[:, 0:128].rearrange("p (a b) -> p a b", b=64),
                                               in1=rinv[:, 8:10].unsqueeze(2).to_broadcast([128, 2, 64]), op=ALU.mult),
             reads=[pkv.res, rinv.res], writes=[qn.res])
        S.op(S.pool, lambda h: h.tensor_tensor(out=qn[:], in0=qn[:], in1=gain[:], op=ALU.mult),
             reads=[qn.res, gain.res], writes=[qn.res])
        q_ = qr[nxt("qr", 2)]
        if rope:
            cosb = cst[:, 0:1, :].to_broadcast([128, 10, 32])
            sinb = cst[:, 1:2, :].to_broadcast([128, 10, 32])
            x1 = qn[:, :, 0:32]
            x2 = qn[:, :, 32:64]
            r = [rt[nxt("rt", 4)] for _ in range(4)]
            S.op(S.pool, lambda h, r=r, cosb=cosb, x1=x1: h.tensor_tensor(out=r[0][:], in0=x1, in1=cosb, op=ALU.mult),
                 reads=[qn.res, cst.res], writes=[r[0].res])
            S.op(S.dve, lambda h, r=r, sinb=sinb, x2=x2: h.tensor_tensor(out=r[1][:], in0=x2, in1=sinb, op=ALU.mult),
                 reads=[qn.res, cst.res], writes=[r[1].res])
            S.op(S.dve, lambda h, r=r, q_=q_: h.tensor_tensor(out=q_[:, :, 0:32], in0=r[0][:], in1=r[1][:], op=ALU.subtract),
                 reads=[r[0].res, r[1].res], writes=[q_.res])
            S.op(S.pool, lambda h, r=r, sinb=sinb, x1=x1: h.tensor_tensor(out=r[2][:], in0=x1, in1=sinb, op=ALU.mult),
                 reads=[qn.res, cst.res], writes=[r[2].res])
            S.op(S.dve, lambda h, r=r, cosb=cosb, x2=x2: h.tensor_tensor(out=r[3][:], in0=x2, in1=cosb, op=ALU.mult),
                 reads=[qn.res, cst.res], writes=[r[3].res])
            S.op(S.dve, lambda h, r=r, q_=q_: h.tensor_tensor(out=q_[:, :, 32:64], in0=r[2][:], in1=r[3][:], op=ALU.add),
                 reads=[r[2].res, r[3].res], writes=[q_.res])
        else:
            S.op(S.dve, lambda h, q_=q_: h.tensor_copy(out=q_[:], in_=qn[:]), reads=[qn.res], writes=[q_.res])
        return q_

    def a_tr(q_, s):
        c0 = s * 128
        for hh in range(8):
            S.op(S.pe, lambda h, hh=hh, q_=q_: h.transpose(out=pqt[:, hh, :], in_=q_[:, hh, :], identity=self.ident[:]),
                 reads=[q_.res, self.ident.res], writes=[pqt.res], inc=(hh == 7))
        for hh in range(2):
            S.op(S.pe, lambda h, hh=hh, q_=q_: h.transpose(out=pkt[:, hh, :], in_=q_[:, 8 + hh, :], identity=self.ident[:]),
                 reads=[q_.res, self.ident.res], writes=[pkt.res], inc=(hh == 1))
        S.op(S.act, lambda h, c0=c0: h.activation(out=QTs[:, :, c0:c0 + 128], in_=pqt[:], func=AF.Copy), reads=[pqt.res], writes=[QTs.res])
        S.op(S.dve, lambda h, c0=c0: h.tensor_copy(out=KTs[:, :, c0:c0 + 128], in_=pkt[:]), reads=[pkt.res], writes=[KTs.res])

    def b_part(k, s):
        si, t0, n = tiles[k]
        uoff = st_info[si][3]
        u0 = uoff + t0
        hT = hTs[k % 2]
        c0 = s * 128
        for half in range(2):
            kk = nxt("pf", 2)
            for kc in range(8):
                S.op(S.pe, lambda h, kc=kc, c0=c0, kk=kk, half=half, hT=hT: h.matmul(pf[kk][:], hT[:, kc, c0:c0 + 128], Wv[:, kc, half * 512:(half + 1) * 512],
                                                                                  start=(kc == 0), stop=(kc == 7)),
                     reads=[hT.res, Wv.res], writes=[pf[kk].res], inc=(kc == 7))
            vi = nxt("vb2", 4)
            if half == 0:
                S.op(S.act, lambda h, kk=kk, vi=vi: h.activation(out=vb2[vi][:], in_=pf[kk][:], func=AF.Copy), reads=[pf[kk].res], writes=[vb2[vi].res])
            else:
                S.op(S.dve, lambda h, kk=kk, vi=vi: h.tensor_copy(out=vb2[vi][:], in_=pf[kk][:]), reads=[pf[kk].res], writes=[vb2[vi].res])
            dstv = self.GV[l] if half == 0 else self.MV[l]
            S.dma(S.sp, [(dstv[u0 + c0:u0 + c0 + 128, :], vb2[vi][:])], reads=[vb2[vi].res], writes=[self.R(dstv.name)])

    def c_part(k, fcs):
        si, t0, n = tiles[k]
        uoff = st_info[si][3]
        u0 = uoff + t0
        hT = hTs[k % 2]
        for fc in fcs:
            kk = nxt("pf", 2)
            for kc in range(8):
                S.op(S.pe, lambda h, kc=kc, fc=fc, kk=kk, n=n, hT=hT: h.matmul(pf[kk][:, :n], Wf[:, kc, fc * 128:(fc + 1) * 128], hT[:, kc, :n], start=(kc == 0), stop=(kc == 7)),
                     reads=[hT.res, Wf.res], writes=[pf[kk].res], inc=(kc == 7))
            if fc < 2:
                for d in range(2):
                    o_ = fo[nxt("fo", 8)]
                    S.op(S.dve, lambda h, kk=kk, n=n, d=d, fc=fc, o_=o_: h.scalar_tensor_tensor(out=o_[:, :n], in0=pf[kk][:, :n], scalar=0.125, in1=eb[d][fc][:, :n],
                                                                                           op0=ALU.mult, op1=ALU.mult),
                         reads=[pf[kk].res, eb[d][fc].res], writes=[o_.res])
                    S.dma(S.pool, [(self.QG[l][d, :, 2 * fc + hh2, u0:u0 + n], o_[hh2 * 64:(hh2 + 1) * 64, :n]) for hh2 in range(2)], reads=[o_.res], writes=[self.R(self.QG[l].name)])
            elif fc < 4:
                c2 = fc - 2
                for d in range(2):
                    o_ = fo[nxt("fo", 8)]
                    S.op(S.dve, lambda h, kk=kk, n=n, d=d, c2=c2, o_=o_: h.tensor_tensor(out=o_[:, :n], in0=pf[kk][:, :n], in1=enb[d][c2][:, :n], op=ALU.mult),
                         reads=[pf[kk].res, enb[d][c2].res], writes=[o_.res])
                    S.dma(S.pool, [(self.KG[l][d, :, 2 * c2 + hh2, u0:u0 + n], o_[hh2 * 64:(hh2 + 1) * 64, :n]) for hh2 in range(2)], reads=[o_.res], writes=[self.R(self.KG[l].name)])
                    o_ = fo[nxt("fo", 8)]
                    S.op(S.dve, lambda h, kk=kk, n=n, d=d, c2=c2, o_=o_: h.tensor_tensor(out=o_[:, :n], in0=pf[kk][:, :n], in1=ebl[d][c2][:, :n], op=ALU.mult),
                         reads=[pf[kk].res, ebl[d][c2].res], writes=[o_.res])
                    S.dma(S.pool, [(self.KH[l][d, :, 2 * c2 + hh2, u0:u0 + n], o_[hh2 * 64:(hh2 + 1) * 64, :n]) for hh2 in range(2)], reads=[o_.res], writes=[self.R(self.KH[l].name)])
            else:
                o_ = fo[nxt("fo", 8)]
                S.op(S.act, lambda h, kk=kk, n=n, o_=o_: h.activation(out=o_[:, :n], in_=pf[kk][:, :n], func=AF.Copy), reads=[pf[kk].res], writes=[o_.res])
                r0 = (fc - 4) * 128
                S.dma(S.pool, [(self.MQK[l][r0:r0 + 128, 2 + u0:2 + u0 + n], o_[:, :n])], reads=[o_.res], writes=[self.R(self.MQK[l].name)])

    for s in range(tiles[0][2] // 128):
        prep_sub(0, s)
    for k, (si, t0, n) in enumerate(tiles):
        A, tmp, src, uoff, rope_ = st_info[si]
        rope = rope_
        nt = n // 128
        u0 = uoff + t0
        hT = hTs[k % 2]
        S.dma(S.sp, [(H2Tv[:, :, u0:u0 + n], hT[:, :, :n])], reads=[hT.res], writes=[self.R(self.H2T[l].name)])
        g_part(k)
        order = [4, 5, 6, 7, 8, 9, 10, 11, 0, 1, 2, 3]
        per = (12 + nt - 1) // nt
        pend = None
        nxt_nt = tiles[k + 1][2] // 128 if k + 1 < len(tiles) else 0
        for s in range(nt):
            xi = prep_load(k + 1, s) if s < nxt_nt else None
            cst = rope_load(k, s)
            a_mm(k, s)
            q_ = a_chain(k, s, cst)
            if pend is not None:
                a_tr(*pend)
            pend = (q_, s)
            b_part(k, s)
            c_part(k, order[s * per:(s + 1) * per])
            if s < nxt_nt:
                prep_sub(k + 1, s, xi)
        a_tr(*pend)
        for s in range(nt, nxt_nt):
            prep_sub(k + 1, s)
        S.dma(S.sp, [(self.QT[l][:, :, u0:u0 + n], QTs[:, :, :n])], reads=[QTs.res], writes=[self.R(self.QT[l].name)])
        S.dma(S.sp, [(self.KT[l][:, :, u0:u0 + n], KTs[:, :, :n])], reads=[KTs.res], writes=[self.R(self.KT[l].name)])


Builder.phase_feat = phase_feat


def phase_attn(self, l, do_ctx):
    S = self.S
    T = self.T
    nbk = T // 128
    ones = self.sb("ones", [128, 128], F32)
    S.op(S.pool, lambda h: h.memset(ones[:], 1.0), writes=[ones.res])
    mP = self.sb("mP", [128, 4, 128], BF16)
    mN = self.sb("mN", [128, 4, 128], BF16)
    mtmp = self.sb("mtmp", [128, 128], F32)
    zer = self.sb("zer", [128, 128], F32)
    S.op(S.pool, lambda h: h.memset(zer[:], 0.0), writes=[zer.res])
    for (m_, sgn) in ((mP, 1), (mN, -1)):
        S.op(S.pool, lambda h, sgn=sgn: h.affine_select(out=mtmp[:], in_=zer[:], pattern=[[-sgn, 128]], compare_op=ALU.is_ge, fill=-30000.0,
                                                         base=0, channel_multiplier=sgn), reads=[zer.res], writes=[mtmp.res])
        S.op(S.pool, lambda h, m_=m_: h.tensor_copy(out=m_[:], in_=mtmp[:].unsqueeze(1).to_broadcast([128, 4, 128])), reads=[mtmp.res], writes=[m_.res])
    esk = self.sb("esk", [128, 2, 4, 128], F32)
    sk8 = self.sb("sk8", [128, 8], F32)
    S.dma(S.sp, [(sk8[64:65, :], self.attn_sink[l:l + 1, :])], writes=[sk8.res])
    S.op(S.act, lambda h: h.activation(out=sk8[64:65, :], in_=sk8[64:65, :], func=AF.Exp), reads=[sk8.res], writes=[sk8.res])
    S.op(S.dve, lambda h: h.tensor_copy(out=esk[64:65].rearrange("p g a b -> p (g a) b"), in_=sk8[64:65, :].unsqueeze(2).to_broadcast([1, 8, 128])),
         reads=[sk8.res], writes=[esk.res])
    KTc = self.sb("KTc", [64, 2, 256], BF16)
    S.dma(S.sp, [(KTc[:], self.KT[l][:, :, 0:256])], reads=[self.R(self.KT[l].name)], writes=[KTc.res])
    Vc = [self.sb(f"Vc{j}", [128, 2, 65], BF16) for j in range(2)]
    Vb = [self.sb(f"Vb{j}", [128, 2, 65], BF16) for j in range(4)]
    KTb = [self.sb(f"KTb{j}", [64, 2, 128], BF16) for j in range(4)]
    for v in Vc + Vb:
        S.op(S.pool, lambda h, v=v: h.memset(v[:], 1.0), writes=[v.res])
    for j in range(2):
        S.dma(S.sp, [(Vc[j][:, :, 0:64], self.VA[l][j * 128:(j + 1) * 128, :].rearrange("p (g d) -> p g d", d=64))],
              reads=[self.R(self.VA[l].name)], writes=[Vc[j].res])
    QTb = [self.sb(f"QTb{j}", [64, 8, 128], BF16) for j in range(2)]
    E = [self.sb(f"E{j}", [128, 4, 128], BF16) for j in range(4)]
    dn = [self.sb(f"dn{j}", [128, 512], F32) for j in range(2)]
    bcs = [self.sb(f"bcs{j}", [64, 512], F32) for j in range(2)]
    aT = [self.sb(f"aT{j}", [64, 4, 128], BF16) for j in range(2)]
    pS = [self.ps(f"pS{j}", [128, 512]) for j in range(3)]
    pO = [self.ps(f"pO{j}", [128, 512]) for j in range(2)]
    pB = [self.ps(f"pB{j}", [64, 512]) for j in range(2)]
    cnt = {}

    def nxt(key, n):
        v = cnt.get(key, 0)
        cnt[key] = v + 1
        return v % n

    def load_kb(m):
        i = m % 4
        u = LC + m * 128
        S.dma(S.sp, [(KTb[i][:], self.KT[l][:, :, u:u + 128])], reads=[self.R(self.KT[l].name)], writes=[KTb[i].res])
        S.dma(S.sp, [(Vb[i][:, :, 0:64], self.VA[l][u:u + 128, :].rearrange("p (g d) -> p g d", d=64))],
              reads=[self.R(self.VA[l].name)], writes=[Vb[i].res])

    pending = []

    def norm(g, po, u0):
        d_ = dn[nxt("dn", 2)]
        S.op(S.dve, lambda h, d_=d_, po=po, g=g: h.tensor_tensor(out=d_[64:65, :], in0=po[64:65, :], in1=esk[64:65, g].rearrange("p a b -> p (a b)"), op=ALU.add),
             reads=[po.res, esk.res], writes=[d_.res])
        S.op(S.dve, lambda h, d_=d_: h.reciprocal(out=d_[64:65, :], in_=d_[64:65, :]), reads=[d_.res], writes=[d_.res])
        pb = pB[nxt("pb", 2)]
        S.op(S.pe, lambda h, d_=d_, pb=pb: h.matmul(pb[:], ones[64:65, 0:64], d_[64:65, :], start=True, stop=True),
             reads=[d_.res, ones.res], writes=[pb.res])
        b_ = bcs[nxt("bcs", 2)]
        S.op(S.act, lambda h, b_=b_, pb=pb: h.activation(out=b_[:], in_=pb[:], func=AF.Copy), reads=[pb.res], writes=[b_.res])
        a_ = aT[nxt("aT", 2)]
        S.op(S.dve, lambda h, a_=a_, b_=b_, po=po: h.tensor_tensor(out=a_[:].rearrange("p a b -> p (a b)"), in0=po[0:64, :], in1=b_[:], op=ALU.mult),
             reads=[po.res, b_.res], writes=[a_.res])
        S.dma(S.pool, [(self.ATT[l][:, 4 * g:4 * g + 4, u0:u0 + 128], a_[:])], reads=[a_.res], writes=[self.R(self.ATT[l].name)])

    def qblock(u0, kbs):
        qi = nxt("q", 2)
        Q = QTb[qi]
        S.dma(S.sp, [(Q[:], self.QT[l][:, :, u0:u0 + 128])], reads=[self.R(self.QT[l].name)], writes=[Q.res])
        for g in range(2):
            po = pO[nxt("po", 2)]
            rhsq = Q[:, 4 * g:4 * g + 4, :].rearrange("p a b -> p (a b)")

            def score(idx, g=g, rhsq=rhsq):
                kt, vt, msk = kbs[idx]
                p = pS[nxt("ps", 3)]
                S.op(S.pe, lambda h, kt=kt, p=p, g=g, rhsq=rhsq, msk=msk: h.matmul(p[:], kt[0][:, g, kt[1]:kt[1] + 128], rhsq, start=True, stop=(msk is None)),
                     reads=[kt[0].res, Q.res], writes=[p.res], inc=(msk is None))
                if msk is not None:
                    S.op(S.pe, lambda h, p=p, msk=msk: h.matmul(p[:], self.ident[:], msk[:].rearrange("p a b -> p (a b)"), start=False, stop=True),
                         reads=[self.ident.res, msk.res], writes=[p.res])
                return p
            ps_list = [score(0)]
            for idx in range(len(kbs)):
                kt, vt, msk = kbs[idx]
                if idx + 1 < len(kbs):
                    ps_list.append(score(idx + 1))
                p = ps_list[idx]
                e = E[nxt("e", 4)]
                S.op(S.act, lambda h, p=p, e=e: h.activation(out=e[:].rearrange("p a b -> p (a b)"), in_=p[:], func=AF.Exp, scale=0.125),
                     reads=[p.res], writes=[e.res])
                S.op(S.pe, lambda h, e=e, vt=vt, po=po, idx=idx, g=g, kbs=kbs: h.matmul(po[0:65, :], vt[:, g, :], e[:].rearrange("p a b -> p (a b)"),
                                                                                      start=(idx == 0), stop=(idx == len(kbs) - 1)),
                     reads=[e.res, vt.res], writes=[po.res], inc=(idx == len(kbs) - 1))
            pending.append((g, po, u0))
            if len(pending) > 1:
                norm(*pending.pop(0))

    ckb = [((KTc, 0), Vc[0], None), ((KTc, 128), Vc[1], None)]
    if do_ctx:
        for n in range(2):
            qblock(n * 128, ckb)
    load_kb(0)
    for n in range(nbk):
        if n + 1 < nbk:
            load_kb(n + 1)
        kbs = []
        if n - 1 >= 0:
            kbs.append(((KTb[(n - 1) % 4], 0), Vb[(n - 1) % 4], mP))
        kbs.append(((KTb[n % 4], 0), Vb[n % 4], None))
        if n + 1 < nbk:
            kbs.append(((KTb[(n + 1) % 4], 0), Vb[(n + 1) % 4], mN))
        qblock(LC + n * 128, kbs + ckb)
    while pending:
        norm(*pending.pop(0))


Builder.phase_attn = phase_attn


def scan_groups(T):
    return [(0, LC)] + [(LC + t0, min(512, T - t0)) for t0 in range(0, T, 512)]


def scan_order(T, d):
    groups = scan_groups(T)
    order = []
    if d == 0:
        for gi, (u0, n) in enumerate(groups):
            for c in range(n // 64):
                order.append((gi, c))
    else:
        gis = [0] + list(range(len(groups) - 1, 0, -1))
        for gi in gis:
            u0, n = groups[gi]
            for c in range(n // 64 - 1, -1, -1):
                order.append((gi, c))
    return order


def phase_gla(self, l):
    S = self.S
    T = self.T
    EL = self.EL
    groups = scan_groups(T)
    ones = self.sb("ones", [64, 64], F32)
    S.op(S.pool, lambda h: h.memset(ones[:], 1.0), writes=[ones.res])
    mtmp = self.sb("mtmp", [64, 64], F32)
    msk = [self.sb(f"msk{d}", [64, 64], BF16) for d in range(2)]
    for d, sgn in ((0, -1), (1, 1)):
        S.op(S.pool, lambda h, sgn=sgn: h.affine_select(out=mtmp[:], in_=ones[:], pattern=[[-sgn, 64]], compare_op=ALU.is_ge, fill=0.0,
                                                         base=0, channel_multiplier=sgn), reads=[ones.res], writes=[mtmp.res])
        S.op(S.pool, lambda h, d=d: h.tensor_copy(out=msk[d][:], in_=mtmp[:]), reads=[mtmp.res], writes=[msk[d].res])
    Sf = [self.sb(f"Sf{d}", [64, 4, 128], F32) for d in range(2)]
    Sb = [self.sb(f"Sb{d}", [64, 4, 128], BF16) for d in range(2)]
    for d in range(2):
        S.op(S.pool, lambda h, d=d: h.memset(Sf[d][:], 0.0), writes=[Sf[d].res])
        S.op(S.pool, lambda h, d=d: h.memset(Sb[d][:], 0.0), writes=[Sb[d].res])
    qg = [[self.sb(f"qg{d}{i}", [64, 4, 512], BF16) for i in range(2)] for d in range(2)]
    kg = [[self.sb(f"kg{d}{i}", [64, 4, 512], BF16) for i in range(2)] for d in range(2)]
    kh = [[self.sb(f"kh{d}{i}", [64, 4, 512], BF16) for i in range(2)] for d in range(2)]
    vg = [[self.sb(f"vg{d}{i}", [64, 8, 512], BF16) for i in range(2)] for d in range(2)]
    am = [[self.sb(f"am{d}{i}", [64, 4, 64], BF16) for i in range(2)] for d in range(2)]
    kt = [[self.sb(f"kt{d}{i}", [64, 4, 64], BF16) for i in range(2)] for d in range(2)]
    ob = [[self.sb(f"ob{d}{i}", [64, 512], F32) for i in range(2)] for d in range(2)]
    pA = [self.ps(f"pA{d}", [64, 256]) for d in range(2)]
    pK = [self.ps(f"pK{d}", [64, 256], BF16) for d in range(2)]
    pO = [self.ps(f"pO{d}", [64, 512]) for d in range(2)]
    pN = [self.ps(f"pN{d}", [64, 512]) for d in range(2)]
    orders = [scan_order(T, d) for d in range(2)]
    nsteps = len(orders[0])
    gcount = [0, 0]
    cur = [None, None]

    def load_group(d, gi):
        i = gcount[d] % 2
        gcount[d] += 1
        u0, n = groups[gi]
        nch = n // 64
        for (dst, srcT) in ((qg[d][i], self.QG[l]), (kg[d][i], self.KG[l]), (kh[d][i], self.KH[l])):
            S.dma(S.sp, [(dst[:, :, :n], srcT[d, :, :, u0:u0 + n])], reads=[self.R(srcT.name)], writes=[dst.res])
        S.dma(S.sp, [(vg[d][i][:, :nch, :], self.GV[l][u0:u0 + n, :].rearrange("(c p) f -> p c f", p=64))], reads=[self.R(self.GV[l].name)],
              writes=[vg[d][i].res])
        return i

    ctxs = {}

    def stageA(step, d):
        gi, c = orders[d][step]
        if cur[d] is None or cur[d][0] != gi:
            cur[d] = (gi, load_group(d, gi))
        bi = cur[d][1]
        u0, n = groups[gi]
        o = c * 64
        Q, Kg, Kh, V = qg[d][bi], kg[d][bi], kh[d][bi], vg[d][bi]
        k2 = step % 2
        AM, KTt = am[d][k2], kt[d][k2]
        ctxs[(step, d)] = (Q, V, AM, KTt, u0, o, c)
        for hh in range(4):
            S.op(S.pe, lambda h, hh=hh, d=d, Kg=Kg, Q=Q, o=o: h.matmul(pA[d][:, hh * 64:(hh + 1) * 64], Kg[:, hh, o:o + 64], Q[:, hh, o:o + 64], start=True, stop=True),
                 reads=[Kg.res, Q.res], writes=[pA[d].res], inc=(hh == 3))
        for hh in range(4):
            S.op(S.pe, lambda h, hh=hh, d=d, Kh=Kh, o=o: h.transpose(out=pK[d][:, hh * 64:(hh + 1) * 64], in_=Kh[:, hh, o:o + 64], identity=self.ident[0:64, 0:64]),
                 reads=[Kh.res, self.ident.res], writes=[pK[d].res], inc=(hh == 3))
        S.op(S.dve, lambda h, d=d, AM=AM: h.tensor_tensor(out=AM[:], in0=pA[d][:].rearrange("p (a b) -> p a b", b=64),
                                                          in1=msk[d][:].unsqueeze(1).to_broadcast([64, 4, 64]), op=ALU.mult),
             reads=[pA[d].res, msk[d].res], writes=[AM.res])
        S.op(S.act, lambda h, d=d, KTt=KTt: h.activation(out=KTt[:].rearrange("p a b -> p (a b)"), in_=pK[d][:], func=AF.Copy), reads=[pK[d].res], writes=[KTt.res])

    def stageB(step, d):
        Q, V, AM, KTt, u0, o, c = ctxs.pop((step, d))
        chunk = (u0 + o) // 64
        OB = ob[d][step % 2]
        for hh in range(4):
            S.op(S.pe, lambda h, hh=hh, d=d, KTt=KTt, V=V, c=c: h.matmul(pN[d][:, hh * 128:(hh + 1) * 128], KTt[:, hh, :], V[:, c, hh * 128:(hh + 1) * 128], start=True, stop=True),
                 reads=[KTt.res, V.res], writes=[pN[d].res], inc=(hh == 3))
        for hh in range(4):
            S.op(S.pe, lambda h, hh=hh, d=d, AM=AM, V=V, c=c: h.matmul(pO[d][:, hh * 128:(hh + 1) * 128], AM[:, hh, :], V[:, c, hh * 128:(hh + 1) * 128], start=True, stop=False),
                 reads=[AM.res, V.res], writes=[pO[d].res], inc=False)
            S.op(S.pe, lambda h, hh=hh, d=d, Q=Q, o=o: h.matmul(pO[d][:, hh * 128:(hh + 1) * 128], Q[:, hh, o:o + 64], Sb[d][:, hh, :], start=False, stop=True),
                 reads=[Q.res, Sb[d].res], writes=[pO[d].res], inc=(hh == 3))
        S.op(S.act, lambda h, d=d, OB=OB: h.activation(out=OB[:], in_=pO[d][:], func=AF.Copy), reads=[pO[d].res], writes=[OB.res])
        S.dma(S.sp, [(self.OG[l][d, u0 + o:u0 + o + 64, :], OB[:])], reads=[OB.res], writes=[self.R(self.OG[l].name)])
        S.op(S.dve, lambda h, d=d, chunk=chunk: h.tensor_tensor(out=Sf[d][:], in0=Sf[d][:], in1=EL[:, :, d, chunk:chunk + 1].to_broadcast([64, 4, 128]), op=ALU.mult),
             reads=[Sf[d].res, self.EL.res], writes=[Sf[d].res])
        S.op(S.dve, lambda h, d=d: h.tensor_tensor(out=Sf[d][:].rearrange("p a b -> p (a b)"), in0=Sf[d][:].rearrange("p a b -> p (a b)"), in1=pN[d][:], op=ALU.add),
             reads=[Sf[d].res, pN[d].res], writes=[Sf[d].res])
        S.op(S.act, lambda h, d=d: h.activation(out=Sb[d][:], in_=Sf[d][:], func=AF.Copy), reads=[Sf[d].res], writes=[Sb[d].res])

    for d in range(2):
        stageA(0, d)
    for step in range(nsteps):
        if step + 1 < nsteps:
            for d in range(2):
                stageA(step + 1, d)
        for d in range(2):
            stageB(step, d)


Builder.phase_gla = phase_gla


LN_KS = float(-0.5 * np.log(128.0))


def phase_ml_gates(self, l):
    S = self.S
    sel = self.sel
    DEC = self.DEC
    T = self.T
    TT = self.TT
    nch = TT // 64
    bA = self.sb("bA", [4, TT], F32)
    bL = self.sb("bL", [4, TT], F32)
    bC = self.sb("bC", [4, TT], F32)
    bG = self.sb("bG", [4, TT], F32)
    bX = self.sb("bX", [4, TT], F32)
    onesr = self.sb("onesr", [4, TT], BF16)
    S.op(S.pool, lambda h: h.memset(onesr[:], 1.0), writes=[onesr.res])
    gl = self.sb("gl", [4, nch], F32)
    gp = self.sb("gp", [4, nch], F32)
    dd = self.sb("dd", [4, nch], F32)
    ibc = self.sb("ibc", [4, 2], F32)
    pD = self.ps("pD", [128, 512])
    for d in range(2):
        S.dma(S.sp, [(bA[:], self.GATES[l][d * 4:(d + 1) * 4, :])], reads=[self.R(self.GATES[l].name)], writes=[bA.res])
        S.dma(S.sp, [(bL[:], self.GATES[l][8 + d * 4:8 + (d + 1) * 4, :])], reads=[self.R(self.GATES[l].name)], writes=[bL.res])
        S.dma(S.sp, [(ibc[:, 0:1], self.mlstm_ib[l, d, :].rearrange("(h o) -> h o", o=1)), (ibc[:, 1:2], self.mlstm_fb[l, d, :].rearrange("(h o) -> h o", o=1))],
              writes=[ibc.res])
        S.op(S.dve, lambda h: h.tensor_scalar(out=ibc[:, 1:2], in0=ibc[:, 1:2], scalar1=-1.0, scalar2=None, op0=ALU.mult), reads=[ibc.res], writes=[ibc.res])
        S.op(S.act, lambda h: h.activation(out=bL[:], in_=bL[:], func=AF.Exp, scale=-1.0, bias=ibc[:, 1:2]), reads=[bL.res, ibc.res], writes=[bL.res])
        S.op(S.act, lambda h: h.activation(out=bL[:], in_=bL[:], func=AF.Ln, bias=1.0), reads=[bL.res], writes=[bL.res])

        def scan(out, src, op1):
            if d == 0:
                S.op(S.dve, lambda h: h.tensor_tensor_scan(out=out[:], data0=onesr[:], data1=src[:], initial=0.0, op0=ALU.mult, op1=op1),
                     reads=[src.res, onesr.res], writes=[out.res])
            else:
                S.op(S.dve, lambda h: h.tensor_tensor_scan(out=rev_last(out[:, 0:LC]), data0=onesr[:, 0:LC], data1=rev_last(src[:, 0:LC]), initial=0.0, op0=ALU.mult, op1=op1),
                     reads=[src.res, onesr.res], writes=[out.res])
                S.op(S.dve, lambda h: h.tensor_tensor_scan(out=rev_last(out[:, LC:TT]), data0=onesr[:, LC:TT], data1=rev_last(src[:, LC:TT]), initial=out[:, 0:1], op0=ALU.mult, op1=op1),
                     reads=[src.res, onesr.res, out.res], writes=[out.res])
        scan(bC, bL, ALU.add)
        S.op(S.dve, lambda h: h.scalar_tensor_tensor(out=bA[:], in0=bA[:], scalar=ibc[:, 0:1], in1=bC[:], op0=ALU.add, op1=ALU.add),
             reads=[bA.res, ibc.res, bC.res], writes=[bA.res])
        scan(bG, bA, ALU.max)
        G3 = bG[:].rearrange("p (c b) -> p c b", b=64)
        lastpos = 63 if d == 0 else 0
        S.op(S.dve, lambda h, lastpos=lastpos, G3=G3: h.tensor_copy(out=gl[:], in_=G3[:, :, lastpos]), reads=[bG.res], writes=[gl.res])
        S.op(S.dve, lambda h: h.memset(gp[:], 0.0), writes=[gp.res])
        if d == 0:
            S.op(S.dve, lambda h: h.tensor_copy(out=gp[:, 1:nch], in_=gl[:, 0:nch - 1]), reads=[gl.res], writes=[gp.res])
        else:
            S.op(S.dve, lambda h: h.tensor_copy(out=gp[:, 0:3], in_=gl[:, 1:4]), reads=[gl.res], writes=[gp.res])
            S.op(S.dve, lambda h: h.tensor_copy(out=gp[:, 4:nch - 1], in_=gl[:, 5:nch]), reads=[gl.res], writes=[gp.res])
            S.op(S.dve, lambda h: h.tensor_copy(out=gp[:, nch - 1:nch], in_=gl[:, 0:1]), reads=[gl.res], writes=[gp.res])
        L3 = bL[:].rearrange("p (c b) -> p c b", b=64)
        X3 = bX[:].rearrange("p (c b) -> p c b", b=64)
        A3 = bA[:].rearrange("p (c b) -> p c b", b=64)
        S.op(S.dve, lambda h, L3=L3, G3=G3: h.tensor_tensor(out=L3, in0=gp[:].unsqueeze(2).to_broadcast([4, nch, 64]), in1=G3, op=ALU.subtract),
             reads=[gp.res, bG.res], writes=[bL.res])
        S.op(S.act, lambda h: h.activation(out=bL[:], in_=bL[:], func=AF.Exp), reads=[bL.res], writes=[bL.res])
        S.op(S.dve, lambda h: h.tensor_tensor(out=bC[:], in0=bC[:], in1=bG[:], op=ALU.subtract), reads=[bC.res, bG.res], writes=[bC.res])
        S.op(S.act, lambda h: h.activation(out=bC[:], in_=bC[:], func=AF.Exp), reads=[bC.res], writes=[bC.res])
        S.op(S.dve, lambda h, X3=X3, A3=A3: h.tensor_tensor(out=X3, in0=A3, in1=gl[:].unsqueeze(2).to_broadcast([4, nch, 64]), op=ALU.subtract),
             reads=[bA.res, gl.res], writes=[bX.res])
        S.op(S.dve, lambda h: h.tensor_scalar(out=bX[:], in0=bX[:], scalar1=LN_KS, scalar2=None, op0=ALU.add), reads=[bX.res], writes=[bX.res])
        S.op(S.act, lambda h: h.activation(out=bX[:], in_=bX[:], func=AF.Exp), reads=[bX.res], writes=[bX.res])
        S.op(S.dve, lambda h: h.tensor_tensor(out=dd[:], in0=gp[:], in1=gl[:], op=ALU.subtract), reads=[gp.res, gl.res], writes=[dd.res])
        S.op(S.act, lambda h: h.activation(out=dd[:], in_=dd[:], func=AF.Exp), reads=[dd.res], writes=[dd.res])
        for hh in range(4):
            S.op(S.pe, lambda h, hh=hh: h.matmul(pD[:, :nch], sel[:, hh, :], dd[:], start=True, stop=True), reads=[self.sel.res, dd.res], writes=[pD.res])
            S.op(S.dve, lambda h, hh=hh, d=d: h.tensor_copy(out=DEC[:, d, hh, :], in_=pD[:, :nch]), reads=[pD.res], writes=[self.DEC.res])
        for qi, buf in enumerate((bA, bG, bL, bC, bX)):
            S.dma(S.sp, [(self.MROWS[l][d, qi, :, :], buf[:])], reads=[buf.res], writes=[self.R(self.MROWS[l].name)])


def phase_ml_conv(self, l):
    S = self.S
    T = self.T
    TT = self.TT
    wcol = self.sb("wcol", [128, 8, 5], F32)
    cb = self.sb("cb", [128, 8], F32)
    S.dma(S.sp, [(wcol[:, :, k], self.conv_w[l, k, :].rearrange("(fc p) -> p fc", p=128)) for k in range(5)], writes=[wcol.res], allow_slow_non_contiguous=True)
    S.dma(S.sp, [(cb[:], self.conv_b[l, :].rearrange("(fc p) -> p fc", p=128))], writes=[cb.res], allow_slow_non_contiguous=True)
    diagw = self.sb("diagw", [128, 8, 5, 128], BF16)
    for fc in range(8):
        for k in range(5):
            e = S.dve if (fc * 5 + k) % 2 == 0 else S.pool
            S.op(e, lambda h, fc=fc, k=k: h.tensor_scalar(out=diagw[:, fc, k, :], in0=self.identf[:], scalar1=wcol[:, fc, k:k + 1], scalar2=None, op0=ALU.mult),
                 reads=[self.identf.res, wcol.res], writes=[diagw.res])
    xq = [self.sb(f"xq{i}", [128, 8, 516], BF16) for i in range(2)]
    oc = [self.sb(f"oc{i}", [128, 512], BF16) for i in range(3)]
    pc = [self.ps(f"pc{i}", [128, 512]) for i in range(2)]
    MQKv = self.MQK[l].rearrange("(c p) t -> p c t", p=128)
    k2 = 0
    for gi, (u0, n) in enumerate(scan_groups(T)):
        X = xq[gi % 2]
        S.dma(S.sp, [(X[:, 0:4, 0:n + 4], MQKv[:, 0:4, u0:u0 + n + 4]), (X[:, 4:8, 0:n + 4], MQKv[:, 4:8, u0:u0 + n + 4])], reads=[self.R(self.MQK[l].name)], writes=[X.res])
        if u0 == 0 or u0 == LC:
            S.op(S.pool, lambda h, X=X: h.memset(X[:, :, 0:2], 0.0), writes=[X.res])
        if u0 + n == LC or u0 + n == TT:
            S.op(S.pool, lambda h, X=X, n=n: h.memset(X[:, :, n + 2:n + 4], 0.0), writes=[X.res])
        for fc in range(8):
            p = pc[k2 % 2]
            o_ = oc[k2 % 3]
            k2 += 1
            for k in range(5):
                S.op(S.pe, lambda h, fc=fc, k=k, p=p, X=X, n=n: h.matmul(p[:, :n], diagw[:, fc, k, :], X[:, fc, k:k + n], start=(k == 0), stop=(k == 4)),
                     reads=[diagw.res, X.res], writes=[p.res], inc=(k == 4))
            S.op(S.act, lambda h, fc=fc, p=p, o_=o_, n=n: h.activation(out=o_[:, :n], in_=p[:, :n], func=AF.Silu, bias=cb[:, fc:fc + 1]), reads=[p.res, cb.res], writes=[o_.res])
            S.dma(S.pool, [(self.MQC[l][fc * 128:(fc + 1) * 128, u0:u0 + n], o_[:, :n])], reads=[o_.res], writes=[self.R(self.MQC[l].name)])


def phase_ml_scan(self, l):
    S = self.S
    T = self.T
    sel = self.sel
    DEC = self.DEC
    groups = scan_groups(T)
    cfill = self.sb("cfill", [64, 64], F32)
    S.op(S.pool, lambda h: h.memset(cfill[:], LN_KS), writes=[cfill.res])
    mb = [self.sb(f"mb{d}", [64, 64], F32) for d in range(2)]
    for d, sgn in ((0, -1), (1, 1)):
        S.op(S.pool, lambda h, sgn=sgn, d=d: h.affine_select(out=mb[d][:], in_=cfill[:], pattern=[[-sgn, 64]], compare_op=ALU.is_ge, fill=-30000.0,
                                                              base=0, channel_multiplier=sgn), reads=[cfill.res], writes=[mb[d].res])
    negones = self.sb("negones", [4, 128], F32)
    S.op(S.pool, lambda h: h.memset(negones[:], -1.0), writes=[negones.res])
    posones = self.sb("posones", [4, 128], F32)
    S.op(S.pool, lambda h: h.memset(posones[:], 1.0), writes=[posones.res])
    mbr = [self.sb(f"mbr{d}", [64, 4, 64], F32) for d in range(2)]
    for d in range(2):
        S.op(S.pool, lambda h, d=d: h.tensor_copy(out=mbr[d][:], in_=mb[d][:].unsqueeze(1).to_broadcast([64, 4, 64])), reads=[mb[d].res], writes=[mbr[d].res])
    Dg = [self.sb(f"Dg{i}", [4, 3, 4, 512], F32) for i in range(2)]
    DgH = [self.sb(f"DgH{i}", [4, 2, 4, 512], BF16) for i in range(2)]
    DgL = [self.sb(f"DgL{i}", [4, 2, 4, 512], BF16) for i in range(2)]
    WB = [self.sb(f"WB{i}", [128, 2, 4, 512], F32) for i in range(2)]
    posb = self.sb("posb", [4, 128], BF16)
    S.op(S.pool, lambda h: h.memset(posb[:], 1.0), writes=[posb.res])
    Cf = self.sb("Cf", [128, 4, 129], F32)
    Cb = self.sb("Cb", [128, 4, 129], BF16)
    qk = [self.sb(f"qk{i}", [128, 8, 512], BF16) for i in range(2)]
    vg = [self.sb(f"vgm{i}", [64, 8, 4, 129], BF16) for i in range(2)]
    rows = [self.sb(f"rows{i}", [4, 5, 512], F32) for i in range(2)]
    for v in vg:
        S.op(S.pool, lambda h, v=v: h.memset(v[:], 1.0), writes=[v.res])
    wT = [self.sb(f"wT{i}", [64, 256], F32) for i in range(3)]
    sT = [self.sb(f"sT{i}", [64, 4, 64], BF16) for i in range(3)]
    qks = [self.sb(f"qks{i}", [128, 8, 64], BF16) for i in range(3)]
    khat = [self.sb(f"khat{i}", [64, 4, 128], BF16) for i in range(3)]
    enm = [self.sb(f"enm{i}", [64, 4], F32) for i in range(3)]
    rr = [self.sb(f"rr{i}", [64, 4], F32) for i in range(3)]
    ho = [self.sb(f"ho{i}", [64, 4, 128], F32) for i in range(3)]
    pWS = self.ps("pWS", [64, 512])
    pB = self.ps("pB", [128, 8, 64])
    pK = self.ps("pK", [64, 4, 128], BF16)
    pO = self.ps("pO", [64, 1024])
    pN = self.ps("pN", [128, 1024])
    pO3 = pO[:].rearrange("p (h e) -> p h e", e=256)
    pN3 = pN[:].rearrange("p (h e) -> p h e", e=256)
    gcount = [0]

    def load_group(d, gi):
        i = gcount[0] % 2
        gcount[0] += 1
        u0, n = groups[gi]
        nchg = n // 64
        S.dma(S.sp, [(qk[i][:, 0:4, :n], self.MQC[l].rearrange("(c p) t -> p c t", p=128)[:, 0:4, u0:u0 + n]),
                     (qk[i][:, 4:8, :n], self.MQC[l].rearrange("(c p) t -> p c t", p=128)[:, 4:8, u0:u0 + n])], reads=[self.R(self.MQC[l].name)], writes=[qk[i].res])
        S.dma(S.sp, [(vg[i][:, c, :, 0:128], self.MV[l][u0 + c * 64:u0 + (c + 1) * 64, :].rearrange("p (h e) -> p h e", e=128)) for c in range(nchg)],
              reads=[self.R(self.MV[l].name)], writes=[vg[i].res])
        S.dma(S.sp, [(rows[i][:, :, :n], self.MROWS[l][d, :, :, u0:u0 + n].rearrange("q h t -> h q t"))], reads=[self.R(self.MROWS[l].name)], writes=[rows[i].res])
        for qi, qs in enumerate((2, 4)):
            srcr = self.MROWS[l][d, qs, :, u0:u0 + n]
            S.dma(S.sp, [(WB[i][:, qi, :, :n], bass.AP(srcr.tensor, srcr.offset, [[0, 128]] + [list(x) for x in srcr.ap]))],
                  reads=[self.R(self.MROWS[l].name)], writes=[WB[i].res])
        for qd, qs in enumerate((1,)):
            S.op(S.pool, lambda h, i=i, qd=qd, qs=qs, n=n: h.tensor_tensor(out=Dg[i][:, qd, :, :n], in0=rows[i][:, qs:qs + 1, :n].to_broadcast([4, 4, n]),
                                                                       in1=self.identf[0:4, 0:4].unsqueeze(2).to_broadcast([4, 4, n]), op=ALU.mult),
                 reads=[rows[i].res, self.identf.res], writes=[Dg[i].res])
        return i

    for d in range(2):
        S.op(S.pool, lambda h: h.memset(Cf[:], 0.0), writes=[Cf.res])
        S.op(S.pool, lambda h: h.memset(Cb[:], 0.0), writes=[Cb.res])
        order = scan_order(T, d)
        cur = None
        info = []
        for (gi, c) in order:
            if cur is None or cur[0] != gi:
                cur = (gi, None)
            info.append((gi, c))
        bufof = {}

        def stageA(step):
            gi, c = order[step]
            if gi not in bufof:
                bufof.clear()
                bufof[gi] = load_group(d, gi)
            bi = bufof[gi]
            o = c * 64
            k2 = step % 3
            QK, R_ = qk[bi], rows[bi]
            DG = Dg[bi]
            S.op(S.pe, lambda h, R_=R_, o=o: h.matmul(pWS[:, 0:256], R_[:, 0, o:o + 64], sel[:, :, 0:64], start=True, stop=False),
                 reads=[R_.res, sel.res], writes=[pWS.res], inc=False)
            S.op(S.pe, lambda h, DG=DG, o=o: h.matmul(pWS[:, 0:256], negones[:, 0:64], DG[:, 0, :, o:o + 64], start=False, stop=False),
                 reads=[DG.res, negones.res], writes=[pWS.res], inc=False)
            S.op(S.pe, lambda h, d=d: h.matmul(pWS[:, 0:256], self.identf[0:64, 0:64], mbr[d][:], start=False, stop=True),
                 reads=[self.identf.res, mbr[d].res], writes=[pWS.res], inc=False)
            for hh in range(4):
                S.op(S.pe, lambda h, hh=hh, QK=QK, o=o: h.matmul(pWS[:, 256 + hh * 64:256 + (hh + 1) * 64], QK[:, 4 + hh, o:o + 64], QK[:, hh, o:o + 64], start=True, stop=True),
                     reads=[QK.res], writes=[pWS.res], inc=(hh == 3))
            S.op(S.act, lambda h, k2=k2: h.activation(out=wT[k2][:], in_=pWS[:, 0:256], func=AF.Exp), reads=[pWS.res], writes=[wT[k2].res])
            S.op(S.dve, lambda h, k2=k2: h.tensor_tensor(out=sT[k2][:].rearrange("p a b -> p (a b)"), in0=pWS[:, 256:512], in1=wT[k2][:], op=ALU.mult),
                 reads=[pWS.res, wT[k2].res], writes=[sT[k2].res])
            S.op(S.dve, lambda h, k2=k2, QK=QK, o=o, bi=bi: h.tensor_tensor(out=qks[k2][:], in0=QK[:, :, o:o + 64], in1=WB[bi][:, :, :, o:o + 64].rearrange("p a b c -> p (a b) c"), op=ALU.mult),
                 reads=[QK.res, WB[bi].res], writes=[qks[k2].res])

        def stageA2(step):
            k2 = step % 3
            for hh in range(4):
                S.op(S.pe, lambda h, hh=hh, k2=k2: h.transpose(out=pK[:, hh, :], in_=qks[k2][:, 4 + hh, :], identity=self.ident[:]),
                     reads=[qks[k2].res, self.ident.res], writes=[pK.res], inc=(hh == 3))
            S.op(S.act, lambda h, k2=k2: h.activation(out=khat[k2][:], in_=pK[:], func=AF.Copy), reads=[pK.res], writes=[khat[k2].res])

        def stageB(step):
            gi, c = order[step]
            u0, n = groups[gi]
            o = c * 64
            chunk = (u0 + o) // 64
            k2 = step % 3
            V, R_ = vgbuf[step], rowbuf[step]
            for hh in range(4):
                S.op(S.pe, lambda h, hh=hh, k2=k2, V=V, c=c: h.matmul(pN[:, hh * 256:hh * 256 + 129], khat[k2][:, hh, :], V[:, c, hh, :], start=True, stop=True),
                     reads=[khat[k2].res, V.res], writes=[pN.res], inc=(hh == 3))
            S.op(S.pe, lambda h, R_=R_, o=o: h.matmul(pO[:, 200:204], R_[:, 3, o:o + 64], self.identf[0:4, 0:4], start=True, stop=True),
                 reads=[R_.res, self.identf.res], writes=[pO.res], inc=False)
            for hh in range(4):
                S.op(S.pe, lambda h, hh=hh, k2=k2, V=V, c=c: h.matmul(pO[:, hh * 256:hh * 256 + 129], sT[k2][:, hh, :], V[:, c, hh, :], start=True, stop=False),
                     reads=[sT[k2].res, V.res], writes=[pO.res], inc=False)
                S.op(S.pe, lambda h, hh=hh, k2=k2: h.matmul(pO[:, hh * 256:hh * 256 + 129], qks[k2][:, hh, :], Cb[:, hh, :], start=False, stop=True),
                     reads=[qks[k2].res, Cb.res], writes=[pO.res], inc=(hh == 3))
            S.op(S.act, lambda h, k2=k2: h.activation(out=enm[k2][:], in_=pO[:, 200:204], func=AF.Copy), reads=[pO.res], writes=[enm[k2].res])
            S.op(S.act, lambda h, k2=k2: h.activation(out=rr[k2][:], in_=pO3[:, :, 128], func=AF.Abs), reads=[pO.res], writes=[rr[k2].res])
            S.op(S.pool, lambda h, d=d, chunk=chunk: h.tensor_tensor(out=Cf[:], in0=Cf[:], in1=DEC[:, d, :, chunk:chunk + 1].to_broadcast([128, 4, 129]), op=ALU.mult),
                 reads=[Cf.res, DEC.res], writes=[Cf.res])
            S.op(S.dve, lambda h: h.tensor_tensor(out=Cf[:], in0=Cf[:], in1=pN3[:, :, 0:129], op=ALU.add), reads=[Cf.res, pN.res], writes=[Cf.res])
            S.op(S.act, lambda h: h.activation(out=Cb[:], in_=Cf[:], func=AF.Copy), reads=[Cf.res], writes=[Cb.res])
            S.op(S.dve, lambda h, k2=k2: h.tensor_tensor(out=rr[k2][:], in0=rr[k2][:], in1=enm[k2][:], op=ALU.max), reads=[rr[k2].res, enm[k2].res], writes=[rr[k2].res])
            S.op(S.dve, lambda h, k2=k2: h.reciprocal(out=rr[k2][:], in_=rr[k2][:]), reads=[rr[k2].res], writes=[rr[k2].res])
            S.op(S.dve, lambda h, k2=k2: h.tensor_tensor(out=ho[k2][:], in0=pO3[:, :, 0:128], in1=rr[k2][:].unsqueeze(2).to_broadcast([64, 4, 128]), op=ALU.mult),
                 reads=[pO.res, rr[k2].res], writes=[ho[k2].res])
            S.dma(S.pool, [(self.OM[l][d, u0 + o:u0 + o + 64, :], ho[k2][:].rearrange("p a b -> p (a b)"))], reads=[ho[k2].res], writes=[self.R(self.OM[l].name)])

        vgbuf = {}
        rowbuf = {}

        def A(step):
            stageA(step)
            gi, c = order[step]
            vgbuf[step] = vg[bufof[gi]]
            rowbuf[step] = rows[bufof[gi]]
        A(0)
        if len(order) > 1:
            A(1)
        stageA2(0)
        for step in range(len(order)):
            if step + 2 < len(order):
                A(step + 2)
            if step + 1 < len(order):
                stageA2(step + 1)
            stageB(step)


Builder.phase_ml_gates = phase_ml_gates
Builder.phase_ml_conv = phase_ml_conv
Builder.phase_ml_scan = phase_ml_scan


def phase_merge(self, l, streams):
    S = self.S
    win = self.w_in[l].rearrange("(kc p) n -> p kc n", p=128)
    Wm = self.sb("Wm", [128, 8, 4096], BF16)
    Wmr = [Res(f"Wm{k}") for k in range(4)]
    for ki, k0 in enumerate(range(0, 8, 2)):
        S.dma(S.pool, [(Wm[:, k0:k0 + 2, 0:512], win[:, k0:k0 + 2, O_GR:O_GR + 512]),
                       (Wm[:, k0:k0 + 2, 512:1024], win[:, k0:k0 + 2, O_MO:O_MO + 512]),
                       (Wm[:, k0:k0 + 2, 1024:4096], win[:, k0:k0 + 2, O_SA:O_SA + 3072])], writes=[Wmr[ki]])
    Woa = self.sb("Woa", [64, 8, D], BF16)
    Wog = self.sb("Wog", [128, 4, D], BF16)
    Wom = self.sb("Wom", [128, 4, D], BF16)
    Wo = self.sb("Wo", [128, 8, D], BF16)
    S.dma(S.pool, [(Woa[:], self.w_out_attn[l].rearrange("(h p) n -> p h n", p=64))], writes=[Woa.res])
    S.dma(S.pool, [(Wog[:], self.w_out_gla[l].rearrange("(c p) n -> p c n", p=128))], writes=[Wog.res])
    S.dma(S.pool, [(Wom[:], self.w_out_mlstm[l].rearrange("(c p) n -> p c n", p=128))], writes=[Wom.res])
    S.dma(S.pool, [(Wo[:, 0:4, :], self.w_o[l].rearrange("(c p) n -> p c n", p=128)[:, 0:4, :]),
                   (Wo[:, 4:8, :], self.w_o[l].rearrange("(c p) n -> p c n", p=128)[:, 4:8, :])], writes=[Wo.res])
    gains = self.sb("gains", [128, 2, 128], F32)
    S.dma(S.sp, [(gains[:, 0, :], bcast_rows(self.gla_norm[l:l + 1, :], 128)), (gains[:, 1, :], bcast_rows(self.mlstm_norm[l:l + 1, :], 128))], writes=[gains.res])
    eps_col = self.eps_col
    hT = [self.sb(f"mhT{i}", [128, 8, 128], BF16) for i in range(2)]
    aTt = [self.sb(f"maT{i}", [64, 8, 128], BF16) for i in range(2)]
    og = [self.sb(f"mog{i}", [128, 2, 512], F32) for i in range(2)]
    om = [self.sb(f"mom{i}", [128, 2, 512], F32) for i in range(2)]
    xr = [self.sb(f"mxr{i}", [128, D], F32) for i in range(2)]
    gts = [self.sb(f"mgt{i}", [128, 8, 512], F32) for i in range(2)]
    sq = self.sb("msq", [128, 512], F32)
    ssqs = [self.sb(f"mssq{i}", [128, 2, 4], F32) for i in range(2)]
    bn = [self.sb(f"mbn{i}", [128, 512], F32) for i in range(2)]
    bbs = [[self.sb(f"mbb{i}{j}", [128, 512], BF16) for j in range(2)] for i in range(2)]
    bTs = [[self.sb(f"mbT{i}{j}", [128, 4, 128], BF16) for j in range(2)] for i in range(2)]
    yb = self.sb("myb", [128, D], BF16)
    yT = self.sb("myT", [128, 8, 128], BF16)
    t1 = [self.sb(f"mt1{i}", [128, 512], F32) for i in range(3)]
    G5 = self.sb("mG5", [128, D], F32)
    pg = [self.ps(f"mpg{i}", [128, 512]) for i in range(3)]
    pT1s = [self.ps("mpT1", [128, 512], BF16)] * 2
    pT2 = self.ps("mpT2", [128, D], BF16)
    py = [self.ps(f"mpy{i}", [128, 512]) for i in range(3)]
    pY = py[0]
    cnt = {}

    def nxt(key, n):
        v = cnt.get(key, 0)
        cnt[key] = v + 1
        return v % n
    H2Tv = self.H2T[l].rearrange("(kc p) t -> p kc t", p=128)
    work = []
    for (tag, src, dst, ntok, row, uoff) in streams:
        for t0 in range(0, ntok, 128):
            work.append((tag, src, dst, row, uoff + t0, t0))

    def stage1(w, i):
        (tag, src, dst, row, u, t0) = work[w]
        H, AT, OGt, OMt, XR, gt, ssq = hT[i], aTt[i], og[i], om[i], xr[i], gts[i], ssqs[i]
        S.dma(S.sp, [(H[:], H2Tv[:, :, u:u + 128])], reads=[self.R(self.H2T[l].name)], writes=[H.res])
        S.dma(S.sp, [(AT[:], self.ATT[l][:, :, u:u + 128])], reads=[self.R(self.ATT[l].name)], writes=[AT.res])
        S.dma(S.sp, [(OGt[:, 0, :], self.OG[l][0, u:u + 128, :]), (OGt[:, 1, :], self.OG[l][1, u:u + 128, :])], reads=[self.R(self.OG[l].name)], writes=[OGt.res])
        S.dma(S.sp, [(OMt[:, 0, :], self.OM[l][0, u:u + 128, :]), (OMt[:, 1, :], self.OM[l][1, u:u + 128, :])], reads=[self.R(self.OM[l].name)], writes=[OMt.res])
        S.dma(S.sp, [(XR[:], src[t0:t0 + 128, :])], reads=[self.R(src.name)], writes=[XR.res])

    def stage1g(w, i):
        H, gt = hT[i], gts[i]
        for blk in range(8):
            p = pg[nxt("pg", 3)]
            for kc in range(8):
                S.op(S.pe, lambda h, kc=kc, blk=blk, p=p, H=H: h.matmul(p[:], H[:, kc, :], Wm[:, kc, blk * 512:(blk + 1) * 512], start=(kc == 0), stop=(kc == 7)),
                     reads=[H.res] + Wmr, writes=[p.res], inc=(kc == 7))
            fn = AF.Silu if blk == 0 else AF.Sigmoid
            S.op(S.act, lambda h, blk=blk, p=p, fn=fn, gt=gt: h.activation(out=gt[:, blk, :], in_=p[:], func=fn), reads=[p.res], writes=[gt.res])

    def stage1c(w, i):
        OGt, OMt, gt, ssq = og[i], om[i], gts[i], ssqs[i]
        for br, Ot in enumerate((OGt, OMt)):
            S.op(S.pool, lambda h, Ot=Ot: h.tensor_tensor(out=Ot[:, 0, :], in0=Ot[:, 0, :], in1=Ot[:, 1, :], op=ALU.add), reads=[Ot.res], writes=[Ot.res])
            S.op(S.act, lambda h, Ot=Ot: h.activation(out=sq[:], in_=Ot[:, 0, :], func=AF.Square), reads=[Ot.res], writes=[sq.res])
            S.op(S.dve, lambda h, br=br, ssq=ssq: h.tensor_reduce(out=ssq[:, br, :], in_=sq[:].rearrange("p (a b) -> p a b", b=128), axis=AX.X, op=ALU.add),
                 reads=[sq.res], writes=[ssq.res])
        S.op(S.act, lambda h, ssq=ssq: h.activation(out=ssq[:], in_=ssq[:], func=AF.Sqrt, scale=1.0 / 128, bias=eps_col[:]),
             reads=[ssq.res, eps_col.res], writes=[ssq.res])
        S.op(S.dve, lambda h, ssq=ssq: h.reciprocal(out=ssq[:], in_=ssq[:]), reads=[ssq.res], writes=[ssq.res])
        for br, Ot in enumerate((OGt, OMt)):
            B_ = bn[br]
            S.op(S.dve, lambda h, br=br, Ot=Ot, B_=B_, ssq=ssq: h.tensor_tensor(out=B_[:].rearrange("p (a b) -> p a b", b=128), in0=Ot[:, 0, :].rearrange("p (a b) -> p a b", b=128),
                                                                              in1=ssq[:, br, :].unsqueeze(2).to_broadcast([128, 4, 128]), op=ALU.mult),
                 reads=[Ot.res, ssq.res], writes=[B_.res])
            S.op(S.pool, lambda h, br=br, B_=B_: h.tensor_tensor(out=B_[:].rearrange("p (a b) -> p a b", b=128), in0=B_[:].rearrange("p (a b) -> p a b", b=128),
                                                              in1=gains[:, br:br + 1, :].to_broadcast([128, 4, 128]), op=ALU.mult),
                 reads=[B_.res, gains.res], writes=[B_.res])

    def stage1d(w, i):
        gt = gts[i]
        for br in range(2):
            B_ = bn[br]
            BB = bbs[i][br]
            S.op(S.dve, lambda h, br=br, B_=B_, BB=BB, gt=gt: h.tensor_tensor(out=BB[:], in0=B_[:], in1=gt[:, br, :], op=ALU.mult), reads=[B_.res, gt.res], writes=[BB.res])

    def stage1b(w, i):
        for br in range(2):
            BB = bbs[i][br]
            pT1 = pT1s[br]
            for c in range(4):
                S.op(S.pe, lambda h, c=c, BB=BB, pT1=pT1: h.transpose(out=pT1[:, c * 128:(c + 1) * 128], in_=BB[:, c * 128:(c + 1) * 128], identity=self.ident[:]),
                     reads=[BB.res, self.ident.res], writes=[pT1.res], inc=(c == 3))
            BT = bTs[i][br]
            if br == 0:
                S.op(S.act, lambda h, BT=BT, pT1=pT1: h.activation(out=BT[:].rearrange("p a b -> p (a b)"), in_=pT1[:], func=AF.Copy), reads=[pT1.res], writes=[BT.res])
            else:
                S.op(S.dve, lambda h, BT=BT, pT1=pT1: h.tensor_copy(out=BT[:].rearrange("p a b -> p (a b)"), in_=pT1[:]), reads=[pT1.res], writes=[BT.res])

    cur_row = [None]

    def stage2(w, i):
        (tag, src, dst, row, u, t0) = work[w]
        AT, XR, gt = aTt[i], xr[i], gts[i]
        if cur_row[0] != row:
            cur_row[0] = row
            srcg = self.MOD[l][row:row + 1, 5 * D:6 * D]
            S.dma(S.sp, [(G5[:], dram_ap(srcg, srcg.offset, [[0, 128], [1, D]]))], reads=[self.R("MOD", l)], writes=[G5.res])
        for half in range(2):
            cs_ = slice(half * 512, (half + 1) * 512)
            for hh in range(8):
                S.op(S.pe, lambda h, hh=hh, AT=AT, cs_=cs_: h.matmul(py[0][:], AT[:, hh, :], Woa[:, hh, cs_], start=(hh == 0), stop=(hh == 7)),
                     reads=[AT.res, Woa.res], writes=[py[0].res], inc=(hh == 7))
            for c in range(4):
                S.op(S.pe, lambda h, c=c, cs_=cs_, BT=bTs[i][0]: h.matmul(py[1][:], BT[:, c, :], Wog[:, c, cs_], start=(c == 0), stop=(c == 3)),
                     reads=[bTs[i][0].res, Wog.res], writes=[py[1].res], inc=(c == 3))
            for c in range(4):
                S.op(S.pe, lambda h, c=c, cs_=cs_, BT=bTs[i][1]: h.matmul(py[2][:], BT[:, c, :], Wom[:, c, cs_], start=(c == 0), stop=(c == 3)),
                     reads=[bTs[i][1].res, Wom.res], writes=[py[2].res], inc=(c == 3))
            S.op(S.dve, lambda h, half=half, gt=gt: h.tensor_tensor(out=t1[0][:], in0=py[0][:], in1=gt[:, 2 + half, :], op=ALU.mult), reads=[py[0].res, gt.res], writes=[t1[0].res])
            S.op(S.dve, lambda h, half=half, gt=gt: h.tensor_tensor(out=t1[1][:], in0=py[1][:], in1=gt[:, 4 + half, :], op=ALU.mult), reads=[py[1].res, gt.res], writes=[t1[1].res])
            S.op(S.dve, lambda h, half=half, gt=gt: h.tensor_tensor(out=t1[2][:], in0=py[2][:], in1=gt[:, 6 + half, :], op=ALU.mult), reads=[py[2].res, gt.res], writes=[t1[2].res])
            S.op(S.dve, lambda h: h.tensor_tensor(out=t1[0][:], in0=t1[0][:], in1=t1[1][:], op=ALU.add), reads=[t1[0].res, t1[1].res], writes=[t1[0].res])
            S.op(S.dve, lambda h, cs_=cs_: h.tensor_tensor(out=yb[:, cs_], in0=t1[0][:], in1=t1[2][:], op=ALU.add), reads=[t1[0].res, t1[2].res], writes=[yb.res])
        for kc in range(8):
            S.op(S.pe, lambda h, kc=kc: h.transpose(out=pT2[:, kc * 128:(kc + 1) * 128], in_=yb[:, kc * 128:(kc + 1) * 128], identity=self.ident[:]),
                 reads=[yb.res, self.ident.res], writes=[pT2.res], inc=(kc == 7))
        S.op(S.act, lambda h: h.activation(out=yT[:].rearrange("p a b -> p (a b)"), in_=pT2[:], func=AF.Copy), reads=[pT2.res], writes=[yT.res])
        for half in range(2):
            cs_ = slice(half * 512, (half + 1) * 512)
            for kc in range(8):
                S.op(S.pe, lambda h, kc=kc, cs_=cs_: h.matmul(pY[:], yT[:, kc, :], Wo[:, kc, cs_], start=(kc == 0), stop=(kc == 7)),
                     reads=[yT.res, Wo.res], writes=[pY.res], inc=(kc == 7))
            tq = t1[1 + half]
            S.op(S.dve, lambda h, cs_=cs_, tq=tq: h.tensor_tensor(out=tq[:], in0=pY[:], in1=G5[:, cs_], op=ALU.mult), reads=[pY.res, G5.res], writes=[tq.res])
            S.op(S.pool, lambda h, cs_=cs_, XR=XR, tq=tq: h.tensor_tensor(out=XR[:, cs_], in0=XR[:, cs_], in1=tq[:], op=ALU.add), reads=[XR.res, tq.res], writes=[XR.res])
        S.dma(S.pool, [(dst[t0:t0 + 128, :], XR[:])], reads=[XR.res], writes=[self.R(dst.name)])

    def stage1all(w, i):
        stage1(w, i)
        stage1c(w, i)
        stage1g(w, i)
        stage1d(w, i)
        stage1b(w, i)

    stage1all(0, 0)
    for w in range(len(work)):
        if w + 1 < len(work):
            stage1all(w + 1, (w + 1) % 2)
        stage2(w, w % 2)


Builder.phase_merge = phase_merge

_NC_CACHE = {}


def kernel(**inputs):
    inp = {k: np.asarray(v) for k, v in inputs.items()}
    Bsz, SEQ, _ = inp["x"].shape
    T = SEQ
    if T not in _NC_CACHE:
        _NC_CACHE[T] = Builder(T).build()
    nc = _NC_CACHE[T]
    in_maps = [make_in_map(inp, b, 0, T) for b in range(Bsz)]
    res = run_bass_kernel_spmd(nc, in_maps, core_ids=list(range(Bsz)))
    out = np.stack([np.asarray(r["y"], dtype=np.float32) for r in res.results], axis=0)
    return out


W_NAMES = ["mod_w", "mod_b", "norm_g", "ffn1_w13", "ffn1_w2", "ffn2_w13", "ffn2_w2", "w_in", "attn_q_norm", "attn_k_norm", "attn_sink", "gla_w2", "gla_b", "mlstm_conv_w", "mlstm_conv_b", "mlstm_ib", "mlstm_fb", "gla_norm", "mlstm_norm", "w_out_attn", "w_out_gla", "w_out_mlstm", "w_o"]


def make_in_map(inp, b, t0, T):
    m = {"x": np.ascontiguousarray(inp["x"][b, t0:t0 + T]), "c": np.ascontiguousarray(inp["c"][b]),
         "ctx": np.ascontiguousarray(inp["ctx"][b]), "c_ctx": np.ascontiguousarray(inp["c_ctx"])}
    for k in W_NAMES:
        m[k] = np.ascontiguousarray(inp[k])
    m["rope_cs"] = rope_table(t0, T)
    return m


def rope_table(t0, T):
    pos = np.arange(t0, t0 + T)
    r = (pos // 64).astype(np.float32)
    col = (pos % 64).astype(np.float32)
    inv = (np.float32(10000.0) ** (-np.arange(16, dtype=np.float32) / np.float32(16))).astype(np.float32)
    ang = np.concatenate([r[:, None] * inv, col[:, None] * inv], axis=-1).astype(np.float32)
    return np.ascontiguousarray(np.stack([np.cos(ang), np.sin(ang)], axis=1).astype(np.float32))
```

```python
import numpy as np
from contextlib import ExitStack
import concourse.bass as bass
import concourse.mybir as mybir
from concourse.bass_utils import run_bass_kernel_spmd

F32 = mybir.dt.float32
BF16 = mybir.dt.bfloat16
AF = mybir.ActivationFunctionType
ALU = mybir.AluOpType
AX = mybir.AxisListType

D = 1024
DFF = 2816
NMOD = 9
LC = 256
EPS = 1e-6
DEPTH = 2
D_IN = 7472


class Res:
    __slots__ = ("name", "w", "r")

    def __init__(self, name=""):
        self.name = name
        self.w = None
        self.r = []


class Eng:
    def __init__(self, name, is_pe=False):
        self.name = name
        self.is_pe = is_pe
        self.ops = []
        self.sems = []
        self.si = 0
        self.cnt = 0
        self.seen = {}
        self.pend_r = []
        self.pend_w = []
        self.pool = []
        self.pi = 0


ROT = 30000


class Sched:
    def __init__(self, nc, es):
        self.nc = nc
        self.es = es
        self.pe = Eng("pe", True)
        self.act = Eng("act")
        self.dve = Eng("dve")
        self.pool = Eng("pool")
        self.sp = Eng("sp")
        self.engs = [self.pe, self.act, self.dve, self.pool, self.sp]
        self.semid = {}
        n_rot = {"pe": 6, "act": 3, "dve": 3, "pool": 3, "sp": 1}
        for e in self.engs:
            for i in range(n_rot[e.name]):
                s = es.enter_context(nc.semaphore(f"s_{e.name}{i}"))
                e.sems.append(s)
        for e, n in ((self.sp, 20), (self.pool, 10), (self.act, 4)):
            for i in range(n):
                s = es.enter_context(nc.semaphore(f"d_{e.name}{i}"))
                e.pool.append([s, 0])
        self.n_ops = 0

    def _need(self, eng, tok, raw):
        if tok is None:
            return None
        sem, val, owner = tok
        if owner == eng.name:
            if eng.is_pe:
                return None
            if not raw:
                return None
        key = id(sem)
        if eng.seen.get(key, 0) >= val:
            return None
        eng.seen[key] = val
        return (sem, val)

    def _waits(self, eng, reads, writes):
        ws = []
        for r in reads:
            w = self._need(eng, r.w, True)
            if w:
                ws.append(w)
        for wr in writes:
            w = self._need(eng, wr.w, False)
            if w:
                ws.append(w)
            for t in wr.r:
                w = self._need(eng, t, False)
                if w:
                    ws.append(w)
        for (sem, val) in ws:
            eng.ops.append(lambda h, sem=sem, val=val: h.wait_ge(sem, val))

    def _record(self, tok, reads, writes):
        for r in reads:
            r.r = [t for t in r.r if t[2] != tok[2] or t[0] is not tok[0]] + [tok]
        for w in writes:
            w.w = tok
            w.r = []

    def op(self, eng, fn, reads=(), writes=(), inc=True):
        self.n_ops += 1
        reads = list(reads)
        writes = list(writes)
        self._waits(eng, reads, writes)
        if not inc:
            eng.ops.append(lambda h, fn=fn: fn(h))
            eng.pend_r += reads
            eng.pend_w += writes
            return
        if eng.cnt >= ROT:
            eng.si += 1
            eng.cnt = 0
        eng.cnt += 1
        sem = eng.sems[eng.si]
        tok = (sem, eng.cnt, eng.name)
        eng.ops.append(lambda h, fn=fn, sem=sem: fn(h).then_inc(sem, 1))
        self._record(tok, reads + eng.pend_r, writes + eng.pend_w)
        eng.pend_r = []
        eng.pend_w = []

    def dma(self, eng, pairs, reads=(), writes=(), **kw):
        self.n_ops += 1
        reads = list(reads)
        writes = list(writes)
        self._waits(eng, reads, writes)
        ent = eng.pool[eng.pi]
        eng.pi = (eng.pi + 1) % len(eng.pool)
        sem = ent[0]
        if ent[1] > 0 and eng.seen.get(id(sem), 0) < ent[1]:
            v = ent[1]
            eng.ops.append(lambda h, sem=sem, v=v: h.wait_ge(sem, v))
            eng.seen[id(sem)] = v
        for (o, i) in pairs:
            ent[1] += 16
            eng.ops.append(lambda h, o=o, i=i, sem=sem: h.dma_start(out=o, in_=i, **kw).then_inc(sem, 16))
        tok = (sem, ent[1], "dma_" + eng.name + str(id(sem)))
        self._record(tok, reads, writes)

    def barrier(self):
        toks = []
        for e in self.engs:
            assert not e.pend_r and not e.pend_w, e.name
            for i in range(e.si + 1):
                v = ROT if i < e.si else e.cnt
                if v > 0:
                    toks.append((e, e.sems[i], v))
            for ent in e.pool:
                if ent[1] > 0:
                    toks.append((None, ent[0], ent[1]))
        for e in self.engs:
            for (own, sem, v) in toks:
                if own is e:
                    continue
                if e.seen.get(id(sem), 0) >= v:
                    continue
                e.seen[id(sem)] = v
                e.ops.append(lambda h, sem=sem, v=v: h.wait_ge(sem, v))

    def finish(self):
        for e in (self.sp, self.pool, self.act):
            for ent in e.pool:
                if ent[1] > 0:
                    self.sp.ops.append(lambda h, sem=ent[0], v=ent[1]: h.wait_ge(sem, v))

    def replay(self):
        nc = self.nc
        with nc.Block() as block:
            @block.tensor
            def _(h):
                for f in self.pe.ops:
                    f(h)

            @block.scalar
            def _(h):
                for f in self.act.ops:
                    f(h)

            @block.vector
            def _(h):
                for f in self.dve.ops:
                    f(h)

            @block.gpsimd
            def _(h):
                for f in self.pool.ops:
                    f(h)

            @block.sync
            def _(h):
                for f in self.sp.ops:
                    f(h)


class Tile:
    def __init__(self, t, name):
        self.t = t
        self.res = Res(name)

    def __getitem__(self, k):
        return self.t[k]


def dram_ap(t, offset, pattern):
    return bass.AP(t.tensor, offset, pattern)


class Builder:
    def __init__(self, T, depth=DEPTH, stop=None, dbg=()):
        self.T = T
        self.depth = depth
        self.stop = stop
        self.dbg = dbg
        self.nc = bass.Bass("TRN2", target_bir_lowering=False)
        self.es = ExitStack()
        self.S = None
        self.dres = {}
        self.rr = {}

    def din(self, name, shape):
        return self.nc.dram_tensor(name, list(shape), F32, kind="ExternalInput").ap()

    def dout(self, name, shape, dt=F32):
        return self.nc.dram_tensor(name, list(shape), dt, kind="ExternalOutput").ap()

    def dscr(self, name, shape, dt=F32):
        if name in self.dbg:
            return self.nc.dram_tensor(name, list(shape), dt, kind="ExternalOutput").ap()
        return self.nc.dram_tensor(name, list(shape), dt).ap()

    def R(self, *key):
        if key not in self.dres:
            self.dres[key] = Res(str(key))
        return self.dres[key]

    def sb(self, name, shape, dt):
        self.uid = getattr(self, "uid", 0) + 1
        name = f"{name}_{self.uid}"
        t = self.cur.enter_context(self.nc.sbuf_tensor(name, list(shape), dt))
        return Tile(t, name)

    def ps(self, name, shape, dt=F32):
        self.uid = getattr(self, "uid", 0) + 1
        name = f"{name}_{self.uid}"
        t = self.cur.enter_context(self.nc.psum_tensor(name, list(shape), dt))
        return Tile(t, name)

    def build(self):
        nc = self.nc
        T = self.T
        L = self.depth
        with self.es as es:
            self.S = S = Sched(nc, es)
            self.x_in = self.din("x", [T, D])
            self.c_in = self.din("c", [D])
            self.ctx_in = self.din("ctx", [LC, D])
            self.cctx_in = self.din("c_ctx", [D])
            self.mod_w = self.din("mod_w", [L, D, NMOD * D])
            self.mod_b = self.din("mod_b", [L, NMOD * D])
            self.norm_g = self.din("norm_g", [L, 3, D])
            self.ffn_w13 = [self.din("ffn1_w13", [L, D, 2 * DFF]), self.din("ffn2_w13", [L, D, 2 * DFF])]
            self.ffn_w2 = [self.din("ffn1_w2", [L, DFF, D]), self.din("ffn2_w2", [L, DFF, D])]
            self.w_in = self.din("w_in", [L, D, D_IN])
            self.attn_q_norm = self.din("attn_q_norm", [L, 64])
            self.attn_k_norm = self.din("attn_k_norm", [L, 64])
            self.attn_sink = self.din("attn_sink", [L, 8])
            self.gla_w2 = self.din("gla_w2", [L, 2, 16, 256])
            self.gla_b = self.din("gla_b", [L, 2, 256])
            self.rope_cs = self.din("rope_cs", [T, 2, 32])
            self.y_out = self.dout("y", [T, D])
            TT = self.TT = LC + T
            self.H2T = [self.dscr(f"H2T{l}", [D, TT], BF16) for l in range(L)]
            self.QT = [self.dscr(f"QT{l}", [64, 8, TT], BF16) for l in range(L)]
            self.KT = [self.dscr(f"KT{l}", [64, 2, TT], BF16) for l in range(L)]
            self.VA = [self.dscr(f"VA{l}", [TT, 128], BF16) for l in range(L)]
            self.GV = [self.dscr(f"GV{l}", [TT, 512], BF16) for l in range(L)]
            self.MV = [self.dscr(f"MV{l}", [TT, 512], BF16) for l in range(L)]
            self.GATES = [self.dscr(f"GATES{l}", [16, TT]) for l in range(L)]
            self.QG = [self.dscr(f"QG{l}", [2, 64, 4, TT], BF16) for l in range(L)]
            self.KG = [self.dscr(f"KG{l}", [2, 64, 4, TT], BF16) for l in range(L)]
            self.KH = [self.dscr(f"KH{l}", [2, 64, 4, TT], BF16) for l in range(L)]
            self.MQK = [self.dscr(f"MQK{l}", [D, TT + 4], BF16) for l in range(L)]
            self.ATT = [self.dscr(f"ATT{l}", [64, 8, TT], BF16) for l in range(L)]
            self.OG = [self.dscr(f"OG{l}", [2, TT, 512]) for l in range(L)]
            self.OM = [self.dscr(f"OM{l}", [2, TT, 512]) for l in range(L)]
            self.MROWS = [self.dscr(f"MROWS{l}", [2, 5, 4, TT]) for l in range(L)]
            self.MQC = [self.dscr(f"MQC{l}", [D, TT], BF16) for l in range(L)]
            self.gla_norm = self.din("gla_norm", [L, 128])
            self.mlstm_norm = self.din("mlstm_norm", [L, 128])
            self.w_out_attn = self.din("w_out_attn", [L, 512, D])
            self.w_out_gla = self.din("w_out_gla", [L, 512, D])
            self.w_out_mlstm = self.din("w_out_mlstm", [L, 512, D])
            self.w_o = self.din("w_o", [L, D, D])
            self.X2 = [self.dscr(f"X2_{l}", [T, D]) for l in range(L)]
            self.C2 = [self.dscr(f"C2_{l}", [LC, D]) for l in range(L)]
            self.X3 = [self.dscr(f"X3_{l}", [T, D]) for l in range(L)]
            self.C3 = [self.dscr(f"C3_{l}", [LC, D]) for l in range(L)]
            self.conv_w = self.din("mlstm_conv_w", [L, 5, D])
            self.conv_b = self.din("mlstm_conv_b", [L, D])
            self.mlstm_ib = self.din("mlstm_ib", [L, 2, 4])
            self.mlstm_fb = self.din("mlstm_fb", [L, 2, 4])
            self.MOD = [self.dscr(f"MOD{l}", [2, NMOD * D]) for l in range(L)]
            self.X1 = [self.dscr(f"X1_{l}", [T, D]) for l in range(L)]
            self.C1 = [self.dscr(f"C1_{l}", [LC, D]) for l in range(L)]
            with ExitStack() as cst:
                self.cur = cst
                self.ident = self.sb("ident", [128, 128], BF16)
                self.identf = self.sb("identf", [128, 128], F32)
                self.eps_col = self.sb("eps_col", [128, 1], F32)
                S.op(S.dve, lambda h: h.memset(self.eps_col[:], EPS), writes=[self.eps_col.res])
                self.make_consts()
                for l in range(L):
                    xin = self.x_in if l == 0 else self.X3[l - 1]
                    cin = self.ctx_in if l == 0 else self.C3[l - 1]
                    with ExitStack() as ph:
                        self.cur = ph
                        self.phase_mod(l)
                        S.barrier()
                    if self.stop == ("mod", l):
                        break
                    with ExitStack() as ph:
                        self.cur = ph
                        self.phase_ffn(l, 0, [("ctx", cin, self.C1[l], LC, 1), ("lat", xin, self.X1[l], T, 0)])
                        S.barrier()
                    if self.stop == ("ffn1", l):
                        break
                    with ExitStack() as lay:
                        self.cur = lay
                        self.EL = self.sb("EL", [64, 4, 2, TT // 64], F32)
                        with ExitStack() as ph:
                            self.cur = ph
                            self.phase_feat(l, [("ctx", self.C1[l], LC, 1, 0, False), ("lat", self.X1[l], T, 0, LC, True)])
                            S.barrier()
                        if self.stop == ("feat", l):
                            break
                        with ExitStack() as ph:
                            self.cur = ph
                            self.phase_attn(l, l < L - 1)
                            S.barrier()
                        if self.stop == ("attn", l):
                            break
                        with ExitStack() as ph:
                            self.cur = ph
                            self.phase_gla(l)
                            S.barrier()
                        if self.stop == ("gla", l):
                            break
                        self.cur = lay
                        self.DEC = self.sb("DEC", [128, 2, 4, TT // 64], F32)
                        self.sel = self.sb("sel", [4, 4, 128], F32)
                        S.op(S.dve, lambda h, sel_t=self.sel: h.tensor_copy(out=sel_t[:], in_=self.identf[0:4, 0:4].unsqueeze(2).to_broadcast([4, 4, 128])),
                             reads=[self.identf.res], writes=[self.sel.res])
                        stop_ml = False
                        for ph_name, ph_fn in (("mlg", self.phase_ml_gates), ("mlc", self.phase_ml_conv), ("mls", self.phase_ml_scan)):
                            with ExitStack() as ph:
                                self.cur = ph
                                ph_fn(l)
                                S.barrier()
                            if self.stop == (ph_name, l):
                                stop_ml = True
                                break
                        if stop_ml:
                            break
                        if self.stop == ("ml", l):
                            break
                    last = (l == L - 1)
                    with ExitStack() as ph:
                        self.cur = ph
                        st = [("lat", self.X1[l], self.X2[l], T, 0, LC)]
                        if not last:
                            st = [("ctx", self.C1[l], self.C2[l], LC, 1, 0)] + st
                        self.phase_merge(l, st)
                        S.barrier()
                    if self.stop == ("merge", l):
                        break
                    with ExitStack() as ph:
                        self.cur = ph
                        xdst = self.y_out if last else self.X3[l]
                        st = [("lat", self.X2[l], xdst, T, 0)]
                        import os
                        if not last and not os.environ.get("NOCTX2"):
                            st = [("ctx", self.C2[l], self.C3[l], LC, 1)] + st
                        self.phase_ffn(l, 1, [(a, b, c, d_, e) for (a, b, c, d_, e) in st])
                        S.barrier()
                    if self.stop == ("ffn2", l):
                        break
                S.finish()
                S.replay()
        return nc

    def make_consts(self):
        S = self.S
        nc = self.nc
        idf = self.identf
        S.op(S.pool, lambda h: h.memset(idf[:], 0.0), writes=[idf.res])
        S.op(S.pool, lambda h: h.affine_select(out=idf[:], in_=idf[:], pattern=[[-1, 128]],
                                                compare_op=ALU.not_equal, fill=1.0, base=0,
                                                channel_multiplier=1),
             reads=[idf.res], writes=[idf.res])
        S.op(S.dve, lambda h: h.tensor_copy(out=self.ident[:], in_=idf[:]), reads=[idf.res], writes=[self.ident.res])

    def phase_mod(self, l):
        S = self.S
        cl = self.sb("cl", [128, 8, 2], F32)
        cs = self.sb("cs", [128, 8, 2], F32)
        S.dma(S.sp, [(cl[:, :, 0], self.c_in.rearrange("(kc p) -> p kc", p=128)),
                     (cl[:, :, 1], self.cctx_in.rearrange("(kc p) -> p kc", p=128))],
              writes=[cl.res], allow_slow_non_contiguous=True)
        S.op(S.act, lambda h: h.activation(out=cs[:], in_=cl[:], func=AF.Silu), reads=[cl.res], writes=[cs.res])
        wm = [self.sb(f"wm{i}", [128, 8, 512], F32) for i in range(4)]
        mb = [self.sb(f"mb{i}", [2, 512], F32) for i in range(4)]
        mo = [self.sb(f"mo{i}", [2, 512], F32) for i in range(2)]
        pm = [self.ps(f"pm{i}", [2, 512]) for i in range(2)]
        mw = self.mod_w[l].rearrange("(kc p) n -> p kc n", p=128)
        for n in range(18):
            i = n % 4
            S.dma(S.sp if n % 2 == 0 else S.act, [(wm[i][:, 2 * q:2 * q + 2, :], mw[:, 2 * q:2 * q + 2, n * 512:(n + 1) * 512]) for q in range(4)], writes=[wm[i].res])
            mbsrc = self.mod_b[l:l + 1, n * 512:(n + 1) * 512]
            S.dma(S.sp, [(mb[i][0:1, :], mbsrc), (mb[i][1:2, :], mbsrc)], writes=[mb[i].res])
            for kc in range(8):
                S.op(S.pe, lambda h, kc=kc, i=i, j=n % 2: h.matmul(pm[j][:], cs[:, kc, :], wm[i][:, kc, :],
                                                           start=(kc == 0), stop=(kc == 7)),
                     reads=[cs.res, wm[i].res], writes=[pm[n % 2].res], inc=(kc == 7))
            S.op(S.dve, lambda h, i=i, j=n % 2: h.tensor_tensor(out=mo[j][:], in0=pm[j][:], in1=mb[i][:], op=ALU.add),
                 reads=[pm[n % 2].res, mb[i].res], writes=[mo[n % 2].res])
            S.dma(S.pool, [(self.MOD[l][:, n * 512:(n + 1) * 512], mo[n % 2][:])], reads=[mo[n % 2].res],
                  writes=[self.R("MOD", l)])

    def load_cols(self, dst_ap, src_row_ap, res):
        self.S.dma(self.S.sp, [(dst_ap, src_row_ap.rearrange("(kc p) -> p kc", p=128))], writes=[res],
                   allow_slow_non_contiguous=True)

    def adaln_cols(self, l, j, row, tag):
        S = self.S
        tmp = self.sb(f"adt_{tag}", [128, 3, 8], F32)
        A = self.sb(f"adA_{tag}", [128, 8], F32)
        MODr = self.MOD[l]
        S.dma(S.sp, [(tmp[:, 0, :], MODr[row, (3 * j) * D:(3 * j + 1) * D].rearrange("(kc p) -> p kc", p=128)),
                     (tmp[:, 1, :], MODr[row, (3 * j + 1) * D:(3 * j + 2) * D].rearrange("(kc p) -> p kc", p=128)),
                     (tmp[:, 2, :], self.norm_g[l, j, :].rearrange("(kc p) -> p kc", p=128))],
              reads=[self.R("MOD", l)], writes=[tmp.res], allow_slow_non_contiguous=True)
        S.op(S.dve, lambda h: h.scalar_tensor_tensor(out=A[:], in0=tmp[:, 1, :], scalar=1.0, in1=tmp[:, 2, :],
                                                      op0=ALU.add, op1=ALU.mult),
             reads=[tmp.res], writes=[A.res])
        return A, tmp

    def gate_bc(self, l, j, row, tag, mul):
        S = self.S
        G = self.sb(f"gate_{tag}", [128, D], F32)
        src = self.MOD[l][row:row + 1, (3 * j + 2) * D:(3 * j + 3) * D]
        src_b = dram_ap(src, src.offset, [[0, 128], [1, D]])
        S.dma(S.sp, [(G[:], src_b)], reads=[self.R("MOD", l)], writes=[G.res])
        if mul != 1.0:
            S.op(S.pool, lambda h: h.tensor_scalar(out=G[:], in0=G[:], scalar1=float(mul), scalar2=None, op0=ALU.mult),
                 reads=[G.res], writes=[G.res])
        return G

    def load_weight_bf16(self, dst, src3, nsplit):
        S = self.S
        kcn = dst.t.shape[1]
        step = max(1, kcn // nsplit)
        dst.parts = []
        for k0 in range(0, kcn, step):
            k1 = min(kcn, k0 + step)
            r = Res(f"wpart{k0}")
            dst.parts.append(r)
            S.dma(S.pool, [(dst[:, k0:k1, :], src3[:, k0:k1, :])], writes=[r])

    def norm_part(self, xt, nb, ss, rs, junk=None):
        S = self.S
        if junk is None:
            junk = self.junk
        S.op(S.act, lambda h: h.activation(out=junk[:], in_=xt[:], func=AF.Square, accum_out=ss[:]),
             reads=[xt.res], writes=[junk.res, ss.res])
        S.op(S.act, lambda h: h.activation(out=rs[:], in_=ss[:], func=AF.Sqrt, scale=1.0 / D, bias=self.eps_col[:]),
             reads=[ss.res], writes=[rs.res])
        S.op(S.dve, lambda h: h.reciprocal(out=rs[:], in_=rs[:]), reads=[rs.res], writes=[rs.res])
        S.op(S.dve, lambda h: h.tensor_scalar(out=nb[:], in0=xt[:], scalar1=rs[:], scalar2=None, op0=ALU.mult),
             reads=[xt.res, rs.res], writes=[nb.res])

    def transpose_part(self, nb, pT, hT, col0, A, sh, evac_engs):
        S = self.S
        for kc in range(8):
            S.op(S.pe, lambda h, kc=kc: h.transpose(out=pT[:, kc * 128:(kc + 1) * 128], in_=nb[:, kc * 128:(kc + 1) * 128],
                                                     identity=self.ident[:]),
                 reads=[nb.res, self.ident.res], writes=[pT.res], inc=(kc == 7))
        for kc in range(8):
            e = evac_engs[kc % len(evac_engs)]
            if e is S.act:
                S.op(e, lambda h, kc=kc: h.activation(out=hT[:, kc, col0:col0 + 128], in_=pT[:, kc * 128:(kc + 1) * 128],
                                                      func=AF.Identity, scale=A[:, kc:kc + 1], bias=sh[:, kc:kc + 1]),
                     reads=[pT.res, A.res, self.shres], writes=[hT.res])
            else:
                S.op(e, lambda h, kc=kc: h.tensor_scalar(out=hT[:, kc, col0:col0 + 128], in0=pT[:, kc * 128:(kc + 1) * 128],
                                                         scalar1=A[:, kc:kc + 1], scalar2=sh[:, kc:kc + 1],
                                                         op0=ALU.mult, op1=ALU.add),
                     reads=[pT.res, A.res, self.shres], writes=[hT.res])

    def phase_ffn(self, l, which, streams):
        S = self.S
        j = 0 if which == 0 else 2
        W13 = self.sb("W13", [128, 8, 2 * DFF], BF16)
        W2 = self.sb("W2", [128, 22, D], BF16)
        self.load_weight_bf16(W13, self.ffn_w13[which][l].rearrange("(kc p) n -> p kc n", p=128), 8)
        self.load_weight_bf16(W2, self.ffn_w2[which][l].rearrange("(fc p) n -> p fc n", p=128), 11)
        import os
        if os.environ.get("FFN_WONLY") and which == 1:
            return
        xl = [self.sb(f"xl{i}", [128, D], F32) for i in range(3)]
        xr = [self.sb(f"xr{i}", [128, D], F32) for i in range(2)]
        nb = [self.sb(f"nb{i}", [128, D], BF16) for i in range(4)]
        ss = [self.sb(f"ss{i}", [128, 1], F32) for i in range(4)]
        rs = [self.sb(f"rs{i}", [128, 1], F32) for i in range(4)]
        hT = self.sb("hT", [128, 8, 512], BF16)
        gT = self.sb("gT", [128, 22, 512], BF16)
        sa = [self.sb(f"sa{i}", [128, 512], F32) for i in range(2)]
        tt = [self.sb(f"tt{i}", [128, 512], F32) for i in range(2)]
        pT = [self.ps(f"pT{i}", [128, D], BF16) for i in range(2)]
        pA = [self.ps(f"pA{i}", [128, 512]) for i in range(2)]
        pB = [self.ps(f"pB{i}", [128, 512]) for i in range(2)]
        pY = [self.ps(f"pY{i}", [128, 512]) for i in range(2)]
        cnt = {"xl": 0, "xr": 0, "nb": 0, "pT": 0, "pAB": 0, "sa": 0, "tt": 0}

        for (tag, src, dst, ntok, row) in streams:
            A, tmp = self.adaln_cols(l, j, row, f"{which}{tag}")
            sh = tmp[:, 0, :]
            self.shres = tmp.res
            G = self.gate_bc(l, j, row, f"{which}{tag}", 0.5)
            tiles = [(t0, min(512, ntok - t0)) for t0 in range(0, ntok, 512)]
            rtag = ("xs", l, which, tag)

            def prep_norm(t0, s):
                i = cnt["xl"] % 3
                cnt["xl"] += 1
                k = cnt["nb"] % 4
                cnt["nb"] += 1
                S.dma(S.sp, [(xl[i][:], src[t0 + s * 128:t0 + (s + 1) * 128, :])], reads=[self.R(src.name)],
                      writes=[xl[i].res])
                self.norm_part(xl[i], nb[k], ss[k], rs[k], junk=nb[k])
                return nb[k]

            def prep_tr(nbt, s):
                k = cnt["pT"] % 2
                cnt["pT"] += 1
                self.transpose_part(nbt, pT[k], hT, s * 128, A, sh, [S.act, S.dve])

            def prep(t0, n):
                for s in range(n // 128):
                    nbt = prep_norm(t0, s)
                    prep_tr(nbt, s)

            prep(*tiles[0])
            for ti, (t0, n) in enumerate(tiles):
                nt = n // 128
                for p in range(22):
                    k = cnt["pAB"] % 2
                    cnt["pAB"] += 1
                    for kc in range(8):
                        S.op(S.pe, lambda h, kc=kc, p=p, k=k, n=n: h.matmul(pA[k][:, :n], W13[:, kc, p * 128:(p + 1) * 128], hT[:, kc, :n],
                                                                        start=(kc == 0), stop=(kc == 7)),
                             reads=W13.parts + [hT.res], writes=[pA[k].res], inc=(kc == 7))
                    for kc in range(8):
                        S.op(S.pe, lambda h, kc=kc, p=p, k=k, n=n: h.matmul(pB[k][:, :n], W13[:, kc, DFF + p * 128:DFF + (p + 1) * 128], hT[:, kc, :n],
                                                                        start=(kc == 0), stop=(kc == 7)),
                             reads=W13.parts + [hT.res], writes=[pB[k].res], inc=(kc == 7))
                    q = cnt["sa"] % 2
                    cnt["sa"] += 1
                    S.op(S.act, lambda h, k=k, q=q, n=n: h.activation(out=sa[q][:, :n], in_=pA[k][:, :n], func=AF.Silu),
                         reads=[pA[k].res], writes=[sa[q].res])
                    S.op(S.dve, lambda h, k=k, q=q, p=p, n=n: h.tensor_tensor(out=gT[:, p, :n], in0=sa[q][:, :n], in1=pB[k][:, :n], op=ALU.mult),
                         reads=[sa[q].res, pB[k].res], writes=[gT.res])
                xrs = []
                for s in range(nt):
                    pass
                if ti + 1 < len(tiles):
                    pending = tiles[ti + 1]
                else:
                    pending = None
                nbts = []
                if pending is not None:
                    for s in range(pending[1] // 128):
                        nbts.append(prep_norm(pending[0], s))
                for s in range(nt):
                    i = cnt["xr"] % 2
                    cnt["xr"] += 1
                    S.dma(S.sp, [(xr[i][:], src[t0 + s * 128:t0 + (s + 1) * 128, :])], reads=[self.R(src.name)],
                          writes=[xr[i].res])
                    for dh in range(2):
                        for fc in range(22):
                            S.op(S.pe, lambda h, fc=fc, dh=dh, s=s: h.matmul(pY[dh][:], gT[:, fc, s * 128:(s + 1) * 128], W2[:, fc, dh * 512:(dh + 1) * 512],
                                                                              start=(fc == 0), stop=(fc == 21)),
                                 reads=[gT.res] + W2.parts, writes=[pY[dh].res], inc=(fc == 21))
                    for dh in range(2):
                        q = cnt["tt"] % 2
                        cnt["tt"] += 1
                        S.op(S.dve, lambda h, dh=dh, q=q, G=G: h.tensor_tensor(out=tt[q][:], in0=pY[dh][:], in1=G[:, dh * 512:(dh + 1) * 512], op=ALU.mult),
                             reads=[pY[dh].res, G.res], writes=[tt[q].res])
                        S.op(S.pool, lambda h, dh=dh, q=q, i=i: h.tensor_tensor(out=xr[i][:, dh * 512:(dh + 1) * 512], in0=xr[i][:, dh * 512:(dh + 1) * 512],
                                                                                 in1=tt[q][:], op=ALU.add),
                             reads=[tt[q].res, xr[i].res], writes=[xr[i].res])
                    S.dma(S.pool, [(dst[t0 + s * 128:t0 + (s + 1) * 128, :], xr[i][:])], reads=[xr[i].res],
                          writes=[self.R(dst.name)])
                for s, nbt in enumerate(nbts):
                    prep_tr(nbt, s)


O_AQ, O_AK, O_AV = 0, 512, 640
O_GQ, O_GK, O_GV, O_GR, O_GG = 768, 1024, 1280, 1792, 2304
O_MQ, O_MK, O_MV, O_MO, O_MI, O_MF = 2336, 2848, 3360, 3872, 4384, 4392
O_SA, O_SG, O_SM = 4400, 5424, 6448


def bcast_rows(ap2d, nparts):
    return bass.AP(ap2d.tensor, ap2d.offset, [[0, nparts]] + [list(x) for x in ap2d.ap[1:]])


def rev_last(ap):
    pat = [list(x) for x in ap.ap]
    st, n = pat[-1]
    return bass.AP(ap.tensor, ap.offset + st * (n - 1), pat[:-1] + [[-st, n]])


def phase_feat(self, l, streams):
    S = self.S
    TT = self.TT
    win = self.w_in[l].rearrange("(kc p) n -> p kc n", p=128)
    Wa = self.sb("Wa", [128, 8, 768], BF16)
    Wv = self.sb("Wv", [128, 8, 1024], BF16)
    Wf = self.sb("Wf", [128, 8, 1536], BF16)
    Wg = self.sb("Wg", [128, 8, 48], BF16)
    S.dma(S.pool, [(Wa[:, 0:4, :], win[:, 0:4, 0:768]), (Wa[:, 4:8, :], win[:, 4:8, 0:768])], writes=[Wa.res])
    for k0 in range(0, 8, 2):
        S.dma(S.pool, [(Wv[:, k0:k0 + 2, 0:512], win[:, k0:k0 + 2, O_GV:O_GV + 512]),
                       (Wv[:, k0:k0 + 2, 512:1024], win[:, k0:k0 + 2, O_MV:O_MV + 512])], writes=[Wv.res])
        S.dma(S.pool, [(Wf[:, k0:k0 + 2, 0:512], win[:, k0:k0 + 2, O_GQ:O_GQ + 512]),
                       (Wf[:, k0:k0 + 2, 512:1536], win[:, k0:k0 + 2, O_MQ:O_MQ + 1024])], writes=[Wf.res])
    S.dma(S.pool, [(Wg[:, :, 0:32], win[:, :, O_GG:O_GG + 32]), (Wg[:, :, 32:48], win[:, :, O_MI:O_MI + 16])], writes=[Wg.res])
    W2p = self.sb("W2p", [32, 2, 256], F32)
    S.op(S.dve, lambda h: h.memset(W2p[:], 0.0), writes=[W2p.res])
    S.dma(S.sp, [(W2p[0:16, 0, :], self.gla_w2[l, 0]), (W2p[16:32, 1, :], self.gla_w2[l, 1])], writes=[W2p.res])
    negb = self.sb("negb", [128, 2, 2], F32)
    S.dma(S.sp, [(negb[:, d, :], self.gla_b[l, d, :].rearrange("(c p) -> p c", p=128)) for d in range(2)],
          writes=[negb.res], allow_slow_non_contiguous=True)
    S.op(S.dve, lambda h: h.tensor_scalar(out=negb[:], in0=negb[:], scalar1=-1.0, scalar2=None, op0=ALU.mult),
         reads=[negb.res], writes=[negb.res])
    gain = self.sb("gain", [128, 10, 64], F32)
    qn_src = self.attn_q_norm[l:l + 1, :]
    kn_src = self.attn_k_norm[l:l + 1, :]
    S.dma(S.sp, [(gain[:, 0:8, :], bass.AP(qn_src.tensor, qn_src.offset, [[0, 128], [0, 8], [1, 64]])),
                 (gain[:, 8:10, :], bass.AP(kn_src.tensor, kn_src.offset, [[0, 128], [0, 2], [1, 64]]))],
          writes=[gain.res])
    mask01 = self.sb("mask01", [128, 8, 64], F32)
    S.op(S.pool, lambda h: h.memset(mask01[:], 1.0), writes=[mask01.res])
    S.op(S.pool, lambda h: h.memset(mask01[:, :, 0:1], 0.0), writes=[mask01.res])
    self.junk = self.sb("junk", [128, D], BF16)

    xl = [self.sb(f"xl{i}", [128, D], F32) for i in range(3)]
    nb = [self.sb(f"nb{i}", [128, D], BF16) for i in range(2)]
    ss = [self.sb(f"ss{i}", [128, 1], F32) for i in range(2)]
    rs = [self.sb(f"rs{i}", [128, 1], F32) for i in range(2)]
    hTs = [self.sb(f"hT{i}", [128, 8, 512], BF16) for i in range(2)]
    sqt = self.sb("sqt", [128, 640], F32)
    ssh = self.sb("ssh", [128, 10], F32)
    rinv = self.sb("rinv", [128, 10], F32)
    qn = self.sb("qn", [128, 10, 64], F32)
    rt = [self.sb(f"rt{i}", [128, 10, 32], F32) for i in range(4)]
    cs_t = [self.sb(f"cst{i}", [128, 2, 32], F32) for i in range(3)]
    qr = [self.sb(f"qr{i}", [128, 10, 64], BF16) for i in range(2)]
    vb = [self.sb(f"vb{i}", [128, 128], BF16) for i in range(4)]
    vb2 = [self.sb(f"vb2{i}", [128, 512], BF16) for i in range(4)]
    QTs = self.sb("QTs", [64, 8, 512], BF16)
    KTs = self.sb("KTs", [64, 2, 512], BF16)
    ggT = self.sb("ggT", [32, 512], F32)
    gts = self.sb("gts", [16, 512], F32)
    ex = [self.sb(f"ex{i}", [128, 512], F32) for i in range(2)]
    csum = [self.sb(f"csum{i}", [128, 512], F32) for i in range(2)]
    eb = [[self.sb(f"eb{d}{c}", [128, 512], F32) for c in range(2)] for d in range(2)]
    enb = [[self.sb(f"enb{d}{c}", [128, 512], F32) for c in range(2)] for d in range(2)]
    ebl = [[self.sb(f"ebl{d}{c}", [128, 512], F32) for c in range(2)] for d in range(2)]
    fo = [self.sb(f"fo{i}", [128, 512], BF16) for i in range(8)]
    elcs = [self.sb(f"elc{i}", [128, 8], F32) for i in range(2)]
    pT = self.ps("pT", [128, D], BF16)
    pq = self.ps("pq", [128, 512])
    pkv = self.ps("pkv", [128, 256])
    pqt = self.ps("pqt", [64, 8, 128], BF16)
    pkt = self.ps("pkt", [64, 2, 128], BF16)
    pf = [self.ps(f"pf{i}", [128, 512]) for i in range(2)]
    pz = self.ps("pz", [128, 512])
    cnt = {"xl": 0, "pf": 0, "fo": 0, "qr": 0, "vb": 0, "vb2": 0, "ex": 0, "rt": 0, "cs": 0, "elc": 0}
    H2Tv = self.H2T[l].rearrange("(kc p) t -> p kc t", p=128)

    def nxt(key, n):
        v = cnt[key] % n
        cnt[key] += 1
        return v

    st_info = []
    for (tag, src, ntok, row, uoff, rope) in streams:
        A, tmp = self.adaln_cols(l, 1, row, f"f{tag}")
        st_info.append((A, tmp, src, uoff, rope))
    tiles = []
    for si, (tag, src, ntok, row, uoff, rope) in enumerate(streams):
        for t0 in range(0, ntok, 512):
            tiles.append((si, t0, min(512, ntok - t0)))

    def prep_load(k, s):
        si, t0, n = tiles[k]
        A, tmp, src, uoff, rope = st_info[si]
        i = nxt("xl", 3)
        S.dma(S.sp, [(xl[i][:], src[t0 + s * 128:t0 + (s + 1) * 128, :])], reads=[self.R(src.name)], writes=[xl[i].res])
        cst = None
        return i

    def rope_load(k, s):
        si, t0, n = tiles[k]
        A, tmp, src, uoff, rope = st_info[si]
        if not rope:
            return None
        cst = cs_t[nxt("cs", 3)]
        S.dma(S.sp, [(cst[:], self.rope_cs[t0 + s * 128:t0 + s * 128 + 128, :, :])], writes=[cst.res])
        return cst

    def prep_sub(k, s, i=None):
        si, t0, n = tiles[k]
        A, tmp, src, uoff, rope = st_info[si]
        hT = hTs[k % 2]
        if i is None:
            i = prep_load(k, s)
        self.norm_part(xl[i], nb[i % 2], ss[i % 2], rs[i % 2])
        self.shres = tmp.res
        self.transpose_part(nb[i % 2], pT, hT, s * 128, A, tmp[:, 0, :], [S.act, S.dve])

    def g_part(k):
        si, t0, n = tiles[k]
        A, tmp, src, uoff, rope = st_info[si]
        hT = hTs[k % 2]
        u0 = uoff + t0
        nch = n // 64
        for kc in range(8):
            S.op(S.pe, lambda h, kc=kc, n=n, hT=hT: h.matmul(pz[0:32, :n], Wg[:, kc, 0:32], hT[:, kc, :n], start=(kc == 0), stop=(kc == 7)),
                 reads=[hT.res, Wg.res], writes=[pz.res], inc=(kc == 7))
        S.op(S.act, lambda h, n=n: h.activation(out=ggT[:, :n], in_=pz[0:32, :n], func=AF.Copy), reads=[pz.res], writes=[ggT.res])
        for kc in range(8):
            S.op(S.pe, lambda h, kc=kc, n=n, hT=hT: h.matmul(pz[0:16, :n], Wg[:, kc, 32:48], hT[:, kc, :n], start=(kc == 0), stop=(kc == 7)),
                 reads=[hT.res, Wg.res], writes=[pz.res], inc=(kc == 7))
        S.op(S.act, lambda h, n=n: h.activation(out=gts[:, :n], in_=pz[0:16, :n], func=AF.Copy), reads=[pz.res], writes=[gts.res])
        S.dma(S.sp, [(self.GATES[l][:, u0:u0 + n], gts[:, :n])], reads=[gts.res], writes=[self.R(self.GATES[l].name)])
        for d in range(2):
            for c2 in range(2):
                S.op(S.pe, lambda h, d=d, c2=c2, n=n: h.matmul(pz[:, :n], W2p[:, d, c2 * 128:(c2 + 1) * 128], ggT[:, :n], start=True, stop=True),
                     reads=[W2p.res, ggT.res], writes=[pz.res])
                e_ = ex[nxt("ex", 2)]
                c_ = csum[(cnt["ex"]) % 2]
                S.op(S.act, lambda h, d=d, c2=c2, n=n, e_=e_: h.activation(out=e_[:, :n], in_=pz[:, :n], func=AF.Exp, scale=-1.0, bias=negb[:, d, c2:c2 + 1]),
                     reads=[pz.res, negb.res], writes=[e_.res])
                S.op(S.act, lambda h, n=n, e_=e_: h.activation(out=e_[:, :n], in_=e_[:, :n], func=AF.Ln, bias=1.0), reads=[e_.res], writes=[e_.res])
                m01 = mask01[:].rearrange("p a b -> p (a b)")[:, :n]
                if d == 0:
                    S.op(S.dve, lambda h, n=n, e_=e_, c_=c_, m01=m01: h.tensor_tensor_scan(out=c_[:, :n], data0=m01, data1=e_[:, :n], initial=0.0, op0=ALU.mult, op1=ALU.add),
                         reads=[e_.res, mask01.res], writes=[c_.res])
                    last = 63
                else:
                    S.op(S.dve, lambda h, n=n, e_=e_, c_=c_, m01=m01: h.tensor_tensor_scan(out=rev_last(c_[:, :n]), data0=m01, data1=rev_last(e_[:, :n]), initial=0.0,
                                                                                         op0=ALU.mult, op1=ALU.add),
                         reads=[e_.res, mask01.res], writes=[c_.res])
                    last = 0
                EB, ENB, EBL = eb[d][c2], enb[d][c2], ebl[d][c2]
                S.op(S.act, lambda h, n=n, c_=c_, EB=EB: h.activation(out=EB[:, :n], in_=c_[:, :n], func=AF.Exp, scale=-1.0 / 16), reads=[c_.res], writes=[EB.res])
                S.op(S.act, lambda h, n=n, c_=c_, ENB=ENB: h.activation(out=ENB[:, :n], in_=c_[:, :n], func=AF.Exp, scale=1.0 / 16), reads=[c_.res], writes=[ENB.res])
                c3 = c_[:, :n].rearrange("p (a b) -> p a b", b=64)
                S.op(S.pool, lambda h, n=n, c_=c_, c3=c3, last=last, nch=nch: h.tensor_tensor(out=c3, in0=c3, in1=c3[:, :, last:last + 1].to_broadcast([128, nch, 64]), op=ALU.subtract),
                     reads=[c_.res], writes=[c_.res])
                S.op(S.act, lambda h, n=n, c_=c_, EBL=EBL: h.activation(out=EBL[:, :n], in_=c_[:, :n], func=AF.Exp, scale=1.0 / 16), reads=[c_.res], writes=[EBL.res])
                ch0 = u0 // 64
                elc = elcs[nxt("elc", 2)]
                S.op(S.pool, lambda h, n=n, EB=EB, last=last, elc=elc, nch=nch: h.tensor_copy(
                    out=elc[:, :nch], in_=EB[:, :n].rearrange("p (a b) -> p a b", b=64)[:, :, last]),
                     reads=[EB.res], writes=[elc.res])
                S.dma(S.sp, [(self.EL[:, 2 * c2 + hh2, d, ch0:ch0 + nch], elc[hh2 * 64:(hh2 + 1) * 64, :nch]) for hh2 in range(2)],
                      reads=[elc.res], writes=[self.EL.res])

    def a_mm(k, s):
        hT = hTs[k % 2]
        c0 = s * 128
        for kc in range(8):
            S.op(S.pe, lambda h, kc=kc, c0=c0, hT=hT: h.matmul(pq[:], hT[:, kc, c0:c0 + 128], Wa[:, kc, 0:512], start=(kc == 0), stop=(kc == 7)),
                 reads=[hT.res, Wa.res], writes=[pq.res], inc=(kc == 7))
        for kc in range(8):
            S.op(S.pe, lambda h, kc=kc, c0=c0, hT=hT: h.matmul(pkv[:], hT[:, kc, c0:c0 + 128], Wa[:, kc, 512:768], start=(kc == 0), stop=(kc == 7)),
                 reads=[hT.res, Wa.res], writes=[pkv.res], inc=(kc == 7))

    def a_chain(k, s, cst=None):
        si, t0, n = tiles[k]
        A, tmp, src, uoff, rope = st_info[si]
        u0 = uoff + t0
        c0 = s * 128
        S.op(S.act, lambda h: h.activation(out=sqt[:, 0:512], in_=pq[:], func=AF.Square), reads=[pq.res], writes=[sqt.res])
        S.op(S.act, lambda h: h.activation(out=sqt[:, 512:640], in_=pkv[:, 0:128], func=AF.Square), reads=[pkv.res], writes=[sqt.res])
        vi = nxt("vb", 4)
        S.op(S.act, lambda h, vi=vi: h.activation(out=vb[vi][:], in_=pkv[:, 128:256], func=AF.Copy), reads=[pkv.res], writes=[vb[vi].res])
        S.dma(S.sp, [(self.VA[l][u0 + c0:u0 + c0 + 128, :], vb[vi][:])], reads=[vb[vi].res], writes=[self.R(self.VA[l].name)])
        S.op(S.dve, lambda h: h.tensor_reduce(out=ssh[:], in_=sqt[:].rearrange("p (a b) -> p a b", b=64), axis=AX.X, op=ALU.add),
             reads=[sqt.res], writes=[ssh.res])
        S.op(S.act, lambda h: h.activation(out=rinv[:], in_=ssh[:], func=AF.Sqrt, scale=1.0 / 64, bias=self.eps_col[:]),
             reads=[ssh.res, self.eps_col.res], writes=[rinv.res])
        S.op(S.dve, lambda h: h.reciprocal(out=rinv[:], in_=rinv[:]), reads=[rinv.res], writes=[rinv.res])
        S.op(S.dve, lambda h: h.tensor_tensor(out=qn[:, 0:8, :], in0=pq[:].rearrange("p (a b) -> p a b", b=64),
                                               in1=rinv[:, 0:8].unsqueeze(2).to_broadcast([128, 8, 64]), op=ALU.mult),
             reads=[pq.res, rinv.res], writes=[qn.res])
        S.op(S.dve, lambda h: h.tensor_tensor(out=qn[:, 8:10, :], in0=pkv[:, 0:128].rearrange("p (a b) -> p a b", b=64),
                                               in1=rinv[:, 8:10].unsqueeze(2).to_broadcast([128, 2, 64]), op=ALU.mult),
             reads=[pkv.res, rinv.res], writes=[qn.res])
        S.op(S.pool, lambda h: h.tensor_tensor(out=qn[:], in0=qn[:], in1=gain[:], op=ALU.mult),
             reads=[qn.res, gain.res], writes=[qn.res])
        q_ = qr[nxt("qr", 2)]
        if rope:
            cosb = cst[:, 0:1, :].to_broadcast([128, 10, 32])
            sinb = cst[:, 1:2, :].to_broadcast([128, 10, 32])
            x1 = qn[:, :, 0:32]
            x2 = qn[:, :, 32:64]
            r = [rt[nxt("rt", 4)] for _ in range(4)]
            S.op(S.pool, lambda h, r=r, cosb=cosb, x1=x1: h.tensor_tensor(out=r[0][:], in0=x1, in1=cosb, op=ALU.mult),
                 reads=[qn.res, cst.res], writes=[r[0].res])
            S.op(S.dve, lambda h, r=r, sinb=sinb, x2=x2: h.tensor_tensor(out=r[1][:], in0=x2, in1=sinb, op=ALU.mult),
                 reads=[qn.res, cst.res], writes=[r[1].res])
            S.op(S.dve, lambda h, r=r, q_=q_: h.tensor_tensor(out=q_[:, :, 0:32], in0=r[0][:], in1=r[1][:], op=ALU.subtract),
                 reads=[r[0].res, r[1].res], writes=[q_.res])
            S.op(S.pool, lambda h, r=r, sinb=sinb, x1=x1: h.tensor_tensor(out=r[2][:], in0=x1, in1=sinb, op=ALU.mult),
                 reads=[qn.res, cst.res], writes=[r[2].res])
            S.op(S.dve, lambda h, r=r, cosb=cosb, x2=x2: h.tensor_tensor(out=r[3][:], in0=x2, in1=cosb, op=ALU.mult),
                 reads=[qn.res, cst.res], writes=[r[3].res])
            S.op(S.dve, lambda h, r=r, q_=q_: h.tensor_tensor(out=q_[:, :, 32:64], in0=r[2][:], in1=r[3][:], op=ALU.add),
                 reads=[r[2].res, r[3].res], writes=[q_.res])
        else:
            S.op(S.dve, lambda h, q_=q_: h.tensor_copy(out=q_[:], in_=qn[:]), reads=[qn.res], writes=[q_.res])
        return q_

    def a_tr(q_, s):
        c0 = s * 128
        for hh in range(8):
            S.op(S.pe, lambda h, hh=hh, q_=q_: h.transpose(out=pqt[:, hh, :], in_=q_[:, hh, :], identity=self.ident[:]),
                 reads=[q_.res, self.ident.res], writes=[pqt.res], inc=(hh == 7))
        for hh in range(2):
            S.op(S.pe, lambda h, hh=hh, q_=q_: h.transpose(out=pkt[:, hh, :], in_=q_[:, 8 + hh, :], identity=self.ident[:]),
                 reads=[q_.res, self.ident.res], writes=[pkt.res], inc=(hh == 1))
        S.op(S.act, lambda h, c0=c0: h.activation(out=QTs[:, :, c0:c0 + 128], in_=pqt[:], func=AF.Copy), reads=[pqt.res], writes=[QTs.res])
        S.op(S.dve, lambda h, c0=c0: h.tensor_copy(out=KTs[:, :, c0:c0 + 128], in_=pkt[:]), reads=[pkt.res], writes=[KTs.res])

    def b_part(k, s):
        si, t0, n = tiles[k]
        uoff = st_info[si][3]
        u0 = uoff + t0
        hT = hTs[k % 2]
        c0 = s * 128
        for half in range(2):
            kk = nxt("pf", 2)
            for kc in range(8):
                S.op(S.pe, lambda h, kc=kc, c0=c0, kk=kk, half=half, hT=hT: h.matmul(pf[kk][:], hT[:, kc, c0:c0 + 128], Wv[:, kc, half * 512:(half + 1) * 512],
                                                                                  start=(kc == 0), stop=(kc == 7)),
                     reads=[hT.res, Wv.res], writes=[pf[kk].res], inc=(kc == 7))
            vi = nxt("vb2", 4)
            if half == 0:
                S.op(S.act, lambda h, kk=kk, vi=vi: h.activation(out=vb2[vi][:], in_=pf[kk][:], func=AF.Copy), reads=[pf[kk].res], writes=[vb2[vi].res])
            else:
                S.op(S.dve, lambda h, kk=kk, vi=vi: h.tensor_copy(out=vb2[vi][:], in_=pf[kk][:]), reads=[pf[kk].res], writes=[vb2[vi].res])
            dstv = self.GV[l] if half == 0 else self.MV[l]
            S.dma(S.sp, [(dstv[u0 + c0:u0 + c0 + 128, :], vb2[vi][:])], reads=[vb2[vi].res], writes=[self.R(dstv.name)])

    def c_part(k, fcs):
        si, t0, n = tiles[k]
        uoff = st_info[si][3]
        u0 = uoff + t0
        hT = hTs[k % 2]
        for fc in fcs:
            kk = nxt("pf", 2)
            for kc in range(8):
                S.op(S.pe, lambda h, kc=kc, fc=fc, kk=kk, n=n, hT=hT: h.matmul(pf[kk][:, :n], Wf[:, kc, fc * 128:(fc + 1) * 128], hT[:, kc, :n], start=(kc == 0), stop=(kc == 7)),
                     reads=[hT.res, Wf.res], writes=[pf[kk].res], inc=(kc == 7))
            if fc < 2:
                for d in range(2):
                    o_ = fo[nxt("fo", 8)]
                    S.op(S.dve, lambda h, kk=kk, n=n, d=d, fc=fc, o_=o_: h.scalar_tensor_tensor(out=o_[:, :n], in0=pf[kk][:, :n], scalar=0.125, in1=eb[d][fc][:, :n],
                                                                                           op0=ALU.mult, op1=ALU.mult),
                         reads=[pf[kk].res, eb[d][fc].res], writes=[o_.res])
                    S.dma(S.pool, [(self.QG[l][d, :, 2 * fc + hh2, u0:u0 + n], o_[hh2 * 64:(hh2 + 1) * 64, :n]) for hh2 in range(2)], reads=[o_.res], writes=[self.R(self.QG[l].name)])
            elif fc < 4:
                c2 = fc - 2
                for d in range(2):
                    o_ = fo[nxt("fo", 8)]
                    S.op(S.dve, lambda h, kk=kk, n=n, d=d, c2=c2, o_=o_: h.tensor_tensor(out=o_[:, :n], in0=pf[kk][:, :n], in1=enb[d][c2][:, :n], op=ALU.mult),
                         reads=[pf[kk].res, enb[d][c2].res], writes=[o_.res])
                    S.dma(S.pool, [(self.KG[l][d, :, 2 * c2 + hh2, u0:u0 + n], o_[hh2 * 64:(hh2 + 1) * 64, :n]) for hh2 in range(2)], reads=[o_.res], writes=[self.R(self.KG[l].name)])
                    o_ = fo[nxt("fo", 8)]
                    S.op(S.dve, lambda h, kk=kk, n=n, d=d, c2=c2, o_=o_: h.tensor_tensor(out=o_[:, :n], in0=pf[kk][:, :n], in1=ebl[d][c2][:, :n], op=ALU.mult),
                         reads=[pf[kk].res, ebl[d][c2].res], writes=[o_.res])
                    S.dma(S.pool, [(self.KH[l][d, :, 2 * c2 + hh2, u0:u0 + n], o_[hh2 * 64:(hh2 + 1) * 64, :n]) for hh2 in range(2)], reads=[o_.res], writes=[self.R(self.KH[l].name)])
            else:
                o_ = fo[nxt("fo", 8)]
                S.op(S.act, lambda h, kk=kk, n=n, o_=o_: h.activation(out=o_[:, :n], in_=pf[kk][:, :n], func=AF.Copy), reads=[pf[kk].res], writes=[o_.res])
                r0 = (fc - 4) * 128
                S.dma(S.pool, [(self.MQK[l][r0:r0 + 128, 2 + u0:2 + u0 + n], o_[:, :n])], reads=[o_.res], writes=[self.R(self.MQK[l].name)])

    for s in range(tiles[0][2] // 128):
        prep_sub(0, s)
    for k, (si, t0, n) in enumerate(tiles):
        A, tmp, src, uoff, rope_ = st_info[si]
        rope = rope_
        nt = n // 128
        u0 = uoff + t0
        hT = hTs[k % 2]
        S.dma(S.sp, [(H2Tv[:, :, u0:u0 + n], hT[:, :, :n])], reads=[hT.res], writes=[self.R(self.H2T[l].name)])
        g_part(k)
        order = [4, 5, 6, 7, 8, 9, 10, 11, 0, 1, 2, 3]
        per = (12 + nt - 1) // nt
        pend = None
        nxt_nt = tiles[k + 1][2] // 128 if k + 1 < len(tiles) else 0
        for s in range(nt):
            xi = prep_load(k + 1, s) if s < nxt_nt else None
            cst = rope_load(k, s)
            a_mm(k, s)
            q_ = a_chain(k, s, cst)
            if pend is not None:
                a_tr(*pend)
            pend = (q_, s)
            b_part(k, s)
            c_part(k, order[s * per:(s + 1) * per])
            if s < nxt_nt:
                prep_sub(k + 1, s, xi)
        a_tr(*pend)
        for s in range(nt, nxt_nt):
            prep_sub(k + 1, s)
        S.dma(S.sp, [(self.QT[l][:, :, u0:u0 + n], QTs[:, :, :n])], reads=[QTs.res], writes=[self.R(self.QT[l].name)])
        S.dma(S.sp, [(self.KT[l][:, :, u0:u0 + n], KTs[:, :, :n])], reads=[KTs.res], writes=[self.R(self.KT[l].name)])


Builder.phase_feat = phase_feat


def phase_attn(self, l, do_ctx):
    S = self.S
    T = self.T
    nbk = T // 128
    ones = self.sb("ones", [128, 128], F32)
    S.op(S.pool, lambda h: h.memset(ones[:], 1.0), writes=[ones.res])
    mP = self.sb("mP", [128, 4, 128], BF16)
    mN = self.sb("mN", [128, 4, 128], BF16)
    mtmp = self.sb("mtmp", [128, 128], F32)
    zer = self.sb("zer", [128, 128], F32)
    S.op(S.pool, lambda h: h.memset(zer[:], 0.0), writes=[zer.res])
    for (m_, sgn) in ((mP, 1), (mN, -1)):
        S.op(S.pool, lambda h, sgn=sgn: h.affine_select(out=mtmp[:], in_=zer[:], pattern=[[-sgn, 128]], compare_op=ALU.is_ge, fill=-30000.0,
                                                         base=0, channel_multiplier=sgn), reads=[zer.res], writes=[mtmp.res])
        S.op(S.pool, lambda h, m_=m_: h.tensor_copy(out=m_[:], in_=mtmp[:].unsqueeze(1).to_broadcast([128, 4, 128])), reads=[mtmp.res], writes=[m_.res])
    esk = self.sb("esk", [128, 2, 4, 128], F32)
    sk8 = self.sb("sk8", [128, 8], F32)
    S.dma(S.sp, [(sk8[64:65, :], self.attn_sink[l:l + 1, :])], writes=[sk8.res])
    S.op(S.act, lambda h: h.activation(out=sk8[64:65, :], in_=sk8[64:65, :], func=AF.Exp), reads=[sk8.res], writes=[sk8.res])
    S.op(S.dve, lambda h: h.tensor_copy(out=esk[64:65].rearrange("p g a b -> p (g a) b"), in_=sk8[64:65, :].unsqueeze(2).to_broadcast([1, 8, 128])),
         reads=[sk8.res], writes=[esk.res])
    KTc = self.sb("KTc", [64, 2, 256], BF16)
    S.dma(S.sp, [(KTc[:], self.KT[l][:, :, 0:256])], reads=[self.R(self.KT[l].name)], writes=[KTc.res])
    Vc = [self.sb(f"Vc{j}", [128, 2, 65], BF16) for j in range(2)]
    Vb = [self.sb(f"Vb{j}", [128, 2, 65], BF16) for j in range(4)]
    KTb = [self.sb(f"KTb{j}", [64, 2, 128], BF16) for j in range(4)]
    for v in Vc + Vb:
        S.op(S.pool, lambda h, v=v: h.memset(v[:], 1.0), writes=[v.res])
    for j in range(2):
        S.dma(S.sp, [(Vc[j][:, :, 0:64], self.VA[l][j * 128:(j + 1) * 128, :].rearrange("p (g d) -> p g d", d=64))],
              reads=[self.R(self.VA[l].name)], writes=[Vc[j].res])
    QTb = [self.sb(f"QTb{j}", [64, 8, 128], BF16) for j in range(2)]
    E = [self.sb(f"E{j}", [128, 4, 128], BF16) for j in range(4)]
    dn = [self.sb(f"dn{j}", [128, 512], F32) for j in range(2)]
    bcs = [self.sb(f"bcs{j}", [64, 512], F32) for j in range(2)]
    aT = [self.sb(f"aT{j}", [64, 4, 128], BF16) for j in range(2)]
    pS = [self.ps(f"pS{j}", [128, 512]) for j in range(3)]
    pO = [self.ps(f"pO{j}", [128, 512]) for j in range(2)]
    pB = [self.ps(f"pB{j}", [64, 512]) for j in range(2)]
    cnt = {}

    def nxt(key, n):
        v = cnt.get(key, 0)
        cnt[key] = v + 1
        return v % n

    def load_kb(m):
        i = m % 4
        u = LC + m * 128
        S.dma(S.sp, [(KTb[i][:], self.KT[l][:, :, u:u + 128])], reads=[self.R(self.KT[l].name)], writes=[KTb[i].res])
        S.dma(S.sp, [(Vb[i][:, :, 0:64], self.VA[l][u:u + 128, :].rearrange("p (g d) -> p g d", d=64))],
              reads=[self.R(self.VA[l].name)], writes=[Vb[i].res])

    pending = []

    def norm(g, po, u0):
        d_ = dn[nxt("dn", 2)]
        S.op(S.dve, lambda h, d_=d_, po=po, g=g: h.tensor_tensor(out=d_[64:65, :], in0=po[64:65, :], in1=esk[64:65, g].rearrange("p a b -> p (a b)"), op=ALU.add),
             reads=[po.res, esk.res], writes=[d_.res])
        S.op(S.dve, lambda h, d_=d_: h.reciprocal(out=d_[64:65, :], in_=d_[64:65, :]), reads=[d_.res], writes=[d_.res])
        pb = pB[nxt("pb", 2)]
        S.op(S.pe, lambda h, d_=d_, pb=pb: h.matmul(pb[:], ones[64:65, 0:64], d_[64:65, :], start=True, stop=True),
             reads=[d_.res, ones.res], writes=[pb.res])
        b_ = bcs[nxt("bcs", 2)]
        S.op(S.act, lambda h, b_=b_, pb=pb: h.activation(out=b_[:], in_=pb[:], func=AF.Copy), reads=[pb.res], writes=[b_.res])
        a_ = aT[nxt("aT", 2)]
        S.op(S.dve, lambda h, a_=a_, b_=b_, po=po: h.tensor_tensor(out=a_[:].rearrange("p a b -> p (a b)"), in0=po[0:64, :], in1=b_[:], op=ALU.mult),
             reads=[po.res, b_.res], writes=[a_.res])
        S.dma(S.pool, [(self.ATT[l][:, 4 * g:4 * g + 4, u0:u0 + 128], a_[:])], reads=[a_.res], writes=[self.R(self.ATT[l].name)])

    def qblock(u0, kbs):
        qi = nxt("q", 2)
        Q = QTb[qi]
        S.dma(S.sp, [(Q[:], self.QT[l][:, :, u0:u0 + 128])], reads=[self.R(self.QT[l].name)], writes=[Q.res])
        for g in range(2):
            po = pO[nxt("po", 2)]
            rhsq = Q[:, 4 * g:4 * g + 4, :].rearrange("p a b -> p (a b)")

            def score(idx, g=g, rhsq=rhsq):
                kt, vt, msk = kbs[idx]
                p = pS[nxt("ps", 3)]
                S.op(S.pe, lambda h, kt=kt, p=p, g=g, rhsq=rhsq, msk=msk: h.matmul(p[:], kt[0][:, g, kt[1]:kt[1] + 128], rhsq, start=True, stop=(msk is None)),
                     reads=[kt[0].res, Q.res], writes=[p.res], inc=(msk is None))
                if msk is not None:
                    S.op(S.pe, lambda h, p=p, msk=msk: h.matmul(p[:], self.ident[:], msk[:].rearrange("p a b -> p (a b)"), start=False, stop=True),
                         reads=[self.ident.res, msk.res], writes=[p.res])
                return p
            ps_list = [score(0)]
            if len(kbs) > 1:
                ps_list.append(score(1))
            for idx in range(len(kbs)):
                kt, vt, msk = kbs[idx]
                if idx + 2 < len(kbs):
                    ps_list.append(score(idx + 2))
                p = ps_list[idx]
                e = E[nxt("e", 4)]
                S.op(S.act, lambda h, p=p, e=e: h.activation(out=e[:].rearrange("p a b -> p (a b)"), in_=p[:], func=AF.Exp, scale=0.125),
                     reads=[p.res], writes=[e.res])
                S.op(S.pe, lambda h, e=e, vt=vt, po=po, idx=idx, g=g, kbs=kbs: h.matmul(po[0:65, :], vt[:, g, :], e[:].rearrange("p a b -> p (a b)"),
                                                                                      start=(idx == 0), stop=(idx == len(kbs) - 1)),
                     reads=[e.res, vt.res], writes=[po.res], inc=(idx == len(kbs) - 1))
            pending.append((g, po, u0))
            if len(pending) > 1:
                norm(*pending.pop(0))

    ckb = [((KTc, 0), Vc[0], None), ((KTc, 128), Vc[1], None)]
    if do_ctx:
        for n in range(2):
            qblock(n * 128, ckb)
    load_kb(0)
    for n in range(nbk):
        if n + 1 < nbk:
            load_kb(n + 1)
        kbs = []
        if n - 1 >= 0:
            kbs.append(((KTb[(n - 1) % 4], 0), Vb[(n - 1) % 4], mP))
        kbs.append(((KTb[n % 4], 0), Vb[n % 4], None))
        if n + 1 < nbk:
            kbs.append(((KTb[(n + 1) % 4], 0), Vb[(n + 1) % 4], mN))
        qblock(LC + n * 128, kbs + ckb)
    while pending:
        norm(*pending.pop(0))


Builder.phase_attn = phase_attn


def scan_groups(T):
    return [(0, LC)] + [(LC + t0, min(512, T - t0)) for t0 in range(0, T, 512)]


def scan_order(T, d):
    groups = scan_groups(T)
    order = []
    if d == 0:
        for gi, (u0, n) in enumerate(groups):
            for c in range(n // 64):
                order.append((gi, c))
    else:
        gis = [0] + list(range(len(groups) - 1, 0, -1))
        for gi in gis:
            u0, n = groups[gi]
            for c in range(n // 64 - 1, -1, -1):
                order.append((gi, c))
    return order


def phase_gla(self, l):
    S = self.S
    T = self.T
    EL = self.EL
    groups = scan_groups(T)
    ones = self.sb("ones", [64, 64], F32)
    S.op(S.pool, lambda h: h.memset(ones[:], 1.0), writes=[ones.res])
    mtmp = self.sb("mtmp", [64, 64], F32)
    msk = [self.sb(f"msk{d}", [64, 64], BF16) for d in range(2)]
    for d, sgn in ((0, -1), (1, 1)):
        S.op(S.pool, lambda h, sgn=sgn: h.affine_select(out=mtmp[:], in_=ones[:], pattern=[[-sgn, 64]], compare_op=ALU.is_ge, fill=0.0,
                                                         base=0, channel_multiplier=sgn), reads=[ones.res], writes=[mtmp.res])
        S.op(S.pool, lambda h, d=d: h.tensor_copy(out=msk[d][:], in_=mtmp[:]), reads=[mtmp.res], writes=[msk[d].res])
    Sf = [self.sb(f"Sf{d}", [64, 4, 128], F32) for d in range(2)]
    Sb = [self.sb(f"Sb{d}", [64, 4, 128], BF16) for d in range(2)]
    for d in range(2):
        S.op(S.pool, lambda h, d=d: h.memset(Sf[d][:], 0.0), writes=[Sf[d].res])
        S.op(S.pool, lambda h, d=d: h.memset(Sb[d][:], 0.0), writes=[Sb[d].res])
    qg = [[self.sb(f"qg{d}{i}", [64, 4, 512], BF16) for i in range(2)] for d in range(2)]
    kg = [[self.sb(f"kg{d}{i}", [64, 4, 512], BF16) for i in range(2)] for d in range(2)]
    kh = [[self.sb(f"kh{d}{i}", [64, 4, 512], BF16) for i in range(2)] for d in range(2)]
    vg = [[self.sb(f"vg{d}{i}", [64, 8, 512], BF16) for i in range(2)] for d in range(2)]
    am = [[self.sb(f"am{d}{i}", [64, 4, 64], BF16) for i in range(2)] for d in range(2)]
    kt = [[self.sb(f"kt{d}{i}", [64, 4, 64], BF16) for i in range(2)] for d in range(2)]
    ob = [[self.sb(f"ob{d}{i}", [64, 512], F32) for i in range(2)] for d in range(2)]
    pA = [self.ps(f"pA{d}", [64, 256]) for d in range(2)]
    pK = [self.ps(f"pK{d}", [64, 256], BF16) for d in range(2)]
    pO = [self.ps(f"pO{d}", [64, 512]) for d in range(2)]
    pN = [self.ps(f"pN{d}", [64, 512]) for d in range(2)]
    orders = [scan_order(T, d) for d in range(2)]
    nsteps = len(orders[0])
    gcount = [0, 0]
    cur = [None, None]

    def load_group(d, gi):
        i = gcount[d] % 2
        gcount[d] += 1
        u0, n = groups[gi]
        nch = n // 64
        for (dst, srcT) in ((qg[d][i], self.QG[l]), (kg[d][i], self.KG[l]), (kh[d][i], self.KH[l])):
            S.dma(S.sp, [(dst[:, :, :n], srcT[d, :, :, u0:u0 + n])], reads=[self.R(srcT.name)], writes=[dst.res])
        S.dma(S.sp, [(vg[d][i][:, :nch, :], self.GV[l][u0:u0 + n, :].rearrange("(c p) f -> p c f", p=64))], reads=[self.R(self.GV[l].name)],
              writes=[vg[d][i].res])
        return i

    ctxs = {}

    def stageA(step, d):
        gi, c = orders[d][step]
        if cur[d] is None or cur[d][0] != gi:
            cur[d] = (gi, load_group(d, gi))
        bi = cur[d][1]
        u0, n = groups[gi]
        o = c * 64
        Q, Kg, Kh, V = qg[d][bi], kg[d][bi], kh[d][bi], vg[d][bi]
        k2 = step % 2
        AM, KTt = am[d][k2], kt[d][k2]
        ctxs[(step, d)] = (Q, V, AM, KTt, u0, o, c)
        for hh in range(4):
            S.op(S.pe, lambda h, hh=hh, d=d, Kg=Kg, Q=Q, o=o: h.matmul(pA[d][:, hh * 64:(hh + 1) * 64], Kg[:, hh, o:o + 64], Q[:, hh, o:o + 64], start=True, stop=True),
                 reads=[Kg.res, Q.res], writes=[pA[d].res], inc=(hh == 3))
        for hh in range(4):
            S.op(S.pe, lambda h, hh=hh, d=d, Kh=Kh, o=o: h.transpose(out=pK[d][:, hh * 64:(hh + 1) * 64], in_=Kh[:, hh, o:o + 64], identity=self.ident[0:64, 0:64]),
                 reads=[Kh.res, self.ident.res], writes=[pK[d].res], inc=(hh == 3))
        S.op(S.dve, lambda h, d=d, AM=AM: h.tensor_tensor(out=AM[:], in0=pA[d][:].rearrange("p (a b) -> p a b", b=64),
                                                          in1=msk[d][:].unsqueeze(1).to_broadcast([64, 4, 64]), op=ALU.mult),
             reads=[pA[d].res, msk[d].res], writes=[AM.res])
        S.op(S.act, lambda h, d=d, KTt=KTt: h.activation(out=KTt[:].rearrange("p a b -> p (a b)"), in_=pK[d][:], func=AF.Copy), reads=[pK[d].res], writes=[KTt.res])

    def stageB(step, d):
        Q, V, AM, KTt, u0, o, c = ctxs.pop((step, d))
        chunk = (u0 + o) // 64
        OB = ob[d][step % 2]
        for hh in range(4):
            S.op(S.pe, lambda h, hh=hh, d=d, KTt=KTt, V=V, c=c: h.matmul(pN[d][:, hh * 128:(hh + 1) * 128], KTt[:, hh, :], V[:, c, hh * 128:(hh + 1) * 128], start=True, stop=True),
                 reads=[KTt.res, V.res], writes=[pN[d].res], inc=(hh == 3))
        for hh in range(4):
            S.op(S.pe, lambda h, hh=hh, d=d, AM=AM, V=V, c=c: h.matmul(pO[d][:, hh * 128:(hh + 1) * 128], AM[:, hh, :], V[:, c, hh * 128:(hh + 1) * 128], start=True, stop=False),
                 reads=[AM.res, V.res], writes=[pO[d].res], inc=False)
            S.op(S.pe, lambda h, hh=hh, d=d, Q=Q, o=o: h.matmul(pO[d][:, hh * 128:(hh + 1) * 128], Q[:, hh, o:o + 64], Sb[d][:, hh, :], start=False, stop=True),
                 reads=[Q.res, Sb[d].res], writes=[pO[d].res], inc=(hh == 3))
        S.op(S.act, lambda h, d=d, OB=OB: h.activation(out=OB[:], in_=pO[d][:], func=AF.Copy), reads=[pO[d].res], writes=[OB.res])
        S.dma(S.sp, [(self.OG[l][d, u0 + o:u0 + o + 64, :], OB[:])], reads=[OB.res], writes=[self.R(self.OG[l].name)])
        S.op(S.dve, lambda h, d=d, chunk=chunk: h.tensor_tensor(out=Sf[d][:], in0=Sf[d][:], in1=EL[:, :, d, chunk:chunk + 1].to_broadcast([64, 4, 128]), op=ALU.mult),
             reads=[Sf[d].res, self.EL.res], writes=[Sf[d].res])
        S.op(S.dve, lambda h, d=d: h.tensor_tensor(out=Sf[d][:].rearrange("p a b -> p (a b)"), in0=Sf[d][:].rearrange("p a b -> p (a b)"), in1=pN[d][:], op=ALU.add),
             reads=[Sf[d].res, pN[d].res], writes=[Sf[d].res])
        S.op(S.act, lambda h, d=d: h.activation(out=Sb[d][:], in_=Sf[d][:], func=AF.Copy), reads=[Sf[d].res], writes=[Sb[d].res])

    for d in range(2):
        stageA(0, d)
    for step in range(nsteps):
        if step + 1 < nsteps:
            for d in range(2):
                stageA(step + 1, d)
        for d in range(2):
            stageB(step, d)


Builder.phase_gla = phase_gla


LN_KS = float(-0.5 * np.log(128.0))


def phase_ml_gates(self, l):
    S = self.S
    sel = self.sel
    DEC = self.DEC
    T = self.T
    TT = self.TT
    nch = TT // 64
    bA = self.sb("bA", [4, TT], F32)
    bL = self.sb("bL", [4, TT], F32)
    bC = self.sb("bC", [4, TT], F32)
    bG = self.sb("bG", [4, TT], F32)
    bX = self.sb("bX", [4, TT], F32)
    onesr = self.sb("onesr", [4, TT], BF16)
    S.op(S.pool, lambda h: h.memset(onesr[:], 1.0), writes=[onesr.res])
    gl = self.sb("gl", [4, nch], F32)
    gp = self.sb("gp", [4, nch], F32)
    dd = self.sb("dd", [4, nch], F32)
    ibc = self.sb("ibc", [4, 2], F32)
    pD = self.ps("pD", [128, 512])
    for d in range(2):
        S.dma(S.sp, [(bA[:], self.GATES[l][d * 4:(d + 1) * 4, :])], reads=[self.R(self.GATES[l].name)], writes=[bA.res])
        S.dma(S.sp, [(bL[:], self.GATES[l][8 + d * 4:8 + (d + 1) * 4, :])], reads=[self.R(self.GATES[l].name)], writes=[bL.res])
        S.dma(S.sp, [(ibc[:, 0:1], self.mlstm_ib[l, d, :].rearrange("(h o) -> h o", o=1)), (ibc[:, 1:2], self.mlstm_fb[l, d, :].rearrange("(h o) -> h o", o=1))],
              writes=[ibc.res])
        S.op(S.dve, lambda h: h.tensor_scalar(out=ibc[:, 1:2], in0=ibc[:, 1:2], scalar1=-1.0, scalar2=None, op0=ALU.mult), reads=[ibc.res], writes=[ibc.res])
        S.op(S.act, lambda h: h.activation(out=bL[:], in_=bL[:], func=AF.Exp, scale=-1.0, bias=ibc[:, 1:2]), reads=[bL.res, ibc.res], writes=[bL.res])
        S.op(S.act, lambda h: h.activation(out=bL[:], in_=bL[:], func=AF.Ln, bias=1.0), reads=[bL.res], writes=[bL.res])

        def scan(out, src, op1):
            if d == 0:
                S.op(S.dve, lambda h: h.tensor_tensor_scan(out=out[:], data0=onesr[:], data1=src[:], initial=0.0, op0=ALU.mult, op1=op1),
                     reads=[src.res, onesr.res], writes=[out.res])
            else:
                S.op(S.dve, lambda h: h.tensor_tensor_scan(out=rev_last(out[:, 0:LC]), data0=onesr[:, 0:LC], data1=rev_last(src[:, 0:LC]), initial=0.0, op0=ALU.mult, op1=op1),
                     reads=[src.res, onesr.res], writes=[out.res])
                S.op(S.dve, lambda h: h.tensor_tensor_scan(out=rev_last(out[:, LC:TT]), data0=onesr[:, LC:TT], data1=rev_last(src[:, LC:TT]), initial=out[:, 0:1], op0=ALU.mult, op1=op1),
                     reads=[src.res, onesr.res, out.res], writes=[out.res])
        scan(bC, bL, ALU.add)
        S.op(S.dve, lambda h: h.scalar_tensor_tensor(out=bA[:], in0=bA[:], scalar=ibc[:, 0:1], in1=bC[:], op0=ALU.add, op1=ALU.add),
             reads=[bA.res, ibc.res, bC.res], writes=[bA.res])
        scan(bG, bA, ALU.max)
        G3 = bG[:].rearrange("p (c b) -> p c b", b=64)
        lastpos = 63 if d == 0 else 0
        S.op(S.dve, lambda h, lastpos=lastpos, G3=G3: h.tensor_copy(out=gl[:], in_=G3[:, :, lastpos]), reads=[bG.res], writes=[gl.res])
        S.op(S.dve, lambda h: h.memset(gp[:], 0.0), writes=[gp.res])
        if d == 0:
            S.op(S.dve, lambda h: h.tensor_copy(out=gp[:, 1:nch], in_=gl[:, 0:nch - 1]), reads=[gl.res], writes=[gp.res])
        else:
            S.op(S.dve, lambda h: h.tensor_copy(out=gp[:, 0:3], in_=gl[:, 1:4]), reads=[gl.res], writes=[gp.res])
            S.op(S.dve, lambda h: h.tensor_copy(out=gp[:, 4:nch - 1], in_=gl[:, 5:nch]), reads=[gl.res], writes=[gp.res])
            S.op(S.dve, lambda h: h.tensor_copy(out=gp[:, nch - 1:nch], in_=gl[:, 0:1]), reads=[gl.res], writes=[gp.res])
        L3 = bL[:].rearrange("p (c b) -> p c b", b=64)
        X3 = bX[:].rearrange("p (c b) -> p c b", b=64)
        A3 = bA[:].rearrange("p (c b) -> p c b", b=64)
        S.op(S.dve, lambda h, L3=L3, G3=G3: h.tensor_tensor(out=L3, in0=gp[:].unsqueeze(2).to_broadcast([4, nch, 64]), in1=G3, op=ALU.subtract),
             reads=[gp.res, bG.res], writes=[bL.res])
        S.op(S.act, lambda h: h.activation(out=bL[:], in_=bL[:], func=AF.Exp), reads=[bL.res], writes=[bL.res])
        S.op(S.dve, lambda h: h.tensor_tensor(out=bC[:], in0=bC[:], in1=bG[:], op=ALU.subtract), reads=[bC.res, bG.res], writes=[bC.res])
        S.op(S.act, lambda h: h.activation(out=bC[:], in_=bC[:], func=AF.Exp), reads=[bC.res], writes=[bC.res])
        S.op(S.dve, lambda h, X3=X3, A3=A3: h.tensor_tensor(out=X3, in0=A3, in1=gl[:].unsqueeze(2).to_broadcast([4, nch, 64]), op=ALU.subtract),
             reads=[bA.res, gl.res], writes=[bX.res])
        S.op(S.dve, lambda h: h.tensor_scalar(out=bX[:], in0=bX[:], scalar1=LN_KS, scalar2=None, op0=ALU.add), reads=[bX.res], writes=[bX.res])
        S.op(S.act, lambda h: h.activation(out=bX[:], in_=bX[:], func=AF.Exp), reads=[bX.res], writes=[bX.res])
        S.op(S.dve, lambda h: h.tensor_tensor(out=dd[:], in0=gp[:], in1=gl[:], op=ALU.subtract), reads=[gp.res, gl.res], writes=[dd.res])
        S.op(S.act, lambda h: h.activation(out=dd[:], in_=dd[:], func=AF.Exp), reads=[dd.res], writes=[dd.res])
        for hh in range(4):
            S.op(S.pe, lambda h, hh=hh: h.matmul(pD[:, :nch], sel[:, hh, :], dd[:], start=True, stop=True), reads=[self.sel.res, dd.res], writes=[pD.res])
            S.op(S.dve, lambda h, hh=hh, d=d: h.tensor_copy(out=DEC[:, d, hh, :], in_=pD[:, :nch]), reads=[pD.res], writes=[self.DEC.res])
        for qi, buf in enumerate((bA, bG, bL, bC, bX)):
            S.dma(S.sp, [(self.MROWS[l][d, qi, :, :], buf[:])], reads=[buf.res], writes=[self.R(self.MROWS[l].name)])


def phase_ml_conv(self, l):
    S = self.S
    T = self.T
    TT = self.TT
    wcol = self.sb("wcol", [128, 8, 5], F32)
    cb = self.sb("cb", [128, 8], F32)
    S.dma(S.sp, [(wcol[:, :, k], self.conv_w[l, k, :].rearrange("(fc p) -> p fc", p=128)) for k in range(5)], writes=[wcol.res], allow_slow_non_contiguous=True)
    S.dma(S.sp, [(cb[:], self.conv_b[l, :].rearrange("(fc p) -> p fc", p=128))], writes=[cb.res], allow_slow_non_contiguous=True)
    diagw = self.sb("diagw", [128, 8, 5, 128], BF16)
    for fc in range(8):
        for k in range(5):
            e = S.dve if (fc * 5 + k) % 2 == 0 else S.pool
            S.op(e, lambda h, fc=fc, k=k: h.tensor_scalar(out=diagw[:, fc, k, :], in0=self.identf[:], scalar1=wcol[:, fc, k:k + 1], scalar2=None, op0=ALU.mult),
                 reads=[self.identf.res, wcol.res], writes=[diagw.res])
    xq = [self.sb(f"xq{i}", [128, 8, 516], BF16) for i in range(2)]
    oc = [self.sb(f"oc{i}", [128, 512], BF16) for i in range(3)]
    pc = [self.ps(f"pc{i}", [128, 512]) for i in range(2)]
    MQKv = self.MQK[l].rearrange("(c p) t -> p c t", p=128)
    k2 = 0
    for gi, (u0, n) in enumerate(scan_groups(T)):
        X = xq[gi % 2]
        S.dma(S.sp, [(X[:, 0:4, 0:n + 4], MQKv[:, 0:4, u0:u0 + n + 4]), (X[:, 4:8, 0:n + 4], MQKv[:, 4:8, u0:u0 + n + 4])], reads=[self.R(self.MQK[l].name)], writes=[X.res])
        if u0 == 0 or u0 == LC:
            S.op(S.pool, lambda h, X=X: h.memset(X[:, :, 0:2], 0.0), writes=[X.res])
        if u0 + n == LC or u0 + n == TT:
            S.op(S.pool, lambda h, X=X, n=n: h.memset(X[:, :, n + 2:n + 4], 0.0), writes=[X.res])
        for fc in range(8):
            p = pc[k2 % 2]
            o_ = oc[k2 % 3]
            k2 += 1
            for k in range(5):
                S.op(S.pe, lambda h, fc=fc, k=k, p=p, X=X, n=n: h.matmul(p[:, :n], diagw[:, fc, k, :], X[:, fc, k:k + n], start=(k == 0), stop=(k == 4)),
                     reads=[diagw.res, X.res], writes=[p.res], inc=(k == 4))
            S.op(S.act, lambda h, fc=fc, p=p, o_=o_, n=n: h.activation(out=o_[:, :n], in_=p[:, :n], func=AF.Silu, bias=cb[:, fc:fc + 1]), reads=[p.res, cb.res], writes=[o_.res])
            S.dma(S.pool, [(self.MQC[l][fc * 128:(fc + 1) * 128, u0:u0 + n], o_[:, :n])], reads=[o_.res], writes=[self.R(self.MQC[l].name)])


def phase_ml_scan(self, l):
    S = self.S
    T = self.T
    sel = self.sel
    DEC = self.DEC
    groups = scan_groups(T)
    cfill = self.sb("cfill", [64, 64], F32)
    S.op(S.pool, lambda h: h.memset(cfill[:], LN_KS), writes=[cfill.res])
    mb = [self.sb(f"mb{d}", [64, 64], F32) for d in range(2)]
    for d, sgn in ((0, -1), (1, 1)):
        S.op(S.pool, lambda h, sgn=sgn, d=d: h.affine_select(out=mb[d][:], in_=cfill[:], pattern=[[-sgn, 64]], compare_op=ALU.is_ge, fill=-30000.0,
                                                              base=0, channel_multiplier=sgn), reads=[cfill.res], writes=[mb[d].res])
    negones = self.sb("negones", [4, 128], F32)
    S.op(S.pool, lambda h: h.memset(negones[:], -1.0), writes=[negones.res])
    posones = self.sb("posones", [4, 128], F32)
    S.op(S.pool, lambda h: h.memset(posones[:], 1.0), writes=[posones.res])
    mbr = [self.sb(f"mbr{d}", [64, 4, 64], F32) for d in range(2)]
    for d in range(2):
        S.op(S.pool, lambda h, d=d: h.tensor_copy(out=mbr[d][:], in_=mb[d][:].unsqueeze(1).to_broadcast([64, 4, 64])), reads=[mb[d].res], writes=[mbr[d].res])
    Dg = [self.sb(f"Dg{i}", [4, 3, 4, 512], F32) for i in range(2)]
    DgH = [self.sb(f"DgH{i}", [4, 2, 4, 512], BF16) for i in range(2)]
    DgL = [self.sb(f"DgL{i}", [4, 2, 4, 512], BF16) for i in range(2)]
    WB = [self.sb(f"WB{i}", [128, 2, 4, 512], F32) for i in range(2)]
    posb = self.sb("posb", [4, 128], BF16)
    S.op(S.pool, lambda h: h.memset(posb[:], 1.0), writes=[posb.res])
    Cf = self.sb("Cf", [128, 4, 129], F32)
    Cb = self.sb("Cb", [128, 4, 129], BF16)
    qk = [self.sb(f"qk{i}", [128, 8, 512], BF16) for i in range(2)]
    vg = [self.sb(f"vgm{i}", [64, 8, 4, 129], BF16) for i in range(2)]
    rows = [self.sb(f"rows{i}", [4, 5, 512], F32) for i in range(2)]
    for v in vg:
        S.op(S.pool, lambda h, v=v: h.memset(v[:], 1.0), writes=[v.res])
    wT = [self.sb(f"wT{i}", [64, 256], F32) for i in range(3)]
    sT = [self.sb(f"sT{i}", [64, 4, 64], BF16) for i in range(3)]
    qks = [self.sb(f"qks{i}", [128, 8, 64], BF16) for i in range(3)]
    khat = [self.sb(f"khat{i}", [64, 4, 128], BF16) for i in range(3)]
    enm = [self.sb(f"enm{i}", [64, 4], F32) for i in range(3)]
    rr = [self.sb(f"rr{i}", [64, 4], F32) for i in range(3)]
    ho = [self.sb(f"ho{i}", [64, 4, 128], F32) for i in range(3)]
    pWS = self.ps("pWS", [64, 512])
    pB = self.ps("pB", [128, 8, 64])
    pK = self.ps("pK", [64, 4, 128], BF16)
    pO = self.ps("pO", [64, 1024])
    pN = self.ps("pN", [128, 1024])
    pO3 = pO[:].rearrange("p (h e) -> p h e", e=256)
    pN3 = pN[:].rearrange("p (h e) -> p h e", e=256)
    gcount = [0]

    def load_group(d, gi):
        i = gcount[0] % 2
        gcount[0] += 1
        u0, n = groups[gi]
        nchg = n // 64
        S.dma(S.sp, [(qk[i][:, 0:4, :n], self.MQC[l].rearrange("(c p) t -> p c t", p=128)[:, 0:4, u0:u0 + n]),
                     (qk[i][:, 4:8, :n], self.MQC[l].rearrange("(c p) t -> p c t", p=128)[:, 4:8, u0:u0 + n])], reads=[self.R(self.MQC[l].name)], writes=[qk[i].res])
        S.dma(S.sp, [(vg[i][:, c, :, 0:128], self.MV[l][u0 + c * 64:u0 + (c + 1) * 64, :].rearrange("p (h e) -> p h e", e=128)) for c in range(nchg)],
              reads=[self.R(self.MV[l].name)], writes=[vg[i].res])
        S.dma(S.sp, [(rows[i][:, :, :n], self.MROWS[l][d, :, :, u0:u0 + n].rearrange("q h t -> h q t"))], reads=[self.R(self.MROWS[l].name)], writes=[rows[i].res])
        for qi, qs in enumerate((2, 4)):
            srcr = self.MROWS[l][d, qs, :, u0:u0 + n]
            S.dma(S.sp, [(WB[i][:, qi, :, :n], bass.AP(srcr.tensor, srcr.offset, [[0, 128]] + [list(x) for x in srcr.ap]))],
                  reads=[self.R(self.MROWS[l].name)], writes=[WB[i].res])
        for qd, qs in enumerate((1,)):
            S.op(S.pool, lambda h, i=i, qd=qd, qs=qs, n=n: h.tensor_tensor(out=Dg[i][:, qd, :, :n], in0=rows[i][:, qs:qs + 1, :n].to_broadcast([4, 4, n]),
                                                                       in1=self.identf[0:4, 0:4].unsqueeze(2).to_broadcast([4, 4, n]), op=ALU.mult),
                 reads=[rows[i].res, self.identf.res], writes=[Dg[i].res])
        return i

    for d in range(2):
        S.op(S.pool, lambda h: h.memset(Cf[:], 0.0), writes=[Cf.res])
        S.op(S.pool, lambda h: h.memset(Cb[:], 0.0), writes=[Cb.res])
        order = scan_order(T, d)
        cur = None
        info = []
        for (gi, c) in order:
            if cur is None or cur[0] != gi:
                cur = (gi, None)
            info.append((gi, c))
        bufof = {}

        def stageA(step):
            gi, c = order[step]
            if gi not in bufof:
                bufof.clear()
                bufof[gi] = load_group(d, gi)
            bi = bufof[gi]
            o = c * 64
            k2 = step % 3
            QK, R_ = qk[bi], rows[bi]
            DG = Dg[bi]
            S.op(S.pe, lambda h, R_=R_, o=o: h.matmul(pWS[:, 0:256], R_[:, 0, o:o + 64], sel[:, :, 0:64], start=True, stop=False),
                 reads=[R_.res, sel.res], writes=[pWS.res], inc=False)
            S.op(S.pe, lambda h, DG=DG, o=o: h.matmul(pWS[:, 0:256], negones[:, 0:64], DG[:, 0, :, o:o + 64], start=False, stop=False),
                 reads=[DG.res, negones.res], writes=[pWS.res], inc=False)
            S.op(S.pe, lambda h, d=d: h.matmul(pWS[:, 0:256], self.identf[0:64, 0:64], mbr[d][:], start=False, stop=True),
                 reads=[self.identf.res, mbr[d].res], writes=[pWS.res], inc=False)
            for hh in range(4):
                S.op(S.pe, lambda h, hh=hh, QK=QK, o=o: h.matmul(pWS[:, 256 + hh * 64:256 + (hh + 1) * 64], QK[:, 4 + hh, o:o + 64], QK[:, hh, o:o + 64], start=True, stop=True),
                     reads=[QK.res], writes=[pWS.res], inc=(hh == 3))
            S.op(S.act, lambda h, k2=k2: h.activation(out=wT[k2][:], in_=pWS[:, 0:256], func=AF.Exp), reads=[pWS.res], writes=[wT[k2].res])
            S.op(S.dve, lambda h, k2=k2: h.tensor_tensor(out=sT[k2][:].rearrange("p a b -> p (a b)"), in0=pWS[:, 256:512], in1=wT[k2][:], op=ALU.mult),
                 reads=[pWS.res, wT[k2].res], writes=[sT[k2].res])
            S.op(S.dve, lambda h, k2=k2, QK=QK, o=o, bi=bi: h.tensor_tensor(out=qks[k2][:], in0=QK[:, :, o:o + 64], in1=WB[bi][:, :, :, o:o + 64].rearrange("p a b c -> p (a b) c"), op=ALU.mult),
                 reads=[QK.res, WB[bi].res], writes=[qks[k2].res])

        def stageA2(step):
            k2 = step % 3
            for hh in range(4):
                S.op(S.pe, lambda h, hh=hh, k2=k2: h.transpose(out=pK[:, hh, :], in_=qks[k2][:, 4 + hh, :], identity=self.ident[:]),
                     reads=[qks[k2].res, self.ident.res], writes=[pK.res], inc=(hh == 3))
            S.op(S.act, lambda h, k2=k2: h.activation(out=khat[k2][:], in_=pK[:], func=AF.Copy), reads=[pK.res], writes=[khat[k2].res])

        def stageB(step):
            gi, c = order[step]
            u0, n = groups[gi]
            o = c * 64
            chunk = (u0 + o) // 64
            k2 = step % 3
            V, R_ = vgbuf[step], rowbuf[step]
            for hh in range(4):
                S.op(S.pe, lambda h, hh=hh, k2=k2, V=V, c=c: h.matmul(pN[:, hh * 256:hh * 256 + 129], khat[k2][:, hh, :], V[:, c, hh, :], start=True, stop=True),
                     reads=[khat[k2].res, V.res], writes=[pN.res], inc=(hh == 3))
            S.op(S.pe, lambda h, R_=R_, o=o: h.matmul(pO[:, 200:204], R_[:, 3, o:o + 64], self.identf[0:4, 0:4], start=True, stop=True),
                 reads=[R_.res, self.identf.res], writes=[pO.res], inc=False)
            for hh in range(4):
                S.op(S.pe, lambda h, hh=hh, k2=k2, V=V, c=c: h.matmul(pO[:, hh * 256:hh * 256 + 129], sT[k2][:, hh, :], V[:, c, hh, :], start=True, stop=False),
                     reads=[sT[k2].res, V.res], writes=[pO.res], inc=False)
                S.op(S.pe, lambda h, hh=hh, k2=k2: h.matmul(pO[:, hh * 256:hh * 256 + 129], qks[k2][:, hh, :], Cb[:, hh, :], start=False, stop=True),
                     reads=[qks[k2].res, Cb.res], writes=[pO.res], inc=(hh == 3))
            S.op(S.act, lambda h, k2=k2: h.activation(out=enm[k2][:], in_=pO[:, 200:204], func=AF.Copy), reads=[pO.res], writes=[enm[k2].res])
            S.op(S.act, lambda h, k2=k2: h.activation(out=rr[k2][:], in_=pO3[:, :, 128], func=AF.Abs), reads=[pO.res], writes=[rr[k2].res])
            S.op(S.pool, lambda h, d=d, chunk=chunk: h.tensor_tensor(out=Cf[:], in0=Cf[:], in1=DEC[:, d, :, chunk:chunk + 1].to_broadcast([128, 4, 129]), op=ALU.mult),
                 reads=[Cf.res, DEC.res], writes=[Cf.res])
            S.op(S.dve, lambda h: h.tensor_tensor(out=Cf[:], in0=Cf[:], in1=pN3[:, :, 0:129], op=ALU.add), reads=[Cf.res, pN.res], writes=[Cf.res])
            S.op(S.act, lambda h: h.activation(out=Cb[:], in_=Cf[:], func=AF.Copy), reads=[Cf.res], writes=[Cb.res])
            S.op(S.dve, lambda h, k2=k2: h.tensor_tensor(out=rr[k2][:], in0=rr[k2][:], in1=enm[k2][:], op=ALU.max), reads=[rr[k2].res, enm[k2].res], writes=[rr[k2].res])
            S.op(S.dve, lambda h, k2=k2: h.reciprocal(out=rr[k2][:], in_=rr[k2][:]), reads=[rr[k2].res], writes=[rr[k2].res])
            S.op(S.dve, lambda h, k2=k2: h.tensor_tensor(out=ho[k2][:], in0=pO3[:, :, 0:128], in1=rr[k2][:].unsqueeze(2).to_broadcast([64, 4, 128]), op=ALU.mult),
                 reads=[pO.res, rr[k2].res], writes=[ho[k2].res])
            S.dma(S.pool, [(self.OM[l][d, u0 + o:u0 + o + 64, :], ho[k2][:].rearrange("p a b -> p (a b)"))], reads=[ho[k2].res], writes=[self.R(self.OM[l].name)])

        vgbuf = {}
        rowbuf = {}

        def A(step):
            stageA(step)
            gi, c = order[step]
            vgbuf[step] = vg[bufof[gi]]
            rowbuf[step] = rows[bufof[gi]]
        A(0)
        if len(order) > 1:
            A(1)
        stageA2(0)
        for step in range(len(order)):
            if step + 2 < len(order):
                A(step + 2)
            if step + 1 < len(order):
                stageA2(step + 1)
            stageB(step)


Builder.phase_ml_gates = phase_ml_gates
Builder.phase_ml_conv = phase_ml_conv
Builder.phase_ml_scan = phase_ml_scan


def phase_merge(self, l, streams):
    S = self.S
    win = self.w_in[l].rearrange("(kc p) n -> p kc n", p=128)
    Wm = self.sb("Wm", [128, 8, 4096], BF16)
    Wmr = [Res(f"Wm{k}") for k in range(4)]
    for ki, k0 in enumerate(range(0, 8, 2)):
        S.dma(S.pool, [(Wm[:, k0:k0 + 2, 0:512], win[:, k0:k0 + 2, O_GR:O_GR + 512]),
                       (Wm[:, k0:k0 + 2, 512:1024], win[:, k0:k0 + 2, O_MO:O_MO + 512]),
                       (Wm[:, k0:k0 + 2, 1024:4096], win[:, k0:k0 + 2, O_SA:O_SA + 3072])], writes=[Wmr[ki]])
    Woa = self.sb("Woa", [64, 8, D], BF16)
    Wog = self.sb("Wog", [128, 4, D], BF16)
    Wom = self.sb("Wom", [128, 4, D], BF16)
    Wo = self.sb("Wo", [128, 8, D], BF16)
    S.dma(S.pool, [(Woa[:], self.w_out_attn[l].rearrange("(h p) n -> p h n", p=64))], writes=[Woa.res])
    S.dma(S.pool, [(Wog[:], self.w_out_gla[l].rearrange("(c p) n -> p c n", p=128))], writes=[Wog.res])
    S.dma(S.pool, [(Wom[:], self.w_out_mlstm[l].rearrange("(c p) n -> p c n", p=128))], writes=[Wom.res])
    S.dma(S.pool, [(Wo[:, 0:4, :], self.w_o[l].rearrange("(c p) n -> p c n", p=128)[:, 0:4, :]),
                   (Wo[:, 4:8, :], self.w_o[l].rearrange("(c p) n -> p c n", p=128)[:, 4:8, :])], writes=[Wo.res])
    gains = self.sb("gains", [128, 2, 128], F32)
    S.dma(S.sp, [(gains[:, 0, :], bcast_rows(self.gla_norm[l:l + 1, :], 128)), (gains[:, 1, :], bcast_rows(self.mlstm_norm[l:l + 1, :], 128))], writes=[gains.res])
    eps_col = self.eps_col
    hT = [self.sb(f"mhT{i}", [128, 8, 128], BF16) for i in range(2)]
    aTt = [self.sb(f"maT{i}", [64, 8, 128], BF16) for i in range(2)]
    og = [self.sb(f"mog{i}", [128, 2, 512], F32) for i in range(2)]
    om = [self.sb(f"mom{i}", [128, 2, 512], F32) for i in range(2)]
    xr = [self.sb(f"mxr{i}", [128, D], F32) for i in range(2)]
    gts = [self.sb(f"mgt{i}", [128, 8, 512], F32) for i in range(2)]
    sq = self.sb("msq", [128, 512], F32)
    ssqs = [self.sb(f"mssq{i}", [128, 2, 4], F32) for i in range(2)]
    bn = [self.sb(f"mbn{i}", [128, 512], F32) for i in range(2)]
    bbs = [[self.sb(f"mbb{i}{j}", [128, 512], BF16) for j in range(2)] for i in range(2)]
    bTs = [[self.sb(f"mbT{i}{j}", [128, 4, 128], BF16) for j in range(2)] for i in range(2)]
    yb = self.sb("myb", [128, D], BF16)
    yT = self.sb("myT", [128, 8, 128], BF16)
    t1 = [self.sb(f"mt1{i}", [128, 512], F32) for i in range(3)]
    G5 = self.sb("mG5", [128, D], F32)
    pg = [self.ps(f"mpg{i}", [128, 512]) for i in range(3)]
    pT1s = [self.ps("mpT1", [128, 512], BF16)] * 2
    pT2 = self.ps("mpT2", [128, D], BF16)
    py = [self.ps(f"mpy{i}", [128, 512]) for i in range(3)]
    pY = py[0]
    cnt = {}

    def nxt(key, n):
        v = cnt.get(key, 0)
        cnt[key] = v + 1
        return v % n
    H2Tv = self.H2T[l].rearrange("(kc p) t -> p kc t", p=128)
    work = []
    for (tag, src, dst, ntok, row, uoff) in streams:
        for t0 in range(0, ntok, 128):
            work.append((tag, src, dst, row, uoff + t0, t0))

    def stage1(w, i):
        (tag, src, dst, row, u, t0) = work[w]
        H, AT, OGt, OMt, XR, gt, ssq = hT[i], aTt[i], og[i], om[i], xr[i], gts[i], ssqs[i]
        S.dma(S.sp, [(H[:], H2Tv[:, :, u:u + 128])], reads=[self.R(self.H2T[l].name)], writes=[H.res])
        S.dma(S.sp, [(AT[:], self.ATT[l][:, :, u:u + 128])], reads=[self.R(self.ATT[l].name)], writes=[AT.res])
        S.dma(S.sp, [(OGt[:, 0, :], self.OG[l][0, u:u + 128, :]), (OGt[:, 1, :], self.OG[l][1, u:u + 128, :])], reads=[self.R(self.OG[l].name)], writes=[OGt.res])
        S.dma(S.sp, [(OMt[:, 0, :], self.OM[l][0, u:u + 128, :]), (OMt[:, 1, :], self.OM[l][1, u:u + 128, :])], reads=[self.R(self.OM[l].name)], writes=[OMt.res])
        S.dma(S.sp, [(XR[:], src[t0:t0 + 128, :])], reads=[self.R(src.name)], writes=[XR.res])

    def stage1g(w, i):
        H, gt = hT[i], gts[i]
        for blk in range(8):
            p = pg[nxt("pg", 3)]
            for kc in range(8):
                S.op(S.pe, lambda h, kc=kc, blk=blk, p=p, H=H: h.matmul(p[:], H[:, kc, :], Wm[:, kc, blk * 512:(blk + 1) * 512], start=(kc == 0), stop=(kc == 7)),
                     reads=[H.res] + Wmr, writes=[p.res], inc=(kc == 7))
            fn = AF.Silu if blk == 0 else AF.Sigmoid
            S.op(S.act, lambda h, blk=blk, p=p, fn=fn, gt=gt: h.activation(out=gt[:, blk, :], in_=p[:], func=fn), reads=[p.res], writes=[gt.res])

    def stage1c(w, i):
        OGt, OMt, gt, ssq = og[i], om[i], gts[i], ssqs[i]
        for br, Ot in enumerate((OGt, OMt)):
            S.op(S.pool, lambda h, Ot=Ot: h.tensor_tensor(out=Ot[:, 0, :], in0=Ot[:, 0, :], in1=Ot[:, 1, :], op=ALU.add), reads=[Ot.res], writes=[Ot.res])
            S.op(S.act, lambda h, Ot=Ot: h.activation(out=sq[:], in_=Ot[:, 0, :], func=AF.Square), reads=[Ot.res], writes=[sq.res])
            S.op(S.dve, lambda h, br=br, ssq=ssq: h.tensor_reduce(out=ssq[:, br, :], in_=sq[:].rearrange("p (a b) -> p a b", b=128), axis=AX.X, op=ALU.add),
                 reads=[sq.res], writes=[ssq.res])
        S.op(S.act, lambda h, ssq=ssq: h.activation(out=ssq[:], in_=ssq[:], func=AF.Sqrt, scale=1.0 / 128, bias=eps_col[:]),
             reads=[ssq.res, eps_col.res], writes=[ssq.res])
        S.op(S.dve, lambda h, ssq=ssq: h.reciprocal(out=ssq[:], in_=ssq[:]), reads=[ssq.res], writes=[ssq.res])
        for br, Ot in enumerate((OGt, OMt)):
            B_ = bn[br]
            S.op(S.dve, lambda h, br=br, Ot=Ot, B_=B_, ssq=ssq: h.tensor_tensor(out=B_[:].rearrange("p (a b) -> p a b", b=128), in0=Ot[:, 0, :].rearrange("p (a b) -> p a b", b=128),
                                                                              in1=ssq[:, br, :].unsqueeze(2).to_broadcast([128, 4, 128]), op=ALU.mult),
                 reads=[Ot.res, ssq.res], writes=[B_.res])
            S.op(S.pool, lambda h, br=br, B_=B_: h.tensor_tensor(out=B_[:].rearrange("p (a b) -> p a b", b=128), in0=B_[:].rearrange("p (a b) -> p a b", b=128),
                                                              in1=gains[:, br:br + 1, :].to_broadcast([128, 4, 128]), op=ALU.mult),
                 reads=[B_.res, gains.res], writes=[B_.res])

    def stage1d(w, i):
        gt = gts[i]
        for br in range(2):
            B_ = bn[br]
            BB = bbs[i][br]
            S.op(S.dve, lambda h, br=br, B_=B_, BB=BB, gt=gt: h.tensor_tensor(out=BB[:], in0=B_[:], in1=gt[:, br, :], op=ALU.mult), reads=[B_.res, gt.res], writes=[BB.res])

    def stage1b(w, i):
        for br in range(2):
            BB = bbs[i][br]
            pT1 = pT1s[br]
            for c in range(4):
                S.op(S.pe, lambda h, c=c, BB=BB, pT1=pT1: h.transpose(out=pT1[:, c * 128:(c + 1) * 128], in_=BB[:, c * 128:(c + 1) * 128], identity=self.ident[:]),
                     reads=[BB.res, self.ident.res], writes=[pT1.res], inc=(c == 3))
            BT = bTs[i][br]
            if br == 0:
                S.op(S.act, lambda h, BT=BT, pT1=pT1: h.activation(out=BT[:].rearrange("p a b -> p (a b)"), in_=pT1[:], func=AF.Copy), reads=[pT1.res], writes=[BT.res])
            else:
                S.op(S.dve, lambda h, BT=BT, pT1=pT1: h.tensor_copy(out=BT[:].rearrange("p a b -> p (a b)"), in_=pT1[:]), reads=[pT1.res], writes=[BT.res])

    cur_row = [None]

    def stage2(w, i):
        (tag, src, dst, row, u, t0) = work[w]
        AT, XR, gt = aTt[i], xr[i], gts[i]
        if cur_row[0] != row:
            cur_row[0] = row
            srcg = self.MOD[l][row:row + 1, 5 * D:6 * D]
            S.dma(S.sp, [(G5[:], dram_ap(srcg, srcg.offset, [[0, 128], [1, D]]))], reads=[self.R("MOD", l)], writes=[G5.res])
        for half in range(2):
            cs_ = slice(half * 512, (half + 1) * 512)
            for hh in range(8):
                S.op(S.pe, lambda h, hh=hh, AT=AT, cs_=cs_: h.matmul(py[0][:], AT[:, hh, :], Woa[:, hh, cs_], start=(hh == 0), stop=(hh == 7)),
                     reads=[AT.res, Woa.res], writes=[py[0].res], inc=(hh == 7))
            for c in range(4):
                S.op(S.pe, lambda h, c=c, cs_=cs_, BT=bTs[i][0]: h.matmul(py[1][:], BT[:, c, :], Wog[:, c, cs_], start=(c == 0), stop=(c == 3)),
                     reads=[bTs[i][0].res, Wog.res], writes=[py[1].res], inc=(c == 3))
            for c in range(4):
                S.op(S.pe, lambda h, c=c, cs_=cs_, BT=bTs[i][1]: h.matmul(py[2][:], BT[:, c, :], Wom[:, c, cs_], start=(c == 0), stop=(c == 3)),
                     reads=[bTs[i][1].res, Wom.res], writes=[py[2].res], inc=(c == 3))
            S.op(S.dve, lambda h, half=half, gt=gt: h.tensor_tensor(out=t1[0][:], in0=py[0][:], in1=gt[:, 2 + half, :], op=ALU.mult), reads=[py[0].res, gt.res], writes=[t1[0].res])
            S.op(S.dve, lambda h, half=half, gt=gt: h.tensor_tensor(out=t1[1][:], in0=py[1][:], in1=gt[:, 4 + half, :], op=ALU.mult), reads=[py[1].res, gt.res], writes=[t1[1].res])
            S.op(S.dve, lambda h, half=half, gt=gt: h.tensor_tensor(out=t1[2][:], in0=py[2][:], in1=gt[:, 6 + half, :], op=ALU.mult), reads=[py[2].res, gt.res], writes=[t1[2].res])
            S.op(S.dve, lambda h: h.tensor_tensor(out=t1[0][:], in0=t1[0][:], in1=t1[1][:], op=ALU.add), reads=[t1[0].res, t1[1].res], writes=[t1[0].res])
            S.op(S.dve, lambda h, cs_=cs_: h.tensor_tensor(out=yb[:, cs_], in0=t1[0][:], in1=t1[2][:], op=ALU.add), reads=[t1[0].res, t1[2].res], writes=[yb.res])
        for kc in range(8):
            S.op(S.pe, lambda h, kc=kc: h.transpose(out=pT2[:, kc * 128:(kc + 1) * 128], in_=yb[:, kc * 128:(kc + 1) * 128], identity=self.ident[:]),
                 reads=[yb.res, self.ident.res], writes=[pT2.res], inc=(kc == 7))
        S.op(S.act, lambda h: h.activation(out=yT[:].rearrange("p a b -> p (a b)"), in_=pT2[:], func=AF.Copy), reads=[pT2.res], writes=[yT.res])
        for half in range(2):
            cs_ = slice(half * 512, (half + 1) * 512)
            for kc in range(8):
                S.op(S.pe, lambda h, kc=kc, cs_=cs_: h.matmul(pY[:], yT[:, kc, :], Wo[:, kc, cs_], start=(kc == 0), stop=(kc == 7)),
                     reads=[yT.res, Wo.res], writes=[pY.res], inc=(kc == 7))
            tq = t1[1 + half]
            S.op(S.dve, lambda h, cs_=cs_, tq=tq: h.tensor_tensor(out=tq[:], in0=pY[:], in1=G5[:, cs_], op=ALU.mult), reads=[pY.res, G5.res], writes=[tq.res])
            S.op(S.pool, lambda h, cs_=cs_, XR=XR, tq=tq: h.tensor_tensor(out=XR[:, cs_], in0=XR[:, cs_], in1=tq[:], op=ALU.add), reads=[XR.res, tq.res], writes=[XR.res])
        S.dma(S.pool, [(dst[t0:t0 + 128, :], XR[:])], reads=[XR.res], writes=[self.R(dst.name)])

    def stage1all(w, i):
        stage1(w, i)
        stage1c(w, i)
        stage1g(w, i)
        stage1d(w, i)
        stage1b(w, i)

    stage1all(0, 0)
    for w in range(len(work)):
        if w + 1 < len(work):
            stage1all(w + 1, (w + 1) % 2)
        stage2(w, w % 2)


Builder.phase_merge = phase_merge

_NC_CACHE = {}


def kernel(**inputs):
    inp = {k: np.asarray(v) for k, v in inputs.items()}
    Bsz, SEQ, _ = inp["x"].shape
    T = SEQ
    if T not in _NC_CACHE:
        _NC_CACHE[T] = Builder(T).build()
    nc = _NC_CACHE[T]
    in_maps = [make_in_map(inp, b, 0, T) for b in range(Bsz)]
    res = run_bass_kernel_spmd(nc, in_maps, core_ids=list(range(Bsz)))
    out = np.stack([np.asarray(r["y"], dtype=np.float32) for r in res.results], axis=0)
    return out


W_NAMES = ["mod_w", "mod_b", "norm_g", "ffn1_w13", "ffn1_w2", "ffn2_w13", "ffn2_w2", "w_in", "attn_q_norm", "attn_k_norm", "attn_sink", "gla_w2", "gla_b", "mlstm_conv_w", "mlstm_conv_b", "mlstm_ib", "mlstm_fb", "gla_norm", "mlstm_norm", "w_out_attn", "w_out_gla", "w_out_mlstm", "w_o"]


def make_in_map(inp, b, t0, T):
    m = {"x": np.ascontiguousarray(inp["x"][b, t0:t0 + T]), "c": np.ascontiguousarray(inp["c"][b]),
         "ctx": np.ascontiguousarray(inp["ctx"][b]), "c_ctx": np.ascontiguousarray(inp["c_ctx"])}
    for k in W_NAMES:
        m[k] = np.ascontiguousarray(inp[k])
    m["rope_cs"] = rope_table(t0, T)
    return m


def rope_table(t0, T):
    pos = np.arange(t0, t0 + T)
    r = (pos // 64).astype(np.float32)
    col = (pos % 64).astype(np.float32)
    inv = (np.float32(10000.0) ** (-np.arange(16, dtype=np.float32) / np.float32(16))).astype(np.float32)
    ang = np.concatenate([r[:, None] * inv, col[:, None] * inv], axis=-1).astype(np.float32)
    return np.ascontiguousarray(np.stack([np.cos(ang), np.sin(ang)], axis=1).astype(np.float32))
```
